# Optimizing a Trainium2 kernel written in Bass

```python
import math
import numpy as np
import jax
import jax.numpy as jnp
from jax import lax

D_MODEL = 1024
BATCH = 4
SEQ = 8192
DEPTH = 2

GRID_W = 64
CTX_LEN = 256
HEAD_DIM = 64
BLOCK_Q = 128
ROPE_THETA = 10000.0
EPS = 1e-6
NEG_INF = -1e30
A_HEADS = 8
A_KV_HEADS = 2
B_HEADS = 8
NA_ROWS = 8
NA_COLS = 16
C_HEADS = 4
D_HEADS = 8
D_KV_HEADS = 2
D_WINDOW = 128
MIX_EVEN = (A_HEADS + B_HEADS) * HEAD_DIM
MIX_ODD = C_HEADS * 2 * HEAD_DIM + D_HEADS * HEAD_DIM
EVEN_SPLITS = (A_HEADS * HEAD_DIM, A_KV_HEADS * HEAD_DIM, A_KV_HEADS * HEAD_DIM,
               B_HEADS * HEAD_DIM, B_HEADS * HEAD_DIM, B_HEADS * HEAD_DIM, MIX_EVEN)
ODD_SPLITS = (C_HEADS * 2 * HEAD_DIM, C_HEADS * 2 * HEAD_DIM, C_HEADS * 2 * HEAD_DIM,
              D_HEADS * HEAD_DIM, D_KV_HEADS * HEAD_DIM, D_KV_HEADS * HEAD_DIM, MIX_ODD)
IN_EVEN = sum(EVEN_SPLITS)
IN_ODD = sum(ODD_SPLITS)
N_EVEN = (DEPTH + 1) // 2
N_ODD = DEPTH // 2
SCALE = HEAD_DIM ** -0.5

kernel_name = 'hybrid_prefix_dit_block'


def rms_norm(x, gain=None):
    xf = x.astype(jnp.float32)
    y = xf * lax.rsqrt(jnp.mean(xf * xf, axis=-1, keepdims=True) + EPS)
    if gain is not None:
        y = y * gain.astype(jnp.float32)
    return y.astype(x.dtype)


def _split(x, sizes):
    idx = [int(i) for i in np.cumsum(sizes)[:-1]]
    return jnp.split(x, idx, axis=-1)


def _modulation(cvec, w_mod, b_mod):
    return jnp.split(jax.nn.silu(cvec) @ w_mod + b_mod, 3, axis=-1)


def lambda_init(layer):
    return 0.8 - 0.6 * math.exp(-0.3 * layer)


def axial_rope_tables(n_tok):
    t = jnp.arange(n_tok, dtype=jnp.int32)
    row = (t // GRID_W).astype(jnp.float32)
    col = (t % GRID_W).astype(jnp.float32)
    quarter = HEAD_DIM // 4
    inv_freq = ROPE_THETA ** (-jnp.arange(quarter, dtype=jnp.float32) / quarter)
    ang_r = row[:, None] * inv_freq
    ang_c = col[:, None] * inv_freq
    return (jnp.cos(ang_r), jnp.sin(ang_r), jnp.cos(ang_c), jnp.sin(ang_c))


def _rot(x, cos, sin):
    x1, x2 = jnp.split(x, 2, axis=-1)
    cos = cos[:, None, :]
    sin = sin[:, None, :]
    return jnp.concatenate([x1 * cos - x2 * sin, x1 * sin + x2 * cos], axis=-1)


def apply_axial_rope(x, tabs):
    cr, sr, cc, sc = tabs
    xr, xcol = jnp.split(x, 2, axis=-1)
    return jnp.concatenate([_rot(xr, cr, sr), _rot(xcol, cc, sc)], axis=-1).astype(x.dtype)


def neighbourhood_tables(n_tok):
    rows = n_tok // GRID_W
    win_r = min(NA_ROWS, rows)
    t = jnp.arange(n_tok, dtype=jnp.int32)
    r = t // GRID_W
    col = t % GRID_W
    rs = jnp.clip(r - win_r // 2, 0, rows - win_r)
    cs = jnp.clip(col - NA_COLS // 2, 0, GRID_W - NA_COLS)
    kr = rs[:, None, None] + jnp.arange(win_r, dtype=jnp.int32)[None, :, None]
    kc = cs[:, None, None] + jnp.arange(NA_COLS, dtype=jnp.int32)[None, None, :]
    key_idx = (kr * GRID_W + kc).reshape(n_tok, win_r * NA_COLS)
    bias_idx = ((kr - r[:, None, None] + NA_ROWS - 1) * (2 * NA_COLS - 1)
                + (kc - col[:, None, None] + NA_COLS - 1)).reshape(n_tok, win_r * NA_COLS)
    return key_idx, bias_idx


def _to_blocks(x):
    b, s = x.shape[:2]
    return jnp.moveaxis(x.reshape((b, s // BLOCK_Q, BLOCK_Q) + x.shape[2:]), 1, 0)


def _from_blocks(y):
    nb, b = y.shape[:2]
    return jnp.moveaxis(y, 0, 1).reshape((b, nb * BLOCK_Q) + y.shape[3:])


def gqa_attend(q5, k, v):
    sc = jnp.einsum('bqkgd,bskd->bkgqs', q5, k).astype(jnp.float32) * SCALE
    p = jax.nn.softmax(sc, axis=-1).astype(v.dtype)
    return jnp.einsum('bkgqs,bskd->bqkgd', p, v)


def diff_attend(q, k, v, lam):
    sc = jnp.einsum('bqhmd,bkhmd->bhmqk', q, k).astype(jnp.float32) * SCALE
    p = jax.nn.softmax(sc, axis=-1)
    a = (p[:, :, 0] - lam * p[:, :, 1]).astype(v.dtype)
    return jnp.einsum('bhqk,bkhe->bqhe', a, v)


def sink_attend(q5, k, v, sink):
    sc = jnp.einsum('bqkgd,bckd->bkgqc', q5, k).astype(jnp.float32) * SCALE
    s_sink = jnp.broadcast_to(sink[None, :, :, None, None], sc.shape[:-1] + (1,))
    p = jax.nn.softmax(jnp.concatenate([s_sink, sc], axis=-1), axis=-1)[..., 1:]
    return jnp.einsum('bkgqc,bckd->bqkgd', p.astype(v.dtype), v)


def neighbourhood_attend(q, k, v, k_ctx, v_ctx, rpb, key_idx, bias_idx):
    b, n_tok, h, d = q.shape
    n_nb = key_idx.shape[-1]
    rpb_flat = rpb.reshape(h, -1).astype(jnp.float32)

    def blk(args):
        qi, ki, bi = args
        kg = k[:, ki]
        vg = v[:, ki]
        s_n = jnp.einsum('bqhd,bqnhd->bhqn', qi, kg).astype(jnp.float32) * SCALE + rpb_flat[:, bi][None]
        s_c = jnp.einsum('bqhd,bchd->bhqc', qi, k_ctx).astype(jnp.float32) * SCALE
        p = jax.nn.softmax(jnp.concatenate([s_n, s_c], axis=-1), axis=-1).astype(v.dtype)
        return (jnp.einsum('bhqn,bqnhd->bqhd', p[..., :n_nb], vg)
                + jnp.einsum('bhqc,bchd->bqhd', p[..., n_nb:], v_ctx))

    nb = n_tok // BLOCK_Q
    out = lax.map(blk, (_to_blocks(q), key_idx.reshape(nb, BLOCK_Q, n_nb), bias_idx.reshape(nb, BLOCK_Q, n_nb)))
    return _from_blocks(out).reshape(b, n_tok, h * d)


def window_attend(q5, k, v, k_ctx, v_ctx, sink):
    b, n_tok, hkv, g, d = q5.shape
    pad = D_WINDOW
    span = BLOCK_Q + 2 * pad
    kp = jnp.pad(k, ((0, 0), (pad, pad), (0, 0), (0, 0)))
    vp = jnp.pad(v, ((0, 0), (pad, pad), (0, 0), (0, 0)))
    starts = jnp.arange(n_tok // BLOCK_Q, dtype=jnp.int32) * BLOCK_Q

    def blk(args):
        qi, st = args
        kw = lax.dynamic_slice_in_dim(kp, st, span, axis=1)
        vw = lax.dynamic_slice_in_dim(vp, st, span, axis=1)
        qpos = st + jnp.arange(BLOCK_Q, dtype=jnp.int32)
        kpos = st - pad + jnp.arange(span, dtype=jnp.int32)
        valid = ((kpos[None, :] >= 0) & (kpos[None, :] < n_tok)
                 & (jnp.abs(qpos[:, None] - kpos[None, :]) <= D_WINDOW))
        s_w = jnp.einsum('bqkgd,bskd->bkgqs', qi, kw).astype(jnp.float32) * SCALE
        s_w = jnp.where(valid, s_w, NEG_INF)
        s_c = jnp.einsum('bqkgd,bckd->bkgqc', qi, k_ctx).astype(jnp.float32) * SCALE
        s_sink = jnp.broadcast_to(sink[None, :, :, None, None], s_w.shape[:-1] + (1,))
        p = jax.nn.softmax(jnp.concatenate([s_sink, s_w, s_c], axis=-1), axis=-1).astype(v.dtype)
        return (jnp.einsum('bkgqs,bskd->bqkgd', p[..., 1:1 + span], vw)
                + jnp.einsum('bkgqc,bckd->bqkgd', p[..., 1 + span:], v_ctx))

    out = lax.map(blk, (_to_blocks(q5), starts))
    return _from_blocks(out).reshape(b, n_tok, hkv * g * d)


def even_mixer(h, hc, w_in, q_gain, k_gain, rpb, rope_tabs, na_tabs, ctx_out):
    b, n_tok, _ = h.shape
    L = hc.shape[1]
    hd = HEAD_DIM
    ga = A_HEADS // A_KV_HEADS
    qa, ka, va, qb, kb, vb, z = _split(h @ w_in, EVEN_SPLITS)
    qa_c, ka_c, va_c, qb_c, kb_c, vb_c, z_c = _split(hc @ w_in, EVEN_SPLITS)
    qa = apply_axial_rope(rms_norm(qa.reshape(b, n_tok, A_HEADS, hd), q_gain), rope_tabs)
    ka = apply_axial_rope(rms_norm(ka.reshape(b, n_tok, A_KV_HEADS, hd), k_gain), rope_tabs)
    ka_c = rms_norm(ka_c.reshape(b, L, A_KV_HEADS, hd), k_gain)
    va_c = va_c.reshape(b, L, A_KV_HEADS, hd)
    k_all = jnp.concatenate([ka, ka_c], axis=1)
    v_all = jnp.concatenate([va.reshape(b, n_tok, A_KV_HEADS, hd), va_c], axis=1)
    qa5 = qa.reshape(b, n_tok, A_KV_HEADS, ga, hd)
    ya = _from_blocks(lax.map(lambda qi: gqa_attend(qi, k_all, v_all), _to_blocks(qa5))).reshape(b, n_tok, -1)
    kb_c = kb_c.reshape(b, L, B_HEADS, hd)
    vb_c = vb_c.reshape(b, L, B_HEADS, hd)
    yb = neighbourhood_attend(qb.reshape(b, n_tok, B_HEADS, hd), kb.reshape(b, n_tok, B_HEADS, hd),
                              vb.reshape(b, n_tok, B_HEADS, hd), kb_c, vb_c, rpb, na_tabs[0], na_tabs[1])
    y = jnp.concatenate([ya, yb], axis=-1) * jax.nn.silu(z)
    if not ctx_out:
        return y, None
    qa_c5 = rms_norm(qa_c.reshape(b, L, A_KV_HEADS, ga, hd), q_gain)
    ya_c = gqa_attend(qa_c5, ka_c, va_c).reshape(b, L, -1)
    yb_c = gqa_attend(qb_c.reshape(b, L, B_HEADS, 1, hd), kb_c, vb_c).reshape(b, L, -1)
    y_ctx = jnp.concatenate([ya_c, yb_c], axis=-1) * jax.nn.silu(z_c)
    return y, y_ctx


def odd_mixer(h, hc, w_in, lam_vecs, subln, sinks, lam0, rope_tabs, ctx_out):
    b, n_tok, _ = h.shape
    L = hc.shape[1]
    hd = HEAD_DIM
    gd = D_HEADS // D_KV_HEADS
    qc, kc, vc, qd, kd, vd, z = _split(h @ w_in, ODD_SPLITS)
    qc_c, kc_c, vc_c, qd_c, kd_c, vd_c, z_c = _split(hc @ w_in, ODD_SPLITS)
    lv = lam_vecs.astype(jnp.float32)
    lam = jnp.exp(jnp.sum(lv[0] * lv[1])) - jnp.exp(jnp.sum(lv[2] * lv[3])) + lam0
    qc = apply_axial_rope(qc.reshape(b, n_tok, 2 * C_HEADS, hd), rope_tabs).reshape(b, n_tok, C_HEADS, 2, hd)
    kc = apply_axial_rope(kc.reshape(b, n_tok, 2 * C_HEADS, hd), rope_tabs).reshape(b, n_tok, C_HEADS, 2, hd)
    kc_c = kc_c.reshape(b, L, C_HEADS, 2, hd)
    vc_c = vc_c.reshape(b, L, C_HEADS, 2 * hd)
    kc_all = jnp.concatenate([kc, kc_c], axis=1)
    vc_all = jnp.concatenate([vc.reshape(b, n_tok, C_HEADS, 2 * hd), vc_c], axis=1)
    yc = _from_blocks(lax.map(lambda qi: diff_attend(qi, kc_all, vc_all, lam), _to_blocks(qc)))
    yc = (rms_norm(yc, subln) * (1.0 - lam0)).reshape(b, n_tok, -1)
    qd5 = apply_axial_rope(qd.reshape(b, n_tok, D_HEADS, hd), rope_tabs).reshape(b, n_tok, D_KV_HEADS, gd, hd)
    kd = apply_axial_rope(kd.reshape(b, n_tok, D_KV_HEADS, hd), rope_tabs)
    kd_c = kd_c.reshape(b, L, D_KV_HEADS, hd)
    vd_c = vd_c.reshape(b, L, D_KV_HEADS, hd)
    sink = sinks.reshape(D_KV_HEADS, gd).astype(jnp.float32)
    yd = window_attend(qd5, kd, vd.reshape(b, n_tok, D_KV_HEADS, hd), kd_c, vd_c, sink)
    y = jnp.concatenate([yc, yd], axis=-1) * jax.nn.silu(z)
    if not ctx_out:
        return y, None
    yc_c = (rms_norm(diff_attend(qc_c.reshape(b, L, C_HEADS, 2, hd), kc_c, vc_c, lam), subln)
            * (1.0 - lam0)).reshape(b, L, -1)
    yd_c = sink_attend(qd_c.reshape(b, L, D_KV_HEADS, gd, hd), kd_c, vd_c, sink).reshape(b, L, -1)
    y_ctx = jnp.concatenate([yc_c, yd_c], axis=-1) * jax.nn.silu(z_c)
    return y, y_ctx


def setup_inputs(seed: int = 0) -> dict:
    key = jax.random.key(seed)
    ks = jax.random.split(key, 17)
    f32 = jnp.float32

    def nrm(k, shape, s):
        return jax.random.normal(k, shape, f32) * s

    return {
        'x': nrm(ks[0], (BATCH, SEQ, D_MODEL), 1.0),
        'c': nrm(ks[1], (BATCH, D_MODEL), 1.0),
        'ctx': nrm(ks[2], (BATCH, CTX_LEN, D_MODEL), 1.0),
        'c_ctx': nrm(ks[3], (D_MODEL,), 1.0),
        'w_mod': nrm(ks[4], (DEPTH, D_MODEL, 3 * D_MODEL), 0.5 * D_MODEL ** -0.5),
        'b_mod': nrm(ks[5], (DEPTH, 3 * D_MODEL), 0.02),
        'w_in_even': nrm(ks[6], (N_EVEN, D_MODEL, IN_EVEN), D_MODEL ** -0.5),
        'w_out_even': nrm(ks[7], (N_EVEN, MIX_EVEN, D_MODEL), MIX_EVEN ** -0.5),
        'a_q_norm': 1.0 + nrm(ks[8], (N_EVEN, HEAD_DIM), 0.02),
        'a_k_norm': 1.0 + nrm(ks[9], (N_EVEN, HEAD_DIM), 0.02),
        'b_rpb': nrm(ks[10], (N_EVEN, B_HEADS, 2 * NA_ROWS - 1, 2 * NA_COLS - 1), 0.1),
        'w_in_odd': nrm(ks[11], (N_ODD, D_MODEL, IN_ODD), D_MODEL ** -0.5),
        'w_out_odd': nrm(ks[12], (N_ODD, MIX_ODD, D_MODEL), MIX_ODD ** -0.5),
        'c_lambda': nrm(ks[13], (N_ODD, 4, HEAD_DIM), 0.1),
        'c_subln': 1.0 + nrm(ks[14], (N_ODD, 2 * HEAD_DIM), 0.02),
        'd_sinks': nrm(ks[15], (N_ODD, D_HEADS), 0.5),
        'final_norm': 1.0 + nrm(ks[16], (D_MODEL,), 0.02),
    }


def reference(x, c, ctx, c_ctx, w_mod, b_mod, w_in_even, w_out_even, a_q_norm, a_k_norm, b_rpb,
              w_in_odd, w_out_odd, c_lambda, c_subln, d_sinks, final_norm):
    n_tok = x.shape[1]
    rope_tabs = axial_rope_tables(n_tok)
    na_tabs = neighbourhood_tables(n_tok)
    xc = ctx
    for l in range(DEPTH):
        last = l == DEPTH - 1
        shift, scale, gate = _modulation(c, w_mod[l], b_mod[l])
        shift_c, scale_c, gate_c = _modulation(c_ctx, w_mod[l], b_mod[l])
        h = rms_norm(x) * (1.0 + scale[:, None]) + shift[:, None]
        hc = rms_norm(xc) * (1.0 + scale_c) + shift_c
        i = l // 2
        if l % 2 == 0:
            y, y_ctx = even_mixer(h, hc, w_in_even[i], a_q_norm[i], a_k_norm[i], b_rpb[i],
                                  rope_tabs, na_tabs, not last)
            w_out = w_out_even[i]
        else:
            y, y_ctx = odd_mixer(h, hc, w_in_odd[i], c_lambda[i], c_subln[i], d_sinks[i],
                                 lambda_init(l), rope_tabs, not last)
            w_out = w_out_odd[i]
        x = x + gate[:, None] * (y @ w_out)
        if not last:
            xc = xc + gate_c * (y_ctx @ w_out)
    return rms_norm(x, final_norm)
```

```python
import contextlib
import math
import numpy as np
import concourse.bass as bass
import concourse.mybir as mybir
from concourse.bass_utils import run_bass_kernel_spmd

F32 = mybir.dt.float32
BF16 = mybir.dt.bfloat16
AF = mybir.ActivationFunctionType
ALU = mybir.AluOpType

D_MODEL = 1024
CTX = 256
HD = 64
GRID_W = 64
SCALE = HD ** -0.5
EPS = 1e-6
NEG = -30000.0
HALO = 512


class Tile:
    __slots__ = ("name", "t", "last_w", "readers", "dsem", "dcount", "excl")

    def __init__(self, name, t=None, excl=False):
        self.name = name
        self.t = t
        self.last_w = None
        self.readers = {}
        self.dsem = None
        self.dcount = 0
        self.excl = excl

    def __getitem__(self, idx):
        return self.t[idx]


class Sched:
    def __init__(self, nc, stack):
        self.nc = nc
        self.stack = stack
        self.sem_stack = stack
        self.dtiles = []
        self.engs = {}
        self.sems = {}
        for en, e in (("pe", nc.tensor), ("act", nc.scalar), ("dve", nc.vector),
                      ("pool", nc.gpsimd), ("sp", nc.sync)):
            self.sems[en] = stack.enter_context(nc.semaphore("s_" + en))
            self.engs[en] = dict(eng=e, count=0, seen={}, key=en)
        self.epoch = 0
        self.nsem = 0
        self.prefix = ""
        self.free_dsems = []

    def sb(self, name, shape, dt):
        return Tile(name, self.stack.enter_context(self.nc.sbuf_tensor("sb_" + self.prefix + name, list(shape), dt)))

    def ps(self, name, shape, dt=F32):
        return Tile(name, self.stack.enter_context(self.nc.psum_tensor("ps_" + name, list(shape), dt)), excl=True)

    def _dsem(self, tile):
        if tile.dsem is None:
            if self.free_dsems:
                key, cnt = self.free_dsems.pop()
                tile.dsem = key
                tile.dcount = cnt
            else:
                key = "d%d" % self.nsem
                self.nsem += 1
                tile.dsem = key
                self.sems[key] = self.sem_stack.enter_context(self.nc.semaphore(key))
            self.dtiles.append(tile)
        return tile.dsem

    def new_epoch(self):
        self.epoch += 1
        for en, E in self.engs.items():
            key = "%s#%d" % (en, self.epoch)
            self.sems[key] = self.sem_stack.enter_context(self.nc.semaphore("s_%s_%d" % (en, self.epoch)))
            E["key"] = key
            E["count"] = 0

    def release_dsems(self):
        for t in self.dtiles:
            self.free_dsems.append((t.dsem, t.dcount))
            t.dsem = None
        self.dtiles = []

    def _wait_deps(self, en, reads, writes):
        E = self.engs[en]
        deps = {}

        def add(ev):
            if ev is None:
                return
            k, v = ev
            if deps.get(k, 0) < v:
                deps[k] = v

        me = E["key"]
        for t in reads:
            add(t.last_w)
            if t.excl:
                for k, v in t.readers.items():
                    if k != me:
                        add((k, v))
        for t in writes:
            add(t.last_w)
            for k, v in t.readers.items():
                if k != me:
                    add((k, v))
        for k, v in deps.items():
            if E["seen"].get(k, 0) < v:
                E["seen"][k] = v
                if k == me and en == "pe":
                    continue
                E["eng"].wait_ge(self.sems[k], v)

    def op(self, en, fn, reads=(), writes=()):
        E = self.engs[en]
        self._wait_deps(en, reads, writes)
        ins = fn(E["eng"])
        E["count"] += 1
        me = E["key"]
        ins.then_inc(self.sems[me], 1)
        for t in reads:
            t.readers[me] = E["count"]
        for t in writes:
            t.last_w = (me, E["count"])
            t.readers = {}
        return ins

    def dma(self, q, out, in_, reads=(), writes=(), sem_tile=None):
        E = self.engs[q]
        self._wait_deps(q, reads, writes)
        st = sem_tile if sem_tile is not None else (list(writes) + list(reads))[0]
        key = self._dsem(st)
        ins = E["eng"].dma_start(out=out, in_=in_)
        st.dcount += 16
        ins.then_inc(self.sems[key], 16)
        for t in reads:
            t.readers[key] = st.dcount
        for t in writes:
            t.last_w = (key, st.dcount)
            t.readers = {}
        return ins

    def barrier(self):
        for en, E in self.engs.items():
            for en2, E2 in self.engs.items():
                k2 = E2["key"]
                if en2 != en and E2["count"] and E["seen"].get(k2, 0) < E2["count"]:
                    E["seen"][k2] = E2["count"]
                    E["eng"].wait_ge(self.sems[k2], E2["count"])
            for t in self.dtiles:
                if t.dcount and E["seen"].get(t.dsem, 0) < t.dcount:
                    E["seen"][t.dsem] = t.dcount
                    E["eng"].wait_ge(self.sems[t.dsem], t.dcount)

    def finish(self, tiles, en="sp"):
        self._wait_deps(en, tiles, tiles)


class Rot:
    def __init__(self, tiles):
        self.tiles = tiles
        self.i = 0

    def next(self):
        t = self.tiles[self.i % len(self.tiles)]
        self.i += 1
        return t


def layer_cfg(layer):
    if layer == 0:
        qa, ka, va, qb, kb, vb, z = 0, 512, 640, 768, 1280, 1792, 2304
        fm = []
        for c in range(4):
            cols = list(range(qa + c * 64, qa + c * 64 + 64)) + list(range(qa + (4 + c) * 64, qa + (4 + c) * 64 + 64))
            fm.append(("qa%d" % c, cols, "q", True))
        fm.append(("ka", list(range(ka, ka + 128)), "k", True))
        for c in range(4):
            fm.append(("qb%d" % c, list(range(qb + c * 128, qb + c * 128 + 128)), None, False))
        for c in range(4):
            fm.append(("kb%d" % c, list(range(kb + c * 128, kb + c * 128 + 128)), None, False))
        tm = [("va", va, 128), ("vb", vb, 512), ("z", z, 1024)]
        dense_k, dense_v = ["ka"], ["va"]
        local_k, local_v = ["kb0", "kb1", "kb2", "kb3"], ["vb"]
    else:
        qc, kc, vc, qd, kd, vd, z = 0, 512, 1024, 1536, 2048, 2176, 2304
        fm = []
        for c in range(4):
            fm.append(("qc%d" % c, list(range(qc + c * 128, qc + c * 128 + 128)), None, True))
        for c in range(4):
            fm.append(("kc%d" % c, list(range(kc + c * 128, kc + c * 128 + 128)), None, True))
        for c in range(4):
            cols = list(range(qd + c * 64, qd + c * 64 + 64)) + list(range(qd + (4 + c) * 64, qd + (4 + c) * 64 + 64))
            fm.append(("qd%d" % c, cols, None, True))
        fm.append(("kd", list(range(kd, kd + 128)), None, True))
        tm = [("vc", vc, 512), ("vd", vd, 128), ("z", z, 1024)]
        dense_k, dense_v = ["kc0", "kc1", "kc2", "kc3"], ["vc"]
        local_k, local_v = ["kd"], ["vd"]
    return dict(fm=fm, tm=tm, dense_k=dense_k, dense_v=dense_v, local_k=local_k, local_v=local_v)


def lambda_init(layer):
    return 0.8 - 0.6 * math.exp(-0.3 * layer)


def emit_layer(nc, S, SH, layer, SEQ):
    HALF = SEQ // 2
    EXT = HALF + 2 * HALO
    NU = EXT + HALF + CTX
    NTo = HALF // 128
    NBo = HALF // 512
    U_OWN = HALO
    U_O = EXT
    U_C = EXT + HALF
    NKD = 2 * HALF + CTX
    NKC = NKD // 128
    last = layer == 1
    cfg = layer_cfg(layer)
    fm, tm = cfg["fm"], cfg["tm"]
    NFM = len(fm)
    fmi = {f[0]: i for i, f in enumerate(fm)}
    NCOL = NFM * 128 + sum(t[2] for t in tm)
    tmoff = {}
    o = NFM * 128
    for name, _, n in tm:
        tmoff[name] = o
        o += n
    lam0 = lambda_init(layer)

    LP = "l%d_" % layer
    S.prefix = LP

    def din(name, shape):
        return nc.dram_tensor(LP + name, list(shape), F32, kind="ExternalInput").ap()

    x_u, xc_in, cvec = SH["x_u"], SH["xc"], SH["cvec"]
    ident_d, blk_d, perm_d, ropeC, ropeS = SH["ident"], SH["blk"], SH["perm"], SH["ropeC"], SH["ropeS"]
    x1_loc, xc1_loc, GA = SH["x1_loc"], SH["xc1_loc"], SH["GA"]
    w_mod = din("w_mod", [8, 128, 3072])
    b_mod = din("b_mod", [128, 24])
    bgate = din("bgate", [128, D_MODEL])
    w_in = din("w_in", [8, 128, NCOL])
    w_out = din("w_out", [8, 128, D_MODEL])
    if layer == 0:
        gains = din("gains", [128, 2])
        bias_i = din("bias_i", [128, 8, 5 * 128])
        bias_e = din("bias_e", [4, 8, 128, 7 * 128])
    else:
        dmask = din("dmask", [128, 4, 128])
        lamv = din("lamv", [128, 256])
        subln = din("subln", [128, 128])
        sinks = din("sinks", [128, 8])
        fnorm = din("fnorm", [128, D_MODEL])
    if last:
        out_x = nc.dram_tensor("out_x", [HALF, D_MODEL], F32, kind="ExternalOutput").ap()
    else:
        out_x = x1_loc
        out_c = xc1_loc

    fmS = nc.dram_tensor(LP + "fmS", [NFM, 128, NU], BF16).ap()
    vdims = {"va": (2, 65), "vb": (8, 65), "vc": (4, 129), "vd": (2, 65)}
    vS = {}
    for name, _, n in tm:
        if name != "z":
            h, d = vdims[name]
            vS[name] = nc.dram_tensor(LP + "vS_" + name, [NU, h * d], BF16).ap()
    zS = nc.dram_tensor(LP + "zS", [NU, D_MODEL], BF16).ap()

    with contextlib.ExitStack() as st:
        S.stack = st
        st1 = contextlib.ExitStack()
        fm_reg = Tile("fm_reg")
        v_reg = Tile("v_reg")
        z_reg = Tile("z_reg")

        ident_f = S.sb("ident_f", [128, 128], F32)
        ident = S.sb("ident", [128, 128], BF16)
        blk = S.sb("blk", [128, 128], F32)
        perm = S.sb("perm", [128, 128], F32)
        epst = S.sb("epst", [128, 1], F32)
        zeros = S.sb("zeros", [128, 128], F32)
        junk = S.sb("junk", [128, D_MODEL], F32)
        ss_r = Rot([S.sb("ss%d" % i, [128, 2], F32) for i in range(2)])
        t1_r = Rot([S.sb("t1%d" % i, [128, 512], F32) for i in range(2)])
        gate_bc = S.sb("gate_bc", [128, 2, D_MODEL], F32)
        modt = S.sb("modt", [128, 24, 2], F32)
        bmodt = S.sb("bmodt", [128, 24], F32)
        cv = S.sb("cv", [128, 8, 2], F32)
        pTb_t = SH["pTb_t"]
        pTb = pTb_t.t
        banks = SH["banks"]
        sel_t = S.sb("sel_t", [128, 4], F32)
        S.dma("sp", sel_t[:], SH["sel"], writes=[sel_t])
        xt2 = S.sb("xt2", [128, D_MODEL], F32)
        G_reg = SH["G_reg"]
        NTo_ = HALF // 128

        RCt = min(512, HALF) // 128

        def grow(r, t):
            return ((t // RCt) * 2 * RCt + r * RCt + (t % RCt)) * 128

        def load_x_tile(xt, utile):
            e = utile
            if layer == 0:
                if e >= (EXT + HALF) // 128:
                    c = e - (EXT + HALF) // 128
                    S.dma("sp", xt[:], xc_in[c * 128:(c + 1) * 128, :], writes=[xt])
                else:
                    S.dma("sp", xt[:], x_u[e * 128:(e + 1) * 128, :], writes=[xt])
                return
            if e >= (EXT + HALF) // 128:
                c = e - (EXT + HALF) // 128
                S.dma("sp", xt[:], xc1_loc[c * 128:(c + 1) * 128, :], writes=[xt])
            elif e >= EXT // 128:
                o = e - EXT // 128
                S.dma("sp", xt[:], GA[grow(0, o):grow(0, o) + 128, :], reads=[G_reg], writes=[xt])
                S.dma("sp", xt2[:], GA[grow(1, o):grow(1, o) + 128, :], reads=[G_reg], writes=[xt2])
                S.op("act", lambda en: en.activation(out=xt[:], in_=xt[:], func=AF.Copy, scale=sel_t[:, 2:3]), reads=[xt, sel_t], writes=[xt])
                S.op("dve", lambda en: en.scalar_tensor_tensor(out=xt[:], in0=xt2[:], scalar=sel_t[:, 3:4], in1=xt[:], op0=ALU.mult, op1=ALU.add),
                     reads=[xt2, sel_t, xt], writes=[xt])
            elif 4 <= e < 4 + NTo_:
                t = e - 4
                S.dma("sp", xt[:], x1_loc[t * 128:(t + 1) * 128, :], writes=[xt])
            elif e < 4:
                gt = grow(0, NTo_ - 4 + e)
                S.dma("sp", xt[:], GA[gt:gt + 128, :], reads=[G_reg], writes=[xt])
                S.op("act", lambda en: en.activation(out=xt[:], in_=xt[:], func=AF.Copy, scale=sel_t[:, 0:1]), reads=[xt, sel_t], writes=[xt])
            else:
                gt = grow(1, e - 4 - NTo_)
                S.dma("sp", xt[:], GA[gt:gt + 128, :], reads=[G_reg], writes=[xt])
                S.op("act", lambda en: en.activation(out=xt[:], in_=xt[:], func=AF.Copy, scale=sel_t[:, 1:2]), reads=[xt, sel_t], writes=[xt])

        S.dma("sp", ident_f[:], ident_d, writes=[ident_f])
        S.dma("sp", blk[:], blk_d, writes=[blk])
        S.dma("sp", perm[:], perm_d, writes=[perm])
        S.dma("sp", cv[:], cvec, writes=[cv])
        S.dma("sp", bmodt[:], b_mod, writes=[bmodt])
        S.dma("sp", gate_bc[:, 0, :], bgate, writes=[gate_bc])
        S.op("dve", lambda e: e.tensor_copy(out=ident[:], in_=ident_f[:]), reads=[ident_f], writes=[ident])
        S.op("pool", lambda e: e.memset(epst[:], EPS), writes=[epst])
        S.op("pool", lambda e: e.memset(zeros[:], 0.0), writes=[zeros])
        S.op("dve", lambda e: e.tensor_copy(out=gate_bc[:, 1, :], in_=gate_bc[:, 0, :]), reads=[gate_bc], writes=[gate_bc])
        S.stack = st
        if layer == 0:
            gn = S.sb("gn", [128, 2], F32)
            bi_t = S.sb("bi_t", [128, 8, 640], F32)
            S.dma("sp", gn[:], gains, writes=[gn])
            S.dma("sp", bi_t[:], bias_i, writes=[bi_t])
        else:
            dm_t = S.sb("dm_t", [128, 4, 128], F32)
            lam_t = S.sb("lam_t", [128, 256], F32)
            sub_t = S.sb("sub_t", [128, 128], F32)
            snk_t = S.sb("snk_t", [128, 8], F32)
            fn_t = S.sb("fn_t", [128, D_MODEL], F32)
            S.dma("sp", dm_t[:], dmask, writes=[dm_t])
            S.dma("sp", lam_t[:], lamv, writes=[lam_t])
            S.dma("sp", sub_t[:], subln, writes=[sub_t])
            S.dma("sp", snk_t[:], sinks, writes=[snk_t])
            S.dma("sp", fn_t[:], fnorm, writes=[fn_t])
            lam_s = S.sb("lam_s", [128, 4], F32)
            lprod = S.sb("lprod", [128, 128], F32)
            esnk = S.sb("esnk", [128, 8], F32)
        S.stack = st1
        win = S.sb("win", [128, 8, NCOL], BF16)
        screp = S.sb("screp", [128, 2, 8, 128], F32)
        stage = Rot([S.sb("stage%d" % i, [128, 1024], F32) for i in range(2)])

        S.op("act", lambda e: e.activation(out=cv[:], in_=cv[:], func=AF.Silu), reads=[cv], writes=[cv])
        for w in range(2):
            for k in range(8):
                S.op("act", lambda e: e.activation(out=screp[:, w, k, :], in_=zeros[:], func=AF.Identity,
                                                   bias=cv[:, k, w:w + 1]), reads=[zeros, cv], writes=[screp])
        pmod = banks[5]
        pg = [banks[1], banks[2], banks[3], banks[4]]
        for k in range(8):
            for pi in range(3):
                stg = stage.next()
                S.dma("sp", stg[:], w_mod[k, :, pi * 1024:(pi + 1) * 1024], writes=[stg])
                for jj in range(8):
                    j = pi * 8 + jj
                    S.op("pe", lambda e: e.matmul(pmod[:, j * 2:j * 2 + 2], lhsT=stg[:, jj * 128:(jj + 1) * 128], rhs=cv[:, k, :],
                                                  start=(k == 0 and j == 0), stop=(k == 7), skip_group_check=True),
                         reads=[stg, cv], writes=[pmod])
                if pi == 2:
                    for w in range(2):
                        for n in range(2):
                            S.op("pe", lambda e: e.matmul(pg[w * 2 + n][:], lhsT=screp[:, w, k, :],
                                                          rhs=stg[:, n * 512:(n + 1) * 512],
                                                          start=(k == 0), stop=(k == 7)),
                                 reads=[stg, screp], writes=[pg[w * 2 + n]])
        for w in range(2):
            S.op("dve", lambda e: e.tensor_tensor(out=modt[:, :, w], in0=pmod[:, 0:48].rearrange("p (j w) -> p j w", w=2)[:, :, w],
                                                  in1=bmodt[:], op=ALU.add), reads=[pmod, bmodt], writes=[modt])
            for n in range(2):
                S.op("dve", lambda e: e.tensor_tensor(out=gate_bc[:, w, n * 512:(n + 1) * 512], in0=pg[w * 2 + n][:],
                                                      in1=gate_bc[:, w, n * 512:(n + 1) * 512], op=ALU.add),
                     reads=[pg[w * 2 + n], gate_bc], writes=[gate_bc])
        S.op("dve", lambda e: e.tensor_scalar_add(out=modt[:, 8:16, :], in0=modt[:, 8:16, :], scalar1=1.0),
             reads=[modt], writes=[modt])

        cnt = 0
        for k in range(8):
            for c0 in range(0, NCOL, 1024):
                cn = min(1024, NCOL - c0)
                stg = stage.next()
                S.dma("sp", stg[:, 0:cn], w_in[k, :, c0:c0 + cn], writes=[stg])
                S.op("pool" if cnt % 2 else "dve", lambda e: e.tensor_copy(out=win[:, k, c0:c0 + cn], in_=stg[:, 0:cn]), reads=[stg], writes=[win])
                cnt += 1

        if layer == 1:
            S.op("dve", lambda e: e.tensor_tensor(out=lprod[:, 0:64], in0=lam_t[:, 0:64], in1=lam_t[:, 64:128], op=ALU.mult), reads=[lam_t], writes=[lprod])
            S.op("dve", lambda e: e.tensor_tensor(out=lprod[:, 64:128], in0=lam_t[:, 128:192], in1=lam_t[:, 192:256], op=ALU.mult), reads=[lam_t, lprod], writes=[lprod])
            S.op("dve", lambda e: e.reduce_sum(out=lam_s[:, 0:2], in_=lprod[:].rearrange("p (a d) -> p a d", a=2), axis=mybir.AxisListType.X), reads=[lprod], writes=[lam_s])
            S.op("act", lambda e: e.activation(out=lam_s[:, 0:2], in_=lam_s[:, 0:2], func=AF.Exp), reads=[lam_s], writes=[lam_s])
            S.op("dve", lambda e: e.tensor_tensor(out=lam_s[:, 2:3], in0=lam_s[:, 0:1], in1=lam_s[:, 1:2], op=ALU.subtract), reads=[lam_s], writes=[lam_s])
            S.op("dve", lambda e: e.tensor_scalar(out=lam_s[:, 3:4], in0=lam_s[:, 2:3], scalar1=lam0, scalar2=-1.0, op0=ALU.add, op1=ALU.mult), reads=[lam_s], writes=[lam_s])
            S.op("act", lambda e: e.activation(out=esnk[:], in_=snk_t[:], func=AF.Exp), reads=[snk_t], writes=[esnk])

        xt_r = Rot([S.sb("xt%d" % i, [128, D_MODEL], F32) for i in range(2)])
        xn_r = Rot([S.sb("xn%d" % i, [128, D_MODEL], BF16) for i in range(2)])
        hT_r = Rot([S.sb("hT%d" % i, [128, 8, 512], BF16) for i in range(2)])
        tabC_r = Rot([S.sb("tabC%d" % i, [128, 512], F32) for i in range(1)])
        tabS_r = Rot([S.sb("tabS%d" % i, [128, 512], F32) for i in range(1)])
        sq_r = Rot([S.sb("sq%d" % i, [128, 512], F32) for i in range(1)])
        rs_r = Rot([S.sb("rs%d" % i, [128, 512], F32) for i in range(1)])
        qn_r = Rot([S.sb("qn%d" % i, [128, 512], F32) for i in range(2)])
        t2_r = Rot([S.sb("t2%d" % i, [128, 512], F32) for i in range(1)])
        fo_r = Rot([S.sb("fo%d" % i, [128, 512], BF16) for i in range(3)])
        vst = {}
        for name in vS:
            h, d = vdims[name]
            vst[name] = Rot([S.sb("vst_%s%d" % (name, i), [128, h, d], BF16) for i in range(2)])
            for t in vst[name].tiles:
                S.op("pool", lambda e: e.memset(t[:], 1.0), writes=[t])
        zst_r = Rot([S.sb("zst%d" % i, [128, D_MODEL], BF16) for i in range(2)])
        bA = Rot([banks[1], banks[2], banks[3]])
        bB = Rot([banks[4], banks[5]])

        def phase1_block(u0, ntiles, src, src_row0, w, fm_list, tm_list, rope_col0):
            ntok = ntiles * 128
            hT = hT_r.next()
            for ti in range(ntiles):
                xt = xt_r.next()
                ss = ss_r.next()
                xn = xn_r.next()
                load_x_tile(xt, u0 // 128 + ti)
                S.op("act", lambda e: e.activation(out=junk[:], in_=xt[:], func=AF.Square, accum_out=ss[:, 0:1]), reads=[xt], writes=[junk, ss])
                S.op("act", lambda e: e.activation(out=ss[:, 1:2], in_=ss[:, 0:1], func=AF.Ln, scale=1.0 / D_MODEL, bias=epst[:]), reads=[ss, epst], writes=[ss])
                S.op("act", lambda e: e.activation(out=ss[:, 1:2], in_=ss[:, 1:2], func=AF.Exp, scale=-0.5), reads=[ss], writes=[ss])
                S.op("act", lambda e: e.activation(out=xn[:], in_=xt[:], func=AF.Copy, scale=ss[:, 1:2]), reads=[xt, ss], writes=[xn])
                for k in range(8):
                    S.op("pe", lambda e: e.transpose(out=pTb[:, k, :], in_=xn[:, k * 128:(k + 1) * 128], identity=ident[:]),
                         reads=[xn, ident], writes=[pTb_t])
                for k in range(8):
                    S.op("dve", lambda e: e.tensor_scalar(out=hT[:, k, ti * 128:(ti + 1) * 128], in0=pTb[:, k, :],
                                                          scalar1=modt[:, 8 + k, w:w + 1], scalar2=modt[:, k, w:w + 1],
                                                          op0=ALU.mult, op1=ALU.add), reads=[pTb_t, modt], writes=[hT])
            rope_needed = any(fm[i][3] for i in fm_list) and rope_col0 is not None
            if rope_needed:
                tC = tabC_r.next()
                tS = tabS_r.next()
                S.dma("sp", tC[:, 0:ntok], ropeC[:, rope_col0:rope_col0 + ntok], writes=[tC])
                S.dma("sp", tS[:, 0:ntok], ropeS[:, rope_col0:rope_col0 + ntok], writes=[tS])
            for i in fm_list:
                name, _, nkind, roped = fm[i]
                pa = bA.next()
                for k in range(8):
                    S.op("pe", lambda e: e.matmul(pa[:, 0:ntok], lhsT=win[:, k, i * 128:(i + 1) * 128], rhs=hT[:, k, 0:ntok],
                                                  start=(k == 0), stop=(k == 7)), reads=[win, hT], writes=[pa])
                fo = fo_r.next()
                do_rope = roped and rope_col0 is not None
                if nkind is None and not do_rope:
                    S.op("act", lambda e: e.activation(out=fo[:, 0:ntok], in_=pa[:, 0:ntok], func=AF.Copy), reads=[pa], writes=[fo])
                else:
                    qn = qn_r.next()
                    if nkind is not None:
                        sq = sq_r.next()
                        rs = rs_r.next()
                        pb = bB.next()
                        gcol = 0 if nkind == "q" else 1
                        S.op("act", lambda e: e.activation(out=sq[:, 0:ntok], in_=pa[:, 0:ntok], func=AF.Square), reads=[pa], writes=[sq])
                        S.op("pe", lambda e: e.matmul(pb[:, 0:ntok], lhsT=blk[:], rhs=sq[:, 0:ntok], start=True, stop=True), reads=[blk, sq], writes=[pb])
                        S.op("act", lambda e: e.activation(out=rs[:, 0:ntok], in_=pb[:, 0:ntok], func=AF.Ln, bias=epst[:]), reads=[pb, epst], writes=[rs])
                        S.op("act", lambda e: e.activation(out=rs[:, 0:ntok], in_=rs[:, 0:ntok], func=AF.Exp, scale=-0.5), reads=[rs], writes=[rs])
                        dst = qn if do_rope else fo
                        S.op("dve", lambda e: e.scalar_tensor_tensor(out=dst[:, 0:ntok], in0=pa[:, 0:ntok], scalar=gn[:, gcol:gcol + 1],
                                                                     in1=rs[:, 0:ntok], op0=ALU.mult, op1=ALU.mult),
                             reads=[pa, gn, rs], writes=[dst])
                    else:
                        S.op("act", lambda e: e.activation(out=qn[:, 0:ntok], in_=pa[:, 0:ntok], func=AF.Copy), reads=[pa], writes=[qn])
                    if do_rope:
                        pb2 = bB.next()
                        t1 = t1_r.next()
                        t2 = t2_r.next()
                        S.op("pe", lambda e: e.matmul(pb2[:, 0:ntok], lhsT=perm[:], rhs=qn[:, 0:ntok], start=True, stop=True), reads=[perm, qn], writes=[pb2])
                        S.op("pool", lambda e: e.tensor_tensor(out=t1[:, 0:ntok], in0=qn[:, 0:ntok], in1=tC[:, 0:ntok], op=ALU.mult), reads=[qn, tC], writes=[t1])
                        S.op("dve", lambda e: e.tensor_tensor(out=t2[:, 0:ntok], in0=pb2[:, 0:ntok], in1=tS[:, 0:ntok], op=ALU.mult), reads=[pb2, tS], writes=[t2])
                        S.op("dve", lambda e: e.tensor_tensor(out=fo[:, 0:ntok], in0=t1[:, 0:ntok], in1=t2[:, 0:ntok], op=ALU.add), reads=[t1, t2], writes=[fo])
                S.dma("sp", fmS[i, :, u0:u0 + ntok], fo[:, 0:ntok], reads=[fo], writes=[fm_reg], sem_tile=fo)
            for name in tm_list:
                col0 = tmoff[name]
                ncols = dict((t[0], t[2]) for t in tm)[name]
                for ti in range(ntiles):
                    for n0 in range(0, ncols, 512):
                        nn = min(512, ncols - n0)
                        pa = bA.next()
                        for k in range(8):
                            S.op("pe", lambda e: e.matmul(pa[:, 0:nn], lhsT=hT[:, k, ti * 128:(ti + 1) * 128],
                                                          rhs=win[:, k, col0 + n0:col0 + n0 + nn], start=(k == 0), stop=(k == 7)),
                                 reads=[win, hT], writes=[pa])
                        if name == "z":
                            if n0 == 0:
                                zst = zst_r.next()
                            S.op("act", lambda e: e.activation(out=zst[:, n0:n0 + nn], in_=pa[:, 0:nn], func=AF.Silu), reads=[pa], writes=[zst])
                            if n0 + nn == ncols:
                                S.dma("sp", zS[u0 + ti * 128:u0 + (ti + 1) * 128, :], zst[:], reads=[zst], writes=[z_reg], sem_tile=zst)
                        else:
                            h, d = vdims[name]
                            dv = d - 1
                            vt = vst[name].next()
                            S.op("dve", lambda e: e.tensor_copy(out=vt[:, :, 0:dv], in_=pa[:, 0:nn].rearrange("p (h d) -> p h d", d=dv)),
                                 reads=[pa], writes=[vt])
                            S.dma("sp", vS[name][u0 + ti * 128:u0 + (ti + 1) * 128, :], vt[:].rearrange("p h d -> p (h d)"),
                                  reads=[vt], writes=[v_reg], sem_tile=vt)


        all_fm = list(range(NFM))
        all_tm = [t[0] for t in tm]
        lk = [fmi[n] for n in cfg["local_k"]]
        dk = [fmi[n] for n in cfg["dense_k"]]
        xc_ap = xc_in
        phase1_block(U_C, 2, xc_ap, 0, 1, all_fm, all_tm, None)
        eblocks = list(range(EXT // 512))
        eblocks = [b for b in eblocks if HALO <= b * 512 < HALO + HALF] + [b for b in eblocks if not (HALO <= b * 512 < HALO + HALF)]
        for b in eblocks:
            u0 = b * 512
            own = HALO <= u0 < HALO + HALF
            if own:
                phase1_block(u0, 4, x_u, u0, 0, all_fm, all_tm, u0)
            else:
                phase1_block(u0, 4, x_u, u0, 0, lk, cfg["local_v"], u0)
        for b in range(HALF // 512):
            u0 = U_O + b * 512
            phase1_block(u0, 4, x_u, u0, 0, dk, cfg["dense_v"], u0)

        S.barrier()
        st1.close()
        st2 = contextlib.ExitStack()
        S.stack = st2
        wout = S.sb("wout", [128, 8, D_MODEL], BF16)
        stage2 = Rot([S.sb("stage2_%d" % i, [128, D_MODEL], F32) for i in range(2)])
        for k in range(8):
            stg = stage2.next()
            S.dma("sp", stg[:], w_out[k], writes=[stg])
            S.op("pool" if k % 2 else "dve", lambda e: e.tensor_copy(out=wout[:, k, :], in_=stg[:]), reads=[stg], writes=[wout])
        bS = Rot([banks[1], banks[2], banks[3]])
        bACC = Rot([banks[4], banks[5], banks[6], banks[7]])
        pT_r = Rot([S.sb("pT%d" % i, [128, 512], BF16) for i in range(3)])
        sb_r = Rot([S.sb("sbias%d" % i, [128, 512], F32) for i in range(2)])
        nbuf_d = 1 if layer == 0 else 2
        KT_r = Rot([S.sb("KT%d" % i, [128, NKD], BF16) for i in range(nbuf_d)])
        VDW = 130 if layer == 0 else 129
        VD_r = Rot([S.sb("VD%d" % i, [128, NKC, VDW], BF16) for i in range(nbuf_d)])
        q_r = Rot([S.sb("qblk%d" % i, [128, 512], BF16) for i in range(3)])
        NLK = 9
        kl_r = Rot([S.sb("kl%d" % i, [128, NLK * 128], BF16) for i in range(2)])
        VLW = 130
        vl_r = Rot([S.sb("vl%d" % i, [128, NLK, VLW], BF16) for i in range(2)])
        y_t = [S.sb("y%d" % i, [128, D_MODEL], F32) for i in range(4)]
        rec_r = Rot([S.sb("rec%d" % i, [128, 8], F32) for i in range(4)])
        zl_r = Rot([S.sb("zl%d" % i, [128, D_MODEL], BF16) for i in range(1)])
        yb_r = Rot([S.sb("yb%d" % i, [128, D_MODEL], BF16) for i in range(1)])
        yT_r = Rot([S.sb("yT%d" % i, [128, 8, 128], BF16) for i in range(1)])
        xo_r = Rot([S.sb("xo%d" % i, [128, D_MODEL], F32) for i in range(1)])
        res_r = Rot([S.sb("res%d" % i, [128, D_MODEL], F32) for i in range(2)])
        be_r = Rot([S.sb("be%d" % i, [128, 7 * 128], F32) for i in range(2)]) if layer == 0 else None
        dcache = {}

        def load_dense(kname, vname, vc0, vw):
            key = (kname, vname, vc0)
            if dcache.get("key") == key:
                return dcache["KT"], dcache["VD"]
            KT = KT_r.next()
            VD = VD_r.next()
            i = fmi[kname]
            S.dma("sp", KT[:, 0:HALF], fmS[i, :, U_OWN:U_OWN + HALF], reads=[fm_reg], writes=[KT])
            S.dma("sp", KT[:, HALF:2 * HALF], fmS[i, :, U_O:U_O + HALF], reads=[fm_reg], writes=[KT])
            S.dma("sp", KT[:, 2 * HALF:NKD], fmS[i, :, U_C:U_C + CTX], reads=[fm_reg], writes=[KT])
            for (c0, u0, n) in ((0, U_OWN, HALF), (HALF // 128, U_O, HALF), (2 * HALF // 128, U_C, CTX)):
                S.dma("sp", VD[:, c0:c0 + n // 128, 0:vw], vS[vname][u0:u0 + n, vc0:vc0 + vw].rearrange("(k p) c -> p k c", p=128),
                      reads=[v_reg], writes=[VD])
            dcache.update(key=key, KT=KT, VD=VD)
            return KT, VD

        def attend(qt, pbase, NQ, kchunks, acc_list, vwidth, bias_fn=None):
            nq = NQ // 128
            n = len(kchunks)
            pend = []
            first = {}

            def issue_s(j):
                kt, kc0, vt, vap = kchunks[j]
                ps = bS.next()
                S.op("pe", lambda e: e.matmul(ps[:, 0:NQ], lhsT=kt[pbase:pbase + 64, kc0:kc0 + 128], rhs=qt[pbase:pbase + 64, 0:NQ],
                                              start=True, stop=True), reads=[kt, qt], writes=[ps])
                pT = pT_r.next()
                b = bias_fn(j) if bias_fn is not None else None
                if b is not None:
                    btile, bap = b
                    sb = sb_r.next()
                    S.op("dve", lambda e: e.scalar_tensor_tensor(out=sb[:, 0:NQ], in0=ps[:, 0:NQ], scalar=SCALE, in1=bap,
                                                                 op0=ALU.mult, op1=ALU.add), reads=[ps, btile], writes=[sb])
                    S.op("act", lambda e: e.activation(out=pT[:, 0:NQ], in_=sb[:, 0:NQ], func=AF.Exp), reads=[sb], writes=[pT])
                else:
                    S.op("act", lambda e: e.activation(out=pT[:, 0:NQ], in_=ps[:, 0:NQ], func=AF.Exp, scale=SCALE), reads=[ps], writes=[pT])
                return pT

            def issue_pv(j, pT):
                kt, kc0, vt, vap = kchunks[j]
                for s in range(nq):
                    acc, c0 = acc_list[s]
                    fst = first.get(id(acc), True)
                    first[id(acc)] = False
                    S.op("pe", lambda e: e.matmul(acc[:, c0:c0 + vwidth], lhsT=pT[:, s * 128:(s + 1) * 128], rhs=vap,
                                                  start=(j == 0 and fst), stop=(j == n - 1), skip_group_check=True),
                         reads=[pT, vt], writes=[acc])

            prev = None
            for j in range(n):
                pT = issue_s(j)
                if prev is not None:
                    issue_pv(prev[0], prev[1])
                prev = (j, pT)
            issue_pv(prev[0], prev[1])

        onesel = S.sb("onesel", [128, 2, 2], BF16)
        S.op("pool", lambda e: e.memset(onesel[:], 0.0), writes=[onesel])
        S.op("pool", lambda e: e.memset(onesel[:, 0, 0:1], 1.0), writes=[onesel])
        S.op("pool", lambda e: e.memset(onesel[:, 1, 1:2], 1.0), writes=[onesel])
        pT2_r = Rot([S.sb("pT2_%d" % i, [128, 1024], BF16) for i in range(3)])
        oT_r = Rot([S.sb("oT%d" % i, [128, 512], F32) for i in range(2)])
        pairs = SH["pairs"]

        def dense_pair(qt, KT, VD, vap_fn, vw, accs, den=None):
            n = NKC

            def issue_s(j):
                pt, ta, tb = pairs[j % 2]
                for m, tt in ((0, ta), (1, tb)):
                    S.op("pe", lambda e: e.matmul(tt[:, 0:512], lhsT=KT[64 * m:64 * m + 64, j * 128:(j + 1) * 128],
                                                  rhs=qt[64 * m:64 * m + 64, 0:512], start=True, stop=True), reads=[KT, qt], writes=[tt])
                pT = pT2_r.next()
                S.op("act", lambda e: e.activation(out=pT[:], in_=pt[:, :], func=AF.Exp, scale=SCALE), reads=[ta, tb], writes=[pT])
                return pT

            def issue_pv(j, pT):
                for m in range(2):
                    S.op("pe", lambda e: e.matmul(accs[m][0:vw, 0:512], lhsT=vap_fn(j, m), rhs=pT[:, m * 512:(m + 1) * 512],
                                                  start=(j == 0), stop=(j == n - 1)), reads=[pT, VD], writes=[accs[m]])
                    if den is not None:
                        S.op("pe", lambda e: e.matmul(den[0:2, 0:512], lhsT=onesel[:, m, :], rhs=pT[:, m * 512:(m + 1) * 512],
                                                      start=(j == 0 and m == 0), stop=(j == n - 1 and m == 1)), reads=[pT, onesel], writes=[den])

            prev = None
            for j in range(n):
                pT = issue_s(j)
                if prev is not None:
                    issue_pv(prev[0], prev[1])
                prev = (j, pT)
            issue_pv(prev[0], prev[1])

        def untranspose(acc, rows, fin, width):
            oT = oT_r.next()
            S.op("act", lambda e: e.activation(out=oT[0:rows, :], in_=acc[0:rows, 0:512], func=AF.Copy), reads=[acc], writes=[oT])
            for sidx in range(4):
                S.op("pe", lambda e: e.transpose(out=fin[:, sidx * width:sidx * width + rows], in_=oT[0:rows, sidx * 128:(sidx + 1) * 128],
                                                 identity=ident_f[0:rows, 0:rows]), reads=[oT, ident_f], writes=[fin])

        def finish_head(acc_list, vwidth, y_tiles, ycol, extra_den=None, scale_ap=None):
            dv = vwidth - 1
            for s, (acc, c0) in enumerate(acc_list):
                rec = rec_r.next()
                if extra_den is not None:
                    S.op("dve", lambda e: e.tensor_tensor(out=rec[:, 0:1], in0=acc[:, c0 + dv:c0 + dv + 1], in1=extra_den, op=ALU.add),
                         reads=[acc, esnk], writes=[rec])
                    S.op("dve", lambda e: e.reciprocal(out=rec[:, 1:2], in_=rec[:, 0:1]), reads=[rec], writes=[rec])
                else:
                    S.op("dve", lambda e: e.reciprocal(out=rec[:, 1:2], in_=acc[:, c0 + dv:c0 + dv + 1]), reads=[acc], writes=[rec])
                yt = y_tiles[s]
                S.op("act", lambda e: e.activation(out=yt[:, ycol:ycol + dv], in_=acc[:, c0:c0 + dv], func=AF.Copy, scale=rec[:, 1:2]),
                     reads=[acc, rec], writes=[yt])

        def out_tile(yt, u_tok, src, src_row, w, dst, dst_row):
            zl = zl_r.next()
            yb = yb_r.next()
            yT = yT_r.next()
            xo = xo_r.next()
            res = res_r.next()
            S.dma("sp", zl[:], zS[u_tok:u_tok + 128, :], reads=[z_reg], writes=[zl])
            load_x_tile(xo, u_tok // 128)
            S.op("dve", lambda e: e.tensor_tensor(out=yb[:], in0=yt[:], in1=zl[:], op=ALU.mult), reads=[yt, zl], writes=[yb])
            for k in range(8):
                S.op("pe", lambda e: e.transpose(out=pTb[:, k, :], in_=yb[:, k * 128:(k + 1) * 128], identity=ident[:]),
                     reads=[yb, ident], writes=[pTb_t])
            S.op("act", lambda e: e.activation(out=yT[:].rearrange("p k t -> p (k t)"), in_=pTb[:].rearrange("p k t -> p (k t)"), func=AF.Copy),
                 reads=[pTb_t], writes=[yT])
            for n in range(2):
                po = bACC.next()
                for k in range(8):
                    S.op("pe", lambda e: e.matmul(po[:], lhsT=yT[:, k, :], rhs=wout[:, k, n * 512:(n + 1) * 512], start=(k == 0), stop=(k == 7)),
                         reads=[yT, wout], writes=[po])
                S.op("dve", lambda e: e.tensor_tensor(out=res[:, n * 512:(n + 1) * 512], in0=po[:], in1=gate_bc[:, w, n * 512:(n + 1) * 512], op=ALU.mult),
                     reads=[po, gate_bc], writes=[res])
            S.op("pool", lambda e: e.tensor_tensor(out=res[:], in0=res[:], in1=xo[:], op=ALU.add), reads=[res, xo], writes=[res])
            if last:
                ss = ss_r.next()
                S.op("act", lambda e: e.activation(out=junk[:], in_=res[:], func=AF.Square, accum_out=ss[:, 0:1]), reads=[res], writes=[junk, ss])
                S.op("act", lambda e: e.activation(out=ss[:, 1:2], in_=ss[:, 0:1], func=AF.Ln, scale=1.0 / D_MODEL, bias=epst[:]), reads=[ss, epst], writes=[ss])
                S.op("act", lambda e: e.activation(out=ss[:, 1:2], in_=ss[:, 1:2], func=AF.Exp, scale=-0.5), reads=[ss], writes=[ss])
                S.op("act", lambda e: e.activation(out=xo[:], in_=res[:], func=AF.Copy, scale=ss[:, 1:2]), reads=[res, ss], writes=[xo])
                S.op("dve", lambda e: e.tensor_tensor(out=res[:], in0=xo[:], in1=fn_t[:], op=ALU.mult), reads=[xo, fn_t], writes=[res])
            S.dma("sp", dst[dst_row:dst_row + 128, :], res[:], reads=[res], sem_tile=res)
            return res

        out_tiles = []

        def load_q(name, u0, n):
            qt = q_r.next()
            S.dma("sp", qt[:, 0:n], fmS[fmi[name], :, u0:u0 + n], reads=[fm_reg], writes=[qt])
            return qt

        def load_local(knames_idx, vname, vc0, vw, utiles):
            kl = kl_r.next()
            vl = vl_r.next()
            pos = 0
            for (ut0, cnt) in utiles:
                S.dma("sp", kl[:, pos * 128:(pos + cnt) * 128], fmS[knames_idx, :, ut0 * 128:(ut0 + cnt) * 128], reads=[fm_reg], writes=[kl])
                S.dma("sp", vl[:, pos:pos + cnt, 0:vw], vS[vname][ut0 * 128:(ut0 + cnt) * 128, vc0:vc0 + vw].rearrange("(k p) c -> p k c", p=128),
                      reads=[v_reg], writes=[vl])
                pos += cnt
            return kl, vl

        UC_T = U_C // 128

        if layer == 0:
            for t in range(2):
                yt = y_t[t]
                u0 = U_C + t * 128
                kl, vl = load_local(fmi["ka"], "va", 0, 130, [(UC_T, 2)])
                for c in range(4):
                    qt = load_q("qa%d" % c, u0, 128)
                    for s in range(2):
                        head = c + 4 * s
                        acc = bACC.next()
                        attend(qt, 64 * s, 128, [(kl, j * 128, vl, vl[:, j, s * 65:(s + 1) * 65]) for j in range(2)], [(acc, 0)], 65)
                        finish_head([(acc, 0)], 65, [yt], head * 64)
                for c in range(4):
                    kl, vl = load_local(fmi["kb%d" % c], "vb", c * 130, 130, [(UC_T, 2)])
                    qt = load_q("qb%d" % c, u0, 128)
                    for s in range(2):
                        head = 2 * c + s
                        acc = bACC.next()
                        attend(qt, 64 * s, 128, [(kl, j * 128, vl, vl[:, j, s * 65:(s + 1) * 65]) for j in range(2)], [(acc, 0)], 65)
                        finish_head([(acc, 0)], 65, [yt], 512 + head * 64)
                out_tiles.append(out_tile(yt, u0, xc_in, t * 128, 1, out_c, t * 128))

        for qb in range(NBo):
            u0 = U_OWN + qb * 512
            if layer == 0:
                KT, VD = load_dense("ka", "va", 0, 130)
                for c in range(4):
                    qt = load_q("qa%d" % c, u0, 512)
                    accs = [banks[5], banks[6]]
                    dense_pair(qt, KT, VD, lambda j, m: VD[:, j, m * 65:(m + 1) * 65], 65, accs)
                    for s in range(2):
                        head = c + 4 * s
                        fin = banks[7]
                        untranspose(accs[s], 65, fin, 65)
                        finish_head([(fin, i * 65) for i in range(4)], 65, y_t, head * 64)
            else:
                for h in range(4):
                    KT, VD = load_dense("kc%d" % h, "vc", h * 129, 129)
                    qt = load_q("qc%d" % h, u0, 512)
                    accs = [banks[5], banks[6]]
                    den = banks[7]
                    dense_pair(qt, KT, VD, lambda j, m: VD[:, j, 0:128], 128, accs, den=den)
                    fins = [banks[1], banks[2]]
                    untranspose(accs[0], 128, fins[0], 128)
                    untranspose(accs[1], 128, fins[1], 128)
                    dfin = banks[3]
                    untranspose(den, 2, dfin, 2)
                    o_m = [[(fins[0], i * 128, i * 2 + 0) for i in range(4)], [(fins[1], i * 128, i * 2 + 1) for i in range(4)]]
                    for s in range(4):
                        rec = rec_r.next()
                        yt = y_t[s]
                        a0, c0, d0 = o_m[0][s]
                        a1, c1, d1 = o_m[1][s]
                        S.op("dve", lambda e: e.reciprocal(out=rec[:, 0:1], in_=dfin[:, d0:d0 + 1]), reads=[dfin], writes=[rec])
                        S.op("dve", lambda e: e.reciprocal(out=rec[:, 1:2], in_=dfin[:, d1:d1 + 1]), reads=[dfin, rec], writes=[rec])
                        S.op("dve", lambda e: e.tensor_tensor(out=rec[:, 2:3], in0=rec[:, 1:2], in1=lam_s[:, 3:4], op=ALU.mult), reads=[rec, lam_s], writes=[rec])
                        t1 = t1_r.next()
                        S.op("act", lambda e: e.activation(out=t1[:, 0:128], in_=a0[:, c0:c0 + 128], func=AF.Copy, scale=rec[:, 0:1]), reads=[a0, rec], writes=[t1])
                        S.op("dve", lambda e: e.scalar_tensor_tensor(out=t1[:, 128:256], in0=a1[:, c1:c1 + 128], scalar=rec[:, 2:3], in1=t1[:, 0:128],
                                                                     op0=ALU.mult, op1=ALU.add), reads=[a1, rec, t1], writes=[t1])
                        S.op("act", lambda e: e.activation(out=t1[:, 256:384], in_=t1[:, 128:256], func=AF.Square, accum_out=rec[:, 3:4]), reads=[t1], writes=[t1, rec])
                        S.op("act", lambda e: e.activation(out=rec[:, 4:5], in_=rec[:, 3:4], func=AF.Ln, scale=1.0 / 128, bias=epst[:]), reads=[rec, epst], writes=[rec])
                        S.op("act", lambda e: e.activation(out=rec[:, 4:5], in_=rec[:, 4:5], func=AF.Exp, scale=-0.5), reads=[rec], writes=[rec])
                        S.op("dve", lambda e: e.tensor_scalar_mul(out=rec[:, 4:5], in0=rec[:, 4:5], scalar1=1.0 - lam0), reads=[rec], writes=[rec])
                        S.op("dve", lambda e: e.scalar_tensor_tensor(out=yt[:, h * 128:(h + 1) * 128], in0=t1[:, 128:256], scalar=rec[:, 4:5], in1=sub_t[:],
                                                                     op0=ALU.mult, op1=ALU.mult), reads=[t1, rec, sub_t], writes=[yt])
            for tl in range(4):
                t = qb * 4 + tl
                ut = U_OWN // 128 + t
                yt = y_t[tl]
                if layer == 0:
                    edge = t < 2 or t >= NTo - 2
                    if edge:
                        et = t if t < 2 else 2 + (t - (NTo - 2))
                        J = 7
                        runs = [(ut - 3, 7), (UC_T, 2)]
                    else:
                        J = 5
                        runs = [(ut - 2, 5), (UC_T, 2)]
                    for c in range(4):
                        kl, vl = load_local(fmi["kb%d" % c], "vb", c * 130, 130, runs)
                        qt = load_q("qb%d" % c, ut * 128, 128)
                        for s in range(2):
                            head = 2 * c + s
                            if edge:
                                be = be_r.next()
                                S.dma("sp", be[:], bias_e[et, head], writes=[be])
                                bfn = (lambda j, be=be: (be, be[:, j * 128:(j + 1) * 128]) if j < 7 else None)
                            else:
                                bfn = (lambda j, head=head: (bi_t, bi_t[:, head, j * 128:(j + 1) * 128]) if j < 5 else None)
                            acc = bACC.next()
                            attend(qt, 64 * s, 128, [(kl, j * 128, vl, vl[:, j, s * 65:(s + 1) * 65]) for j in range(J + 2)],
                                   [(acc, 0)], 65, bias_fn=bfn)
                            finish_head([(acc, 0)], 65, [yt], 512 + head * 64)
                else:
                    kl, vl = load_local(fmi["kd"], "vd", 0, 130, [(ut - 1, 3), (UC_T, 2)])
                    for c in range(4):
                        qt = load_q("qd%d" % c, ut * 128, 128)
                        for s in range(2):
                            head = c + 4 * s

                            def bfn(j, t=t):
                                if j == 0:
                                    mi = 0 if t == 0 else 2
                                    return (dm_t, dm_t[:, mi, :])
                                if j == 2:
                                    mi = 1 if t == NTo - 1 else 3
                                    return (dm_t, dm_t[:, mi, :])
                                return None
                            acc = bACC.next()
                            attend(qt, 64 * s, 128, [(kl, j * 128, vl, vl[:, j, s * 65:(s + 1) * 65]) for j in range(5)],
                                   [(acc, 0)], 65, bias_fn=bfn)
                            finish_head([(acc, 0)], 65, [yt], 512 + head * 64, extra_den=esnk[:, head:head + 1])
                out_tiles.append(out_tile(yt, ut * 128, x_u, ut * 128, 0, out_x, t * 128))
        if last:
            S.finish(out_tiles)
        else:
            S.barrier()
        st2.close()
    if not last:
        S.release_dsems()
        S.new_epoch()


def build_fused(SEQ, B):
    HALF = SEQ // 2
    EXT = HALF + 2 * HALO
    nc = bass.Bass("TRN2", target_bir_lowering=False)

    def din(name, shape):
        return nc.dram_tensor(name, list(shape), F32, kind="ExternalInput").ap()

    SH = dict(x_u=din("x_u", [EXT + HALF, D_MODEL]), xc=din("xc", [CTX, D_MODEL]), cvec=din("cvec", [128, 8, 2]),
              ident=din("ident", [128, 128]), blk=din("blk", [128, 128]), perm=din("perm", [128, 128]),
              ropeC=din("ropeC", [128, EXT + HALF]), ropeS=din("ropeS", [128, EXT + HALF]), sel=din("sel", [128, 4]))
    SH["x1_loc"] = nc.dram_tensor("x1_loc", [HALF, D_MODEL], F32).ap()
    SH["xc1_loc"] = nc.dram_tensor("xc1_loc", [CTX, D_MODEL], F32).ap()
    SH["GA"] = nc.dram_tensor("x1_all", [2 * HALF, D_MODEL], F32).ap()
    with contextlib.ExitStack() as st0:
        S = Sched(nc, st0)
        SH["pTb_t"] = S.ps("bankT", [128, 8, 128], BF16)
        pairA = st0.enter_context(nc.psum_tensor("ps_pairA", [128, 1024], F32))
        pairB = st0.enter_context(nc.psum_tensor("ps_pairB", [128, 1024], F32))
        b1 = Tile("bank1", pairA[:, 0:512], excl=True)
        b2 = Tile("bank2", pairA[:, 512:1024], excl=True)
        b3 = Tile("bank3", pairB[:, 0:512], excl=True)
        b4 = Tile("bank4", pairB[:, 512:1024], excl=True)
        SH["pairs"] = [(pairA, b1, b2), (pairB, b3, b4)]
        SH["banks"] = [SH["pTb_t"], b1, b2, b3, b4] + [S.ps("bank%d" % i, [128, 512], F32) for i in range(5, 8)]
        SH["G_reg"] = Tile("G_reg")
        emit_layer(nc, S, SH, 0, SEQ)
        S.sems["cc"] = st0.enter_context(nc.semaphore("s_cc"))
        RC = min(512, HALF)
        for i in range(HALF // RC):
            nc.gpsimd.collective_compute("AllGather", ALU.bypass, replica_groups=[[2 * b, 2 * b + 1] for b in range(B)],
                                         ins=[SH["x1_loc"][i * RC:(i + 1) * RC, :]],
                                         outs=[SH["GA"][i * 2 * RC:(i + 1) * 2 * RC, :]]).then_inc(S.sems["cc"], 1)
        SH["G_reg"].last_w = ("cc", HALF // RC)
        emit_layer(nc, S, SH, 1, SEQ)
    return nc


def rope_tables(pos):
    row = (pos // GRID_W).astype(np.float32)
    col = (pos % GRID_W).astype(np.float32)
    q = HD // 4
    inv = (10000.0 ** (-np.arange(q, dtype=np.float32) / q)).astype(np.float32)
    ar = row[None, :] * inv[:, None]
    ac = col[None, :] * inv[:, None]
    cr, sr, cc, sc = np.cos(ar), np.sin(ar), np.cos(ac), np.sin(ac)
    C = np.concatenate([cr, cr, cc, cc], axis=0)
    Sg = np.concatenate([-sr, sr, -sc, sc], axis=0)
    return (np.concatenate([C, C], 0).astype(np.float32), np.concatenate([Sg, Sg], 0).astype(np.float32))


def nbr_bias(rpb, g, gk, NT):
    rows = NT * 2
    out = np.full((8, 128, 128), NEG, np.float32)
    if gk < 0 or gk >= NT:
        return out
    ql = np.arange(128)
    r = 2 * g + ql // 64
    c = ql % 64
    kr = 2 * gk + ql // 64
    kc = ql % 64
    win_r = min(8, rows)
    rs = np.clip(r - win_r // 2, 0, rows - win_r)
    cs = np.clip(c - 8, 0, GRID_W - 16)
    valid = ((kr[:, None] >= rs[None, :]) & (kr[:, None] < rs[None, :] + win_r)
             & (kc[:, None] >= cs[None, :]) & (kc[:, None] < cs[None, :] + 16))
    di = kr[:, None] - r[None, :] + 7
    dj = kc[:, None] - c[None, :] + 15
    di = np.clip(di, 0, 14)
    dj = np.clip(dj, 0, 30)
    vals = rpb[:, di, dj]
    return np.where(valid[None], vals, np.float32(NEG)).astype(np.float32)


def chunk_rows(w):
    return np.ascontiguousarray(w.reshape(8, 128, w.shape[1]))


def prep_layer_inputs(layer, SEQ, xs, xcs, p):
    B = xs.shape[0]
    HALF = SEQ // 2
    EXT = HALF + 2 * HALO
    NT = SEQ // 128
    NTo = HALF // 128
    cfg = layer_cfg(layer)
    wi = p["w_in_even"][0] if layer == 0 else p["w_in_odd"][0]
    wo = p["w_out_even"][0] if layer == 0 else p["w_out_odd"][0]
    cols = []
    for f in cfg["fm"]:
        cols += f[1]
    for name, c0, n in cfg["tm"]:
        cols += list(range(c0, c0 + n))
    w_in_l = chunk_rows(np.ascontiguousarray(wi[:, cols]))
    w_out_l = chunk_rows(wo)
    w_mod_l = chunk_rows(p["w_mod"][layer])
    b_mod_l = np.ascontiguousarray(p["b_mod"][layer].reshape(24, 128).T)
    bgate = np.ascontiguousarray(np.broadcast_to(p["b_mod"][layer][2048:3072][None, :], (128, D_MODEL)))
    ident = np.eye(128, dtype=np.float32)
    blk = np.zeros((128, 128), np.float32)
    blk[:64, :64] = 1.0 / 64
    blk[64:, 64:] = 1.0 / 64
    perm = np.zeros((128, 128), np.float32)
    for m in range(128):
        k = m + 16 if (m % 32) < 16 else m - 16
        perm[k, m] = 1.0
    maps = []
    for b in range(B):
        for half in range(2):
            T0 = half * HALF
            pos_e = np.arange(T0 - HALO, T0 + HALF + HALO)
            valid_e = (pos_e >= 0) & (pos_e < SEQ)
            x_e = np.zeros((EXT, D_MODEL), np.float32)
            x_e[valid_e] = xs[b, pos_e[valid_e]]
            T1 = (1 - half) * HALF
            pos_o = np.arange(T1, T1 + HALF)
            x_u = np.concatenate([x_e, xs[b, pos_o]], axis=0)
            pos_u = np.concatenate([np.clip(pos_e, 0, SEQ - 1), pos_o])
            C, Sg = rope_tables(pos_u)
            cvec = np.stack([p["c"][b].reshape(8, 128).T, p["c_ctx"].reshape(8, 128).T], axis=-1)
            m = dict(x_u=x_u, xc=np.ascontiguousarray(xcs[b]), cvec=np.ascontiguousarray(cvec), w_mod=w_mod_l, b_mod=b_mod_l,
                     bgate=bgate, w_in=w_in_l, w_out=w_out_l, ident=ident, blk=blk, perm=perm, ropeC=C, ropeS=Sg)
            G0 = T0 // 128
            if layer == 0:
                m["gains"] = np.ascontiguousarray(np.stack([np.tile(p["a_q_norm"][0], 2), np.tile(p["a_k_norm"][0], 2)], axis=-1))
                rpb = p["b_rpb"][0]
                gi = min(max(G0 + 2, 2), NT - 3) if NT >= 6 else 0
                bi = np.stack([nbr_bias(rpb, gi, gi + j, NT) for j in range(-2, 3)], axis=0)
                m["bias_i"] = np.ascontiguousarray(bi.transpose(2, 1, 0, 3).reshape(128, 8, 640))
                ets = [0, 1, NTo - 2, NTo - 1]
                be = np.stack([np.stack([nbr_bias(rpb, G0 + t, G0 + t + j, NT) for j in range(-3, 4)], axis=0) for t in ets], axis=0)
                m["bias_e"] = np.ascontiguousarray(be.transpose(0, 2, 3, 1, 4).reshape(4, 8, 128, 896))
            else:
                a = np.arange(128)
                tri_prev = np.where(a[:, None] >= a[None, :], 0.0, NEG).astype(np.float32)
                tri_next = np.where(a[:, None] <= a[None, :], 0.0, NEG).astype(np.float32)
                full = np.full((128, 128), NEG, np.float32)
                first_prev = full if G0 == 0 else tri_prev
                last_next = full if G0 + NTo == NT else tri_next
                m["dmask"] = np.ascontiguousarray(np.stack([first_prev, last_next, tri_prev, tri_next], axis=1))
                m["lamv"] = np.ascontiguousarray(np.broadcast_to(p["c_lambda"][0].reshape(1, 256), (128, 256)))
                m["subln"] = np.ascontiguousarray(np.broadcast_to((p["c_subln"][0])[None, :], (128, 128)))
                m["sinks"] = np.ascontiguousarray(np.broadcast_to(p["d_sinks"][0][None, :], (128, 8)))
                m["fnorm"] = np.ascontiguousarray(np.broadcast_to(p["final_norm"][None, :], (128, D_MODEL)))
            maps.append(m)
    return maps


def prep_fused_inputs(SEQ, xs, xcs, p):
    m0 = prep_layer_inputs(0, SEQ, xs, xcs, p)
    m1 = prep_layer_inputs(1, SEQ, xs, xcs, p)
    shared = ("x_u", "xc", "cvec", "ident", "blk", "perm", "ropeC", "ropeS")
    maps = []
    for i, (a, b) in enumerate(zip(m0, m1)):
        half = i % 2
        m = {k: a[k] for k in shared}
        for k, v in a.items():
            if k not in shared:
                m["l0_" + k] = v
        for k, v in b.items():
            if k not in shared:
                m["l1_" + k] = v
        sel = np.zeros((128, 4), np.float32)
        sel[:, 0] = 1.0 if half == 1 else 0.0
        sel[:, 1] = 1.0 if half == 0 else 0.0
        sel[:, 2] = 1.0 if half == 1 else 0.0
        sel[:, 3] = 1.0 if half == 0 else 0.0
        m["sel"] = sel
        maps.append(m)
    return maps


def run_fused(SEQ, xs, xcs, p, runner=None):
    B = xs.shape[0]
    key = ("fused", SEQ, B)
    if key not in _NC_CACHE:
        _NC_CACHE[key] = build_fused(SEQ, B)
    nc = _NC_CACHE[key]
    maps = prep_fused_inputs(SEQ, xs, xcs, p)
    if runner is None:
        res = run_bass_kernel_spmd(nc, maps, core_ids=list(range(len(maps)))).results
    else:
        res = runner(nc, maps)
    HALF = SEQ // 2
    xo = np.zeros_like(xs)
    for b in range(B):
        for half in range(2):
            xo[b, half * HALF:(half + 1) * HALF] = res[2 * b + half]["out_x"]
    return xo


_NC_CACHE = {}


def kernel(x, c, ctx, c_ctx, w_mod, b_mod, w_in_even, w_out_even, a_q_norm, a_k_norm, b_rpb,
           w_in_odd, w_out_odd, c_lambda, c_subln, d_sinks, final_norm):
    p = dict(c=np.asarray(c, np.float32), c_ctx=np.asarray(c_ctx, np.float32), w_mod=np.asarray(w_mod, np.float32),
             b_mod=np.asarray(b_mod, np.float32), w_in_even=np.asarray(w_in_even, np.float32),
             w_out_even=np.asarray(w_out_even, np.float32), a_q_norm=np.asarray(a_q_norm, np.float32),
             a_k_norm=np.asarray(a_k_norm, np.float32), b_rpb=np.asarray(b_rpb, np.float32),
             w_in_odd=np.asarray(w_in_odd, np.float32), w_out_odd=np.asarray(w_out_odd, np.float32),
             c_lambda=np.asarray(c_lambda, np.float32), c_subln=np.asarray(c_subln, np.float32),
             d_sinks=np.asarray(d_sinks, np.float32), final_norm=np.asarray(final_norm, np.float32))
    xs = np.asarray(x, np.float32)
    xcs = np.asarray(ctx, np.float32)
    SEQ = xs.shape[1]
    return run_fused(SEQ, xs, xcs, p)
```

```python
import contextlib
import math
import numpy as np
import concourse.bass as bass
import concourse.mybir as mybir
from concourse.bass_utils import run_bass_kernel_spmd

F32 = mybir.dt.float32
BF16 = mybir.dt.bfloat16
AF = mybir.ActivationFunctionType
ALU = mybir.AluOpType

D_MODEL = 1024
CTX = 256
HD = 64
GRID_W = 64
SCALE = HD ** -0.5
EPS = 1e-6
NEG = -30000.0
HALO = 512


class Tile:
    __slots__ = ("name", "t", "last_w", "readers", "dsem", "dcount", "excl")

    def __init__(self, name, t=None, excl=False):
        self.name = name
        self.t = t
        self.last_w = None
        self.readers = {}
        self.dsem = None
        self.dcount = 0
        self.excl = excl

    def __getitem__(self, idx):
        return self.t[idx]


class Sched:
    def __init__(self, nc, stack):
        self.nc = nc
        self.stack = stack
        self.sem_stack = stack
        self.dtiles = []
        self.engs = {}
        self.sems = {}
        for en, e in (("pe", nc.tensor), ("act", nc.scalar), ("dve", nc.vector),
                      ("pool", nc.gpsimd), ("sp", nc.sync)):
            self.sems[en] = stack.enter_context(nc.semaphore("s_" + en))
            self.engs[en] = dict(eng=e, count=0, seen={}, key=en)
        self.epoch = 0
        self.nsem = 0
        self.prefix = ""
        self.free_dsems = []

    def sb(self, name, shape, dt):
        return Tile(name, self.stack.enter_context(self.nc.sbuf_tensor("sb_" + self.prefix + name, list(shape), dt)))

    def ps(self, name, shape, dt=F32):
        return Tile(name, self.stack.enter_context(self.nc.psum_tensor("ps_" + name, list(shape), dt)), excl=True)

    def _dsem(self, tile):
        if tile.dsem is None:
            if self.free_dsems:
                key, cnt = self.free_dsems.pop()
                tile.dsem = key
                tile.dcount = cnt
            else:
                key = "d%d" % self.nsem
                self.nsem += 1
                tile.dsem = key
                self.sems[key] = self.sem_stack.enter_context(self.nc.semaphore(key))
            self.dtiles.append(tile)
        return tile.dsem

    def new_epoch(self):
        self.epoch += 1
        for en, E in self.engs.items():
            key = "%s#%d" % (en, self.epoch)
            self.sems[key] = self.sem_stack.enter_context(self.nc.semaphore("s_%s_%d" % (en, self.epoch)))
            E["key"] = key
            E["count"] = 0

    def release_dsems(self):
        for t in self.dtiles:
            self.free_dsems.append((t.dsem, t.dcount))
            t.dsem = None
        self.dtiles = []

    def _wait_deps(self, en, reads, writes):
        E = self.engs[en]
        deps = {}

        def add(ev):
            if ev is None:
                return
            k, v = ev
            if deps.get(k, 0) < v:
                deps[k] = v

        me = E["key"]
        for t in reads:
            add(t.last_w)
            if t.excl:
                for k, v in t.readers.items():
                    if k != me:
                        add((k, v))
        for t in writes:
            add(t.last_w)
            for k, v in t.readers.items():
                if k != me:
                    add((k, v))
        for k, v in deps.items():
            if E["seen"].get(k, 0) < v:
                E["seen"][k] = v
                if k == me and en == "pe":
                    continue
                E["eng"].wait_ge(self.sems[k], v)

    def op(self, en, fn, reads=(), writes=()):
        E = self.engs[en]
        self._wait_deps(en, reads, writes)
        ins = fn(E["eng"])
        E["count"] += 1
        me = E["key"]
        ins.then_inc(self.sems[me], 1)
        for t in reads:
            t.readers[me] = E["count"]
        for t in writes:
            t.last_w = (me, E["count"])
            t.readers = {}
        return ins

    def dma(self, q, out, in_, reads=(), writes=(), sem_tile=None):
        E = self.engs[q]
        self._wait_deps(q, reads, writes)
        st = sem_tile if sem_tile is not None else (list(writes) + list(reads))[0]
        key = self._dsem(st)
        ins = E["eng"].dma_start(out=out, in_=in_)
        st.dcount += 16
        ins.then_inc(self.sems[key], 16)
        for t in reads:
            t.readers[key] = st.dcount
        for t in writes:
            t.last_w = (key, st.dcount)
            t.readers = {}
        return ins

    def barrier(self):
        for en, E in self.engs.items():
            for en2, E2 in self.engs.items():
                k2 = E2["key"]
                if en2 != en and E2["count"] and E["seen"].get(k2, 0) < E2["count"]:
                    E["seen"][k2] = E2["count"]
                    E["eng"].wait_ge(self.sems[k2], E2["count"])
            for t in self.dtiles:
                if t.dcount and E["seen"].get(t.dsem, 0) < t.dcount:
                    E["seen"][t.dsem] = t.dcount
                    E["eng"].wait_ge(self.sems[t.dsem], t.dcount)

    def finish(self, tiles, en="sp"):
        self._wait_deps(en, tiles, tiles)


class Rot:
    def __init__(self, tiles):
        self.tiles = tiles
        self.i = 0

    def next(self):
        t = self.tiles[self.i % len(self.tiles)]
        self.i += 1
        return t


def layer_cfg(layer):
    if layer == 0:
        qa, ka, va, qb, kb, vb, z = 0, 512, 640, 768, 1280, 1792, 2304
        fm = []
        for c in range(4):
            cols = list(range(qa + c * 64, qa + c * 64 + 64)) + list(range(qa + (4 + c) * 64, qa + (4 + c) * 64 + 64))
            fm.append(("qa%d" % c, cols, "q", True))
        fm.append(("ka", list(range(ka, ka + 128)), "k", True))
        for c in range(4):
            fm.append(("qb%d" % c, list(range(qb + c * 128, qb + c * 128 + 128)), None, False))
        for c in range(4):
            fm.append(("kb%d" % c, list(range(kb + c * 128, kb + c * 128 + 128)), None, False))
        tm = [("va", va, 128), ("vb", vb, 512), ("z", z, 1024)]
        dense_k, dense_v = ["ka"], ["va"]
        local_k, local_v = ["kb0", "kb1", "kb2", "kb3"], ["vb"]
    else:
        qc, kc, vc, qd, kd, vd, z = 0, 512, 1024, 1536, 2048, 2176, 2304
        fm = []
        for c in range(4):
            fm.append(("qc%d" % c, list(range(qc + c * 128, qc + c * 128 + 128)), None, True))
        for c in range(4):
            fm.append(("kc%d" % c, list(range(kc + c * 128, kc + c * 128 + 128)), None, True))
        for c in range(4):
            cols = list(range(qd + c * 64, qd + c * 64 + 64)) + list(range(qd + (4 + c) * 64, qd + (4 + c) * 64 + 64))
            fm.append(("qd%d" % c, cols, None, True))
        fm.append(("kd", list(range(kd, kd + 128)), None, True))
        tm = [("vc", vc, 512), ("vd", vd, 128), ("z", z, 1024)]
        dense_k, dense_v = ["kc0", "kc1", "kc2", "kc3"], ["vc"]
        local_k, local_v = ["kd"], ["vd"]
    return dict(fm=fm, tm=tm, dense_k=dense_k, dense_v=dense_v, local_k=local_k, local_v=local_v)


def lambda_init(layer):
    return 0.8 - 0.6 * math.exp(-0.3 * layer)


def emit_layer(nc, S, SH, layer, SEQ):
    HALF = SEQ // 2
    EXT = HALF + 2 * HALO
    NU = EXT + HALF + CTX
    NTo = HALF // 128
    NBo = HALF // 512
    U_OWN = HALO
    U_O = EXT
    U_C = EXT + HALF
    NKD = 2 * HALF + CTX
    NKC = NKD // 128
    last = layer == 1
    cfg = layer_cfg(layer)
    fm, tm = cfg["fm"], cfg["tm"]
    NFM = len(fm)
    fmi = {f[0]: i for i, f in enumerate(fm)}
    NCOL = NFM * 128 + sum(t[2] for t in tm)
    tmoff = {}
    o = NFM * 128
    for name, _, n in tm:
        tmoff[name] = o
        o += n
    lam0 = lambda_init(layer)

    LP = "l%d_" % layer
    S.prefix = LP

    def din(name, shape):
        return nc.dram_tensor(LP + name, list(shape), F32, kind="ExternalInput").ap()

    x_u, xc_in, cvec = SH["x_u"], SH["xc"], SH["cvec"]
    ident_d, blk_d, perm_d, ropeC, ropeS = SH["ident"], SH["blk"], SH["perm"], SH["ropeC"], SH["ropeS"]
    x1_loc, xc1_loc, GA = SH["x1_loc"], SH["xc1_loc"], SH["GA"]
    w_mod = din("w_mod", [8, 128, 3072])
    b_mod = din("b_mod", [128, 24])
    bgate = din("bgate", [128, D_MODEL])
    w_in = din("w_in", [8, 128, NCOL])
    w_out = din("w_out", [8, 128, D_MODEL])
    if layer == 0:
        gains = din("gains", [128, 2])
        bias_i = din("bias_i", [128, 8, 5 * 128])
        bias_e = din("bias_e", [4, 8, 128, 7 * 128])
    else:
        dmask = din("dmask", [128, 4, 128])
        lamv = din("lamv", [128, 256])
        subln = din("subln", [128, 128])
        sinks = din("sinks", [128, 8])
        fnorm = din("fnorm", [128, D_MODEL])
    if last:
        out_x = nc.dram_tensor("out_x", [HALF, D_MODEL], F32, kind="ExternalOutput").ap()
    else:
        out_x = x1_loc
        out_c = xc1_loc

    fmS = nc.dram_tensor(LP + "fmS", [NFM, 128, NU], BF16).ap()
    vdims = {"va": (2, 65), "vb": (8, 65), "vc": (4, 129), "vd": (2, 65)}
    vS = {}
    for name, _, n in tm:
        if name != "z":
            h, d = vdims[name]
            vS[name] = nc.dram_tensor(LP + "vS_" + name, [NU, h * d], BF16).ap()
    zS = nc.dram_tensor(LP + "zS", [NU, D_MODEL], BF16).ap()

    with contextlib.ExitStack() as st:
        S.stack = st
        st1 = contextlib.ExitStack()
        fm_reg = Tile("fm_reg")
        v_reg = Tile("v_reg")
        z_reg = Tile("z_reg")

        ident_f = S.sb("ident_f", [128, 128], F32)
        ident = S.sb("ident", [128, 128], BF16)
        blk = S.sb("blk", [128, 128], F32)
        perm = S.sb("perm", [128, 128], F32)
        epst = S.sb("epst", [128, 1], F32)
        zeros = S.sb("zeros", [128, 128], F32)
        junk = S.sb("junk", [128, D_MODEL], F32)
        ss_r = Rot([S.sb("ss%d" % i, [128, 2], F32) for i in range(2)])
        t1_r = Rot([S.sb("t1%d" % i, [128, 512], F32) for i in range(2)])
        gate_bc = S.sb("gate_bc", [128, 2, D_MODEL], F32)
        modt = S.sb("modt", [128, 24, 2], F32)
        bmodt = S.sb("bmodt", [128, 24], F32)
        cv = S.sb("cv", [128, 8, 2], F32)
        pTb_t = SH["pTb_t"]
        pTb = pTb_t.t
        banks = SH["banks"]
        sel_t = S.sb("sel_t", [128, 4], F32)
        S.dma("sp", sel_t[:], SH["sel"], writes=[sel_t])
        xt2 = S.sb("xt2", [128, D_MODEL], F32)
        G_reg = SH["G_reg"]
        NTo_ = HALF // 128

        RCt = min(512, HALF) // 128

        def grow(r, t):
            return ((t // RCt) * 2 * RCt + r * RCt + (t % RCt)) * 128

        def load_x_tile(xt, utile):
            e = utile
            if layer == 0:
                if e >= (EXT + HALF) // 128:
                    c = e - (EXT + HALF) // 128
                    S.dma("sp", xt[:], xc_in[c * 128:(c + 1) * 128, :], writes=[xt])
                else:
                    S.dma("sp", xt[:], x_u[e * 128:(e + 1) * 128, :], writes=[xt])
                return
            if e >= (EXT + HALF) // 128:
                c = e - (EXT + HALF) // 128
                S.dma("sp", xt[:], xc1_loc[c * 128:(c + 1) * 128, :], writes=[xt])
            elif e >= EXT // 128:
                o = e - EXT // 128
                S.dma("sp", xt[:], GA[grow(0, o):grow(0, o) + 128, :], reads=[G_reg], writes=[xt])
                S.dma("sp", xt2[:], GA[grow(1, o):grow(1, o) + 128, :], reads=[G_reg], writes=[xt2])
                S.op("act", lambda en: en.activation(out=xt[:], in_=xt[:], func=AF.Copy, scale=sel_t[:, 2:3]), reads=[xt, sel_t], writes=[xt])
                S.op("dve", lambda en: en.scalar_tensor_tensor(out=xt[:], in0=xt2[:], scalar=sel_t[:, 3:4], in1=xt[:], op0=ALU.mult, op1=ALU.add),
                     reads=[xt2, sel_t, xt], writes=[xt])
            elif 4 <= e < 4 + NTo_:
                t = e - 4
                S.dma("sp", xt[:], x1_loc[t * 128:(t + 1) * 128, :], writes=[xt])
            elif e < 4:
                gt = grow(0, NTo_ - 4 + e)
                S.dma("sp", xt[:], GA[gt:gt + 128, :], reads=[G_reg], writes=[xt])
                S.op("act", lambda en: en.activation(out=xt[:], in_=xt[:], func=AF.Copy, scale=sel_t[:, 0:1]), reads=[xt, sel_t], writes=[xt])
            else:
                gt = grow(1, e - 4 - NTo_)
                S.dma("sp", xt[:], GA[gt:gt + 128, :], reads=[G_reg], writes=[xt])
                S.op("act", lambda en: en.activation(out=xt[:], in_=xt[:], func=AF.Copy, scale=sel_t[:, 1:2]), reads=[xt, sel_t], writes=[xt])

        S.dma("sp", ident_f[:], ident_d, writes=[ident_f])
        S.dma("sp", blk[:], blk_d, writes=[blk])
        S.dma("sp", perm[:], perm_d, writes=[perm])
        S.dma("sp", cv[:], cvec, writes=[cv])
        S.dma("sp", bmodt[:], b_mod, writes=[bmodt])
        S.dma("sp", gate_bc[:, 0, :], bgate, writes=[gate_bc])
        S.op("dve", lambda e: e.tensor_copy(out=ident[:], in_=ident_f[:]), reads=[ident_f], writes=[ident])
        S.op("pool", lambda e: e.memset(epst[:], EPS), writes=[epst])
        S.op("pool", lambda e: e.memset(zeros[:], 0.0), writes=[zeros])
        S.op("dve", lambda e: e.tensor_copy(out=gate_bc[:, 1, :], in_=gate_bc[:, 0, :]), reads=[gate_bc], writes=[gate_bc])
        S.stack = st
        if layer == 0:
            gn = S.sb("gn", [128, 2], F32)
            bi_t = S.sb("bi_t", [128, 8, 640], F32)
            S.dma("sp", gn[:], gains, writes=[gn])
            S.dma("sp", bi_t[:], bias_i, writes=[bi_t])
        else:
            dm_t = S.sb("dm_t", [128, 4, 128], F32)
            lam_t = S.sb("lam_t", [128, 256], F32)
            sub_t = S.sb("sub_t", [128, 128], F32)
            snk_t = S.sb("snk_t", [128, 8], F32)
            fn_t = S.sb("fn_t", [128, D_MODEL], F32)
            S.dma("sp", dm_t[:], dmask, writes=[dm_t])
            S.dma("sp", lam_t[:], lamv, writes=[lam_t])
            S.dma("sp", sub_t[:], subln, writes=[sub_t])
            S.dma("sp", snk_t[:], sinks, writes=[snk_t])
            S.dma("sp", fn_t[:], fnorm, writes=[fn_t])
            lam_s = S.sb("lam_s", [128, 4], F32)
            lprod = S.sb("lprod", [128, 128], F32)
            esnk = S.sb("esnk", [128, 8], F32)
        S.stack = st1
        win = S.sb("win", [128, 8, NCOL], BF16)
        screp = S.sb("screp", [128, 2, 8, 128], F32)
        stage = Rot([S.sb("stage%d" % i, [128, 1024], F32) for i in range(2)])

        S.op("act", lambda e: e.activation(out=cv[:], in_=cv[:], func=AF.Silu), reads=[cv], writes=[cv])
        for w in range(2):
            for k in range(8):
                S.op("act", lambda e: e.activation(out=screp[:, w, k, :], in_=zeros[:], func=AF.Identity,
                                                   bias=cv[:, k, w:w + 1]), reads=[zeros, cv], writes=[screp])
        pmod = banks[5]
        pg = [banks[1], banks[2], banks[3], banks[4]]
        for k in range(8):
            for pi in range(3):
                stg = stage.next()
                S.dma("sp", stg[:], w_mod[k, :, pi * 1024:(pi + 1) * 1024], writes=[stg])
                for jj in range(8):
                    j = pi * 8 + jj
                    S.op("pe", lambda e: e.matmul(pmod[:, j * 2:j * 2 + 2], lhsT=stg[:, jj * 128:(jj + 1) * 128], rhs=cv[:, k, :],
                                                  start=(k == 0 and j == 0), stop=(k == 7), skip_group_check=True),
                         reads=[stg, cv], writes=[pmod])
                if pi == 2:
                    for w in range(2):
                        for n in range(2):
                            S.op("pe", lambda e: e.matmul(pg[w * 2 + n][:], lhsT=screp[:, w, k, :],
                                                          rhs=stg[:, n * 512:(n + 1) * 512],
                                                          start=(k == 0), stop=(k == 7)),
                                 reads=[stg, screp], writes=[pg[w * 2 + n]])
        for w in range(2):
            S.op("dve", lambda e: e.tensor_tensor(out=modt[:, :, w], in0=pmod[:, 0:48].rearrange("p (j w) -> p j w", w=2)[:, :, w],
                                                  in1=bmodt[:], op=ALU.add), reads=[pmod, bmodt], writes=[modt])
            for n in range(2):
                S.op("dve", lambda e: e.tensor_tensor(out=gate_bc[:, w, n * 512:(n + 1) * 512], in0=pg[w * 2 + n][:],
                                                      in1=gate_bc[:, w, n * 512:(n + 1) * 512], op=ALU.add),
                     reads=[pg[w * 2 + n], gate_bc], writes=[gate_bc])
        S.op("dve", lambda e: e.tensor_scalar_add(out=modt[:, 8:16, :], in0=modt[:, 8:16, :], scalar1=1.0),
             reads=[modt], writes=[modt])

        cnt = 0
        for k in range(8):
            for c0 in range(0, NCOL, 1024):
                cn = min(1024, NCOL - c0)
                stg = stage.next()
                S.dma("sp", stg[:, 0:cn], w_in[k, :, c0:c0 + cn], writes=[stg])
                S.op("pool" if cnt % 2 else "dve", lambda e: e.tensor_copy(out=win[:, k, c0:c0 + cn], in_=stg[:, 0:cn]), reads=[stg], writes=[win])
                cnt += 1

        if layer == 1:
            S.op("dve", lambda e: e.tensor_tensor(out=lprod[:, 0:64], in0=lam_t[:, 0:64], in1=lam_t[:, 64:128], op=ALU.mult), reads=[lam_t], writes=[lprod])
            S.op("dve", lambda e: e.tensor_tensor(out=lprod[:, 64:128], in0=lam_t[:, 128:192], in1=lam_t[:, 192:256], op=ALU.mult), reads=[lam_t, lprod], writes=[lprod])
            S.op("dve", lambda e: e.reduce_sum(out=lam_s[:, 0:2], in_=lprod[:].rearrange("p (a d) -> p a d", a=2), axis=mybir.AxisListType.X), reads=[lprod], writes=[lam_s])
            S.op("act", lambda e: e.activation(out=lam_s[:, 0:2], in_=lam_s[:, 0:2], func=AF.Exp), reads=[lam_s], writes=[lam_s])
            S.op("dve", lambda e: e.tensor_tensor(out=lam_s[:, 2:3], in0=lam_s[:, 0:1], in1=lam_s[:, 1:2], op=ALU.subtract), reads=[lam_s], writes=[lam_s])
            S.op("dve", lambda e: e.tensor_scalar(out=lam_s[:, 3:4], in0=lam_s[:, 2:3], scalar1=lam0, scalar2=-1.0, op0=ALU.add, op1=ALU.mult), reads=[lam_s], writes=[lam_s])
            S.op("act", lambda e: e.activation(out=esnk[:], in_=snk_t[:], func=AF.Exp), reads=[snk_t], writes=[esnk])

        xt_r = Rot([S.sb("xt%d" % i, [128, D_MODEL], F32) for i in range(2)])
        xn_r = Rot([S.sb("xn%d" % i, [128, D_MODEL], BF16) for i in range(2)])
        hT_r = Rot([S.sb("hT%d" % i, [128, 8, 512], BF16) for i in range(2)])
        tabC_r = Rot([S.sb("tabC%d" % i, [128, 512], F32) for i in range(1)])
        tabS_r = Rot([S.sb("tabS%d" % i, [128, 512], F32) for i in range(1)])
        sq_r = Rot([S.sb("sq%d" % i, [128, 512], F32) for i in range(1)])
        rs_r = Rot([S.sb("rs%d" % i, [128, 512], F32) for i in range(1)])
        qn_r = Rot([S.sb("qn%d" % i, [128, 512], F32) for i in range(2)])
        t2_r = Rot([S.sb("t2%d" % i, [128, 512], F32) for i in range(1)])
        fo_r = Rot([S.sb("fo%d" % i, [128, 512], BF16) for i in range(3)])
        vst = {}
        for name in vS:
            h, d = vdims[name]
            vst[name] = Rot([S.sb("vst_%s%d" % (name, i), [128, h, d], BF16) for i in range(2)])
            for t in vst[name].tiles:
                S.op("pool", lambda e: e.memset(t[:], 1.0), writes=[t])
        zst_r = Rot([S.sb("zst%d" % i, [128, D_MODEL], BF16) for i in range(2)])
        bA = Rot([banks[1], banks[2], banks[3]])
        bB = Rot([banks[4], banks[5]])

        def phase1_block(u0, ntiles, src, src_row0, w, fm_list, tm_list, rope_col0):
            ntok = ntiles * 128
            hT = hT_r.next()
            for ti in range(ntiles):
                xt = xt_r.next()
                ss = ss_r.next()
                xn = xn_r.next()
                load_x_tile(xt, u0 // 128 + ti)
                S.op("act", lambda e: e.activation(out=junk[:], in_=xt[:], func=AF.Square, accum_out=ss[:, 0:1]), reads=[xt], writes=[junk, ss])
                S.op("act", lambda e: e.activation(out=ss[:, 1:2], in_=ss[:, 0:1], func=AF.Ln, scale=1.0 / D_MODEL, bias=epst[:]), reads=[ss, epst], writes=[ss])
                S.op("act", lambda e: e.activation(out=ss[:, 1:2], in_=ss[:, 1:2], func=AF.Exp, scale=-0.5), reads=[ss], writes=[ss])
                S.op("act", lambda e: e.activation(out=xn[:], in_=xt[:], func=AF.Copy, scale=ss[:, 1:2]), reads=[xt, ss], writes=[xn])
                for k in range(8):
                    S.op("pe", lambda e: e.transpose(out=pTb[:, k, :], in_=xn[:, k * 128:(k + 1) * 128], identity=ident[:]),
                         reads=[xn, ident], writes=[pTb_t])
                for k in range(8):
                    S.op("dve", lambda e: e.tensor_scalar(out=hT[:, k, ti * 128:(ti + 1) * 128], in0=pTb[:, k, :],
                                                          scalar1=modt[:, 8 + k, w:w + 1], scalar2=modt[:, k, w:w + 1],
                                                          op0=ALU.mult, op1=ALU.add), reads=[pTb_t, modt], writes=[hT])
            rope_needed = any(fm[i][3] for i in fm_list) and rope_col0 is not None
            if rope_needed:
                tC = tabC_r.next()
                tS = tabS_r.next()
                S.dma("sp", tC[:, 0:ntok], ropeC[:, rope_col0:rope_col0 + ntok], writes=[tC])
                S.dma("sp", tS[:, 0:ntok], ropeS[:, rope_col0:rope_col0 + ntok], writes=[tS])
            for i in fm_list:
                name, _, nkind, roped = fm[i]
                pa = bA.next()
                for k in range(8):
                    S.op("pe", lambda e: e.matmul(pa[:, 0:ntok], lhsT=win[:, k, i * 128:(i + 1) * 128], rhs=hT[:, k, 0:ntok],
                                                  start=(k == 0), stop=(k == 7)), reads=[win, hT], writes=[pa])
                fo = fo_r.next()
                do_rope = roped and rope_col0 is not None
                if nkind is None and not do_rope:
                    S.op("act", lambda e: e.activation(out=fo[:, 0:ntok], in_=pa[:, 0:ntok], func=AF.Copy), reads=[pa], writes=[fo])
                else:
                    qn = qn_r.next()
                    if nkind is not None:
                        sq = sq_r.next()
                        rs = rs_r.next()
                        pb = bB.next()
                        gcol = 0 if nkind == "q" else 1
                        S.op("act", lambda e: e.activation(out=sq[:, 0:ntok], in_=pa[:, 0:ntok], func=AF.Square), reads=[pa], writes=[sq])
                        S.op("pe", lambda e: e.matmul(pb[:, 0:ntok], lhsT=blk[:], rhs=sq[:, 0:ntok], start=True, stop=True), reads=[blk, sq], writes=[pb])
                        S.op("act", lambda e: e.activation(out=rs[:, 0:ntok], in_=pb[:, 0:ntok], func=AF.Ln, bias=epst[:]), reads=[pb, epst], writes=[rs])
                        S.op("act", lambda e: e.activation(out=rs[:, 0:ntok], in_=rs[:, 0:ntok], func=AF.Exp, scale=-0.5), reads=[rs], writes=[rs])
                        dst = qn if do_rope else fo
                        S.op("dve", lambda e: e.scalar_tensor_tensor(out=dst[:, 0:ntok], in0=pa[:, 0:ntok], scalar=gn[:, gcol:gcol + 1],
                                                                     in1=rs[:, 0:ntok], op0=ALU.mult, op1=ALU.mult),
                             reads=[pa, gn, rs], writes=[dst])
                    else:
                        S.op("act", lambda e: e.activation(out=qn[:, 0:ntok], in_=pa[:, 0:ntok], func=AF.Copy), reads=[pa], writes=[qn])
                    if do_rope:
                        pb2 = bB.next()
                        t1 = t1_r.next()
                        t2 = t2_r.next()
                        S.op("pe", lambda e: e.matmul(pb2[:, 0:ntok], lhsT=perm[:], rhs=qn[:, 0:ntok], start=True, stop=True), reads=[perm, qn], writes=[pb2])
                        S.op("pool", lambda e: e.tensor_tensor(out=t1[:, 0:ntok], in0=qn[:, 0:ntok], in1=tC[:, 0:ntok], op=ALU.mult), reads=[qn, tC], writes=[t1])
                        S.op("dve", lambda e: e.tensor_tensor(out=t2[:, 0:ntok], in0=pb2[:, 0:ntok], in1=tS[:, 0:ntok], op=ALU.mult), reads=[pb2, tS], writes=[t2])
                        S.op("dve", lambda e: e.tensor_tensor(out=fo[:, 0:ntok], in0=t1[:, 0:ntok], in1=t2[:, 0:ntok], op=ALU.add), reads=[t1, t2], writes=[fo])
                S.dma("sp", fmS[i, :, u0:u0 + ntok], fo[:, 0:ntok], reads=[fo], writes=[fm_reg], sem_tile=fo)
            for name in tm_list:
                col0 = tmoff[name]
                ncols = dict((t[0], t[2]) for t in tm)[name]
                for ti in range(ntiles):
                    for n0 in range(0, ncols, 512):
                        nn = min(512, ncols - n0)
                        pa = bA.next()
                        for k in range(8):
                            S.op("pe", lambda e: e.matmul(pa[:, 0:nn], lhsT=hT[:, k, ti * 128:(ti + 1) * 128],
                                                          rhs=win[:, k, col0 + n0:col0 + n0 + nn], start=(k == 0), stop=(k == 7)),
                                 reads=[win, hT], writes=[pa])
                        if name == "z":
                            if n0 == 0:
                                zst = zst_r.next()
                            S.op("act", lambda e: e.activation(out=zst[:, n0:n0 + nn], in_=pa[:, 0:nn], func=AF.Silu), reads=[pa], writes=[zst])
                            if n0 + nn == ncols:
                                S.dma("sp", zS[u0 + ti * 128:u0 + (ti + 1) * 128, :], zst[:], reads=[zst], writes=[z_reg], sem_tile=zst)
                        else:
                            h, d = vdims[name]
                            dv = d - 1
                            vt = vst[name].next()
                            S.op("dve", lambda e: e.tensor_copy(out=vt[:, :, 0:dv], in_=pa[:, 0:nn].rearrange("p (h d) -> p h d", d=dv)),
                                 reads=[pa], writes=[vt])
                            S.dma("sp", vS[name][u0 + ti * 128:u0 + (ti + 1) * 128, :], vt[:].rearrange("p h d -> p (h d)"),
                                  reads=[vt], writes=[v_reg], sem_tile=vt)


        all_fm = list(range(NFM))
        all_tm = [t[0] for t in tm]
        lk = [fmi[n] for n in cfg["local_k"]]
        dk = [fmi[n] for n in cfg["dense_k"]]
        xc_ap = xc_in
        phase1_block(U_C, 2, xc_ap, 0, 1, all_fm, all_tm, None)
        eblocks = list(range(EXT // 512))
        eblocks = [b for b in eblocks if HALO <= b * 512 < HALO + HALF] + [b for b in eblocks if not (HALO <= b * 512 < HALO + HALF)]
        for b in eblocks:
            u0 = b * 512
            own = HALO <= u0 < HALO + HALF
            if own:
                phase1_block(u0, 4, x_u, u0, 0, all_fm, all_tm, u0)
            else:
                phase1_block(u0, 4, x_u, u0, 0, lk, cfg["local_v"], u0)
        for b in range(HALF // 512):
            u0 = U_O + b * 512
            phase1_block(u0, 4, x_u, u0, 0, dk, cfg["dense_v"], u0)

        S.barrier()
        st1.close()
        st2 = contextlib.ExitStack()
        S.stack = st2
        wout = S.sb("wout", [128, 8, D_MODEL], BF16)
        stage2 = Rot([S.sb("stage2_%d" % i, [128, D_MODEL], F32) for i in range(2)])
        for k in range(8):
            stg = stage2.next()
            S.dma("sp", stg[:], w_out[k], writes=[stg])
            S.op("pool" if k % 2 else "dve", lambda e: e.tensor_copy(out=wout[:, k, :], in_=stg[:]), reads=[stg], writes=[wout])
        bS = Rot([banks[1], banks[2], banks[3]])
        bACC = Rot([banks[5], banks[6], banks[7]])
        pT_r = Rot([S.sb("pT%d" % i, [128, 512], BF16) for i in range(3)])
        sb_r = Rot([S.sb("sbias%d" % i, [128, 512], F32) for i in range(2)])
        nbuf_d = 1 if layer == 0 else 2
        KT_r = Rot([S.sb("KT%d" % i, [128, NKD], BF16) for i in range(nbuf_d)])
        VDW = 130 if layer == 0 else 129
        VD_r = Rot([S.sb("VD%d" % i, [128, NKC, VDW], BF16) for i in range(nbuf_d)])
        q_r = Rot([S.sb("qblk%d" % i, [128, 512], BF16) for i in range(6)])
        NLK = 9
        kl_r = Rot([S.sb("kl%d" % i, [128, NLK * 128], BF16) for i in range(2)])
        VLW = 130
        vl_r = Rot([S.sb("vl%d" % i, [128, NLK, VLW], BF16) for i in range(2)])
        y_t = [S.sb("y%d" % i, [128, D_MODEL], F32) for i in range(4)]
        rec_r = Rot([S.sb("rec%d" % i, [128, 8], F32) for i in range(4)])
        zl_r = Rot([S.sb("zl%d" % i, [128, D_MODEL], BF16) for i in range(1)])
        yb_r = Rot([S.sb("yb%d" % i, [128, D_MODEL], BF16) for i in range(1)])
        yT_r = Rot([S.sb("yT%d" % i, [128, 8, 128], BF16) for i in range(1)])
        xo_r = Rot([S.sb("xo%d" % i, [128, D_MODEL], F32) for i in range(1)])
        res_r = Rot([S.sb("res%d" % i, [128, D_MODEL], F32) for i in range(2)])
        be_r = Rot([S.sb("be%d" % i, [128, 7 * 128], F32) for i in range(2)]) if layer == 0 else None
        dcache = {}

        def load_dense(kname, vname, vc0, vw):
            key = (kname, vname, vc0)
            if dcache.get("key") == key:
                return dcache["KT"], dcache["VD"]
            KT = KT_r.next()
            VD = VD_r.next()
            i = fmi[kname]
            S.dma("sp", KT[:, 0:HALF], fmS[i, :, U_OWN:U_OWN + HALF], reads=[fm_reg], writes=[KT])
            S.dma("sp", KT[:, HALF:2 * HALF], fmS[i, :, U_O:U_O + HALF], reads=[fm_reg], writes=[KT])
            S.dma("sp", KT[:, 2 * HALF:NKD], fmS[i, :, U_C:U_C + CTX], reads=[fm_reg], writes=[KT])
            for (c0, u0, n) in ((0, U_OWN, HALF), (HALF // 128, U_O, HALF), (2 * HALF // 128, U_C, CTX)):
                S.dma("sp", VD[:, c0:c0 + n // 128, 0:vw], vS[vname][u0:u0 + n, vc0:vc0 + vw].rearrange("(k p) c -> p k c", p=128),
                      reads=[v_reg], writes=[VD])
            dcache.update(key=key, KT=KT, VD=VD)
            return KT, VD

        def attend(qt, pbase, NQ, kchunks, acc_list, vwidth, bias_fn=None):
            nq = NQ // 128
            n = len(kchunks)
            pend = []
            first = {}

            def issue_s(j):
                kt, kc0, vt, vap = kchunks[j]
                ps = bS.next()
                S.op("pe", lambda e: e.matmul(ps[:, 0:NQ], lhsT=kt[pbase:pbase + 64, kc0:kc0 + 128], rhs=qt[pbase:pbase + 64, 0:NQ],
                                              start=True, stop=True), reads=[kt, qt], writes=[ps])
                pT = pT_r.next()
                b = bias_fn(j) if bias_fn is not None else None
                if b is not None:
                    btile, bap = b
                    sb = sb_r.next()
                    S.op("dve", lambda e: e.scalar_tensor_tensor(out=sb[:, 0:NQ], in0=ps[:, 0:NQ], scalar=SCALE, in1=bap,
                                                                 op0=ALU.mult, op1=ALU.add), reads=[ps, btile], writes=[sb])
                    S.op("act", lambda e: e.activation(out=pT[:, 0:NQ], in_=sb[:, 0:NQ], func=AF.Exp), reads=[sb], writes=[pT])
                else:
                    S.op("act", lambda e: e.activation(out=pT[:, 0:NQ], in_=ps[:, 0:NQ], func=AF.Exp, scale=SCALE), reads=[ps], writes=[pT])
                return pT

            def issue_pv(j, pT):
                kt, kc0, vt, vap = kchunks[j]
                for s in range(nq):
                    acc, c0 = acc_list[s]
                    fst = first.get(id(acc), True)
                    first[id(acc)] = False
                    S.op("pe", lambda e: e.matmul(acc[:, c0:c0 + vwidth], lhsT=pT[:, s * 128:(s + 1) * 128], rhs=vap,
                                                  start=(j == 0 and fst), stop=(j == n - 1), skip_group_check=True),
                         reads=[pT, vt], writes=[acc])

            prev = None
            for j in range(n):
                pT = issue_s(j)
                if prev is not None:
                    issue_pv(prev[0], prev[1])
                prev = (j, pT)
            issue_pv(prev[0], prev[1])

        onesel = S.sb("onesel", [128, 2, 2], BF16)
        S.op("pool", lambda e: e.memset(onesel[:], 0.0), writes=[onesel])
        S.op("pool", lambda e: e.memset(onesel[:, 0, 0:1], 1.0), writes=[onesel])
        S.op("pool", lambda e: e.memset(onesel[:, 1, 1:2], 1.0), writes=[onesel])
        if layer == 1:
            dmb = S.sb("dmb", [128, 3, 384], F32)
            S.op("pool", lambda e: e.memset(dmb[:], 0.0), writes=[dmb])
            for var, (pi, ni) in enumerate(((2, 3), (0, 3), (2, 1))):
                S.op("dve", lambda e: e.tensor_copy(out=dmb[:, var, 0:128], in_=dm_t[:, pi, :]), reads=[dm_t], writes=[dmb])
                S.op("dve", lambda e: e.tensor_copy(out=dmb[:, var, 256:384], in_=dm_t[:, ni, :]), reads=[dm_t], writes=[dmb])
        pT2_r = Rot([S.sb("pT2_%d" % i, [128, 1024], BF16) for i in range(3)])
        oT_r = Rot([S.sb("oT%d" % i, [128, 512], F32) for i in range(2)])
        pairs = SH["pairs"]

        def dense_pair(qt, KT, VD, vap_fn, vw, accs, den=None):
            n = NKC

            def issue_s(j):
                pt, ta, tb = pairs[j % 2]
                for m, tt in ((0, ta), (1, tb)):
                    S.op("pe", lambda e: e.matmul(tt[:, 0:512], lhsT=KT[64 * m:64 * m + 64, j * 128:(j + 1) * 128],
                                                  rhs=qt[64 * m:64 * m + 64, 0:512], start=True, stop=True), reads=[KT, qt], writes=[tt])
                pT = pT2_r.next()
                S.op("act", lambda e: e.activation(out=pT[:], in_=pt[:, :], func=AF.Exp, scale=SCALE), reads=[ta, tb], writes=[pT])
                return pT

            def issue_pv(j, pT):
                for m in range(2):
                    S.op("pe", lambda e: e.matmul(accs[m][0:vw, 0:512], lhsT=vap_fn(j, m), rhs=pT[:, m * 512:(m + 1) * 512],
                                                  start=(j == 0), stop=(j == n - 1)), reads=[pT, VD], writes=[accs[m]])
                    if den is not None:
                        S.op("pe", lambda e: e.matmul(den[0:2, 0:512], lhsT=onesel[:, m, :], rhs=pT[:, m * 512:(m + 1) * 512],
                                                      start=(j == 0 and m == 0), stop=(j == n - 1 and m == 1)), reads=[pT, onesel], writes=[den])

            prev = None
            for j in range(n):
                pT = issue_s(j)
                if prev is not None:
                    issue_pv(prev[0], prev[1])
                prev = (j, pT)
            issue_pv(prev[0], prev[1])

        def untranspose(acc, rows, fin, width):
            oT = oT_r.next()
            S.op("act", lambda e: e.activation(out=oT[0:rows, :], in_=acc[0:rows, 0:512], func=AF.Copy), reads=[acc], writes=[oT])
            for sidx in range(4):
                S.op("pe", lambda e: e.transpose(out=fin[:, sidx * width:sidx * width + rows], in_=oT[0:rows, sidx * 128:(sidx + 1) * 128],
                                                 identity=ident_f[0:rows, 0:rows]), reads=[oT, ident_f], writes=[fin])

        wide_i = [0]

        def wide_a(job):
            qt, pbase, kl, nch, nb, bias = job["qt"], job["pbase"], job["kl"], job["nch"], job["nb"], job["bias"]
            pt, ta, tb = pairs[wide_i[0] % 2]
            wide_i[0] += 1
            for j in range(nch):
                tt = ta if j < 4 else tb
                S.op("pe", lambda e: e.matmul(pt[:, j * 128:(j + 1) * 128], lhsT=kl[pbase:pbase + 64, j * 128:(j + 1) * 128],
                                              rhs=qt[pbase:pbase + 64, 0:128], start=True, stop=True), reads=[kl, qt], writes=[tt])
            pT = pT2_r.next()
            used = [ta] + ([tb] if nch > 4 else [])
            if bias is not None:
                btile, bap = bias
                sbw = stage2.next()
                S.op("dve", lambda e: e.scalar_tensor_tensor(out=sbw[:, 0:nb * 128], in0=pt[:, 0:nb * 128], scalar=SCALE, in1=bap,
                                                             op0=ALU.mult, op1=ALU.add), reads=used + [btile], writes=[sbw])
                S.op("act", lambda e: e.activation(out=pT[:, 0:nb * 128], in_=sbw[:, 0:nb * 128], func=AF.Exp), reads=[sbw], writes=[pT])
                if nch > nb:
                    S.op("act", lambda e: e.activation(out=pT[:, nb * 128:nch * 128], in_=pt[:, nb * 128:nch * 128], func=AF.Exp, scale=SCALE),
                         reads=used, writes=[pT])
            else:
                S.op("act", lambda e: e.activation(out=pT[:, 0:nch * 128], in_=pt[:, 0:nch * 128], func=AF.Exp, scale=SCALE), reads=used, writes=[pT])
            job["pT"] = pT

        def wide_b(job):
            pT, vl, vc0, nch = job["pT"], job["vl"], job["vc0"], job["nch"]
            acc = bACC.next()
            for j in range(nch):
                S.op("pe", lambda e: e.matmul(acc[:, 0:65], lhsT=pT[:, j * 128:(j + 1) * 128], rhs=vl[:, j, vc0:vc0 + 65],
                                              start=(j == 0), stop=(j == nch - 1)), reads=[pT, vl], writes=[acc])
            finish_head([(acc, 0)], 65, [job["yt"]], job["ycol"], extra_den=job.get("extra_den"))

        def run_wide(jobs):
            prev = None
            for job in jobs:
                wide_a(job)
                if prev is not None:
                    wide_b(prev)
                prev = job
            if prev is not None:
                wide_b(prev)

        def finish_head(acc_list, vwidth, y_tiles, ycol, extra_den=None, scale_ap=None):
            dv = vwidth - 1
            for s, (acc, c0) in enumerate(acc_list):
                rec = rec_r.next()
                if extra_den is not None:
                    S.op("dve", lambda e: e.tensor_tensor(out=rec[:, 0:1], in0=acc[:, c0 + dv:c0 + dv + 1], in1=extra_den, op=ALU.add),
                         reads=[acc, esnk], writes=[rec])
                    S.op("dve", lambda e: e.reciprocal(out=rec[:, 1:2], in_=rec[:, 0:1]), reads=[rec], writes=[rec])
                else:
                    S.op("dve", lambda e: e.reciprocal(out=rec[:, 1:2], in_=acc[:, c0 + dv:c0 + dv + 1]), reads=[acc], writes=[rec])
                yt = y_tiles[s]
                S.op("act", lambda e: e.activation(out=yt[:, ycol:ycol + dv], in_=acc[:, c0:c0 + dv], func=AF.Copy, scale=rec[:, 1:2]),
                     reads=[acc, rec], writes=[yt])

        def out_tile(yt, u_tok, src, src_row, w, dst, dst_row):
            zl = zl_r.next()
            yb = yb_r.next()
            yT = yT_r.next()
            xo = xo_r.next()
            res = res_r.next()
            S.dma("sp", zl[:], zS[u_tok:u_tok + 128, :], reads=[z_reg], writes=[zl])
            load_x_tile(xo, u_tok // 128)
            S.op("dve", lambda e: e.tensor_tensor(out=yb[:], in0=yt[:], in1=zl[:], op=ALU.mult), reads=[yt, zl], writes=[yb])
            for k in range(8):
                S.op("pe", lambda e: e.transpose(out=pTb[:, k, :], in_=yb[:, k * 128:(k + 1) * 128], identity=ident[:]),
                     reads=[yb, ident], writes=[pTb_t])
            S.op("act", lambda e: e.activation(out=yT[:].rearrange("p k t -> p (k t)"), in_=pTb[:].rearrange("p k t -> p (k t)"), func=AF.Copy),
                 reads=[pTb_t], writes=[yT])
            for n in range(2):
                po = bACC.next()
                for k in range(8):
                    S.op("pe", lambda e: e.matmul(po[:], lhsT=yT[:, k, :], rhs=wout[:, k, n * 512:(n + 1) * 512], start=(k == 0), stop=(k == 7)),
                         reads=[yT, wout], writes=[po])
                S.op("dve", lambda e: e.tensor_tensor(out=res[:, n * 512:(n + 1) * 512], in0=po[:], in1=gate_bc[:, w, n * 512:(n + 1) * 512], op=ALU.mult),
                     reads=[po, gate_bc], writes=[res])
            S.op("pool", lambda e: e.tensor_tensor(out=res[:], in0=res[:], in1=xo[:], op=ALU.add), reads=[res, xo], writes=[res])
            if last:
                ss = ss_r.next()
                S.op("act", lambda e: e.activation(out=junk[:], in_=res[:], func=AF.Square, accum_out=ss[:, 0:1]), reads=[res], writes=[junk, ss])
                S.op("act", lambda e: e.activation(out=ss[:, 1:2], in_=ss[:, 0:1], func=AF.Ln, scale=1.0 / D_MODEL, bias=epst[:]), reads=[ss, epst], writes=[ss])
                S.op("act", lambda e: e.activation(out=ss[:, 1:2], in_=ss[:, 1:2], func=AF.Exp, scale=-0.5), reads=[ss], writes=[ss])
                S.op("act", lambda e: e.activation(out=xo[:], in_=res[:], func=AF.Copy, scale=ss[:, 1:2]), reads=[res, ss], writes=[xo])
                S.op("dve", lambda e: e.tensor_tensor(out=res[:], in0=xo[:], in1=fn_t[:], op=ALU.mult), reads=[xo, fn_t], writes=[res])
            S.dma("sp", dst[dst_row:dst_row + 128, :], res[:], reads=[res], sem_tile=res)
            return res

        out_tiles = []

        def load_q(name, u0, n):
            qt = q_r.next()
            S.dma("sp", qt[:, 0:n], fmS[fmi[name], :, u0:u0 + n], reads=[fm_reg], writes=[qt])
            return qt

        def load_local(knames_idx, vname, vc0, vw, utiles):
            kl = kl_r.next()
            vl = vl_r.next()
            pos = 0
            for (ut0, cnt) in utiles:
                S.dma("sp", kl[:, pos * 128:(pos + cnt) * 128], fmS[knames_idx, :, ut0 * 128:(ut0 + cnt) * 128], reads=[fm_reg], writes=[kl])
                S.dma("sp", vl[:, pos:pos + cnt, 0:vw], vS[vname][ut0 * 128:(ut0 + cnt) * 128, vc0:vc0 + vw].rearrange("(k p) c -> p k c", p=128),
                      reads=[v_reg], writes=[vl])
                pos += cnt
            return kl, vl

        UC_T = U_C // 128

        if layer == 0:
            for t in range(2):
                yt = y_t[t]
                u0 = U_C + t * 128
                jobs = []
                kl, vl = load_local(fmi["ka"], "va", 0, 130, [(UC_T, 2)])
                for c in range(4):
                    qt = load_q("qa%d" % c, u0, 128)
                    for s_ in range(2):
                        jobs.append(dict(qt=qt, pbase=64 * s_, kl=kl, vl=vl, vc0=s_ * 65, nch=2, nb=0, bias=None, yt=yt, ycol=(c + 4 * s_) * 64))
                run_wide(jobs)
                for c in range(4):
                    kl, vl = load_local(fmi["kb%d" % c], "vb", c * 130, 130, [(UC_T, 2)])
                    qt = load_q("qb%d" % c, u0, 128)
                    run_wide([dict(qt=qt, pbase=64 * s_, kl=kl, vl=vl, vc0=s_ * 65, nch=2, nb=0, bias=None, yt=yt, ycol=512 + (2 * c + s_) * 64)
                              for s_ in range(2)])
                out_tiles.append(out_tile(yt, u0, xc_in, t * 128, 1, out_c, t * 128))

        for qb in range(NBo):
            u0 = U_OWN + qb * 512
            if layer == 0:
                KT, VD = load_dense("ka", "va", 0, 130)
                for c in range(4):
                    qt = load_q("qa%d" % c, u0, 512)
                    accs = [banks[5], banks[6]]
                    dense_pair(qt, KT, VD, lambda j, m: VD[:, j, m * 65:(m + 1) * 65], 65, accs)
                    for s in range(2):
                        head = c + 4 * s
                        fin = banks[7]
                        untranspose(accs[s], 65, fin, 65)
                        finish_head([(fin, i * 65) for i in range(4)], 65, y_t, head * 64)
            else:
                for h in range(4):
                    KT, VD = load_dense("kc%d" % h, "vc", h * 129, 129)
                    qt = load_q("qc%d" % h, u0, 512)
                    accs = [banks[5], banks[6]]
                    den = banks[7]
                    dense_pair(qt, KT, VD, lambda j, m: VD[:, j, 0:128], 128, accs, den=den)
                    fins = [banks[1], banks[2]]
                    untranspose(accs[0], 128, fins[0], 128)
                    untranspose(accs[1], 128, fins[1], 128)
                    dfin = banks[3]
                    untranspose(den, 2, dfin, 2)
                    o_m = [[(fins[0], i * 128, i * 2 + 0) for i in range(4)], [(fins[1], i * 128, i * 2 + 1) for i in range(4)]]
                    for s in range(4):
                        rec = rec_r.next()
                        yt = y_t[s]
                        a0, c0, d0 = o_m[0][s]
                        a1, c1, d1 = o_m[1][s]
                        S.op("dve", lambda e: e.reciprocal(out=rec[:, 0:1], in_=dfin[:, d0:d0 + 1]), reads=[dfin], writes=[rec])
                        S.op("dve", lambda e: e.reciprocal(out=rec[:, 1:2], in_=dfin[:, d1:d1 + 1]), reads=[dfin, rec], writes=[rec])
                        S.op("dve", lambda e: e.tensor_tensor(out=rec[:, 2:3], in0=rec[:, 1:2], in1=lam_s[:, 3:4], op=ALU.mult), reads=[rec, lam_s], writes=[rec])
                        t1 = t1_r.next()
                        S.op("act", lambda e: e.activation(out=t1[:, 0:128], in_=a0[:, c0:c0 + 128], func=AF.Copy, scale=rec[:, 0:1]), reads=[a0, rec], writes=[t1])
                        S.op("dve", lambda e: e.scalar_tensor_tensor(out=t1[:, 128:256], in0=a1[:, c1:c1 + 128], scalar=rec[:, 2:3], in1=t1[:, 0:128],
                                                                     op0=ALU.mult, op1=ALU.add), reads=[a1, rec, t1], writes=[t1])
                        S.op("act", lambda e: e.activation(out=t1[:, 256:384], in_=t1[:, 128:256], func=AF.Square, accum_out=rec[:, 3:4]), reads=[t1], writes=[t1, rec])
                        S.op("act", lambda e: e.activation(out=rec[:, 4:5], in_=rec[:, 3:4], func=AF.Ln, scale=1.0 / 128, bias=epst[:]), reads=[rec, epst], writes=[rec])
                        S.op("act", lambda e: e.activation(out=rec[:, 4:5], in_=rec[:, 4:5], func=AF.Exp, scale=-0.5), reads=[rec], writes=[rec])
                        S.op("dve", lambda e: e.tensor_scalar_mul(out=rec[:, 4:5], in0=rec[:, 4:5], scalar1=1.0 - lam0), reads=[rec], writes=[rec])
                        S.op("dve", lambda e: e.scalar_tensor_tensor(out=yt[:, h * 128:(h + 1) * 128], in0=t1[:, 128:256], scalar=rec[:, 4:5], in1=sub_t[:],
                                                                     op0=ALU.mult, op1=ALU.mult), reads=[t1, rec, sub_t], writes=[yt])
            for tl in range(4):
                t = qb * 4 + tl
                ut = U_OWN // 128 + t
                yt = y_t[tl]
                if layer == 0:
                    edge = t < 2 or t >= NTo - 2
                    if edge:
                        et = t if t < 2 else 2 + (t - (NTo - 2))
                        J = 7
                        runs = [(ut - 3, 7), (UC_T, 2)]
                    else:
                        J = 5
                        runs = [(ut - 2, 5), (UC_T, 2)]
                    for c in range(4):
                        kl, vl = load_local(fmi["kb%d" % c], "vb", c * 130, 130, runs)
                        qt = load_q("qb%d" % c, ut * 128, 128)
                        if not edge:
                            run_wide([dict(qt=qt, pbase=64 * s_, kl=kl, vl=vl, vc0=s_ * 65, nch=7, nb=5,
                                           bias=(bi_t, bi_t[:, 2 * c + s_, 0:640]), yt=yt, ycol=512 + (2 * c + s_) * 64) for s_ in range(2)])
                            continue
                        for s in range(2):
                            head = 2 * c + s
                            if edge:
                                be = be_r.next()
                                S.dma("sp", be[:], bias_e[et, head], writes=[be])
                                bfn = (lambda j, be=be: (be, be[:, j * 128:(j + 1) * 128]) if j < 7 else None)
                            else:
                                bfn = (lambda j, head=head: (bi_t, bi_t[:, head, j * 128:(j + 1) * 128]) if j < 5 else None)
                            acc = bACC.next()
                            attend(qt, 64 * s, 128, [(kl, j * 128, vl, vl[:, j, s * 65:(s + 1) * 65]) for j in range(J + 2)],
                                   [(acc, 0)], 65, bias_fn=bfn)
                            finish_head([(acc, 0)], 65, [yt], 512 + head * 64)
                else:
                    kl, vl = load_local(fmi["kd"], "vd", 0, 130, [(ut - 1, 3), (UC_T, 2)])
                    var = 1 if t == 0 else (2 if t == NTo - 1 else 0)
                    jobs = []
                    for c in range(4):
                        qt = load_q("qd%d" % c, ut * 128, 128)
                        for s_ in range(2):
                            head = c + 4 * s_
                            jobs.append(dict(qt=qt, pbase=64 * s_, kl=kl, vl=vl, vc0=s_ * 65, nch=5, nb=3, bias=(dmb, dmb[:, var, :]),
                                             yt=yt, ycol=512 + head * 64, extra_den=esnk[:, head:head + 1]))
                    run_wide(jobs)
                out_tiles.append(out_tile(yt, ut * 128, x_u, ut * 128, 0, out_x, t * 128))
        if last:
            S.finish(out_tiles)
        else:
            S.barrier()
        st2.close()
    if not last:
        S.release_dsems()
        S.new_epoch()


def build_fused(SEQ, B):
    HALF = SEQ // 2
    EXT = HALF + 2 * HALO
    nc = bass.Bass("TRN2", target_bir_lowering=False)

    def din(name, shape):
        return nc.dram_tensor(name, list(shape), F32, kind="ExternalInput").ap()

    SH = dict(x_u=din("x_u", [EXT + HALF, D_MODEL]), xc=din("xc", [CTX, D_MODEL]), cvec=din("cvec", [128, 8, 2]),
              ident=din("ident", [128, 128]), blk=din("blk", [128, 128]), perm=din("perm", [128, 128]),
              ropeC=din("ropeC", [128, EXT + HALF]), ropeS=din("ropeS", [128, EXT + HALF]), sel=din("sel", [128, 4]))
    SH["x1_loc"] = nc.dram_tensor("x1_loc", [HALF, D_MODEL], F32).ap()
    SH["xc1_loc"] = nc.dram_tensor("xc1_loc", [CTX, D_MODEL], F32).ap()
    SH["GA"] = nc.dram_tensor("x1_all", [2 * HALF, D_MODEL], F32).ap()
    with contextlib.ExitStack() as st0:
        S = Sched(nc, st0)
        SH["pTb_t"] = S.ps("bankT", [128, 8, 128], BF16)
        pairA = st0.enter_context(nc.psum_tensor("ps_pairA", [128, 1024], F32))
        pairB = st0.enter_context(nc.psum_tensor("ps_pairB", [128, 1024], F32))
        b1 = Tile("bank1", pairA[:, 0:512], excl=True)
        b2 = Tile("bank2", pairA[:, 512:1024], excl=True)
        b3 = Tile("bank3", pairB[:, 0:512], excl=True)
        b4 = Tile("bank4", pairB[:, 512:1024], excl=True)
        SH["pairs"] = [(pairA, b1, b2), (pairB, b3, b4)]
        SH["banks"] = [SH["pTb_t"], b1, b2, b3, b4] + [S.ps("bank%d" % i, [128, 512], F32) for i in range(5, 8)]
        SH["G_reg"] = Tile("G_reg")
        emit_layer(nc, S, SH, 0, SEQ)
        S.sems["cc"] = st0.enter_context(nc.semaphore("s_cc"))
        RC = min(512, HALF)
        for i in range(HALF // RC):
            nc.gpsimd.collective_compute("AllGather", ALU.bypass, replica_groups=[[2 * b, 2 * b + 1] for b in range(B)],
                                         ins=[SH["x1_loc"][i * RC:(i + 1) * RC, :]],
                                         outs=[SH["GA"][i * 2 * RC:(i + 1) * 2 * RC, :]]).then_inc(S.sems["cc"], 1)
        SH["G_reg"].last_w = ("cc", HALF // RC)
        emit_layer(nc, S, SH, 1, SEQ)
    return nc


def rope_tables(pos):
    row = (pos // GRID_W).astype(np.float32)
    col = (pos % GRID_W).astype(np.float32)
    q = HD // 4
    inv = (10000.0 ** (-np.arange(q, dtype=np.float32) / q)).astype(np.float32)
    ar = row[None, :] * inv[:, None]
    ac = col[None, :] * inv[:, None]
    cr, sr, cc, sc = np.cos(ar), np.sin(ar), np.cos(ac), np.sin(ac)
    C = np.concatenate([cr, cr, cc, cc], axis=0)
    Sg = np.concatenate([-sr, sr, -sc, sc], axis=0)
    return (np.concatenate([C, C], 0).astype(np.float32), np.concatenate([Sg, Sg], 0).astype(np.float32))


def nbr_bias(rpb, g, gk, NT):
    rows = NT * 2
    out = np.full((8, 128, 128), NEG, np.float32)
    if gk < 0 or gk >= NT:
        return out
    ql = np.arange(128)
    r = 2 * g + ql // 64
    c = ql % 64
    kr = 2 * gk + ql // 64
    kc = ql % 64
    win_r = min(8, rows)
    rs = np.clip(r - win_r // 2, 0, rows - win_r)
    cs = np.clip(c - 8, 0, GRID_W - 16)
    valid = ((kr[:, None] >= rs[None, :]) & (kr[:, None] < rs[None, :] + win_r)
             & (kc[:, None] >= cs[None, :]) & (kc[:, None] < cs[None, :] + 16))
    di = kr[:, None] - r[None, :] + 7
    dj = kc[:, None] - c[None, :] + 15
    di = np.clip(di, 0, 14)
    dj = np.clip(dj, 0, 30)
    vals = rpb[:, di, dj]
    return np.where(valid[None], vals, np.float32(NEG)).astype(np.float32)


def chunk_rows(w):
    return np.ascontiguousarray(w.reshape(8, 128, w.shape[1]))


def prep_layer_inputs(layer, SEQ, xs, xcs, p):
    B = xs.shape[0]
    HALF = SEQ // 2
    EXT = HALF + 2 * HALO
    NT = SEQ // 128
    NTo = HALF // 128
    cfg = layer_cfg(layer)
    wi = p["w_in_even"][0] if layer == 0 else p["w_in_odd"][0]
    wo = p["w_out_even"][0] if layer == 0 else p["w_out_odd"][0]
    cols = []
    for f in cfg["fm"]:
        cols += f[1]
    for name, c0, n in cfg["tm"]:
        cols += list(range(c0, c0 + n))
    w_in_l = chunk_rows(np.ascontiguousarray(wi[:, cols]))
    w_out_l = chunk_rows(wo)
    w_mod_l = chunk_rows(p["w_mod"][layer])
    b_mod_l = np.ascontiguousarray(p["b_mod"][layer].reshape(24, 128).T)
    bgate = np.ascontiguousarray(np.broadcast_to(p["b_mod"][layer][2048:3072][None, :], (128, D_MODEL)))
    ident = np.eye(128, dtype=np.float32)
    blk = np.zeros((128, 128), np.float32)
    blk[:64, :64] = 1.0 / 64
    blk[64:, 64:] = 1.0 / 64
    perm = np.zeros((128, 128), np.float32)
    for m in range(128):
        k = m + 16 if (m % 32) < 16 else m - 16
        perm[k, m] = 1.0
    maps = []
    for b in range(B):
        for half in range(2):
            T0 = half * HALF
            pos_e = np.arange(T0 - HALO, T0 + HALF + HALO)
            valid_e = (pos_e >= 0) & (pos_e < SEQ)
            x_e = np.zeros((EXT, D_MODEL), np.float32)
            x_e[valid_e] = xs[b, pos_e[valid_e]]
            T1 = (1 - half) * HALF
            pos_o = np.arange(T1, T1 + HALF)
            x_u = np.concatenate([x_e, xs[b, pos_o]], axis=0)
            pos_u = np.concatenate([np.clip(pos_e, 0, SEQ - 1), pos_o])
            C, Sg = rope_tables(pos_u)
            cvec = np.stack([p["c"][b].reshape(8, 128).T, p["c_ctx"].reshape(8, 128).T], axis=-1)
            m = dict(x_u=x_u, xc=np.ascontiguousarray(xcs[b]), cvec=np.ascontiguousarray(cvec), w_mod=w_mod_l, b_mod=b_mod_l,
                     bgate=bgate, w_in=w_in_l, w_out=w_out_l, ident=ident, blk=blk, perm=perm, ropeC=C, ropeS=Sg)
            G0 = T0 // 128
            if layer == 0:
                m["gains"] = np.ascontiguousarray(np.stack([np.tile(p["a_q_norm"][0], 2), np.tile(p["a_k_norm"][0], 2)], axis=-1))
                rpb = p["b_rpb"][0]
                gi = min(max(G0 + 2, 2), NT - 3) if NT >= 6 else 0
                bi = np.stack([nbr_bias(rpb, gi, gi + j, NT) for j in range(-2, 3)], axis=0)
                m["bias_i"] = np.ascontiguousarray(bi.transpose(2, 1, 0, 3).reshape(128, 8, 640))
                ets = [0, 1, NTo - 2, NTo - 1]
                be = np.stack([np.stack([nbr_bias(rpb, G0 + t, G0 + t + j, NT) for j in range(-3, 4)], axis=0) for t in ets], axis=0)
                m["bias_e"] = np.ascontiguousarray(be.transpose(0, 2, 3, 1, 4).reshape(4, 8, 128, 896))
            else:
                a = np.arange(128)
                tri_prev = np.where(a[:, None] >= a[None, :], 0.0, NEG).astype(np.float32)
                tri_next = np.where(a[:, None] <= a[None, :], 0.0, NEG).astype(np.float32)
                full = np.full((128, 128), NEG, np.float32)
                first_prev = full if G0 == 0 else tri_prev
                last_next = full if G0 + NTo == NT else tri_next
                m["dmask"] = np.ascontiguousarray(np.stack([first_prev, last_next, tri_prev, tri_next], axis=1))
                m["lamv"] = np.ascontiguousarray(np.broadcast_to(p["c_lambda"][0].reshape(1, 256), (128, 256)))
                m["subln"] = np.ascontiguousarray(np.broadcast_to((p["c_subln"][0])[None, :], (128, 128)))
                m["sinks"] = np.ascontiguousarray(np.broadcast_to(p["d_sinks"][0][None, :], (128, 8)))
                m["fnorm"] = np.ascontiguousarray(np.broadcast_to(p["final_norm"][None, :], (128, D_MODEL)))
            maps.append(m)
    return maps


def prep_fused_inputs(SEQ, xs, xcs, p):
    m0 = prep_layer_inputs(0, SEQ, xs, xcs, p)
    m1 = prep_layer_inputs(1, SEQ, xs, xcs, p)
    shared = ("x_u", "xc", "cvec", "ident", "blk", "perm", "ropeC", "ropeS")
    maps = []
    for i, (a, b) in enumerate(zip(m0, m1)):
        half = i % 2
        m = {k: a[k] for k in shared}
        for k, v in a.items():
            if k not in shared:
                m["l0_" + k] = v
        for k, v in b.items():
            if k not in shared:
                m["l1_" + k] = v
        sel = np.zeros((128, 4), np.float32)
        sel[:, 0] = 1.0 if half == 1 else 0.0
        sel[:, 1] = 1.0 if half == 0 else 0.0
        sel[:, 2] = 1.0 if half == 1 else 0.0
        sel[:, 3] = 1.0 if half == 0 else 0.0
        m["sel"] = sel
        maps.append(m)
    return maps


def run_fused(SEQ, xs, xcs, p, runner=None):
    B = xs.shape[0]
    key = ("fused", SEQ, B)
    if key not in _NC_CACHE:
        _NC_CACHE[key] = build_fused(SEQ, B)
    nc = _NC_CACHE[key]
    maps = prep_fused_inputs(SEQ, xs, xcs, p)
    if runner is None:
        res = run_bass_kernel_spmd(nc, maps, core_ids=list(range(len(maps)))).results
    else:
        res = runner(nc, maps)
    HALF = SEQ // 2
    xo = np.zeros_like(xs)
    for b in range(B):
        for half in range(2):
            xo[b, half * HALF:(half + 1) * HALF] = res[2 * b + half]["out_x"]
    return xo


_NC_CACHE = {}


def kernel(x, c, ctx, c_ctx, w_mod, b_mod, w_in_even, w_out_even, a_q_norm, a_k_norm, b_rpb,
           w_in_odd, w_out_odd, c_lambda, c_subln, d_sinks, final_norm):
    p = dict(c=np.asarray(c, np.float32), c_ctx=np.asarray(c_ctx, np.float32), w_mod=np.asarray(w_mod, np.float32),
             b_mod=np.asarray(b_mod, np.float32), w_in_even=np.asarray(w_in_even, np.float32),
             w_out_even=np.asarray(w_out_even, np.float32), a_q_norm=np.asarray(a_q_norm, np.float32),
             a_k_norm=np.asarray(a_k_norm, np.float32), b_rpb=np.asarray(b_rpb, np.float32),
             w_in_odd=np.asarray(w_in_odd, np.float32), w_out_odd=np.asarray(w_out_odd, np.float32),
             c_lambda=np.asarray(c_lambda, np.float32), c_subln=np.asarray(c_subln, np.float32),
             d_sinks=np.asarray(d_sinks, np.float32), final_norm=np.asarray(final_norm, np.float32))
    xs = np.asarray(x, np.float32)
    xcs = np.asarray(ctx, np.float32)
    SEQ = xs.shape[1]
    return run_fused(SEQ, xs, xcs, p)
```

```python
import contextlib
import math
import numpy as np
import concourse.bass as bass
import concourse.mybir as mybir
from concourse.bass_utils import run_bass_kernel_spmd

F32 = mybir.dt.float32
BF16 = mybir.dt.bfloat16
AF = mybir.ActivationFunctionType
ALU = mybir.AluOpType

D_MODEL = 1024
CTX = 256
HD = 64
GRID_W = 64
SCALE = HD ** -0.5
EPS = 1e-6
NEG = -30000.0
HALO = 512


class Tile:
    __slots__ = ("name", "t", "last_w", "readers", "dsem", "dcount", "excl")

    def __init__(self, name, t=None, excl=False):
        self.name = name
        self.t = t
        self.last_w = None
        self.readers = {}
        self.dsem = None
        self.dcount = 0
        self.excl = excl

    def __getitem__(self, idx):
        return self.t[idx]


class Sched:
    def __init__(self, nc, stack):
        self.nc = nc
        self.stack = stack
        self.sem_stack = stack
        self.dtiles = []
        self.engs = {}
        self.sems = {}
        for en, e in (("pe", nc.tensor), ("act", nc.scalar), ("dve", nc.vector),
                      ("pool", nc.gpsimd), ("sp", nc.sync)):
            self.sems[en] = stack.enter_context(nc.semaphore("s_" + en))
            self.engs[en] = dict(eng=e, count=0, seen={}, key=en)
        self.epoch = 0
        self.nsem = 0
        self.prefix = ""
        self.free_dsems = []

    def sb(self, name, shape, dt):
        return Tile(name, self.stack.enter_context(self.nc.sbuf_tensor("sb_" + self.prefix + name, list(shape), dt)))

    def ps(self, name, shape, dt=F32):
        return Tile(name, self.stack.enter_context(self.nc.psum_tensor("ps_" + name, list(shape), dt)), excl=True)

    def _dsem(self, tile):
        if tile.dsem is None:
            if self.free_dsems:
                key, cnt = self.free_dsems.pop()
                tile.dsem = key
                tile.dcount = cnt
            else:
                key = "d%d" % self.nsem
                self.nsem += 1
                tile.dsem = key
                self.sems[key] = self.sem_stack.enter_context(self.nc.semaphore(key))
            self.dtiles.append(tile)
        return tile.dsem

    def new_epoch(self):
        self.epoch += 1
        for en, E in self.engs.items():
            key = "%s#%d" % (en, self.epoch)
            self.sems[key] = self.sem_stack.enter_context(self.nc.semaphore("s_%s_%d" % (en, self.epoch)))
            E["key"] = key
            E["count"] = 0

    def release_dsems(self):
        for t in self.dtiles:
            self.free_dsems.append((t.dsem, t.dcount))
            t.dsem = None
        self.dtiles = []

    def _wait_deps(self, en, reads, writes):
        E = self.engs[en]
        deps = {}

        def add(ev):
            if ev is None:
                return
            k, v = ev
            if deps.get(k, 0) < v:
                deps[k] = v

        me = E["key"]
        for t in reads:
            add(t.last_w)
            if t.excl:
                for k, v in t.readers.items():
                    if k != me:
                        add((k, v))
        for t in writes:
            add(t.last_w)
            for k, v in t.readers.items():
                if k != me:
                    add((k, v))
        for k, v in deps.items():
            if E["seen"].get(k, 0) < v:
                E["seen"][k] = v
                if k == me and en == "pe":
                    continue
                E["eng"].wait_ge(self.sems[k], v)

    def op(self, en, fn, reads=(), writes=()):
        E = self.engs[en]
        self._wait_deps(en, reads, writes)
        ins = fn(E["eng"])
        E["count"] += 1
        me = E["key"]
        ins.then_inc(self.sems[me], 1)
        for t in reads:
            t.readers[me] = E["count"]
        for t in writes:
            t.last_w = (me, E["count"])
            t.readers = {}
        return ins

    def dma(self, q, out, in_, reads=(), writes=(), sem_tile=None):
        E = self.engs[q]
        self._wait_deps(q, reads, writes)
        st = sem_tile if sem_tile is not None else (list(writes) + list(reads))[0]
        key = self._dsem(st)
        ins = E["eng"].dma_start(out=out, in_=in_)
        st.dcount += 16
        ins.then_inc(self.sems[key], 16)
        for t in reads:
            t.readers[key] = st.dcount
        for t in writes:
            t.last_w = (key, st.dcount)
            t.readers = {}
        return ins

    def barrier(self):
        for en, E in self.engs.items():
            for en2, E2 in self.engs.items():
                k2 = E2["key"]
                if en2 != en and E2["count"] and E["seen"].get(k2, 0) < E2["count"]:
                    E["seen"][k2] = E2["count"]
                    E["eng"].wait_ge(self.sems[k2], E2["count"])
            for t in self.dtiles:
                if t.dcount and E["seen"].get(t.dsem, 0) < t.dcount:
                    E["seen"][t.dsem] = t.dcount
                    E["eng"].wait_ge(self.sems[t.dsem], t.dcount)

    def finish(self, tiles, en="sp"):
        self._wait_deps(en, tiles, tiles)


class Rot:
    def __init__(self, tiles):
        self.tiles = tiles
        self.i = 0

    def next(self):
        t = self.tiles[self.i % len(self.tiles)]
        self.i += 1
        return t


def layer_cfg(layer):
    if layer == 0:
        qa, ka, va, qb, kb, vb, z = 0, 512, 640, 768, 1280, 1792, 2304
        fm = []
        for c in range(4):
            cols = list(range(qa + c * 64, qa + c * 64 + 64)) + list(range(qa + (4 + c) * 64, qa + (4 + c) * 64 + 64))
            fm.append(("qa%d" % c, cols, "q", True))
        fm.append(("ka", list(range(ka, ka + 128)), "k", True))
        for c in range(4):
            fm.append(("qb%d" % c, list(range(qb + c * 128, qb + c * 128 + 128)), None, False))
        for c in range(4):
            fm.append(("kb%d" % c, list(range(kb + c * 128, kb + c * 128 + 128)), None, False))
        tm = [("va", va, 128), ("vb", vb, 512), ("z", z, 1024)]
        dense_k, dense_v = ["ka"], ["va"]
        local_k, local_v = ["kb0", "kb1", "kb2", "kb3"], ["vb"]
    else:
        qc, kc, vc, qd, kd, vd, z = 0, 512, 1024, 1536, 2048, 2176, 2304
        fm = []
        for c in range(4):
            fm.append(("qc%d" % c, list(range(qc + c * 128, qc + c * 128 + 128)), None, True))
        for c in range(4):
            fm.append(("kc%d" % c, list(range(kc + c * 128, kc + c * 128 + 128)), None, True))
        for c in range(4):
            cols = list(range(qd + c * 64, qd + c * 64 + 64)) + list(range(qd + (4 + c) * 64, qd + (4 + c) * 64 + 64))
            fm.append(("qd%d" % c, cols, None, True))
        fm.append(("kd", list(range(kd, kd + 128)), None, True))
        tm = [("vc", vc, 512), ("vd", vd, 128), ("z", z, 1024)]
        dense_k, dense_v = ["kc0", "kc1", "kc2", "kc3"], ["vc"]
        local_k, local_v = ["kd"], ["vd"]
    return dict(fm=fm, tm=tm, dense_k=dense_k, dense_v=dense_v, local_k=local_k, local_v=local_v)


def lambda_init(layer):
    return 0.8 - 0.6 * math.exp(-0.3 * layer)


def emit_layer(nc, S, SH, layer, SEQ):
    HALF = SEQ // 2
    EXT = HALF + 2 * HALO
    NU = EXT + HALF + CTX
    NTo = HALF // 128
    NBo = HALF // 512
    U_OWN = HALO
    U_O = EXT
    U_C = EXT + HALF
    NKD = 2 * HALF + CTX
    NKC = NKD // 128
    last = layer == 1
    cfg = layer_cfg(layer)
    fm, tm = cfg["fm"], cfg["tm"]
    NFM = len(fm)
    fmi = {f[0]: i for i, f in enumerate(fm)}
    NCOL = NFM * 128 + sum(t[2] for t in tm)
    tmoff = {}
    o = NFM * 128
    for name, _, n in tm:
        tmoff[name] = o
        o += n
    lam0 = lambda_init(layer)

    LP = "l%d_" % layer
    S.prefix = LP

    def din(name, shape):
        return nc.dram_tensor(LP + name, list(shape), F32, kind="ExternalInput").ap()

    x_u, xc_in, cvec = SH["x_u"], SH["xc"], SH["cvec"]
    ident_d, blk_d, perm_d, ropeC, ropeS = SH["ident"], SH["blk"], SH["perm"], SH["ropeC"], SH["ropeS"]
    x1_loc, xc1_loc, GA = SH["x1_loc"], SH["xc1_loc"], SH["GA"]
    w_mod = din("w_mod", [8, 128, 3072])
    b_mod = din("b_mod", [128, 24])
    bgate = din("bgate", [128, D_MODEL])
    w_in = din("w_in", [8, 128, NCOL])
    w_out = din("w_out", [8, 128, D_MODEL])
    if layer == 0:
        gains = din("gains", [128, 2])
        bias_i = din("bias_i", [128, 8, 5 * 128])
        bias_e = din("bias_e", [4, 8, 128, 7 * 128])
    else:
        dmask = din("dmask", [128, 4, 128])
        lamv = din("lamv", [128, 256])
        subln = din("subln", [128, 128])
        sinks = din("sinks", [128, 8])
        fnorm = din("fnorm", [128, D_MODEL])
    if last:
        out_x = nc.dram_tensor("out_x", [HALF, D_MODEL], F32, kind="ExternalOutput").ap()
    else:
        out_x = x1_loc
        out_c = xc1_loc

    fmS = nc.dram_tensor(LP + "fmS", [NFM, 128, NU], BF16).ap()
    vdims = {"va": (2, 65), "vb": (8, 65), "vc": (4, 129), "vd": (2, 65)}
    vS = {}
    for name, _, n in tm:
        if name != "z":
            h, d = vdims[name]
            vS[name] = nc.dram_tensor(LP + "vS_" + name, [NU, h * d], BF16).ap()
    zS = nc.dram_tensor(LP + "zS", [NU, D_MODEL], BF16).ap()

    with contextlib.ExitStack() as st:
        S.stack = st
        st1 = contextlib.ExitStack()
        fm_reg = Tile("fm_reg")
        v_reg = Tile("v_reg")
        z_reg = Tile("z_reg")

        ident_f = S.sb("ident_f", [128, 128], F32)
        ident = S.sb("ident", [128, 128], BF16)
        blk = S.sb("blk", [128, 128], F32)
        perm = S.sb("perm", [128, 128], F32)
        epst = S.sb("epst", [128, 1], F32)
        zeros = S.sb("zeros", [128, 128], F32)
        junk = S.sb("junk", [128, D_MODEL], F32)
        ss_r = Rot([S.sb("ss%d" % i, [128, 2], F32) for i in range(2)])
        t1_r = Rot([S.sb("t1%d" % i, [128, 512], F32) for i in range(2)])
        gate_bc = S.sb("gate_bc", [128, 2, D_MODEL], F32)
        modt = S.sb("modt", [128, 24, 2], F32)
        bmodt = S.sb("bmodt", [128, 24], F32)
        cv = S.sb("cv", [128, 8, 2], F32)
        pTb_t = SH["pTb_t"]
        pTb = pTb_t.t
        banks = SH["banks"]
        sel_t = S.sb("sel_t", [128, 4], F32)
        S.dma("sp", sel_t[:], SH["sel"], writes=[sel_t])
        xt2 = S.sb("xt2", [128, D_MODEL], F32)
        G_reg = SH["G_reg"]
        NTo_ = HALF // 128

        RCt = min(512, HALF) // 128

        def grow(r, t):
            return ((t // RCt) * 2 * RCt + r * RCt + (t % RCt)) * 128

        def load_x_tile(xt, utile):
            e = utile
            if layer == 0:
                if e >= (EXT + HALF) // 128:
                    c = e - (EXT + HALF) // 128
                    S.dma("sp", xt[:], xc_in[c * 128:(c + 1) * 128, :], writes=[xt])
                else:
                    S.dma("sp", xt[:], x_u[e * 128:(e + 1) * 128, :], writes=[xt])
                return
            if e >= (EXT + HALF) // 128:
                c = e - (EXT + HALF) // 128
                S.dma("sp", xt[:], xc1_loc[c * 128:(c + 1) * 128, :], writes=[xt])
            elif e >= EXT // 128:
                o = e - EXT // 128
                S.dma("sp", xt[:], GA[grow(0, o):grow(0, o) + 128, :], reads=[G_reg], writes=[xt])
                S.dma("sp", xt2[:], GA[grow(1, o):grow(1, o) + 128, :], reads=[G_reg], writes=[xt2])
                S.op("act", lambda en: en.activation(out=xt[:], in_=xt[:], func=AF.Copy, scale=sel_t[:, 2:3]), reads=[xt, sel_t], writes=[xt])
                S.op("dve", lambda en: en.scalar_tensor_tensor(out=xt[:], in0=xt2[:], scalar=sel_t[:, 3:4], in1=xt[:], op0=ALU.mult, op1=ALU.add),
                     reads=[xt2, sel_t, xt], writes=[xt])
            elif 4 <= e < 4 + NTo_:
                t = e - 4
                S.dma("sp", xt[:], x1_loc[t * 128:(t + 1) * 128, :], writes=[xt])
            elif e < 4:
                gt = grow(0, NTo_ - 4 + e)
                S.dma("sp", xt[:], GA[gt:gt + 128, :], reads=[G_reg], writes=[xt])
                S.op("act", lambda en: en.activation(out=xt[:], in_=xt[:], func=AF.Copy, scale=sel_t[:, 0:1]), reads=[xt, sel_t], writes=[xt])
            else:
                gt = grow(1, e - 4 - NTo_)
                S.dma("sp", xt[:], GA[gt:gt + 128, :], reads=[G_reg], writes=[xt])
                S.op("act", lambda en: en.activation(out=xt[:], in_=xt[:], func=AF.Copy, scale=sel_t[:, 1:2]), reads=[xt, sel_t], writes=[xt])

        S.dma("sp", ident_f[:], ident_d, writes=[ident_f])
        S.dma("sp", blk[:], blk_d, writes=[blk])
        S.dma("sp", perm[:], perm_d, writes=[perm])
        S.dma("sp", cv[:], cvec, writes=[cv])
        S.dma("sp", bmodt[:], b_mod, writes=[bmodt])
        S.dma("sp", gate_bc[:, 0, :], bgate, writes=[gate_bc])
        S.op("dve", lambda e: e.tensor_copy(out=ident[:], in_=ident_f[:]), reads=[ident_f], writes=[ident])
        S.op("pool", lambda e: e.memset(epst[:], EPS), writes=[epst])
        S.op("pool", lambda e: e.memset(zeros[:], 0.0), writes=[zeros])
        S.op("dve", lambda e: e.tensor_copy(out=gate_bc[:, 1, :], in_=gate_bc[:, 0, :]), reads=[gate_bc], writes=[gate_bc])
        S.stack = st
        if layer == 0:
            gn = S.sb("gn", [128, 2], F32)
            bi_t = S.sb("bi_t", [128, 8, 640], F32)
            S.dma("sp", gn[:], gains, writes=[gn])
            S.dma("sp", bi_t[:], bias_i, writes=[bi_t])
        else:
            dm_t = S.sb("dm_t", [128, 4, 128], F32)
            lam_t = S.sb("lam_t", [128, 256], F32)
            sub_t = S.sb("sub_t", [128, 128], F32)
            snk_t = S.sb("snk_t", [128, 8], F32)
            fn_t = S.sb("fn_t", [128, D_MODEL], F32)
            S.dma("sp", dm_t[:], dmask, writes=[dm_t])
            S.dma("sp", lam_t[:], lamv, writes=[lam_t])
            S.dma("sp", sub_t[:], subln, writes=[sub_t])
            S.dma("sp", snk_t[:], sinks, writes=[snk_t])
            S.dma("sp", fn_t[:], fnorm, writes=[fn_t])
            lam_s = S.sb("lam_s", [128, 4], F32)
            lprod = S.sb("lprod", [128, 128], F32)
            esnk = S.sb("esnk", [128, 8], F32)
        S.stack = st1
        win = S.sb("win", [128, 8, NCOL], BF16)
        screp = S.sb("screp", [128, 2, 8, 128], F32)
        stage = Rot([S.sb("stage%d" % i, [128, 1024], F32) for i in range(2)])

        S.op("act", lambda e: e.activation(out=cv[:], in_=cv[:], func=AF.Silu), reads=[cv], writes=[cv])
        for w in range(2):
            for k in range(8):
                S.op("act", lambda e: e.activation(out=screp[:, w, k, :], in_=zeros[:], func=AF.Identity,
                                                   bias=cv[:, k, w:w + 1]), reads=[zeros, cv], writes=[screp])
        pmod = banks[5]
        pg = [banks[1], banks[2], banks[3], banks[4]]
        for k in range(8):
            for pi in range(3):
                stg = stage.next()
                S.dma("sp", stg[:], w_mod[k, :, pi * 1024:(pi + 1) * 1024], writes=[stg])
                for jj in range(8):
                    j = pi * 8 + jj
                    S.op("pe", lambda e: e.matmul(pmod[:, j * 2:j * 2 + 2], lhsT=stg[:, jj * 128:(jj + 1) * 128], rhs=cv[:, k, :],
                                                  start=(k == 0 and j == 0), stop=(k == 7), skip_group_check=True),
                         reads=[stg, cv], writes=[pmod])
                if pi == 2:
                    for w in range(2):
                        for n in range(2):
                            S.op("pe", lambda e: e.matmul(pg[w * 2 + n][:], lhsT=screp[:, w, k, :],
                                                          rhs=stg[:, n * 512:(n + 1) * 512],
                                                          start=(k == 0), stop=(k == 7)),
                                 reads=[stg, screp], writes=[pg[w * 2 + n]])
        for w in range(2):
            S.op("dve", lambda e: e.tensor_tensor(out=modt[:, :, w], in0=pmod[:, 0:48].rearrange("p (j w) -> p j w", w=2)[:, :, w],
                                                  in1=bmodt[:], op=ALU.add), reads=[pmod, bmodt], writes=[modt])
            for n in range(2):
                S.op("dve", lambda e: e.tensor_tensor(out=gate_bc[:, w, n * 512:(n + 1) * 512], in0=pg[w * 2 + n][:],
                                                      in1=gate_bc[:, w, n * 512:(n + 1) * 512], op=ALU.add),
                     reads=[pg[w * 2 + n], gate_bc], writes=[gate_bc])
        S.op("dve", lambda e: e.tensor_scalar_add(out=modt[:, 8:16, :], in0=modt[:, 8:16, :], scalar1=1.0),
             reads=[modt], writes=[modt])

        cnt = 0
        for k in range(8):
            for c0 in range(0, NCOL, 1024):
                cn = min(1024, NCOL - c0)
                stg = stage.next()
                S.dma("sp", stg[:, 0:cn], w_in[k, :, c0:c0 + cn], writes=[stg])
                S.op("pool" if cnt % 2 else "dve", lambda e: e.tensor_copy(out=win[:, k, c0:c0 + cn], in_=stg[:, 0:cn]), reads=[stg], writes=[win])
                cnt += 1

        if layer == 1:
            S.op("dve", lambda e: e.tensor_tensor(out=lprod[:, 0:64], in0=lam_t[:, 0:64], in1=lam_t[:, 64:128], op=ALU.mult), reads=[lam_t], writes=[lprod])
            S.op("dve", lambda e: e.tensor_tensor(out=lprod[:, 64:128], in0=lam_t[:, 128:192], in1=lam_t[:, 192:256], op=ALU.mult), reads=[lam_t, lprod], writes=[lprod])
            S.op("dve", lambda e: e.reduce_sum(out=lam_s[:, 0:2], in_=lprod[:].rearrange("p (a d) -> p a d", a=2), axis=mybir.AxisListType.X), reads=[lprod], writes=[lam_s])
            S.op("act", lambda e: e.activation(out=lam_s[:, 0:2], in_=lam_s[:, 0:2], func=AF.Exp), reads=[lam_s], writes=[lam_s])
            S.op("dve", lambda e: e.tensor_tensor(out=lam_s[:, 2:3], in0=lam_s[:, 0:1], in1=lam_s[:, 1:2], op=ALU.subtract), reads=[lam_s], writes=[lam_s])
            S.op("dve", lambda e: e.tensor_scalar(out=lam_s[:, 3:4], in0=lam_s[:, 2:3], scalar1=lam0, scalar2=-1.0, op0=ALU.add, op1=ALU.mult), reads=[lam_s], writes=[lam_s])
            S.op("act", lambda e: e.activation(out=esnk[:], in_=snk_t[:], func=AF.Exp), reads=[snk_t], writes=[esnk])

        xt_r = Rot([S.sb("xt%d" % i, [128, D_MODEL], F32) for i in range(2)])
        xn_r = Rot([S.sb("xn%d" % i, [128, D_MODEL], BF16) for i in range(2)])
        hT_r = Rot([S.sb("hT%d" % i, [128, 8, 512], BF16) for i in range(2)])
        tabC_r = Rot([S.sb("tabC%d" % i, [128, 512], F32) for i in range(1)])
        tabS_r = Rot([S.sb("tabS%d" % i, [128, 512], F32) for i in range(1)])
        sq_r = Rot([S.sb("sq%d" % i, [128, 512], F32) for i in range(1)])
        rs_r = Rot([S.sb("rs%d" % i, [128, 512], F32) for i in range(1)])
        qn_r = Rot([S.sb("qn%d" % i, [128, 512], F32) for i in range(2)])
        t2_r = Rot([S.sb("t2%d" % i, [128, 512], F32) for i in range(1)])
        fo_r = Rot([S.sb("fo%d" % i, [128, 512], BF16) for i in range(3)])
        vst = {}
        for name in vS:
            h, d = vdims[name]
            vst[name] = Rot([S.sb("vst_%s%d" % (name, i), [128, h, d], BF16) for i in range(2)])
            for t in vst[name].tiles:
                S.op("pool", lambda e: e.memset(t[:], 1.0), writes=[t])
        zst_r = Rot([S.sb("zst%d" % i, [128, D_MODEL], BF16) for i in range(2)])
        bA = Rot([banks[1], banks[2], banks[3]])
        bB = Rot([banks[4], banks[5]])

        def phase1_block(u0, ntiles, src, src_row0, w, fm_list, tm_list, rope_col0):
            ntok = ntiles * 128
            hT = hT_r.next()
            for ti in range(ntiles):
                xt = xt_r.next()
                ss = ss_r.next()
                xn = xn_r.next()
                load_x_tile(xt, u0 // 128 + ti)
                S.op("act", lambda e: e.activation(out=junk[:], in_=xt[:], func=AF.Square, accum_out=ss[:, 0:1]), reads=[xt], writes=[junk, ss])
                S.op("act", lambda e: e.activation(out=ss[:, 1:2], in_=ss[:, 0:1], func=AF.Ln, scale=1.0 / D_MODEL, bias=epst[:]), reads=[ss, epst], writes=[ss])
                S.op("act", lambda e: e.activation(out=ss[:, 1:2], in_=ss[:, 1:2], func=AF.Exp, scale=-0.5), reads=[ss], writes=[ss])
                S.op("act", lambda e: e.activation(out=xn[:], in_=xt[:], func=AF.Copy, scale=ss[:, 1:2]), reads=[xt, ss], writes=[xn])
                for k in range(8):
                    S.op("pe", lambda e: e.transpose(out=pTb[:, k, :], in_=xn[:, k * 128:(k + 1) * 128], identity=ident[:]),
                         reads=[xn, ident], writes=[pTb_t])
                for k in range(8):
                    S.op("dve", lambda e: e.tensor_scalar(out=hT[:, k, ti * 128:(ti + 1) * 128], in0=pTb[:, k, :],
                                                          scalar1=modt[:, 8 + k, w:w + 1], scalar2=modt[:, k, w:w + 1],
                                                          op0=ALU.mult, op1=ALU.add), reads=[pTb_t, modt], writes=[hT])
            rope_needed = any(fm[i][3] for i in fm_list) and rope_col0 is not None
            if rope_needed:
                tC = tabC_r.next()
                tS = tabS_r.next()
                S.dma("sp", tC[:, 0:ntok], ropeC[:, rope_col0:rope_col0 + ntok], writes=[tC])
                S.dma("sp", tS[:, 0:ntok], ropeS[:, rope_col0:rope_col0 + ntok], writes=[tS])
            for i in fm_list:
                name, _, nkind, roped = fm[i]
                pa = bA.next()
                for k in range(8):
                    S.op("pe", lambda e: e.matmul(pa[:, 0:ntok], lhsT=win[:, k, i * 128:(i + 1) * 128], rhs=hT[:, k, 0:ntok],
                                                  start=(k == 0), stop=(k == 7)), reads=[win, hT], writes=[pa])
                fo = fo_r.next()
                do_rope = roped and rope_col0 is not None
                if nkind is None and not do_rope:
                    S.op("act", lambda e: e.activation(out=fo[:, 0:ntok], in_=pa[:, 0:ntok], func=AF.Copy), reads=[pa], writes=[fo])
                else:
                    qn = qn_r.next()
                    if nkind is not None:
                        sq = sq_r.next()
                        rs = rs_r.next()
                        pb = bB.next()
                        gcol = 0 if nkind == "q" else 1
                        S.op("act", lambda e: e.activation(out=sq[:, 0:ntok], in_=pa[:, 0:ntok], func=AF.Square), reads=[pa], writes=[sq])
                        S.op("pe", lambda e: e.matmul(pb[:, 0:ntok], lhsT=blk[:], rhs=sq[:, 0:ntok], start=True, stop=True), reads=[blk, sq], writes=[pb])
                        S.op("act", lambda e: e.activation(out=rs[:, 0:ntok], in_=pb[:, 0:ntok], func=AF.Ln, bias=epst[:]), reads=[pb, epst], writes=[rs])
                        S.op("act", lambda e: e.activation(out=rs[:, 0:ntok], in_=rs[:, 0:ntok], func=AF.Exp, scale=-0.5), reads=[rs], writes=[rs])
                        dst = qn if do_rope else fo
                        S.op("dve", lambda e: e.scalar_tensor_tensor(out=dst[:, 0:ntok], in0=pa[:, 0:ntok], scalar=gn[:, gcol:gcol + 1],
                                                                     in1=rs[:, 0:ntok], op0=ALU.mult, op1=ALU.mult),
                             reads=[pa, gn, rs], writes=[dst])
                    else:
                        S.op("act", lambda e: e.activation(out=qn[:, 0:ntok], in_=pa[:, 0:ntok], func=AF.Copy), reads=[pa], writes=[qn])
                    if do_rope:
                        pb2 = bB.next()
                        t1 = t1_r.next()
                        t2 = t2_r.next()
                        S.op("pe", lambda e: e.matmul(pb2[:, 0:ntok], lhsT=perm[:], rhs=qn[:, 0:ntok], start=True, stop=True), reads=[perm, qn], writes=[pb2])
                        S.op("pool", lambda e: e.tensor_tensor(out=t1[:, 0:ntok], in0=qn[:, 0:ntok], in1=tC[:, 0:ntok], op=ALU.mult), reads=[qn, tC], writes=[t1])
                        S.op("dve", lambda e: e.tensor_tensor(out=t2[:, 0:ntok], in0=pb2[:, 0:ntok], in1=tS[:, 0:ntok], op=ALU.mult), reads=[pb2, tS], writes=[t2])
                        S.op("dve", lambda e: e.tensor_tensor(out=fo[:, 0:ntok], in0=t1[:, 0:ntok], in1=t2[:, 0:ntok], op=ALU.add), reads=[t1, t2], writes=[fo])
                S.dma("sp", fmS[i, :, u0:u0 + ntok], fo[:, 0:ntok], reads=[fo], writes=[fm_reg], sem_tile=fo)
            for name in tm_list:
                col0 = tmoff[name]
                ncols = dict((t[0], t[2]) for t in tm)[name]
                for ti in range(ntiles):
                    for n0 in range(0, ncols, 512):
                        nn = min(512, ncols - n0)
                        pa = bA.next()
                        for k in range(8):
                            S.op("pe", lambda e: e.matmul(pa[:, 0:nn], lhsT=hT[:, k, ti * 128:(ti + 1) * 128],
                                                          rhs=win[:, k, col0 + n0:col0 + n0 + nn], start=(k == 0), stop=(k == 7)),
                                 reads=[win, hT], writes=[pa])
                        if name == "z":
                            if n0 == 0:
                                zst = zst_r.next()
                            S.op("act", lambda e: e.activation(out=zst[:, n0:n0 + nn], in_=pa[:, 0:nn], func=AF.Silu), reads=[pa], writes=[zst])
                            if n0 + nn == ncols:
                                S.dma("sp", zS[u0 + ti * 128:u0 + (ti + 1) * 128, :], zst[:], reads=[zst], writes=[z_reg], sem_tile=zst)
                        else:
                            h, d = vdims[name]
                            dv = d - 1
                            vt = vst[name].next()
                            S.op("dve", lambda e: e.tensor_copy(out=vt[:, :, 0:dv], in_=pa[:, 0:nn].rearrange("p (h d) -> p h d", d=dv)),
                                 reads=[pa], writes=[vt])
                            S.dma("sp", vS[name][u0 + ti * 128:u0 + (ti + 1) * 128, :], vt[:].rearrange("p h d -> p (h d)"),
                                  reads=[vt], writes=[v_reg], sem_tile=vt)


        all_fm = list(range(NFM))
        all_tm = [t[0] for t in tm]
        lk = [fmi[n] for n in cfg["local_k"]]
        dk = [fmi[n] for n in cfg["dense_k"]]
        xc_ap = xc_in
        phase1_block(U_C, 2, xc_ap, 0, 1, all_fm, all_tm, None)
        eblocks = list(range(EXT // 512))
        eblocks = [b for b in eblocks if HALO <= b * 512 < HALO + HALF] + [b for b in eblocks if not (HALO <= b * 512 < HALO + HALF)]
        for b in eblocks:
            u0 = b * 512
            own = HALO <= u0 < HALO + HALF
            if own:
                phase1_block(u0, 4, x_u, u0, 0, all_fm, all_tm, u0)
            else:
                phase1_block(u0, 4, x_u, u0, 0, lk, cfg["local_v"], u0)
        for b in range(HALF // 512):
            u0 = U_O + b * 512
            phase1_block(u0, 4, x_u, u0, 0, dk, cfg["dense_v"], u0)

        S.barrier()
        st1.close()
        st2 = contextlib.ExitStack()
        S.stack = st2
        wout = S.sb("wout", [128, 8, D_MODEL], BF16)
        stage2 = Rot([S.sb("stage2_%d" % i, [128, D_MODEL], F32) for i in range(2)])
        for k in range(8):
            stg = stage2.next()
            S.dma("sp", stg[:], w_out[k], writes=[stg])
            S.op("pool" if k % 2 else "dve", lambda e: e.tensor_copy(out=wout[:, k, :], in_=stg[:]), reads=[stg], writes=[wout])
        bS = Rot([banks[1], banks[2], banks[3]])
        bACC = Rot([banks[5], banks[6], banks[7]])
        pT_r = Rot([S.sb("pT%d" % i, [128, 512], BF16) for i in range(3)])
        sb_r = Rot([S.sb("sbias%d" % i, [128, 512], F32) for i in range(2)])
        nbuf_d = 1 if layer == 0 else 2
        KT_r = Rot([S.sb("KT%d" % i, [128, NKD], BF16) for i in range(nbuf_d)])
        VDW = 130 if layer == 0 else 129
        VD_r = Rot([S.sb("VD%d" % i, [128, NKC, VDW], BF16) for i in range(nbuf_d)])
        q_r = Rot([S.sb("qblk%d" % i, [128, 512], BF16) for i in range(6)])
        NLK = 9
        kl_r = Rot([S.sb("kl%d" % i, [128, NLK * 128], BF16) for i in range(2)])
        VLW = 130
        vl_r = Rot([S.sb("vl%d" % i, [128, NLK, VLW], BF16) for i in range(2)])
        y_t = [S.sb("y%d" % i, [128, D_MODEL], F32) for i in range(4)]
        rec_r = Rot([S.sb("rec%d" % i, [128, 8], F32) for i in range(4)])
        zl_r = Rot([S.sb("zl%d" % i, [128, D_MODEL], BF16) for i in range(1)])
        yb_r = Rot([S.sb("yb%d" % i, [128, D_MODEL], BF16) for i in range(1)])
        yT_r = Rot([S.sb("yT%d" % i, [128, 8, 128], BF16) for i in range(1)])
        xo_r = Rot([S.sb("xo%d" % i, [128, D_MODEL], F32) for i in range(1)])
        res_r = Rot([S.sb("res%d" % i, [128, D_MODEL], F32) for i in range(2)])
        be_r = Rot([S.sb("be%d" % i, [128, 7 * 128], F32) for i in range(2)]) if layer == 0 else None
        dcache = {}

        def load_dense(kname, vname, vc0, vw):
            key = (kname, vname, vc0)
            if dcache.get("key") == key:
                return dcache["KT"], dcache["VD"]
            KT = KT_r.next()
            VD = VD_r.next()
            i = fmi[kname]
            S.dma("sp", KT[:, 0:HALF], fmS[i, :, U_OWN:U_OWN + HALF], reads=[fm_reg], writes=[KT])
            S.dma("sp", KT[:, HALF:2 * HALF], fmS[i, :, U_O:U_O + HALF], reads=[fm_reg], writes=[KT])
            S.dma("sp", KT[:, 2 * HALF:NKD], fmS[i, :, U_C:U_C + CTX], reads=[fm_reg], writes=[KT])
            for (c0, u0, n) in ((0, U_OWN, HALF), (HALF // 128, U_O, HALF), (2 * HALF // 128, U_C, CTX)):
                S.dma("sp", VD[:, c0:c0 + n // 128, 0:vw], vS[vname][u0:u0 + n, vc0:vc0 + vw].rearrange("(k p) c -> p k c", p=128),
                      reads=[v_reg], writes=[VD])
            dcache.update(key=key, KT=KT, VD=VD)
            return KT, VD

        def attend(qt, pbase, NQ, kchunks, acc_list, vwidth, bias_fn=None):
            nq = NQ // 128
            n = len(kchunks)
            pend = []
            first = {}

            def issue_s(j):
                kt, kc0, vt, vap = kchunks[j]
                ps = bS.next()
                S.op("pe", lambda e: e.matmul(ps[:, 0:NQ], lhsT=kt[pbase:pbase + 64, kc0:kc0 + 128], rhs=qt[pbase:pbase + 64, 0:NQ],
                                              start=True, stop=True), reads=[kt, qt], writes=[ps])
                pT = pT_r.next()
                b = bias_fn(j) if bias_fn is not None else None
                if b is not None:
                    btile, bap = b
                    sb = sb_r.next()
                    S.op("dve", lambda e: e.scalar_tensor_tensor(out=sb[:, 0:NQ], in0=ps[:, 0:NQ], scalar=SCALE, in1=bap,
                                                                 op0=ALU.mult, op1=ALU.add), reads=[ps, btile], writes=[sb])
                    S.op("act", lambda e: e.activation(out=pT[:, 0:NQ], in_=sb[:, 0:NQ], func=AF.Exp), reads=[sb], writes=[pT])
                else:
                    S.op("act", lambda e: e.activation(out=pT[:, 0:NQ], in_=ps[:, 0:NQ], func=AF.Exp, scale=SCALE), reads=[ps], writes=[pT])
                return pT

            def issue_pv(j, pT):
                kt, kc0, vt, vap = kchunks[j]
                for s in range(nq):
                    acc, c0 = acc_list[s]
                    fst = first.get(id(acc), True)
                    first[id(acc)] = False
                    S.op("pe", lambda e: e.matmul(acc[:, c0:c0 + vwidth], lhsT=pT[:, s * 128:(s + 1) * 128], rhs=vap,
                                                  start=(j == 0 and fst), stop=(j == n - 1), skip_group_check=True),
                         reads=[pT, vt], writes=[acc])

            prev = None
            for j in range(n):
                pT = issue_s(j)
                if prev is not None:
                    issue_pv(prev[0], prev[1])
                prev = (j, pT)
            issue_pv(prev[0], prev[1])

        onesel_f = S.sb("onesel_f", [128, 2, 2], F32)
        S.op("pool", lambda e: e.memset(onesel_f[:], 0.0), writes=[onesel_f])
        S.op("pool", lambda e: e.memset(onesel_f[:, 0, 0:1], 1.0), writes=[onesel_f])
        S.op("pool", lambda e: e.memset(onesel_f[:, 1, 1:2], 1.0), writes=[onesel_f])
        den_acc = S.sb("den_acc", [128, 1024], F32)
        if layer == 1:
            dmb = S.sb("dmb", [128, 3, 384], F32)
            S.op("pool", lambda e: e.memset(dmb[:], 0.0), writes=[dmb])
            for var, (pi, ni) in enumerate(((2, 3), (0, 3), (2, 1))):
                S.op("dve", lambda e: e.tensor_copy(out=dmb[:, var, 0:128], in_=dm_t[:, pi, :]), reads=[dm_t], writes=[dmb])
                S.op("dve", lambda e: e.tensor_copy(out=dmb[:, var, 256:384], in_=dm_t[:, ni, :]), reads=[dm_t], writes=[dmb])
        pT2_r = Rot([S.sb("pT2_%d" % i, [128, 1024], BF16) for i in range(3)])
        oT_r = Rot([S.sb("oT%d" % i, [128, 512], F32) for i in range(2)])
        pairs = SH["pairs"]

        def dense_pair(qt, KT, VD, vap_fn, vw, accs, den=None):
            n = NKC

            def issue_s(j):
                pt, ta, tb = pairs[j % 2]
                for m, tt in ((0, ta), (1, tb)):
                    S.op("pe", lambda e: e.matmul(tt[:, 0:512], lhsT=KT[64 * m:64 * m + 64, j * 128:(j + 1) * 128],
                                                  rhs=qt[64 * m:64 * m + 64, 0:512], start=True, stop=True), reads=[KT, qt], writes=[tt])
                pT = pT2_r.next()
                S.op("act", lambda e: e.activation(out=pT[:], in_=pt[:, :], func=AF.Exp, scale=SCALE), reads=[ta, tb], writes=[pT])
                return pT

            def issue_pv(j, pT):
                for m in range(2):
                    S.op("pe", lambda e: e.matmul(accs[m][0:vw, 0:512], lhsT=vap_fn(j, m), rhs=pT[:, m * 512:(m + 1) * 512],
                                                  start=(j == 0), stop=(j == n - 1)), reads=[pT, VD], writes=[accs[m]])
                if den is not None:
                    if j == 0:
                        S.op("dve", lambda e: e.tensor_copy(out=den_acc[:], in_=pT[:]), reads=[pT], writes=[den_acc])
                    else:
                        S.op("dve", lambda e: e.tensor_tensor(out=den_acc[:], in0=den_acc[:], in1=pT[:], op=ALU.add), reads=[pT, den_acc], writes=[den_acc])
                    if j == n - 1:
                        for m in range(2):
                            S.op("pe", lambda e: e.matmul(den[0:2, 0:512], lhsT=onesel_f[:, m, :], rhs=den_acc[:, m * 512:(m + 1) * 512],
                                                          start=(m == 0), stop=(m == 1)), reads=[den_acc, onesel_f], writes=[den])

            prev = None
            for j in range(n):
                pT = issue_s(j)
                if prev is not None:
                    issue_pv(prev[0], prev[1])
                prev = (j, pT)
            issue_pv(prev[0], prev[1])

        def untranspose(acc, rows, fin, width):
            oT = oT_r.next()
            S.op("act", lambda e: e.activation(out=oT[0:rows, :], in_=acc[0:rows, 0:512], func=AF.Copy), reads=[acc], writes=[oT])
            for sidx in range(4):
                S.op("pe", lambda e: e.transpose(out=fin[:, sidx * width:sidx * width + rows], in_=oT[0:rows, sidx * 128:(sidx + 1) * 128],
                                                 identity=ident_f[0:rows, 0:rows]), reads=[oT, ident_f], writes=[fin])

        wide_i = [0]

        def wide_a(job):
            qt, pbase, kl, nch, nb, bias = job["qt"], job["pbase"], job["kl"], job["nch"], job["nb"], job["bias"]
            pt, ta, tb = pairs[wide_i[0] % 2]
            wide_i[0] += 1
            for j in range(nch):
                tt = ta if j < 4 else tb
                S.op("pe", lambda e: e.matmul(pt[:, j * 128:(j + 1) * 128], lhsT=kl[pbase:pbase + 64, j * 128:(j + 1) * 128],
                                              rhs=qt[pbase:pbase + 64, 0:128], start=True, stop=True), reads=[kl, qt], writes=[tt])
            pT = pT2_r.next()
            used = [ta] + ([tb] if nch > 4 else [])
            if bias is not None:
                btile, bap = bias
                sbw = stage2.next()
                S.op("dve", lambda e: e.scalar_tensor_tensor(out=sbw[:, 0:nb * 128], in0=pt[:, 0:nb * 128], scalar=SCALE, in1=bap,
                                                             op0=ALU.mult, op1=ALU.add), reads=used + [btile], writes=[sbw])
                S.op("act", lambda e: e.activation(out=pT[:, 0:nb * 128], in_=sbw[:, 0:nb * 128], func=AF.Exp), reads=[sbw], writes=[pT])
                if nch > nb:
                    S.op("act", lambda e: e.activation(out=pT[:, nb * 128:nch * 128], in_=pt[:, nb * 128:nch * 128], func=AF.Exp, scale=SCALE),
                         reads=used, writes=[pT])
            else:
                S.op("act", lambda e: e.activation(out=pT[:, 0:nch * 128], in_=pt[:, 0:nch * 128], func=AF.Exp, scale=SCALE), reads=used, writes=[pT])
            job["pT"] = pT

        def wide_b(job):
            pT, vl, vc0, nch = job["pT"], job["vl"], job["vc0"], job["nch"]
            acc = bACC.next()
            for j in range(nch):
                S.op("pe", lambda e: e.matmul(acc[:, 0:65], lhsT=pT[:, j * 128:(j + 1) * 128], rhs=vl[:, j, vc0:vc0 + 65],
                                              start=(j == 0), stop=(j == nch - 1)), reads=[pT, vl], writes=[acc])
            finish_head([(acc, 0)], 65, [job["yt"]], job["ycol"], extra_den=job.get("extra_den"))

        def run_wide(jobs):
            prev = None
            for job in jobs:
                wide_a(job)
                if prev is not None:
                    wide_b(prev)
                prev = job
            if prev is not None:
                wide_b(prev)

        def finish_head(acc_list, vwidth, y_tiles, ycol, extra_den=None, scale_ap=None):
            dv = vwidth - 1
            for s, (acc, c0) in enumerate(acc_list):
                rec = rec_r.next()
                if extra_den is not None:
                    S.op("dve", lambda e: e.tensor_tensor(out=rec[:, 0:1], in0=acc[:, c0 + dv:c0 + dv + 1], in1=extra_den, op=ALU.add),
                         reads=[acc, esnk], writes=[rec])
                    S.op("dve", lambda e: e.reciprocal(out=rec[:, 1:2], in_=rec[:, 0:1]), reads=[rec], writes=[rec])
                else:
                    S.op("dve", lambda e: e.reciprocal(out=rec[:, 1:2], in_=acc[:, c0 + dv:c0 + dv + 1]), reads=[acc], writes=[rec])
                yt = y_tiles[s]
                S.op("act", lambda e: e.activation(out=yt[:, ycol:ycol + dv], in_=acc[:, c0:c0 + dv], func=AF.Copy, scale=rec[:, 1:2]),
                     reads=[acc, rec], writes=[yt])

        def out_tile(yt, u_tok, src, src_row, w, dst, dst_row):
            zl = zl_r.next()
            yb = yb_r.next()
            yT = yT_r.next()
            xo = xo_r.next()
            res = res_r.next()
            S.dma("sp", zl[:], zS[u_tok:u_tok + 128, :], reads=[z_reg], writes=[zl])
            load_x_tile(xo, u_tok // 128)
            S.op("dve", lambda e: e.tensor_tensor(out=yb[:], in0=yt[:], in1=zl[:], op=ALU.mult), reads=[yt, zl], writes=[yb])
            for k in range(8):
                S.op("pe", lambda e: e.transpose(out=pTb[:, k, :], in_=yb[:, k * 128:(k + 1) * 128], identity=ident[:]),
                     reads=[yb, ident], writes=[pTb_t])
            S.op("act", lambda e: e.activation(out=yT[:].rearrange("p k t -> p (k t)"), in_=pTb[:].rearrange("p k t -> p (k t)"), func=AF.Copy),
                 reads=[pTb_t], writes=[yT])
            for n in range(2):
                po = bACC.next()
                for k in range(8):
                    S.op("pe", lambda e: e.matmul(po[:], lhsT=yT[:, k, :], rhs=wout[:, k, n * 512:(n + 1) * 512], start=(k == 0), stop=(k == 7)),
                         reads=[yT, wout], writes=[po])
                S.op("dve", lambda e: e.tensor_tensor(out=res[:, n * 512:(n + 1) * 512], in0=po[:], in1=gate_bc[:, w, n * 512:(n + 1) * 512], op=ALU.mult),
                     reads=[po, gate_bc], writes=[res])
            S.op("pool", lambda e: e.tensor_tensor(out=res[:], in0=res[:], in1=xo[:], op=ALU.add), reads=[res, xo], writes=[res])
            if last:
                ss = ss_r.next()
                S.op("act", lambda e: e.activation(out=junk[:], in_=res[:], func=AF.Square, accum_out=ss[:, 0:1]), reads=[res], writes=[junk, ss])
                S.op("act", lambda e: e.activation(out=ss[:, 1:2], in_=ss[:, 0:1], func=AF.Ln, scale=1.0 / D_MODEL, bias=epst[:]), reads=[ss, epst], writes=[ss])
                S.op("act", lambda e: e.activation(out=ss[:, 1:2], in_=ss[:, 1:2], func=AF.Exp, scale=-0.5), reads=[ss], writes=[ss])
                S.op("act", lambda e: e.activation(out=xo[:], in_=res[:], func=AF.Copy, scale=ss[:, 1:2]), reads=[res, ss], writes=[xo])
                S.op("dve", lambda e: e.tensor_tensor(out=res[:], in0=xo[:], in1=fn_t[:], op=ALU.mult), reads=[xo, fn_t], writes=[res])
            S.dma("sp", dst[dst_row:dst_row + 128, :], res[:], reads=[res], sem_tile=res)
            return res

        out_tiles = []

        def load_q(name, u0, n):
            qt = q_r.next()
            S.dma("sp", qt[:, 0:n], fmS[fmi[name], :, u0:u0 + n], reads=[fm_reg], writes=[qt])
            return qt

        def load_local(knames_idx, vname, vc0, vw, utiles):
            kl = kl_r.next()
            vl = vl_r.next()
            pos = 0
            for (ut0, cnt) in utiles:
                S.dma("sp", kl[:, pos * 128:(pos + cnt) * 128], fmS[knames_idx, :, ut0 * 128:(ut0 + cnt) * 128], reads=[fm_reg], writes=[kl])
                S.dma("sp", vl[:, pos:pos + cnt, 0:vw], vS[vname][ut0 * 128:(ut0 + cnt) * 128, vc0:vc0 + vw].rearrange("(k p) c -> p k c", p=128),
                      reads=[v_reg], writes=[vl])
                pos += cnt
            return kl, vl

        UC_T = U_C // 128

        if layer == 0:
            for t in range(2):
                yt = y_t[t]
                u0 = U_C + t * 128
                jobs = []
                kl, vl = load_local(fmi["ka"], "va", 0, 130, [(UC_T, 2)])
                for c in range(4):
                    qt = load_q("qa%d" % c, u0, 128)
                    for s_ in range(2):
                        jobs.append(dict(qt=qt, pbase=64 * s_, kl=kl, vl=vl, vc0=s_ * 65, nch=2, nb=0, bias=None, yt=yt, ycol=(c + 4 * s_) * 64))
                run_wide(jobs)
                for c in range(4):
                    kl, vl = load_local(fmi["kb%d" % c], "vb", c * 130, 130, [(UC_T, 2)])
                    qt = load_q("qb%d" % c, u0, 128)
                    run_wide([dict(qt=qt, pbase=64 * s_, kl=kl, vl=vl, vc0=s_ * 65, nch=2, nb=0, bias=None, yt=yt, ycol=512 + (2 * c + s_) * 64)
                              for s_ in range(2)])
                out_tiles.append(out_tile(yt, u0, xc_in, t * 128, 1, out_c, t * 128))

        for qb in range(NBo):
            u0 = U_OWN + qb * 512
            if layer == 0:
                KT, VD = load_dense("ka", "va", 0, 130)
                for c in range(4):
                    qt = load_q("qa%d" % c, u0, 512)
                    accs = [banks[5], banks[6]]
                    dense_pair(qt, KT, VD, lambda j, m: VD[:, j, m * 65:(m + 1) * 65], 65, accs)
                    for s in range(2):
                        head = c + 4 * s
                        fin = banks[7]
                        untranspose(accs[s], 65, fin, 65)
                        finish_head([(fin, i * 65) for i in range(4)], 65, y_t, head * 64)
            else:
                for h in range(4):
                    KT, VD = load_dense("kc%d" % h, "vc", h * 129, 129)
                    qt = load_q("qc%d" % h, u0, 512)
                    accs = [banks[5], banks[6]]
                    den = banks[7]
                    dense_pair(qt, KT, VD, lambda j, m: VD[:, j, 0:128], 128, accs, den=den)
                    fins = [banks[1], banks[2]]
                    untranspose(accs[0], 128, fins[0], 128)
                    untranspose(accs[1], 128, fins[1], 128)
                    dfin = banks[3]
                    untranspose(den, 2, dfin, 2)
                    o_m = [[(fins[0], i * 128, i * 2 + 0) for i in range(4)], [(fins[1], i * 128, i * 2 + 1) for i in range(4)]]
                    for s in range(4):
                        rec = rec_r.next()
                        yt = y_t[s]
                        a0, c0, d0 = o_m[0][s]
                        a1, c1, d1 = o_m[1][s]
                        S.op("dve", lambda e: e.reciprocal(out=rec[:, 0:1], in_=dfin[:, d0:d0 + 1]), reads=[dfin], writes=[rec])
                        S.op("dve", lambda e: e.reciprocal(out=rec[:, 1:2], in_=dfin[:, d1:d1 + 1]), reads=[dfin, rec], writes=[rec])
                        S.op("dve", lambda e: e.tensor_tensor(out=rec[:, 2:3], in0=rec[:, 1:2], in1=lam_s[:, 3:4], op=ALU.mult), reads=[rec, lam_s], writes=[rec])
                        t1 = t1_r.next()
                        S.op("act", lambda e: e.activation(out=t1[:, 0:128], in_=a0[:, c0:c0 + 128], func=AF.Copy, scale=rec[:, 0:1]), reads=[a0, rec], writes=[t1])
                        S.op("dve", lambda e: e.scalar_tensor_tensor(out=t1[:, 128:256], in0=a1[:, c1:c1 + 128], scalar=rec[:, 2:3], in1=t1[:, 0:128],
                                                                     op0=ALU.mult, op1=ALU.add), reads=[a1, rec, t1], writes=[t1])
                        S.op("act", lambda e: e.activation(out=t1[:, 256:384], in_=t1[:, 128:256], func=AF.Square, accum_out=rec[:, 3:4]), reads=[t1], writes=[t1, rec])
                        S.op("act", lambda e: e.activation(out=rec[:, 4:5], in_=rec[:, 3:4], func=AF.Ln, scale=1.0 / 128, bias=epst[:]), reads=[rec, epst], writes=[rec])
                        S.op("act", lambda e: e.activation(out=rec[:, 4:5], in_=rec[:, 4:5], func=AF.Exp, scale=-0.5), reads=[rec], writes=[rec])
                        S.op("dve", lambda e: e.tensor_scalar_mul(out=rec[:, 4:5], in0=rec[:, 4:5], scalar1=1.0 - lam0), reads=[rec], writes=[rec])
                        S.op("dve", lambda e: e.scalar_tensor_tensor(out=yt[:, h * 128:(h + 1) * 128], in0=t1[:, 128:256], scalar=rec[:, 4:5], in1=sub_t[:],
                                                                     op0=ALU.mult, op1=ALU.mult), reads=[t1, rec, sub_t], writes=[yt])
            for tl in range(4):
                t = qb * 4 + tl
                ut = U_OWN // 128 + t
                yt = y_t[tl]
                if layer == 0:
                    edge = t < 2 or t >= NTo - 2
                    if edge:
                        et = t if t < 2 else 2 + (t - (NTo - 2))
                        J = 7
                        runs = [(ut - 3, 7), (UC_T, 2)]
                    else:
                        J = 5
                        runs = [(ut - 2, 5), (UC_T, 2)]
                    for c in range(4):
                        kl, vl = load_local(fmi["kb%d" % c], "vb", c * 130, 130, runs)
                        qt = load_q("qb%d" % c, ut * 128, 128)
                        if not edge:
                            run_wide([dict(qt=qt, pbase=64 * s_, kl=kl, vl=vl, vc0=s_ * 65, nch=7, nb=5,
                                           bias=(bi_t, bi_t[:, 2 * c + s_, 0:640]), yt=yt, ycol=512 + (2 * c + s_) * 64) for s_ in range(2)])
                            continue
                        for s in range(2):
                            head = 2 * c + s
                            if edge:
                                be = be_r.next()
                                S.dma("sp", be[:], bias_e[et, head], writes=[be])
                                bfn = (lambda j, be=be: (be, be[:, j * 128:(j + 1) * 128]) if j < 7 else None)
                            else:
                                bfn = (lambda j, head=head: (bi_t, bi_t[:, head, j * 128:(j + 1) * 128]) if j < 5 else None)
                            acc = bACC.next()
                            attend(qt, 64 * s, 128, [(kl, j * 128, vl, vl[:, j, s * 65:(s + 1) * 65]) for j in range(J + 2)],
                                   [(acc, 0)], 65, bias_fn=bfn)
                            finish_head([(acc, 0)], 65, [yt], 512 + head * 64)
                else:
                    kl, vl = load_local(fmi["kd"], "vd", 0, 130, [(ut - 1, 3), (UC_T, 2)])
                    var = 1 if t == 0 else (2 if t == NTo - 1 else 0)
                    jobs = []
                    for c in range(4):
                        qt = load_q("qd%d" % c, ut * 128, 128)
                        for s_ in range(2):
                            head = c + 4 * s_
                            jobs.append(dict(qt=qt, pbase=64 * s_, kl=kl, vl=vl, vc0=s_ * 65, nch=5, nb=3, bias=(dmb, dmb[:, var, :]),
                                             yt=yt, ycol=512 + head * 64, extra_den=esnk[:, head:head + 1]))
                    run_wide(jobs)
                out_tiles.append(out_tile(yt, ut * 128, x_u, ut * 128, 0, out_x, t * 128))
        if last:
            S.finish(out_tiles)
        else:
            S.barrier()
        st2.close()
    if not last:
        S.release_dsems()
        S.new_epoch()


def build_fused(SEQ, B):
    HALF = SEQ // 2
    EXT = HALF + 2 * HALO
    nc = bass.Bass("TRN2", target_bir_lowering=False)

    def din(name, shape):
        return nc.dram_tensor(name, list(shape), F32, kind="ExternalInput").ap()

    SH = dict(x_u=din("x_u", [EXT + HALF, D_MODEL]), xc=din("xc", [CTX, D_MODEL]), cvec=din("cvec", [128, 8, 2]),
              ident=din("ident", [128, 128]), blk=din("blk", [128, 128]), perm=din("perm", [128, 128]),
              ropeC=din("ropeC", [128, EXT + HALF]), ropeS=din("ropeS", [128, EXT + HALF]), sel=din("sel", [128, 4]))
    SH["x1_loc"] = nc.dram_tensor("x1_loc", [HALF, D_MODEL], F32).ap()
    SH["xc1_loc"] = nc.dram_tensor("xc1_loc", [CTX, D_MODEL], F32).ap()
    SH["GA"] = nc.dram_tensor("x1_all", [2 * HALF, D_MODEL], F32).ap()
    with contextlib.ExitStack() as st0:
        S = Sched(nc, st0)
        SH["pTb_t"] = S.ps("bankT", [128, 8, 128], BF16)
        pairA = st0.enter_context(nc.psum_tensor("ps_pairA", [128, 1024], F32))
        pairB = st0.enter_context(nc.psum_tensor("ps_pairB", [128, 1024], F32))
        b1 = Tile("bank1", pairA[:, 0:512], excl=True)
        b2 = Tile("bank2", pairA[:, 512:1024], excl=True)
        b3 = Tile("bank3", pairB[:, 0:512], excl=True)
        b4 = Tile("bank4", pairB[:, 512:1024], excl=True)
        SH["pairs"] = [(pairA, b1, b2), (pairB, b3, b4)]
        SH["banks"] = [SH["pTb_t"], b1, b2, b3, b4] + [S.ps("bank%d" % i, [128, 512], F32) for i in range(5, 8)]
        SH["G_reg"] = Tile("G_reg")
        emit_layer(nc, S, SH, 0, SEQ)
        S.sems["cc"] = st0.enter_context(nc.semaphore("s_cc"))
        RC = min(512, HALF)
        for i in range(HALF // RC):
            nc.gpsimd.collective_compute("AllGather", ALU.bypass, replica_groups=[[2 * b, 2 * b + 1] for b in range(B)],
                                         ins=[SH["x1_loc"][i * RC:(i + 1) * RC, :]],
                                         outs=[SH["GA"][i * 2 * RC:(i + 1) * 2 * RC, :]]).then_inc(S.sems["cc"], 1)
        SH["G_reg"].last_w = ("cc", HALF // RC)
        emit_layer(nc, S, SH, 1, SEQ)
    return nc


def rope_tables(pos):
    row = (pos // GRID_W).astype(np.float32)
    col = (pos % GRID_W).astype(np.float32)
    q = HD // 4
    inv = (10000.0 ** (-np.arange(q, dtype=np.float32) / q)).astype(np.float32)
    ar = row[None, :] * inv[:, None]
    ac = col[None, :] * inv[:, None]
    cr, sr, cc, sc = np.cos(ar), np.sin(ar), np.cos(ac), np.sin(ac)
    C = np.concatenate([cr, cr, cc, cc], axis=0)
    Sg = np.concatenate([-sr, sr, -sc, sc], axis=0)
    return (np.concatenate([C, C], 0).astype(np.float32), np.concatenate([Sg, Sg], 0).astype(np.float32))


def nbr_bias(rpb, g, gk, NT):
    rows = NT * 2
    out = np.full((8, 128, 128), NEG, np.float32)
    if gk < 0 or gk >= NT:
        return out
    ql = np.arange(128)
    r = 2 * g + ql // 64
    c = ql % 64
    kr = 2 * gk + ql // 64
    kc = ql % 64
    win_r = min(8, rows)
    rs = np.clip(r - win_r // 2, 0, rows - win_r)
    cs = np.clip(c - 8, 0, GRID_W - 16)
    valid = ((kr[:, None] >= rs[None, :]) & (kr[:, None] < rs[None, :] + win_r)
             & (kc[:, None] >= cs[None, :]) & (kc[:, None] < cs[None, :] + 16))
    di = kr[:, None] - r[None, :] + 7
    dj = kc[:, None] - c[None, :] + 15
    di = np.clip(di, 0, 14)
    dj = np.clip(dj, 0, 30)
    vals = rpb[:, di, dj]
    return np.where(valid[None], vals, np.float32(NEG)).astype(np.float32)


def chunk_rows(w):
    return np.ascontiguousarray(w.reshape(8, 128, w.shape[1]))


def prep_layer_inputs(layer, SEQ, xs, xcs, p):
    B = xs.shape[0]
    HALF = SEQ // 2
    EXT = HALF + 2 * HALO
    NT = SEQ // 128
    NTo = HALF // 128
    cfg = layer_cfg(layer)
    wi = p["w_in_even"][0] if layer == 0 else p["w_in_odd"][0]
    wo = p["w_out_even"][0] if layer == 0 else p["w_out_odd"][0]
    cols = []
    for f in cfg["fm"]:
        cols += f[1]
    for name, c0, n in cfg["tm"]:
        cols += list(range(c0, c0 + n))
    w_in_l = chunk_rows(np.ascontiguousarray(wi[:, cols]))
    w_out_l = chunk_rows(wo)
    w_mod_l = chunk_rows(p["w_mod"][layer])
    b_mod_l = np.ascontiguousarray(p["b_mod"][layer].reshape(24, 128).T)
    bgate = np.ascontiguousarray(np.broadcast_to(p["b_mod"][layer][2048:3072][None, :], (128, D_MODEL)))
    ident = np.eye(128, dtype=np.float32)
    blk = np.zeros((128, 128), np.float32)
    blk[:64, :64] = 1.0 / 64
    blk[64:, 64:] = 1.0 / 64
    perm = np.zeros((128, 128), np.float32)
    for m in range(128):
        k = m + 16 if (m % 32) < 16 else m - 16
        perm[k, m] = 1.0
    maps = []
    for b in range(B):
        for half in range(2):
            T0 = half * HALF
            pos_e = np.arange(T0 - HALO, T0 + HALF + HALO)
            valid_e = (pos_e >= 0) & (pos_e < SEQ)
            x_e = np.zeros((EXT, D_MODEL), np.float32)
            x_e[valid_e] = xs[b, pos_e[valid_e]]
            T1 = (1 - half) * HALF
            pos_o = np.arange(T1, T1 + HALF)
            x_u = np.concatenate([x_e, xs[b, pos_o]], axis=0)
            pos_u = np.concatenate([np.clip(pos_e, 0, SEQ - 1), pos_o])
            C, Sg = rope_tables(pos_u)
            cvec = np.stack([p["c"][b].reshape(8, 128).T, p["c_ctx"].reshape(8, 128).T], axis=-1)
            m = dict(x_u=x_u, xc=np.ascontiguousarray(xcs[b]), cvec=np.ascontiguousarray(cvec), w_mod=w_mod_l, b_mod=b_mod_l,
                     bgate=bgate, w_in=w_in_l, w_out=w_out_l, ident=ident, blk=blk, perm=perm, ropeC=C, ropeS=Sg)
            G0 = T0 // 128
            if layer == 0:
                m["gains"] = np.ascontiguousarray(np.stack([np.tile(p["a_q_norm"][0], 2), np.tile(p["a_k_norm"][0], 2)], axis=-1))
                rpb = p["b_rpb"][0]
                gi = min(max(G0 + 2, 2), NT - 3) if NT >= 6 else 0
                bi = np.stack([nbr_bias(rpb, gi, gi + j, NT) for j in range(-2, 3)], axis=0)
                m["bias_i"] = np.ascontiguousarray(bi.transpose(2, 1, 0, 3).reshape(128, 8, 640))
                ets = [0, 1, NTo - 2, NTo - 1]
                be = np.stack([np.stack([nbr_bias(rpb, G0 + t, G0 + t + j, NT) for j in range(-3, 4)], axis=0) for t in ets], axis=0)
                m["bias_e"] = np.ascontiguousarray(be.transpose(0, 2, 3, 1, 4).reshape(4, 8, 128, 896))
            else:
                a = np.arange(128)
                tri_prev = np.where(a[:, None] >= a[None, :], 0.0, NEG).astype(np.float32)
                tri_next = np.where(a[:, None] <= a[None, :], 0.0, NEG).astype(np.float32)
                full = np.full((128, 128), NEG, np.float32)
                first_prev = full if G0 == 0 else tri_prev
                last_next = full if G0 + NTo == NT else tri_next
                m["dmask"] = np.ascontiguousarray(np.stack([first_prev, last_next, tri_prev, tri_next], axis=1))
                m["lamv"] = np.ascontiguousarray(np.broadcast_to(p["c_lambda"][0].reshape(1, 256), (128, 256)))
                m["subln"] = np.ascontiguousarray(np.broadcast_to((p["c_subln"][0])[None, :], (128, 128)))
                m["sinks"] = np.ascontiguousarray(np.broadcast_to(p["d_sinks"][0][None, :], (128, 8)))
                m["fnorm"] = np.ascontiguousarray(np.broadcast_to(p["final_norm"][None, :], (128, D_MODEL)))
            maps.append(m)
    return maps


def prep_fused_inputs(SEQ, xs, xcs, p):
    m0 = prep_layer_inputs(0, SEQ, xs, xcs, p)
    m1 = prep_layer_inputs(1, SEQ, xs, xcs, p)
    shared = ("x_u", "xc", "cvec", "ident", "blk", "perm", "ropeC", "ropeS")
    maps = []
    for i, (a, b) in enumerate(zip(m0, m1)):
        half = i % 2
        m = {k: a[k] for k in shared}
        for k, v in a.items():
            if k not in shared:
                m["l0_" + k] = v
        for k, v in b.items():
            if k not in shared:
                m["l1_" + k] = v
        sel = np.zeros((128, 4), np.float32)
        sel[:, 0] = 1.0 if half == 1 else 0.0
        sel[:, 1] = 1.0 if half == 0 else 0.0
        sel[:, 2] = 1.0 if half == 1 else 0.0
        sel[:, 3] = 1.0 if half == 0 else 0.0
        m["sel"] = sel
        maps.append(m)
    return maps


def run_fused(SEQ, xs, xcs, p, runner=None):
    B = xs.shape[0]
    key = ("fused", SEQ, B)
    if key not in _NC_CACHE:
        _NC_CACHE[key] = build_fused(SEQ, B)
    nc = _NC_CACHE[key]
    maps = prep_fused_inputs(SEQ, xs, xcs, p)
    if runner is None:
        res = run_bass_kernel_spmd(nc, maps, core_ids=list(range(len(maps)))).results
    else:
        res = runner(nc, maps)
    HALF = SEQ // 2
    xo = np.zeros_like(xs)
    for b in range(B):
        for half in range(2):
            xo[b, half * HALF:(half + 1) * HALF] = res[2 * b + half]["out_x"]
    return xo


_NC_CACHE = {}


def kernel(x, c, ctx, c_ctx, w_mod, b_mod, w_in_even, w_out_even, a_q_norm, a_k_norm, b_rpb,
           w_in_odd, w_out_odd, c_lambda, c_subln, d_sinks, final_norm):
    p = dict(c=np.asarray(c, np.float32), c_ctx=np.asarray(c_ctx, np.float32), w_mod=np.asarray(w_mod, np.float32),
             b_mod=np.asarray(b_mod, np.float32), w_in_even=np.asarray(w_in_even, np.float32),
             w_out_even=np.asarray(w_out_even, np.float32), a_q_norm=np.asarray(a_q_norm, np.float32),
             a_k_norm=np.asarray(a_k_norm, np.float32), b_rpb=np.asarray(b_rpb, np.float32),
             w_in_odd=np.asarray(w_in_odd, np.float32), w_out_odd=np.asarray(w_out_odd, np.float32),
             c_lambda=np.asarray(c_lambda, np.float32), c_subln=np.asarray(c_subln, np.float32),
             d_sinks=np.asarray(d_sinks, np.float32), final_norm=np.asarray(final_norm, np.float32))
    xs = np.asarray(x, np.float32)
    xcs = np.asarray(ctx, np.float32)
    SEQ = xs.shape[1]
    return run_fused(SEQ, xs, xcs, p)
```

```python
import contextlib
import math
import numpy as np
import concourse.bass as bass
import concourse.mybir as mybir
from concourse.bass_utils import run_bass_kernel_spmd

F32 = mybir.dt.float32
BF16 = mybir.dt.bfloat16
AF = mybir.ActivationFunctionType
ALU = mybir.AluOpType

D_MODEL = 1024
CTX = 256
HD = 64
GRID_W = 64
SCALE = HD ** -0.5
EPS = 1e-6
NEG = -30000.0
HALO = 512


class Tile:
    __slots__ = ("name", "t", "last_w", "readers", "dsem", "dcount", "excl")

    def __init__(self, name, t=None, excl=False):
        self.name = name
        self.t = t
        self.last_w = None
        self.readers = {}
        self.dsem = None
        self.dcount = 0
        self.excl = excl

    def __getitem__(self, idx):
        return self.t[idx]


class Sched:
    def __init__(self, nc, stack):
        self.nc = nc
        self.stack = stack
        self.sem_stack = stack
        self.dtiles = []
        self.engs = {}
        self.sems = {}
        for en, e in (("pe", nc.tensor), ("act", nc.scalar), ("dve", nc.vector),
                      ("pool", nc.gpsimd), ("sp", nc.sync)):
            self.sems[en] = stack.enter_context(nc.semaphore("s_" + en))
            self.engs[en] = dict(eng=e, count=0, seen={}, key=en)
        self.epoch = 0
        self.nsem = 0
        self.prefix = ""
        self.free_dsems = []

    def sb(self, name, shape, dt):
        return Tile(name, self.stack.enter_context(self.nc.sbuf_tensor("sb_" + self.prefix + name, list(shape), dt)))

    def ps(self, name, shape, dt=F32):
        return Tile(name, self.stack.enter_context(self.nc.psum_tensor("ps_" + name, list(shape), dt)), excl=True)

    def _dsem(self, tile):
        if tile.dsem is None:
            if self.free_dsems:
                key, cnt = self.free_dsems.pop()
                tile.dsem = key
                tile.dcount = cnt
            else:
                key = "d%d" % self.nsem
                self.nsem += 1
                tile.dsem = key
                self.sems[key] = self.sem_stack.enter_context(self.nc.semaphore(key))
            self.dtiles.append(tile)
        return tile.dsem

    def new_epoch(self):
        self.epoch += 1
        for en, E in self.engs.items():
            key = "%s#%d" % (en, self.epoch)
            self.sems[key] = self.sem_stack.enter_context(self.nc.semaphore("s_%s_%d" % (en, self.epoch)))
            E["key"] = key
            E["count"] = 0

    def release_dsems(self):
        for t in self.dtiles:
            self.free_dsems.append((t.dsem, t.dcount))
            t.dsem = None
        self.dtiles = []

    def _wait_deps(self, en, reads, writes):
        E = self.engs[en]
        deps = {}

        def add(ev):
            if ev is None:
                return
            k, v = ev
            if deps.get(k, 0) < v:
                deps[k] = v

        me = E["key"]
        for t in reads:
            add(t.last_w)
            if t.excl:
                for k, v in t.readers.items():
                    if k != me:
                        add((k, v))
        for t in writes:
            add(t.last_w)
            for k, v in t.readers.items():
                if k != me:
                    add((k, v))
        for k, v in deps.items():
            if E["seen"].get(k, 0) < v:
                E["seen"][k] = v
                if k == me and en == "pe":
                    continue
                E["eng"].wait_ge(self.sems[k], v)

    def op(self, en, fn, reads=(), writes=()):
        E = self.engs[en]
        self._wait_deps(en, reads, writes)
        ins = fn(E["eng"])
        E["count"] += 1
        me = E["key"]
        ins.then_inc(self.sems[me], 1)
        for t in reads:
            t.readers[me] = E["count"]
        for t in writes:
            t.last_w = (me, E["count"])
            t.readers = {}
        return ins

    def dma(self, q, out, in_, reads=(), writes=(), sem_tile=None):
        E = self.engs[q]
        self._wait_deps(q, reads, writes)
        st = sem_tile if sem_tile is not None else (list(writes) + list(reads))[0]
        key = self._dsem(st)
        ins = E["eng"].dma_start(out=out, in_=in_)
        st.dcount += 16
        ins.then_inc(self.sems[key], 16)
        for t in reads:
            t.readers[key] = st.dcount
        for t in writes:
            t.last_w = (key, st.dcount)
            t.readers = {}
        return ins

    def barrier(self):
        for en, E in self.engs.items():
            for en2, E2 in self.engs.items():
                k2 = E2["key"]
                if en2 != en and E2["count"] and E["seen"].get(k2, 0) < E2["count"]:
                    E["seen"][k2] = E2["count"]
                    E["eng"].wait_ge(self.sems[k2], E2["count"])
            for t in self.dtiles:
                if t.dcount and E["seen"].get(t.dsem, 0) < t.dcount:
                    E["seen"][t.dsem] = t.dcount
                    E["eng"].wait_ge(self.sems[t.dsem], t.dcount)

    def finish(self, tiles, en="sp"):
        self._wait_deps(en, tiles, tiles)


class Rot:
    def __init__(self, tiles):
        self.tiles = tiles
        self.i = 0

    def next(self):
        t = self.tiles[self.i % len(self.tiles)]
        self.i += 1
        return t


def layer_cfg(layer):
    if layer == 0:
        qa, ka, va, qb, kb, vb, z = 0, 512, 640, 768, 1280, 1792, 2304
        fm = []
        for c in range(4):
            cols = list(range(qa + c * 64, qa + c * 64 + 64)) + list(range(qa + (4 + c) * 64, qa + (4 + c) * 64 + 64))
            fm.append(("qa%d" % c, cols, "q", True))
        fm.append(("ka", list(range(ka, ka + 128)), "k", True))
        for c in range(4):
            fm.append(("qb%d" % c, list(range(qb + c * 128, qb + c * 128 + 128)), None, False))
        for c in range(4):
            fm.append(("kb%d" % c, list(range(kb + c * 128, kb + c * 128 + 128)), None, False))
        tm = [("va", va, 128), ("vb", vb, 512), ("z", z, 1024)]
        dense_k, dense_v = ["ka"], ["va"]
        local_k, local_v = ["kb0", "kb1", "kb2", "kb3"], ["vb"]
    else:
        qc, kc, vc, qd, kd, vd, z = 0, 512, 1024, 1536, 2048, 2176, 2304
        fm = []
        for c in range(4):
            fm.append(("qc%d" % c, list(range(qc + c * 128, qc + c * 128 + 128)), None, True))
        for c in range(4):
            fm.append(("kc%d" % c, list(range(kc + c * 128, kc + c * 128 + 128)), None, True))
        for c in range(4):
            cols = list(range(qd + c * 64, qd + c * 64 + 64)) + list(range(qd + (4 + c) * 64, qd + (4 + c) * 64 + 64))
            fm.append(("qd%d" % c, cols, None, True))
        fm.append(("kd", list(range(kd, kd + 128)), None, True))
        tm = [("vc", vc, 512), ("vd", vd, 128), ("z", z, 1024)]
        dense_k, dense_v = ["kc0", "kc1", "kc2", "kc3"], ["vc"]
        local_k, local_v = ["kd"], ["vd"]
    return dict(fm=fm, tm=tm, dense_k=dense_k, dense_v=dense_v, local_k=local_k, local_v=local_v)


def lambda_init(layer):
    return 0.8 - 0.6 * math.exp(-0.3 * layer)


def emit_layer(nc, S, SH, layer, SEQ):
    HALF = SEQ // 2
    EXT = HALF + 2 * HALO
    NU = EXT + HALF + CTX
    NTo = HALF // 128
    NBo = HALF // 512
    U_OWN = HALO
    U_O = EXT
    U_C = EXT + HALF
    NKD = 2 * HALF + CTX
    NKC = NKD // 128
    last = layer == 1
    cfg = layer_cfg(layer)
    fm, tm = cfg["fm"], cfg["tm"]
    NFM = len(fm)
    fmi = {f[0]: i for i, f in enumerate(fm)}
    NCOL = NFM * 128 + sum(t[2] for t in tm)
    tmoff = {}
    o = NFM * 128
    for name, _, n in tm:
        tmoff[name] = o
        o += n
    lam0 = lambda_init(layer)

    LP = "l%d_" % layer
    S.prefix = LP

    def din(name, shape):
        return nc.dram_tensor(LP + name, list(shape), F32, kind="ExternalInput").ap()

    x_u, xc_in, cvec = SH["x_u"], SH["xc"], SH["cvec"]
    ident_d, blk_d, perm_d, ropeC, ropeS = SH["ident"], SH["blk"], SH["perm"], SH["ropeC"], SH["ropeS"]
    x1_loc, xc1_loc, GA = SH["x1_loc"], SH["xc1_loc"], SH["GA"]
    w_mod = din("w_mod", [8, 128, 3072])
    b_mod = din("b_mod", [128, 24])
    bgate = din("bgate", [128, D_MODEL])
    w_in = din("w_in", [8, 128, NCOL])
    w_out = din("w_out", [8, 128, D_MODEL])
    if layer == 0:
        gains = din("gains", [128, 2])
        bias_i = din("bias_i", [128, 8, 5 * 128])
        bias_e = din("bias_e", [4, 8, 128, 7 * 128])
    else:
        dmask = din("dmask", [128, 4, 128])
        lamv = din("lamv", [128, 256])
        subln = din("subln", [128, 128])
        sinks = din("sinks", [128, 8])
        fnorm = din("fnorm", [128, D_MODEL])
    if last:
        out_x = nc.dram_tensor("out_x", [HALF, D_MODEL], F32, kind="ExternalOutput").ap()
    else:
        out_x = x1_loc
        out_c = xc1_loc

    fmS = nc.dram_tensor(LP + "fmS", [NFM, 128, NU], BF16).ap()
    vdims = {"va": (2, 65), "vb": (8, 65), "vc": (4, 129), "vd": (2, 65)}
    vS = {}
    for name, _, n in tm:
        if name != "z":
            h, d = vdims[name]
            vS[name] = nc.dram_tensor(LP + "vS_" + name, [NU, h * d], BF16).ap()
    zS = nc.dram_tensor(LP + "zS", [NU, D_MODEL], BF16).ap()

    with contextlib.ExitStack() as st:
        S.stack = st
        st1 = contextlib.ExitStack()
        fm_reg = Tile("fm_reg")
        v_reg = Tile("v_reg")
        z_reg = Tile("z_reg")

        ident_f = S.sb("ident_f", [128, 128], F32)
        ident = S.sb("ident", [128, 128], BF16)
        blk = S.sb("blk", [128, 128], F32)
        perm = S.sb("perm", [128, 128], F32)
        epst = S.sb("epst", [128, 1], F32)
        zeros = S.sb("zeros", [128, 128], F32)
        junk = S.sb("junk", [128, D_MODEL], F32)
        ss_r = Rot([S.sb("ss%d" % i, [128, 2], F32) for i in range(2)])
        t1_r = Rot([S.sb("t1%d" % i, [128, 512], F32) for i in range(2)])
        gate_bc = S.sb("gate_bc", [128, 2, D_MODEL], F32)
        modt = S.sb("modt", [128, 24, 2], F32)
        bmodt = S.sb("bmodt", [128, 24], F32)
        cv = S.sb("cv", [128, 8, 2], F32)
        pTb_t = SH["pTb_t"]
        pTb = pTb_t.t
        banks = SH["banks"]
        sel_t = S.sb("sel_t", [128, 4], F32)
        S.dma("sp", sel_t[:], SH["sel"], writes=[sel_t])
        xt2 = S.sb("xt2", [128, D_MODEL], F32)
        G_reg = SH["G_reg"]
        NTo_ = HALF // 128

        RCt = min(512, HALF) // 128

        def grow(r, t):
            return ((t // RCt) * 2 * RCt + r * RCt + (t % RCt)) * 128

        def load_x_tile(xt, utile):
            e = utile
            if layer == 0:
                if e >= (EXT + HALF) // 128:
                    c = e - (EXT + HALF) // 128
                    S.dma("sp", xt[:], xc_in[c * 128:(c + 1) * 128, :], writes=[xt])
                else:
                    S.dma("sp", xt[:], x_u[e * 128:(e + 1) * 128, :], writes=[xt])
                return
            if e >= (EXT + HALF) // 128:
                c = e - (EXT + HALF) // 128
                S.dma("sp", xt[:], xc1_loc[c * 128:(c + 1) * 128, :], writes=[xt])
            elif e >= EXT // 128:
                o = e - EXT // 128
                S.dma("sp", xt[:], GA[grow(0, o):grow(0, o) + 128, :], reads=[G_reg], writes=[xt])
                S.dma("sp", xt2[:], GA[grow(1, o):grow(1, o) + 128, :], reads=[G_reg], writes=[xt2])
                S.op("act", lambda en: en.activation(out=xt[:], in_=xt[:], func=AF.Copy, scale=sel_t[:, 2:3]), reads=[xt, sel_t], writes=[xt])
                S.op("dve", lambda en: en.scalar_tensor_tensor(out=xt[:], in0=xt2[:], scalar=sel_t[:, 3:4], in1=xt[:], op0=ALU.mult, op1=ALU.add),
                     reads=[xt2, sel_t, xt], writes=[xt])
            elif 4 <= e < 4 + NTo_:
                t = e - 4
                S.dma("sp", xt[:], x1_loc[t * 128:(t + 1) * 128, :], writes=[xt])
            elif e < 4:
                gt = grow(0, NTo_ - 4 + e)
                S.dma("sp", xt[:], GA[gt:gt + 128, :], reads=[G_reg], writes=[xt])
                S.op("act", lambda en: en.activation(out=xt[:], in_=xt[:], func=AF.Copy, scale=sel_t[:, 0:1]), reads=[xt, sel_t], writes=[xt])
            else:
                gt = grow(1, e - 4 - NTo_)
                S.dma("sp", xt[:], GA[gt:gt + 128, :], reads=[G_reg], writes=[xt])
                S.op("act", lambda en: en.activation(out=xt[:], in_=xt[:], func=AF.Copy, scale=sel_t[:, 1:2]), reads=[xt, sel_t], writes=[xt])

        S.dma("sp", ident_f[:], ident_d, writes=[ident_f])
        S.dma("sp", blk[:], blk_d, writes=[blk])
        S.dma("sp", perm[:], perm_d, writes=[perm])
        S.dma("sp", cv[:], cvec, writes=[cv])
        S.dma("sp", bmodt[:], b_mod, writes=[bmodt])
        S.dma("sp", gate_bc[:, 0, :], bgate, writes=[gate_bc])
        S.op("dve", lambda e: e.tensor_copy(out=ident[:], in_=ident_f[:]), reads=[ident_f], writes=[ident])
        S.op("pool", lambda e: e.memset(epst[:], EPS), writes=[epst])
        S.op("pool", lambda e: e.memset(zeros[:], 0.0), writes=[zeros])
        S.op("dve", lambda e: e.tensor_copy(out=gate_bc[:, 1, :], in_=gate_bc[:, 0, :]), reads=[gate_bc], writes=[gate_bc])
        S.stack = st
        if layer == 0:
            gn = S.sb("gn", [128, 2], F32)
            bi_t = S.sb("bi_t", [128, 8, 640], F32)
            S.dma("sp", gn[:], gains, writes=[gn])
            S.dma("sp", bi_t[:], bias_i, writes=[bi_t])
        else:
            dm_t = S.sb("dm_t", [128, 4, 128], F32)
            lam_t = S.sb("lam_t", [128, 256], F32)
            sub_t = S.sb("sub_t", [128, 128], F32)
            snk_t = S.sb("snk_t", [128, 8], F32)
            fn_t = S.sb("fn_t", [128, D_MODEL], F32)
            S.dma("sp", dm_t[:], dmask, writes=[dm_t])
            S.dma("sp", lam_t[:], lamv, writes=[lam_t])
            S.dma("sp", sub_t[:], subln, writes=[sub_t])
            S.dma("sp", snk_t[:], sinks, writes=[snk_t])
            S.dma("sp", fn_t[:], fnorm, writes=[fn_t])
            lam_s = S.sb("lam_s", [128, 4], F32)
            lprod = S.sb("lprod", [128, 128], F32)
            esnk = S.sb("esnk", [128, 8], F32)
        S.stack = st1
        win = S.sb("win", [128, 8, NCOL], BF16)
        screp = S.sb("screp", [128, 2, 8, 128], F32)
        stage = Rot([S.sb("stage%d" % i, [128, 1024], F32) for i in range(2)])

        S.op("act", lambda e: e.activation(out=cv[:], in_=cv[:], func=AF.Silu), reads=[cv], writes=[cv])
        for w in range(2):
            for k in range(8):
                S.op("act", lambda e: e.activation(out=screp[:, w, k, :], in_=zeros[:], func=AF.Identity,
                                                   bias=cv[:, k, w:w + 1]), reads=[zeros, cv], writes=[screp])
        pmod = banks[5]
        pg = [banks[1], banks[2], banks[3], banks[4]]
        for k in range(8):
            for pi in range(3):
                stg = stage.next()
                S.dma("sp", stg[:], w_mod[k, :, pi * 1024:(pi + 1) * 1024], writes=[stg])
                for jj in range(8):
                    j = pi * 8 + jj
                    S.op("pe", lambda e: e.matmul(pmod[:, j * 2:j * 2 + 2], lhsT=stg[:, jj * 128:(jj + 1) * 128], rhs=cv[:, k, :],
                                                  start=(k == 0 and j == 0), stop=(k == 7), skip_group_check=True),
                         reads=[stg, cv], writes=[pmod])
                if pi == 2:
                    for w in range(2):
                        for n in range(2):
                            S.op("pe", lambda e: e.matmul(pg[w * 2 + n][:], lhsT=screp[:, w, k, :],
                                                          rhs=stg[:, n * 512:(n + 1) * 512],
                                                          start=(k == 0), stop=(k == 7)),
                                 reads=[stg, screp], writes=[pg[w * 2 + n]])
        for w in range(2):
            S.op("dve", lambda e: e.tensor_tensor(out=modt[:, :, w], in0=pmod[:, 0:48].rearrange("p (j w) -> p j w", w=2)[:, :, w],
                                                  in1=bmodt[:], op=ALU.add), reads=[pmod, bmodt], writes=[modt])
            for n in range(2):
                S.op("dve", lambda e: e.tensor_tensor(out=gate_bc[:, w, n * 512:(n + 1) * 512], in0=pg[w * 2 + n][:],
                                                      in1=gate_bc[:, w, n * 512:(n + 1) * 512], op=ALU.add),
                     reads=[pg[w * 2 + n], gate_bc], writes=[gate_bc])
        S.op("dve", lambda e: e.tensor_scalar_add(out=modt[:, 8:16, :], in0=modt[:, 8:16, :], scalar1=1.0),
             reads=[modt], writes=[modt])

        cnt = 0
        for k in range(8):
            for c0 in range(0, NCOL, 1024):
                cn = min(1024, NCOL - c0)
                stg = stage.next()
                S.dma("sp", stg[:, 0:cn], w_in[k, :, c0:c0 + cn], writes=[stg])
                S.op("pool" if cnt % 2 else "dve", lambda e: e.tensor_copy(out=win[:, k, c0:c0 + cn], in_=stg[:, 0:cn]), reads=[stg], writes=[win])
                cnt += 1

        if layer == 1:
            S.op("dve", lambda e: e.tensor_tensor(out=lprod[:, 0:64], in0=lam_t[:, 0:64], in1=lam_t[:, 64:128], op=ALU.mult), reads=[lam_t], writes=[lprod])
            S.op("dve", lambda e: e.tensor_tensor(out=lprod[:, 64:128], in0=lam_t[:, 128:192], in1=lam_t[:, 192:256], op=ALU.mult), reads=[lam_t, lprod], writes=[lprod])
            S.op("dve", lambda e: e.reduce_sum(out=lam_s[:, 0:2], in_=lprod[:].rearrange("p (a d) -> p a d", a=2), axis=mybir.AxisListType.X), reads=[lprod], writes=[lam_s])
            S.op("act", lambda e: e.activation(out=lam_s[:, 0:2], in_=lam_s[:, 0:2], func=AF.Exp), reads=[lam_s], writes=[lam_s])
            S.op("dve", lambda e: e.tensor_tensor(out=lam_s[:, 2:3], in0=lam_s[:, 0:1], in1=lam_s[:, 1:2], op=ALU.subtract), reads=[lam_s], writes=[lam_s])
            S.op("dve", lambda e: e.tensor_scalar(out=lam_s[:, 3:4], in0=lam_s[:, 2:3], scalar1=lam0, scalar2=-1.0, op0=ALU.add, op1=ALU.mult), reads=[lam_s], writes=[lam_s])
            S.op("act", lambda e: e.activation(out=esnk[:], in_=snk_t[:], func=AF.Exp), reads=[snk_t], writes=[esnk])

        xt_r = Rot([S.sb("xt%d" % i, [128, D_MODEL], F32) for i in range(2)])
        xn_r = Rot([S.sb("xn%d" % i, [128, D_MODEL], BF16) for i in range(2)])
        hT_r = Rot([S.sb("hT%d" % i, [128, 8, 512], BF16) for i in range(2)])
        tabC_r = Rot([S.sb("tabC%d" % i, [128, 512], F32) for i in range(1)])
        tabS_r = Rot([S.sb("tabS%d" % i, [128, 512], F32) for i in range(1)])
        sq_r = Rot([S.sb("sq%d" % i, [128, 512], F32) for i in range(2)])
        rs_r = Rot([S.sb("rs%d" % i, [128, 512], F32) for i in range(2)])
        qn_r = Rot([S.sb("qn%d" % i, [128, 512], F32) for i in range(3)])
        t2_r = Rot([S.sb("t2%d" % i, [128, 512], F32) for i in range(2)])
        fo_r = Rot([S.sb("fo%d" % i, [128, 512], BF16) for i in range(4)])
        vst = {}
        for name in vS:
            h, d = vdims[name]
            vst[name] = Rot([S.sb("vst_%s%d" % (name, i), [128, h, d], BF16) for i in range(2)])
            for t in vst[name].tiles:
                S.op("pool", lambda e: e.memset(t[:], 1.0), writes=[t])
        zst_r = Rot([S.sb("zst%d" % i, [128, D_MODEL], BF16) for i in range(2)])
        bA = Rot([banks[1], banks[2], banks[3]])
        bB = Rot([banks[4], banks[5]])

        def phase1_block(u0, ntiles, src, src_row0, w, fm_list, tm_list, rope_col0):
            ntok = ntiles * 128
            hT = hT_r.next()
            for ti in range(ntiles):
                xt = xt_r.next()
                ss = ss_r.next()
                xn = xn_r.next()
                load_x_tile(xt, u0 // 128 + ti)
                S.op("act", lambda e: e.activation(out=junk[:], in_=xt[:], func=AF.Square, accum_out=ss[:, 0:1]), reads=[xt], writes=[junk, ss])
                S.op("act", lambda e: e.activation(out=ss[:, 1:2], in_=ss[:, 0:1], func=AF.Ln, scale=1.0 / D_MODEL, bias=epst[:]), reads=[ss, epst], writes=[ss])
                S.op("act", lambda e: e.activation(out=ss[:, 1:2], in_=ss[:, 1:2], func=AF.Exp, scale=-0.5), reads=[ss], writes=[ss])
                S.op("act", lambda e: e.activation(out=xn[:], in_=xt[:], func=AF.Copy, scale=ss[:, 1:2]), reads=[xt, ss], writes=[xn])
                for k in range(8):
                    S.op("pe", lambda e: e.transpose(out=pTb[:, k, :], in_=xn[:, k * 128:(k + 1) * 128], identity=ident[:]),
                         reads=[xn, ident], writes=[pTb_t])
                for k in range(8):
                    S.op("dve", lambda e: e.tensor_scalar(out=hT[:, k, ti * 128:(ti + 1) * 128], in0=pTb[:, k, :],
                                                          scalar1=modt[:, 8 + k, w:w + 1], scalar2=modt[:, k, w:w + 1],
                                                          op0=ALU.mult, op1=ALU.add), reads=[pTb_t, modt], writes=[hT])
            rope_needed = any(fm[i][3] for i in fm_list) and rope_col0 is not None
            if rope_needed:
                tC = tabC_r.next()
                tS = tabS_r.next()
                S.dma("sp", tC[:, 0:ntok], ropeC[:, rope_col0:rope_col0 + ntok], writes=[tC])
                S.dma("sp", tS[:, 0:ntok], ropeS[:, rope_col0:rope_col0 + ntok], writes=[tS])
            chs = [dict(i=i) for i in fm_list]

            def st_main(ch):
                i = ch["i"]
                pa = bA.next()
                for k in range(8):
                    S.op("pe", lambda e: e.matmul(pa[:, 0:ntok], lhsT=win[:, k, i * 128:(i + 1) * 128], rhs=hT[:, k, 0:ntok],
                                                  start=(k == 0), stop=(k == 7)), reads=[win, hT], writes=[pa])
                ch["pa"] = pa

            def st_norm(ch):
                i = ch["i"]
                name, _, nkind, roped = fm[i]
                pa = ch["pa"]
                fo = fo_r.next()
                ch["fo"] = fo
                do_rope = roped and rope_col0 is not None
                ch["do_rope"] = do_rope
                if nkind is None and not do_rope:
                    S.op("act", lambda e: e.activation(out=fo[:, 0:ntok], in_=pa[:, 0:ntok], func=AF.Copy), reads=[pa], writes=[fo])
                    return
                qn = qn_r.next()
                ch["qn"] = qn
                if nkind is not None:
                    sq = sq_r.next()
                    rs = rs_r.next()
                    pb = bB.next()
                    gcol = 0 if nkind == "q" else 1
                    S.op("act", lambda e: e.activation(out=sq[:, 0:ntok], in_=pa[:, 0:ntok], func=AF.Square), reads=[pa], writes=[sq])
                    S.op("pe", lambda e: e.matmul(pb[:, 0:ntok], lhsT=blk[:], rhs=sq[:, 0:ntok], start=True, stop=True), reads=[blk, sq], writes=[pb])
                    S.op("act", lambda e: e.activation(out=rs[:, 0:ntok], in_=pb[:, 0:ntok], func=AF.Ln, bias=epst[:]), reads=[pb, epst], writes=[rs])
                    S.op("act", lambda e: e.activation(out=rs[:, 0:ntok], in_=rs[:, 0:ntok], func=AF.Exp, scale=-0.5), reads=[rs], writes=[rs])
                    dst = qn if do_rope else fo
                    S.op("dve", lambda e: e.scalar_tensor_tensor(out=dst[:, 0:ntok], in0=pa[:, 0:ntok], scalar=gn[:, gcol:gcol + 1],
                                                                 in1=rs[:, 0:ntok], op0=ALU.mult, op1=ALU.mult),
                         reads=[pa, gn, rs], writes=[dst])
                else:
                    S.op("act", lambda e: e.activation(out=qn[:, 0:ntok], in_=pa[:, 0:ntok], func=AF.Copy), reads=[pa], writes=[qn])

            def st_rope(ch):
                i = ch["i"]
                fo = ch["fo"]
                if ch["do_rope"]:
                    qn = ch["qn"]
                    pb2 = bB.next()
                    t1 = t1_r.next()
                    t2 = t2_r.next()
                    S.op("pe", lambda e: e.matmul(pb2[:, 0:ntok], lhsT=perm[:], rhs=qn[:, 0:ntok], start=True, stop=True), reads=[perm, qn], writes=[pb2])
                    S.op("pool", lambda e: e.tensor_tensor(out=t1[:, 0:ntok], in0=qn[:, 0:ntok], in1=tC[:, 0:ntok], op=ALU.mult), reads=[qn, tC], writes=[t1])
                    S.op("dve", lambda e: e.tensor_tensor(out=t2[:, 0:ntok], in0=pb2[:, 0:ntok], in1=tS[:, 0:ntok], op=ALU.mult), reads=[pb2, tS], writes=[t2])
                    S.op("dve", lambda e: e.tensor_tensor(out=fo[:, 0:ntok], in0=t1[:, 0:ntok], in1=t2[:, 0:ntok], op=ALU.add), reads=[t1, t2], writes=[fo])
                S.dma("sp", fmS[i, :, u0:u0 + ntok], fo[:, 0:ntok], reads=[fo], writes=[fm_reg], sem_tile=fo)

            nchs = len(chs)
            for step in range(nchs + 2):
                if step < nchs:
                    st_main(chs[step])
                if 0 <= step - 1 < nchs:
                    st_norm(chs[step - 1])
                if 0 <= step - 2 < nchs:
                    st_rope(chs[step - 2])
            for name in tm_list:
                col0 = tmoff[name]
                ncols = dict((t[0], t[2]) for t in tm)[name]
                for ti in range(ntiles):
                    for n0 in range(0, ncols, 512):
                        nn = min(512, ncols - n0)
                        pa = bA.next()
                        for k in range(8):
                            S.op("pe", lambda e: e.matmul(pa[:, 0:nn], lhsT=hT[:, k, ti * 128:(ti + 1) * 128],
                                                          rhs=win[:, k, col0 + n0:col0 + n0 + nn], start=(k == 0), stop=(k == 7)),
                                 reads=[win, hT], writes=[pa])
                        if name == "z":
                            if n0 == 0:
                                zst = zst_r.next()
                            S.op("act", lambda e: e.activation(out=zst[:, n0:n0 + nn], in_=pa[:, 0:nn], func=AF.Silu), reads=[pa], writes=[zst])
                            if n0 + nn == ncols:
                                S.dma("sp", zS[u0 + ti * 128:u0 + (ti + 1) * 128, :], zst[:], reads=[zst], writes=[z_reg], sem_tile=zst)
                        else:
                            h, d = vdims[name]
                            dv = d - 1
                            vt = vst[name].next()
                            S.op("dve", lambda e: e.tensor_copy(out=vt[:, :, 0:dv], in_=pa[:, 0:nn].rearrange("p (h d) -> p h d", d=dv)),
                                 reads=[pa], writes=[vt])
                            S.dma("sp", vS[name][u0 + ti * 128:u0 + (ti + 1) * 128, :], vt[:].rearrange("p h d -> p (h d)"),
                                  reads=[vt], writes=[v_reg], sem_tile=vt)


        all_fm = list(range(NFM))
        all_tm = [t[0] for t in tm]
        lk = [fmi[n] for n in cfg["local_k"]]
        dk = [fmi[n] for n in cfg["dense_k"]]
        xc_ap = xc_in
        phase1_block(U_C, 2, xc_ap, 0, 1, all_fm, all_tm, None)
        eblocks = list(range(EXT // 512))
        eblocks = [b for b in eblocks if HALO <= b * 512 < HALO + HALF] + [b for b in eblocks if not (HALO <= b * 512 < HALO + HALF)]
        for b in eblocks:
            u0 = b * 512
            own = HALO <= u0 < HALO + HALF
            if own:
                phase1_block(u0, 4, x_u, u0, 0, all_fm, all_tm, u0)
            else:
                phase1_block(u0, 4, x_u, u0, 0, lk, cfg["local_v"], u0)
        for b in range(HALF // 512):
            u0 = U_O + b * 512
            phase1_block(u0, 4, x_u, u0, 0, dk, cfg["dense_v"], u0)

        S.barrier()
        st1.close()
        st2 = contextlib.ExitStack()
        S.stack = st2
        wout = S.sb("wout", [128, 8, D_MODEL], BF16)
        stage2 = Rot([S.sb("stage2_%d" % i, [128, D_MODEL], F32) for i in range(2)])
        for k in range(8):
            stg = stage2.next()
            S.dma("sp", stg[:], w_out[k], writes=[stg])
            S.op("pool" if k % 2 else "dve", lambda e: e.tensor_copy(out=wout[:, k, :], in_=stg[:]), reads=[stg], writes=[wout])
        bS = Rot([banks[1], banks[2], banks[3]])
        bACC = Rot([banks[5], banks[6], banks[7]])
        pT_r = Rot([S.sb("pT%d" % i, [128, 512], BF16) for i in range(3)])
        sb_r = Rot([S.sb("sbias%d" % i, [128, 512], F32) for i in range(2)])
        nbuf_d = 1 if layer == 0 else 2
        KT_r = Rot([S.sb("KT%d" % i, [128, NKD], BF16) for i in range(nbuf_d)])
        VDW = 130 if layer == 0 else 129
        VD_r = Rot([S.sb("VD%d" % i, [128, NKC, VDW], BF16) for i in range(nbuf_d)])
        q_r = Rot([S.sb("qblk%d" % i, [128, 512], BF16) for i in range(6)])
        NLK = 9
        kl_r = Rot([S.sb("kl%d" % i, [128, NLK * 128], BF16) for i in range(2)])
        VLW = 130
        vl_r = Rot([S.sb("vl%d" % i, [128, NLK, VLW], BF16) for i in range(2)])
        y_t = [S.sb("y%d" % i, [128, D_MODEL], F32) for i in range(4)]
        rec_r = Rot([S.sb("rec%d" % i, [128, 8], F32) for i in range(4)])
        zl_r = Rot([S.sb("zl%d" % i, [128, D_MODEL], BF16) for i in range(1)])
        yb_r = Rot([S.sb("yb%d" % i, [128, D_MODEL], BF16) for i in range(1)])
        yT_r = Rot([S.sb("yT%d" % i, [128, 8, 128], BF16) for i in range(1)])
        xo_r = Rot([S.sb("xo%d" % i, [128, D_MODEL], F32) for i in range(1)])
        res_r = Rot([S.sb("res%d" % i, [128, D_MODEL], F32) for i in range(2)])
        be_r = Rot([S.sb("be%d" % i, [128, 7 * 128], F32) for i in range(2)]) if layer == 0 else None
        dcache = {}

        def load_dense(kname, vname, vc0, vw):
            key = (kname, vname, vc0)
            if dcache.get("key") == key:
                return dcache["KT"], dcache["VD"]
            KT = KT_r.next()
            VD = VD_r.next()
            i = fmi[kname]
            S.dma("sp", KT[:, 0:HALF], fmS[i, :, U_OWN:U_OWN + HALF], reads=[fm_reg], writes=[KT])
            S.dma("sp", KT[:, HALF:2 * HALF], fmS[i, :, U_O:U_O + HALF], reads=[fm_reg], writes=[KT])
            S.dma("sp", KT[:, 2 * HALF:NKD], fmS[i, :, U_C:U_C + CTX], reads=[fm_reg], writes=[KT])
            for (c0, u0, n) in ((0, U_OWN, HALF), (HALF // 128, U_O, HALF), (2 * HALF // 128, U_C, CTX)):
                S.dma("sp", VD[:, c0:c0 + n // 128, 0:vw], vS[vname][u0:u0 + n, vc0:vc0 + vw].rearrange("(k p) c -> p k c", p=128),
                      reads=[v_reg], writes=[VD])
            dcache.update(key=key, KT=KT, VD=VD)
            return KT, VD

        def attend(qt, pbase, NQ, kchunks, acc_list, vwidth, bias_fn=None):
            nq = NQ // 128
            n = len(kchunks)
            pend = []
            first = {}

            def issue_s(j):
                kt, kc0, vt, vap = kchunks[j]
                ps = bS.next()
                S.op("pe", lambda e: e.matmul(ps[:, 0:NQ], lhsT=kt[pbase:pbase + 64, kc0:kc0 + 128], rhs=qt[pbase:pbase + 64, 0:NQ],
                                              start=True, stop=True), reads=[kt, qt], writes=[ps])
                pT = pT_r.next()
                b = bias_fn(j) if bias_fn is not None else None
                if b is not None:
                    btile, bap = b
                    sb = sb_r.next()
                    S.op("dve", lambda e: e.scalar_tensor_tensor(out=sb[:, 0:NQ], in0=ps[:, 0:NQ], scalar=SCALE, in1=bap,
                                                                 op0=ALU.mult, op1=ALU.add), reads=[ps, btile], writes=[sb])
                    S.op("act", lambda e: e.activation(out=pT[:, 0:NQ], in_=sb[:, 0:NQ], func=AF.Exp), reads=[sb], writes=[pT])
                else:
                    S.op("act", lambda e: e.activation(out=pT[:, 0:NQ], in_=ps[:, 0:NQ], func=AF.Exp, scale=SCALE), reads=[ps], writes=[pT])
                return pT

            def issue_pv(j, pT):
                kt, kc0, vt, vap = kchunks[j]
                for s in range(nq):
                    acc, c0 = acc_list[s]
                    fst = first.get(id(acc), True)
                    first[id(acc)] = False
                    S.op("pe", lambda e: e.matmul(acc[:, c0:c0 + vwidth], lhsT=pT[:, s * 128:(s + 1) * 128], rhs=vap,
                                                  start=(j == 0 and fst), stop=(j == n - 1), skip_group_check=True),
                         reads=[pT, vt], writes=[acc])

            prev = None
            for j in range(n):
                pT = issue_s(j)
                if prev is not None:
                    issue_pv(prev[0], prev[1])
                prev = (j, pT)
            issue_pv(prev[0], prev[1])

        onesel_f = S.sb("onesel_f", [128, 2, 2], F32)
        S.op("pool", lambda e: e.memset(onesel_f[:], 0.0), writes=[onesel_f])
        S.op("pool", lambda e: e.memset(onesel_f[:, 0, 0:1], 1.0), writes=[onesel_f])
        S.op("pool", lambda e: e.memset(onesel_f[:, 1, 1:2], 1.0), writes=[onesel_f])
        den_acc = S.sb("den_acc", [128, 1024], F32)
        if layer == 1:
            dmb = S.sb("dmb", [128, 3, 384], F32)
            S.op("pool", lambda e: e.memset(dmb[:], 0.0), writes=[dmb])
            for var, (pi, ni) in enumerate(((2, 3), (0, 3), (2, 1))):
                S.op("dve", lambda e: e.tensor_copy(out=dmb[:, var, 0:128], in_=dm_t[:, pi, :]), reads=[dm_t], writes=[dmb])
                S.op("dve", lambda e: e.tensor_copy(out=dmb[:, var, 256:384], in_=dm_t[:, ni, :]), reads=[dm_t], writes=[dmb])
        pT2_r = Rot([S.sb("pT2_%d" % i, [128, 1024], BF16) for i in range(3)])
        oT_r = Rot([S.sb("oT%d" % i, [128, 512], F32) for i in range(2)])
        pairs = SH["pairs"]

        def dense_pair(qt, KT, VD, vap_fn, vw, accs, den=None):
            n = NKC

            def issue_s(j):
                pt, ta, tb = pairs[j % 2]
                for m, tt in ((0, ta), (1, tb)):
                    S.op("pe", lambda e: e.matmul(tt[:, 0:512], lhsT=KT[64 * m:64 * m + 64, j * 128:(j + 1) * 128],
                                                  rhs=qt[64 * m:64 * m + 64, 0:512], start=True, stop=True), reads=[KT, qt], writes=[tt])
                pT = pT2_r.next()
                S.op("act", lambda e: e.activation(out=pT[:], in_=pt[:, :], func=AF.Exp, scale=SCALE), reads=[ta, tb], writes=[pT])
                return pT

            def issue_pv(j, pT):
                for m in range(2):
                    S.op("pe", lambda e: e.matmul(accs[m][0:vw, 0:512], lhsT=vap_fn(j, m), rhs=pT[:, m * 512:(m + 1) * 512],
                                                  start=(j == 0), stop=(j == n - 1)), reads=[pT, VD], writes=[accs[m]])
                if den is not None:
                    if j == 0:
                        S.op("dve", lambda e: e.tensor_copy(out=den_acc[:], in_=pT[:]), reads=[pT], writes=[den_acc])
                    else:
                        S.op("dve", lambda e: e.tensor_tensor(out=den_acc[:], in0=den_acc[:], in1=pT[:], op=ALU.add), reads=[pT, den_acc], writes=[den_acc])
                    if j == n - 1:
                        for m in range(2):
                            S.op("pe", lambda e: e.matmul(den[0:2, 0:512], lhsT=onesel_f[:, m, :], rhs=den_acc[:, m * 512:(m + 1) * 512],
                                                          start=(m == 0), stop=(m == 1)), reads=[den_acc, onesel_f], writes=[den])

            prev = None
            for j in range(n):
                pT = issue_s(j)
                if prev is not None:
                    issue_pv(prev[0], prev[1])
                prev = (j, pT)
            issue_pv(prev[0], prev[1])

        def untranspose(acc, rows, fin, width):
            oT = oT_r.next()
            S.op("act", lambda e: e.activation(out=oT[0:rows, :], in_=acc[0:rows, 0:512], func=AF.Copy), reads=[acc], writes=[oT])
            for sidx in range(4):
                S.op("pe", lambda e: e.transpose(out=fin[:, sidx * width:sidx * width + rows], in_=oT[0:rows, sidx * 128:(sidx + 1) * 128],
                                                 identity=ident_f[0:rows, 0:rows]), reads=[oT, ident_f], writes=[fin])

        wide_i = [0]

        def wide_a(job):
            qt, pbase, kl, nch, nb, bias = job["qt"], job["pbase"], job["kl"], job["nch"], job["nb"], job["bias"]
            pt, ta, tb = pairs[wide_i[0] % 2]
            wide_i[0] += 1
            for j in range(nch):
                tt = ta if j < 4 else tb
                S.op("pe", lambda e: e.matmul(pt[:, j * 128:(j + 1) * 128], lhsT=kl[pbase:pbase + 64, j * 128:(j + 1) * 128],
                                              rhs=qt[pbase:pbase + 64, 0:128], start=True, stop=True), reads=[kl, qt], writes=[tt])
            pT = pT2_r.next()
            used = [ta] + ([tb] if nch > 4 else [])
            if bias is not None:
                btile, bap = bias
                sbw = stage2.next()
                S.op("dve", lambda e: e.scalar_tensor_tensor(out=sbw[:, 0:nb * 128], in0=pt[:, 0:nb * 128], scalar=SCALE, in1=bap,
                                                             op0=ALU.mult, op1=ALU.add), reads=used + [btile], writes=[sbw])
                S.op("act", lambda e: e.activation(out=pT[:, 0:nb * 128], in_=sbw[:, 0:nb * 128], func=AF.Exp), reads=[sbw], writes=[pT])
                if nch > nb:
                    S.op("act", lambda e: e.activation(out=pT[:, nb * 128:nch * 128], in_=pt[:, nb * 128:nch * 128], func=AF.Exp, scale=SCALE),
                         reads=used, writes=[pT])
            else:
                S.op("act", lambda e: e.activation(out=pT[:, 0:nch * 128], in_=pt[:, 0:nch * 128], func=AF.Exp, scale=SCALE), reads=used, writes=[pT])
            job["pT"] = pT

        def wide_b(job):
            pT, vl, vc0, nch = job["pT"], job["vl"], job["vc0"], job["nch"]
            acc = bACC.next()
            for j in range(nch):
                S.op("pe", lambda e: e.matmul(acc[:, 0:65], lhsT=pT[:, j * 128:(j + 1) * 128], rhs=vl[:, j, vc0:vc0 + 65],
                                              start=(j == 0), stop=(j == nch - 1)), reads=[pT, vl], writes=[acc])
            finish_head([(acc, 0)], 65, [job["yt"]], job["ycol"], extra_den=job.get("extra_den"))

        def run_wide(jobs):
            prev = None
            for job in jobs:
                wide_a(job)
                if prev is not None:
                    wide_b(prev)
                prev = job
            if prev is not None:
                wide_b(prev)

        def finish_head(acc_list, vwidth, y_tiles, ycol, extra_den=None, scale_ap=None):
            dv = vwidth - 1
            for s, (acc, c0) in enumerate(acc_list):
                rec = rec_r.next()
                if extra_den is not None:
                    S.op("dve", lambda e: e.tensor_tensor(out=rec[:, 0:1], in0=acc[:, c0 + dv:c0 + dv + 1], in1=extra_den, op=ALU.add),
                         reads=[acc, esnk], writes=[rec])
                    S.op("dve", lambda e: e.reciprocal(out=rec[:, 1:2], in_=rec[:, 0:1]), reads=[rec], writes=[rec])
                else:
                    S.op("dve", lambda e: e.reciprocal(out=rec[:, 1:2], in_=acc[:, c0 + dv:c0 + dv + 1]), reads=[acc], writes=[rec])
                yt = y_tiles[s]
                S.op("act", lambda e: e.activation(out=yt[:, ycol:ycol + dv], in_=acc[:, c0:c0 + dv], func=AF.Copy, scale=rec[:, 1:2]),
                     reads=[acc, rec], writes=[yt])

        def out_tile(yt, u_tok, src, src_row, w, dst, dst_row):
            zl = zl_r.next()
            yb = yb_r.next()
            yT = yT_r.next()
            xo = xo_r.next()
            res = res_r.next()
            S.dma("sp", zl[:], zS[u_tok:u_tok + 128, :], reads=[z_reg], writes=[zl])
            load_x_tile(xo, u_tok // 128)
            S.op("dve", lambda e: e.tensor_tensor(out=yb[:], in0=yt[:], in1=zl[:], op=ALU.mult), reads=[yt, zl], writes=[yb])
            for k in range(8):
                S.op("pe", lambda e: e.transpose(out=pTb[:, k, :], in_=yb[:, k * 128:(k + 1) * 128], identity=ident[:]),
                     reads=[yb, ident], writes=[pTb_t])
            S.op("act", lambda e: e.activation(out=yT[:].rearrange("p k t -> p (k t)"), in_=pTb[:].rearrange("p k t -> p (k t)"), func=AF.Copy),
                 reads=[pTb_t], writes=[yT])
            for n in range(2):
                po = bACC.next()
                for k in range(8):
                    S.op("pe", lambda e: e.matmul(po[:], lhsT=yT[:, k, :], rhs=wout[:, k, n * 512:(n + 1) * 512], start=(k == 0), stop=(k == 7)),
                         reads=[yT, wout], writes=[po])
                S.op("dve", lambda e: e.tensor_tensor(out=res[:, n * 512:(n + 1) * 512], in0=po[:], in1=gate_bc[:, w, n * 512:(n + 1) * 512], op=ALU.mult),
                     reads=[po, gate_bc], writes=[res])
            S.op("pool", lambda e: e.tensor_tensor(out=res[:], in0=res[:], in1=xo[:], op=ALU.add), reads=[res, xo], writes=[res])
            if last:
                ss = ss_r.next()
                S.op("act", lambda e: e.activation(out=junk[:], in_=res[:], func=AF.Square, accum_out=ss[:, 0:1]), reads=[res], writes=[junk, ss])
                S.op("act", lambda e: e.activation(out=ss[:, 1:2], in_=ss[:, 0:1], func=AF.Ln, scale=1.0 / D_MODEL, bias=epst[:]), reads=[ss, epst], writes=[ss])
                S.op("act", lambda e: e.activation(out=ss[:, 1:2], in_=ss[:, 1:2], func=AF.Exp, scale=-0.5), reads=[ss], writes=[ss])
                S.op("act", lambda e: e.activation(out=xo[:], in_=res[:], func=AF.Copy, scale=ss[:, 1:2]), reads=[res, ss], writes=[xo])
                S.op("dve", lambda e: e.tensor_tensor(out=res[:], in0=xo[:], in1=fn_t[:], op=ALU.mult), reads=[xo, fn_t], writes=[res])
            S.dma("sp", dst[dst_row:dst_row + 128, :], res[:], reads=[res], sem_tile=res)
            return res

        out_tiles = []

        def load_q(name, u0, n):
            qt = q_r.next()
            S.dma("sp", qt[:, 0:n], fmS[fmi[name], :, u0:u0 + n], reads=[fm_reg], writes=[qt])
            return qt

        def load_local(knames_idx, vname, vc0, vw, utiles):
            kl = kl_r.next()
            vl = vl_r.next()
            pos = 0
            for (ut0, cnt) in utiles:
                S.dma("sp", kl[:, pos * 128:(pos + cnt) * 128], fmS[knames_idx, :, ut0 * 128:(ut0 + cnt) * 128], reads=[fm_reg], writes=[kl])
                S.dma("sp", vl[:, pos:pos + cnt, 0:vw], vS[vname][ut0 * 128:(ut0 + cnt) * 128, vc0:vc0 + vw].rearrange("(k p) c -> p k c", p=128),
                      reads=[v_reg], writes=[vl])
                pos += cnt
            return kl, vl

        UC_T = U_C // 128

        if layer == 0:
            for t in range(2):
                yt = y_t[t]
                u0 = U_C + t * 128
                jobs = []
                kl, vl = load_local(fmi["ka"], "va", 0, 130, [(UC_T, 2)])
                for c in range(4):
                    qt = load_q("qa%d" % c, u0, 128)
                    for s_ in range(2):
                        jobs.append(dict(qt=qt, pbase=64 * s_, kl=kl, vl=vl, vc0=s_ * 65, nch=2, nb=0, bias=None, yt=yt, ycol=(c + 4 * s_) * 64))
                run_wide(jobs)
                for c in range(4):
                    kl, vl = load_local(fmi["kb%d" % c], "vb", c * 130, 130, [(UC_T, 2)])
                    qt = load_q("qb%d" % c, u0, 128)
                    run_wide([dict(qt=qt, pbase=64 * s_, kl=kl, vl=vl, vc0=s_ * 65, nch=2, nb=0, bias=None, yt=yt, ycol=512 + (2 * c + s_) * 64)
                              for s_ in range(2)])
                out_tiles.append(out_tile(yt, u0, xc_in, t * 128, 1, out_c, t * 128))

        for qb in range(NBo):
            u0 = U_OWN + qb * 512
            if layer == 0:
                KT, VD = load_dense("ka", "va", 0, 130)
                for c in range(4):
                    qt = load_q("qa%d" % c, u0, 512)
                    accs = [banks[5], banks[6]]
                    dense_pair(qt, KT, VD, lambda j, m: VD[:, j, m * 65:(m + 1) * 65], 65, accs)
                    for s in range(2):
                        head = c + 4 * s
                        fin = banks[7]
                        untranspose(accs[s], 65, fin, 65)
                        finish_head([(fin, i * 65) for i in range(4)], 65, y_t, head * 64)
            else:
                for h in range(4):
                    KT, VD = load_dense("kc%d" % h, "vc", h * 129, 129)
                    qt = load_q("qc%d" % h, u0, 512)
                    accs = [banks[5], banks[6]]
                    den = banks[7]
                    dense_pair(qt, KT, VD, lambda j, m: VD[:, j, 0:128], 128, accs, den=den)
                    fins = [banks[1], banks[2]]
                    untranspose(accs[0], 128, fins[0], 128)
                    untranspose(accs[1], 128, fins[1], 128)
                    dfin = banks[3]
                    untranspose(den, 2, dfin, 2)
                    o_m = [[(fins[0], i * 128, i * 2 + 0) for i in range(4)], [(fins[1], i * 128, i * 2 + 1) for i in range(4)]]
                    for s in range(4):
                        rec = rec_r.next()
                        yt = y_t[s]
                        a0, c0, d0 = o_m[0][s]
                        a1, c1, d1 = o_m[1][s]
                        S.op("dve", lambda e: e.reciprocal(out=rec[:, 0:1], in_=dfin[:, d0:d0 + 1]), reads=[dfin], writes=[rec])
                        S.op("dve", lambda e: e.reciprocal(out=rec[:, 1:2], in_=dfin[:, d1:d1 + 1]), reads=[dfin, rec], writes=[rec])
                        S.op("dve", lambda e: e.tensor_tensor(out=rec[:, 2:3], in0=rec[:, 1:2], in1=lam_s[:, 3:4], op=ALU.mult), reads=[rec, lam_s], writes=[rec])
                        t1 = t1_r.next()
                        S.op("act", lambda e: e.activation(out=t1[:, 0:128], in_=a0[:, c0:c0 + 128], func=AF.Copy, scale=rec[:, 0:1]), reads=[a0, rec], writes=[t1])
                        S.op("dve", lambda e: e.scalar_tensor_tensor(out=t1[:, 128:256], in0=a1[:, c1:c1 + 128], scalar=rec[:, 2:3], in1=t1[:, 0:128],
                                                                     op0=ALU.mult, op1=ALU.add), reads=[a1, rec, t1], writes=[t1])
                        S.op("act", lambda e: e.activation(out=t1[:, 256:384], in_=t1[:, 128:256], func=AF.Square, accum_out=rec[:, 3:4]), reads=[t1], writes=[t1, rec])
                        S.op("act", lambda e: e.activation(out=rec[:, 4:5], in_=rec[:, 3:4], func=AF.Ln, scale=1.0 / 128, bias=epst[:]), reads=[rec, epst], writes=[rec])
                        S.op("act", lambda e: e.activation(out=rec[:, 4:5], in_=rec[:, 4:5], func=AF.Exp, scale=-0.5), reads=[rec], writes=[rec])
                        S.op("dve", lambda e: e.tensor_scalar_mul(out=rec[:, 4:5], in0=rec[:, 4:5], scalar1=1.0 - lam0), reads=[rec], writes=[rec])
                        S.op("dve", lambda e: e.scalar_tensor_tensor(out=yt[:, h * 128:(h + 1) * 128], in0=t1[:, 128:256], scalar=rec[:, 4:5], in1=sub_t[:],
                                                                     op0=ALU.mult, op1=ALU.mult), reads=[t1, rec, sub_t], writes=[yt])
            for tl in range(4):
                t = qb * 4 + tl
                ut = U_OWN // 128 + t
                yt = y_t[tl]
                if layer == 0:
                    edge = t < 2 or t >= NTo - 2
                    if edge:
                        et = t if t < 2 else 2 + (t - (NTo - 2))
                        J = 7
                        runs = [(ut - 3, 7), (UC_T, 2)]
                    else:
                        J = 5
                        runs = [(ut - 2, 5), (UC_T, 2)]
                    for c in range(4):
                        kl, vl = load_local(fmi["kb%d" % c], "vb", c * 130, 130, runs)
                        qt = load_q("qb%d" % c, ut * 128, 128)
                        if not edge:
                            run_wide([dict(qt=qt, pbase=64 * s_, kl=kl, vl=vl, vc0=s_ * 65, nch=7, nb=5,
                                           bias=(bi_t, bi_t[:, 2 * c + s_, 0:640]), yt=yt, ycol=512 + (2 * c + s_) * 64) for s_ in range(2)])
                            continue
                        for s in range(2):
                            head = 2 * c + s
                            if edge:
                                be = be_r.next()
                                S.dma("sp", be[:], bias_e[et, head], writes=[be])
                                bfn = (lambda j, be=be: (be, be[:, j * 128:(j + 1) * 128]) if j < 7 else None)
                            else:
                                bfn = (lambda j, head=head: (bi_t, bi_t[:, head, j * 128:(j + 1) * 128]) if j < 5 else None)
                            acc = bACC.next()
                            attend(qt, 64 * s, 128, [(kl, j * 128, vl, vl[:, j, s * 65:(s + 1) * 65]) for j in range(J + 2)],
                                   [(acc, 0)], 65, bias_fn=bfn)
                            finish_head([(acc, 0)], 65, [yt], 512 + head * 64)
                else:
                    kl, vl = load_local(fmi["kd"], "vd", 0, 130, [(ut - 1, 3), (UC_T, 2)])
                    var = 1 if t == 0 else (2 if t == NTo - 1 else 0)
                    jobs = []
                    for c in range(4):
                        qt = load_q("qd%d" % c, ut * 128, 128)
                        for s_ in range(2):
                            head = c + 4 * s_
                            jobs.append(dict(qt=qt, pbase=64 * s_, kl=kl, vl=vl, vc0=s_ * 65, nch=5, nb=3, bias=(dmb, dmb[:, var, :]),
                                             yt=yt, ycol=512 + head * 64, extra_den=esnk[:, head:head + 1]))
                    run_wide(jobs)
                out_tiles.append(out_tile(yt, ut * 128, x_u, ut * 128, 0, out_x, t * 128))
        if last:
            S.finish(out_tiles)
        else:
            S.barrier()
        st2.close()
    if not last:
        S.release_dsems()
        S.new_epoch()


def build_fused(SEQ, B):
    HALF = SEQ // 2
    EXT = HALF + 2 * HALO
    nc = bass.Bass("TRN2", target_bir_lowering=False)

    def din(name, shape):
        return nc.dram_tensor(name, list(shape), F32, kind="ExternalInput").ap()

    SH = dict(x_u=din("x_u", [EXT + HALF, D_MODEL]), xc=din("xc", [CTX, D_MODEL]), cvec=din("cvec", [128, 8, 2]),
              ident=din("ident", [128, 128]), blk=din("blk", [128, 128]), perm=din("perm", [128, 128]),
              ropeC=din("ropeC", [128, EXT + HALF]), ropeS=din("ropeS", [128, EXT + HALF]), sel=din("sel", [128, 4]))
    SH["x1_loc"] = nc.dram_tensor("x1_loc", [HALF, D_MODEL], F32).ap()
    SH["xc1_loc"] = nc.dram_tensor("xc1_loc", [CTX, D_MODEL], F32).ap()
    SH["GA"] = nc.dram_tensor("x1_all", [2 * HALF, D_MODEL], F32).ap()
    with contextlib.ExitStack() as st0:
        S = Sched(nc, st0)
        SH["pTb_t"] = S.ps("bankT", [128, 8, 128], BF16)
        pairA = st0.enter_context(nc.psum_tensor("ps_pairA", [128, 1024], F32))
        pairB = st0.enter_context(nc.psum_tensor("ps_pairB", [128, 1024], F32))
        b1 = Tile("bank1", pairA[:, 0:512], excl=True)
        b2 = Tile("bank2", pairA[:, 512:1024], excl=True)
        b3 = Tile("bank3", pairB[:, 0:512], excl=True)
        b4 = Tile("bank4", pairB[:, 512:1024], excl=True)
        SH["pairs"] = [(pairA, b1, b2), (pairB, b3, b4)]
        SH["banks"] = [SH["pTb_t"], b1, b2, b3, b4] + [S.ps("bank%d" % i, [128, 512], F32) for i in range(5, 8)]
        SH["G_reg"] = Tile("G_reg")
        emit_layer(nc, S, SH, 0, SEQ)
        S.sems["cc"] = st0.enter_context(nc.semaphore("s_cc"))
        RC = min(512, HALF)
        for i in range(HALF // RC):
            nc.gpsimd.collective_compute("AllGather", ALU.bypass, replica_groups=[[2 * b, 2 * b + 1] for b in range(B)],
                                         ins=[SH["x1_loc"][i * RC:(i + 1) * RC, :]],
                                         outs=[SH["GA"][i * 2 * RC:(i + 1) * 2 * RC, :]]).then_inc(S.sems["cc"], 1)
        SH["G_reg"].last_w = ("cc", HALF // RC)
        emit_layer(nc, S, SH, 1, SEQ)
    return nc


def rope_tables(pos):
    row = (pos // GRID_W).astype(np.float32)
    col = (pos % GRID_W).astype(np.float32)
    q = HD // 4
    inv = (10000.0 ** (-np.arange(q, dtype=np.float32) / q)).astype(np.float32)
    ar = row[None, :] * inv[:, None]
    ac = col[None, :] * inv[:, None]
    cr, sr, cc, sc = np.cos(ar), np.sin(ar), np.cos(ac), np.sin(ac)
    C = np.concatenate([cr, cr, cc, cc], axis=0)
    Sg = np.concatenate([-sr, sr, -sc, sc], axis=0)
    return (np.concatenate([C, C], 0).astype(np.float32), np.concatenate([Sg, Sg], 0).astype(np.float32))


def nbr_bias(rpb, g, gk, NT):
    rows = NT * 2
    out = np.full((8, 128, 128), NEG, np.float32)
    if gk < 0 or gk >= NT:
        return out
    ql = np.arange(128)
    r = 2 * g + ql // 64
    c = ql % 64
    kr = 2 * gk + ql // 64
    kc = ql % 64
    win_r = min(8, rows)
    rs = np.clip(r - win_r // 2, 0, rows - win_r)
    cs = np.clip(c - 8, 0, GRID_W - 16)
    valid = ((kr[:, None] >= rs[None, :]) & (kr[:, None] < rs[None, :] + win_r)
             & (kc[:, None] >= cs[None, :]) & (kc[:, None] < cs[None, :] + 16))
    di = kr[:, None] - r[None, :] + 7
    dj = kc[:, None] - c[None, :] + 15
    di = np.clip(di, 0, 14)
    dj = np.clip(dj, 0, 30)
    vals = rpb[:, di, dj]
    return np.where(valid[None], vals, np.float32(NEG)).astype(np.float32)


def chunk_rows(w):
    return np.ascontiguousarray(w.reshape(8, 128, w.shape[1]))


def prep_layer_inputs(layer, SEQ, xs, xcs, p):
    B = xs.shape[0]
    HALF = SEQ // 2
    EXT = HALF + 2 * HALO
    NT = SEQ // 128
    NTo = HALF // 128
    cfg = layer_cfg(layer)
    wi = p["w_in_even"][0] if layer == 0 else p["w_in_odd"][0]
    wo = p["w_out_even"][0] if layer == 0 else p["w_out_odd"][0]
    cols = []
    for f in cfg["fm"]:
        cols += f[1]
    for name, c0, n in cfg["tm"]:
        cols += list(range(c0, c0 + n))
    w_in_l = chunk_rows(np.ascontiguousarray(wi[:, cols]))
    w_out_l = chunk_rows(wo)
    w_mod_l = chunk_rows(p["w_mod"][layer])
    b_mod_l = np.ascontiguousarray(p["b_mod"][layer].reshape(24, 128).T)
    bgate = np.ascontiguousarray(np.broadcast_to(p["b_mod"][layer][2048:3072][None, :], (128, D_MODEL)))
    ident = np.eye(128, dtype=np.float32)
    blk = np.zeros((128, 128), np.float32)
    blk[:64, :64] = 1.0 / 64
    blk[64:, 64:] = 1.0 / 64
    perm = np.zeros((128, 128), np.float32)
    for m in range(128):
        k = m + 16 if (m % 32) < 16 else m - 16
        perm[k, m] = 1.0
    maps = []
    for b in range(B):
        for half in range(2):
            T0 = half * HALF
            pos_e = np.arange(T0 - HALO, T0 + HALF + HALO)
            valid_e = (pos_e >= 0) & (pos_e < SEQ)
            x_e = np.zeros((EXT, D_MODEL), np.float32)
            x_e[valid_e] = xs[b, pos_e[valid_e]]
            T1 = (1 - half) * HALF
            pos_o = np.arange(T1, T1 + HALF)
            x_u = np.concatenate([x_e, xs[b, pos_o]], axis=0)
            pos_u = np.concatenate([np.clip(pos_e, 0, SEQ - 1), pos_o])
            C, Sg = rope_tables(pos_u)
            cvec = np.stack([p["c"][b].reshape(8, 128).T, p["c_ctx"].reshape(8, 128).T], axis=-1)
            m = dict(x_u=x_u, xc=np.ascontiguousarray(xcs[b]), cvec=np.ascontiguousarray(cvec), w_mod=w_mod_l, b_mod=b_mod_l,
                     bgate=bgate, w_in=w_in_l, w_out=w_out_l, ident=ident, blk=blk, perm=perm, ropeC=C, ropeS=Sg)
            G0 = T0 // 128
            if layer == 0:
                m["gains"] = np.ascontiguousarray(np.stack([np.tile(p["a_q_norm"][0], 2), np.tile(p["a_k_norm"][0], 2)], axis=-1))
                rpb = p["b_rpb"][0]
                gi = min(max(G0 + 2, 2), NT - 3) if NT >= 6 else 0
                bi = np.stack([nbr_bias(rpb, gi, gi + j, NT) for j in range(-2, 3)], axis=0)
                m["bias_i"] = np.ascontiguousarray(bi.transpose(2, 1, 0, 3).reshape(128, 8, 640))
                ets = [0, 1, NTo - 2, NTo - 1]
                be = np.stack([np.stack([nbr_bias(rpb, G0 + t, G0 + t + j, NT) for j in range(-3, 4)], axis=0) for t in ets], axis=0)
                m["bias_e"] = np.ascontiguousarray(be.transpose(0, 2, 3, 1, 4).reshape(4, 8, 128, 896))
            else:
                a = np.arange(128)
                tri_prev = np.where(a[:, None] >= a[None, :], 0.0, NEG).astype(np.float32)
                tri_next = np.where(a[:, None] <= a[None, :], 0.0, NEG).astype(np.float32)
                full = np.full((128, 128), NEG, np.float32)
                first_prev = full if G0 == 0 else tri_prev
                last_next = full if G0 + NTo == NT else tri_next
                m["dmask"] = np.ascontiguousarray(np.stack([first_prev, last_next, tri_prev, tri_next], axis=1))
                m["lamv"] = np.ascontiguousarray(np.broadcast_to(p["c_lambda"][0].reshape(1, 256), (128, 256)))
                m["subln"] = np.ascontiguousarray(np.broadcast_to((p["c_subln"][0])[None, :], (128, 128)))
                m["sinks"] = np.ascontiguousarray(np.broadcast_to(p["d_sinks"][0][None, :], (128, 8)))
                m["fnorm"] = np.ascontiguousarray(np.broadcast_to(p["final_norm"][None, :], (128, D_MODEL)))
            maps.append(m)
    return maps


def prep_fused_inputs(SEQ, xs, xcs, p):
    m0 = prep_layer_inputs(0, SEQ, xs, xcs, p)
    m1 = prep_layer_inputs(1, SEQ, xs, xcs, p)
    shared = ("x_u", "xc", "cvec", "ident", "blk", "perm", "ropeC", "ropeS")
    maps = []
    for i, (a, b) in enumerate(zip(m0, m1)):
        half = i % 2
        m = {k: a[k] for k in shared}
        for k, v in a.items():
            if k not in shared:
                m["l0_" + k] = v
        for k, v in b.items():
            if k not in shared:
                m["l1_" + k] = v
        sel = np.zeros((128, 4), np.float32)
        sel[:, 0] = 1.0 if half == 1 else 0.0
        sel[:, 1] = 1.0 if half == 0 else 0.0
        sel[:, 2] = 1.0 if half == 1 else 0.0
        sel[:, 3] = 1.0 if half == 0 else 0.0
        m["sel"] = sel
        maps.append(m)
    return maps


def run_fused(SEQ, xs, xcs, p, runner=None):
    B = xs.shape[0]
    key = ("fused", SEQ, B)
    if key not in _NC_CACHE:
        _NC_CACHE[key] = build_fused(SEQ, B)
    nc = _NC_CACHE[key]
    maps = prep_fused_inputs(SEQ, xs, xcs, p)
    if runner is None:
        res = run_bass_kernel_spmd(nc, maps, core_ids=list(range(len(maps)))).results
    else:
        res = runner(nc, maps)
    HALF = SEQ // 2
    xo = np.zeros_like(xs)
    for b in range(B):
        for half in range(2):
            xo[b, half * HALF:(half + 1) * HALF] = res[2 * b + half]["out_x"]
    return xo


_NC_CACHE = {}


def kernel(x, c, ctx, c_ctx, w_mod, b_mod, w_in_even, w_out_even, a_q_norm, a_k_norm, b_rpb,
           w_in_odd, w_out_odd, c_lambda, c_subln, d_sinks, final_norm):
    p = dict(c=np.asarray(c, np.float32), c_ctx=np.asarray(c_ctx, np.float32), w_mod=np.asarray(w_mod, np.float32),
             b_mod=np.asarray(b_mod, np.float32), w_in_even=np.asarray(w_in_even, np.float32),
             w_out_even=np.asarray(w_out_even, np.float32), a_q_norm=np.asarray(a_q_norm, np.float32),
             a_k_norm=np.asarray(a_k_norm, np.float32), b_rpb=np.asarray(b_rpb, np.float32),
             w_in_odd=np.asarray(w_in_odd, np.float32), w_out_odd=np.asarray(w_out_odd, np.float32),
             c_lambda=np.asarray(c_lambda, np.float32), c_subln=np.asarray(c_subln, np.float32),
             d_sinks=np.asarray(d_sinks, np.float32), final_norm=np.asarray(final_norm, np.float32))
    xs = np.asarray(x, np.float32)
    xcs = np.asarray(ctx, np.float32)
    SEQ = xs.shape[1]
    return run_fused(SEQ, xs, xcs, p)
```

```python
import contextlib
import math
import numpy as np
import concourse.bass as bass
import concourse.mybir as mybir
from concourse.bass_utils import run_bass_kernel_spmd

F32 = mybir.dt.float32
BF16 = mybir.dt.bfloat16
AF = mybir.ActivationFunctionType
ALU = mybir.AluOpType

D_MODEL = 1024
CTX = 256
HD = 64
GRID_W = 64
SCALE = HD ** -0.5
EPS = 1e-6
NEG = -30000.0
HALO = 512
LQ = "pool"


class Tile:
    __slots__ = ("name", "t", "last_w", "readers", "dsem", "dcount", "excl")

    def __init__(self, name, t=None, excl=False):
        self.name = name
        self.t = t
        self.last_w = None
        self.readers = {}
        self.dsem = None
        self.dcount = 0
        self.excl = excl

    def __getitem__(self, idx):
        return self.t[idx]


class Sched:
    def __init__(self, nc, stack):
        self.nc = nc
        self.stack = stack
        self.sem_stack = stack
        self.dtiles = []
        self.engs = {}
        self.sems = {}
        for en, e in (("pe", nc.tensor), ("act", nc.scalar), ("dve", nc.vector),
                      ("pool", nc.gpsimd), ("sp", nc.sync)):
            self.sems[en] = stack.enter_context(nc.semaphore("s_" + en))
            self.engs[en] = dict(eng=e, count=0, seen={}, key=en)
        self.epoch = 0
        self.nsem = 0
        self.prefix = ""
        self.free_dsems = []

    def sb(self, name, shape, dt):
        return Tile(name, self.stack.enter_context(self.nc.sbuf_tensor("sb_" + self.prefix + name, list(shape), dt)))

    def ps(self, name, shape, dt=F32):
        return Tile(name, self.stack.enter_context(self.nc.psum_tensor("ps_" + name, list(shape), dt)), excl=True)

    def _dsem(self, tile):
        if tile.dsem is None:
            if self.free_dsems:
                key, cnt = self.free_dsems.pop()
                tile.dsem = key
                tile.dcount = cnt
            else:
                key = "d%d" % self.nsem
                self.nsem += 1
                tile.dsem = key
                self.sems[key] = self.sem_stack.enter_context(self.nc.semaphore(key))
            self.dtiles.append(tile)
        return tile.dsem

    def new_epoch(self):
        self.epoch += 1
        for en, E in self.engs.items():
            key = "%s#%d" % (en, self.epoch)
            self.sems[key] = self.sem_stack.enter_context(self.nc.semaphore("s_%s_%d" % (en, self.epoch)))
            E["key"] = key
            E["count"] = 0

    def release_dsems(self):
        for t in self.dtiles:
            self.free_dsems.append((t.dsem, t.dcount))
            t.dsem = None
        self.dtiles = []

    def _wait_deps(self, en, reads, writes):
        E = self.engs[en]
        deps = {}

        def add(ev):
            if ev is None:
                return
            k, v = ev
            if deps.get(k, 0) < v:
                deps[k] = v

        me = E["key"]
        for t in reads:
            add(t.last_w)
            if t.excl:
                for k, v in t.readers.items():
                    if k != me:
                        add((k, v))
        for t in writes:
            add(t.last_w)
            for k, v in t.readers.items():
                if k != me:
                    add((k, v))
        for k, v in deps.items():
            if E["seen"].get(k, 0) < v:
                E["seen"][k] = v
                if k == me and en == "pe":
                    continue
                E["eng"].wait_ge(self.sems[k], v)

    def op(self, en, fn, reads=(), writes=()):
        E = self.engs[en]
        self._wait_deps(en, reads, writes)
        ins = fn(E["eng"])
        E["count"] += 1
        me = E["key"]
        ins.then_inc(self.sems[me], 1)
        for t in reads:
            t.readers[me] = E["count"]
        for t in writes:
            t.last_w = (me, E["count"])
            t.readers = {}
        return ins

    def dma(self, q, out, in_, reads=(), writes=(), sem_tile=None):
        E = self.engs[q]
        self._wait_deps(q, reads, writes)
        st = sem_tile if sem_tile is not None else (list(writes) + list(reads))[0]
        key = self._dsem(st)
        ins = E["eng"].dma_start(out=out, in_=in_)
        st.dcount += 16
        ins.then_inc(self.sems[key], 16)
        for t in reads:
            t.readers[key] = st.dcount
        for t in writes:
            t.last_w = (key, st.dcount)
            t.readers = {}
        return ins

    def barrier(self):
        for en, E in self.engs.items():
            for en2, E2 in self.engs.items():
                k2 = E2["key"]
                if en2 != en and E2["count"] and E["seen"].get(k2, 0) < E2["count"]:
                    E["seen"][k2] = E2["count"]
                    E["eng"].wait_ge(self.sems[k2], E2["count"])
            for t in self.dtiles:
                if t.dcount and E["seen"].get(t.dsem, 0) < t.dcount:
                    E["seen"][t.dsem] = t.dcount
                    E["eng"].wait_ge(self.sems[t.dsem], t.dcount)

    def finish(self, tiles, en="sp"):
        self._wait_deps(en, tiles, tiles)


class Rot:
    def __init__(self, tiles):
        self.tiles = tiles
        self.i = 0

    def next(self):
        t = self.tiles[self.i % len(self.tiles)]
        self.i += 1
        return t


def layer_cfg(layer):
    if layer == 0:
        qa, ka, va, qb, kb, vb, z = 0, 512, 640, 768, 1280, 1792, 2304
        fm = []
        for c in range(4):
            cols = list(range(qa + c * 64, qa + c * 64 + 64)) + list(range(qa + (4 + c) * 64, qa + (4 + c) * 64 + 64))
            fm.append(("qa%d" % c, cols, "q", True))
        fm.append(("ka", list(range(ka, ka + 128)), "k", True))
        for c in range(4):
            fm.append(("qb%d" % c, list(range(qb + c * 128, qb + c * 128 + 128)), None, False))
        for c in range(4):
            fm.append(("kb%d" % c, list(range(kb + c * 128, kb + c * 128 + 128)), None, False))
        tm = [("va", va, 128), ("vb", vb, 512), ("z", z, 1024)]
        dense_k, dense_v = ["ka"], ["va"]
        local_k, local_v = ["kb0", "kb1", "kb2", "kb3"], ["vb"]
    else:
        qc, kc, vc, qd, kd, vd, z = 0, 512, 1024, 1536, 2048, 2176, 2304
        fm = []
        for c in range(4):
            fm.append(("qc%d" % c, list(range(qc + c * 128, qc + c * 128 + 128)), None, True))
        for c in range(4):
            fm.append(("kc%d" % c, list(range(kc + c * 128, kc + c * 128 + 128)), None, True))
        for c in range(4):
            cols = list(range(qd + c * 64, qd + c * 64 + 64)) + list(range(qd + (4 + c) * 64, qd + (4 + c) * 64 + 64))
            fm.append(("qd%d" % c, cols, None, True))
        fm.append(("kd", list(range(kd, kd + 128)), None, True))
        tm = [("vc", vc, 512), ("vd", vd, 128), ("z", z, 1024)]
        dense_k, dense_v = ["kc0", "kc1", "kc2", "kc3"], ["vc"]
        local_k, local_v = ["kd"], ["vd"]
    return dict(fm=fm, tm=tm, dense_k=dense_k, dense_v=dense_v, local_k=local_k, local_v=local_v)


def lambda_init(layer):
    return 0.8 - 0.6 * math.exp(-0.3 * layer)


def emit_layer(nc, S, SH, layer, SEQ):
    HALF = SEQ // 2
    EXT = HALF + 2 * HALO
    NU = EXT + HALF + CTX
    NTo = HALF // 128
    NBo = HALF // 512
    U_OWN = HALO
    U_O = EXT
    U_C = EXT + HALF
    NKD = 2 * HALF + CTX
    NKC = NKD // 128
    last = layer == 1
    cfg = layer_cfg(layer)
    fm, tm = cfg["fm"], cfg["tm"]
    NFM = len(fm)
    fmi = {f[0]: i for i, f in enumerate(fm)}
    NCOL = NFM * 128 + sum(t[2] for t in tm)
    tmoff = {}
    o = NFM * 128
    for name, _, n in tm:
        tmoff[name] = o
        o += n
    lam0 = lambda_init(layer)

    LP = "l%d_" % layer
    S.prefix = LP

    def din(name, shape):
        return nc.dram_tensor(LP + name, list(shape), F32, kind="ExternalInput").ap()

    x_u, xc_in, cvec = SH["x_u"], SH["xc"], SH["cvec"]
    ident_d, blk_d, perm_d, ropeC, ropeS = SH["ident"], SH["blk"], SH["perm"], SH["ropeC"], SH["ropeS"]
    x1_loc, xc1_loc, GA = SH["x1_loc"], SH["xc1_loc"], SH["GA"]
    w_mod = din("w_mod", [8, 128, 3072])
    b_mod = din("b_mod", [128, 24])
    bgate = din("bgate", [128, D_MODEL])
    w_in = din("w_in", [8, 128, NCOL])
    w_out = din("w_out", [8, 128, D_MODEL])
    if layer == 0:
        gains = din("gains", [128, 2])
        bias_i = din("bias_i", [128, 8, 5 * 128])
        bias_e = din("bias_e", [4, 8, 128, 7 * 128])
    else:
        dmask = din("dmask", [128, 4, 128])
        lamv = din("lamv", [128, 256])
        subln = din("subln", [128, 128])
        sinks = din("sinks", [128, 8])
        fnorm = din("fnorm", [128, D_MODEL])
    if last:
        out_x = nc.dram_tensor("out_x", [HALF, D_MODEL], F32, kind="ExternalOutput").ap()
    else:
        out_x = x1_loc
        out_c = xc1_loc

    fmS = nc.dram_tensor(LP + "fmS", [NFM, 128, NU], BF16).ap()
    vdims = {"va": (2, 65), "vb": (8, 65), "vc": (4, 129), "vd": (2, 65)}
    vS = {}
    for name, _, n in tm:
        if name != "z":
            h, d = vdims[name]
            vS[name] = nc.dram_tensor(LP + "vS_" + name, [NU, h * d], BF16).ap()
    zS = nc.dram_tensor(LP + "zS", [NU, D_MODEL], BF16).ap()

    with contextlib.ExitStack() as st:
        S.stack = st
        st1 = contextlib.ExitStack()
        fm_reg = Tile("fm_reg")
        v_reg = Tile("v_reg")
        z_reg = Tile("z_reg")

        ident_f = S.sb("ident_f", [128, 128], F32)
        ident = S.sb("ident", [128, 128], BF16)
        blk = S.sb("blk", [128, 128], F32)
        perm = S.sb("perm", [128, 128], F32)
        epst = S.sb("epst", [128, 1], F32)
        zeros = S.sb("zeros", [128, 128], F32)
        junk = S.sb("junk", [128, D_MODEL], F32)
        ss_r = Rot([S.sb("ss%d" % i, [128, 2], F32) for i in range(2)])
        t1_r = Rot([S.sb("t1%d" % i, [128, 512], F32) for i in range(2)])
        gate_bc = S.sb("gate_bc", [128, 2, D_MODEL], F32)
        modt = S.sb("modt", [128, 24, 2], F32)
        bmodt = S.sb("bmodt", [128, 24], F32)
        cv = S.sb("cv", [128, 8, 2], F32)
        pTb_t = SH["pTb_t"]
        pTb = pTb_t.t
        banks = SH["banks"]
        sel_t = S.sb("sel_t", [128, 4], F32)
        S.dma(LQ, sel_t[:], SH["sel"], writes=[sel_t])
        xt2 = S.sb("xt2", [128, D_MODEL], F32)
        G_reg = SH["G_reg"]
        NTo_ = HALF // 128

        RCt = min(512, HALF) // 128

        def grow(r, t):
            return ((t // RCt) * 2 * RCt + r * RCt + (t % RCt)) * 128

        def load_x_tile(xt, utile):
            e = utile
            if layer == 0:
                if e >= (EXT + HALF) // 128:
                    c = e - (EXT + HALF) // 128
                    S.dma(LQ, xt[:], xc_in[c * 128:(c + 1) * 128, :], writes=[xt])
                else:
                    S.dma(LQ, xt[:], x_u[e * 128:(e + 1) * 128, :], writes=[xt])
                return
            if e >= (EXT + HALF) // 128:
                c = e - (EXT + HALF) // 128
                S.dma(LQ, xt[:], xc1_loc[c * 128:(c + 1) * 128, :], writes=[xt])
            elif e >= EXT // 128:
                o = e - EXT // 128
                S.dma(LQ, xt[:], GA[grow(0, o):grow(0, o) + 128, :], reads=[G_reg], writes=[xt])
                S.dma(LQ, xt2[:], GA[grow(1, o):grow(1, o) + 128, :], reads=[G_reg], writes=[xt2])
                S.op("act", lambda en: en.activation(out=xt[:], in_=xt[:], func=AF.Copy, scale=sel_t[:, 2:3]), reads=[xt, sel_t], writes=[xt])
                S.op("dve", lambda en: en.scalar_tensor_tensor(out=xt[:], in0=xt2[:], scalar=sel_t[:, 3:4], in1=xt[:], op0=ALU.mult, op1=ALU.add),
                     reads=[xt2, sel_t, xt], writes=[xt])
            elif 4 <= e < 4 + NTo_:
                t = e - 4
                S.dma(LQ, xt[:], x1_loc[t * 128:(t + 1) * 128, :], writes=[xt])
            elif e < 4:
                gt = grow(0, NTo_ - 4 + e)
                S.dma(LQ, xt[:], GA[gt:gt + 128, :], reads=[G_reg], writes=[xt])
                S.op("act", lambda en: en.activation(out=xt[:], in_=xt[:], func=AF.Copy, scale=sel_t[:, 0:1]), reads=[xt, sel_t], writes=[xt])
            else:
                gt = grow(1, e - 4 - NTo_)
                S.dma(LQ, xt[:], GA[gt:gt + 128, :], reads=[G_reg], writes=[xt])
                S.op("act", lambda en: en.activation(out=xt[:], in_=xt[:], func=AF.Copy, scale=sel_t[:, 1:2]), reads=[xt, sel_t], writes=[xt])

        S.dma(LQ, ident_f[:], ident_d, writes=[ident_f])
        S.dma(LQ, blk[:], blk_d, writes=[blk])
        S.dma(LQ, perm[:], perm_d, writes=[perm])
        S.dma(LQ, cv[:], cvec, writes=[cv])
        S.dma(LQ, bmodt[:], b_mod, writes=[bmodt])
        S.dma(LQ, gate_bc[:, 0, :], bgate, writes=[gate_bc])
        S.op("dve", lambda e: e.tensor_copy(out=ident[:], in_=ident_f[:]), reads=[ident_f], writes=[ident])
        S.op("pool", lambda e: e.memset(epst[:], EPS), writes=[epst])
        S.op("pool", lambda e: e.memset(zeros[:], 0.0), writes=[zeros])
        S.op("dve", lambda e: e.tensor_copy(out=gate_bc[:, 1, :], in_=gate_bc[:, 0, :]), reads=[gate_bc], writes=[gate_bc])
        S.stack = st
        if layer == 0:
            gn = S.sb("gn", [128, 2], F32)
            bi_t = S.sb("bi_t", [128, 8, 640], F32)
            S.dma(LQ, gn[:], gains, writes=[gn])
            S.dma(LQ, bi_t[:], bias_i, writes=[bi_t])
        else:
            dm_t = S.sb("dm_t", [128, 4, 128], F32)
            lam_t = S.sb("lam_t", [128, 256], F32)
            sub_t = S.sb("sub_t", [128, 128], F32)
            snk_t = S.sb("snk_t", [128, 8], F32)
            fn_t = S.sb("fn_t", [128, D_MODEL], F32)
            S.dma(LQ, dm_t[:], dmask, writes=[dm_t])
            S.dma(LQ, lam_t[:], lamv, writes=[lam_t])
            S.dma(LQ, sub_t[:], subln, writes=[sub_t])
            S.dma(LQ, snk_t[:], sinks, writes=[snk_t])
            S.dma(LQ, fn_t[:], fnorm, writes=[fn_t])
            lam_s = S.sb("lam_s", [128, 4], F32)
            lprod = S.sb("lprod", [128, 128], F32)
            esnk = S.sb("esnk", [128, 8], F32)
        S.stack = st1
        win = S.sb("win", [128, 8, NCOL], BF16)
        screp = S.sb("screp", [128, 2, 8, 128], F32)
        stage = Rot([S.sb("stage%d" % i, [128, 1024], F32) for i in range(2)])

        S.op("act", lambda e: e.activation(out=cv[:], in_=cv[:], func=AF.Silu), reads=[cv], writes=[cv])
        for w in range(2):
            for k in range(8):
                S.op("act", lambda e: e.activation(out=screp[:, w, k, :], in_=zeros[:], func=AF.Identity,
                                                   bias=cv[:, k, w:w + 1]), reads=[zeros, cv], writes=[screp])
        pmod = banks[5]
        pg = [banks[1], banks[2], banks[3], banks[4]]
        for k in range(8):
            for pi in range(3):
                stg = stage.next()
                S.dma(LQ, stg[:], w_mod[k, :, pi * 1024:(pi + 1) * 1024], writes=[stg])
                for jj in range(8):
                    j = pi * 8 + jj
                    S.op("pe", lambda e: e.matmul(pmod[:, j * 2:j * 2 + 2], lhsT=stg[:, jj * 128:(jj + 1) * 128], rhs=cv[:, k, :],
                                                  start=(k == 0 and j == 0), stop=(k == 7), skip_group_check=True),
                         reads=[stg, cv], writes=[pmod])
                if pi == 2:
                    for w in range(2):
                        for n in range(2):
                            S.op("pe", lambda e: e.matmul(pg[w * 2 + n][:], lhsT=screp[:, w, k, :],
                                                          rhs=stg[:, n * 512:(n + 1) * 512],
                                                          start=(k == 0), stop=(k == 7)),
                                 reads=[stg, screp], writes=[pg[w * 2 + n]])
        for w in range(2):
            S.op("dve", lambda e: e.tensor_tensor(out=modt[:, :, w], in0=pmod[:, 0:48].rearrange("p (j w) -> p j w", w=2)[:, :, w],
                                                  in1=bmodt[:], op=ALU.add), reads=[pmod, bmodt], writes=[modt])
            for n in range(2):
                S.op("dve", lambda e: e.tensor_tensor(out=gate_bc[:, w, n * 512:(n + 1) * 512], in0=pg[w * 2 + n][:],
                                                      in1=gate_bc[:, w, n * 512:(n + 1) * 512], op=ALU.add),
                     reads=[pg[w * 2 + n], gate_bc], writes=[gate_bc])
        S.op("dve", lambda e: e.tensor_scalar_add(out=modt[:, 8:16, :], in0=modt[:, 8:16, :], scalar1=1.0),
             reads=[modt], writes=[modt])

        cnt = 0
        for k in range(8):
            for c0 in range(0, NCOL, 1024):
                cn = min(1024, NCOL - c0)
                stg = stage.next()
                S.dma(LQ, stg[:, 0:cn], w_in[k, :, c0:c0 + cn], writes=[stg])
                S.op("dve", lambda e: e.tensor_copy(out=win[:, k, c0:c0 + cn], in_=stg[:, 0:cn]), reads=[stg], writes=[win])
                cnt += 1

        if layer == 1:
            S.op("dve", lambda e: e.tensor_tensor(out=lprod[:, 0:64], in0=lam_t[:, 0:64], in1=lam_t[:, 64:128], op=ALU.mult), reads=[lam_t], writes=[lprod])
            S.op("dve", lambda e: e.tensor_tensor(out=lprod[:, 64:128], in0=lam_t[:, 128:192], in1=lam_t[:, 192:256], op=ALU.mult), reads=[lam_t, lprod], writes=[lprod])
            S.op("dve", lambda e: e.reduce_sum(out=lam_s[:, 0:2], in_=lprod[:].rearrange("p (a d) -> p a d", a=2), axis=mybir.AxisListType.X), reads=[lprod], writes=[lam_s])
            S.op("act", lambda e: e.activation(out=lam_s[:, 0:2], in_=lam_s[:, 0:2], func=AF.Exp), reads=[lam_s], writes=[lam_s])
            S.op("dve", lambda e: e.tensor_tensor(out=lam_s[:, 2:3], in0=lam_s[:, 0:1], in1=lam_s[:, 1:2], op=ALU.subtract), reads=[lam_s], writes=[lam_s])
            S.op("dve", lambda e: e.tensor_scalar(out=lam_s[:, 3:4], in0=lam_s[:, 2:3], scalar1=lam0, scalar2=-1.0, op0=ALU.add, op1=ALU.mult), reads=[lam_s], writes=[lam_s])
            S.op("act", lambda e: e.activation(out=esnk[:], in_=snk_t[:], func=AF.Exp), reads=[snk_t], writes=[esnk])

        xt_r = Rot([S.sb("xt%d" % i, [128, D_MODEL], F32) for i in range(2)])
        xn_r = Rot([S.sb("xn%d" % i, [128, D_MODEL], BF16) for i in range(2)])
        hT_r = Rot([S.sb("hT%d" % i, [128, 8, 512], BF16) for i in range(2)])
        tabC_r = Rot([S.sb("tabC%d" % i, [128, 512], F32) for i in range(1)])
        tabS_r = Rot([S.sb("tabS%d" % i, [128, 512], F32) for i in range(1)])
        sq_r = Rot([S.sb("sq%d" % i, [128, 512], F32) for i in range(2)])
        rs_r = Rot([S.sb("rs%d" % i, [128, 512], F32) for i in range(2)])
        qn_r = Rot([S.sb("qn%d" % i, [128, 512], F32) for i in range(3)])
        t2_r = Rot([S.sb("t2%d" % i, [128, 512], F32) for i in range(2)])
        fo_r = Rot([S.sb("fo%d" % i, [128, 512], BF16) for i in range(4)])
        vst = {}
        for name in vS:
            h, d = vdims[name]
            vst[name] = Rot([S.sb("vst_%s%d" % (name, i), [128, h, d], BF16) for i in range(2)])
            for t in vst[name].tiles:
                S.op("pool", lambda e: e.memset(t[:], 1.0), writes=[t])
        zst_r = Rot([S.sb("zst%d" % i, [128, D_MODEL], BF16) for i in range(2)])
        bA = Rot([banks[1], banks[2], banks[3]])
        bB = Rot([banks[4], banks[5]])

        def phase1_block(u0, ntiles, src, src_row0, w, fm_list, tm_list, rope_col0):
            ntok = ntiles * 128
            hT = hT_r.next()
            for ti in range(ntiles):
                xt = xt_r.next()
                ss = ss_r.next()
                xn = xn_r.next()
                load_x_tile(xt, u0 // 128 + ti)
                S.op("act", lambda e: e.activation(out=junk[:], in_=xt[:], func=AF.Square, accum_out=ss[:, 0:1]), reads=[xt], writes=[junk, ss])
                S.op("act", lambda e: e.activation(out=ss[:, 1:2], in_=ss[:, 0:1], func=AF.Ln, scale=1.0 / D_MODEL, bias=epst[:]), reads=[ss, epst], writes=[ss])
                S.op("act", lambda e: e.activation(out=ss[:, 1:2], in_=ss[:, 1:2], func=AF.Exp, scale=-0.5), reads=[ss], writes=[ss])
                S.op("act", lambda e: e.activation(out=xn[:], in_=xt[:], func=AF.Copy, scale=ss[:, 1:2]), reads=[xt, ss], writes=[xn])
                for k in range(8):
                    S.op("pe", lambda e: e.transpose(out=pTb[:, k, :], in_=xn[:, k * 128:(k + 1) * 128], identity=ident[:]),
                         reads=[xn, ident], writes=[pTb_t])
                for k in range(8):
                    S.op("dve", lambda e: e.tensor_scalar(out=hT[:, k, ti * 128:(ti + 1) * 128], in0=pTb[:, k, :],
                                                          scalar1=modt[:, 8 + k, w:w + 1], scalar2=modt[:, k, w:w + 1],
                                                          op0=ALU.mult, op1=ALU.add), reads=[pTb_t, modt], writes=[hT])
            rope_needed = any(fm[i][3] for i in fm_list) and rope_col0 is not None
            if rope_needed:
                tC = tabC_r.next()
                tS = tabS_r.next()
                S.dma(LQ, tC[:, 0:ntok], ropeC[:, rope_col0:rope_col0 + ntok], writes=[tC])
                S.dma(LQ, tS[:, 0:ntok], ropeS[:, rope_col0:rope_col0 + ntok], writes=[tS])
            chs = [dict(i=i) for i in fm_list]

            def st_main(ch):
                i = ch["i"]
                pa = bA.next()
                for k in range(8):
                    S.op("pe", lambda e: e.matmul(pa[:, 0:ntok], lhsT=win[:, k, i * 128:(i + 1) * 128], rhs=hT[:, k, 0:ntok],
                                                  start=(k == 0), stop=(k == 7)), reads=[win, hT], writes=[pa])
                ch["pa"] = pa

            def st_norm(ch):
                i = ch["i"]
                name, _, nkind, roped = fm[i]
                pa = ch["pa"]
                fo = fo_r.next()
                ch["fo"] = fo
                do_rope = roped and rope_col0 is not None
                ch["do_rope"] = do_rope
                if nkind is None and not do_rope:
                    S.op("act", lambda e: e.activation(out=fo[:, 0:ntok], in_=pa[:, 0:ntok], func=AF.Copy), reads=[pa], writes=[fo])
                    return
                qn = qn_r.next()
                ch["qn"] = qn
                if nkind is not None:
                    sq = sq_r.next()
                    rs = rs_r.next()
                    pb = bB.next()
                    gcol = 0 if nkind == "q" else 1
                    S.op("act", lambda e: e.activation(out=sq[:, 0:ntok], in_=pa[:, 0:ntok], func=AF.Square), reads=[pa], writes=[sq])
                    S.op("pe", lambda e: e.matmul(pb[:, 0:ntok], lhsT=blk[:], rhs=sq[:, 0:ntok], start=True, stop=True), reads=[blk, sq], writes=[pb])
                    S.op("act", lambda e: e.activation(out=rs[:, 0:ntok], in_=pb[:, 0:ntok], func=AF.Ln, bias=epst[:]), reads=[pb, epst], writes=[rs])
                    S.op("act", lambda e: e.activation(out=rs[:, 0:ntok], in_=rs[:, 0:ntok], func=AF.Exp, scale=-0.5), reads=[rs], writes=[rs])
                    dst = qn if do_rope else fo
                    S.op("dve", lambda e: e.scalar_tensor_tensor(out=dst[:, 0:ntok], in0=pa[:, 0:ntok], scalar=gn[:, gcol:gcol + 1],
                                                                 in1=rs[:, 0:ntok], op0=ALU.mult, op1=ALU.mult),
                         reads=[pa, gn, rs], writes=[dst])
                else:
                    S.op("act", lambda e: e.activation(out=qn[:, 0:ntok], in_=pa[:, 0:ntok], func=AF.Copy), reads=[pa], writes=[qn])

            def st_rope(ch):
                i = ch["i"]
                fo = ch["fo"]
                if ch["do_rope"]:
                    qn = ch["qn"]
                    pb2 = bB.next()
                    t1 = t1_r.next()
                    t2 = t2_r.next()
                    S.op("pe", lambda e: e.matmul(pb2[:, 0:ntok], lhsT=perm[:], rhs=qn[:, 0:ntok], start=True, stop=True), reads=[perm, qn], writes=[pb2])
                    S.op("dve", lambda e: e.tensor_tensor(out=t1[:, 0:ntok], in0=qn[:, 0:ntok], in1=tC[:, 0:ntok], op=ALU.mult), reads=[qn, tC], writes=[t1])
                    S.op("dve", lambda e: e.tensor_tensor(out=t2[:, 0:ntok], in0=pb2[:, 0:ntok], in1=tS[:, 0:ntok], op=ALU.mult), reads=[pb2, tS], writes=[t2])
                    S.op("dve", lambda e: e.tensor_tensor(out=fo[:, 0:ntok], in0=t1[:, 0:ntok], in1=t2[:, 0:ntok], op=ALU.add), reads=[t1, t2], writes=[fo])
                S.dma("sp", fmS[i, :, u0:u0 + ntok], fo[:, 0:ntok], reads=[fo], writes=[fm_reg], sem_tile=fo)

            nchs = len(chs)
            for step in range(nchs + 2):
                if step < nchs:
                    st_main(chs[step])
                if 0 <= step - 1 < nchs:
                    st_norm(chs[step - 1])
                if 0 <= step - 2 < nchs:
                    st_rope(chs[step - 2])
            for name in tm_list:
                col0 = tmoff[name]
                ncols = dict((t[0], t[2]) for t in tm)[name]
                for ti in range(ntiles):
                    for n0 in range(0, ncols, 512):
                        nn = min(512, ncols - n0)
                        pa = bA.next()
                        for k in range(8):
                            S.op("pe", lambda e: e.matmul(pa[:, 0:nn], lhsT=hT[:, k, ti * 128:(ti + 1) * 128],
                                                          rhs=win[:, k, col0 + n0:col0 + n0 + nn], start=(k == 0), stop=(k == 7)),
                                 reads=[win, hT], writes=[pa])
                        if name == "z":
                            if n0 == 0:
                                zst = zst_r.next()
                            S.op("act", lambda e: e.activation(out=zst[:, n0:n0 + nn], in_=pa[:, 0:nn], func=AF.Silu), reads=[pa], writes=[zst])
                            if n0 + nn == ncols:
                                S.dma("sp", zS[u0 + ti * 128:u0 + (ti + 1) * 128, :], zst[:], reads=[zst], writes=[z_reg], sem_tile=zst)
                        else:
                            h, d = vdims[name]
                            dv = d - 1
                            vt = vst[name].next()
                            S.op("dve", lambda e: e.tensor_copy(out=vt[:, :, 0:dv], in_=pa[:, 0:nn].rearrange("p (h d) -> p h d", d=dv)),
                                 reads=[pa], writes=[vt])
                            S.dma("sp", vS[name][u0 + ti * 128:u0 + (ti + 1) * 128, :], vt[:].rearrange("p h d -> p (h d)"),
                                  reads=[vt], writes=[v_reg], sem_tile=vt)


        all_fm = list(range(NFM))
        all_tm = [t[0] for t in tm]
        lk = [fmi[n] for n in cfg["local_k"]]
        dk = [fmi[n] for n in cfg["dense_k"]]
        xc_ap = xc_in
        phase1_block(U_C, 2, xc_ap, 0, 1, all_fm, all_tm, None)
        eblocks = list(range(EXT // 512))
        eblocks = [b for b in eblocks if HALO <= b * 512 < HALO + HALF] + [b for b in eblocks if not (HALO <= b * 512 < HALO + HALF)]
        for b in eblocks:
            u0 = b * 512
            own = HALO <= u0 < HALO + HALF
            if own:
                phase1_block(u0, 4, x_u, u0, 0, all_fm, all_tm, u0)
            else:
                phase1_block(u0, 4, x_u, u0, 0, lk, cfg["local_v"], u0)
        for b in range(HALF // 512):
            u0 = U_O + b * 512
            phase1_block(u0, 4, x_u, u0, 0, dk, cfg["dense_v"], u0)

        S.barrier()
        st1.close()
        st2 = contextlib.ExitStack()
        S.stack = st2
        wout = S.sb("wout", [128, 8, D_MODEL], BF16)
        stage2 = Rot([S.sb("stage2_%d" % i, [128, D_MODEL], F32) for i in range(2)])
        for k in range(8):
            stg = stage2.next()
            S.dma(LQ, stg[:], w_out[k], writes=[stg])
            S.op("dve", lambda e: e.tensor_copy(out=wout[:, k, :], in_=stg[:]), reads=[stg], writes=[wout])
        bS = Rot([banks[1], banks[2], banks[3]])
        bACC = Rot([banks[5], banks[6], banks[7]])
        pT_r = Rot([S.sb("pT%d" % i, [128, 512], BF16) for i in range(3)])
        sb_r = Rot([S.sb("sbias%d" % i, [128, 512], F32) for i in range(2)])
        nbuf_d = 1 if layer == 0 else 2
        KT_r = Rot([S.sb("KT%d" % i, [128, NKD], BF16) for i in range(nbuf_d)])
        VDW = 130 if layer == 0 else 129
        VD_r = Rot([S.sb("VD%d" % i, [128, NKC, VDW], BF16) for i in range(nbuf_d)])
        q_r = Rot([S.sb("qblk%d" % i, [128, 512], BF16) for i in range(6)])
        NLK = 9
        kl_r = Rot([S.sb("kl%d" % i, [128, NLK * 128], BF16) for i in range(2)])
        VLW = 130
        vl_r = Rot([S.sb("vl%d" % i, [128, NLK, VLW], BF16) for i in range(2)])
        y_t = [S.sb("y%d" % i, [128, D_MODEL], F32) for i in range(4)]
        rec_r = Rot([S.sb("rec%d" % i, [128, 8], F32) for i in range(4)])
        zl_r = Rot([S.sb("zl%d" % i, [128, D_MODEL], BF16) for i in range(1)])
        yb_r = Rot([S.sb("yb%d" % i, [128, D_MODEL], BF16) for i in range(1)])
        yT_r = Rot([S.sb("yT%d" % i, [128, 8, 128], BF16) for i in range(1)])
        xo_r = Rot([S.sb("xo%d" % i, [128, D_MODEL], F32) for i in range(1)])
        res_r = Rot([S.sb("res%d" % i, [128, D_MODEL], F32) for i in range(2)])
        be_r = Rot([S.sb("be%d" % i, [128, 7 * 128], F32) for i in range(2)]) if layer == 0 else None
        dcache = {}

        def load_dense(kname, vname, vc0, vw):
            key = (kname, vname, vc0)
            if dcache.get("key") == key:
                return dcache["KT"], dcache["VD"]
            KT = KT_r.next()
            VD = VD_r.next()
            i = fmi[kname]
            S.dma(LQ, KT[:, 0:HALF], fmS[i, :, U_OWN:U_OWN + HALF], reads=[fm_reg], writes=[KT])
            S.dma(LQ, KT[:, HALF:2 * HALF], fmS[i, :, U_O:U_O + HALF], reads=[fm_reg], writes=[KT])
            S.dma(LQ, KT[:, 2 * HALF:NKD], fmS[i, :, U_C:U_C + CTX], reads=[fm_reg], writes=[KT])
            for (c0, u0, n) in ((0, U_OWN, HALF), (HALF // 128, U_O, HALF), (2 * HALF // 128, U_C, CTX)):
                S.dma(LQ, VD[:, c0:c0 + n // 128, 0:vw], vS[vname][u0:u0 + n, vc0:vc0 + vw].rearrange("(k p) c -> p k c", p=128),
                      reads=[v_reg], writes=[VD])
            dcache.update(key=key, KT=KT, VD=VD)
            return KT, VD

        def attend(qt, pbase, NQ, kchunks, acc_list, vwidth, bias_fn=None):
            nq = NQ // 128
            n = len(kchunks)
            pend = []
            first = {}

            def issue_s(j):
                kt, kc0, vt, vap = kchunks[j]
                ps = bS.next()
                S.op("pe", lambda e: e.matmul(ps[:, 0:NQ], lhsT=kt[pbase:pbase + 64, kc0:kc0 + 128], rhs=qt[pbase:pbase + 64, 0:NQ],
                                              start=True, stop=True), reads=[kt, qt], writes=[ps])
                pT = pT_r.next()
                b = bias_fn(j) if bias_fn is not None else None
                if b is not None:
                    btile, bap = b
                    sb = sb_r.next()
                    S.op("dve", lambda e: e.scalar_tensor_tensor(out=sb[:, 0:NQ], in0=ps[:, 0:NQ], scalar=SCALE, in1=bap,
                                                                 op0=ALU.mult, op1=ALU.add), reads=[ps, btile], writes=[sb])
                    S.op("act", lambda e: e.activation(out=pT[:, 0:NQ], in_=sb[:, 0:NQ], func=AF.Exp), reads=[sb], writes=[pT])
                else:
                    S.op("act", lambda e: e.activation(out=pT[:, 0:NQ], in_=ps[:, 0:NQ], func=AF.Exp, scale=SCALE), reads=[ps], writes=[pT])
                return pT

            def issue_pv(j, pT):
                kt, kc0, vt, vap = kchunks[j]
                for s in range(nq):
                    acc, c0 = acc_list[s]
                    fst = first.get(id(acc), True)
                    first[id(acc)] = False
                    S.op("pe", lambda e: e.matmul(acc[:, c0:c0 + vwidth], lhsT=pT[:, s * 128:(s + 1) * 128], rhs=vap,
                                                  start=(j == 0 and fst), stop=(j == n - 1), skip_group_check=True),
                         reads=[pT, vt], writes=[acc])

            prev = None
            for j in range(n):
                pT = issue_s(j)
                if prev is not None:
                    issue_pv(prev[0], prev[1])
                prev = (j, pT)
            issue_pv(prev[0], prev[1])

        onesel_f = S.sb("onesel_f", [128, 2, 2], F32)
        S.op("pool", lambda e: e.memset(onesel_f[:], 0.0), writes=[onesel_f])
        S.op("pool", lambda e: e.memset(onesel_f[:, 0, 0:1], 1.0), writes=[onesel_f])
        S.op("pool", lambda e: e.memset(onesel_f[:, 1, 1:2], 1.0), writes=[onesel_f])
        den_acc = S.sb("den_acc", [128, 1024], F32)
        if layer == 1:
            dmb = S.sb("dmb", [128, 3, 384], F32)
            S.op("pool", lambda e: e.memset(dmb[:], 0.0), writes=[dmb])
            for var, (pi, ni) in enumerate(((2, 3), (0, 3), (2, 1))):
                S.op("dve", lambda e: e.tensor_copy(out=dmb[:, var, 0:128], in_=dm_t[:, pi, :]), reads=[dm_t], writes=[dmb])
                S.op("dve", lambda e: e.tensor_copy(out=dmb[:, var, 256:384], in_=dm_t[:, ni, :]), reads=[dm_t], writes=[dmb])
        pT2_r = Rot([S.sb("pT2_%d" % i, [128, 1024], BF16) for i in range(3)])
        oT_r = Rot([S.sb("oT%d" % i, [128, 512], F32) for i in range(2)])
        pairs = SH["pairs"]

        def dense_pair(qt, KT, VD, vap_fn, vw, accs, den=None):
            n = NKC

            def issue_s(j):
                pt, ta, tb = pairs[j % 2]
                for m, tt in ((0, ta), (1, tb)):
                    S.op("pe", lambda e: e.matmul(tt[:, 0:512], lhsT=KT[64 * m:64 * m + 64, j * 128:(j + 1) * 128],
                                                  rhs=qt[64 * m:64 * m + 64, 0:512], start=True, stop=True), reads=[KT, qt], writes=[tt])
                pT = pT2_r.next()
                S.op("act", lambda e: e.activation(out=pT[:], in_=pt[:, :], func=AF.Exp, scale=SCALE), reads=[ta, tb], writes=[pT])
                return pT

            def issue_pv(j, pT):
                for m in range(2):
                    S.op("pe", lambda e: e.matmul(accs[m][0:vw, 0:512], lhsT=vap_fn(j, m), rhs=pT[:, m * 512:(m + 1) * 512],
                                                  start=(j == 0), stop=(j == n - 1)), reads=[pT, VD], writes=[accs[m]])
                if den is not None:
                    if j == 0:
                        S.op("dve", lambda e: e.tensor_copy(out=den_acc[:], in_=pT[:]), reads=[pT], writes=[den_acc])
                    else:
                        S.op("dve", lambda e: e.tensor_tensor(out=den_acc[:], in0=den_acc[:], in1=pT[:], op=ALU.add), reads=[pT, den_acc], writes=[den_acc])
                    if j == n - 1:
                        for m in range(2):
                            S.op("pe", lambda e: e.matmul(den[0:2, 0:512], lhsT=onesel_f[:, m, :], rhs=den_acc[:, m * 512:(m + 1) * 512],
                                                          start=(m == 0), stop=(m == 1)), reads=[den_acc, onesel_f], writes=[den])

            prev = None
            for j in range(n):
                pT = issue_s(j)
                if prev is not None:
                    issue_pv(prev[0], prev[1])
                prev = (j, pT)
            issue_pv(prev[0], prev[1])

        def untranspose(acc, rows, fin, width):
            oT = oT_r.next()
            S.op("act", lambda e: e.activation(out=oT[0:rows, :], in_=acc[0:rows, 0:512], func=AF.Copy), reads=[acc], writes=[oT])
            for sidx in range(4):
                S.op("pe", lambda e: e.transpose(out=fin[:, sidx * width:sidx * width + rows], in_=oT[0:rows, sidx * 128:(sidx + 1) * 128],
                                                 identity=ident_f[0:rows, 0:rows]), reads=[oT, ident_f], writes=[fin])

        wide_i = [0]

        def wide_a(job):
            qt, pbase, kl, nch, nb, bias = job["qt"], job["pbase"], job["kl"], job["nch"], job["nb"], job["bias"]
            pt, ta, tb = pairs[wide_i[0] % 2]
            wide_i[0] += 1
            for j in range(nch):
                tt = ta if j < 4 else tb
                S.op("pe", lambda e: e.matmul(pt[:, j * 128:(j + 1) * 128], lhsT=kl[pbase:pbase + 64, j * 128:(j + 1) * 128],
                                              rhs=qt[pbase:pbase + 64, 0:128], start=True, stop=True), reads=[kl, qt], writes=[tt])
            pT = pT2_r.next()
            used = [ta] + ([tb] if nch > 4 else [])
            if bias is not None:
                btile, bap = bias
                sbw = stage2.next()
                S.op("dve", lambda e: e.scalar_tensor_tensor(out=sbw[:, 0:nb * 128], in0=pt[:, 0:nb * 128], scalar=SCALE, in1=bap,
                                                             op0=ALU.mult, op1=ALU.add), reads=used + [btile], writes=[sbw])
                S.op("act", lambda e: e.activation(out=pT[:, 0:nb * 128], in_=sbw[:, 0:nb * 128], func=AF.Exp), reads=[sbw], writes=[pT])
                if nch > nb:
                    S.op("act", lambda e: e.activation(out=pT[:, nb * 128:nch * 128], in_=pt[:, nb * 128:nch * 128], func=AF.Exp, scale=SCALE),
                         reads=used, writes=[pT])
            else:
                S.op("act", lambda e: e.activation(out=pT[:, 0:nch * 128], in_=pt[:, 0:nch * 128], func=AF.Exp, scale=SCALE), reads=used, writes=[pT])
            job["pT"] = pT

        def wide_b(job):
            pT, vl, vc0, nch = job["pT"], job["vl"], job["vc0"], job["nch"]
            acc = bACC.next()
            for j in range(nch):
                S.op("pe", lambda e: e.matmul(acc[:, 0:65], lhsT=pT[:, j * 128:(j + 1) * 128], rhs=vl[:, j, vc0:vc0 + 65],
                                              start=(j == 0), stop=(j == nch - 1)), reads=[pT, vl], writes=[acc])
            finish_head([(acc, 0)], 65, [job["yt"]], job["ycol"], extra_den=job.get("extra_den"))

        def run_wide(jobs):
            prev = None
            for job in jobs:
                wide_a(job)
                if prev is not None:
                    wide_b(prev)
                prev = job
            if prev is not None:
                wide_b(prev)

        def finish_head(acc_list, vwidth, y_tiles, ycol, extra_den=None, scale_ap=None):
            dv = vwidth - 1
            for s, (acc, c0) in enumerate(acc_list):
                rec = rec_r.next()
                if extra_den is not None:
                    S.op("dve", lambda e: e.tensor_tensor(out=rec[:, 0:1], in0=acc[:, c0 + dv:c0 + dv + 1], in1=extra_den, op=ALU.add),
                         reads=[acc, esnk], writes=[rec])
                    S.op("dve", lambda e: e.reciprocal(out=rec[:, 1:2], in_=rec[:, 0:1]), reads=[rec], writes=[rec])
                else:
                    S.op("dve", lambda e: e.reciprocal(out=rec[:, 1:2], in_=acc[:, c0 + dv:c0 + dv + 1]), reads=[acc], writes=[rec])
                yt = y_tiles[s]
                S.op("act", lambda e: e.activation(out=yt[:, ycol:ycol + dv], in_=acc[:, c0:c0 + dv], func=AF.Copy, scale=rec[:, 1:2]),
                     reads=[acc, rec], writes=[yt])

        def out_tile(yt, u_tok, src, src_row, w, dst, dst_row):
            zl = zl_r.next()
            yb = yb_r.next()
            yT = yT_r.next()
            xo = xo_r.next()
            res = res_r.next()
            S.dma(LQ, zl[:], zS[u_tok:u_tok + 128, :], reads=[z_reg], writes=[zl])
            load_x_tile(xo, u_tok // 128)
            S.op("dve", lambda e: e.tensor_tensor(out=yb[:], in0=yt[:], in1=zl[:], op=ALU.mult), reads=[yt, zl], writes=[yb])
            for k in range(8):
                S.op("pe", lambda e: e.transpose(out=pTb[:, k, :], in_=yb[:, k * 128:(k + 1) * 128], identity=ident[:]),
                     reads=[yb, ident], writes=[pTb_t])
            S.op("act", lambda e: e.activation(out=yT[:].rearrange("p k t -> p (k t)"), in_=pTb[:].rearrange("p k t -> p (k t)"), func=AF.Copy),
                 reads=[pTb_t], writes=[yT])
            for n in range(2):
                po = bACC.next()
                for k in range(8):
                    S.op("pe", lambda e: e.matmul(po[:], lhsT=yT[:, k, :], rhs=wout[:, k, n * 512:(n + 1) * 512], start=(k == 0), stop=(k == 7)),
                         reads=[yT, wout], writes=[po])
                S.op("dve", lambda e: e.tensor_tensor(out=res[:, n * 512:(n + 1) * 512], in0=po[:], in1=gate_bc[:, w, n * 512:(n + 1) * 512], op=ALU.mult),
                     reads=[po, gate_bc], writes=[res])
            S.op("dve", lambda e: e.tensor_tensor(out=res[:], in0=res[:], in1=xo[:], op=ALU.add), reads=[res, xo], writes=[res])
            if last:
                ss = ss_r.next()
                S.op("act", lambda e: e.activation(out=junk[:], in_=res[:], func=AF.Square, accum_out=ss[:, 0:1]), reads=[res], writes=[junk, ss])
                S.op("act", lambda e: e.activation(out=ss[:, 1:2], in_=ss[:, 0:1], func=AF.Ln, scale=1.0 / D_MODEL, bias=epst[:]), reads=[ss, epst], writes=[ss])
                S.op("act", lambda e: e.activation(out=ss[:, 1:2], in_=ss[:, 1:2], func=AF.Exp, scale=-0.5), reads=[ss], writes=[ss])
                S.op("act", lambda e: e.activation(out=xo[:], in_=res[:], func=AF.Copy, scale=ss[:, 1:2]), reads=[res, ss], writes=[xo])
                S.op("dve", lambda e: e.tensor_tensor(out=res[:], in0=xo[:], in1=fn_t[:], op=ALU.mult), reads=[xo, fn_t], writes=[res])
            S.dma("sp", dst[dst_row:dst_row + 128, :], res[:], reads=[res], sem_tile=res)
            return res

        out_tiles = []

        def load_q(name, u0, n):
            qt = q_r.next()
            S.dma(LQ, qt[:, 0:n], fmS[fmi[name], :, u0:u0 + n], reads=[fm_reg], writes=[qt])
            return qt

        def load_local(knames_idx, vname, vc0, vw, utiles):
            kl = kl_r.next()
            vl = vl_r.next()
            pos = 0
            for (ut0, cnt) in utiles:
                S.dma(LQ, kl[:, pos * 128:(pos + cnt) * 128], fmS[knames_idx, :, ut0 * 128:(ut0 + cnt) * 128], reads=[fm_reg], writes=[kl])
                S.dma(LQ, vl[:, pos:pos + cnt, 0:vw], vS[vname][ut0 * 128:(ut0 + cnt) * 128, vc0:vc0 + vw].rearrange("(k p) c -> p k c", p=128),
                      reads=[v_reg], writes=[vl])
                pos += cnt
            return kl, vl

        UC_T = U_C // 128

        if layer == 0:
            for t in range(2):
                yt = y_t[t]
                u0 = U_C + t * 128
                jobs = []
                kl, vl = load_local(fmi["ka"], "va", 0, 130, [(UC_T, 2)])
                for c in range(4):
                    qt = load_q("qa%d" % c, u0, 128)
                    for s_ in range(2):
                        jobs.append(dict(qt=qt, pbase=64 * s_, kl=kl, vl=vl, vc0=s_ * 65, nch=2, nb=0, bias=None, yt=yt, ycol=(c + 4 * s_) * 64))
                run_wide(jobs)
                for c in range(4):
                    kl, vl = load_local(fmi["kb%d" % c], "vb", c * 130, 130, [(UC_T, 2)])
                    qt = load_q("qb%d" % c, u0, 128)
                    run_wide([dict(qt=qt, pbase=64 * s_, kl=kl, vl=vl, vc0=s_ * 65, nch=2, nb=0, bias=None, yt=yt, ycol=512 + (2 * c + s_) * 64)
                              for s_ in range(2)])
                out_tiles.append(out_tile(yt, u0, xc_in, t * 128, 1, out_c, t * 128))

        for qb in range(NBo):
            u0 = U_OWN + qb * 512
            if layer == 0:
                KT, VD = load_dense("ka", "va", 0, 130)
                for c in range(4):
                    qt = load_q("qa%d" % c, u0, 512)
                    accs = [banks[5], banks[6]]
                    dense_pair(qt, KT, VD, lambda j, m: VD[:, j, m * 65:(m + 1) * 65], 65, accs)
                    for s in range(2):
                        head = c + 4 * s
                        fin = banks[7]
                        untranspose(accs[s], 65, fin, 65)
                        finish_head([(fin, i * 65) for i in range(4)], 65, y_t, head * 64)
            else:
                for h in range(4):
                    KT, VD = load_dense("kc%d" % h, "vc", h * 129, 129)
                    qt = load_q("qc%d" % h, u0, 512)
                    accs = [banks[5], banks[6]]
                    den = banks[7]
                    dense_pair(qt, KT, VD, lambda j, m: VD[:, j, 0:128], 128, accs, den=den)
                    fins = [banks[1], banks[2]]
                    untranspose(accs[0], 128, fins[0], 128)
                    untranspose(accs[1], 128, fins[1], 128)
                    dfin = banks[3]
                    untranspose(den, 2, dfin, 2)
                    o_m = [[(fins[0], i * 128, i * 2 + 0) for i in range(4)], [(fins[1], i * 128, i * 2 + 1) for i in range(4)]]
                    for s in range(4):
                        rec = rec_r.next()
                        yt = y_t[s]
                        a0, c0, d0 = o_m[0][s]
                        a1, c1, d1 = o_m[1][s]
                        S.op("dve", lambda e: e.reciprocal(out=rec[:, 0:1], in_=dfin[:, d0:d0 + 1]), reads=[dfin], writes=[rec])
                        S.op("dve", lambda e: e.reciprocal(out=rec[:, 1:2], in_=dfin[:, d1:d1 + 1]), reads=[dfin, rec], writes=[rec])
                        S.op("dve", lambda e: e.tensor_tensor(out=rec[:, 2:3], in0=rec[:, 1:2], in1=lam_s[:, 3:4], op=ALU.mult), reads=[rec, lam_s], writes=[rec])
                        t1 = t1_r.next()
                        S.op("act", lambda e: e.activation(out=t1[:, 0:128], in_=a0[:, c0:c0 + 128], func=AF.Copy, scale=rec[:, 0:1]), reads=[a0, rec], writes=[t1])
                        S.op("dve", lambda e: e.scalar_tensor_tensor(out=t1[:, 128:256], in0=a1[:, c1:c1 + 128], scalar=rec[:, 2:3], in1=t1[:, 0:128],
                                                                     op0=ALU.mult, op1=ALU.add), reads=[a1, rec, t1], writes=[t1])
                        S.op("act", lambda e: e.activation(out=t1[:, 256:384], in_=t1[:, 128:256], func=AF.Square, accum_out=rec[:, 3:4]), reads=[t1], writes=[t1, rec])
                        S.op("act", lambda e: e.activation(out=rec[:, 4:5], in_=rec[:, 3:4], func=AF.Ln, scale=1.0 / 128, bias=epst[:]), reads=[rec, epst], writes=[rec])
                        S.op("act", lambda e: e.activation(out=rec[:, 4:5], in_=rec[:, 4:5], func=AF.Exp, scale=-0.5), reads=[rec], writes=[rec])
                        S.op("dve", lambda e: e.tensor_scalar_mul(out=rec[:, 4:5], in0=rec[:, 4:5], scalar1=1.0 - lam0), reads=[rec], writes=[rec])
                        S.op("dve", lambda e: e.scalar_tensor_tensor(out=yt[:, h * 128:(h + 1) * 128], in0=t1[:, 128:256], scalar=rec[:, 4:5], in1=sub_t[:],
                                                                     op0=ALU.mult, op1=ALU.mult), reads=[t1, rec, sub_t], writes=[yt])
            for tl in range(4):
                t = qb * 4 + tl
                ut = U_OWN // 128 + t
                yt = y_t[tl]
                if layer == 0:
                    edge = t < 2 or t >= NTo - 2
                    if edge:
                        et = t if t < 2 else 2 + (t - (NTo - 2))
                        J = 7
                        runs = [(ut - 3, 7), (UC_T, 2)]
                    else:
                        J = 5
                        runs = [(ut - 2, 5), (UC_T, 2)]
                    for c in range(4):
                        kl, vl = load_local(fmi["kb%d" % c], "vb", c * 130, 130, runs)
                        qt = load_q("qb%d" % c, ut * 128, 128)
                        if not edge:
                            run_wide([dict(qt=qt, pbase=64 * s_, kl=kl, vl=vl, vc0=s_ * 65, nch=7, nb=5,
                                           bias=(bi_t, bi_t[:, 2 * c + s_, 0:640]), yt=yt, ycol=512 + (2 * c + s_) * 64) for s_ in range(2)])
                            continue
                        for s in range(2):
                            head = 2 * c + s
                            if edge:
                                be = be_r.next()
                                S.dma(LQ, be[:], bias_e[et, head], writes=[be])
                                bfn = (lambda j, be=be: (be, be[:, j * 128:(j + 1) * 128]) if j < 7 else None)
                            else:
                                bfn = (lambda j, head=head: (bi_t, bi_t[:, head, j * 128:(j + 1) * 128]) if j < 5 else None)
                            acc = bACC.next()
                            attend(qt, 64 * s, 128, [(kl, j * 128, vl, vl[:, j, s * 65:(s + 1) * 65]) for j in range(J + 2)],
                                   [(acc, 0)], 65, bias_fn=bfn)
                            finish_head([(acc, 0)], 65, [yt], 512 + head * 64)
                else:
                    kl, vl = load_local(fmi["kd"], "vd", 0, 130, [(ut - 1, 3), (UC_T, 2)])
                    var = 1 if t == 0 else (2 if t == NTo - 1 else 0)
                    jobs = []
                    for c in range(4):
                        qt = load_q("qd%d" % c, ut * 128, 128)
                        for s_ in range(2):
                            head = c + 4 * s_
                            jobs.append(dict(qt=qt, pbase=64 * s_, kl=kl, vl=vl, vc0=s_ * 65, nch=5, nb=3, bias=(dmb, dmb[:, var, :]),
                                             yt=yt, ycol=512 + head * 64, extra_den=esnk[:, head:head + 1]))
                    run_wide(jobs)
                out_tiles.append(out_tile(yt, ut * 128, x_u, ut * 128, 0, out_x, t * 128))
        if last:
            S.finish(out_tiles)
        else:
            S.barrier()
        st2.close()
    if not last:
        S.release_dsems()
        S.new_epoch()


def build_fused(SEQ, B):
    HALF = SEQ // 2
    EXT = HALF + 2 * HALO
    nc = bass.Bass("TRN2", target_bir_lowering=False)

    def din(name, shape):
        return nc.dram_tensor(name, list(shape), F32, kind="ExternalInput").ap()

    SH = dict(x_u=din("x_u", [EXT + HALF, D_MODEL]), xc=din("xc", [CTX, D_MODEL]), cvec=din("cvec", [128, 8, 2]),
              ident=din("ident", [128, 128]), blk=din("blk", [128, 128]), perm=din("perm", [128, 128]),
              ropeC=din("ropeC", [128, EXT + HALF]), ropeS=din("ropeS", [128, EXT + HALF]), sel=din("sel", [128, 4]))
    SH["x1_loc"] = nc.dram_tensor("x1_loc", [HALF, D_MODEL], F32).ap()
    SH["xc1_loc"] = nc.dram_tensor("xc1_loc", [CTX, D_MODEL], F32).ap()
    SH["GA"] = nc.dram_tensor("x1_all", [2 * HALF, D_MODEL], F32).ap()
    with contextlib.ExitStack() as st0:
        S = Sched(nc, st0)
        SH["pTb_t"] = S.ps("bankT", [128, 8, 128], BF16)
        pairA = st0.enter_context(nc.psum_tensor("ps_pairA", [128, 1024], F32))
        pairB = st0.enter_context(nc.psum_tensor("ps_pairB", [128, 1024], F32))
        b1 = Tile("bank1", pairA[:, 0:512], excl=True)
        b2 = Tile("bank2", pairA[:, 512:1024], excl=True)
        b3 = Tile("bank3", pairB[:, 0:512], excl=True)
        b4 = Tile("bank4", pairB[:, 512:1024], excl=True)
        SH["pairs"] = [(pairA, b1, b2), (pairB, b3, b4)]
        SH["banks"] = [SH["pTb_t"], b1, b2, b3, b4] + [S.ps("bank%d" % i, [128, 512], F32) for i in range(5, 8)]
        SH["G_reg"] = Tile("G_reg")
        emit_layer(nc, S, SH, 0, SEQ)
        S.sems["cc"] = st0.enter_context(nc.semaphore("s_cc"))
        RC = min(512, HALF)
        for i in range(HALF // RC):
            nc.gpsimd.collective_compute("AllGather", ALU.bypass, replica_groups=[[2 * b, 2 * b + 1] for b in range(B)],
                                         ins=[SH["x1_loc"][i * RC:(i + 1) * RC, :]],
                                         outs=[SH["GA"][i * 2 * RC:(i + 1) * 2 * RC, :]]).then_inc(S.sems["cc"], 1)
        SH["G_reg"].last_w = ("cc", HALF // RC)
        emit_layer(nc, S, SH, 1, SEQ)
    return nc


def rope_tables(pos):
    row = (pos // GRID_W).astype(np.float32)
    col = (pos % GRID_W).astype(np.float32)
    q = HD // 4
    inv = (10000.0 ** (-np.arange(q, dtype=np.float32) / q)).astype(np.float32)
    ar = row[None, :] * inv[:, None]
    ac = col[None, :] * inv[:, None]
    cr, sr, cc, sc = np.cos(ar), np.sin(ar), np.cos(ac), np.sin(ac)
    C = np.concatenate([cr, cr, cc, cc], axis=0)
    Sg = np.concatenate([-sr, sr, -sc, sc], axis=0)
    return (np.concatenate([C, C], 0).astype(np.float32), np.concatenate([Sg, Sg], 0).astype(np.float32))


def nbr_bias(rpb, g, gk, NT):
    rows = NT * 2
    out = np.full((8, 128, 128), NEG, np.float32)
    if gk < 0 or gk >= NT:
        return out
    ql = np.arange(128)
    r = 2 * g + ql // 64
    c = ql % 64
    kr = 2 * gk + ql // 64
    kc = ql % 64
    win_r = min(8, rows)
    rs = np.clip(r - win_r // 2, 0, rows - win_r)
    cs = np.clip(c - 8, 0, GRID_W - 16)
    valid = ((kr[:, None] >= rs[None, :]) & (kr[:, None] < rs[None, :] + win_r)
             & (kc[:, None] >= cs[None, :]) & (kc[:, None] < cs[None, :] + 16))
    di = kr[:, None] - r[None, :] + 7
    dj = kc[:, None] - c[None, :] + 15
    di = np.clip(di, 0, 14)
    dj = np.clip(dj, 0, 30)
    vals = rpb[:, di, dj]
    return np.where(valid[None], vals, np.float32(NEG)).astype(np.float32)


def chunk_rows(w):
    return np.ascontiguousarray(w.reshape(8, 128, w.shape[1]))


def prep_layer_inputs(layer, SEQ, xs, xcs, p):
    B = xs.shape[0]
    HALF = SEQ // 2
    EXT = HALF + 2 * HALO
    NT = SEQ // 128
    NTo = HALF // 128
    cfg = layer_cfg(layer)
    wi = p["w_in_even"][0] if layer == 0 else p["w_in_odd"][0]
    wo = p["w_out_even"][0] if layer == 0 else p["w_out_odd"][0]
    cols = []
    for f in cfg["fm"]:
        cols += f[1]
    for name, c0, n in cfg["tm"]:
        cols += list(range(c0, c0 + n))
    w_in_l = chunk_rows(np.ascontiguousarray(wi[:, cols]))
    w_out_l = chunk_rows(wo)
    w_mod_l = chunk_rows(p["w_mod"][layer])
    b_mod_l = np.ascontiguousarray(p["b_mod"][layer].reshape(24, 128).T)
    bgate = np.ascontiguousarray(np.broadcast_to(p["b_mod"][layer][2048:3072][None, :], (128, D_MODEL)))
    ident = np.eye(128, dtype=np.float32)
    blk = np.zeros((128, 128), np.float32)
    blk[:64, :64] = 1.0 / 64
    blk[64:, 64:] = 1.0 / 64
    perm = np.zeros((128, 128), np.float32)
    for m in range(128):
        k = m + 16 if (m % 32) < 16 else m - 16
        perm[k, m] = 1.0
    maps = []
    for b in range(B):
        for half in range(2):
            T0 = half * HALF
            pos_e = np.arange(T0 - HALO, T0 + HALF + HALO)
            valid_e = (pos_e >= 0) & (pos_e < SEQ)
            x_e = np.zeros((EXT, D_MODEL), np.float32)
            x_e[valid_e] = xs[b, pos_e[valid_e]]
            T1 = (1 - half) * HALF
            pos_o = np.arange(T1, T1 + HALF)
            x_u = np.concatenate([x_e, xs[b, pos_o]], axis=0)
            pos_u = np.concatenate([np.clip(pos_e, 0, SEQ - 1), pos_o])
            C, Sg = rope_tables(pos_u)
            cvec = np.stack([p["c"][b].reshape(8, 128).T, p["c_ctx"].reshape(8, 128).T], axis=-1)
            m = dict(x_u=x_u, xc=np.ascontiguousarray(xcs[b]), cvec=np.ascontiguousarray(cvec), w_mod=w_mod_l, b_mod=b_mod_l,
                     bgate=bgate, w_in=w_in_l, w_out=w_out_l, ident=ident, blk=blk, perm=perm, ropeC=C, ropeS=Sg)
            G0 = T0 // 128
            if layer == 0:
                m["gains"] = np.ascontiguousarray(np.stack([np.tile(p["a_q_norm"][0], 2), np.tile(p["a_k_norm"][0], 2)], axis=-1))
                rpb = p["b_rpb"][0]
                gi = min(max(G0 + 2, 2), NT - 3) if NT >= 6 else 0
                bi = np.stack([nbr_bias(rpb, gi, gi + j, NT) for j in range(-2, 3)], axis=0)
                m["bias_i"] = np.ascontiguousarray(bi.transpose(2, 1, 0, 3).reshape(128, 8, 640))
                ets = [0, 1, NTo - 2, NTo - 1]
                be = np.stack([np.stack([nbr_bias(rpb, G0 + t, G0 + t + j, NT) for j in range(-3, 4)], axis=0) for t in ets], axis=0)
                m["bias_e"] = np.ascontiguousarray(be.transpose(0, 2, 3, 1, 4).reshape(4, 8, 128, 896))
            else:
                a = np.arange(128)
                tri_prev = np.where(a[:, None] >= a[None, :], 0.0, NEG).astype(np.float32)
                tri_next = np.where(a[:, None] <= a[None, :], 0.0, NEG).astype(np.float32)
                full = np.full((128, 128), NEG, np.float32)
                first_prev = full if G0 == 0 else tri_prev
                last_next = full if G0 + NTo == NT else tri_next
                m["dmask"] = np.ascontiguousarray(np.stack([first_prev, last_next, tri_prev, tri_next], axis=1))
                m["lamv"] = np.ascontiguousarray(np.broadcast_to(p["c_lambda"][0].reshape(1, 256), (128, 256)))
                m["subln"] = np.ascontiguousarray(np.broadcast_to((p["c_subln"][0])[None, :], (128, 128)))
                m["sinks"] = np.ascontiguousarray(np.broadcast_to(p["d_sinks"][0][None, :], (128, 8)))
                m["fnorm"] = np.ascontiguousarray(np.broadcast_to(p["final_norm"][None, :], (128, D_MODEL)))
            maps.append(m)
    return maps


def prep_fused_inputs(SEQ, xs, xcs, p):
    m0 = prep_layer_inputs(0, SEQ, xs, xcs, p)
    m1 = prep_layer_inputs(1, SEQ, xs, xcs, p)
    shared = ("x_u", "xc", "cvec", "ident", "blk", "perm", "ropeC", "ropeS")
    maps = []
    for i, (a, b) in enumerate(zip(m0, m1)):
        half = i % 2
        m = {k: a[k] for k in shared}
        for k, v in a.items():
            if k not in shared:
                m["l0_" + k] = v
        for k, v in b.items():
            if k not in shared:
                m["l1_" + k] = v
        sel = np.zeros((128, 4), np.float32)
        sel[:, 0] = 1.0 if half == 1 else 0.0
        sel[:, 1] = 1.0 if half == 0 else 0.0
        sel[:, 2] = 1.0 if half == 1 else 0.0
        sel[:, 3] = 1.0 if half == 0 else 0.0
        m["sel"] = sel
        maps.append(m)
    return maps


def run_fused(SEQ, xs, xcs, p, runner=None):
    B = xs.shape[0]
    key = ("fused", SEQ, B)
    if key not in _NC_CACHE:
        _NC_CACHE[key] = build_fused(SEQ, B)
    nc = _NC_CACHE[key]
    maps = prep_fused_inputs(SEQ, xs, xcs, p)
    if runner is None:
        res = run_bass_kernel_spmd(nc, maps, core_ids=list(range(len(maps)))).results
    else:
        res = runner(nc, maps)
    HALF = SEQ // 2
    xo = np.zeros_like(xs)
    for b in range(B):
        for half in range(2):
            xo[b, half * HALF:(half + 1) * HALF] = res[2 * b + half]["out_x"]
    return xo


_NC_CACHE = {}


def kernel(x, c, ctx, c_ctx, w_mod, b_mod, w_in_even, w_out_even, a_q_norm, a_k_norm, b_rpb,
           w_in_odd, w_out_odd, c_lambda, c_subln, d_sinks, final_norm):
    p = dict(c=np.asarray(c, np.float32), c_ctx=np.asarray(c_ctx, np.float32), w_mod=np.asarray(w_mod, np.float32),
             b_mod=np.asarray(b_mod, np.float32), w_in_even=np.asarray(w_in_even, np.float32),
             w_out_even=np.asarray(w_out_even, np.float32), a_q_norm=np.asarray(a_q_norm, np.float32),
             a_k_norm=np.asarray(a_k_norm, np.float32), b_rpb=np.asarray(b_rpb, np.float32),
             w_in_odd=np.asarray(w_in_odd, np.float32), w_out_odd=np.asarray(w_out_odd, np.float32),
             c_lambda=np.asarray(c_lambda, np.float32), c_subln=np.asarray(c_subln, np.float32),
             d_sinks=np.asarray(d_sinks, np.float32), final_norm=np.asarray(final_norm, np.float32))
    xs = np.asarray(x, np.float32)
    xcs = np.asarray(ctx, np.float32)
    SEQ = xs.shape[1]
    return run_fused(SEQ, xs, xcs, p)
```

```python
import contextlib
import math
import numpy as np
import concourse.bass as bass
import concourse.mybir as mybir
from concourse.bass_utils import run_bass_kernel_spmd

F32 = mybir.dt.float32
BF16 = mybir.dt.bfloat16
AF = mybir.ActivationFunctionType
ALU = mybir.AluOpType

D_MODEL = 1024
CTX = 256
HD = 64
GRID_W = 64
SCALE = HD ** -0.5
EPS = 1e-6
NEG = -30000.0
HALO = 512
LQ = "sp"
SQ = "pool"


class Tile:
    __slots__ = ("name", "t", "last_w", "readers", "dsem", "dcount", "excl")

    def __init__(self, name, t=None, excl=False):
        self.name = name
        self.t = t
        self.last_w = None
        self.readers = {}
        self.dsem = None
        self.dcount = 0
        self.excl = excl

    def __getitem__(self, idx):
        return self.t[idx]


class Sched:
    def __init__(self, nc, stack):
        self.nc = nc
        self.stack = stack
        self.sem_stack = stack
        self.dtiles = []
        self.engs = {}
        self.sems = {}
        for en, e in (("pe", nc.tensor), ("act", nc.scalar), ("dve", nc.vector),
                      ("pool", nc.gpsimd), ("sp", nc.sync)):
            self.sems[en] = stack.enter_context(nc.semaphore("s_" + en))
            self.engs[en] = dict(eng=e, count=0, seen={}, key=en)
        self.epoch = 0
        self.nsem = 0
        self.prefix = ""
        self.free_dsems = []

    def sb(self, name, shape, dt):
        return Tile(name, self.stack.enter_context(self.nc.sbuf_tensor("sb_" + self.prefix + name, list(shape), dt)))

    def ps(self, name, shape, dt=F32):
        return Tile(name, self.stack.enter_context(self.nc.psum_tensor("ps_" + name, list(shape), dt)), excl=True)

    def _dsem(self, tile):
        if tile.dsem is None:
            if self.free_dsems:
                key, cnt = self.free_dsems.pop()
                tile.dsem = key
                tile.dcount = cnt
            else:
                key = "d%d" % self.nsem
                self.nsem += 1
                tile.dsem = key
                self.sems[key] = self.sem_stack.enter_context(self.nc.semaphore(key))
            self.dtiles.append(tile)
        return tile.dsem

    def new_epoch(self):
        self.epoch += 1
        for en, E in self.engs.items():
            key = "%s#%d" % (en, self.epoch)
            self.sems[key] = self.sem_stack.enter_context(self.nc.semaphore("s_%s_%d" % (en, self.epoch)))
            E["key"] = key
            E["count"] = 0

    def release_dsems(self):
        for t in self.dtiles:
            self.free_dsems.append((t.dsem, t.dcount))
            t.dsem = None
        self.dtiles = []

    def _wait_deps(self, en, reads, writes):
        E = self.engs[en]
        deps = {}

        def add(ev):
            if ev is None:
                return
            k, v = ev
            if deps.get(k, 0) < v:
                deps[k] = v

        me = E["key"]
        for t in reads:
            add(t.last_w)
            if t.excl:
                for k, v in t.readers.items():
                    if k != me:
                        add((k, v))
        for t in writes:
            add(t.last_w)
            for k, v in t.readers.items():
                if k != me:
                    add((k, v))
        for k, v in deps.items():
            if E["seen"].get(k, 0) < v:
                E["seen"][k] = v
                if k == me and en == "pe":
                    continue
                E["eng"].wait_ge(self.sems[k], v)

    def op(self, en, fn, reads=(), writes=()):
        E = self.engs[en]
        self._wait_deps(en, reads, writes)
        ins = fn(E["eng"])
        E["count"] += 1
        me = E["key"]
        ins.then_inc(self.sems[me], 1)
        for t in reads:
            t.readers[me] = E["count"]
        for t in writes:
            t.last_w = (me, E["count"])
            t.readers = {}
        return ins

    def dma(self, q, out, in_, reads=(), writes=(), sem_tile=None):
        E = self.engs[q]
        self._wait_deps(q, reads, writes)
        st = sem_tile if sem_tile is not None else (list(writes) + list(reads))[0]
        key = self._dsem(st)
        ins = E["eng"].dma_start(out=out, in_=in_)
        st.dcount += 16
        ins.then_inc(self.sems[key], 16)
        for t in reads:
            t.readers[key] = st.dcount
        for t in writes:
            t.last_w = (key, st.dcount)
            t.readers = {}
        return ins

    def barrier(self):
        for en, E in self.engs.items():
            for en2, E2 in self.engs.items():
                k2 = E2["key"]
                if en2 != en and E2["count"] and E["seen"].get(k2, 0) < E2["count"]:
                    E["seen"][k2] = E2["count"]
                    E["eng"].wait_ge(self.sems[k2], E2["count"])
            for t in self.dtiles:
                if t.dcount and E["seen"].get(t.dsem, 0) < t.dcount:
                    E["seen"][t.dsem] = t.dcount
                    E["eng"].wait_ge(self.sems[t.dsem], t.dcount)

    def finish(self, tiles, en="sp"):
        self._wait_deps(en, tiles, tiles)


class Rot:
    def __init__(self, tiles):
        self.tiles = tiles
        self.i = 0

    def next(self):
        t = self.tiles[self.i % len(self.tiles)]
        self.i += 1
        return t


def layer_cfg(layer):
    if layer == 0:
        qa, ka, va, qb, kb, vb, z = 0, 512, 640, 768, 1280, 1792, 2304
        fm = []
        for c in range(4):
            cols = list(range(qa + c * 64, qa + c * 64 + 64)) + list(range(qa + (4 + c) * 64, qa + (4 + c) * 64 + 64))
            fm.append(("qa%d" % c, cols, "q", True))
        fm.append(("ka", list(range(ka, ka + 128)), "k", True))
        for c in range(4):
            fm.append(("qb%d" % c, list(range(qb + c * 128, qb + c * 128 + 128)), None, False))
        for c in range(4):
            fm.append(("kb%d" % c, list(range(kb + c * 128, kb + c * 128 + 128)), None, False))
        tm = [("va", va, 128), ("vb", vb, 512), ("z", z, 1024)]
        dense_k, dense_v = ["ka"], ["va"]
        local_k, local_v = ["kb0", "kb1", "kb2", "kb3"], ["vb"]
    else:
        qc, kc, vc, qd, kd, vd, z = 0, 512, 1024, 1536, 2048, 2176, 2304
        fm = []
        for c in range(4):
            fm.append(("qc%d" % c, list(range(qc + c * 128, qc + c * 128 + 128)), None, True))
        for c in range(4):
            fm.append(("kc%d" % c, list(range(kc + c * 128, kc + c * 128 + 128)), None, True))
        for c in range(4):
            cols = list(range(qd + c * 64, qd + c * 64 + 64)) + list(range(qd + (4 + c) * 64, qd + (4 + c) * 64 + 64))
            fm.append(("qd%d" % c, cols, None, True))
        fm.append(("kd", list(range(kd, kd + 128)), None, True))
        tm = [("vc", vc, 512), ("vd", vd, 128), ("z", z, 1024)]
        dense_k, dense_v = ["kc0", "kc1", "kc2", "kc3"], ["vc"]
        local_k, local_v = ["kd"], ["vd"]
    return dict(fm=fm, tm=tm, dense_k=dense_k, dense_v=dense_v, local_k=local_k, local_v=local_v)


def lambda_init(layer):
    return 0.8 - 0.6 * math.exp(-0.3 * layer)


def emit_layer(nc, S, SH, layer, SEQ):
    HALF = SEQ // 2
    EXT = HALF + 2 * HALO
    NU = EXT + HALF + CTX
    NTo = HALF // 128
    NBo = HALF // 512
    U_OWN = HALO
    U_O = EXT
    U_C = EXT + HALF
    NKD = 2 * HALF + CTX
    NKC = NKD // 128
    last = layer == 1
    cfg = layer_cfg(layer)
    fm, tm = cfg["fm"], cfg["tm"]
    NFM = len(fm)
    fmi = {f[0]: i for i, f in enumerate(fm)}
    NCOL = NFM * 128 + sum(t[2] for t in tm)
    tmoff = {}
    o = NFM * 128
    for name, _, n in tm:
        tmoff[name] = o
        o += n
    lam0 = lambda_init(layer)

    LP = "l%d_" % layer
    S.prefix = LP

    def din(name, shape):
        return nc.dram_tensor(LP + name, list(shape), F32, kind="ExternalInput").ap()

    x_u, xc_in, cvec = SH["x_u"], SH["xc"], SH["cvec"]
    ident_d, blk_d, perm_d, ropeC, ropeS = SH["ident"], SH["blk"], SH["perm"], SH["ropeC"], SH["ropeS"]
    x1_loc, xc1_loc, GA = SH["x1_loc"], SH["xc1_loc"], SH["GA"]
    w_mod = din("w_mod", [8, 128, 3072])
    b_mod = din("b_mod", [128, 24])
    bgate = din("bgate", [128, D_MODEL])
    w_in = din("w_in", [8, 128, NCOL])
    w_out = din("w_out", [8, 128, D_MODEL])
    if layer == 0:
        gains = din("gains", [128, 2])
        bias_i = din("bias_i", [128, 8, 5 * 128])
        bias_e = din("bias_e", [4, 8, 128, 7 * 128])
    else:
        dmask = din("dmask", [128, 4, 128])
        lamv = din("lamv", [128, 256])
        subln = din("subln", [128, 128])
        sinks = din("sinks", [128, 8])
        fnorm = din("fnorm", [128, D_MODEL])
    if last:
        out_x = nc.dram_tensor("out_x", [HALF, D_MODEL], F32, kind="ExternalOutput").ap()
    else:
        out_x = x1_loc
        out_c = xc1_loc

    fmS = nc.dram_tensor(LP + "fmS", [NFM, 128, NU], BF16).ap()
    vdims = {"va": (2, 65), "vb": (8, 65), "vc": (4, 129), "vd": (2, 65)}
    vS = {}
    for name, _, n in tm:
        if name != "z":
            h, d = vdims[name]
            vS[name] = nc.dram_tensor(LP + "vS_" + name, [NU, h * d], BF16).ap()
    zS = nc.dram_tensor(LP + "zS", [NU, D_MODEL], BF16).ap()

    with contextlib.ExitStack() as st:
        S.stack = st
        st1 = contextlib.ExitStack()
        fm_reg = Tile("fm_reg")
        v_reg = Tile("v_reg")
        z_reg = Tile("z_reg")

        ident_f = S.sb("ident_f", [128, 128], F32)
        ident = S.sb("ident", [128, 128], BF16)
        blk = S.sb("blk", [128, 128], F32)
        perm = S.sb("perm", [128, 128], F32)
        epst = S.sb("epst", [128, 1], F32)
        zeros = S.sb("zeros", [128, 128], F32)
        junk = S.sb("junk", [128, D_MODEL], F32)
        ss_r = Rot([S.sb("ss%d" % i, [128, 2], F32) for i in range(2)])
        t1_r = Rot([S.sb("t1%d" % i, [128, 512], F32) for i in range(2)])
        gate_bc = S.sb("gate_bc", [128, 2, D_MODEL], F32)
        modt = S.sb("modt", [128, 24, 2], F32)
        bmodt = S.sb("bmodt", [128, 24], F32)
        cv = S.sb("cv", [128, 8, 2], F32)
        pTb_t = SH["pTb_t"]
        pTb = pTb_t.t
        banks = SH["banks"]
        sel_t = S.sb("sel_t", [128, 4], F32)
        S.dma(LQ, sel_t[:], SH["sel"], writes=[sel_t])
        xt2 = S.sb("xt2", [128, D_MODEL], F32)
        G_reg = SH["G_reg"]
        NTo_ = HALF // 128

        RCt = min(512, HALF) // 128

        def grow(r, t):
            return ((t // RCt) * 2 * RCt + r * RCt + (t % RCt)) * 128

        def load_x_tile(xt, utile):
            e = utile
            if layer == 0:
                if e >= (EXT + HALF) // 128:
                    c = e - (EXT + HALF) // 128
                    S.dma(LQ, xt[:], xc_in[c * 128:(c + 1) * 128, :], writes=[xt])
                else:
                    S.dma(LQ, xt[:], x_u[e * 128:(e + 1) * 128, :], writes=[xt])
                return
            if e >= (EXT + HALF) // 128:
                c = e - (EXT + HALF) // 128
                S.dma(LQ, xt[:], xc1_loc[c * 128:(c + 1) * 128, :], writes=[xt])
            elif e >= EXT // 128:
                o = e - EXT // 128
                S.dma(LQ, xt[:], GA[grow(0, o):grow(0, o) + 128, :], reads=[G_reg], writes=[xt])
                S.dma(LQ, xt2[:], GA[grow(1, o):grow(1, o) + 128, :], reads=[G_reg], writes=[xt2])
                S.op("act", lambda en: en.activation(out=xt[:], in_=xt[:], func=AF.Copy, scale=sel_t[:, 2:3]), reads=[xt, sel_t], writes=[xt])
                S.op("dve", lambda en: en.scalar_tensor_tensor(out=xt[:], in0=xt2[:], scalar=sel_t[:, 3:4], in1=xt[:], op0=ALU.mult, op1=ALU.add),
                     reads=[xt2, sel_t, xt], writes=[xt])
            elif 4 <= e < 4 + NTo_:
                t = e - 4
                S.dma(LQ, xt[:], x1_loc[t * 128:(t + 1) * 128, :], writes=[xt])
            elif e < 4:
                gt = grow(0, NTo_ - 4 + e)
                S.dma(LQ, xt[:], GA[gt:gt + 128, :], reads=[G_reg], writes=[xt])
                S.op("act", lambda en: en.activation(out=xt[:], in_=xt[:], func=AF.Copy, scale=sel_t[:, 0:1]), reads=[xt, sel_t], writes=[xt])
            else:
                gt = grow(1, e - 4 - NTo_)
                S.dma(LQ, xt[:], GA[gt:gt + 128, :], reads=[G_reg], writes=[xt])
                S.op("act", lambda en: en.activation(out=xt[:], in_=xt[:], func=AF.Copy, scale=sel_t[:, 1:2]), reads=[xt, sel_t], writes=[xt])

        S.dma(LQ, ident_f[:], ident_d, writes=[ident_f])
        S.dma(LQ, blk[:], blk_d, writes=[blk])
        S.dma(LQ, perm[:], perm_d, writes=[perm])
        S.dma(LQ, cv[:], cvec, writes=[cv])
        S.dma(LQ, bmodt[:], b_mod, writes=[bmodt])
        S.dma(LQ, gate_bc[:, 0, :], bgate, writes=[gate_bc])
        S.op("dve", lambda e: e.tensor_copy(out=ident[:], in_=ident_f[:]), reads=[ident_f], writes=[ident])
        S.op("pool", lambda e: e.memset(epst[:], EPS), writes=[epst])
        S.op("pool", lambda e: e.memset(zeros[:], 0.0), writes=[zeros])
        S.op("dve", lambda e: e.tensor_copy(out=gate_bc[:, 1, :], in_=gate_bc[:, 0, :]), reads=[gate_bc], writes=[gate_bc])
        S.stack = st
        if layer == 0:
            gn = S.sb("gn", [128, 2], F32)
            bi_t = S.sb("bi_t", [128, 8, 640], F32)
            S.dma(LQ, gn[:], gains, writes=[gn])
            S.dma(LQ, bi_t[:], bias_i, writes=[bi_t])
        else:
            dm_t = S.sb("dm_t", [128, 4, 128], F32)
            lam_t = S.sb("lam_t", [128, 256], F32)
            sub_t = S.sb("sub_t", [128, 128], F32)
            snk_t = S.sb("snk_t", [128, 8], F32)
            fn_t = S.sb("fn_t", [128, D_MODEL], F32)
            S.dma(LQ, dm_t[:], dmask, writes=[dm_t])
            S.dma(LQ, lam_t[:], lamv, writes=[lam_t])
            S.dma(LQ, sub_t[:], subln, writes=[sub_t])
            S.dma(LQ, snk_t[:], sinks, writes=[snk_t])
            S.dma(LQ, fn_t[:], fnorm, writes=[fn_t])
            lam_s = S.sb("lam_s", [128, 4], F32)
            lprod = S.sb("lprod", [128, 128], F32)
            esnk = S.sb("esnk", [128, 8], F32)
        S.stack = st1
        win = S.sb("win", [128, 8, NCOL], BF16)
        screp = S.sb("screp", [128, 2, 8, 128], F32)
        stage = Rot([S.sb("stage%d" % i, [128, 1024], F32) for i in range(2)])

        S.op("act", lambda e: e.activation(out=cv[:], in_=cv[:], func=AF.Silu), reads=[cv], writes=[cv])
        for w in range(2):
            for k in range(8):
                S.op("act", lambda e: e.activation(out=screp[:, w, k, :], in_=zeros[:], func=AF.Identity,
                                                   bias=cv[:, k, w:w + 1]), reads=[zeros, cv], writes=[screp])
        pmod = banks[5]
        pg = [banks[1], banks[2], banks[3], banks[4]]
        for k in range(8):
            for pi in range(3):
                stg = stage.next()
                S.dma(LQ, stg[:], w_mod[k, :, pi * 1024:(pi + 1) * 1024], writes=[stg])
                for jj in range(8):
                    j = pi * 8 + jj
                    S.op("pe", lambda e: e.matmul(pmod[:, j * 2:j * 2 + 2], lhsT=stg[:, jj * 128:(jj + 1) * 128], rhs=cv[:, k, :],
                                                  start=(k == 0 and j == 0), stop=(k == 7), skip_group_check=True),
                         reads=[stg, cv], writes=[pmod])
                if pi == 2:
                    for w in range(2):
                        for n in range(2):
                            S.op("pe", lambda e: e.matmul(pg[w * 2 + n][:], lhsT=screp[:, w, k, :],
                                                          rhs=stg[:, n * 512:(n + 1) * 512],
                                                          start=(k == 0), stop=(k == 7)),
                                 reads=[stg, screp], writes=[pg[w * 2 + n]])
        for w in range(2):
            S.op("dve", lambda e: e.tensor_tensor(out=modt[:, :, w], in0=pmod[:, 0:48].rearrange("p (j w) -> p j w", w=2)[:, :, w],
                                                  in1=bmodt[:], op=ALU.add), reads=[pmod, bmodt], writes=[modt])
            for n in range(2):
                S.op("dve", lambda e: e.tensor_tensor(out=gate_bc[:, w, n * 512:(n + 1) * 512], in0=pg[w * 2 + n][:],
                                                      in1=gate_bc[:, w, n * 512:(n + 1) * 512], op=ALU.add),
                     reads=[pg[w * 2 + n], gate_bc], writes=[gate_bc])
        S.op("dve", lambda e: e.tensor_scalar_add(out=modt[:, 8:16, :], in0=modt[:, 8:16, :], scalar1=1.0),
             reads=[modt], writes=[modt])

        cnt = 0
        for k in range(8):
            for c0 in range(0, NCOL, 1024):
                cn = min(1024, NCOL - c0)
                stg = stage.next()
                S.dma(LQ, stg[:, 0:cn], w_in[k, :, c0:c0 + cn], writes=[stg])
                S.op("dve", lambda e: e.tensor_copy(out=win[:, k, c0:c0 + cn], in_=stg[:, 0:cn]), reads=[stg], writes=[win])
                cnt += 1

        if layer == 1:
            S.op("dve", lambda e: e.tensor_tensor(out=lprod[:, 0:64], in0=lam_t[:, 0:64], in1=lam_t[:, 64:128], op=ALU.mult), reads=[lam_t], writes=[lprod])
            S.op("dve", lambda e: e.tensor_tensor(out=lprod[:, 64:128], in0=lam_t[:, 128:192], in1=lam_t[:, 192:256], op=ALU.mult), reads=[lam_t, lprod], writes=[lprod])
            S.op("dve", lambda e: e.reduce_sum(out=lam_s[:, 0:2], in_=lprod[:].rearrange("p (a d) -> p a d", a=2), axis=mybir.AxisListType.X), reads=[lprod], writes=[lam_s])
            S.op("act", lambda e: e.activation(out=lam_s[:, 0:2], in_=lam_s[:, 0:2], func=AF.Exp), reads=[lam_s], writes=[lam_s])
            S.op("dve", lambda e: e.tensor_tensor(out=lam_s[:, 2:3], in0=lam_s[:, 0:1], in1=lam_s[:, 1:2], op=ALU.subtract), reads=[lam_s], writes=[lam_s])
            S.op("dve", lambda e: e.tensor_scalar(out=lam_s[:, 3:4], in0=lam_s[:, 2:3], scalar1=lam0, scalar2=-1.0, op0=ALU.add, op1=ALU.mult), reads=[lam_s], writes=[lam_s])
            S.op("act", lambda e: e.activation(out=esnk[:], in_=snk_t[:], func=AF.Exp), reads=[snk_t], writes=[esnk])

        xt_r = Rot([S.sb("xt%d" % i, [128, D_MODEL], F32) for i in range(2)])
        xn_r = Rot([S.sb("xn%d" % i, [128, D_MODEL], BF16) for i in range(2)])
        hT_r = Rot([S.sb("hT%d" % i, [128, 8, 512], BF16) for i in range(2)])
        tabC_r = Rot([S.sb("tabC%d" % i, [128, 512], F32) for i in range(1)])
        tabS_r = Rot([S.sb("tabS%d" % i, [128, 512], F32) for i in range(1)])
        sq_r = Rot([S.sb("sq%d" % i, [128, 512], F32) for i in range(2)])
        rs_r = Rot([S.sb("rs%d" % i, [128, 512], F32) for i in range(2)])
        qn_r = Rot([S.sb("qn%d" % i, [128, 512], F32) for i in range(3)])
        t2_r = Rot([S.sb("t2%d" % i, [128, 512], F32) for i in range(2)])
        fo_r = Rot([S.sb("fo%d" % i, [128, 512], BF16) for i in range(4)])
        vst = {}
        for name in vS:
            h, d = vdims[name]
            vst[name] = Rot([S.sb("vst_%s%d" % (name, i), [128, h, d], BF16) for i in range(2)])
            for t in vst[name].tiles:
                S.op("pool", lambda e: e.memset(t[:], 1.0), writes=[t])
        zst_r = Rot([S.sb("zst%d" % i, [128, D_MODEL], BF16) for i in range(2)])
        bA = Rot([banks[1], banks[2], banks[3]])
        bB = Rot([banks[4], banks[5]])

        def phase1_block(u0, ntiles, src, src_row0, w, fm_list, tm_list, rope_col0):
            ntok = ntiles * 128
            hT = hT_r.next()
            for ti in range(ntiles):
                xt = xt_r.next()
                ss = ss_r.next()
                xn = xn_r.next()
                load_x_tile(xt, u0 // 128 + ti)
                S.op("act", lambda e: e.activation(out=junk[:], in_=xt[:], func=AF.Square, accum_out=ss[:, 0:1]), reads=[xt], writes=[junk, ss])
                S.op("act", lambda e: e.activation(out=ss[:, 1:2], in_=ss[:, 0:1], func=AF.Ln, scale=1.0 / D_MODEL, bias=epst[:]), reads=[ss, epst], writes=[ss])
                S.op("act", lambda e: e.activation(out=ss[:, 1:2], in_=ss[:, 1:2], func=AF.Exp, scale=-0.5), reads=[ss], writes=[ss])
                S.op("act", lambda e: e.activation(out=xn[:], in_=xt[:], func=AF.Copy, scale=ss[:, 1:2]), reads=[xt, ss], writes=[xn])
                for k in range(8):
                    S.op("pe", lambda e: e.transpose(out=pTb[:, k, :], in_=xn[:, k * 128:(k + 1) * 128], identity=ident[:]),
                         reads=[xn, ident], writes=[pTb_t])
                for k in range(8):
                    S.op("dve", lambda e: e.tensor_scalar(out=hT[:, k, ti * 128:(ti + 1) * 128], in0=pTb[:, k, :],
                                                          scalar1=modt[:, 8 + k, w:w + 1], scalar2=modt[:, k, w:w + 1],
                                                          op0=ALU.mult, op1=ALU.add), reads=[pTb_t, modt], writes=[hT])
            rope_needed = any(fm[i][3] for i in fm_list) and rope_col0 is not None
            if rope_needed:
                tC = tabC_r.next()
                tS = tabS_r.next()
                S.dma(LQ, tC[:, 0:ntok], ropeC[:, rope_col0:rope_col0 + ntok], writes=[tC])
                S.dma(LQ, tS[:, 0:ntok], ropeS[:, rope_col0:rope_col0 + ntok], writes=[tS])
            chs = [dict(i=i) for i in fm_list]

            def st_main(ch):
                i = ch["i"]
                pa = bA.next()
                for k in range(8):
                    S.op("pe", lambda e: e.matmul(pa[:, 0:ntok], lhsT=win[:, k, i * 128:(i + 1) * 128], rhs=hT[:, k, 0:ntok],
                                                  start=(k == 0), stop=(k == 7)), reads=[win, hT], writes=[pa])
                ch["pa"] = pa

            def st_norm(ch):
                i = ch["i"]
                name, _, nkind, roped = fm[i]
                pa = ch["pa"]
                fo = fo_r.next()
                ch["fo"] = fo
                do_rope = roped and rope_col0 is not None
                ch["do_rope"] = do_rope
                if nkind is None and not do_rope:
                    S.op("act", lambda e: e.activation(out=fo[:, 0:ntok], in_=pa[:, 0:ntok], func=AF.Copy), reads=[pa], writes=[fo])
                    return
                qn = qn_r.next()
                ch["qn"] = qn
                if nkind is not None:
                    sq = sq_r.next()
                    rs = rs_r.next()
                    pb = bB.next()
                    gcol = 0 if nkind == "q" else 1
                    S.op("act", lambda e: e.activation(out=sq[:, 0:ntok], in_=pa[:, 0:ntok], func=AF.Square), reads=[pa], writes=[sq])
                    S.op("pe", lambda e: e.matmul(pb[:, 0:ntok], lhsT=blk[:], rhs=sq[:, 0:ntok], start=True, stop=True), reads=[blk, sq], writes=[pb])
                    S.op("act", lambda e: e.activation(out=rs[:, 0:ntok], in_=pb[:, 0:ntok], func=AF.Ln, bias=epst[:]), reads=[pb, epst], writes=[rs])
                    S.op("act", lambda e: e.activation(out=rs[:, 0:ntok], in_=rs[:, 0:ntok], func=AF.Exp, scale=-0.5), reads=[rs], writes=[rs])
                    dst = qn if do_rope else fo
                    S.op("dve", lambda e: e.scalar_tensor_tensor(out=dst[:, 0:ntok], in0=pa[:, 0:ntok], scalar=gn[:, gcol:gcol + 1],
                                                                 in1=rs[:, 0:ntok], op0=ALU.mult, op1=ALU.mult),
                         reads=[pa, gn, rs], writes=[dst])
                else:
                    S.op("act", lambda e: e.activation(out=qn[:, 0:ntok], in_=pa[:, 0:ntok], func=AF.Copy), reads=[pa], writes=[qn])

            def st_rope(ch):
                i = ch["i"]
                fo = ch["fo"]
                if ch["do_rope"]:
                    qn = ch["qn"]
                    pb2 = bB.next()
                    t1 = t1_r.next()
                    t2 = t2_r.next()
                    S.op("pe", lambda e: e.matmul(pb2[:, 0:ntok], lhsT=perm[:], rhs=qn[:, 0:ntok], start=True, stop=True), reads=[perm, qn], writes=[pb2])
                    S.op("dve", lambda e: e.tensor_tensor(out=t1[:, 0:ntok], in0=qn[:, 0:ntok], in1=tC[:, 0:ntok], op=ALU.mult), reads=[qn, tC], writes=[t1])
                    S.op("dve", lambda e: e.tensor_tensor(out=t2[:, 0:ntok], in0=pb2[:, 0:ntok], in1=tS[:, 0:ntok], op=ALU.mult), reads=[pb2, tS], writes=[t2])
                    S.op("dve", lambda e: e.tensor_tensor(out=fo[:, 0:ntok], in0=t1[:, 0:ntok], in1=t2[:, 0:ntok], op=ALU.add), reads=[t1, t2], writes=[fo])
                S.dma(SQ, fmS[i, :, u0:u0 + ntok], fo[:, 0:ntok], reads=[fo], writes=[fm_reg], sem_tile=fo)

            nchs = len(chs)
            for step in range(nchs + 2):
                if step < nchs:
                    st_main(chs[step])
                if 0 <= step - 1 < nchs:
                    st_norm(chs[step - 1])
                if 0 <= step - 2 < nchs:
                    st_rope(chs[step - 2])
            for name in tm_list:
                col0 = tmoff[name]
                ncols = dict((t[0], t[2]) for t in tm)[name]
                for ti in range(ntiles):
                    for n0 in range(0, ncols, 512):
                        nn = min(512, ncols - n0)
                        pa = bA.next()
                        for k in range(8):
                            S.op("pe", lambda e: e.matmul(pa[:, 0:nn], lhsT=hT[:, k, ti * 128:(ti + 1) * 128],
                                                          rhs=win[:, k, col0 + n0:col0 + n0 + nn], start=(k == 0), stop=(k == 7)),
                                 reads=[win, hT], writes=[pa])
                        if name == "z":
                            if n0 == 0:
                                zst = zst_r.next()
                            S.op("act", lambda e: e.activation(out=zst[:, n0:n0 + nn], in_=pa[:, 0:nn], func=AF.Silu), reads=[pa], writes=[zst])
                            if n0 + nn == ncols:
                                S.dma(SQ, zS[u0 + ti * 128:u0 + (ti + 1) * 128, :], zst[:], reads=[zst], writes=[z_reg], sem_tile=zst)
                        else:
                            h, d = vdims[name]
                            dv = d - 1
                            vt = vst[name].next()
                            S.op("dve", lambda e: e.tensor_copy(out=vt[:, :, 0:dv], in_=pa[:, 0:nn].rearrange("p (h d) -> p h d", d=dv)),
                                 reads=[pa], writes=[vt])
                            S.dma(SQ, vS[name][u0 + ti * 128:u0 + (ti + 1) * 128, :], vt[:].rearrange("p h d -> p (h d)"),
                                  reads=[vt], writes=[v_reg], sem_tile=vt)


        all_fm = list(range(NFM))
        all_tm = [t[0] for t in tm]
        lk = [fmi[n] for n in cfg["local_k"]]
        dk = [fmi[n] for n in cfg["dense_k"]]
        xc_ap = xc_in
        phase1_block(U_C, 2, xc_ap, 0, 1, all_fm, all_tm, None)
        eblocks = list(range(EXT // 512))
        eblocks = [b for b in eblocks if HALO <= b * 512 < HALO + HALF] + [b for b in eblocks if not (HALO <= b * 512 < HALO + HALF)]
        for b in eblocks:
            u0 = b * 512
            own = HALO <= u0 < HALO + HALF
            if own:
                phase1_block(u0, 4, x_u, u0, 0, all_fm, all_tm, u0)
            else:
                phase1_block(u0, 4, x_u, u0, 0, lk, cfg["local_v"], u0)
        for b in range(HALF // 512):
            u0 = U_O + b * 512
            phase1_block(u0, 4, x_u, u0, 0, dk, cfg["dense_v"], u0)

        S.barrier()
        st1.close()
        st2 = contextlib.ExitStack()
        S.stack = st2
        wout = S.sb("wout", [128, 8, D_MODEL], BF16)
        stage2 = Rot([S.sb("stage2_%d" % i, [128, D_MODEL], F32) for i in range(2)])
        for k in range(8):
            stg = stage2.next()
            S.dma(LQ, stg[:], w_out[k], writes=[stg])
            S.op("dve", lambda e: e.tensor_copy(out=wout[:, k, :], in_=stg[:]), reads=[stg], writes=[wout])
        bS = Rot([banks[1], banks[2], banks[3]])
        bACC = Rot([banks[5], banks[6], banks[7]])
        pT_r = Rot([S.sb("pT%d" % i, [128, 512], BF16) for i in range(3)])
        sb_r = Rot([S.sb("sbias%d" % i, [128, 512], F32) for i in range(2)])
        nbuf_d = 1 if layer == 0 else 2
        KT_r = Rot([S.sb("KT%d" % i, [128, NKD], BF16) for i in range(nbuf_d)])
        VDW = 130 if layer == 0 else 129
        VD_r = Rot([S.sb("VD%d" % i, [128, NKC, VDW], BF16) for i in range(nbuf_d)])
        q_r = Rot([S.sb("qblk%d" % i, [128, 512], BF16) for i in range(6)])
        NLK = 9
        kl_r = Rot([S.sb("kl%d" % i, [128, NLK * 128], BF16) for i in range(2)])
        VLW = 130
        vl_r = Rot([S.sb("vl%d" % i, [128, NLK, VLW], BF16) for i in range(2)])
        y_t = [S.sb("y%d" % i, [128, D_MODEL], F32) for i in range(4)]
        rec_r = Rot([S.sb("rec%d" % i, [128, 8], F32) for i in range(4)])
        zl_r = Rot([S.sb("zl%d" % i, [128, D_MODEL], BF16) for i in range(1)])
        yb_r = Rot([S.sb("yb%d" % i, [128, D_MODEL], BF16) for i in range(1)])
        yT_r = Rot([S.sb("yT%d" % i, [128, 8, 128], BF16) for i in range(1)])
        xo_r = Rot([S.sb("xo%d" % i, [128, D_MODEL], F32) for i in range(1)])
        res_r = Rot([S.sb("res%d" % i, [128, D_MODEL], F32) for i in range(2)])
        be_r = Rot([S.sb("be%d" % i, [128, 7 * 128], F32) for i in range(2)]) if layer == 0 else None
        dcache = {}

        def load_dense(kname, vname, vc0, vw):
            key = (kname, vname, vc0)
            if dcache.get("key") == key:
                return dcache["KT"], dcache["VD"]
            KT = KT_r.next()
            VD = VD_r.next()
            i = fmi[kname]
            S.dma(LQ, KT[:, 0:HALF], fmS[i, :, U_OWN:U_OWN + HALF], reads=[fm_reg], writes=[KT])
            S.dma(LQ, KT[:, HALF:2 * HALF], fmS[i, :, U_O:U_O + HALF], reads=[fm_reg], writes=[KT])
            S.dma(LQ, KT[:, 2 * HALF:NKD], fmS[i, :, U_C:U_C + CTX], reads=[fm_reg], writes=[KT])
            for (c0, u0, n) in ((0, U_OWN, HALF), (HALF // 128, U_O, HALF), (2 * HALF // 128, U_C, CTX)):
                S.dma(LQ, VD[:, c0:c0 + n // 128, 0:vw], vS[vname][u0:u0 + n, vc0:vc0 + vw].rearrange("(k p) c -> p k c", p=128),
                      reads=[v_reg], writes=[VD])
            dcache.update(key=key, KT=KT, VD=VD)
            return KT, VD

        def attend(qt, pbase, NQ, kchunks, acc_list, vwidth, bias_fn=None):
            nq = NQ // 128
            n = len(kchunks)
            pend = []
            first = {}

            def issue_s(j):
                kt, kc0, vt, vap = kchunks[j]
                ps = bS.next()
                S.op("pe", lambda e: e.matmul(ps[:, 0:NQ], lhsT=kt[pbase:pbase + 64, kc0:kc0 + 128], rhs=qt[pbase:pbase + 64, 0:NQ],
                                              start=True, stop=True), reads=[kt, qt], writes=[ps])
                pT = pT_r.next()
                b = bias_fn(j) if bias_fn is not None else None
                if b is not None:
                    btile, bap = b
                    sb = sb_r.next()
                    S.op("dve", lambda e: e.scalar_tensor_tensor(out=sb[:, 0:NQ], in0=ps[:, 0:NQ], scalar=SCALE, in1=bap,
                                                                 op0=ALU.mult, op1=ALU.add), reads=[ps, btile], writes=[sb])
                    S.op("act", lambda e: e.activation(out=pT[:, 0:NQ], in_=sb[:, 0:NQ], func=AF.Exp), reads=[sb], writes=[pT])
                else:
                    S.op("act", lambda e: e.activation(out=pT[:, 0:NQ], in_=ps[:, 0:NQ], func=AF.Exp, scale=SCALE), reads=[ps], writes=[pT])
                return pT

            def issue_pv(j, pT):
                kt, kc0, vt, vap = kchunks[j]
                for s in range(nq):
                    acc, c0 = acc_list[s]
                    fst = first.get(id(acc), True)
                    first[id(acc)] = False
                    S.op("pe", lambda e: e.matmul(acc[:, c0:c0 + vwidth], lhsT=pT[:, s * 128:(s + 1) * 128], rhs=vap,
                                                  start=(j == 0 and fst), stop=(j == n - 1), skip_group_check=True),
                         reads=[pT, vt], writes=[acc])

            prev = None
            for j in range(n):
                pT = issue_s(j)
                if prev is not None:
                    issue_pv(prev[0], prev[1])
                prev = (j, pT)
            issue_pv(prev[0], prev[1])

        onesel_f = S.sb("onesel_f", [128, 2, 2], F32)
        S.op("pool", lambda e: e.memset(onesel_f[:], 0.0), writes=[onesel_f])
        S.op("pool", lambda e: e.memset(onesel_f[:, 0, 0:1], 1.0), writes=[onesel_f])
        S.op("pool", lambda e: e.memset(onesel_f[:, 1, 1:2], 1.0), writes=[onesel_f])
        den_acc = S.sb("den_acc", [128, 1024], F32)
        if layer == 1:
            dmb = S.sb("dmb", [128, 3, 384], F32)
            S.op("pool", lambda e: e.memset(dmb[:], 0.0), writes=[dmb])
            for var, (pi, ni) in enumerate(((2, 3), (0, 3), (2, 1))):
                S.op("dve", lambda e: e.tensor_copy(out=dmb[:, var, 0:128], in_=dm_t[:, pi, :]), reads=[dm_t], writes=[dmb])
                S.op("dve", lambda e: e.tensor_copy(out=dmb[:, var, 256:384], in_=dm_t[:, ni, :]), reads=[dm_t], writes=[dmb])
        pT2_r = Rot([S.sb("pT2_%d" % i, [128, 1024], BF16) for i in range(3)])
        oT_r = Rot([S.sb("oT%d" % i, [128, 512], F32) for i in range(2)])
        pairs = SH["pairs"]

        def dense_pair(qt, KT, VD, vap_fn, vw, accs, den=None):
            n = NKC

            def issue_s(j):
                pt, ta, tb = pairs[j % 2]
                for m, tt in ((0, ta), (1, tb)):
                    S.op("pe", lambda e: e.matmul(tt[:, 0:512], lhsT=KT[64 * m:64 * m + 64, j * 128:(j + 1) * 128],
                                                  rhs=qt[64 * m:64 * m + 64, 0:512], start=True, stop=True), reads=[KT, qt], writes=[tt])
                pT = pT2_r.next()
                S.op("act", lambda e: e.activation(out=pT[:], in_=pt[:, :], func=AF.Exp, scale=SCALE), reads=[ta, tb], writes=[pT])
                return pT

            def issue_pv(j, pT):
                for m in range(2):
                    S.op("pe", lambda e: e.matmul(accs[m][0:vw, 0:512], lhsT=vap_fn(j, m), rhs=pT[:, m * 512:(m + 1) * 512],
                                                  start=(j == 0), stop=(j == n - 1)), reads=[pT, VD], writes=[accs[m]])
                if den is not None:
                    if j == 0:
                        S.op("dve", lambda e: e.tensor_copy(out=den_acc[:], in_=pT[:]), reads=[pT], writes=[den_acc])
                    else:
                        S.op("dve", lambda e: e.tensor_tensor(out=den_acc[:], in0=den_acc[:], in1=pT[:], op=ALU.add), reads=[pT, den_acc], writes=[den_acc])
                    if j == n - 1:
                        for m in range(2):
                            S.op("pe", lambda e: e.matmul(den[0:2, 0:512], lhsT=onesel_f[:, m, :], rhs=den_acc[:, m * 512:(m + 1) * 512],
                                                          start=(m == 0), stop=(m == 1)), reads=[den_acc, onesel_f], writes=[den])

            prev = None
            for j in range(n):
                pT = issue_s(j)
                if prev is not None:
                    issue_pv(prev[0], prev[1])
                prev = (j, pT)
            issue_pv(prev[0], prev[1])

        def untranspose(acc, rows, fin, width):
            oT = oT_r.next()
            S.op("act", lambda e: e.activation(out=oT[0:rows, :], in_=acc[0:rows, 0:512], func=AF.Copy), reads=[acc], writes=[oT])
            for sidx in range(4):
                S.op("pe", lambda e: e.transpose(out=fin[:, sidx * width:sidx * width + rows], in_=oT[0:rows, sidx * 128:(sidx + 1) * 128],
                                                 identity=ident_f[0:rows, 0:rows]), reads=[oT, ident_f], writes=[fin])

        wide_i = [0]

        def wide_a(job):
            qt, pbase, kl, nch, nb, bias = job["qt"], job["pbase"], job["kl"], job["nch"], job["nb"], job["bias"]
            pt, ta, tb = pairs[wide_i[0] % 2]
            wide_i[0] += 1
            for j in range(nch):
                tt = ta if j < 4 else tb
                S.op("pe", lambda e: e.matmul(pt[:, j * 128:(j + 1) * 128], lhsT=kl[pbase:pbase + 64, j * 128:(j + 1) * 128],
                                              rhs=qt[pbase:pbase + 64, 0:128], start=True, stop=True), reads=[kl, qt], writes=[tt])
            pT = pT2_r.next()
            used = [ta] + ([tb] if nch > 4 else [])
            if bias is not None:
                btile, bap = bias
                sbw = stage2.next()
                S.op("dve", lambda e: e.scalar_tensor_tensor(out=sbw[:, 0:nb * 128], in0=pt[:, 0:nb * 128], scalar=SCALE, in1=bap,
                                                             op0=ALU.mult, op1=ALU.add), reads=used + [btile], writes=[sbw])
                S.op("act", lambda e: e.activation(out=pT[:, 0:nb * 128], in_=sbw[:, 0:nb * 128], func=AF.Exp), reads=[sbw], writes=[pT])
                if nch > nb:
                    S.op("act", lambda e: e.activation(out=pT[:, nb * 128:nch * 128], in_=pt[:, nb * 128:nch * 128], func=AF.Exp, scale=SCALE),
                         reads=used, writes=[pT])
            else:
                S.op("act", lambda e: e.activation(out=pT[:, 0:nch * 128], in_=pt[:, 0:nch * 128], func=AF.Exp, scale=SCALE), reads=used, writes=[pT])
            job["pT"] = pT

        def wide_b(job):
            pT, vl, vc0, nch = job["pT"], job["vl"], job["vc0"], job["nch"]
            acc = bACC.next()
            for j in range(nch):
                S.op("pe", lambda e: e.matmul(acc[:, 0:65], lhsT=pT[:, j * 128:(j + 1) * 128], rhs=vl[:, j, vc0:vc0 + 65],
                                              start=(j == 0), stop=(j == nch - 1)), reads=[pT, vl], writes=[acc])
            finish_head([(acc, 0)], 65, [job["yt"]], job["ycol"], extra_den=job.get("extra_den"))

        def run_wide(jobs):
            prev = None
            for job in jobs:
                wide_a(job)
                if prev is not None:
                    wide_b(prev)
                prev = job
            if prev is not None:
                wide_b(prev)

        def finish_head(acc_list, vwidth, y_tiles, ycol, extra_den=None, scale_ap=None):
            dv = vwidth - 1
            for s, (acc, c0) in enumerate(acc_list):
                rec = rec_r.next()
                if extra_den is not None:
                    S.op("dve", lambda e: e.tensor_tensor(out=rec[:, 0:1], in0=acc[:, c0 + dv:c0 + dv + 1], in1=extra_den, op=ALU.add),
                         reads=[acc, esnk], writes=[rec])
                    S.op("dve", lambda e: e.reciprocal(out=rec[:, 1:2], in_=rec[:, 0:1]), reads=[rec], writes=[rec])
                else:
                    S.op("dve", lambda e: e.reciprocal(out=rec[:, 1:2], in_=acc[:, c0 + dv:c0 + dv + 1]), reads=[acc], writes=[rec])
                yt = y_tiles[s]
                S.op("act", lambda e: e.activation(out=yt[:, ycol:ycol + dv], in_=acc[:, c0:c0 + dv], func=AF.Copy, scale=rec[:, 1:2]),
                     reads=[acc, rec], writes=[yt])

        def out_tile(yt, u_tok, src, src_row, w, dst, dst_row):
            zl = zl_r.next()
            yb = yb_r.next()
            yT = yT_r.next()
            xo = xo_r.next()
            res = res_r.next()
            S.dma(LQ, zl[:], zS[u_tok:u_tok + 128, :], reads=[z_reg], writes=[zl])
            load_x_tile(xo, u_tok // 128)
            S.op("dve", lambda e: e.tensor_tensor(out=yb[:], in0=yt[:], in1=zl[:], op=ALU.mult), reads=[yt, zl], writes=[yb])
            for k in range(8):
                S.op("pe", lambda e: e.transpose(out=pTb[:, k, :], in_=yb[:, k * 128:(k + 1) * 128], identity=ident[:]),
                     reads=[yb, ident], writes=[pTb_t])
            S.op("act", lambda e: e.activation(out=yT[:].rearrange("p k t -> p (k t)"), in_=pTb[:].rearrange("p k t -> p (k t)"), func=AF.Copy),
                 reads=[pTb_t], writes=[yT])
            for n in range(2):
                po = bACC.next()
                for k in range(8):
                    S.op("pe", lambda e: e.matmul(po[:], lhsT=yT[:, k, :], rhs=wout[:, k, n * 512:(n + 1) * 512], start=(k == 0), stop=(k == 7)),
                         reads=[yT, wout], writes=[po])
                S.op("dve", lambda e: e.tensor_tensor(out=res[:, n * 512:(n + 1) * 512], in0=po[:], in1=gate_bc[:, w, n * 512:(n + 1) * 512], op=ALU.mult),
                     reads=[po, gate_bc], writes=[res])
            S.op("dve", lambda e: e.tensor_tensor(out=res[:], in0=res[:], in1=xo[:], op=ALU.add), reads=[res, xo], writes=[res])
            if last:
                ss = ss_r.next()
                S.op("act", lambda e: e.activation(out=junk[:], in_=res[:], func=AF.Square, accum_out=ss[:, 0:1]), reads=[res], writes=[junk, ss])
                S.op("act", lambda e: e.activation(out=ss[:, 1:2], in_=ss[:, 0:1], func=AF.Ln, scale=1.0 / D_MODEL, bias=epst[:]), reads=[ss, epst], writes=[ss])
                S.op("act", lambda e: e.activation(out=ss[:, 1:2], in_=ss[:, 1:2], func=AF.Exp, scale=-0.5), reads=[ss], writes=[ss])
                S.op("act", lambda e: e.activation(out=xo[:], in_=res[:], func=AF.Copy, scale=ss[:, 1:2]), reads=[res, ss], writes=[xo])
                S.op("dve", lambda e: e.tensor_tensor(out=res[:], in0=xo[:], in1=fn_t[:], op=ALU.mult), reads=[xo, fn_t], writes=[res])
            S.dma(SQ, dst[dst_row:dst_row + 128, :], res[:], reads=[res], sem_tile=res)
            return res

        out_tiles = []

        def load_q(name, u0, n):
            qt = q_r.next()
            S.dma(LQ, qt[:, 0:n], fmS[fmi[name], :, u0:u0 + n], reads=[fm_reg], writes=[qt])
            return qt

        def load_local(knames_idx, vname, vc0, vw, utiles):
            kl = kl_r.next()
            vl = vl_r.next()
            pos = 0
            for (ut0, cnt) in utiles:
                S.dma(LQ, kl[:, pos * 128:(pos + cnt) * 128], fmS[knames_idx, :, ut0 * 128:(ut0 + cnt) * 128], reads=[fm_reg], writes=[kl])
                S.dma(LQ, vl[:, pos:pos + cnt, 0:vw], vS[vname][ut0 * 128:(ut0 + cnt) * 128, vc0:vc0 + vw].rearrange("(k p) c -> p k c", p=128),
                      reads=[v_reg], writes=[vl])
                pos += cnt
            return kl, vl

        UC_T = U_C // 128

        if layer == 0:
            for t in range(2):
                yt = y_t[t]
                u0 = U_C + t * 128
                jobs = []
                kl, vl = load_local(fmi["ka"], "va", 0, 130, [(UC_T, 2)])
                for c in range(4):
                    qt = load_q("qa%d" % c, u0, 128)
                    for s_ in range(2):
                        jobs.append(dict(qt=qt, pbase=64 * s_, kl=kl, vl=vl, vc0=s_ * 65, nch=2, nb=0, bias=None, yt=yt, ycol=(c + 4 * s_) * 64))
                run_wide(jobs)
                for c in range(4):
                    kl, vl = load_local(fmi["kb%d" % c], "vb", c * 130, 130, [(UC_T, 2)])
                    qt = load_q("qb%d" % c, u0, 128)
                    run_wide([dict(qt=qt, pbase=64 * s_, kl=kl, vl=vl, vc0=s_ * 65, nch=2, nb=0, bias=None, yt=yt, ycol=512 + (2 * c + s_) * 64)
                              for s_ in range(2)])
                out_tiles.append(out_tile(yt, u0, xc_in, t * 128, 1, out_c, t * 128))

        for qb in range(NBo):
            u0 = U_OWN + qb * 512
            if layer == 0:
                KT, VD = load_dense("ka", "va", 0, 130)
                for c in range(4):
                    qt = load_q("qa%d" % c, u0, 512)
                    accs = [banks[5], banks[6]]
                    dense_pair(qt, KT, VD, lambda j, m: VD[:, j, m * 65:(m + 1) * 65], 65, accs)
                    for s in range(2):
                        head = c + 4 * s
                        fin = banks[7]
                        untranspose(accs[s], 65, fin, 65)
                        finish_head([(fin, i * 65) for i in range(4)], 65, y_t, head * 64)
            else:
                for h in range(4):
                    KT, VD = load_dense("kc%d" % h, "vc", h * 129, 129)
                    qt = load_q("qc%d" % h, u0, 512)
                    accs = [banks[5], banks[6]]
                    den = banks[7]
                    dense_pair(qt, KT, VD, lambda j, m: VD[:, j, 0:128], 128, accs, den=den)
                    fins = [banks[1], banks[2]]
                    untranspose(accs[0], 128, fins[0], 128)
                    untranspose(accs[1], 128, fins[1], 128)
                    dfin = banks[3]
                    untranspose(den, 2, dfin, 2)
                    o_m = [[(fins[0], i * 128, i * 2 + 0) for i in range(4)], [(fins[1], i * 128, i * 2 + 1) for i in range(4)]]
                    for s in range(4):
                        rec = rec_r.next()
                        yt = y_t[s]
                        a0, c0, d0 = o_m[0][s]
                        a1, c1, d1 = o_m[1][s]
                        S.op("dve", lambda e: e.reciprocal(out=rec[:, 0:1], in_=dfin[:, d0:d0 + 1]), reads=[dfin], writes=[rec])
                        S.op("dve", lambda e: e.reciprocal(out=rec[:, 1:2], in_=dfin[:, d1:d1 + 1]), reads=[dfin, rec], writes=[rec])
                        S.op("dve", lambda e: e.tensor_tensor(out=rec[:, 2:3], in0=rec[:, 1:2], in1=lam_s[:, 3:4], op=ALU.mult), reads=[rec, lam_s], writes=[rec])
                        t1 = t1_r.next()
                        S.op("act", lambda e: e.activation(out=t1[:, 0:128], in_=a0[:, c0:c0 + 128], func=AF.Copy, scale=rec[:, 0:1]), reads=[a0, rec], writes=[t1])
                        S.op("dve", lambda e: e.scalar_tensor_tensor(out=t1[:, 128:256], in0=a1[:, c1:c1 + 128], scalar=rec[:, 2:3], in1=t1[:, 0:128],
                                                                     op0=ALU.mult, op1=ALU.add), reads=[a1, rec, t1], writes=[t1])
                        S.op("act", lambda e: e.activation(out=t1[:, 256:384], in_=t1[:, 128:256], func=AF.Square, accum_out=rec[:, 3:4]), reads=[t1], writes=[t1, rec])
                        S.op("act", lambda e: e.activation(out=rec[:, 4:5], in_=rec[:, 3:4], func=AF.Ln, scale=1.0 / 128, bias=epst[:]), reads=[rec, epst], writes=[rec])
                        S.op("act", lambda e: e.activation(out=rec[:, 4:5], in_=rec[:, 4:5], func=AF.Exp, scale=-0.5), reads=[rec], writes=[rec])
                        S.op("dve", lambda e: e.tensor_scalar_mul(out=rec[:, 4:5], in0=rec[:, 4:5], scalar1=1.0 - lam0), reads=[rec], writes=[rec])
                        S.op("dve", lambda e: e.scalar_tensor_tensor(out=yt[:, h * 128:(h + 1) * 128], in0=t1[:, 128:256], scalar=rec[:, 4:5], in1=sub_t[:],
                                                                     op0=ALU.mult, op1=ALU.mult), reads=[t1, rec, sub_t], writes=[yt])
            for tl in range(4):
                t = qb * 4 + tl
                ut = U_OWN // 128 + t
                yt = y_t[tl]
                if layer == 0:
                    edge = t < 2 or t >= NTo - 2
                    if edge:
                        et = t if t < 2 else 2 + (t - (NTo - 2))
                        J = 7
                        runs = [(ut - 3, 7), (UC_T, 2)]
                    else:
                        J = 5
                        runs = [(ut - 2, 5), (UC_T, 2)]
                    for c in range(4):
                        kl, vl = load_local(fmi["kb%d" % c], "vb", c * 130, 130, runs)
                        qt = load_q("qb%d" % c, ut * 128, 128)
                        if not edge:
                            run_wide([dict(qt=qt, pbase=64 * s_, kl=kl, vl=vl, vc0=s_ * 65, nch=7, nb=5,
                                           bias=(bi_t, bi_t[:, 2 * c + s_, 0:640]), yt=yt, ycol=512 + (2 * c + s_) * 64) for s_ in range(2)])
                            continue
                        for s in range(2):
                            head = 2 * c + s
                            if edge:
                                be = be_r.next()
                                S.dma(LQ, be[:], bias_e[et, head], writes=[be])
                                bfn = (lambda j, be=be: (be, be[:, j * 128:(j + 1) * 128]) if j < 7 else None)
                            else:
                                bfn = (lambda j, head=head: (bi_t, bi_t[:, head, j * 128:(j + 1) * 128]) if j < 5 else None)
                            acc = bACC.next()
                            attend(qt, 64 * s, 128, [(kl, j * 128, vl, vl[:, j, s * 65:(s + 1) * 65]) for j in range(J + 2)],
                                   [(acc, 0)], 65, bias_fn=bfn)
                            finish_head([(acc, 0)], 65, [yt], 512 + head * 64)
                else:
                    kl, vl = load_local(fmi["kd"], "vd", 0, 130, [(ut - 1, 3), (UC_T, 2)])
                    var = 1 if t == 0 else (2 if t == NTo - 1 else 0)
                    jobs = []
                    for c in range(4):
                        qt = load_q("qd%d" % c, ut * 128, 128)
                        for s_ in range(2):
                            head = c + 4 * s_
                            jobs.append(dict(qt=qt, pbase=64 * s_, kl=kl, vl=vl, vc0=s_ * 65, nch=5, nb=3, bias=(dmb, dmb[:, var, :]),
                                             yt=yt, ycol=512 + head * 64, extra_den=esnk[:, head:head + 1]))
                    run_wide(jobs)
                out_tiles.append(out_tile(yt, ut * 128, x_u, ut * 128, 0, out_x, t * 128))
        if last:
            S.finish(out_tiles)
        else:
            S.barrier()
        st2.close()
    if not last:
        S.release_dsems()
        S.new_epoch()


def build_fused(SEQ, B):
    HALF = SEQ // 2
    EXT = HALF + 2 * HALO
    nc = bass.Bass("TRN2", target_bir_lowering=False)

    def din(name, shape):
        return nc.dram_tensor(name, list(shape), F32, kind="ExternalInput").ap()

    SH = dict(x_u=din("x_u", [EXT + HALF, D_MODEL]), xc=din("xc", [CTX, D_MODEL]), cvec=din("cvec", [128, 8, 2]),
              ident=din("ident", [128, 128]), blk=din("blk", [128, 128]), perm=din("perm", [128, 128]),
              ropeC=din("ropeC", [128, EXT + HALF]), ropeS=din("ropeS", [128, EXT + HALF]), sel=din("sel", [128, 4]))
    SH["x1_loc"] = nc.dram_tensor("x1_loc", [HALF, D_MODEL], F32).ap()
    SH["xc1_loc"] = nc.dram_tensor("xc1_loc", [CTX, D_MODEL], F32).ap()
    SH["GA"] = nc.dram_tensor("x1_all", [2 * HALF, D_MODEL], F32).ap()
    with contextlib.ExitStack() as st0:
        S = Sched(nc, st0)
        SH["pTb_t"] = S.ps("bankT", [128, 8, 128], BF16)
        pairA = st0.enter_context(nc.psum_tensor("ps_pairA", [128, 1024], F32))
        pairB = st0.enter_context(nc.psum_tensor("ps_pairB", [128, 1024], F32))
        b1 = Tile("bank1", pairA[:, 0:512], excl=True)
        b2 = Tile("bank2", pairA[:, 512:1024], excl=True)
        b3 = Tile("bank3", pairB[:, 0:512], excl=True)
        b4 = Tile("bank4", pairB[:, 512:1024], excl=True)
        SH["pairs"] = [(pairA, b1, b2), (pairB, b3, b4)]
        SH["banks"] = [SH["pTb_t"], b1, b2, b3, b4] + [S.ps("bank%d" % i, [128, 512], F32) for i in range(5, 8)]
        SH["G_reg"] = Tile("G_reg")
        emit_layer(nc, S, SH, 0, SEQ)
        S.sems["cc"] = st0.enter_context(nc.semaphore("s_cc"))
        RC = min(512, HALF)
        for i in range(HALF // RC):
            nc.gpsimd.collective_compute("AllGather", ALU.bypass, replica_groups=[[2 * b, 2 * b + 1] for b in range(B)],
                                         ins=[SH["x1_loc"][i * RC:(i + 1) * RC, :]],
                                         outs=[SH["GA"][i * 2 * RC:(i + 1) * 2 * RC, :]]).then_inc(S.sems["cc"], 1)
        SH["G_reg"].last_w = ("cc", HALF // RC)
        emit_layer(nc, S, SH, 1, SEQ)
    return nc


def rope_tables(pos):
    row = (pos // GRID_W).astype(np.float32)
    col = (pos % GRID_W).astype(np.float32)
    q = HD // 4
    inv = (10000.0 ** (-np.arange(q, dtype=np.float32) / q)).astype(np.float32)
    ar = row[None, :] * inv[:, None]
    ac = col[None, :] * inv[:, None]
    cr, sr, cc, sc = np.cos(ar), np.sin(ar), np.cos(ac), np.sin(ac)
    C = np.concatenate([cr, cr, cc, cc], axis=0)
    Sg = np.concatenate([-sr, sr, -sc, sc], axis=0)
    return (np.concatenate([C, C], 0).astype(np.float32), np.concatenate([Sg, Sg], 0).astype(np.float32))


def nbr_bias(rpb, g, gk, NT):
    rows = NT * 2
    out = np.full((8, 128, 128), NEG, np.float32)
    if gk < 0 or gk >= NT:
        return out
    ql = np.arange(128)
    r = 2 * g + ql // 64
    c = ql % 64
    kr = 2 * gk + ql // 64
    kc = ql % 64
    win_r = min(8, rows)
    rs = np.clip(r - win_r // 2, 0, rows - win_r)
    cs = np.clip(c - 8, 0, GRID_W - 16)
    valid = ((kr[:, None] >= rs[None, :]) & (kr[:, None] < rs[None, :] + win_r)
             & (kc[:, None] >= cs[None, :]) & (kc[:, None] < cs[None, :] + 16))
    di = kr[:, None] - r[None, :] + 7
    dj = kc[:, None] - c[None, :] + 15
    di = np.clip(di, 0, 14)
    dj = np.clip(dj, 0, 30)
    vals = rpb[:, di, dj]
    return np.where(valid[None], vals, np.float32(NEG)).astype(np.float32)


def chunk_rows(w):
    return np.ascontiguousarray(w.reshape(8, 128, w.shape[1]))


def prep_layer_inputs(layer, SEQ, xs, xcs, p):
    B = xs.shape[0]
    HALF = SEQ // 2
    EXT = HALF + 2 * HALO
    NT = SEQ // 128
    NTo = HALF // 128
    cfg = layer_cfg(layer)
    wi = p["w_in_even"][0] if layer == 0 else p["w_in_odd"][0]
    wo = p["w_out_even"][0] if layer == 0 else p["w_out_odd"][0]
    cols = []
    for f in cfg["fm"]:
        cols += f[1]
    for name, c0, n in cfg["tm"]:
        cols += list(range(c0, c0 + n))
    w_in_l = chunk_rows(np.ascontiguousarray(wi[:, cols]))
    w_out_l = chunk_rows(wo)
    w_mod_l = chunk_rows(p["w_mod"][layer])
    b_mod_l = np.ascontiguousarray(p["b_mod"][layer].reshape(24, 128).T)
    bgate = np.ascontiguousarray(np.broadcast_to(p["b_mod"][layer][2048:3072][None, :], (128, D_MODEL)))
    ident = np.eye(128, dtype=np.float32)
    blk = np.zeros((128, 128), np.float32)
    blk[:64, :64] = 1.0 / 64
    blk[64:, 64:] = 1.0 / 64
    perm = np.zeros((128, 128), np.float32)
    for m in range(128):
        k = m + 16 if (m % 32) < 16 else m - 16
        perm[k, m] = 1.0
    maps = []
    for b in range(B):
        for half in range(2):
            T0 = half * HALF
            pos_e = np.arange(T0 - HALO, T0 + HALF + HALO)
            valid_e = (pos_e >= 0) & (pos_e < SEQ)
            x_e = np.zeros((EXT, D_MODEL), np.float32)
            x_e[valid_e] = xs[b, pos_e[valid_e]]
            T1 = (1 - half) * HALF
            pos_o = np.arange(T1, T1 + HALF)
            x_u = np.concatenate([x_e, xs[b, pos_o]], axis=0)
            pos_u = np.concatenate([np.clip(pos_e, 0, SEQ - 1), pos_o])
            C, Sg = rope_tables(pos_u)
            cvec = np.stack([p["c"][b].reshape(8, 128).T, p["c_ctx"].reshape(8, 128).T], axis=-1)
            m = dict(x_u=x_u, xc=np.ascontiguousarray(xcs[b]), cvec=np.ascontiguousarray(cvec), w_mod=w_mod_l, b_mod=b_mod_l,
                     bgate=bgate, w_in=w_in_l, w_out=w_out_l, ident=ident, blk=blk, perm=perm, ropeC=C, ropeS=Sg)
            G0 = T0 // 128
            if layer == 0:
                m["gains"] = np.ascontiguousarray(np.stack([np.tile(p["a_q_norm"][0], 2), np.tile(p["a_k_norm"][0], 2)], axis=-1))
                rpb = p["b_rpb"][0]
                gi = min(max(G0 + 2, 2), NT - 3) if NT >= 6 else 0
                bi = np.stack([nbr_bias(rpb, gi, gi + j, NT) for j in range(-2, 3)], axis=0)
                m["bias_i"] = np.ascontiguousarray(bi.transpose(2, 1, 0, 3).reshape(128, 8, 640))
                ets = [0, 1, NTo - 2, NTo - 1]
                be = np.stack([np.stack([nbr_bias(rpb, G0 + t, G0 + t + j, NT) for j in range(-3, 4)], axis=0) for t in ets], axis=0)
                m["bias_e"] = np.ascontiguousarray(be.transpose(0, 2, 3, 1, 4).reshape(4, 8, 128, 896))
            else:
                a = np.arange(128)
                tri_prev = np.where(a[:, None] >= a[None, :], 0.0, NEG).astype(np.float32)
                tri_next = np.where(a[:, None] <= a[None, :], 0.0, NEG).astype(np.float32)
                full = np.full((128, 128), NEG, np.float32)
                first_prev = full if G0 == 0 else tri_prev
                last_next = full if G0 + NTo == NT else tri_next
                m["dmask"] = np.ascontiguousarray(np.stack([first_prev, last_next, tri_prev, tri_next], axis=1))
                m["lamv"] = np.ascontiguousarray(np.broadcast_to(p["c_lambda"][0].reshape(1, 256), (128, 256)))
                m["subln"] = np.ascontiguousarray(np.broadcast_to((p["c_subln"][0])[None, :], (128, 128)))
                m["sinks"] = np.ascontiguousarray(np.broadcast_to(p["d_sinks"][0][None, :], (128, 8)))
                m["fnorm"] = np.ascontiguousarray(np.broadcast_to(p["final_norm"][None, :], (128, D_MODEL)))
            maps.append(m)
    return maps


def prep_fused_inputs(SEQ, xs, xcs, p):
    m0 = prep_layer_inputs(0, SEQ, xs, xcs, p)
    m1 = prep_layer_inputs(1, SEQ, xs, xcs, p)
    shared = ("x_u", "xc", "cvec", "ident", "blk", "perm", "ropeC", "ropeS")
    maps = []
    for i, (a, b) in enumerate(zip(m0, m1)):
        half = i % 2
        m = {k: a[k] for k in shared}
        for k, v in a.items():
            if k not in shared:
                m["l0_" + k] = v
        for k, v in b.items():
            if k not in shared:
                m["l1_" + k] = v
        sel = np.zeros((128, 4), np.float32)
        sel[:, 0] = 1.0 if half == 1 else 0.0
        sel[:, 1] = 1.0 if half == 0 else 0.0
        sel[:, 2] = 1.0 if half == 1 else 0.0
        sel[:, 3] = 1.0 if half == 0 else 0.0
        m["sel"] = sel
        maps.append(m)
    return maps


def run_fused(SEQ, xs, xcs, p, runner=None):
    B = xs.shape[0]
    key = ("fused", SEQ, B)
    if key not in _NC_CACHE:
        _NC_CACHE[key] = build_fused(SEQ, B)
    nc = _NC_CACHE[key]
    maps = prep_fused_inputs(SEQ, xs, xcs, p)
    if runner is None:
        res = run_bass_kernel_spmd(nc, maps, core_ids=list(range(len(maps)))).results
    else:
        res = runner(nc, maps)
    HALF = SEQ // 2
    xo = np.zeros_like(xs)
    for b in range(B):
        for half in range(2):
            xo[b, half * HALF:(half + 1) * HALF] = res[2 * b + half]["out_x"]
    return xo


_NC_CACHE = {}


def kernel(x, c, ctx, c_ctx, w_mod, b_mod, w_in_even, w_out_even, a_q_norm, a_k_norm, b_rpb,
           w_in_odd, w_out_odd, c_lambda, c_subln, d_sinks, final_norm):
    p = dict(c=np.asarray(c, np.float32), c_ctx=np.asarray(c_ctx, np.float32), w_mod=np.asarray(w_mod, np.float32),
             b_mod=np.asarray(b_mod, np.float32), w_in_even=np.asarray(w_in_even, np.float32),
             w_out_even=np.asarray(w_out_even, np.float32), a_q_norm=np.asarray(a_q_norm, np.float32),
             a_k_norm=np.asarray(a_k_norm, np.float32), b_rpb=np.asarray(b_rpb, np.float32),
             w_in_odd=np.asarray(w_in_odd, np.float32), w_out_odd=np.asarray(w_out_odd, np.float32),
             c_lambda=np.asarray(c_lambda, np.float32), c_subln=np.asarray(c_subln, np.float32),
             d_sinks=np.asarray(d_sinks, np.float32), final_norm=np.asarray(final_norm, np.float32))
    xs = np.asarray(x, np.float32)
    xcs = np.asarray(ctx, np.float32)
    SEQ = xs.shape[1]
    return run_fused(SEQ, xs, xcs, p)
```

```python
import contextlib
import math
import numpy as np
import concourse.bass as bass
import concourse.mybir as mybir
from concourse.bass_utils import run_bass_kernel_spmd

F32 = mybir.dt.float32
BF16 = mybir.dt.bfloat16
AF = mybir.ActivationFunctionType
ALU = mybir.AluOpType

D_MODEL = 1024
CTX = 256
HD = 64
GRID_W = 64
SCALE = HD ** -0.5
EPS = 1e-6
NEG = -30000.0
HALO = 512
LQ = "sp"
SQ = "pool"


class Tile:
    __slots__ = ("name", "t", "last_w", "readers", "dsem", "dcount", "excl")

    def __init__(self, name, t=None, excl=False):
        self.name = name
        self.t = t
        self.last_w = None
        self.readers = {}
        self.dsem = None
        self.dcount = 0
        self.excl = excl

    def __getitem__(self, idx):
        return self.t[idx]


class Sched:
    def __init__(self, nc, stack):
        self.nc = nc
        self.stack = stack
        self.sem_stack = stack
        self.dtiles = []
        self.engs = {}
        self.sems = {}
        for en, e in (("pe", nc.tensor), ("act", nc.scalar), ("dve", nc.vector),
                      ("pool", nc.gpsimd), ("sp", nc.sync)):
            self.sems[en] = stack.enter_context(nc.semaphore("s_" + en))
            self.engs[en] = dict(eng=e, count=0, seen={}, key=en)
        self.epoch = 0
        self.nsem = 0
        self.prefix = ""
        self.free_dsems = []

    def sb(self, name, shape, dt):
        return Tile(name, self.stack.enter_context(self.nc.sbuf_tensor("sb_" + self.prefix + name, list(shape), dt)))

    def ps(self, name, shape, dt=F32):
        return Tile(name, self.stack.enter_context(self.nc.psum_tensor("ps_" + name, list(shape), dt)), excl=True)

    def _dsem(self, tile):
        if tile.dsem is None:
            if self.free_dsems:
                key, cnt = self.free_dsems.pop()
                tile.dsem = key
                tile.dcount = cnt
            else:
                key = "d%d" % self.nsem
                self.nsem += 1
                tile.dsem = key
                self.sems[key] = self.sem_stack.enter_context(self.nc.semaphore(key))
            self.dtiles.append(tile)
        return tile.dsem

    def new_epoch(self):
        self.epoch += 1
        for en, E in self.engs.items():
            key = "%s#%d" % (en, self.epoch)
            self.sems[key] = self.sem_stack.enter_context(self.nc.semaphore("s_%s_%d" % (en, self.epoch)))
            E["key"] = key
            E["count"] = 0

    def release_dsems(self):
        for t in self.dtiles:
            self.free_dsems.append((t.dsem, t.dcount))
            t.dsem = None
        self.dtiles = []

    def _wait_deps(self, en, reads, writes):
        E = self.engs[en]
        deps = {}

        def add(ev):
            if ev is None:
                return
            k, v = ev
            if deps.get(k, 0) < v:
                deps[k] = v

        me = E["key"]
        for t in reads:
            add(t.last_w)
            if t.excl:
                for k, v in t.readers.items():
                    if k != me:
                        add((k, v))
        for t in writes:
            add(t.last_w)
            for k, v in t.readers.items():
                if k != me:
                    add((k, v))
        for k, v in deps.items():
            if E["seen"].get(k, 0) < v:
                E["seen"][k] = v
                if k == me and en == "pe":
                    continue
                E["eng"].wait_ge(self.sems[k], v)

    def op(self, en, fn, reads=(), writes=()):
        E = self.engs[en]
        self._wait_deps(en, reads, writes)
        ins = fn(E["eng"])
        E["count"] += 1
        me = E["key"]
        ins.then_inc(self.sems[me], 1)
        for t in reads:
            t.readers[me] = E["count"]
        for t in writes:
            t.last_w = (me, E["count"])
            t.readers = {}
        return ins

    def dma(self, q, out, in_, reads=(), writes=(), sem_tile=None):
        E = self.engs[q]
        self._wait_deps(q, reads, writes)
        st = sem_tile if sem_tile is not None else (list(writes) + list(reads))[0]
        key = self._dsem(st)
        ins = E["eng"].dma_start(out=out, in_=in_)
        st.dcount += 16
        ins.then_inc(self.sems[key], 16)
        for t in reads:
            t.readers[key] = st.dcount
        for t in writes:
            t.last_w = (key, st.dcount)
            t.readers = {}
        return ins

    def barrier(self):
        for en, E in self.engs.items():
            for en2, E2 in self.engs.items():
                k2 = E2["key"]
                if en2 != en and E2["count"] and E["seen"].get(k2, 0) < E2["count"]:
                    E["seen"][k2] = E2["count"]
                    E["eng"].wait_ge(self.sems[k2], E2["count"])
            for t in self.dtiles:
                if t.dcount and E["seen"].get(t.dsem, 0) < t.dcount:
                    E["seen"][t.dsem] = t.dcount
                    E["eng"].wait_ge(self.sems[t.dsem], t.dcount)

    def finish(self, tiles, en="sp"):
        self._wait_deps(en, tiles, tiles)


class Rot:
    def __init__(self, tiles):
        self.tiles = tiles
        self.i = 0

    def next(self):
        t = self.tiles[self.i % len(self.tiles)]
        self.i += 1
        return t


def layer_cfg(layer):
    if layer == 0:
        qa, ka, va, qb, kb, vb, z = 0, 512, 640, 768, 1280, 1792, 2304
        fm = []
        for c in range(4):
            cols = list(range(qa + c * 64, qa + c * 64 + 64)) + list(range(qa + (4 + c) * 64, qa + (4 + c) * 64 + 64))
            fm.append(("qa%d" % c, cols, "q", True))
        fm.append(("ka", list(range(ka, ka + 128)), "k", True))
        for c in range(4):
            fm.append(("qb%d" % c, list(range(qb + c * 128, qb + c * 128 + 128)), None, False))
        for c in range(4):
            fm.append(("kb%d" % c, list(range(kb + c * 128, kb + c * 128 + 128)), None, False))
        tm = [("va", va, 128), ("vb", vb, 512), ("z", z, 1024)]
        dense_k, dense_v = ["ka"], ["va"]
        local_k, local_v = ["kb0", "kb1", "kb2", "kb3"], ["vb"]
    else:
        qc, kc, vc, qd, kd, vd, z = 0, 512, 1024, 1536, 2048, 2176, 2304
        fm = []
        for c in range(4):
            fm.append(("qc%d" % c, list(range(qc + c * 128, qc + c * 128 + 128)), None, True))
        for c in range(4):
            fm.append(("kc%d" % c, list(range(kc + c * 128, kc + c * 128 + 128)), None, True))
        for c in range(4):
            cols = list(range(qd + c * 64, qd + c * 64 + 64)) + list(range(qd + (4 + c) * 64, qd + (4 + c) * 64 + 64))
            fm.append(("qd%d" % c, cols, None, True))
        fm.append(("kd", list(range(kd, kd + 128)), None, True))
        tm = [("vc", vc, 512), ("vd", vd, 128), ("z", z, 1024)]
        dense_k, dense_v = ["kc0", "kc1", "kc2", "kc3"], ["vc"]
        local_k, local_v = ["kd"], ["vd"]
    return dict(fm=fm, tm=tm, dense_k=dense_k, dense_v=dense_v, local_k=local_k, local_v=local_v)


def lambda_init(layer):
    return 0.8 - 0.6 * math.exp(-0.3 * layer)


def emit_layer(nc, S, SH, layer, SEQ):
    HALF = SEQ // 2
    EXT = HALF + 2 * HALO
    NU = EXT + HALF + CTX
    NTo = HALF // 128
    NBo = HALF // 512
    U_OWN = HALO
    U_O = EXT
    U_C = EXT + HALF
    NKD = 2 * HALF + CTX
    NKC = NKD // 128
    last = layer == 1
    cfg = layer_cfg(layer)
    fm, tm = cfg["fm"], cfg["tm"]
    NFM = len(fm)
    fmi = {f[0]: i for i, f in enumerate(fm)}
    NCOL = NFM * 128 + sum(t[2] for t in tm)
    tmoff = {}
    o = NFM * 128
    for name, _, n in tm:
        tmoff[name] = o
        o += n
    lam0 = lambda_init(layer)

    LP = "l%d_" % layer
    S.prefix = LP

    def din(name, shape):
        return nc.dram_tensor(LP + name, list(shape), F32, kind="ExternalInput").ap()

    x_u, xc_in, cvec = SH["x_u"], SH["xc"], SH["cvec"]
    ident_d, blk_d, perm_d, ropeC, ropeS = SH["ident"], SH["blk"], SH["perm"], SH["ropeC"], SH["ropeS"]
    x1_loc, xc1_loc, GA = SH["x1_loc"], SH["xc1_loc"], SH["GA"]
    w_mod = din("w_mod", [8, 128, 3072])
    b_mod = din("b_mod", [128, 24])
    bgate = din("bgate", [128, D_MODEL])
    w_in = din("w_in", [8, 128, NCOL])
    w_out = din("w_out", [8, 128, D_MODEL])
    if layer == 0:
        gains = din("gains", [128, 2])
        bias_i = din("bias_i", [128, 8, 5 * 128])
        bias_e = din("bias_e", [4, 8, 128, 7 * 128])
    else:
        dmask = din("dmask", [128, 4, 128])
        lamv = din("lamv", [128, 256])
        subln = din("subln", [128, 128])
        sinks = din("sinks", [128, 8])
        fnorm = din("fnorm", [128, D_MODEL])
    if last:
        out_x = nc.dram_tensor("out_x", [HALF, D_MODEL], F32, kind="ExternalOutput").ap()
    else:
        out_x = x1_loc
        out_c = xc1_loc

    fmS = nc.dram_tensor(LP + "fmS", [NFM, 128, NU], BF16).ap()
    vdims = {"va": (2, 65), "vb": (8, 65), "vc": (4, 129), "vd": (2, 65)}
    vS = {}
    for name, _, n in tm:
        if name != "z":
            h, d = vdims[name]
            vS[name] = nc.dram_tensor(LP + "vS_" + name, [NU, h * d], BF16).ap()
    zS = nc.dram_tensor(LP + "zS", [NU, D_MODEL], BF16).ap()

    with contextlib.ExitStack() as st:
        S.stack = st
        st1 = contextlib.ExitStack()
        fm_reg = Tile("fm_reg")
        v_reg = Tile("v_reg")
        z_reg = Tile("z_reg")

        ident_f = S.sb("ident_f", [128, 128], F32)
        ident = S.sb("ident", [128, 128], BF16)
        blk = S.sb("blk", [128, 128], F32)
        perm = S.sb("perm", [128, 128], F32)
        epst = S.sb("epst", [128, 1], F32)
        zeros = S.sb("zeros", [128, 128], F32)
        junk = S.sb("junk", [128, D_MODEL], F32)
        ss_r = Rot([S.sb("ss%d" % i, [128, 2], F32) for i in range(2)])
        t1_r = Rot([S.sb("t1%d" % i, [128, 512], F32) for i in range(2)])
        gate_bc = S.sb("gate_bc", [128, 2, D_MODEL], F32)
        modt = S.sb("modt", [128, 24, 2], F32)
        bmodt = S.sb("bmodt", [128, 24], F32)
        cv = S.sb("cv", [128, 8, 2], F32)
        pTb_t = SH["pTb_t"]
        pTb = pTb_t.t
        banks = SH["banks"]
        sel_t = S.sb("sel_t", [128, 4], F32)
        S.dma(LQ, sel_t[:], SH["sel"], writes=[sel_t])
        xt2 = S.sb("xt2", [128, D_MODEL], F32)
        G_reg = SH["G_reg"]
        NTo_ = HALF // 128

        RCt = min(512, HALF) // 128

        def grow(r, t):
            return ((t // RCt) * 2 * RCt + r * RCt + (t % RCt)) * 128

        def load_x_tile(xt, utile):
            e = utile
            if layer == 0:
                if e >= (EXT + HALF) // 128:
                    c = e - (EXT + HALF) // 128
                    S.dma(LQ, xt[:], xc_in[c * 128:(c + 1) * 128, :], writes=[xt])
                else:
                    S.dma(LQ, xt[:], x_u[e * 128:(e + 1) * 128, :], writes=[xt])
                return
            if e >= (EXT + HALF) // 128:
                c = e - (EXT + HALF) // 128
                S.dma(LQ, xt[:], xc1_loc[c * 128:(c + 1) * 128, :], writes=[xt])
            elif e >= EXT // 128:
                o = e - EXT // 128
                S.dma(LQ, xt[:], GA[grow(0, o):grow(0, o) + 128, :], reads=[G_reg], writes=[xt])
                S.dma(LQ, xt2[:], GA[grow(1, o):grow(1, o) + 128, :], reads=[G_reg], writes=[xt2])
                S.op("act", lambda en: en.activation(out=xt[:], in_=xt[:], func=AF.Copy, scale=sel_t[:, 2:3]), reads=[xt, sel_t], writes=[xt])
                S.op("dve", lambda en: en.scalar_tensor_tensor(out=xt[:], in0=xt2[:], scalar=sel_t[:, 3:4], in1=xt[:], op0=ALU.mult, op1=ALU.add),
                     reads=[xt2, sel_t, xt], writes=[xt])
            elif 4 <= e < 4 + NTo_:
                t = e - 4
                S.dma(LQ, xt[:], x1_loc[t * 128:(t + 1) * 128, :], writes=[xt])
            elif e < 4:
                gt = grow(0, NTo_ - 4 + e)
                S.dma(LQ, xt[:], GA[gt:gt + 128, :], reads=[G_reg], writes=[xt])
                S.op("act", lambda en: en.activation(out=xt[:], in_=xt[:], func=AF.Copy, scale=sel_t[:, 0:1]), reads=[xt, sel_t], writes=[xt])
            else:
                gt = grow(1, e - 4 - NTo_)
                S.dma(LQ, xt[:], GA[gt:gt + 128, :], reads=[G_reg], writes=[xt])
                S.op("act", lambda en: en.activation(out=xt[:], in_=xt[:], func=AF.Copy, scale=sel_t[:, 1:2]), reads=[xt, sel_t], writes=[xt])

        S.dma(LQ, ident_f[:], ident_d, writes=[ident_f])
        S.dma(LQ, blk[:], blk_d, writes=[blk])
        S.dma(LQ, perm[:], perm_d, writes=[perm])
        S.dma(LQ, cv[:], cvec, writes=[cv])
        S.dma(LQ, bmodt[:], b_mod, writes=[bmodt])
        S.dma(LQ, gate_bc[:, 0, :], bgate, writes=[gate_bc])
        S.op("dve", lambda e: e.tensor_copy(out=ident[:], in_=ident_f[:]), reads=[ident_f], writes=[ident])
        S.op("pool", lambda e: e.memset(epst[:], EPS), writes=[epst])
        S.op("pool", lambda e: e.memset(zeros[:], 0.0), writes=[zeros])
        S.op("dve", lambda e: e.tensor_copy(out=gate_bc[:, 1, :], in_=gate_bc[:, 0, :]), reads=[gate_bc], writes=[gate_bc])
        S.stack = st
        if layer == 0:
            gn = S.sb("gn", [128, 2], F32)
            bi_t = S.sb("bi_t", [128, 8, 640], F32)
            S.dma(LQ, gn[:], gains, writes=[gn])
            S.dma(LQ, bi_t[:], bias_i, writes=[bi_t])
        else:
            dm_t = S.sb("dm_t", [128, 4, 128], F32)
            lam_t = S.sb("lam_t", [128, 256], F32)
            sub_t = S.sb("sub_t", [128, 128], F32)
            snk_t = S.sb("snk_t", [128, 8], F32)
            fn_t = S.sb("fn_t", [128, D_MODEL], F32)
            S.dma(LQ, dm_t[:], dmask, writes=[dm_t])
            S.dma(LQ, lam_t[:], lamv, writes=[lam_t])
            S.dma(LQ, sub_t[:], subln, writes=[sub_t])
            S.dma(LQ, snk_t[:], sinks, writes=[snk_t])
            S.dma(LQ, fn_t[:], fnorm, writes=[fn_t])
            lam_s = S.sb("lam_s", [128, 4], F32)
            lprod = S.sb("lprod", [128, 128], F32)
            esnk = S.sb("esnk", [128, 8], F32)
        S.stack = st1
        win = S.sb("win", [128, 8, NCOL], BF16)
        screp = S.sb("screp", [128, 2, 8, 128], F32)
        stage = Rot([S.sb("stage%d" % i, [128, 1024], F32) for i in range(2)])

        S.op("act", lambda e: e.activation(out=cv[:], in_=cv[:], func=AF.Silu), reads=[cv], writes=[cv])
        for w in range(2):
            for k in range(8):
                S.op("act", lambda e: e.activation(out=screp[:, w, k, :], in_=zeros[:], func=AF.Identity,
                                                   bias=cv[:, k, w:w + 1]), reads=[zeros, cv], writes=[screp])
        pmod = banks[5]
        pg = [banks[1], banks[2], banks[3], banks[4]]
        for k in range(8):
            for pi in range(3):
                stg = stage.next()
                S.dma(LQ, stg[:], w_mod[k, :, pi * 1024:(pi + 1) * 1024], writes=[stg])
                for jj in range(8):
                    j = pi * 8 + jj
                    S.op("pe", lambda e: e.matmul(pmod[:, j * 2:j * 2 + 2], lhsT=stg[:, jj * 128:(jj + 1) * 128], rhs=cv[:, k, :],
                                                  start=(k == 0 and j == 0), stop=(k == 7), skip_group_check=True),
                         reads=[stg, cv], writes=[pmod])
                if pi == 2:
                    for w in range(2):
                        for n in range(2):
                            S.op("pe", lambda e: e.matmul(pg[w * 2 + n][:], lhsT=screp[:, w, k, :],
                                                          rhs=stg[:, n * 512:(n + 1) * 512],
                                                          start=(k == 0), stop=(k == 7)),
                                 reads=[stg, screp], writes=[pg[w * 2 + n]])
        for w in range(2):
            S.op("dve", lambda e: e.tensor_tensor(out=modt[:, :, w], in0=pmod[:, 0:48].rearrange("p (j w) -> p j w", w=2)[:, :, w],
                                                  in1=bmodt[:], op=ALU.add), reads=[pmod, bmodt], writes=[modt])
            for n in range(2):
                S.op("dve", lambda e: e.tensor_tensor(out=gate_bc[:, w, n * 512:(n + 1) * 512], in0=pg[w * 2 + n][:],
                                                      in1=gate_bc[:, w, n * 512:(n + 1) * 512], op=ALU.add),
                     reads=[pg[w * 2 + n], gate_bc], writes=[gate_bc])
        S.op("dve", lambda e: e.tensor_scalar_add(out=modt[:, 8:16, :], in0=modt[:, 8:16, :], scalar1=1.0),
             reads=[modt], writes=[modt])

        cnt = 0
        for k in range(8):
            for c0 in range(0, NCOL, 1024):
                cn = min(1024, NCOL - c0)
                stg = stage.next()
                S.dma(LQ, stg[:, 0:cn], w_in[k, :, c0:c0 + cn], writes=[stg])
                S.op("dve", lambda e: e.tensor_copy(out=win[:, k, c0:c0 + cn], in_=stg[:, 0:cn]), reads=[stg], writes=[win])
                cnt += 1

        if layer == 1:
            S.op("dve", lambda e: e.tensor_tensor(out=lprod[:, 0:64], in0=lam_t[:, 0:64], in1=lam_t[:, 64:128], op=ALU.mult), reads=[lam_t], writes=[lprod])
            S.op("dve", lambda e: e.tensor_tensor(out=lprod[:, 64:128], in0=lam_t[:, 128:192], in1=lam_t[:, 192:256], op=ALU.mult), reads=[lam_t, lprod], writes=[lprod])
            S.op("dve", lambda e: e.reduce_sum(out=lam_s[:, 0:2], in_=lprod[:].rearrange("p (a d) -> p a d", a=2), axis=mybir.AxisListType.X), reads=[lprod], writes=[lam_s])
            S.op("act", lambda e: e.activation(out=lam_s[:, 0:2], in_=lam_s[:, 0:2], func=AF.Exp), reads=[lam_s], writes=[lam_s])
            S.op("dve", lambda e: e.tensor_tensor(out=lam_s[:, 2:3], in0=lam_s[:, 0:1], in1=lam_s[:, 1:2], op=ALU.subtract), reads=[lam_s], writes=[lam_s])
            S.op("dve", lambda e: e.tensor_scalar(out=lam_s[:, 3:4], in0=lam_s[:, 2:3], scalar1=lam0, scalar2=-1.0, op0=ALU.add, op1=ALU.mult), reads=[lam_s], writes=[lam_s])
            S.op("act", lambda e: e.activation(out=esnk[:], in_=snk_t[:], func=AF.Exp), reads=[snk_t], writes=[esnk])

        xt_r = Rot([S.sb("xt%d" % i, [128, D_MODEL], F32) for i in range(2)])
        xn_r = Rot([S.sb("xn%d" % i, [128, D_MODEL], BF16) for i in range(2)])
        hT_r = Rot([S.sb("hT%d" % i, [128, 8, 512], BF16) for i in range(2)])
        tabC_r = Rot([S.sb("tabC%d" % i, [128, 512], F32) for i in range(1)])
        tabS_r = Rot([S.sb("tabS%d" % i, [128, 512], F32) for i in range(1)])
        sq_r = Rot([S.sb("sq%d" % i, [128, 512], F32) for i in range(2)])
        rs_r = Rot([S.sb("rs%d" % i, [128, 512], F32) for i in range(2)])
        qn_r = Rot([S.sb("qn%d" % i, [128, 512], F32) for i in range(3)])
        t2_r = Rot([S.sb("t2%d" % i, [128, 512], F32) for i in range(2)])
        fo_r = Rot([S.sb("fo%d" % i, [128, 512], BF16) for i in range(4)])
        vst = {}
        for name in vS:
            h, d = vdims[name]
            vst[name] = Rot([S.sb("vst_%s%d" % (name, i), [128, h, d], BF16) for i in range(2)])
            for t in vst[name].tiles:
                S.op("pool", lambda e: e.memset(t[:], 1.0), writes=[t])
        zst_r = Rot([S.sb("zst%d" % i, [128, D_MODEL], BF16) for i in range(2)])
        bA = Rot([banks[1], banks[2], banks[3]])
        bB = Rot([banks[4], banks[5]])

        def p1_prep(bd):
            u0, ntiles, w = bd["u0"], bd["ntiles"], bd["w"]
            hT = hT_r.next()
            bd["hT"] = hT
            for ti in range(ntiles):
                xt = xt_r.next()
                ss = ss_r.next()
                xn = xn_r.next()
                load_x_tile(xt, u0 // 128 + ti)
                S.op("act", lambda e: e.activation(out=junk[:], in_=xt[:], func=AF.Square, accum_out=ss[:, 0:1]), reads=[xt], writes=[junk, ss])
                S.op("act", lambda e: e.activation(out=ss[:, 1:2], in_=ss[:, 0:1], func=AF.Ln, scale=1.0 / D_MODEL, bias=epst[:]), reads=[ss, epst], writes=[ss])
                S.op("act", lambda e: e.activation(out=ss[:, 1:2], in_=ss[:, 1:2], func=AF.Exp, scale=-0.5), reads=[ss], writes=[ss])
                S.op("act", lambda e: e.activation(out=xn[:], in_=xt[:], func=AF.Copy, scale=ss[:, 1:2]), reads=[xt, ss], writes=[xn])
                for k in range(8):
                    S.op("pe", lambda e: e.transpose(out=pTb[:, k, :], in_=xn[:, k * 128:(k + 1) * 128], identity=ident[:]),
                         reads=[xn, ident], writes=[pTb_t])
                for k in range(8):
                    S.op("dve", lambda e: e.tensor_scalar(out=hT[:, k, ti * 128:(ti + 1) * 128], in0=pTb[:, k, :],
                                                          scalar1=modt[:, 8 + k, w:w + 1], scalar2=modt[:, k, w:w + 1],
                                                          op0=ALU.mult, op1=ALU.add), reads=[pTb_t, modt], writes=[hT])
                yield

        def p1_mm(bd):
            u0, ntiles, fm_list, tm_list, rope_col0, hT = bd["u0"], bd["ntiles"], bd["fm_list"], bd["tm_list"], bd["rope_col0"], bd["hT"]
            ntok = ntiles * 128
            rope_needed = any(fm[i][3] for i in fm_list) and rope_col0 is not None
            if rope_needed:
                tC = tabC_r.next()
                tS = tabS_r.next()
                S.dma(LQ, tC[:, 0:ntok], ropeC[:, rope_col0:rope_col0 + ntok], writes=[tC])
                S.dma(LQ, tS[:, 0:ntok], ropeS[:, rope_col0:rope_col0 + ntok], writes=[tS])
            chs = [dict(i=i) for i in fm_list]

            def st_main(ch):
                i = ch["i"]
                pa = bA.next()
                for k in range(8):
                    S.op("pe", lambda e: e.matmul(pa[:, 0:ntok], lhsT=win[:, k, i * 128:(i + 1) * 128], rhs=hT[:, k, 0:ntok],
                                                  start=(k == 0), stop=(k == 7)), reads=[win, hT], writes=[pa])
                ch["pa"] = pa

            def st_norm(ch):
                i = ch["i"]
                name, _, nkind, roped = fm[i]
                pa = ch["pa"]
                fo = fo_r.next()
                ch["fo"] = fo
                do_rope = roped and rope_col0 is not None
                ch["do_rope"] = do_rope
                if nkind is None and not do_rope:
                    S.op("act", lambda e: e.activation(out=fo[:, 0:ntok], in_=pa[:, 0:ntok], func=AF.Copy), reads=[pa], writes=[fo])
                    return
                qn = qn_r.next()
                ch["qn"] = qn
                if nkind is not None:
                    sq = sq_r.next()
                    rs = rs_r.next()
                    pb = bB.next()
                    gcol = 0 if nkind == "q" else 1
                    S.op("act", lambda e: e.activation(out=sq[:, 0:ntok], in_=pa[:, 0:ntok], func=AF.Square), reads=[pa], writes=[sq])
                    S.op("pe", lambda e: e.matmul(pb[:, 0:ntok], lhsT=blk[:], rhs=sq[:, 0:ntok], start=True, stop=True), reads=[blk, sq], writes=[pb])
                    S.op("act", lambda e: e.activation(out=rs[:, 0:ntok], in_=pb[:, 0:ntok], func=AF.Ln, bias=epst[:]), reads=[pb, epst], writes=[rs])
                    S.op("act", lambda e: e.activation(out=rs[:, 0:ntok], in_=rs[:, 0:ntok], func=AF.Exp, scale=-0.5), reads=[rs], writes=[rs])
                    dst = qn if do_rope else fo
                    S.op("dve", lambda e: e.scalar_tensor_tensor(out=dst[:, 0:ntok], in0=pa[:, 0:ntok], scalar=gn[:, gcol:gcol + 1],
                                                                 in1=rs[:, 0:ntok], op0=ALU.mult, op1=ALU.mult),
                         reads=[pa, gn, rs], writes=[dst])
                else:
                    S.op("act", lambda e: e.activation(out=qn[:, 0:ntok], in_=pa[:, 0:ntok], func=AF.Copy), reads=[pa], writes=[qn])

            def st_rope(ch):
                i = ch["i"]
                fo = ch["fo"]
                if ch["do_rope"]:
                    qn = ch["qn"]
                    pb2 = bB.next()
                    t1 = t1_r.next()
                    t2 = t2_r.next()
                    S.op("pe", lambda e: e.matmul(pb2[:, 0:ntok], lhsT=perm[:], rhs=qn[:, 0:ntok], start=True, stop=True), reads=[perm, qn], writes=[pb2])
                    S.op("dve", lambda e: e.tensor_tensor(out=t1[:, 0:ntok], in0=qn[:, 0:ntok], in1=tC[:, 0:ntok], op=ALU.mult), reads=[qn, tC], writes=[t1])
                    S.op("dve", lambda e: e.tensor_tensor(out=t2[:, 0:ntok], in0=pb2[:, 0:ntok], in1=tS[:, 0:ntok], op=ALU.mult), reads=[pb2, tS], writes=[t2])
                    S.op("dve", lambda e: e.tensor_tensor(out=fo[:, 0:ntok], in0=t1[:, 0:ntok], in1=t2[:, 0:ntok], op=ALU.add), reads=[t1, t2], writes=[fo])
                S.dma(SQ, fmS[i, :, u0:u0 + ntok], fo[:, 0:ntok], reads=[fo], writes=[fm_reg], sem_tile=fo)

            nchs = len(chs)
            for step in range(nchs + 2):
                if step < nchs:
                    st_main(chs[step])
                if 0 <= step - 1 < nchs:
                    st_norm(chs[step - 1])
                if 0 <= step - 2 < nchs:
                    st_rope(chs[step - 2])
                yield
            for name in tm_list:
                col0 = tmoff[name]
                ncols = dict((t[0], t[2]) for t in tm)[name]
                for ti in range(ntiles):
                    for n0 in range(0, ncols, 512):
                        nn = min(512, ncols - n0)
                        pa = bA.next()
                        for k in range(8):
                            S.op("pe", lambda e: e.matmul(pa[:, 0:nn], lhsT=hT[:, k, ti * 128:(ti + 1) * 128],
                                                          rhs=win[:, k, col0 + n0:col0 + n0 + nn], start=(k == 0), stop=(k == 7)),
                                 reads=[win, hT], writes=[pa])
                        if name == "z":
                            if n0 == 0:
                                zst = zst_r.next()
                            S.op("act", lambda e: e.activation(out=zst[:, n0:n0 + nn], in_=pa[:, 0:nn], func=AF.Silu), reads=[pa], writes=[zst])
                            if n0 + nn == ncols:
                                S.dma(SQ, zS[u0 + ti * 128:u0 + (ti + 1) * 128, :], zst[:], reads=[zst], writes=[z_reg], sem_tile=zst)
                        else:
                            h, d = vdims[name]
                            dv = d - 1
                            vt = vst[name].next()
                            S.op("dve", lambda e: e.tensor_copy(out=vt[:, :, 0:dv], in_=pa[:, 0:nn].rearrange("p (h d) -> p h d", d=dv)),
                                 reads=[pa], writes=[vt])
                            S.dma(SQ, vS[name][u0 + ti * 128:u0 + (ti + 1) * 128, :], vt[:].rearrange("p h d -> p (h d)"),
                                  reads=[vt], writes=[v_reg], sem_tile=vt)
                        yield


        all_fm = list(range(NFM))
        all_tm = [t[0] for t in tm]
        lk = [fmi[n] for n in cfg["local_k"]]
        dk = [fmi[n] for n in cfg["dense_k"]]
        blocks = [dict(u0=U_C, ntiles=2, w=1, fm_list=all_fm, tm_list=all_tm, rope_col0=None)]
        eblocks = list(range(EXT // 512))
        eblocks = [b for b in eblocks if HALO <= b * 512 < HALO + HALF] + [b for b in eblocks if not (HALO <= b * 512 < HALO + HALF)]
        for b in eblocks:
            u0 = b * 512
            own = HALO <= u0 < HALO + HALF
            if own:
                blocks.append(dict(u0=u0, ntiles=4, w=0, fm_list=all_fm, tm_list=all_tm, rope_col0=u0))
            else:
                blocks.append(dict(u0=u0, ntiles=4, w=0, fm_list=lk, tm_list=cfg["local_v"], rope_col0=u0))
        for b in range(HALF // 512):
            u0 = U_O + b * 512
            blocks.append(dict(u0=u0, ntiles=4, w=0, fm_list=dk, tm_list=cfg["dense_v"], rope_col0=u0))
        for _ in p1_prep(blocks[0]):
            pass
        for bi, bd in enumerate(blocks):
            gm = p1_mm(bd)
            gp = p1_prep(blocks[bi + 1]) if bi + 1 < len(blocks) else None
            nsteps = len(bd["fm_list"]) + 2 + sum(((dict((t[0], t[2]) for t in tm)[nm] + 511) // 512) * bd["ntiles"] for nm in bd["tm_list"])
            ntl = blocks[bi + 1]["ntiles"] if gp is not None else 0
            every = max(1, nsteps // (ntl + 1)) if ntl else 0
            k = 0
            for _ in gm:
                k += 1
                if gp is not None and every and k % every == 0:
                    next(gp, None)
            if gp is not None:
                for _ in gp:
                    pass

        S.barrier()
        st1.close()
        st2 = contextlib.ExitStack()
        S.stack = st2
        wout = S.sb("wout", [128, 8, D_MODEL], BF16)
        stage2 = Rot([S.sb("stage2_%d" % i, [128, D_MODEL], F32) for i in range(2)])
        for k in range(8):
            stg = stage2.next()
            S.dma(LQ, stg[:], w_out[k], writes=[stg])
            S.op("dve", lambda e: e.tensor_copy(out=wout[:, k, :], in_=stg[:]), reads=[stg], writes=[wout])
        bS = Rot([banks[1], banks[2], banks[3]])
        bACC = Rot([banks[5], banks[6], banks[7]])
        pT_r = Rot([S.sb("pT%d" % i, [128, 512], BF16) for i in range(3)])
        sb_r = Rot([S.sb("sbias%d" % i, [128, 512], F32) for i in range(2)])
        nbuf_d = 1 if layer == 0 else 2
        KT_r = Rot([S.sb("KT%d" % i, [128, NKD], BF16) for i in range(nbuf_d)])
        VDW = 130 if layer == 0 else 129
        VD_r = Rot([S.sb("VD%d" % i, [128, NKC, VDW], BF16) for i in range(nbuf_d)])
        q_r = Rot([S.sb("qblk%d" % i, [128, 512], BF16) for i in range(6)])
        NLK = 9
        kl_r = Rot([S.sb("kl%d" % i, [128, NLK * 128], BF16) for i in range(2)])
        VLW = 130
        vl_r = Rot([S.sb("vl%d" % i, [128, NLK, VLW], BF16) for i in range(2)])
        y_t = [S.sb("y%d" % i, [128, D_MODEL], F32) for i in range(4)]
        rec_r = Rot([S.sb("rec%d" % i, [128, 8], F32) for i in range(4)])
        zl_r = Rot([S.sb("zl%d" % i, [128, D_MODEL], BF16) for i in range(1)])
        yb_r = Rot([S.sb("yb%d" % i, [128, D_MODEL], BF16) for i in range(1)])
        yT_r = Rot([S.sb("yT%d" % i, [128, 8, 128], BF16) for i in range(1)])
        xo_r = Rot([S.sb("xo%d" % i, [128, D_MODEL], F32) for i in range(1)])
        res_r = Rot([S.sb("res%d" % i, [128, D_MODEL], F32) for i in range(2)])
        be_r = Rot([S.sb("be%d" % i, [128, 7 * 128], F32) for i in range(2)]) if layer == 0 else None
        dcache = {}

        def load_dense(kname, vname, vc0, vw):
            key = (kname, vname, vc0)
            if dcache.get("key") == key:
                return dcache["KT"], dcache["VD"]
            KT = KT_r.next()
            VD = VD_r.next()
            i = fmi[kname]
            S.dma(LQ, KT[:, 0:HALF], fmS[i, :, U_OWN:U_OWN + HALF], reads=[fm_reg], writes=[KT])
            S.dma(LQ, KT[:, HALF:2 * HALF], fmS[i, :, U_O:U_O + HALF], reads=[fm_reg], writes=[KT])
            S.dma(LQ, KT[:, 2 * HALF:NKD], fmS[i, :, U_C:U_C + CTX], reads=[fm_reg], writes=[KT])
            for (c0, u0, n) in ((0, U_OWN, HALF), (HALF // 128, U_O, HALF), (2 * HALF // 128, U_C, CTX)):
                S.dma(LQ, VD[:, c0:c0 + n // 128, 0:vw], vS[vname][u0:u0 + n, vc0:vc0 + vw].rearrange("(k p) c -> p k c", p=128),
                      reads=[v_reg], writes=[VD])
            dcache.update(key=key, KT=KT, VD=VD)
            return KT, VD

        def attend(qt, pbase, NQ, kchunks, acc_list, vwidth, bias_fn=None):
            nq = NQ // 128
            n = len(kchunks)
            pend = []
            first = {}

            def issue_s(j):
                kt, kc0, vt, vap = kchunks[j]
                ps = bS.next()
                S.op("pe", lambda e: e.matmul(ps[:, 0:NQ], lhsT=kt[pbase:pbase + 64, kc0:kc0 + 128], rhs=qt[pbase:pbase + 64, 0:NQ],
                                              start=True, stop=True), reads=[kt, qt], writes=[ps])
                pT = pT_r.next()
                b = bias_fn(j) if bias_fn is not None else None
                if b is not None:
                    btile, bap = b
                    sb = sb_r.next()
                    S.op("dve", lambda e: e.scalar_tensor_tensor(out=sb[:, 0:NQ], in0=ps[:, 0:NQ], scalar=SCALE, in1=bap,
                                                                 op0=ALU.mult, op1=ALU.add), reads=[ps, btile], writes=[sb])
                    S.op("act", lambda e: e.activation(out=pT[:, 0:NQ], in_=sb[:, 0:NQ], func=AF.Exp), reads=[sb], writes=[pT])
                else:
                    S.op("act", lambda e: e.activation(out=pT[:, 0:NQ], in_=ps[:, 0:NQ], func=AF.Exp, scale=SCALE), reads=[ps], writes=[pT])
                return pT

            def issue_pv(j, pT):
                kt, kc0, vt, vap = kchunks[j]
                for s in range(nq):
                    acc, c0 = acc_list[s]
                    fst = first.get(id(acc), True)
                    first[id(acc)] = False
                    S.op("pe", lambda e: e.matmul(acc[:, c0:c0 + vwidth], lhsT=pT[:, s * 128:(s + 1) * 128], rhs=vap,
                                                  start=(j == 0 and fst), stop=(j == n - 1), skip_group_check=True),
                         reads=[pT, vt], writes=[acc])

            prev = None
            for j in range(n):
                pT = issue_s(j)
                if prev is not None:
                    issue_pv(prev[0], prev[1])
                prev = (j, pT)
            issue_pv(prev[0], prev[1])

        onesel_f = S.sb("onesel_f", [128, 2, 2], F32)
        S.op("pool", lambda e: e.memset(onesel_f[:], 0.0), writes=[onesel_f])
        S.op("pool", lambda e: e.memset(onesel_f[:, 0, 0:1], 1.0), writes=[onesel_f])
        S.op("pool", lambda e: e.memset(onesel_f[:, 1, 1:2], 1.0), writes=[onesel_f])
        den_acc = S.sb("den_acc", [128, 1024], F32)
        if layer == 1:
            dmb = S.sb("dmb", [128, 3, 384], F32)
            S.op("pool", lambda e: e.memset(dmb[:], 0.0), writes=[dmb])
            for var, (pi, ni) in enumerate(((2, 3), (0, 3), (2, 1))):
                S.op("dve", lambda e: e.tensor_copy(out=dmb[:, var, 0:128], in_=dm_t[:, pi, :]), reads=[dm_t], writes=[dmb])
                S.op("dve", lambda e: e.tensor_copy(out=dmb[:, var, 256:384], in_=dm_t[:, ni, :]), reads=[dm_t], writes=[dmb])
        pT2_r = Rot([S.sb("pT2_%d" % i, [128, 1024], BF16) for i in range(3)])
        oT_r = Rot([S.sb("oT%d" % i, [128, 512], F32) for i in range(2)])
        pairs = SH["pairs"]

        def dense_pair(qt, KT, VD, vap_fn, vw, accs, den=None):
            n = NKC

            def issue_s(j):
                pt, ta, tb = pairs[j % 2]
                for m, tt in ((0, ta), (1, tb)):
                    S.op("pe", lambda e: e.matmul(tt[:, 0:512], lhsT=KT[64 * m:64 * m + 64, j * 128:(j + 1) * 128],
                                                  rhs=qt[64 * m:64 * m + 64, 0:512], start=True, stop=True), reads=[KT, qt], writes=[tt])
                pT = pT2_r.next()
                S.op("act", lambda e: e.activation(out=pT[:], in_=pt[:, :], func=AF.Exp, scale=SCALE), reads=[ta, tb], writes=[pT])
                return pT

            def issue_pv(j, pT):
                for m in range(2):
                    S.op("pe", lambda e: e.matmul(accs[m][0:vw, 0:512], lhsT=vap_fn(j, m), rhs=pT[:, m * 512:(m + 1) * 512],
                                                  start=(j == 0), stop=(j == n - 1)), reads=[pT, VD], writes=[accs[m]])
                if den is not None:
                    if j == 0:
                        S.op("dve", lambda e: e.tensor_copy(out=den_acc[:], in_=pT[:]), reads=[pT], writes=[den_acc])
                    else:
                        S.op("dve", lambda e: e.tensor_tensor(out=den_acc[:], in0=den_acc[:], in1=pT[:], op=ALU.add), reads=[pT, den_acc], writes=[den_acc])
                    if j == n - 1:
                        for m in range(2):
                            S.op("pe", lambda e: e.matmul(den[0:2, 0:512], lhsT=onesel_f[:, m, :], rhs=den_acc[:, m * 512:(m + 1) * 512],
                                                          start=(m == 0), stop=(m == 1)), reads=[den_acc, onesel_f], writes=[den])

            prev = None
            for j in range(n):
                pT = issue_s(j)
                if prev is not None:
                    issue_pv(prev[0], prev[1])
                prev = (j, pT)
            issue_pv(prev[0], prev[1])

        def untranspose(acc, rows, fin, width):
            oT = oT_r.next()
            S.op("act", lambda e: e.activation(out=oT[0:rows, :], in_=acc[0:rows, 0:512], func=AF.Copy), reads=[acc], writes=[oT])
            for sidx in range(4):
                S.op("pe", lambda e: e.transpose(out=fin[:, sidx * width:sidx * width + rows], in_=oT[0:rows, sidx * 128:(sidx + 1) * 128],
                                                 identity=ident_f[0:rows, 0:rows]), reads=[oT, ident_f], writes=[fin])

        wide_i = [0]

        def wide_a(job):
            qt, pbase, kl, nch, nb, bias = job["qt"], job["pbase"], job["kl"], job["nch"], job["nb"], job["bias"]
            pt, ta, tb = pairs[wide_i[0] % 2]
            wide_i[0] += 1
            for j in range(nch):
                tt = ta if j < 4 else tb
                S.op("pe", lambda e: e.matmul(pt[:, j * 128:(j + 1) * 128], lhsT=kl[pbase:pbase + 64, j * 128:(j + 1) * 128],
                                              rhs=qt[pbase:pbase + 64, 0:128], start=True, stop=True), reads=[kl, qt], writes=[tt])
            pT = pT2_r.next()
            used = [ta] + ([tb] if nch > 4 else [])
            if bias is not None:
                btile, bap = bias
                sbw = stage2.next()
                S.op("dve", lambda e: e.scalar_tensor_tensor(out=sbw[:, 0:nb * 128], in0=pt[:, 0:nb * 128], scalar=SCALE, in1=bap,
                                                             op0=ALU.mult, op1=ALU.add), reads=used + [btile], writes=[sbw])
                S.op("act", lambda e: e.activation(out=pT[:, 0:nb * 128], in_=sbw[:, 0:nb * 128], func=AF.Exp), reads=[sbw], writes=[pT])
                if nch > nb:
                    S.op("act", lambda e: e.activation(out=pT[:, nb * 128:nch * 128], in_=pt[:, nb * 128:nch * 128], func=AF.Exp, scale=SCALE),
                         reads=used, writes=[pT])
            else:
                S.op("act", lambda e: e.activation(out=pT[:, 0:nch * 128], in_=pt[:, 0:nch * 128], func=AF.Exp, scale=SCALE), reads=used, writes=[pT])
            job["pT"] = pT

        def wide_b(job):
            pT, vl, vc0, nch = job["pT"], job["vl"], job["vc0"], job["nch"]
            acc = bACC.next()
            for j in range(nch):
                S.op("pe", lambda e: e.matmul(acc[:, 0:65], lhsT=pT[:, j * 128:(j + 1) * 128], rhs=vl[:, j, vc0:vc0 + 65],
                                              start=(j == 0), stop=(j == nch - 1)), reads=[pT, vl], writes=[acc])
            finish_head([(acc, 0)], 65, [job["yt"]], job["ycol"], extra_den=job.get("extra_den"))

        def run_wide(jobs):
            prev = None
            for job in jobs:
                wide_a(job)
                if prev is not None:
                    wide_b(prev)
                prev = job
            if prev is not None:
                wide_b(prev)

        def finish_head(acc_list, vwidth, y_tiles, ycol, extra_den=None, scale_ap=None):
            dv = vwidth - 1
            for s, (acc, c0) in enumerate(acc_list):
                rec = rec_r.next()
                if extra_den is not None:
                    S.op("dve", lambda e: e.tensor_tensor(out=rec[:, 0:1], in0=acc[:, c0 + dv:c0 + dv + 1], in1=extra_den, op=ALU.add),
                         reads=[acc, esnk], writes=[rec])
                    S.op("dve", lambda e: e.reciprocal(out=rec[:, 1:2], in_=rec[:, 0:1]), reads=[rec], writes=[rec])
                else:
                    S.op("dve", lambda e: e.reciprocal(out=rec[:, 1:2], in_=acc[:, c0 + dv:c0 + dv + 1]), reads=[acc], writes=[rec])
                yt = y_tiles[s]
                S.op("act", lambda e: e.activation(out=yt[:, ycol:ycol + dv], in_=acc[:, c0:c0 + dv], func=AF.Copy, scale=rec[:, 1:2]),
                     reads=[acc, rec], writes=[yt])

        def out_tile(yt, u_tok, src, src_row, w, dst, dst_row):
            zl = zl_r.next()
            yb = yb_r.next()
            yT = yT_r.next()
            xo = xo_r.next()
            res = res_r.next()
            S.dma(LQ, zl[:], zS[u_tok:u_tok + 128, :], reads=[z_reg], writes=[zl])
            load_x_tile(xo, u_tok // 128)
            S.op("dve", lambda e: e.tensor_tensor(out=yb[:], in0=yt[:], in1=zl[:], op=ALU.mult), reads=[yt, zl], writes=[yb])
            for k in range(8):
                S.op("pe", lambda e: e.transpose(out=pTb[:, k, :], in_=yb[:, k * 128:(k + 1) * 128], identity=ident[:]),
                     reads=[yb, ident], writes=[pTb_t])
            S.op("act", lambda e: e.activation(out=yT[:].rearrange("p k t -> p (k t)"), in_=pTb[:].rearrange("p k t -> p (k t)"), func=AF.Copy),
                 reads=[pTb_t], writes=[yT])
            for n in range(2):
                po = bACC.next()
                for k in range(8):
                    S.op("pe", lambda e: e.matmul(po[:], lhsT=yT[:, k, :], rhs=wout[:, k, n * 512:(n + 1) * 512], start=(k == 0), stop=(k == 7)),
                         reads=[yT, wout], writes=[po])
                S.op("dve", lambda e: e.tensor_tensor(out=res[:, n * 512:(n + 1) * 512], in0=po[:], in1=gate_bc[:, w, n * 512:(n + 1) * 512], op=ALU.mult),
                     reads=[po, gate_bc], writes=[res])
            S.op("dve", lambda e: e.tensor_tensor(out=res[:], in0=res[:], in1=xo[:], op=ALU.add), reads=[res, xo], writes=[res])
            if last:
                ss = ss_r.next()
                S.op("act", lambda e: e.activation(out=junk[:], in_=res[:], func=AF.Square, accum_out=ss[:, 0:1]), reads=[res], writes=[junk, ss])
                S.op("act", lambda e: e.activation(out=ss[:, 1:2], in_=ss[:, 0:1], func=AF.Ln, scale=1.0 / D_MODEL, bias=epst[:]), reads=[ss, epst], writes=[ss])
                S.op("act", lambda e: e.activation(out=ss[:, 1:2], in_=ss[:, 1:2], func=AF.Exp, scale=-0.5), reads=[ss], writes=[ss])
                S.op("act", lambda e: e.activation(out=xo[:], in_=res[:], func=AF.Copy, scale=ss[:, 1:2]), reads=[res, ss], writes=[xo])
                S.op("dve", lambda e: e.tensor_tensor(out=res[:], in0=xo[:], in1=fn_t[:], op=ALU.mult), reads=[xo, fn_t], writes=[res])
            S.dma(SQ, dst[dst_row:dst_row + 128, :], res[:], reads=[res], sem_tile=res)
            return res

        out_tiles = []

        def load_q(name, u0, n):
            qt = q_r.next()
            S.dma(LQ, qt[:, 0:n], fmS[fmi[name], :, u0:u0 + n], reads=[fm_reg], writes=[qt])
            return qt

        def load_local(knames_idx, vname, vc0, vw, utiles):
            kl = kl_r.next()
            vl = vl_r.next()
            pos = 0
            for (ut0, cnt) in utiles:
                S.dma(LQ, kl[:, pos * 128:(pos + cnt) * 128], fmS[knames_idx, :, ut0 * 128:(ut0 + cnt) * 128], reads=[fm_reg], writes=[kl])
                S.dma(LQ, vl[:, pos:pos + cnt, 0:vw], vS[vname][ut0 * 128:(ut0 + cnt) * 128, vc0:vc0 + vw].rearrange("(k p) c -> p k c", p=128),
                      reads=[v_reg], writes=[vl])
                pos += cnt
            return kl, vl

        UC_T = U_C // 128

        if layer == 0:
            for t in range(2):
                yt = y_t[t]
                u0 = U_C + t * 128
                jobs = []
                kl, vl = load_local(fmi["ka"], "va", 0, 130, [(UC_T, 2)])
                for c in range(4):
                    qt = load_q("qa%d" % c, u0, 128)
                    for s_ in range(2):
                        jobs.append(dict(qt=qt, pbase=64 * s_, kl=kl, vl=vl, vc0=s_ * 65, nch=2, nb=0, bias=None, yt=yt, ycol=(c + 4 * s_) * 64))
                run_wide(jobs)
                for c in range(4):
                    kl, vl = load_local(fmi["kb%d" % c], "vb", c * 130, 130, [(UC_T, 2)])
                    qt = load_q("qb%d" % c, u0, 128)
                    run_wide([dict(qt=qt, pbase=64 * s_, kl=kl, vl=vl, vc0=s_ * 65, nch=2, nb=0, bias=None, yt=yt, ycol=512 + (2 * c + s_) * 64)
                              for s_ in range(2)])
                out_tiles.append(out_tile(yt, u0, xc_in, t * 128, 1, out_c, t * 128))

        for qb in range(NBo):
            u0 = U_OWN + qb * 512
            if layer == 0:
                KT, VD = load_dense("ka", "va", 0, 130)
                for c in range(4):
                    qt = load_q("qa%d" % c, u0, 512)
                    accs = [banks[5], banks[6]]
                    dense_pair(qt, KT, VD, lambda j, m: VD[:, j, m * 65:(m + 1) * 65], 65, accs)
                    for s in range(2):
                        head = c + 4 * s
                        fin = banks[7]
                        untranspose(accs[s], 65, fin, 65)
                        finish_head([(fin, i * 65) for i in range(4)], 65, y_t, head * 64)
            else:
                for h in range(4):
                    KT, VD = load_dense("kc%d" % h, "vc", h * 129, 129)
                    qt = load_q("qc%d" % h, u0, 512)
                    accs = [banks[5], banks[6]]
                    den = banks[7]
                    dense_pair(qt, KT, VD, lambda j, m: VD[:, j, 0:128], 128, accs, den=den)
                    fins = [banks[1], banks[2]]
                    untranspose(accs[0], 128, fins[0], 128)
                    untranspose(accs[1], 128, fins[1], 128)
                    dfin = banks[3]
                    untranspose(den, 2, dfin, 2)
                    o_m = [[(fins[0], i * 128, i * 2 + 0) for i in range(4)], [(fins[1], i * 128, i * 2 + 1) for i in range(4)]]
                    for s in range(4):
                        rec = rec_r.next()
                        yt = y_t[s]
                        a0, c0, d0 = o_m[0][s]
                        a1, c1, d1 = o_m[1][s]
                        S.op("dve", lambda e: e.reciprocal(out=rec[:, 0:1], in_=dfin[:, d0:d0 + 1]), reads=[dfin], writes=[rec])
                        S.op("dve", lambda e: e.reciprocal(out=rec[:, 1:2], in_=dfin[:, d1:d1 + 1]), reads=[dfin, rec], writes=[rec])
                        S.op("dve", lambda e: e.tensor_tensor(out=rec[:, 2:3], in0=rec[:, 1:2], in1=lam_s[:, 3:4], op=ALU.mult), reads=[rec, lam_s], writes=[rec])
                        t1 = t1_r.next()
                        S.op("act", lambda e: e.activation(out=t1[:, 0:128], in_=a0[:, c0:c0 + 128], func=AF.Copy, scale=rec[:, 0:1]), reads=[a0, rec], writes=[t1])
                        S.op("dve", lambda e: e.scalar_tensor_tensor(out=t1[:, 128:256], in0=a1[:, c1:c1 + 128], scalar=rec[:, 2:3], in1=t1[:, 0:128],
                                                                     op0=ALU.mult, op1=ALU.add), reads=[a1, rec, t1], writes=[t1])
                        S.op("act", lambda e: e.activation(out=t1[:, 256:384], in_=t1[:, 128:256], func=AF.Square, accum_out=rec[:, 3:4]), reads=[t1], writes=[t1, rec])
                        S.op("act", lambda e: e.activation(out=rec[:, 4:5], in_=rec[:, 3:4], func=AF.Ln, scale=1.0 / 128, bias=epst[:]), reads=[rec, epst], writes=[rec])
                        S.op("act", lambda e: e.activation(out=rec[:, 4:5], in_=rec[:, 4:5], func=AF.Exp, scale=-0.5), reads=[rec], writes=[rec])
                        S.op("dve", lambda e: e.tensor_scalar_mul(out=rec[:, 4:5], in0=rec[:, 4:5], scalar1=1.0 - lam0), reads=[rec], writes=[rec])
                        S.op("dve", lambda e: e.scalar_tensor_tensor(out=yt[:, h * 128:(h + 1) * 128], in0=t1[:, 128:256], scalar=rec[:, 4:5], in1=sub_t[:],
                                                                     op0=ALU.mult, op1=ALU.mult), reads=[t1, rec, sub_t], writes=[yt])
            for tl in range(4):
                t = qb * 4 + tl
                ut = U_OWN // 128 + t
                yt = y_t[tl]
                if layer == 0:
                    edge = t < 2 or t >= NTo - 2
                    if edge:
                        et = t if t < 2 else 2 + (t - (NTo - 2))
                        J = 7
                        runs = [(ut - 3, 7), (UC_T, 2)]
                    else:
                        J = 5
                        runs = [(ut - 2, 5), (UC_T, 2)]
                    for c in range(4):
                        kl, vl = load_local(fmi["kb%d" % c], "vb", c * 130, 130, runs)
                        qt = load_q("qb%d" % c, ut * 128, 128)
                        if not edge:
                            run_wide([dict(qt=qt, pbase=64 * s_, kl=kl, vl=vl, vc0=s_ * 65, nch=7, nb=5,
                                           bias=(bi_t, bi_t[:, 2 * c + s_, 0:640]), yt=yt, ycol=512 + (2 * c + s_) * 64) for s_ in range(2)])
                            continue
                        for s in range(2):
                            head = 2 * c + s
                            if edge:
                                be = be_r.next()
                                S.dma(LQ, be[:], bias_e[et, head], writes=[be])
                                bfn = (lambda j, be=be: (be, be[:, j * 128:(j + 1) * 128]) if j < 7 else None)
                            else:
                                bfn = (lambda j, head=head: (bi_t, bi_t[:, head, j * 128:(j + 1) * 128]) if j < 5 else None)
                            acc = bACC.next()
                            attend(qt, 64 * s, 128, [(kl, j * 128, vl, vl[:, j, s * 65:(s + 1) * 65]) for j in range(J + 2)],
                                   [(acc, 0)], 65, bias_fn=bfn)
                            finish_head([(acc, 0)], 65, [yt], 512 + head * 64)
                else:
                    kl, vl = load_local(fmi["kd"], "vd", 0, 130, [(ut - 1, 3), (UC_T, 2)])
                    var = 1 if t == 0 else (2 if t == NTo - 1 else 0)
                    jobs = []
                    for c in range(4):
                        qt = load_q("qd%d" % c, ut * 128, 128)
                        for s_ in range(2):
                            head = c + 4 * s_
                            jobs.append(dict(qt=qt, pbase=64 * s_, kl=kl, vl=vl, vc0=s_ * 65, nch=5, nb=3, bias=(dmb, dmb[:, var, :]),
                                             yt=yt, ycol=512 + head * 64, extra_den=esnk[:, head:head + 1]))
                    run_wide(jobs)
                out_tiles.append(out_tile(yt, ut * 128, x_u, ut * 128, 0, out_x, t * 128))
        if last:
            S.finish(out_tiles)
        else:
            S.barrier()
        st2.close()
    if not last:
        S.release_dsems()
        S.new_epoch()


def build_fused(SEQ, B):
    HALF = SEQ // 2
    EXT = HALF + 2 * HALO
    nc = bass.Bass("TRN2", target_bir_lowering=False)

    def din(name, shape):
        return nc.dram_tensor(name, list(shape), F32, kind="ExternalInput").ap()

    SH = dict(x_u=din("x_u", [EXT + HALF, D_MODEL]), xc=din("xc", [CTX, D_MODEL]), cvec=din("cvec", [128, 8, 2]),
              ident=din("ident", [128, 128]), blk=din("blk", [128, 128]), perm=din("perm", [128, 128]),
              ropeC=din("ropeC", [128, EXT + HALF]), ropeS=din("ropeS", [128, EXT + HALF]), sel=din("sel", [128, 4]))
    SH["x1_loc"] = nc.dram_tensor("x1_loc", [HALF, D_MODEL], F32).ap()
    SH["xc1_loc"] = nc.dram_tensor("xc1_loc", [CTX, D_MODEL], F32).ap()
    SH["GA"] = nc.dram_tensor("x1_all", [2 * HALF, D_MODEL], F32).ap()
    with contextlib.ExitStack() as st0:
        S = Sched(nc, st0)
        SH["pTb_t"] = S.ps("bankT", [128, 8, 128], BF16)
        pairA = st0.enter_context(nc.psum_tensor("ps_pairA", [128, 1024], F32))
        pairB = st0.enter_context(nc.psum_tensor("ps_pairB", [128, 1024], F32))
        b1 = Tile("bank1", pairA[:, 0:512], excl=True)
        b2 = Tile("bank2", pairA[:, 512:1024], excl=True)
        b3 = Tile("bank3", pairB[:, 0:512], excl=True)
        b4 = Tile("bank4", pairB[:, 512:1024], excl=True)
        SH["pairs"] = [(pairA, b1, b2), (pairB, b3, b4)]
        SH["banks"] = [SH["pTb_t"], b1, b2, b3, b4] + [S.ps("bank%d" % i, [128, 512], F32) for i in range(5, 8)]
        SH["G_reg"] = Tile("G_reg")
        emit_layer(nc, S, SH, 0, SEQ)
        S.sems["cc"] = st0.enter_context(nc.semaphore("s_cc"))
        RC = min(512, HALF)
        for i in range(HALF // RC):
            nc.gpsimd.collective_compute("AllGather", ALU.bypass, replica_groups=[[2 * b, 2 * b + 1] for b in range(B)],
                                         ins=[SH["x1_loc"][i * RC:(i + 1) * RC, :]],
                                         outs=[SH["GA"][i * 2 * RC:(i + 1) * 2 * RC, :]]).then_inc(S.sems["cc"], 1)
        SH["G_reg"].last_w = ("cc", HALF // RC)
        emit_layer(nc, S, SH, 1, SEQ)
    return nc


def rope_tables(pos):
    row = (pos // GRID_W).astype(np.float32)
    col = (pos % GRID_W).astype(np.float32)
    q = HD // 4
    inv = (10000.0 ** (-np.arange(q, dtype=np.float32) / q)).astype(np.float32)
    ar = row[None, :] * inv[:, None]
    ac = col[None, :] * inv[:, None]
    cr, sr, cc, sc = np.cos(ar), np.sin(ar), np.cos(ac), np.sin(ac)
    C = np.concatenate([cr, cr, cc, cc], axis=0)
    Sg = np.concatenate([-sr, sr, -sc, sc], axis=0)
    return (np.concatenate([C, C], 0).astype(np.float32), np.concatenate([Sg, Sg], 0).astype(np.float32))


def nbr_bias(rpb, g, gk, NT):
    rows = NT * 2
    out = np.full((8, 128, 128), NEG, np.float32)
    if gk < 0 or gk >= NT:
        return out
    ql = np.arange(128)
    r = 2 * g + ql // 64
    c = ql % 64
    kr = 2 * gk + ql // 64
    kc = ql % 64
    win_r = min(8, rows)
    rs = np.clip(r - win_r // 2, 0, rows - win_r)
    cs = np.clip(c - 8, 0, GRID_W - 16)
    valid = ((kr[:, None] >= rs[None, :]) & (kr[:, None] < rs[None, :] + win_r)
             & (kc[:, None] >= cs[None, :]) & (kc[:, None] < cs[None, :] + 16))
    di = kr[:, None] - r[None, :] + 7
    dj = kc[:, None] - c[None, :] + 15
    di = np.clip(di, 0, 14)
    dj = np.clip(dj, 0, 30)
    vals = rpb[:, di, dj]
    return np.where(valid[None], vals, np.float32(NEG)).astype(np.float32)


def chunk_rows(w):
    return np.ascontiguousarray(w.reshape(8, 128, w.shape[1]))


def prep_layer_inputs(layer, SEQ, xs, xcs, p):
    B = xs.shape[0]
    HALF = SEQ // 2
    EXT = HALF + 2 * HALO
    NT = SEQ // 128
    NTo = HALF // 128
    cfg = layer_cfg(layer)
    wi = p["w_in_even"][0] if layer == 0 else p["w_in_odd"][0]
    wo = p["w_out_even"][0] if layer == 0 else p["w_out_odd"][0]
    cols = []
    for f in cfg["fm"]:
        cols += f[1]
    for name, c0, n in cfg["tm"]:
        cols += list(range(c0, c0 + n))
    w_in_l = chunk_rows(np.ascontiguousarray(wi[:, cols]))
    w_out_l = chunk_rows(wo)
    w_mod_l = chunk_rows(p["w_mod"][layer])
    b_mod_l = np.ascontiguousarray(p["b_mod"][layer].reshape(24, 128).T)
    bgate = np.ascontiguousarray(np.broadcast_to(p["b_mod"][layer][2048:3072][None, :], (128, D_MODEL)))
    ident = np.eye(128, dtype=np.float32)
    blk = np.zeros((128, 128), np.float32)
    blk[:64, :64] = 1.0 / 64
    blk[64:, 64:] = 1.0 / 64
    perm = np.zeros((128, 128), np.float32)
    for m in range(128):
        k = m + 16 if (m % 32) < 16 else m - 16
        perm[k, m] = 1.0
    maps = []
    for b in range(B):
        for half in range(2):
            T0 = half * HALF
            pos_e = np.arange(T0 - HALO, T0 + HALF + HALO)
            valid_e = (pos_e >= 0) & (pos_e < SEQ)
            x_e = np.zeros((EXT, D_MODEL), np.float32)
            x_e[valid_e] = xs[b, pos_e[valid_e]]
            T1 = (1 - half) * HALF
            pos_o = np.arange(T1, T1 + HALF)
            x_u = np.concatenate([x_e, xs[b, pos_o]], axis=0)
            pos_u = np.concatenate([np.clip(pos_e, 0, SEQ - 1), pos_o])
            C, Sg = rope_tables(pos_u)
            cvec = np.stack([p["c"][b].reshape(8, 128).T, p["c_ctx"].reshape(8, 128).T], axis=-1)
            m = dict(x_u=x_u, xc=np.ascontiguousarray(xcs[b]), cvec=np.ascontiguousarray(cvec), w_mod=w_mod_l, b_mod=b_mod_l,
                     bgate=bgate, w_in=w_in_l, w_out=w_out_l, ident=ident, blk=blk, perm=perm, ropeC=C, ropeS=Sg)
            G0 = T0 // 128
            if layer == 0:
                m["gains"] = np.ascontiguousarray(np.stack([np.tile(p["a_q_norm"][0], 2), np.tile(p["a_k_norm"][0], 2)], axis=-1))
                rpb = p["b_rpb"][0]
                gi = min(max(G0 + 2, 2), NT - 3) if NT >= 6 else 0
                bi = np.stack([nbr_bias(rpb, gi, gi + j, NT) for j in range(-2, 3)], axis=0)
                m["bias_i"] = np.ascontiguousarray(bi.transpose(2, 1, 0, 3).reshape(128, 8, 640))
                ets = [0, 1, NTo - 2, NTo - 1]
                be = np.stack([np.stack([nbr_bias(rpb, G0 + t, G0 + t + j, NT) for j in range(-3, 4)], axis=0) for t in ets], axis=0)
                m["bias_e"] = np.ascontiguousarray(be.transpose(0, 2, 3, 1, 4).reshape(4, 8, 128, 896))
            else:
                a = np.arange(128)
                tri_prev = np.where(a[:, None] >= a[None, :], 0.0, NEG).astype(np.float32)
                tri_next = np.where(a[:, None] <= a[None, :], 0.0, NEG).astype(np.float32)
                full = np.full((128, 128), NEG, np.float32)
                first_prev = full if G0 == 0 else tri_prev
                last_next = full if G0 + NTo == NT else tri_next
                m["dmask"] = np.ascontiguousarray(np.stack([first_prev, last_next, tri_prev, tri_next], axis=1))
                m["lamv"] = np.ascontiguousarray(np.broadcast_to(p["c_lambda"][0].reshape(1, 256), (128, 256)))
                m["subln"] = np.ascontiguousarray(np.broadcast_to((p["c_subln"][0])[None, :], (128, 128)))
                m["sinks"] = np.ascontiguousarray(np.broadcast_to(p["d_sinks"][0][None, :], (128, 8)))
                m["fnorm"] = np.ascontiguousarray(np.broadcast_to(p["final_norm"][None, :], (128, D_MODEL)))
            maps.append(m)
    return maps


def prep_fused_inputs(SEQ, xs, xcs, p):
    m0 = prep_layer_inputs(0, SEQ, xs, xcs, p)
    m1 = prep_layer_inputs(1, SEQ, xs, xcs, p)
    shared = ("x_u", "xc", "cvec", "ident", "blk", "perm", "ropeC", "ropeS")
    maps = []
    for i, (a, b) in enumerate(zip(m0, m1)):
        half = i % 2
        m = {k: a[k] for k in shared}
        for k, v in a.items():
            if k not in shared:
                m["l0_" + k] = v
        for k, v in b.items():
            if k not in shared:
                m["l1_" + k] = v
        sel = np.zeros((128, 4), np.float32)
        sel[:, 0] = 1.0 if half == 1 else 0.0
        sel[:, 1] = 1.0 if half == 0 else 0.0
        sel[:, 2] = 1.0 if half == 1 else 0.0
        sel[:, 3] = 1.0 if half == 0 else 0.0
        m["sel"] = sel
        maps.append(m)
    return maps


def run_fused(SEQ, xs, xcs, p, runner=None):
    B = xs.shape[0]
    key = ("fused", SEQ, B)
    if key not in _NC_CACHE:
        _NC_CACHE[key] = build_fused(SEQ, B)
    nc = _NC_CACHE[key]
    maps = prep_fused_inputs(SEQ, xs, xcs, p)
    if runner is None:
        res = run_bass_kernel_spmd(nc, maps, core_ids=list(range(len(maps)))).results
    else:
        res = runner(nc, maps)
    HALF = SEQ // 2
    xo = np.zeros_like(xs)
    for b in range(B):
        for half in range(2):
            xo[b, half * HALF:(half + 1) * HALF] = res[2 * b + half]["out_x"]
    return xo


_NC_CACHE = {}


def kernel(x, c, ctx, c_ctx, w_mod, b_mod, w_in_even, w_out_even, a_q_norm, a_k_norm, b_rpb,
           w_in_odd, w_out_odd, c_lambda, c_subln, d_sinks, final_norm):
    p = dict(c=np.asarray(c, np.float32), c_ctx=np.asarray(c_ctx, np.float32), w_mod=np.asarray(w_mod, np.float32),
             b_mod=np.asarray(b_mod, np.float32), w_in_even=np.asarray(w_in_even, np.float32),
             w_out_even=np.asarray(w_out_even, np.float32), a_q_norm=np.asarray(a_q_norm, np.float32),
             a_k_norm=np.asarray(a_k_norm, np.float32), b_rpb=np.asarray(b_rpb, np.float32),
             w_in_odd=np.asarray(w_in_odd, np.float32), w_out_odd=np.asarray(w_out_odd, np.float32),
             c_lambda=np.asarray(c_lambda, np.float32), c_subln=np.asarray(c_subln, np.float32),
             d_sinks=np.asarray(d_sinks, np.float32), final_norm=np.asarray(final_norm, np.float32))
    xs = np.asarray(x, np.float32)
    xcs = np.asarray(ctx, np.float32)
    SEQ = xs.shape[1]
    return run_fused(SEQ, xs, xcs, p)
```

```python
import contextlib
import math
import numpy as np
import concourse.bass as bass
import concourse.mybir as mybir
from concourse.bass_utils import run_bass_kernel_spmd

F32 = mybir.dt.float32
BF16 = mybir.dt.bfloat16
AF = mybir.ActivationFunctionType
ALU = mybir.AluOpType

D_MODEL = 1024
CTX = 256
HD = 64
GRID_W = 64
SCALE = HD ** -0.5
EPS = 1e-6
NEG = -30000.0
HALO = 512
LQ = "sp"
SQ = "pool"


class Tile:
    __slots__ = ("name", "t", "last_w", "readers", "dsem", "dcount", "excl")

    def __init__(self, name, t=None, excl=False):
        self.name = name
        self.t = t
        self.last_w = None
        self.readers = {}
        self.dsem = None
        self.dcount = 0
        self.excl = excl

    def __getitem__(self, idx):
        return self.t[idx]


class Sched:
    def __init__(self, nc, stack):
        self.nc = nc
        self.stack = stack
        self.sem_stack = stack
        self.dtiles = []
        self.engs = {}
        self.sems = {}
        for en, e in (("pe", nc.tensor), ("act", nc.scalar), ("dve", nc.vector),
                      ("pool", nc.gpsimd), ("sp", nc.sync)):
            self.sems[en] = stack.enter_context(nc.semaphore("s_" + en))
            self.engs[en] = dict(eng=e, count=0, seen={}, key=en)
        self.epoch = 0
        self.nsem = 0
        self.prefix = ""
        self.free_dsems = []

    def sb(self, name, shape, dt):
        return Tile(name, self.stack.enter_context(self.nc.sbuf_tensor("sb_" + self.prefix + name, list(shape), dt)))

    def ps(self, name, shape, dt=F32):
        return Tile(name, self.stack.enter_context(self.nc.psum_tensor("ps_" + name, list(shape), dt)), excl=True)

    def _dsem(self, tile):
        if tile.dsem is None:
            if self.free_dsems:
                key, cnt = self.free_dsems.pop()
                tile.dsem = key
                tile.dcount = cnt
            else:
                key = "d%d" % self.nsem
                self.nsem += 1
                tile.dsem = key
                self.sems[key] = self.sem_stack.enter_context(self.nc.semaphore(key))
            self.dtiles.append(tile)
        return tile.dsem

    def new_epoch(self):
        self.epoch += 1
        for en, E in self.engs.items():
            key = "%s#%d" % (en, self.epoch)
            self.sems[key] = self.sem_stack.enter_context(self.nc.semaphore("s_%s_%d" % (en, self.epoch)))
            E["key"] = key
            E["count"] = 0

    def release_dsems(self):
        for t in self.dtiles:
            self.free_dsems.append((t.dsem, t.dcount))
            t.dsem = None
        self.dtiles = []

    def _wait_deps(self, en, reads, writes):
        E = self.engs[en]
        deps = {}

        def add(ev):
            if ev is None:
                return
            k, v = ev
            if deps.get(k, 0) < v:
                deps[k] = v

        me = E["key"]
        for t in reads:
            add(t.last_w)
            if t.excl:
                for k, v in t.readers.items():
                    if k != me:
                        add((k, v))
        for t in writes:
            add(t.last_w)
            for k, v in t.readers.items():
                if k != me:
                    add((k, v))
        for k, v in deps.items():
            if E["seen"].get(k, 0) < v:
                E["seen"][k] = v
                if k == me and en == "pe":
                    continue
                E["eng"].wait_ge(self.sems[k], v)

    def op(self, en, fn, reads=(), writes=()):
        E = self.engs[en]
        self._wait_deps(en, reads, writes)
        ins = fn(E["eng"])
        E["count"] += 1
        me = E["key"]
        ins.then_inc(self.sems[me], 1)
        for t in reads:
            t.readers[me] = E["count"]
        for t in writes:
            t.last_w = (me, E["count"])
            t.readers = {}
        return ins

    def dma(self, q, out, in_, reads=(), writes=(), sem_tile=None):
        E = self.engs[q]
        self._wait_deps(q, reads, writes)
        st = sem_tile if sem_tile is not None else (list(writes) + list(reads))[0]
        key = self._dsem(st)
        ins = E["eng"].dma_start(out=out, in_=in_)
        st.dcount += 16
        ins.then_inc(self.sems[key], 16)
        for t in reads:
            t.readers[key] = st.dcount
        for t in writes:
            t.last_w = (key, st.dcount)
            t.readers = {}
        return ins

    def barrier(self):
        for en, E in self.engs.items():
            for en2, E2 in self.engs.items():
                k2 = E2["key"]
                if en2 != en and E2["count"] and E["seen"].get(k2, 0) < E2["count"]:
                    E["seen"][k2] = E2["count"]
                    E["eng"].wait_ge(self.sems[k2], E2["count"])
            for t in self.dtiles:
                if t.dcount and E["seen"].get(t.dsem, 0) < t.dcount:
                    E["seen"][t.dsem] = t.dcount
                    E["eng"].wait_ge(self.sems[t.dsem], t.dcount)

    def finish(self, tiles, en="sp"):
        self._wait_deps(en, tiles, tiles)


class Rot:
    def __init__(self, tiles):
        self.tiles = tiles
        self.i = 0

    def next(self):
        t = self.tiles[self.i % len(self.tiles)]
        self.i += 1
        return t


def layer_cfg(layer):
    if layer == 0:
        qa, ka, va, qb, kb, vb, z = 0, 512, 640, 768, 1280, 1792, 2304
        fm = []
        for c in range(4):
            cols = list(range(qa + c * 64, qa + c * 64 + 64)) + list(range(qa + (4 + c) * 64, qa + (4 + c) * 64 + 64))
            fm.append(("qa%d" % c, cols, "q", True))
        fm.append(("ka", list(range(ka, ka + 128)), "k", True))
        for c in range(4):
            fm.append(("qb%d" % c, list(range(qb + c * 128, qb + c * 128 + 128)), None, False))
        for c in range(4):
            fm.append(("kb%d" % c, list(range(kb + c * 128, kb + c * 128 + 128)), None, False))
        tm = [("va", va, 128), ("vb", vb, 512), ("z", z, 1024)]
        dense_k, dense_v = ["ka"], ["va"]
        local_k, local_v = ["kb0", "kb1", "kb2", "kb3"], ["vb"]
    else:
        qc, kc, vc, qd, kd, vd, z = 0, 512, 1024, 1536, 2048, 2176, 2304
        fm = []
        for c in range(4):
            fm.append(("qc%d" % c, list(range(qc + c * 128, qc + c * 128 + 128)), None, True))
        for c in range(4):
            fm.append(("kc%d" % c, list(range(kc + c * 128, kc + c * 128 + 128)), None, True))
        for c in range(4):
            cols = list(range(qd + c * 64, qd + c * 64 + 64)) + list(range(qd + (4 + c) * 64, qd + (4 + c) * 64 + 64))
            fm.append(("qd%d" % c, cols, None, True))
        fm.append(("kd", list(range(kd, kd + 128)), None, True))
        tm = [("vc", vc, 512), ("vd", vd, 128), ("z", z, 1024)]
        dense_k, dense_v = ["kc0", "kc1", "kc2", "kc3"], ["vc"]
        local_k, local_v = ["kd"], ["vd"]
    return dict(fm=fm, tm=tm, dense_k=dense_k, dense_v=dense_v, local_k=local_k, local_v=local_v)


def lambda_init(layer):
    return 0.8 - 0.6 * math.exp(-0.3 * layer)


def emit_layer(nc, S, SH, layer, SEQ):
    HALF = SEQ // 2
    EXT = HALF + 2 * HALO
    NU = EXT + HALF + CTX
    NTo = HALF // 128
    NBo = HALF // 512
    U_OWN = HALO
    U_O = EXT
    U_C = EXT + HALF
    NKD = 2 * HALF + CTX
    NKC = NKD // 128
    last = layer == 1
    cfg = layer_cfg(layer)
    fm, tm = cfg["fm"], cfg["tm"]
    NFM = len(fm)
    fmi = {f[0]: i for i, f in enumerate(fm)}
    NCOL = NFM * 128 + sum(t[2] for t in tm)
    tmoff = {}
    o = NFM * 128
    for name, _, n in tm:
        tmoff[name] = o
        o += n
    lam0 = lambda_init(layer)

    LP = "l%d_" % layer
    S.prefix = LP

    def din(name, shape):
        return nc.dram_tensor(LP + name, list(shape), F32, kind="ExternalInput").ap()

    x_u, xc_in, cvec = SH["x_u"], SH["xc"], SH["cvec"]
    ident_d, blk_d, perm_d, ropeC, ropeS = SH["ident"], SH["blk"], SH["perm"], SH["ropeC"], SH["ropeS"]
    x1_loc, xc1_loc, GA = SH["x1_loc"], SH["xc1_loc"], SH["GA"]
    w_mod = din("w_mod", [8, 128, 3072])
    b_mod = din("b_mod", [128, 24])
    bgate = din("bgate", [128, D_MODEL])
    w_in = din("w_in", [8, 128, NCOL])
    w_out = din("w_out", [8, 128, D_MODEL])
    if layer == 0:
        gains = din("gains", [128, 2])
        bias_i = din("bias_i", [128, 8, 5 * 128])
        bias_e = din("bias_e", [4, 8, 128, 7 * 128])
    else:
        dmask = din("dmask", [128, 4, 128])
        lamv = din("lamv", [128, 256])
        subln = din("subln", [128, 128])
        sinks = din("sinks", [128, 8])
        fnorm = din("fnorm", [128, D_MODEL])
    if last:
        out_x = nc.dram_tensor("out_x", [HALF, D_MODEL], F32, kind="ExternalOutput").ap()
    else:
        out_x = x1_loc
        out_c = xc1_loc

    fmS = nc.dram_tensor(LP + "fmS", [NFM, 128, NU], BF16).ap()
    vdims = {"va": (2, 65), "vb": (8, 65), "vc": (4, 129), "vd": (2, 65)}
    vS = {}
    for name, _, n in tm:
        if name != "z":
            h, d = vdims[name]
            vS[name] = nc.dram_tensor(LP + "vS_" + name, [NU, h * d], BF16).ap()
    zS = nc.dram_tensor(LP + "zS", [NU, D_MODEL], BF16).ap()

    with contextlib.ExitStack() as st:
        S.stack = st
        st1 = contextlib.ExitStack()
        fm_reg = Tile("fm_reg")
        v_reg = Tile("v_reg")
        z_reg = Tile("z_reg")

        ident_f = S.sb("ident_f", [128, 128], F32)
        ident = S.sb("ident", [128, 128], BF16)
        blk = S.sb("blk", [128, 128], F32)
        perm = S.sb("perm", [128, 128], F32)
        epst = S.sb("epst", [128, 1], F32)
        zeros = S.sb("zeros", [128, 128], F32)
        junk = S.sb("junk", [128, D_MODEL], F32)
        ss_r = Rot([S.sb("ss%d" % i, [128, 2], F32) for i in range(2)])
        t1_r = Rot([S.sb("t1%d" % i, [128, 512], F32) for i in range(2)])
        gate_bc = S.sb("gate_bc", [128, 2, D_MODEL], F32)
        modt = S.sb("modt", [128, 24, 2], F32)
        bmodt = S.sb("bmodt", [128, 24], F32)
        cv = S.sb("cv", [128, 8, 2], F32)
        pTb_t = SH["pTb_t"]
        pTb = pTb_t.t
        banks = SH["banks"]
        sel_t = S.sb("sel_t", [128, 4], F32)
        S.dma(LQ, sel_t[:], SH["sel"], writes=[sel_t])
        xt2 = S.sb("xt2", [128, D_MODEL], F32)
        G_reg = SH["G_reg"]
        NTo_ = HALF // 128

        RCt = min(512, HALF) // 128

        def grow(r, t):
            return ((t // RCt) * 2 * RCt + r * RCt + (t % RCt)) * 128

        def load_x_tile(xt, utile):
            e = utile
            if layer == 0:
                if e >= (EXT + HALF) // 128:
                    c = e - (EXT + HALF) // 128
                    S.dma(LQ, xt[:], xc_in[c * 128:(c + 1) * 128, :], writes=[xt])
                else:
                    S.dma(LQ, xt[:], x_u[e * 128:(e + 1) * 128, :], writes=[xt])
                return
            if e >= (EXT + HALF) // 128:
                c = e - (EXT + HALF) // 128
                S.dma(LQ, xt[:], xc1_loc[c * 128:(c + 1) * 128, :], writes=[xt])
            elif e >= EXT // 128:
                o = e - EXT // 128
                S.dma(LQ, xt[:], GA[grow(0, o):grow(0, o) + 128, :], reads=[G_reg], writes=[xt])
                S.dma(LQ, xt2[:], GA[grow(1, o):grow(1, o) + 128, :], reads=[G_reg], writes=[xt2])
                S.op("act", lambda en: en.activation(out=xt[:], in_=xt[:], func=AF.Copy, scale=sel_t[:, 2:3]), reads=[xt, sel_t], writes=[xt])
                S.op("dve", lambda en: en.scalar_tensor_tensor(out=xt[:], in0=xt2[:], scalar=sel_t[:, 3:4], in1=xt[:], op0=ALU.mult, op1=ALU.add),
                     reads=[xt2, sel_t, xt], writes=[xt])
            elif 4 <= e < 4 + NTo_:
                t = e - 4
                S.dma(LQ, xt[:], x1_loc[t * 128:(t + 1) * 128, :], writes=[xt])
            elif e < 4:
                gt = grow(0, NTo_ - 4 + e)
                S.dma(LQ, xt[:], GA[gt:gt + 128, :], reads=[G_reg], writes=[xt])
                S.op("act", lambda en: en.activation(out=xt[:], in_=xt[:], func=AF.Copy, scale=sel_t[:, 0:1]), reads=[xt, sel_t], writes=[xt])
            else:
                gt = grow(1, e - 4 - NTo_)
                S.dma(LQ, xt[:], GA[gt:gt + 128, :], reads=[G_reg], writes=[xt])
                S.op("act", lambda en: en.activation(out=xt[:], in_=xt[:], func=AF.Copy, scale=sel_t[:, 1:2]), reads=[xt, sel_t], writes=[xt])

        S.dma(LQ, ident_f[:], ident_d, writes=[ident_f])
        S.dma(LQ, blk[:], blk_d, writes=[blk])
        S.dma(LQ, perm[:], perm_d, writes=[perm])
        S.dma(LQ, cv[:], cvec, writes=[cv])
        S.dma(LQ, bmodt[:], b_mod, writes=[bmodt])
        S.dma(LQ, gate_bc[:, 0, :], bgate, writes=[gate_bc])
        S.op("dve", lambda e: e.tensor_copy(out=ident[:], in_=ident_f[:]), reads=[ident_f], writes=[ident])
        S.op("pool", lambda e: e.memset(epst[:], EPS), writes=[epst])
        S.op("pool", lambda e: e.memset(zeros[:], 0.0), writes=[zeros])
        S.op("dve", lambda e: e.tensor_copy(out=gate_bc[:, 1, :], in_=gate_bc[:, 0, :]), reads=[gate_bc], writes=[gate_bc])
        S.stack = st
        if layer == 0:
            gn = S.sb("gn", [128, 2], F32)
            bi_t = S.sb("bi_t", [128, 8, 640], F32)
            S.dma(LQ, gn[:], gains, writes=[gn])
            S.dma(LQ, bi_t[:], bias_i, writes=[bi_t])
        else:
            dm_t = S.sb("dm_t", [128, 4, 128], F32)
            lam_t = S.sb("lam_t", [128, 256], F32)
            sub_t = S.sb("sub_t", [128, 128], F32)
            snk_t = S.sb("snk_t", [128, 8], F32)
            fn_t = S.sb("fn_t", [128, D_MODEL], F32)
            S.dma(LQ, dm_t[:], dmask, writes=[dm_t])
            S.dma(LQ, lam_t[:], lamv, writes=[lam_t])
            S.dma(LQ, sub_t[:], subln, writes=[sub_t])
            S.dma(LQ, snk_t[:], sinks, writes=[snk_t])
            S.dma(LQ, fn_t[:], fnorm, writes=[fn_t])
            lam_s = S.sb("lam_s", [128, 4], F32)
            lprod = S.sb("lprod", [128, 128], F32)
            esnk = S.sb("esnk", [128, 8], F32)
        S.stack = st1
        win = S.sb("win", [128, 8, NCOL], BF16)
        screp = S.sb("screp", [128, 2, 8, 128], F32)
        stage = Rot([S.sb("stage%d" % i, [128, 1024], F32) for i in range(2)])

        S.op("act", lambda e: e.activation(out=cv[:], in_=cv[:], func=AF.Silu), reads=[cv], writes=[cv])
        for w in range(2):
            for k in range(8):
                S.op("act", lambda e: e.activation(out=screp[:, w, k, :], in_=zeros[:], func=AF.Identity,
                                                   bias=cv[:, k, w:w + 1]), reads=[zeros, cv], writes=[screp])
        pmod = banks[5]
        pg = [banks[1], banks[2], banks[3], banks[4]]
        for k in range(8):
            for pi in range(3):
                stg = stage.next()
                S.dma(LQ, stg[:], w_mod[k, :, pi * 1024:(pi + 1) * 1024], writes=[stg])
                for jj in range(8):
                    j = pi * 8 + jj
                    S.op("pe", lambda e: e.matmul(pmod[:, j * 2:j * 2 + 2], lhsT=stg[:, jj * 128:(jj + 1) * 128], rhs=cv[:, k, :],
                                                  start=(k == 0 and j == 0), stop=(k == 7), skip_group_check=True),
                         reads=[stg, cv], writes=[pmod])
                if pi == 2:
                    for w in range(2):
                        for n in range(2):
                            S.op("pe", lambda e: e.matmul(pg[w * 2 + n][:], lhsT=screp[:, w, k, :],
                                                          rhs=stg[:, n * 512:(n + 1) * 512],
                                                          start=(k == 0), stop=(k == 7)),
                                 reads=[stg, screp], writes=[pg[w * 2 + n]])
        for w in range(2):
            S.op("dve", lambda e: e.tensor_tensor(out=modt[:, :, w], in0=pmod[:, 0:48].rearrange("p (j w) -> p j w", w=2)[:, :, w],
                                                  in1=bmodt[:], op=ALU.add), reads=[pmod, bmodt], writes=[modt])
            for n in range(2):
                S.op("dve", lambda e: e.tensor_tensor(out=gate_bc[:, w, n * 512:(n + 1) * 512], in0=pg[w * 2 + n][:],
                                                      in1=gate_bc[:, w, n * 512:(n + 1) * 512], op=ALU.add),
                     reads=[pg[w * 2 + n], gate_bc], writes=[gate_bc])
        S.op("dve", lambda e: e.tensor_scalar_add(out=modt[:, 8:16, :], in0=modt[:, 8:16, :], scalar1=1.0),
             reads=[modt], writes=[modt])

        cnt = 0
        for k in range(8):
            for c0 in range(0, NCOL, 1024):
                cn = min(1024, NCOL - c0)
                stg = stage.next()
                S.dma(LQ, stg[:, 0:cn], w_in[k, :, c0:c0 + cn], writes=[stg])
                S.op("dve", lambda e: e.tensor_copy(out=win[:, k, c0:c0 + cn], in_=stg[:, 0:cn]), reads=[stg], writes=[win])
                cnt += 1

        if layer == 1:
            S.op("dve", lambda e: e.tensor_tensor(out=lprod[:, 0:64], in0=lam_t[:, 0:64], in1=lam_t[:, 64:128], op=ALU.mult), reads=[lam_t], writes=[lprod])
            S.op("dve", lambda e: e.tensor_tensor(out=lprod[:, 64:128], in0=lam_t[:, 128:192], in1=lam_t[:, 192:256], op=ALU.mult), reads=[lam_t, lprod], writes=[lprod])
            S.op("dve", lambda e: e.reduce_sum(out=lam_s[:, 0:2], in_=lprod[:].rearrange("p (a d) -> p a d", a=2), axis=mybir.AxisListType.X), reads=[lprod], writes=[lam_s])
            S.op("act", lambda e: e.activation(out=lam_s[:, 0:2], in_=lam_s[:, 0:2], func=AF.Exp), reads=[lam_s], writes=[lam_s])
            S.op("dve", lambda e: e.tensor_tensor(out=lam_s[:, 2:3], in0=lam_s[:, 0:1], in1=lam_s[:, 1:2], op=ALU.subtract), reads=[lam_s], writes=[lam_s])
            S.op("dve", lambda e: e.tensor_scalar(out=lam_s[:, 3:4], in0=lam_s[:, 2:3], scalar1=lam0, scalar2=-1.0, op0=ALU.add, op1=ALU.mult), reads=[lam_s], writes=[lam_s])
            S.op("act", lambda e: e.activation(out=esnk[:], in_=snk_t[:], func=AF.Exp), reads=[snk_t], writes=[esnk])

        xt_r = Rot([S.sb("xt%d" % i, [128, D_MODEL], F32) for i in range(2)])
        xn_r = Rot([S.sb("xn%d" % i, [128, D_MODEL], BF16) for i in range(2)])
        hT_r = Rot([S.sb("hT%d" % i, [128, 8, 512], BF16) for i in range(2)])
        tabC_r = Rot([S.sb("tabC%d" % i, [128, 512], F32) for i in range(1)])
        tabS_r = Rot([S.sb("tabS%d" % i, [128, 512], F32) for i in range(1)])
        sq_r = Rot([S.sb("sq%d" % i, [128, 512], F32) for i in range(2)])
        rs_r = Rot([S.sb("rs%d" % i, [128, 512], F32) for i in range(2)])
        qn_r = Rot([S.sb("qn%d" % i, [128, 512], F32) for i in range(3)])
        t2_r = Rot([S.sb("t2%d" % i, [128, 512], F32) for i in range(2)])
        fo_r = Rot([S.sb("fo%d" % i, [128, 512], BF16) for i in range(4)])
        vst = {}
        for name in vS:
            h, d = vdims[name]
            vst[name] = Rot([S.sb("vst_%s%d" % (name, i), [128, h, d], BF16) for i in range(2)])
            for t in vst[name].tiles:
                S.op("pool", lambda e: e.memset(t[:], 1.0), writes=[t])
        zst_r = Rot([S.sb("zst%d" % i, [128, D_MODEL], BF16) for i in range(2)])
        bA = Rot([banks[1], banks[2], banks[3]])
        bB = Rot([banks[4], banks[5]])

        def p1_prep(bd):
            u0, ntiles, w = bd["u0"], bd["ntiles"], bd["w"]
            hT = hT_r.next()
            bd["hT"] = hT
            for ti in range(ntiles):
                xt = xt_r.next()
                ss = ss_r.next()
                xn = xn_r.next()
                load_x_tile(xt, u0 // 128 + ti)
                S.op("act", lambda e: e.activation(out=junk[:], in_=xt[:], func=AF.Square, accum_out=ss[:, 0:1]), reads=[xt], writes=[junk, ss])
                S.op("act", lambda e: e.activation(out=ss[:, 1:2], in_=ss[:, 0:1], func=AF.Ln, scale=1.0 / D_MODEL, bias=epst[:]), reads=[ss, epst], writes=[ss])
                S.op("act", lambda e: e.activation(out=ss[:, 1:2], in_=ss[:, 1:2], func=AF.Exp, scale=-0.5), reads=[ss], writes=[ss])
                S.op("act", lambda e: e.activation(out=xn[:], in_=xt[:], func=AF.Copy, scale=ss[:, 1:2]), reads=[xt, ss], writes=[xn])
                for k in range(8):
                    S.op("pe", lambda e: e.transpose(out=pTb[:, k, :], in_=xn[:, k * 128:(k + 1) * 128], identity=ident[:]),
                         reads=[xn, ident], writes=[pTb_t])
                for k in range(8):
                    S.op("dve", lambda e: e.tensor_scalar(out=hT[:, k, ti * 128:(ti + 1) * 128], in0=pTb[:, k, :],
                                                          scalar1=modt[:, 8 + k, w:w + 1], scalar2=modt[:, k, w:w + 1],
                                                          op0=ALU.mult, op1=ALU.add), reads=[pTb_t, modt], writes=[hT])
                yield

        def p1_mm(bd):
            u0, ntiles, fm_list, tm_list, rope_col0, hT = bd["u0"], bd["ntiles"], bd["fm_list"], bd["tm_list"], bd["rope_col0"], bd["hT"]
            ntok = ntiles * 128
            rope_needed = any(fm[i][3] for i in fm_list) and rope_col0 is not None
            if rope_needed:
                tC = tabC_r.next()
                tS = tabS_r.next()
                S.dma(LQ, tC[:, 0:ntok], ropeC[:, rope_col0:rope_col0 + ntok], writes=[tC])
                S.dma(LQ, tS[:, 0:ntok], ropeS[:, rope_col0:rope_col0 + ntok], writes=[tS])
            chs = [dict(i=i) for i in fm_list]

            def st_main(ch):
                i = ch["i"]
                pa = bA.next()
                for k in range(8):
                    S.op("pe", lambda e: e.matmul(pa[:, 0:ntok], lhsT=win[:, k, i * 128:(i + 1) * 128], rhs=hT[:, k, 0:ntok],
                                                  start=(k == 0), stop=(k == 7)), reads=[win, hT], writes=[pa])
                ch["pa"] = pa

            def st_norm(ch):
                i = ch["i"]
                name, _, nkind, roped = fm[i]
                pa = ch["pa"]
                fo = fo_r.next()
                ch["fo"] = fo
                do_rope = roped and rope_col0 is not None
                ch["do_rope"] = do_rope
                if nkind is None and not do_rope:
                    S.op("act", lambda e: e.activation(out=fo[:, 0:ntok], in_=pa[:, 0:ntok], func=AF.Copy), reads=[pa], writes=[fo])
                    return
                qn = qn_r.next()
                ch["qn"] = qn
                if nkind is not None:
                    sq = sq_r.next()
                    rs = rs_r.next()
                    pb = bB.next()
                    gcol = 0 if nkind == "q" else 1
                    S.op("act", lambda e: e.activation(out=sq[:, 0:ntok], in_=pa[:, 0:ntok], func=AF.Square), reads=[pa], writes=[sq])
                    S.op("pe", lambda e: e.matmul(pb[:, 0:ntok], lhsT=blk[:], rhs=sq[:, 0:ntok], start=True, stop=True), reads=[blk, sq], writes=[pb])
                    S.op("act", lambda e: e.activation(out=rs[:, 0:ntok], in_=pb[:, 0:ntok], func=AF.Ln, bias=epst[:]), reads=[pb, epst], writes=[rs])
                    S.op("act", lambda e: e.activation(out=rs[:, 0:ntok], in_=rs[:, 0:ntok], func=AF.Exp, scale=-0.5), reads=[rs], writes=[rs])
                    dst = qn if do_rope else fo
                    S.op("dve", lambda e: e.scalar_tensor_tensor(out=dst[:, 0:ntok], in0=pa[:, 0:ntok], scalar=gn[:, gcol:gcol + 1],
                                                                 in1=rs[:, 0:ntok], op0=ALU.mult, op1=ALU.mult),
                         reads=[pa, gn, rs], writes=[dst])
                else:
                    S.op("act", lambda e: e.activation(out=qn[:, 0:ntok], in_=pa[:, 0:ntok], func=AF.Copy), reads=[pa], writes=[qn])

            def st_rope(ch):
                i = ch["i"]
                fo = ch["fo"]
                if ch["do_rope"]:
                    qn = ch["qn"]
                    pb2 = bB.next()
                    t1 = t1_r.next()
                    t2 = t2_r.next()
                    S.op("pe", lambda e: e.matmul(pb2[:, 0:ntok], lhsT=perm[:], rhs=qn[:, 0:ntok], start=True, stop=True), reads=[perm, qn], writes=[pb2])
                    S.op("dve", lambda e: e.tensor_tensor(out=t1[:, 0:ntok], in0=qn[:, 0:ntok], in1=tC[:, 0:ntok], op=ALU.mult), reads=[qn, tC], writes=[t1])
                    S.op("dve", lambda e: e.tensor_tensor(out=t2[:, 0:ntok], in0=pb2[:, 0:ntok], in1=tS[:, 0:ntok], op=ALU.mult), reads=[pb2, tS], writes=[t2])
                    S.op("dve", lambda e: e.tensor_tensor(out=fo[:, 0:ntok], in0=t1[:, 0:ntok], in1=t2[:, 0:ntok], op=ALU.add), reads=[t1, t2], writes=[fo])
                S.dma(SQ, fmS[i, :, u0:u0 + ntok], fo[:, 0:ntok], reads=[fo], writes=[fm_reg], sem_tile=fo)

            nchs = len(chs)
            for step in range(nchs + 2):
                if step < nchs:
                    st_main(chs[step])
                if 0 <= step - 1 < nchs:
                    st_norm(chs[step - 1])
                if 0 <= step - 2 < nchs:
                    st_rope(chs[step - 2])
                yield
            for name in tm_list:
                col0 = tmoff[name]
                ncols = dict((t[0], t[2]) for t in tm)[name]
                for ti in range(ntiles):
                    for n0 in range(0, ncols, 512):
                        nn = min(512, ncols - n0)
                        pa = bA.next()
                        for k in range(8):
                            S.op("pe", lambda e: e.matmul(pa[:, 0:nn], lhsT=hT[:, k, ti * 128:(ti + 1) * 128],
                                                          rhs=win[:, k, col0 + n0:col0 + n0 + nn], start=(k == 0), stop=(k == 7)),
                                 reads=[win, hT], writes=[pa])
                        if name == "z":
                            if n0 == 0:
                                zst = zst_r.next()
                            S.op("act", lambda e: e.activation(out=zst[:, n0:n0 + nn], in_=pa[:, 0:nn], func=AF.Silu), reads=[pa], writes=[zst])
                            if n0 + nn == ncols:
                                S.dma(SQ, zS[u0 + ti * 128:u0 + (ti + 1) * 128, :], zst[:], reads=[zst], writes=[z_reg], sem_tile=zst)
                        else:
                            h, d = vdims[name]
                            dv = d - 1
                            vt = vst[name].next()
                            S.op("dve", lambda e: e.tensor_copy(out=vt[:, :, 0:dv], in_=pa[:, 0:nn].rearrange("p (h d) -> p h d", d=dv)),
                                 reads=[pa], writes=[vt])
                            S.dma(SQ, vS[name][u0 + ti * 128:u0 + (ti + 1) * 128, :], vt[:].rearrange("p h d -> p (h d)"),
                                  reads=[vt], writes=[v_reg], sem_tile=vt)
                        yield


        all_fm = list(range(NFM))
        all_tm = [t[0] for t in tm]
        lk = [fmi[n] for n in cfg["local_k"]]
        dk = [fmi[n] for n in cfg["dense_k"]]
        blocks = [dict(u0=U_C, ntiles=2, w=1, fm_list=all_fm, tm_list=all_tm, rope_col0=None)]
        eblocks = list(range(EXT // 512))
        eblocks = [b for b in eblocks if HALO <= b * 512 < HALO + HALF] + [b for b in eblocks if not (HALO <= b * 512 < HALO + HALF)]
        for b in eblocks:
            u0 = b * 512
            own = HALO <= u0 < HALO + HALF
            if own:
                blocks.append(dict(u0=u0, ntiles=4, w=0, fm_list=all_fm, tm_list=all_tm, rope_col0=u0))
            else:
                blocks.append(dict(u0=u0, ntiles=4, w=0, fm_list=lk, tm_list=cfg["local_v"], rope_col0=u0))
        for b in range(HALF // 512):
            u0 = U_O + b * 512
            blocks.append(dict(u0=u0, ntiles=4, w=0, fm_list=dk, tm_list=cfg["dense_v"], rope_col0=u0))
        for _ in p1_prep(blocks[0]):
            pass
        for bi, bd in enumerate(blocks):
            gm = p1_mm(bd)
            gp = p1_prep(blocks[bi + 1]) if bi + 1 < len(blocks) else None
            nsteps = len(bd["fm_list"]) + 2 + sum(((dict((t[0], t[2]) for t in tm)[nm] + 511) // 512) * bd["ntiles"] for nm in bd["tm_list"])
            ntl = blocks[bi + 1]["ntiles"] if gp is not None else 0
            every = max(1, nsteps // (ntl + 1)) if ntl else 0
            k = 0
            for _ in gm:
                k += 1
                if gp is not None and every and k % every == 0:
                    next(gp, None)
            if gp is not None:
                for _ in gp:
                    pass

        S.barrier()
        st1.close()
        st2 = contextlib.ExitStack()
        S.stack = st2
        wout = S.sb("wout", [128, 8, D_MODEL], BF16)
        stage2 = Rot([S.sb("stage2_%d" % i, [128, D_MODEL], F32) for i in range(2)])
        for k in range(8):
            stg = stage2.next()
            S.dma(LQ, stg[:], w_out[k], writes=[stg])
            S.op("dve", lambda e: e.tensor_copy(out=wout[:, k, :], in_=stg[:]), reads=[stg], writes=[wout])
        bS = Rot([banks[1], banks[2], banks[3]])
        bACC = Rot([banks[5], banks[6], banks[7]])
        pT_r = Rot([S.sb("pT%d" % i, [128, 512], BF16) for i in range(3)])
        sb_r = Rot([S.sb("sbias%d" % i, [128, 512], F32) for i in range(2)])
        nbuf_d = 1 if layer == 0 else 2
        KT_r = Rot([S.sb("KT%d" % i, [128, NKD], BF16) for i in range(nbuf_d)])
        VDW = 130 if layer == 0 else 129
        VD_r = Rot([S.sb("VD%d" % i, [128, NKC, VDW], BF16) for i in range(nbuf_d)])
        q_r = Rot([S.sb("qblk%d" % i, [128, 512], BF16) for i in range(6)])
        NLK = 9
        kl_r = Rot([S.sb("kl%d" % i, [128, NLK * 128], BF16) for i in range(2)])
        VLW = 130
        vl_r = Rot([S.sb("vl%d" % i, [128, NLK, VLW], BF16) for i in range(2)])
        y_t = [S.sb("y%d" % i, [128, D_MODEL], F32) for i in range(4)]
        rec_r = Rot([S.sb("rec%d" % i, [128, 8], F32) for i in range(4)])
        zl_r = Rot([S.sb("zl%d" % i, [128, D_MODEL], BF16) for i in range(1)])
        yb_r = Rot([S.sb("yb%d" % i, [128, D_MODEL], BF16) for i in range(1)])
        yT_r = Rot([S.sb("yT%d" % i, [128, 8, 128], BF16) for i in range(1)])
        xo_r = Rot([S.sb("xo%d" % i, [128, D_MODEL], F32) for i in range(1)])
        res_r = Rot([S.sb("res%d" % i, [128, D_MODEL], F32) for i in range(2)])
        be_r = Rot([S.sb("be%d" % i, [128, 7 * 128], F32) for i in range(2)]) if layer == 0 else None
        dcache = {}

        def load_dense(kname, vname, vc0, vw):
            key = (kname, vname, vc0)
            if dcache.get("key") == key:
                return dcache["KT"], dcache["VD"]
            KT = KT_r.next()
            VD = VD_r.next()
            i = fmi[kname]
            S.dma(LQ, KT[:, 0:HALF], fmS[i, :, U_OWN:U_OWN + HALF], reads=[fm_reg], writes=[KT])
            S.dma(LQ, KT[:, HALF:2 * HALF], fmS[i, :, U_O:U_O + HALF], reads=[fm_reg], writes=[KT])
            S.dma(LQ, KT[:, 2 * HALF:NKD], fmS[i, :, U_C:U_C + CTX], reads=[fm_reg], writes=[KT])
            for (c0, u0, n) in ((0, U_OWN, HALF), (HALF // 128, U_O, HALF), (2 * HALF // 128, U_C, CTX)):
                S.dma(LQ, VD[:, c0:c0 + n // 128, 0:vw], vS[vname][u0:u0 + n, vc0:vc0 + vw].rearrange("(k p) c -> p k c", p=128),
                      reads=[v_reg], writes=[VD])
            dcache.update(key=key, KT=KT, VD=VD)
            return KT, VD

        def attend(qt, pbase, NQ, kchunks, acc_list, vwidth, bias_fn=None):
            nq = NQ // 128
            n = len(kchunks)
            pend = []
            first = {}

            def issue_s(j):
                kt, kc0, vt, vap = kchunks[j]
                ps = bS.next()
                S.op("pe", lambda e: e.matmul(ps[:, 0:NQ], lhsT=kt[pbase:pbase + 64, kc0:kc0 + 128], rhs=qt[pbase:pbase + 64, 0:NQ],
                                              start=True, stop=True), reads=[kt, qt], writes=[ps])
                pT = pT_r.next()
                b = bias_fn(j) if bias_fn is not None else None
                if b is not None:
                    btile, bap = b
                    sb = sb_r.next()
                    S.op("dve", lambda e: e.scalar_tensor_tensor(out=sb[:, 0:NQ], in0=ps[:, 0:NQ], scalar=SCALE, in1=bap,
                                                                 op0=ALU.mult, op1=ALU.add), reads=[ps, btile], writes=[sb])
                    S.op("act", lambda e: e.activation(out=pT[:, 0:NQ], in_=sb[:, 0:NQ], func=AF.Exp), reads=[sb], writes=[pT])
                else:
                    S.op("act", lambda e: e.activation(out=pT[:, 0:NQ], in_=ps[:, 0:NQ], func=AF.Exp, scale=SCALE), reads=[ps], writes=[pT])
                return pT

            def issue_pv(j, pT):
                kt, kc0, vt, vap = kchunks[j]
                for s in range(nq):
                    acc, c0 = acc_list[s]
                    fst = first.get(id(acc), True)
                    first[id(acc)] = False
                    S.op("pe", lambda e: e.matmul(acc[:, c0:c0 + vwidth], lhsT=pT[:, s * 128:(s + 1) * 128], rhs=vap,
                                                  start=(j == 0 and fst), stop=(j == n - 1), skip_group_check=True),
                         reads=[pT, vt], writes=[acc])

            prev = None
            for j in range(n):
                pT = issue_s(j)
                if prev is not None:
                    issue_pv(prev[0], prev[1])
                prev = (j, pT)
            issue_pv(prev[0], prev[1])

        onesel_f = S.sb("onesel_f", [128, 2, 2], F32)
        S.op("pool", lambda e: e.memset(onesel_f[:], 0.0), writes=[onesel_f])
        S.op("pool", lambda e: e.memset(onesel_f[:, 0, 0:1], 1.0), writes=[onesel_f])
        S.op("pool", lambda e: e.memset(onesel_f[:, 1, 1:2], 1.0), writes=[onesel_f])
        den_acc = S.sb("den_acc", [128, 1024], F32)
        if layer == 1:
            dmb = S.sb("dmb", [128, 3, 384], F32)
            S.op("pool", lambda e: e.memset(dmb[:], 0.0), writes=[dmb])
            for var, (pi, ni) in enumerate(((2, 3), (0, 3), (2, 1))):
                S.op("dve", lambda e: e.tensor_copy(out=dmb[:, var, 0:128], in_=dm_t[:, pi, :]), reads=[dm_t], writes=[dmb])
                S.op("dve", lambda e: e.tensor_copy(out=dmb[:, var, 256:384], in_=dm_t[:, ni, :]), reads=[dm_t], writes=[dmb])
        rec16_r = Rot([S.sb("rec16_%d" % i, [128, 16], F32) for i in range(2)])
        Tq = S.sb("Tq", [128, 4, 256], F32)
        pT2_r = Rot([S.sb("pT2_%d" % i, [128, 1024], BF16) for i in range(3)])
        oT_r = Rot([S.sb("oT%d" % i, [128, 512], F32) for i in range(2)])
        pairs = SH["pairs"]

        def dense_pair(qt, KT, VD, vap_fn, vw, accs, den=None):
            n = NKC

            def issue_s(j):
                pt, ta, tb = pairs[j % 2]
                for m, tt in ((0, ta), (1, tb)):
                    S.op("pe", lambda e: e.matmul(tt[:, 0:512], lhsT=KT[64 * m:64 * m + 64, j * 128:(j + 1) * 128],
                                                  rhs=qt[64 * m:64 * m + 64, 0:512], start=True, stop=True), reads=[KT, qt], writes=[tt])
                pT = pT2_r.next()
                S.op("act", lambda e: e.activation(out=pT[:], in_=pt[:, :], func=AF.Exp, scale=SCALE), reads=[ta, tb], writes=[pT])
                return pT

            def issue_pv(j, pT):
                for m in range(2):
                    S.op("pe", lambda e: e.matmul(accs[m][0:vw, 0:512], lhsT=vap_fn(j, m), rhs=pT[:, m * 512:(m + 1) * 512],
                                                  start=(j == 0), stop=(j == n - 1)), reads=[pT, VD], writes=[accs[m]])
                if den is not None:
                    if j == 0:
                        S.op("dve", lambda e: e.tensor_copy(out=den_acc[:], in_=pT[:]), reads=[pT], writes=[den_acc])
                    else:
                        S.op("dve", lambda e: e.tensor_tensor(out=den_acc[:], in0=den_acc[:], in1=pT[:], op=ALU.add), reads=[pT, den_acc], writes=[den_acc])
                    if j == n - 1:
                        for m in range(2):
                            S.op("pe", lambda e: e.matmul(den[0:2, 0:512], lhsT=onesel_f[:, m, :], rhs=den_acc[:, m * 512:(m + 1) * 512],
                                                          start=(m == 0), stop=(m == 1)), reads=[den_acc, onesel_f], writes=[den])

            prev = None
            for j in range(n):
                pT = issue_s(j)
                if prev is not None:
                    issue_pv(prev[0], prev[1])
                prev = (j, pT)
            issue_pv(prev[0], prev[1])

        def untranspose(acc, rows, fin, width):
            oT = oT_r.next()
            S.op("act", lambda e: e.activation(out=oT[0:rows, :], in_=acc[0:rows, 0:512], func=AF.Copy), reads=[acc], writes=[oT])
            for sidx in range(4):
                S.op("pe", lambda e: e.transpose(out=fin[:, sidx * width:sidx * width + rows], in_=oT[0:rows, sidx * 128:(sidx + 1) * 128],
                                                 identity=ident_f[0:rows, 0:rows]), reads=[oT, ident_f], writes=[fin])

        wide_i = [0]

        def wide_a(job):
            qt, pbase, kl, nch, nb, bias = job["qt"], job["pbase"], job["kl"], job["nch"], job["nb"], job["bias"]
            pt, ta, tb = pairs[wide_i[0] % 2]
            wide_i[0] += 1
            for j in range(nch):
                tt = ta if j < 4 else tb
                S.op("pe", lambda e: e.matmul(pt[:, j * 128:(j + 1) * 128], lhsT=kl[pbase:pbase + 64, j * 128:(j + 1) * 128],
                                              rhs=qt[pbase:pbase + 64, 0:128], start=True, stop=True), reads=[kl, qt], writes=[tt])
            pT = pT2_r.next()
            used = [ta] + ([tb] if nch > 4 else [])
            if bias is not None:
                btile, bap = bias
                sbw = stage2.next()
                S.op("dve", lambda e: e.scalar_tensor_tensor(out=sbw[:, 0:nb * 128], in0=pt[:, 0:nb * 128], scalar=SCALE, in1=bap,
                                                             op0=ALU.mult, op1=ALU.add), reads=used + [btile], writes=[sbw])
                S.op("act", lambda e: e.activation(out=pT[:, 0:nb * 128], in_=sbw[:, 0:nb * 128], func=AF.Exp), reads=[sbw], writes=[pT])
                if nch > nb:
                    S.op("act", lambda e: e.activation(out=pT[:, nb * 128:nch * 128], in_=pt[:, nb * 128:nch * 128], func=AF.Exp, scale=SCALE),
                         reads=used, writes=[pT])
            else:
                S.op("act", lambda e: e.activation(out=pT[:, 0:nch * 128], in_=pt[:, 0:nch * 128], func=AF.Exp, scale=SCALE), reads=used, writes=[pT])
            job["pT"] = pT

        def wide_b(job):
            pT, vl, vc0, nch = job["pT"], job["vl"], job["vc0"], job["nch"]
            acc = bACC.next()
            for j in range(nch):
                S.op("pe", lambda e: e.matmul(acc[:, 0:65], lhsT=pT[:, j * 128:(j + 1) * 128], rhs=vl[:, j, vc0:vc0 + 65],
                                              start=(j == 0), stop=(j == nch - 1)), reads=[pT, vl], writes=[acc])
            finish_head([(acc, 0)], 65, [job["yt"]], job["ycol"], extra_den=job.get("extra_den"))

        def run_wide(jobs):
            prev = None
            for job in jobs:
                wide_a(job)
                if prev is not None:
                    wide_b(prev)
                prev = job
            if prev is not None:
                wide_b(prev)

        def finish_head(acc_list, vwidth, y_tiles, ycol, extra_den=None, scale_ap=None):
            dv = vwidth - 1
            for s, (acc, c0) in enumerate(acc_list):
                rec = rec_r.next()
                if extra_den is not None:
                    S.op("dve", lambda e: e.tensor_tensor(out=rec[:, 0:1], in0=acc[:, c0 + dv:c0 + dv + 1], in1=extra_den, op=ALU.add),
                         reads=[acc, esnk], writes=[rec])
                    S.op("dve", lambda e: e.reciprocal(out=rec[:, 1:2], in_=rec[:, 0:1]), reads=[rec], writes=[rec])
                else:
                    S.op("dve", lambda e: e.reciprocal(out=rec[:, 1:2], in_=acc[:, c0 + dv:c0 + dv + 1]), reads=[acc], writes=[rec])
                yt = y_tiles[s]
                S.op("act", lambda e: e.activation(out=yt[:, ycol:ycol + dv], in_=acc[:, c0:c0 + dv], func=AF.Copy, scale=rec[:, 1:2]),
                     reads=[acc, rec], writes=[yt])

        def out_tile(yt, u_tok, src, src_row, w, dst, dst_row):
            zl = zl_r.next()
            yb = yb_r.next()
            yT = yT_r.next()
            xo = xo_r.next()
            res = res_r.next()
            S.dma(LQ, zl[:], zS[u_tok:u_tok + 128, :], reads=[z_reg], writes=[zl])
            load_x_tile(xo, u_tok // 128)
            S.op("dve", lambda e: e.tensor_tensor(out=yb[:], in0=yt[:], in1=zl[:], op=ALU.mult), reads=[yt, zl], writes=[yb])
            for k in range(8):
                S.op("pe", lambda e: e.transpose(out=pTb[:, k, :], in_=yb[:, k * 128:(k + 1) * 128], identity=ident[:]),
                     reads=[yb, ident], writes=[pTb_t])
            S.op("act", lambda e: e.activation(out=yT[:].rearrange("p k t -> p (k t)"), in_=pTb[:].rearrange("p k t -> p (k t)"), func=AF.Copy),
                 reads=[pTb_t], writes=[yT])
            for n in range(2):
                po = bACC.next()
                for k in range(8):
                    S.op("pe", lambda e: e.matmul(po[:], lhsT=yT[:, k, :], rhs=wout[:, k, n * 512:(n + 1) * 512], start=(k == 0), stop=(k == 7)),
                         reads=[yT, wout], writes=[po])
                S.op("dve", lambda e: e.tensor_tensor(out=res[:, n * 512:(n + 1) * 512], in0=po[:], in1=gate_bc[:, w, n * 512:(n + 1) * 512], op=ALU.mult),
                     reads=[po, gate_bc], writes=[res])
            S.op("dve", lambda e: e.tensor_tensor(out=res[:], in0=res[:], in1=xo[:], op=ALU.add), reads=[res, xo], writes=[res])
            if last:
                ss = ss_r.next()
                S.op("act", lambda e: e.activation(out=junk[:], in_=res[:], func=AF.Square, accum_out=ss[:, 0:1]), reads=[res], writes=[junk, ss])
                S.op("act", lambda e: e.activation(out=ss[:, 1:2], in_=ss[:, 0:1], func=AF.Ln, scale=1.0 / D_MODEL, bias=epst[:]), reads=[ss, epst], writes=[ss])
                S.op("act", lambda e: e.activation(out=ss[:, 1:2], in_=ss[:, 1:2], func=AF.Exp, scale=-0.5), reads=[ss], writes=[ss])
                S.op("act", lambda e: e.activation(out=xo[:], in_=res[:], func=AF.Copy, scale=ss[:, 1:2]), reads=[res, ss], writes=[xo])
                S.op("dve", lambda e: e.tensor_tensor(out=res[:], in0=xo[:], in1=fn_t[:], op=ALU.mult), reads=[xo, fn_t], writes=[res])
            S.dma(SQ, dst[dst_row:dst_row + 128, :], res[:], reads=[res], sem_tile=res)
            return res

        out_tiles = []

        def load_q(name, u0, n):
            qt = q_r.next()
            S.dma(LQ, qt[:, 0:n], fmS[fmi[name], :, u0:u0 + n], reads=[fm_reg], writes=[qt])
            return qt

        def load_local(knames_idx, vname, vc0, vw, utiles):
            kl = kl_r.next()
            vl = vl_r.next()
            pos = 0
            for (ut0, cnt) in utiles:
                S.dma(LQ, kl[:, pos * 128:(pos + cnt) * 128], fmS[knames_idx, :, ut0 * 128:(ut0 + cnt) * 128], reads=[fm_reg], writes=[kl])
                S.dma(LQ, vl[:, pos:pos + cnt, 0:vw], vS[vname][ut0 * 128:(ut0 + cnt) * 128, vc0:vc0 + vw].rearrange("(k p) c -> p k c", p=128),
                      reads=[v_reg], writes=[vl])
                pos += cnt
            return kl, vl

        UC_T = U_C // 128

        if layer == 0:
            for t in range(2):
                yt = y_t[t]
                u0 = U_C + t * 128
                jobs = []
                kl, vl = load_local(fmi["ka"], "va", 0, 130, [(UC_T, 2)])
                for c in range(4):
                    qt = load_q("qa%d" % c, u0, 128)
                    for s_ in range(2):
                        jobs.append(dict(qt=qt, pbase=64 * s_, kl=kl, vl=vl, vc0=s_ * 65, nch=2, nb=0, bias=None, yt=yt, ycol=(c + 4 * s_) * 64))
                run_wide(jobs)
                for c in range(4):
                    kl, vl = load_local(fmi["kb%d" % c], "vb", c * 130, 130, [(UC_T, 2)])
                    qt = load_q("qb%d" % c, u0, 128)
                    run_wide([dict(qt=qt, pbase=64 * s_, kl=kl, vl=vl, vc0=s_ * 65, nch=2, nb=0, bias=None, yt=yt, ycol=512 + (2 * c + s_) * 64)
                              for s_ in range(2)])
                out_tiles.append(out_tile(yt, u0, xc_in, t * 128, 1, out_c, t * 128))

        for qb in range(NBo):
            u0 = U_OWN + qb * 512
            if layer == 0:
                KT, VD = load_dense("ka", "va", 0, 130)
                for c in range(4):
                    qt = load_q("qa%d" % c, u0, 512)
                    accs = [banks[5], banks[6]]
                    dense_pair(qt, KT, VD, lambda j, m: VD[:, j, m * 65:(m + 1) * 65], 65, accs)
                    for s in range(2):
                        head = c + 4 * s
                        fin = banks[7]
                        untranspose(accs[s], 65, fin, 65)
                        finish_head([(fin, i * 65) for i in range(4)], 65, y_t, head * 64)
            else:
                for h in range(4):
                    KT, VD = load_dense("kc%d" % h, "vc", h * 129, 129)
                    qt = load_q("qc%d" % h, u0, 512)
                    accs = [banks[5], banks[6]]
                    den = banks[7]
                    dense_pair(qt, KT, VD, lambda j, m: VD[:, j, 0:128], 128, accs, den=den)
                    fins = [banks[1], banks[2]]
                    untranspose(accs[0], 128, fins[0], 128)
                    untranspose(accs[1], 128, fins[1], 128)
                    dfin = banks[3]
                    untranspose(den, 2, dfin, 2)
                    o_m = [[(fins[0], i * 128, i * 2 + 0) for i in range(4)], [(fins[1], i * 128, i * 2 + 1) for i in range(4)]]
                    R = rec16_r.next()
                    S.op("dve", lambda e: e.reciprocal(out=R[:, 0:8], in_=dfin[:, 0:8]), reads=[dfin], writes=[R])
                    S.op("dve", lambda e: e.tensor_scalar(out=R[:, 8:12], in0=R[:, 0:8].rearrange("p (s m) -> p s m", m=2)[:, :, 1],
                                                          scalar1=lam_s[:, 3:4], scalar2=None, op0=ALU.mult), reads=[R, lam_s], writes=[R])
                    for s in range(4):
                        a0, c0, d0 = o_m[0][s]
                        a1, c1, d1 = o_m[1][s]
                        S.op("act", lambda e: e.activation(out=Tq[:, s, 0:128], in_=a0[:, c0:c0 + 128], func=AF.Copy, scale=R[:, 2 * s:2 * s + 1]),
                             reads=[a0, R], writes=[Tq])
                        S.op("dve", lambda e: e.scalar_tensor_tensor(out=Tq[:, s, 128:256], in0=a1[:, c1:c1 + 128], scalar=R[:, 8 + s:9 + s], in1=Tq[:, s, 0:128],
                                                                     op0=ALU.mult, op1=ALU.add), reads=[a1, R, Tq], writes=[Tq])
                        S.op("act", lambda e: e.activation(out=Tq[:, s, 0:128], in_=Tq[:, s, 128:256], func=AF.Square, accum_out=R[:, 12 + s:13 + s]),
                             reads=[Tq], writes=[Tq, R])
                    S.op("act", lambda e: e.activation(out=R[:, 12:16], in_=R[:, 12:16], func=AF.Ln, scale=1.0 / 128, bias=epst[:]), reads=[R, epst], writes=[R])
                    S.op("act", lambda e: e.activation(out=R[:, 12:16], in_=R[:, 12:16], func=AF.Exp, scale=-0.5), reads=[R], writes=[R])
                    S.op("dve", lambda e: e.tensor_scalar_mul(out=R[:, 12:16], in0=R[:, 12:16], scalar1=1.0 - lam0), reads=[R], writes=[R])
                    for s in range(4):
                        yt = y_t[s]
                        S.op("dve", lambda e: e.scalar_tensor_tensor(out=yt[:, h * 128:(h + 1) * 128], in0=Tq[:, s, 128:256], scalar=R[:, 12 + s:13 + s], in1=sub_t[:],
                                                                     op0=ALU.mult, op1=ALU.mult), reads=[Tq, R, sub_t], writes=[yt])
            for tl in range(4):
                t = qb * 4 + tl
                ut = U_OWN // 128 + t
                yt = y_t[tl]
                if layer == 0:
                    edge = t < 2 or t >= NTo - 2
                    if edge:
                        et = t if t < 2 else 2 + (t - (NTo - 2))
                        lo, hi = ((-2, 3), (-2, 2), (-2, 2), (-3, 2))[et]
                    else:
                        lo, hi = -2, 2
                    nb = hi - lo + 1
                    runs = [(ut + lo, nb), (UC_T, 2)]
                    for c in range(4):
                        kl, vl = load_local(fmi["kb%d" % c], "vb", c * 130, 130, runs)
                        qt = load_q("qb%d" % c, ut * 128, 128)
                        jobs = []
                        for s_ in range(2):
                            head = 2 * c + s_
                            if edge:
                                be = be_r.next()
                                S.dma(LQ, be[:], bias_e[et, head], writes=[be])
                                bias = (be, be[:, (lo + 3) * 128:(hi + 4) * 128])
                            else:
                                bias = (bi_t, bi_t[:, head, 0:640])
                            jobs.append(dict(qt=qt, pbase=64 * s_, kl=kl, vl=vl, vc0=s_ * 65, nch=nb + 2, nb=nb, bias=bias,
                                             yt=yt, ycol=512 + head * 64))
                        run_wide(jobs)
                else:
                    kl, vl = load_local(fmi["kd"], "vd", 0, 130, [(ut - 1, 3), (UC_T, 2)])
                    var = 1 if t == 0 else (2 if t == NTo - 1 else 0)
                    jobs = []
                    for c in range(4):
                        qt = load_q("qd%d" % c, ut * 128, 128)
                        for s_ in range(2):
                            head = c + 4 * s_
                            jobs.append(dict(qt=qt, pbase=64 * s_, kl=kl, vl=vl, vc0=s_ * 65, nch=5, nb=3, bias=(dmb, dmb[:, var, :]),
                                             yt=yt, ycol=512 + head * 64, extra_den=esnk[:, head:head + 1]))
                    run_wide(jobs)
                out_tiles.append(out_tile(yt, ut * 128, x_u, ut * 128, 0, out_x, t * 128))
        if last:
            S.finish(out_tiles)
        else:
            S.barrier()
        st2.close()
    if not last:
        S.release_dsems()
        S.new_epoch()


def build_fused(SEQ, B):
    HALF = SEQ // 2
    EXT = HALF + 2 * HALO
    nc = bass.Bass("TRN2", target_bir_lowering=False)

    def din(name, shape):
        return nc.dram_tensor(name, list(shape), F32, kind="ExternalInput").ap()

    SH = dict(x_u=din("x_u", [EXT + HALF, D_MODEL]), xc=din("xc", [CTX, D_MODEL]), cvec=din("cvec", [128, 8, 2]),
              ident=din("ident", [128, 128]), blk=din("blk", [128, 128]), perm=din("perm", [128, 128]),
              ropeC=din("ropeC", [128, EXT + HALF]), ropeS=din("ropeS", [128, EXT + HALF]), sel=din("sel", [128, 4]))
    SH["x1_loc"] = nc.dram_tensor("x1_loc", [HALF, D_MODEL], F32).ap()
    SH["xc1_loc"] = nc.dram_tensor("xc1_loc", [CTX, D_MODEL], F32).ap()
    SH["GA"] = nc.dram_tensor("x1_all", [2 * HALF, D_MODEL], F32).ap()
    with contextlib.ExitStack() as st0:
        S = Sched(nc, st0)
        SH["pTb_t"] = S.ps("bankT", [128, 8, 128], BF16)
        pairA = st0.enter_context(nc.psum_tensor("ps_pairA", [128, 1024], F32))
        pairB = st0.enter_context(nc.psum_tensor("ps_pairB", [128, 1024], F32))
        b1 = Tile("bank1", pairA[:, 0:512], excl=True)
        b2 = Tile("bank2", pairA[:, 512:1024], excl=True)
        b3 = Tile("bank3", pairB[:, 0:512], excl=True)
        b4 = Tile("bank4", pairB[:, 512:1024], excl=True)
        SH["pairs"] = [(pairA, b1, b2), (pairB, b3, b4)]
        SH["banks"] = [SH["pTb_t"], b1, b2, b3, b4] + [S.ps("bank%d" % i, [128, 512], F32) for i in range(5, 8)]
        SH["G_reg"] = Tile("G_reg")
        emit_layer(nc, S, SH, 0, SEQ)
        S.sems["cc"] = st0.enter_context(nc.semaphore("s_cc"))
        RC = min(512, HALF)
        for i in range(HALF // RC):
            nc.gpsimd.collective_compute("AllGather", ALU.bypass, replica_groups=[[2 * b, 2 * b + 1] for b in range(B)],
                                         ins=[SH["x1_loc"][i * RC:(i + 1) * RC, :]],
                                         outs=[SH["GA"][i * 2 * RC:(i + 1) * 2 * RC, :]]).then_inc(S.sems["cc"], 1)
        SH["G_reg"].last_w = ("cc", HALF // RC)
        emit_layer(nc, S, SH, 1, SEQ)
    return nc


def rope_tables(pos):
    row = (pos // GRID_W).astype(np.float32)
    col = (pos % GRID_W).astype(np.float32)
    q = HD // 4
    inv = (10000.0 ** (-np.arange(q, dtype=np.float32) / q)).astype(np.float32)
    ar = row[None, :] * inv[:, None]
    ac = col[None, :] * inv[:, None]
    cr, sr, cc, sc = np.cos(ar), np.sin(ar), np.cos(ac), np.sin(ac)
    C = np.concatenate([cr, cr, cc, cc], axis=0)
    Sg = np.concatenate([-sr, sr, -sc, sc], axis=0)
    return (np.concatenate([C, C], 0).astype(np.float32), np.concatenate([Sg, Sg], 0).astype(np.float32))


def nbr_bias(rpb, g, gk, NT):
    rows = NT * 2
    out = np.full((8, 128, 128), NEG, np.float32)
    if gk < 0 or gk >= NT:
        return out
    ql = np.arange(128)
    r = 2 * g + ql // 64
    c = ql % 64
    kr = 2 * gk + ql // 64
    kc = ql % 64
    win_r = min(8, rows)
    rs = np.clip(r - win_r // 2, 0, rows - win_r)
    cs = np.clip(c - 8, 0, GRID_W - 16)
    valid = ((kr[:, None] >= rs[None, :]) & (kr[:, None] < rs[None, :] + win_r)
             & (kc[:, None] >= cs[None, :]) & (kc[:, None] < cs[None, :] + 16))
    di = kr[:, None] - r[None, :] + 7
    dj = kc[:, None] - c[None, :] + 15
    di = np.clip(di, 0, 14)
    dj = np.clip(dj, 0, 30)
    vals = rpb[:, di, dj]
    return np.where(valid[None], vals, np.float32(NEG)).astype(np.float32)


def chunk_rows(w):
    return np.ascontiguousarray(w.reshape(8, 128, w.shape[1]))


def prep_layer_inputs(layer, SEQ, xs, xcs, p):
    B = xs.shape[0]
    HALF = SEQ // 2
    EXT = HALF + 2 * HALO
    NT = SEQ // 128
    NTo = HALF // 128
    cfg = layer_cfg(layer)
    wi = p["w_in_even"][0] if layer == 0 else p["w_in_odd"][0]
    wo = p["w_out_even"][0] if layer == 0 else p["w_out_odd"][0]
    cols = []
    for f in cfg["fm"]:
        cols += f[1]
    for name, c0, n in cfg["tm"]:
        cols += list(range(c0, c0 + n))
    w_in_l = chunk_rows(np.ascontiguousarray(wi[:, cols]))
    w_out_l = chunk_rows(wo)
    w_mod_l = chunk_rows(p["w_mod"][layer])
    b_mod_l = np.ascontiguousarray(p["b_mod"][layer].reshape(24, 128).T)
    bgate = np.ascontiguousarray(np.broadcast_to(p["b_mod"][layer][2048:3072][None, :], (128, D_MODEL)))
    ident = np.eye(128, dtype=np.float32)
    blk = np.zeros((128, 128), np.float32)
    blk[:64, :64] = 1.0 / 64
    blk[64:, 64:] = 1.0 / 64
    perm = np.zeros((128, 128), np.float32)
    for m in range(128):
        k = m + 16 if (m % 32) < 16 else m - 16
        perm[k, m] = 1.0
    maps = []
    for b in range(B):
        for half in range(2):
            T0 = half * HALF
            pos_e = np.arange(T0 - HALO, T0 + HALF + HALO)
            valid_e = (pos_e >= 0) & (pos_e < SEQ)
            x_e = np.zeros((EXT, D_MODEL), np.float32)
            x_e[valid_e] = xs[b, pos_e[valid_e]]
            T1 = (1 - half) * HALF
            pos_o = np.arange(T1, T1 + HALF)
            x_u = np.concatenate([x_e, xs[b, pos_o]], axis=0)
            pos_u = np.concatenate([np.clip(pos_e, 0, SEQ - 1), pos_o])
            C, Sg = rope_tables(pos_u)
            cvec = np.stack([p["c"][b].reshape(8, 128).T, p["c_ctx"].reshape(8, 128).T], axis=-1)
            m = dict(x_u=x_u, xc=np.ascontiguousarray(xcs[b]), cvec=np.ascontiguousarray(cvec), w_mod=w_mod_l, b_mod=b_mod_l,
                     bgate=bgate, w_in=w_in_l, w_out=w_out_l, ident=ident, blk=blk, perm=perm, ropeC=C, ropeS=Sg)
            G0 = T0 // 128
            if layer == 0:
                m["gains"] = np.ascontiguousarray(np.stack([np.tile(p["a_q_norm"][0], 2), np.tile(p["a_k_norm"][0], 2)], axis=-1))
                rpb = p["b_rpb"][0]
                gi = min(max(G0 + 2, 2), NT - 3) if NT >= 6 else 0
                bi = np.stack([nbr_bias(rpb, gi, gi + j, NT) for j in range(-2, 3)], axis=0)
                m["bias_i"] = np.ascontiguousarray(bi.transpose(2, 1, 0, 3).reshape(128, 8, 640))
                ets = [0, 1, NTo - 2, NTo - 1]
                be = np.stack([np.stack([nbr_bias(rpb, G0 + t, G0 + t + j, NT) for j in range(-3, 4)], axis=0) for t in ets], axis=0)
                m["bias_e"] = np.ascontiguousarray(be.transpose(0, 2, 3, 1, 4).reshape(4, 8, 128, 896))
            else:
                a = np.arange(128)
                tri_prev = np.where(a[:, None] >= a[None, :], 0.0, NEG).astype(np.float32)
                tri_next = np.where(a[:, None] <= a[None, :], 0.0, NEG).astype(np.float32)
                full = np.full((128, 128), NEG, np.float32)
                first_prev = full if G0 == 0 else tri_prev
                last_next = full if G0 + NTo == NT else tri_next
                m["dmask"] = np.ascontiguousarray(np.stack([first_prev, last_next, tri_prev, tri_next], axis=1))
                m["lamv"] = np.ascontiguousarray(np.broadcast_to(p["c_lambda"][0].reshape(1, 256), (128, 256)))
                m["subln"] = np.ascontiguousarray(np.broadcast_to((p["c_subln"][0])[None, :], (128, 128)))
                m["sinks"] = np.ascontiguousarray(np.broadcast_to(p["d_sinks"][0][None, :], (128, 8)))
                m["fnorm"] = np.ascontiguousarray(np.broadcast_to(p["final_norm"][None, :], (128, D_MODEL)))
            maps.append(m)
    return maps


def prep_fused_inputs(SEQ, xs, xcs, p):
    m0 = prep_layer_inputs(0, SEQ, xs, xcs, p)
    m1 = prep_layer_inputs(1, SEQ, xs, xcs, p)
    shared = ("x_u", "xc", "cvec", "ident", "blk", "perm", "ropeC", "ropeS")
    maps = []
    for i, (a, b) in enumerate(zip(m0, m1)):
        half = i % 2
        m = {k: a[k] for k in shared}
        for k, v in a.items():
            if k not in shared:
                m["l0_" + k] = v
        for k, v in b.items():
            if k not in shared:
                m["l1_" + k] = v
        sel = np.zeros((128, 4), np.float32)
        sel[:, 0] = 1.0 if half == 1 else 0.0
        sel[:, 1] = 1.0 if half == 0 else 0.0
        sel[:, 2] = 1.0 if half == 1 else 0.0
        sel[:, 3] = 1.0 if half == 0 else 0.0
        m["sel"] = sel
        maps.append(m)
    return maps


def run_fused(SEQ, xs, xcs, p, runner=None):
    B = xs.shape[0]
    key = ("fused", SEQ, B)
    if key not in _NC_CACHE:
        _NC_CACHE[key] = build_fused(SEQ, B)
    nc = _NC_CACHE[key]
    maps = prep_fused_inputs(SEQ, xs, xcs, p)
    if runner is None:
        res = run_bass_kernel_spmd(nc, maps, core_ids=list(range(len(maps)))).results
    else:
        res = runner(nc, maps)
    HALF = SEQ // 2
    xo = np.zeros_like(xs)
    for b in range(B):
        for half in range(2):
            xo[b, half * HALF:(half + 1) * HALF] = res[2 * b + half]["out_x"]
    return xo


_NC_CACHE = {}


def kernel(x, c, ctx, c_ctx, w_mod, b_mod, w_in_even, w_out_even, a_q_norm, a_k_norm, b_rpb,
           w_in_odd, w_out_odd, c_lambda, c_subln, d_sinks, final_norm):
    p = dict(c=np.asarray(c, np.float32), c_ctx=np.asarray(c_ctx, np.float32), w_mod=np.asarray(w_mod, np.float32),
             b_mod=np.asarray(b_mod, np.float32), w_in_even=np.asarray(w_in_even, np.float32),
             w_out_even=np.asarray(w_out_even, np.float32), a_q_norm=np.asarray(a_q_norm, np.float32),
             a_k_norm=np.asarray(a_k_norm, np.float32), b_rpb=np.asarray(b_rpb, np.float32),
             w_in_odd=np.asarray(w_in_odd, np.float32), w_out_odd=np.asarray(w_out_odd, np.float32),
             c_lambda=np.asarray(c_lambda, np.float32), c_subln=np.asarray(c_subln, np.float32),
             d_sinks=np.asarray(d_sinks, np.float32), final_norm=np.asarray(final_norm, np.float32))
    xs = np.asarray(x, np.float32)
    xcs = np.asarray(ctx, np.float32)
    SEQ = xs.shape[1]
    return run_fused(SEQ, xs, xcs, p)
```

```python
import contextlib
import math
import numpy as np
import concourse.bass as bass
import concourse.mybir as mybir
from concourse.bass_utils import run_bass_kernel_spmd

F32 = mybir.dt.float32
BF16 = mybir.dt.bfloat16
AF = mybir.ActivationFunctionType
ALU = mybir.AluOpType

D_MODEL = 1024
CTX = 256
HD = 64
GRID_W = 64
SCALE = HD ** -0.5
EPS = 1e-6
NEG = -30000.0
HALO = 512
LQ = "sp"
SQ = "pool"


class Tile:
    __slots__ = ("name", "t", "last_w", "readers", "dsem", "dcount", "excl")

    def __init__(self, name, t=None, excl=False):
        self.name = name
        self.t = t
        self.last_w = None
        self.readers = {}
        self.dsem = None
        self.dcount = 0
        self.excl = excl

    def __getitem__(self, idx):
        return self.t[idx]


class Sched:
    def __init__(self, nc, stack):
        self.nc = nc
        self.stack = stack
        self.sem_stack = stack
        self.dtiles = []
        self.engs = {}
        self.sems = {}
        for en, e in (("pe", nc.tensor), ("act", nc.scalar), ("dve", nc.vector),
                      ("pool", nc.gpsimd), ("sp", nc.sync)):
            self.sems[en] = stack.enter_context(nc.semaphore("s_" + en))
            self.engs[en] = dict(eng=e, count=0, seen={}, key=en)
        self.epoch = 0
        self.nsem = 0
        self.prefix = ""
        self.free_dsems = []

    def sb(self, name, shape, dt):
        return Tile(name, self.stack.enter_context(self.nc.sbuf_tensor("sb_" + self.prefix + name, list(shape), dt)))

    def ps(self, name, shape, dt=F32):
        return Tile(name, self.stack.enter_context(self.nc.psum_tensor("ps_" + name, list(shape), dt)), excl=True)

    def _dsem(self, tile):
        if tile.dsem is None:
            if self.free_dsems:
                key, cnt = self.free_dsems.pop()
                tile.dsem = key
                tile.dcount = cnt
            else:
                key = "d%d" % self.nsem
                self.nsem += 1
                tile.dsem = key
                self.sems[key] = self.sem_stack.enter_context(self.nc.semaphore(key))
            self.dtiles.append(tile)
        return tile.dsem

    def new_epoch(self):
        self.epoch += 1
        for en, E in self.engs.items():
            key = "%s#%d" % (en, self.epoch)
            self.sems[key] = self.sem_stack.enter_context(self.nc.semaphore("s_%s_%d" % (en, self.epoch)))
            E["key"] = key
            E["count"] = 0

    def release_dsems(self):
        for t in self.dtiles:
            self.free_dsems.append((t.dsem, t.dcount))
            t.dsem = None
        self.dtiles = []

    def _wait_deps(self, en, reads, writes):
        E = self.engs[en]
        deps = {}

        def add(ev):
            if ev is None:
                return
            k, v = ev
            if deps.get(k, 0) < v:
                deps[k] = v

        me = E["key"]
        for t in reads:
            add(t.last_w)
            if t.excl:
                for k, v in t.readers.items():
                    if k != me:
                        add((k, v))
        for t in writes:
            add(t.last_w)
            for k, v in t.readers.items():
                if k != me:
                    add((k, v))
        for k, v in deps.items():
            if E["seen"].get(k, 0) < v:
                E["seen"][k] = v
                if k == me and en == "pe":
                    continue
                E["eng"].wait_ge(self.sems[k], v)

    def op(self, en, fn, reads=(), writes=()):
        E = self.engs[en]
        self._wait_deps(en, reads, writes)
        ins = fn(E["eng"])
        E["count"] += 1
        me = E["key"]
        ins.then_inc(self.sems[me], 1)
        for t in reads:
            t.readers[me] = E["count"]
        for t in writes:
            t.last_w = (me, E["count"])
            t.readers = {}
        return ins

    def dma(self, q, out, in_, reads=(), writes=(), sem_tile=None):
        E = self.engs[q]
        self._wait_deps(q, reads, writes)
        st = sem_tile if sem_tile is not None else (list(writes) + list(reads))[0]
        key = self._dsem(st)
        ins = E["eng"].dma_start(out=out, in_=in_)
        st.dcount += 16
        ins.then_inc(self.sems[key], 16)
        for t in reads:
            t.readers[key] = st.dcount
        for t in writes:
            t.last_w = (key, st.dcount)
            t.readers = {}
        return ins

    def barrier(self):
        for en, E in self.engs.items():
            for en2, E2 in self.engs.items():
                k2 = E2["key"]
                if en2 != en and E2["count"] and E["seen"].get(k2, 0) < E2["count"]:
                    E["seen"][k2] = E2["count"]
                    E["eng"].wait_ge(self.sems[k2], E2["count"])
            for t in self.dtiles:
                if t.dcount and E["seen"].get(t.dsem, 0) < t.dcount:
                    E["seen"][t.dsem] = t.dcount
                    E["eng"].wait_ge(self.sems[t.dsem], t.dcount)

    def finish(self, tiles, en="sp"):
        self._wait_deps(en, tiles, tiles)


class Rot:
    def __init__(self, tiles):
        self.tiles = tiles
        self.i = 0

    def next(self):
        t = self.tiles[self.i % len(self.tiles)]
        self.i += 1
        return t


def layer_cfg(layer):
    if layer == 0:
        qa, ka, va, qb, kb, vb, z = 0, 512, 640, 768, 1280, 1792, 2304
        fm = []
        for c in range(4):
            cols = list(range(qa + c * 64, qa + c * 64 + 64)) + list(range(qa + (4 + c) * 64, qa + (4 + c) * 64 + 64))
            fm.append(("qa%d" % c, cols, "q", True))
        fm.append(("ka", list(range(ka, ka + 128)), "k", True))
        for c in range(4):
            fm.append(("qb%d" % c, list(range(qb + c * 128, qb + c * 128 + 128)), None, False))
        for c in range(4):
            fm.append(("kb%d" % c, list(range(kb + c * 128, kb + c * 128 + 128)), None, False))
        tm = [("va", va, 128), ("vb", vb, 512), ("z", z, 1024)]
        dense_k, dense_v = ["ka"], ["va"]
        local_k, local_v = ["kb0", "kb1", "kb2", "kb3"], ["vb"]
    else:
        qc, kc, vc, qd, kd, vd, z = 0, 512, 1024, 1536, 2048, 2176, 2304
        fm = []
        for c in range(4):
            fm.append(("qc%d" % c, list(range(qc + c * 128, qc + c * 128 + 128)), None, True))
        for c in range(4):
            fm.append(("kc%d" % c, list(range(kc + c * 128, kc + c * 128 + 128)), None, True))
        for c in range(4):
            cols = list(range(qd + c * 64, qd + c * 64 + 64)) + list(range(qd + (4 + c) * 64, qd + (4 + c) * 64 + 64))
            fm.append(("qd%d" % c, cols, None, True))
        fm.append(("kd", list(range(kd, kd + 128)), None, True))
        tm = [("vc", vc, 512), ("vd", vd, 128), ("z", z, 1024)]
        dense_k, dense_v = ["kc0", "kc1", "kc2", "kc3"], ["vc"]
        local_k, local_v = ["kd"], ["vd"]
    return dict(fm=fm, tm=tm, dense_k=dense_k, dense_v=dense_v, local_k=local_k, local_v=local_v)


def lambda_init(layer):
    return 0.8 - 0.6 * math.exp(-0.3 * layer)


def emit_layer(nc, S, SH, layer, SEQ):
    HALF = SEQ // 2
    EXT = HALF + 2 * HALO
    NU = EXT + HALF + CTX
    NTo = HALF // 128
    NBo = HALF // 512
    U_OWN = HALO
    U_O = EXT
    U_C = EXT + HALF
    NKD = 2 * HALF + CTX
    NKC = NKD // 128
    last = layer == 1
    cfg = layer_cfg(layer)
    fm, tm = cfg["fm"], cfg["tm"]
    NFM = len(fm)
    fmi = {f[0]: i for i, f in enumerate(fm)}
    NCOL = NFM * 128 + sum(t[2] for t in tm)
    tmoff = {}
    o = NFM * 128
    for name, _, n in tm:
        tmoff[name] = o
        o += n
    lam0 = lambda_init(layer)

    LP = "l%d_" % layer
    S.prefix = LP

    def din(name, shape):
        return nc.dram_tensor(LP + name, list(shape), F32, kind="ExternalInput").ap()

    x_u, xc_in, cvec = SH["x_u"], SH["xc"], SH["cvec"]
    ident_d, blk_d, perm_d, ropeC, ropeS = SH["ident"], SH["blk"], SH["perm"], SH["ropeC"], SH["ropeS"]
    x1_loc, xc1_loc, GA = SH["x1_loc"], SH["xc1_loc"], SH["GA"]
    w_mod = din("w_mod", [8, 128, 3072])
    b_mod = din("b_mod", [128, 24])
    bgate = din("bgate", [128, D_MODEL])
    w_in = din("w_in", [8, 128, NCOL])
    w_out = din("w_out", [8, 128, D_MODEL])
    if layer == 0:
        gains = din("gains", [128, 2])
        bias_i = din("bias_i", [128, 8, 5 * 128])
        bias_e = din("bias_e", [4, 8, 128, 7 * 128])
    else:
        dmask = din("dmask", [128, 4, 128])
        lamv = din("lamv", [128, 256])
        subln = din("subln", [128, 128])
        sinks = din("sinks", [128, 8])
        fnorm = din("fnorm", [128, D_MODEL])
    if last:
        out_x = nc.dram_tensor("out_x", [HALF, D_MODEL], F32, kind="ExternalOutput").ap()
    else:
        out_x = x1_loc
        out_c = xc1_loc

    fmS = nc.dram_tensor(LP + "fmS", [NFM, 128, NU], BF16).ap()
    vdims = {"va": (2, 65), "vb": (8, 65), "vc": (4, 129), "vd": (2, 65)}
    vS = {}
    for name, _, n in tm:
        if name != "z":
            h, d = vdims[name]
            vS[name] = nc.dram_tensor(LP + "vS_" + name, [NU, h * d], BF16).ap()
    zS = nc.dram_tensor(LP + "zS", [NU, D_MODEL], BF16).ap()

    with contextlib.ExitStack() as st:
        S.stack = st
        st1 = contextlib.ExitStack()
        fm_reg = Tile("fm_reg")
        v_reg = Tile("v_reg")
        z_reg = Tile("z_reg")

        ident_f = S.sb("ident_f", [128, 128], F32)
        ident = S.sb("ident", [128, 128], BF16)
        blk = S.sb("blk", [128, 128], F32)
        perm = S.sb("perm", [128, 128], F32)
        epst = S.sb("epst", [128, 1], F32)
        zeros = S.sb("zeros", [128, 128], F32)
        junk = S.sb("junk", [128, D_MODEL], F32)
        ss_r = Rot([S.sb("ss%d" % i, [128, 2], F32) for i in range(2)])
        t1_r = Rot([S.sb("t1%d" % i, [128, 512], F32) for i in range(2)])
        gate_bc = S.sb("gate_bc", [128, 2, D_MODEL], F32)
        modt = S.sb("modt", [128, 24, 2], F32)
        bmodt = S.sb("bmodt", [128, 24], F32)
        cv = S.sb("cv", [128, 8, 2], F32)
        pTb_t = SH["pTb_t"]
        pTb = pTb_t.t
        banks = SH["banks"]
        sel_t = S.sb("sel_t", [128, 4], F32)
        S.dma(LQ, sel_t[:], SH["sel"], writes=[sel_t])
        xt2 = S.sb("xt2", [128, D_MODEL], F32)
        G_reg = SH["G_reg"]
        NTo_ = HALF // 128

        RCt = min(512, HALF) // 128

        def grow(r, t):
            return ((t // RCt) * 2 * RCt + r * RCt + (t % RCt)) * 128

        def load_x_tile(xt, utile):
            e = utile
            if layer == 0:
                if e >= (EXT + HALF) // 128:
                    c = e - (EXT + HALF) // 128
                    S.dma(LQ, xt[:], xc_in[c * 128:(c + 1) * 128, :], writes=[xt])
                else:
                    S.dma(LQ, xt[:], x_u[e * 128:(e + 1) * 128, :], writes=[xt])
                return
            if e >= (EXT + HALF) // 128:
                c = e - (EXT + HALF) // 128
                S.dma(LQ, xt[:], xc1_loc[c * 128:(c + 1) * 128, :], writes=[xt])
            elif e >= EXT // 128:
                o = e - EXT // 128
                S.dma(LQ, xt[:], GA[grow(0, o):grow(0, o) + 128, :], reads=[G_reg], writes=[xt])
                S.dma(LQ, xt2[:], GA[grow(1, o):grow(1, o) + 128, :], reads=[G_reg], writes=[xt2])
                S.op("act", lambda en: en.activation(out=xt[:], in_=xt[:], func=AF.Copy, scale=sel_t[:, 2:3]), reads=[xt, sel_t], writes=[xt])
                S.op("dve", lambda en: en.scalar_tensor_tensor(out=xt[:], in0=xt2[:], scalar=sel_t[:, 3:4], in1=xt[:], op0=ALU.mult, op1=ALU.add),
                     reads=[xt2, sel_t, xt], writes=[xt])
            elif 4 <= e < 4 + NTo_:
                t = e - 4
                S.dma(LQ, xt[:], x1_loc[t * 128:(t + 1) * 128, :], writes=[xt])
            elif e < 4:
                gt = grow(0, NTo_ - 4 + e)
                S.dma(LQ, xt[:], GA[gt:gt + 128, :], reads=[G_reg], writes=[xt])
                S.op("act", lambda en: en.activation(out=xt[:], in_=xt[:], func=AF.Copy, scale=sel_t[:, 0:1]), reads=[xt, sel_t], writes=[xt])
            else:
                gt = grow(1, e - 4 - NTo_)
                S.dma(LQ, xt[:], GA[gt:gt + 128, :], reads=[G_reg], writes=[xt])
                S.op("act", lambda en: en.activation(out=xt[:], in_=xt[:], func=AF.Copy, scale=sel_t[:, 1:2]), reads=[xt, sel_t], writes=[xt])

        S.dma(LQ, ident_f[:], ident_d, writes=[ident_f])
        S.dma(LQ, blk[:], blk_d, writes=[blk])
        S.dma(LQ, perm[:], perm_d, writes=[perm])
        S.dma(LQ, cv[:], cvec, writes=[cv])
        S.dma(LQ, bmodt[:], b_mod, writes=[bmodt])
        S.dma(LQ, gate_bc[:, 0, :], bgate, writes=[gate_bc])
        S.op("dve", lambda e: e.tensor_copy(out=ident[:], in_=ident_f[:]), reads=[ident_f], writes=[ident])
        S.op("pool", lambda e: e.memset(epst[:], EPS), writes=[epst])
        S.op("pool", lambda e: e.memset(zeros[:], 0.0), writes=[zeros])
        S.op("dve", lambda e: e.tensor_copy(out=gate_bc[:, 1, :], in_=gate_bc[:, 0, :]), reads=[gate_bc], writes=[gate_bc])
        S.stack = st
        if layer == 0:
            gn = S.sb("gn", [128, 2], F32)
            bi_t = S.sb("bi_t", [128, 8, 640], F32)
            S.dma(LQ, gn[:], gains, writes=[gn])
            S.dma(LQ, bi_t[:], bias_i, writes=[bi_t])
        else:
            dm_t = S.sb("dm_t", [128, 4, 128], F32)
            lam_t = S.sb("lam_t", [128, 256], F32)
            sub_t = S.sb("sub_t", [128, 128], F32)
            snk_t = S.sb("snk_t", [128, 8], F32)
            fn_t = S.sb("fn_t", [128, D_MODEL], F32)
            S.dma(LQ, dm_t[:], dmask, writes=[dm_t])
            S.dma(LQ, lam_t[:], lamv, writes=[lam_t])
            S.dma(LQ, sub_t[:], subln, writes=[sub_t])
            S.dma(LQ, snk_t[:], sinks, writes=[snk_t])
            S.dma(LQ, fn_t[:], fnorm, writes=[fn_t])
            lam_s = S.sb("lam_s", [128, 4], F32)
            lprod = S.sb("lprod", [128, 128], F32)
            esnk = S.sb("esnk", [128, 8], F32)
        S.stack = st1
        win = S.sb("win", [128, 8, NCOL], BF16)
        screp = S.sb("screp", [128, 2, 8, 128], F32)
        stage = Rot([S.sb("stage%d" % i, [128, 1024], F32) for i in range(2)])

        S.op("act", lambda e: e.activation(out=cv[:], in_=cv[:], func=AF.Silu), reads=[cv], writes=[cv])
        for w in range(2):
            for k in range(8):
                S.op("act", lambda e: e.activation(out=screp[:, w, k, :], in_=zeros[:], func=AF.Identity,
                                                   bias=cv[:, k, w:w + 1]), reads=[zeros, cv], writes=[screp])
        pmod = banks[5]
        pg = [banks[1], banks[2], banks[3], banks[4]]
        for k in range(8):
            for pi in range(3):
                stg = stage.next()
                S.dma(LQ, stg[:], w_mod[k, :, pi * 1024:(pi + 1) * 1024], writes=[stg])
                for jj in range(8):
                    j = pi * 8 + jj
                    S.op("pe", lambda e: e.matmul(pmod[:, j * 2:j * 2 + 2], lhsT=stg[:, jj * 128:(jj + 1) * 128], rhs=cv[:, k, :],
                                                  start=(k == 0 and j == 0), stop=(k == 7), skip_group_check=True),
                         reads=[stg, cv], writes=[pmod])
                if pi == 2:
                    for w in range(2):
                        for n in range(2):
                            S.op("pe", lambda e: e.matmul(pg[w * 2 + n][:], lhsT=screp[:, w, k, :],
                                                          rhs=stg[:, n * 512:(n + 1) * 512],
                                                          start=(k == 0), stop=(k == 7)),
                                 reads=[stg, screp], writes=[pg[w * 2 + n]])
        for w in range(2):
            S.op("dve", lambda e: e.tensor_tensor(out=modt[:, :, w], in0=pmod[:, 0:48].rearrange("p (j w) -> p j w", w=2)[:, :, w],
                                                  in1=bmodt[:], op=ALU.add), reads=[pmod, bmodt], writes=[modt])
            for n in range(2):
                S.op("dve", lambda e: e.tensor_tensor(out=gate_bc[:, w, n * 512:(n + 1) * 512], in0=pg[w * 2 + n][:],
                                                      in1=gate_bc[:, w, n * 512:(n + 1) * 512], op=ALU.add),
                     reads=[pg[w * 2 + n], gate_bc], writes=[gate_bc])
        S.op("dve", lambda e: e.tensor_scalar_add(out=modt[:, 8:16, :], in0=modt[:, 8:16, :], scalar1=1.0),
             reads=[modt], writes=[modt])

        cnt = 0
        for k in range(8):
            for c0 in range(0, NCOL, 1024):
                cn = min(1024, NCOL - c0)
                stg = stage.next()
                S.dma(LQ, stg[:, 0:cn], w_in[k, :, c0:c0 + cn], writes=[stg])
                S.op("dve", lambda e: e.tensor_copy(out=win[:, k, c0:c0 + cn], in_=stg[:, 0:cn]), reads=[stg], writes=[win])
                cnt += 1

        if layer == 1:
            S.op("dve", lambda e: e.tensor_tensor(out=lprod[:, 0:64], in0=lam_t[:, 0:64], in1=lam_t[:, 64:128], op=ALU.mult), reads=[lam_t], writes=[lprod])
            S.op("dve", lambda e: e.tensor_tensor(out=lprod[:, 64:128], in0=lam_t[:, 128:192], in1=lam_t[:, 192:256], op=ALU.mult), reads=[lam_t, lprod], writes=[lprod])
            S.op("dve", lambda e: e.reduce_sum(out=lam_s[:, 0:2], in_=lprod[:].rearrange("p (a d) -> p a d", a=2), axis=mybir.AxisListType.X), reads=[lprod], writes=[lam_s])
            S.op("act", lambda e: e.activation(out=lam_s[:, 0:2], in_=lam_s[:, 0:2], func=AF.Exp), reads=[lam_s], writes=[lam_s])
            S.op("dve", lambda e: e.tensor_tensor(out=lam_s[:, 2:3], in0=lam_s[:, 0:1], in1=lam_s[:, 1:2], op=ALU.subtract), reads=[lam_s], writes=[lam_s])
            S.op("dve", lambda e: e.tensor_scalar(out=lam_s[:, 3:4], in0=lam_s[:, 2:3], scalar1=lam0, scalar2=-1.0, op0=ALU.add, op1=ALU.mult), reads=[lam_s], writes=[lam_s])
            S.op("act", lambda e: e.activation(out=esnk[:], in_=snk_t[:], func=AF.Exp), reads=[snk_t], writes=[esnk])

        xt_r = Rot([S.sb("xt%d" % i, [128, D_MODEL], F32) for i in range(2)])
        xn_r = Rot([S.sb("xn%d" % i, [128, D_MODEL], BF16) for i in range(2)])
        hT_r = Rot([S.sb("hT%d" % i, [128, 8, 512], BF16) for i in range(2)])
        tabC_r = Rot([S.sb("tabC%d" % i, [128, 512], F32) for i in range(1)])
        tabS_r = Rot([S.sb("tabS%d" % i, [128, 512], F32) for i in range(1)])
        sq_r = Rot([S.sb("sq%d" % i, [128, 512], F32) for i in range(2)])
        rs_r = Rot([S.sb("rs%d" % i, [128, 512], F32) for i in range(2)])
        qn_r = Rot([S.sb("qn%d" % i, [128, 512], F32) for i in range(3)])
        t2_r = Rot([S.sb("t2%d" % i, [128, 512], F32) for i in range(2)])
        fo_r = Rot([S.sb("fo%d" % i, [128, 512], BF16) for i in range(4)])
        vst = {}
        for name in vS:
            h, d = vdims[name]
            vst[name] = Rot([S.sb("vst_%s%d" % (name, i), [128, h, d], BF16) for i in range(2)])
            for t in vst[name].tiles:
                S.op("pool", lambda e: e.memset(t[:], 1.0), writes=[t])
        zst_r = Rot([S.sb("zst%d" % i, [128, D_MODEL], BF16) for i in range(2)])
        bA = Rot([banks[1], banks[2], banks[3]])
        bB = Rot([banks[4], banks[5]])

        def p1_prep(bd):
            u0, ntiles, w = bd["u0"], bd["ntiles"], bd["w"]
            hT = hT_r.next()
            bd["hT"] = hT
            for ti in range(ntiles):
                xt = xt_r.next()
                ss = ss_r.next()
                xn = xn_r.next()
                load_x_tile(xt, u0 // 128 + ti)
                S.op("act", lambda e: e.activation(out=junk[:], in_=xt[:], func=AF.Square, accum_out=ss[:, 0:1]), reads=[xt], writes=[junk, ss])
                S.op("act", lambda e: e.activation(out=ss[:, 1:2], in_=ss[:, 0:1], func=AF.Ln, scale=1.0 / D_MODEL, bias=epst[:]), reads=[ss, epst], writes=[ss])
                S.op("act", lambda e: e.activation(out=ss[:, 1:2], in_=ss[:, 1:2], func=AF.Exp, scale=-0.5), reads=[ss], writes=[ss])
                S.op("act", lambda e: e.activation(out=xn[:], in_=xt[:], func=AF.Copy, scale=ss[:, 1:2]), reads=[xt, ss], writes=[xn])
                for k in range(8):
                    S.op("pe", lambda e: e.transpose(out=pTb[:, k, :], in_=xn[:, k * 128:(k + 1) * 128], identity=ident[:]),
                         reads=[xn, ident], writes=[pTb_t])
                for k in range(8):
                    S.op("dve", lambda e: e.tensor_scalar(out=hT[:, k, ti * 128:(ti + 1) * 128], in0=pTb[:, k, :],
                                                          scalar1=modt[:, 8 + k, w:w + 1], scalar2=modt[:, k, w:w + 1],
                                                          op0=ALU.mult, op1=ALU.add), reads=[pTb_t, modt], writes=[hT])
                yield

        def p1_mm(bd):
            u0, ntiles, fm_list, tm_list, rope_col0, hT = bd["u0"], bd["ntiles"], bd["fm_list"], bd["tm_list"], bd["rope_col0"], bd["hT"]
            ntok = ntiles * 128
            rope_needed = any(fm[i][3] for i in fm_list) and rope_col0 is not None
            if rope_needed:
                tC = tabC_r.next()
                tS = tabS_r.next()
                S.dma(LQ, tC[:, 0:ntok], ropeC[:, rope_col0:rope_col0 + ntok], writes=[tC])
                S.dma(LQ, tS[:, 0:ntok], ropeS[:, rope_col0:rope_col0 + ntok], writes=[tS])
            chs = [dict(i=i) for i in fm_list]

            def st_main(ch):
                i = ch["i"]
                pa = bA.next()
                for k in range(8):
                    S.op("pe", lambda e: e.matmul(pa[:, 0:ntok], lhsT=win[:, k, i * 128:(i + 1) * 128], rhs=hT[:, k, 0:ntok],
                                                  start=(k == 0), stop=(k == 7)), reads=[win, hT], writes=[pa])
                ch["pa"] = pa

            def st_norm(ch):
                i = ch["i"]
                name, _, nkind, roped = fm[i]
                pa = ch["pa"]
                fo = fo_r.next()
                ch["fo"] = fo
                do_rope = roped and rope_col0 is not None
                ch["do_rope"] = do_rope
                if nkind is None and not do_rope:
                    S.op("act", lambda e: e.activation(out=fo[:, 0:ntok], in_=pa[:, 0:ntok], func=AF.Copy), reads=[pa], writes=[fo])
                    return
                qn = qn_r.next()
                ch["qn"] = qn
                if nkind is not None:
                    sq = sq_r.next()
                    rs = rs_r.next()
                    pb = bB.next()
                    gcol = 0 if nkind == "q" else 1
                    S.op("act", lambda e: e.activation(out=sq[:, 0:ntok], in_=pa[:, 0:ntok], func=AF.Square), reads=[pa], writes=[sq])
                    S.op("pe", lambda e: e.matmul(pb[:, 0:ntok], lhsT=blk[:], rhs=sq[:, 0:ntok], start=True, stop=True), reads=[blk, sq], writes=[pb])
                    S.op("act", lambda e: e.activation(out=rs[:, 0:ntok], in_=pb[:, 0:ntok], func=AF.Ln, bias=epst[:]), reads=[pb, epst], writes=[rs])
                    S.op("act", lambda e: e.activation(out=rs[:, 0:ntok], in_=rs[:, 0:ntok], func=AF.Exp, scale=-0.5), reads=[rs], writes=[rs])
                    dst = qn if do_rope else fo
                    S.op("dve", lambda e: e.scalar_tensor_tensor(out=dst[:, 0:ntok], in0=pa[:, 0:ntok], scalar=gn[:, gcol:gcol + 1],
                                                                 in1=rs[:, 0:ntok], op0=ALU.mult, op1=ALU.mult),
                         reads=[pa, gn, rs], writes=[dst])
                else:
                    S.op("act", lambda e: e.activation(out=qn[:, 0:ntok], in_=pa[:, 0:ntok], func=AF.Copy), reads=[pa], writes=[qn])

            def st_rope(ch):
                i = ch["i"]
                fo = ch["fo"]
                if ch["do_rope"]:
                    qn = ch["qn"]
                    pb2 = bB.next()
                    t1 = t1_r.next()
                    t2 = t2_r.next()
                    S.op("pe", lambda e: e.matmul(pb2[:, 0:ntok], lhsT=perm[:], rhs=qn[:, 0:ntok], start=True, stop=True), reads=[perm, qn], writes=[pb2])
                    S.op("dve", lambda e: e.tensor_tensor(out=t1[:, 0:ntok], in0=qn[:, 0:ntok], in1=tC[:, 0:ntok], op=ALU.mult), reads=[qn, tC], writes=[t1])
                    S.op("dve", lambda e: e.tensor_tensor(out=t2[:, 0:ntok], in0=pb2[:, 0:ntok], in1=tS[:, 0:ntok], op=ALU.mult), reads=[pb2, tS], writes=[t2])
                    S.op("dve", lambda e: e.tensor_tensor(out=fo[:, 0:ntok], in0=t1[:, 0:ntok], in1=t2[:, 0:ntok], op=ALU.add), reads=[t1, t2], writes=[fo])
                S.dma(SQ, fmS[i, :, u0:u0 + ntok], fo[:, 0:ntok], reads=[fo], writes=[fm_reg], sem_tile=fo)

            nchs = len(chs)
            for step in range(nchs + 2):
                if step < nchs:
                    st_main(chs[step])
                if 0 <= step - 1 < nchs:
                    st_norm(chs[step - 1])
                if 0 <= step - 2 < nchs:
                    st_rope(chs[step - 2])
                yield
            for name in tm_list:
                col0 = tmoff[name]
                ncols = dict((t[0], t[2]) for t in tm)[name]
                for ti in range(ntiles):
                    for n0 in range(0, ncols, 512):
                        nn = min(512, ncols - n0)
                        pa = bA.next()
                        for k in range(8):
                            S.op("pe", lambda e: e.matmul(pa[:, 0:nn], lhsT=hT[:, k, ti * 128:(ti + 1) * 128],
                                                          rhs=win[:, k, col0 + n0:col0 + n0 + nn], start=(k == 0), stop=(k == 7)),
                                 reads=[win, hT], writes=[pa])
                        if name == "z":
                            if n0 == 0:
                                zst = zst_r.next()
                            S.op("act", lambda e: e.activation(out=zst[:, n0:n0 + nn], in_=pa[:, 0:nn], func=AF.Silu), reads=[pa], writes=[zst])
                            if n0 + nn == ncols:
                                S.dma(SQ, zS[u0 + ti * 128:u0 + (ti + 1) * 128, :], zst[:], reads=[zst], writes=[z_reg], sem_tile=zst)
                        else:
                            h, d = vdims[name]
                            dv = d - 1
                            vt = vst[name].next()
                            S.op("dve", lambda e: e.tensor_copy(out=vt[:, :, 0:dv], in_=pa[:, 0:nn].rearrange("p (h d) -> p h d", d=dv)),
                                 reads=[pa], writes=[vt])
                            S.dma(SQ, vS[name][u0 + ti * 128:u0 + (ti + 1) * 128, :], vt[:].rearrange("p h d -> p (h d)"),
                                  reads=[vt], writes=[v_reg], sem_tile=vt)
                        yield


        all_fm = list(range(NFM))
        all_tm = [t[0] for t in tm]
        lk = [fmi[n] for n in cfg["local_k"]]
        dk = [fmi[n] for n in cfg["dense_k"]]
        blocks = [dict(u0=U_C, ntiles=2, w=1, fm_list=all_fm, tm_list=all_tm, rope_col0=None)]
        eblocks = list(range(EXT // 512))
        eblocks = [b for b in eblocks if HALO <= b * 512 < HALO + HALF] + [b for b in eblocks if not (HALO <= b * 512 < HALO + HALF)]
        for b in eblocks:
            u0 = b * 512
            own = HALO <= u0 < HALO + HALF
            if own:
                blocks.append(dict(u0=u0, ntiles=4, w=0, fm_list=all_fm, tm_list=all_tm, rope_col0=u0))
            else:
                blocks.append(dict(u0=u0, ntiles=4, w=0, fm_list=lk, tm_list=cfg["local_v"], rope_col0=u0))
        for b in range(HALF // 512):
            u0 = U_O + b * 512
            blocks.append(dict(u0=u0, ntiles=4, w=0, fm_list=dk, tm_list=cfg["dense_v"], rope_col0=u0))
        for _ in p1_prep(blocks[0]):
            pass
        for bi, bd in enumerate(blocks):
            gm = p1_mm(bd)
            gp = p1_prep(blocks[bi + 1]) if bi + 1 < len(blocks) else None
            nsteps = len(bd["fm_list"]) + 2 + sum(((dict((t[0], t[2]) for t in tm)[nm] + 511) // 512) * bd["ntiles"] for nm in bd["tm_list"])
            ntl = blocks[bi + 1]["ntiles"] if gp is not None else 0
            every = max(1, nsteps // (ntl + 1)) if ntl else 0
            k = 0
            for _ in gm:
                k += 1
                if gp is not None and every and k % every == 0:
                    next(gp, None)
            if gp is not None:
                for _ in gp:
                    pass

        S.barrier()
        st1.close()
        st2 = contextlib.ExitStack()
        S.stack = st2
        wout = S.sb("wout", [128, 8, D_MODEL], BF16)
        stage2 = Rot([S.sb("stage2_%d" % i, [128, D_MODEL], F32) for i in range(2)])
        for k in range(8):
            stg = stage2.next()
            S.dma(LQ, stg[:], w_out[k], writes=[stg])
            S.op("dve", lambda e: e.tensor_copy(out=wout[:, k, :], in_=stg[:]), reads=[stg], writes=[wout])
        bS = Rot([banks[1], banks[2], banks[3]])
        bACC = Rot([banks[5], banks[6], banks[7]])
        nbuf_d = 1 if layer == 0 else 2
        KT_r = Rot([S.sb("KT%d" % i, [128, NKD], BF16) for i in range(nbuf_d)])
        VDW = 130 if layer == 0 else 129
        VD_r = Rot([S.sb("VD%d" % i, [128, NKC, VDW], BF16) for i in range(nbuf_d)])
        q_r = Rot([S.sb("qblk%d" % i, [128, 512], BF16) for i in range(6)])
        NLK = 9
        kl_r = Rot([S.sb("kl%d" % i, [128, NLK * 128], BF16) for i in range(5)])
        VLW = 130
        vl_r = Rot([S.sb("vl%d" % i, [128, NLK, VLW], BF16) for i in range(5)])
        y_t = [S.sb("y%d" % i, [128, D_MODEL], F32) for i in range(4)]
        rec_r = Rot([S.sb("rec%d" % i, [128, 8], F32) for i in range(4)])
        zl_r = Rot([S.sb("zl%d" % i, [128, D_MODEL], BF16) for i in range(1)])
        yb_r = Rot([S.sb("yb%d" % i, [128, D_MODEL], BF16) for i in range(1)])
        yT_r = Rot([S.sb("yT%d" % i, [128, 8, 128], BF16) for i in range(1)])
        xo_r = Rot([S.sb("xo%d" % i, [128, D_MODEL], F32) for i in range(1)])
        res_r = Rot([S.sb("res%d" % i, [128, D_MODEL], F32) for i in range(2)])
        be_r = Rot([S.sb("be%d" % i, [128, 7 * 128], F32) for i in range(3)]) if layer == 0 else None
        dcache = {}

        def load_dense(kname, vname, vc0, vw):
            key = (kname, vname, vc0)
            if dcache.get("key") == key:
                return dcache["KT"], dcache["VD"]
            KT = KT_r.next()
            VD = VD_r.next()
            i = fmi[kname]
            S.dma(LQ, KT[:, 0:HALF], fmS[i, :, U_OWN:U_OWN + HALF], reads=[fm_reg], writes=[KT])
            S.dma(LQ, KT[:, HALF:2 * HALF], fmS[i, :, U_O:U_O + HALF], reads=[fm_reg], writes=[KT])
            S.dma(LQ, KT[:, 2 * HALF:NKD], fmS[i, :, U_C:U_C + CTX], reads=[fm_reg], writes=[KT])
            for (c0, u0, n) in ((0, U_OWN, HALF), (HALF // 128, U_O, HALF), (2 * HALF // 128, U_C, CTX)):
                S.dma(LQ, VD[:, c0:c0 + n // 128, 0:vw], vS[vname][u0:u0 + n, vc0:vc0 + vw].rearrange("(k p) c -> p k c", p=128),
                      reads=[v_reg], writes=[VD])
            dcache.update(key=key, KT=KT, VD=VD)
            return KT, VD

        def attend(qt, pbase, NQ, kchunks, acc_list, vwidth, bias_fn=None):
            nq = NQ // 128
            n = len(kchunks)
            pend = []
            first = {}

            def issue_s(j):
                kt, kc0, vt, vap = kchunks[j]
                ps = bS.next()
                S.op("pe", lambda e: e.matmul(ps[:, 0:NQ], lhsT=kt[pbase:pbase + 64, kc0:kc0 + 128], rhs=qt[pbase:pbase + 64, 0:NQ],
                                              start=True, stop=True), reads=[kt, qt], writes=[ps])
                pT = pT_r.next()
                b = bias_fn(j) if bias_fn is not None else None
                if b is not None:
                    btile, bap = b
                    sb = sb_r.next()
                    S.op("dve", lambda e: e.scalar_tensor_tensor(out=sb[:, 0:NQ], in0=ps[:, 0:NQ], scalar=SCALE, in1=bap,
                                                                 op0=ALU.mult, op1=ALU.add), reads=[ps, btile], writes=[sb])
                    S.op("act", lambda e: e.activation(out=pT[:, 0:NQ], in_=sb[:, 0:NQ], func=AF.Exp), reads=[sb], writes=[pT])
                else:
                    S.op("act", lambda e: e.activation(out=pT[:, 0:NQ], in_=ps[:, 0:NQ], func=AF.Exp, scale=SCALE), reads=[ps], writes=[pT])
                return pT

            def issue_pv(j, pT):
                kt, kc0, vt, vap = kchunks[j]
                for s in range(nq):
                    acc, c0 = acc_list[s]
                    fst = first.get(id(acc), True)
                    first[id(acc)] = False
                    S.op("pe", lambda e: e.matmul(acc[:, c0:c0 + vwidth], lhsT=pT[:, s * 128:(s + 1) * 128], rhs=vap,
                                                  start=(j == 0 and fst), stop=(j == n - 1), skip_group_check=True),
                         reads=[pT, vt], writes=[acc])

            prev = None
            for j in range(n):
                pT = issue_s(j)
                if prev is not None:
                    issue_pv(prev[0], prev[1])
                prev = (j, pT)
            issue_pv(prev[0], prev[1])

        onesel_f = S.sb("onesel_f", [128, 2, 2], F32)
        S.op("pool", lambda e: e.memset(onesel_f[:], 0.0), writes=[onesel_f])
        S.op("pool", lambda e: e.memset(onesel_f[:, 0, 0:1], 1.0), writes=[onesel_f])
        S.op("pool", lambda e: e.memset(onesel_f[:, 1, 1:2], 1.0), writes=[onesel_f])
        den_acc = S.sb("den_acc", [128, 1024], F32) if layer == 1 else None
        if layer == 1:
            dmb = S.sb("dmb", [128, 3, 384], F32)
            S.op("pool", lambda e: e.memset(dmb[:], 0.0), writes=[dmb])
            for var, (pi, ni) in enumerate(((2, 3), (0, 3), (2, 1))):
                S.op("dve", lambda e: e.tensor_copy(out=dmb[:, var, 0:128], in_=dm_t[:, pi, :]), reads=[dm_t], writes=[dmb])
                S.op("dve", lambda e: e.tensor_copy(out=dmb[:, var, 256:384], in_=dm_t[:, ni, :]), reads=[dm_t], writes=[dmb])
        rec16_r = Rot([S.sb("rec16_%d" % i, [128, 16], F32) for i in range(2)])
        Tq = S.sb("Tq", [128, 4, 256], F32) if layer == 1 else None
        pT2_r = Rot([S.sb("pT2_%d" % i, [128, 1024], BF16) for i in range(3)])
        oT_r = Rot([S.sb("oT%d" % i, [128, 512], F32) for i in range(2)])
        pairs = SH["pairs"]

        def dense_pair(qt, KT, VD, vap_fn, vw, accs, den=None):
            n = NKC

            def issue_s(j):
                pt, ta, tb = pairs[j % 2]
                for m, tt in ((0, ta), (1, tb)):
                    S.op("pe", lambda e: e.matmul(tt[:, 0:512], lhsT=KT[64 * m:64 * m + 64, j * 128:(j + 1) * 128],
                                                  rhs=qt[64 * m:64 * m + 64, 0:512], start=True, stop=True), reads=[KT, qt], writes=[tt])
                pT = pT2_r.next()
                S.op("act", lambda e: e.activation(out=pT[:], in_=pt[:, :], func=AF.Exp, scale=SCALE), reads=[ta, tb], writes=[pT])
                return pT

            def issue_pv(j, pT):
                for m in range(2):
                    S.op("pe", lambda e: e.matmul(accs[m][0:vw, 0:512], lhsT=vap_fn(j, m), rhs=pT[:, m * 512:(m + 1) * 512],
                                                  start=(j == 0), stop=(j == n - 1)), reads=[pT, VD], writes=[accs[m]])
                if den is not None:
                    if j == 0:
                        S.op("dve", lambda e: e.tensor_copy(out=den_acc[:], in_=pT[:]), reads=[pT], writes=[den_acc])
                    else:
                        S.op("dve", lambda e: e.tensor_tensor(out=den_acc[:], in0=den_acc[:], in1=pT[:], op=ALU.add), reads=[pT, den_acc], writes=[den_acc])
                    if j == n - 1:
                        for m in range(2):
                            S.op("pe", lambda e: e.matmul(den[0:2, 0:512], lhsT=onesel_f[:, m, :], rhs=den_acc[:, m * 512:(m + 1) * 512],
                                                          start=(m == 0), stop=(m == 1)), reads=[den_acc, onesel_f], writes=[den])

            pend = []
            for j in range(n):
                pend.append((j, issue_s(j)))
                if len(pend) > 2:
                    issue_pv(*pend.pop(0))
            while pend:
                issue_pv(*pend.pop(0))

        def untranspose(acc, rows, fin, width):
            oT = oT_r.next()
            S.op("act", lambda e: e.activation(out=oT[0:rows, :], in_=acc[0:rows, 0:512], func=AF.Copy), reads=[acc], writes=[oT])
            for sidx in range(4):
                S.op("pe", lambda e: e.transpose(out=fin[:, sidx * width:sidx * width + rows], in_=oT[0:rows, sidx * 128:(sidx + 1) * 128],
                                                 identity=ident_f[0:rows, 0:rows]), reads=[oT, ident_f], writes=[fin])

        wide_i = [0]

        def wide_a(job):
            qt, pbase, kl, nch, nb, bias = job["qt"], job["pbase"], job["kl"], job["nch"], job["nb"], job["bias"]
            pt, ta, tb = pairs[wide_i[0] % 2]
            wide_i[0] += 1
            for j in range(nch):
                tt = ta if j < 4 else tb
                S.op("pe", lambda e: e.matmul(pt[:, j * 128:(j + 1) * 128], lhsT=kl[pbase:pbase + 64, j * 128:(j + 1) * 128],
                                              rhs=qt[pbase:pbase + 64, 0:128], start=True, stop=True), reads=[kl, qt], writes=[tt])
            pT = pT2_r.next()
            used = [ta] + ([tb] if nch > 4 else [])
            if bias is not None:
                btile, bap = bias
                sbw = stage2.next()
                S.op("dve", lambda e: e.scalar_tensor_tensor(out=sbw[:, 0:nb * 128], in0=pt[:, 0:nb * 128], scalar=SCALE, in1=bap,
                                                             op0=ALU.mult, op1=ALU.add), reads=used + [btile], writes=[sbw])
                S.op("act", lambda e: e.activation(out=pT[:, 0:nb * 128], in_=sbw[:, 0:nb * 128], func=AF.Exp), reads=[sbw], writes=[pT])
                if nch > nb:
                    S.op("act", lambda e: e.activation(out=pT[:, nb * 128:nch * 128], in_=pt[:, nb * 128:nch * 128], func=AF.Exp, scale=SCALE),
                         reads=used, writes=[pT])
            else:
                S.op("act", lambda e: e.activation(out=pT[:, 0:nch * 128], in_=pt[:, 0:nch * 128], func=AF.Exp, scale=SCALE), reads=used, writes=[pT])
            job["pT"] = pT

        def wide_b(job):
            pT, vl, vc0, nch = job["pT"], job["vl"], job["vc0"], job["nch"]
            acc = bACC.next()
            for j in range(nch):
                S.op("pe", lambda e: e.matmul(acc[:, 0:65], lhsT=pT[:, j * 128:(j + 1) * 128], rhs=vl[:, j, vc0:vc0 + 65],
                                              start=(j == 0), stop=(j == nch - 1)), reads=[pT, vl], writes=[acc])
            finish_head([(acc, 0)], 65, [job["yt"]], job["ycol"], extra_den=job.get("extra_den"))

        def run_wide(jobs):
            pend = []
            for job in jobs:
                if "pre" in job:
                    job["pre"]()
                wide_a(job)
                pend.append(job)
                if len(pend) > 2:
                    wide_b(pend.pop(0))
            while pend:
                wide_b(pend.pop(0))

        def finish_head(acc_list, vwidth, y_tiles, ycol, extra_den=None, scale_ap=None):
            dv = vwidth - 1
            for s, (acc, c0) in enumerate(acc_list):
                rec = rec_r.next()
                if extra_den is not None:
                    S.op("dve", lambda e: e.tensor_tensor(out=rec[:, 0:1], in0=acc[:, c0 + dv:c0 + dv + 1], in1=extra_den, op=ALU.add),
                         reads=[acc, esnk], writes=[rec])
                    S.op("dve", lambda e: e.reciprocal(out=rec[:, 1:2], in_=rec[:, 0:1]), reads=[rec], writes=[rec])
                else:
                    S.op("dve", lambda e: e.reciprocal(out=rec[:, 1:2], in_=acc[:, c0 + dv:c0 + dv + 1]), reads=[acc], writes=[rec])
                yt = y_tiles[s]
                S.op("act", lambda e: e.activation(out=yt[:, ycol:ycol + dv], in_=acc[:, c0:c0 + dv], func=AF.Copy, scale=rec[:, 1:2]),
                     reads=[acc, rec], writes=[yt])

        def out_tile(yt, u_tok, src, src_row, w, dst, dst_row):
            zl = zl_r.next()
            yb = yb_r.next()
            yT = yT_r.next()
            xo = xo_r.next()
            res = res_r.next()
            S.dma(LQ, zl[:], zS[u_tok:u_tok + 128, :], reads=[z_reg], writes=[zl])
            load_x_tile(xo, u_tok // 128)
            S.op("dve", lambda e: e.tensor_tensor(out=yb[:], in0=yt[:], in1=zl[:], op=ALU.mult), reads=[yt, zl], writes=[yb])
            for k in range(8):
                S.op("pe", lambda e: e.transpose(out=pTb[:, k, :], in_=yb[:, k * 128:(k + 1) * 128], identity=ident[:]),
                     reads=[yb, ident], writes=[pTb_t])
            S.op("act", lambda e: e.activation(out=yT[:].rearrange("p k t -> p (k t)"), in_=pTb[:].rearrange("p k t -> p (k t)"), func=AF.Copy),
                 reads=[pTb_t], writes=[yT])
            for n in range(2):
                po = bACC.next()
                for k in range(8):
                    S.op("pe", lambda e: e.matmul(po[:], lhsT=yT[:, k, :], rhs=wout[:, k, n * 512:(n + 1) * 512], start=(k == 0), stop=(k == 7)),
                         reads=[yT, wout], writes=[po])
                S.op("dve", lambda e: e.tensor_tensor(out=res[:, n * 512:(n + 1) * 512], in0=po[:], in1=gate_bc[:, w, n * 512:(n + 1) * 512], op=ALU.mult),
                     reads=[po, gate_bc], writes=[res])
            S.op("dve", lambda e: e.tensor_tensor(out=res[:], in0=res[:], in1=xo[:], op=ALU.add), reads=[res, xo], writes=[res])
            if last:
                ss = ss_r.next()
                S.op("act", lambda e: e.activation(out=junk[:], in_=res[:], func=AF.Square, accum_out=ss[:, 0:1]), reads=[res], writes=[junk, ss])
                S.op("act", lambda e: e.activation(out=ss[:, 1:2], in_=ss[:, 0:1], func=AF.Ln, scale=1.0 / D_MODEL, bias=epst[:]), reads=[ss, epst], writes=[ss])
                S.op("act", lambda e: e.activation(out=ss[:, 1:2], in_=ss[:, 1:2], func=AF.Exp, scale=-0.5), reads=[ss], writes=[ss])
                S.op("act", lambda e: e.activation(out=xo[:], in_=res[:], func=AF.Copy, scale=ss[:, 1:2]), reads=[res, ss], writes=[xo])
                S.op("dve", lambda e: e.tensor_tensor(out=res[:], in0=xo[:], in1=fn_t[:], op=ALU.mult), reads=[xo, fn_t], writes=[res])
            S.dma(SQ, dst[dst_row:dst_row + 128, :], res[:], reads=[res], sem_tile=res)
            return res

        out_tiles = []

        def load_q(name, u0, n):
            qt = q_r.next()
            S.dma(LQ, qt[:, 0:n], fmS[fmi[name], :, u0:u0 + n], reads=[fm_reg], writes=[qt])
            return qt

        def load_local(knames_idx, vname, vc0, vw, utiles):
            kl = kl_r.next()
            vl = vl_r.next()
            pos = 0
            for (ut0, cnt) in utiles:
                S.dma(LQ, kl[:, pos * 128:(pos + cnt) * 128], fmS[knames_idx, :, ut0 * 128:(ut0 + cnt) * 128], reads=[fm_reg], writes=[kl])
                S.dma(LQ, vl[:, pos:pos + cnt, 0:vw], vS[vname][ut0 * 128:(ut0 + cnt) * 128, vc0:vc0 + vw].rearrange("(k p) c -> p k c", p=128),
                      reads=[v_reg], writes=[vl])
                pos += cnt
            return kl, vl

        UC_T = U_C // 128

        if layer == 0:
            for t in range(2):
                yt = y_t[t]
                u0 = U_C + t * 128
                jobs = []
                kl, vl = load_local(fmi["ka"], "va", 0, 130, [(UC_T, 2)])
                for c in range(4):
                    qt = load_q("qa%d" % c, u0, 128)
                    for s_ in range(2):
                        jobs.append(dict(qt=qt, pbase=64 * s_, kl=kl, vl=vl, vc0=s_ * 65, nch=2, nb=0, bias=None, yt=yt, ycol=(c + 4 * s_) * 64))
                run_wide(jobs)
                for c in range(4):
                    kl, vl = load_local(fmi["kb%d" % c], "vb", c * 130, 130, [(UC_T, 2)])
                    qt = load_q("qb%d" % c, u0, 128)
                    run_wide([dict(qt=qt, pbase=64 * s_, kl=kl, vl=vl, vc0=s_ * 65, nch=2, nb=0, bias=None, yt=yt, ycol=512 + (2 * c + s_) * 64)
                              for s_ in range(2)])
                out_tiles.append(out_tile(yt, u0, xc_in, t * 128, 1, out_c, t * 128))

        for qb in range(NBo):
            u0 = U_OWN + qb * 512
            if layer == 0:
                KT, VD = load_dense("ka", "va", 0, 130)
                for c in range(4):
                    qt = load_q("qa%d" % c, u0, 512)
                    accs = [banks[5], banks[6]]
                    dense_pair(qt, KT, VD, lambda j, m: VD[:, j, m * 65:(m + 1) * 65], 65, accs)
                    for s in range(2):
                        head = c + 4 * s
                        fin = banks[7]
                        untranspose(accs[s], 65, fin, 65)
                        finish_head([(fin, i * 65) for i in range(4)], 65, y_t, head * 64)
            else:
                for h in range(4):
                    KT, VD = load_dense("kc%d" % h, "vc", h * 129, 129)
                    qt = load_q("qc%d" % h, u0, 512)
                    accs = [banks[5], banks[6]]
                    den = banks[7]
                    dense_pair(qt, KT, VD, lambda j, m: VD[:, j, 0:128], 128, accs, den=den)
                    fins = [banks[1], banks[2]]
                    untranspose(accs[0], 128, fins[0], 128)
                    untranspose(accs[1], 128, fins[1], 128)
                    dfin = banks[3]
                    untranspose(den, 2, dfin, 2)
                    o_m = [[(fins[0], i * 128, i * 2 + 0) for i in range(4)], [(fins[1], i * 128, i * 2 + 1) for i in range(4)]]
                    R = rec16_r.next()
                    S.op("dve", lambda e: e.reciprocal(out=R[:, 0:8], in_=dfin[:, 0:8]), reads=[dfin], writes=[R])
                    S.op("dve", lambda e: e.tensor_scalar(out=R[:, 8:12], in0=R[:, 0:8].rearrange("p (s m) -> p s m", m=2)[:, :, 1],
                                                          scalar1=lam_s[:, 3:4], scalar2=None, op0=ALU.mult), reads=[R, lam_s], writes=[R])
                    for s in range(4):
                        a0, c0, d0 = o_m[0][s]
                        a1, c1, d1 = o_m[1][s]
                        S.op("act", lambda e: e.activation(out=Tq[:, s, 0:128], in_=a0[:, c0:c0 + 128], func=AF.Copy, scale=R[:, 2 * s:2 * s + 1]),
                             reads=[a0, R], writes=[Tq])
                        S.op("dve", lambda e: e.scalar_tensor_tensor(out=Tq[:, s, 128:256], in0=a1[:, c1:c1 + 128], scalar=R[:, 8 + s:9 + s], in1=Tq[:, s, 0:128],
                                                                     op0=ALU.mult, op1=ALU.add), reads=[a1, R, Tq], writes=[Tq])
                        S.op("act", lambda e: e.activation(out=Tq[:, s, 0:128], in_=Tq[:, s, 128:256], func=AF.Square, accum_out=R[:, 12 + s:13 + s]),
                             reads=[Tq], writes=[Tq, R])
                    S.op("act", lambda e: e.activation(out=R[:, 12:16], in_=R[:, 12:16], func=AF.Ln, scale=1.0 / 128, bias=epst[:]), reads=[R, epst], writes=[R])
                    S.op("act", lambda e: e.activation(out=R[:, 12:16], in_=R[:, 12:16], func=AF.Exp, scale=-0.5), reads=[R], writes=[R])
                    S.op("dve", lambda e: e.tensor_scalar_mul(out=R[:, 12:16], in0=R[:, 12:16], scalar1=1.0 - lam0), reads=[R], writes=[R])
                    for s in range(4):
                        yt = y_t[s]
                        S.op("dve", lambda e: e.scalar_tensor_tensor(out=yt[:, h * 128:(h + 1) * 128], in0=Tq[:, s, 128:256], scalar=R[:, 12 + s:13 + s], in1=sub_t[:],
                                                                     op0=ALU.mult, op1=ALU.mult), reads=[Tq, R, sub_t], writes=[yt])
            for tl in range(4):
                t = qb * 4 + tl
                ut = U_OWN // 128 + t
                yt = y_t[tl]
                if layer == 0:
                    edge = t < 2 or t >= NTo - 2
                    if edge:
                        et = t if t < 2 else 2 + (t - (NTo - 2))
                        lo, hi = ((-2, 3), (-2, 2), (-2, 2), (-3, 2))[et]
                    else:
                        lo, hi = -2, 2
                    nb = hi - lo + 1
                    runs = [(ut + lo, nb), (UC_T, 2)]
                    jobs = []
                    for c in range(4):
                        kl, vl = load_local(fmi["kb%d" % c], "vb", c * 130, 130, runs)
                        qt = load_q("qb%d" % c, ut * 128, 128)
                        for s_ in range(2):
                            head = 2 * c + s_
                            job = dict(qt=qt, pbase=64 * s_, kl=kl, vl=vl, vc0=s_ * 65, nch=nb + 2, nb=nb,
                                       yt=yt, ycol=512 + head * 64)
                            if edge:
                                def pre(job=job, et=et, head=head, lo=lo, hi=hi):
                                    be = be_r.next()
                                    S.dma(LQ, be[:], bias_e[et, head], writes=[be])
                                    job["bias"] = (be, be[:, (lo + 3) * 128:(hi + 4) * 128])
                                job["pre"] = pre
                            else:
                                job["bias"] = (bi_t, bi_t[:, head, 0:640])
                            jobs.append(job)
                    run_wide(jobs)
                else:
                    kl, vl = load_local(fmi["kd"], "vd", 0, 130, [(ut - 1, 3), (UC_T, 2)])
                    var = 1 if t == 0 else (2 if t == NTo - 1 else 0)
                    jobs = []
                    for c in range(4):
                        qt = load_q("qd%d" % c, ut * 128, 128)
                        for s_ in range(2):
                            head = c + 4 * s_
                            jobs.append(dict(qt=qt, pbase=64 * s_, kl=kl, vl=vl, vc0=s_ * 65, nch=5, nb=3, bias=(dmb, dmb[:, var, :]),
                                             yt=yt, ycol=512 + head * 64, extra_den=esnk[:, head:head + 1]))
                    run_wide(jobs)
                out_tiles.append(out_tile(yt, ut * 128, x_u, ut * 128, 0, out_x, t * 128))
        if last:
            S.finish(out_tiles)
        else:
            S.barrier()
        st2.close()
    if not last:
        S.release_dsems()
        S.new_epoch()


def build_fused(SEQ, B):
    HALF = SEQ // 2
    EXT = HALF + 2 * HALO
    nc = bass.Bass("TRN2", target_bir_lowering=False)

    def din(name, shape):
        return nc.dram_tensor(name, list(shape), F32, kind="ExternalInput").ap()

    SH = dict(x_u=din("x_u", [EXT + HALF, D_MODEL]), xc=din("xc", [CTX, D_MODEL]), cvec=din("cvec", [128, 8, 2]),
              ident=din("ident", [128, 128]), blk=din("blk", [128, 128]), perm=din("perm", [128, 128]),
              ropeC=din("ropeC", [128, EXT + HALF]), ropeS=din("ropeS", [128, EXT + HALF]), sel=din("sel", [128, 4]))
    SH["x1_loc"] = nc.dram_tensor("x1_loc", [HALF, D_MODEL], F32).ap()
    SH["xc1_loc"] = nc.dram_tensor("xc1_loc", [CTX, D_MODEL], F32).ap()
    SH["GA"] = nc.dram_tensor("x1_all", [2 * HALF, D_MODEL], F32).ap()
    with contextlib.ExitStack() as st0:
        S = Sched(nc, st0)
        SH["pTb_t"] = S.ps("bankT", [128, 8, 128], BF16)
        pairA = st0.enter_context(nc.psum_tensor("ps_pairA", [128, 1024], F32))
        pairB = st0.enter_context(nc.psum_tensor("ps_pairB", [128, 1024], F32))
        b1 = Tile("bank1", pairA[:, 0:512], excl=True)
        b2 = Tile("bank2", pairA[:, 512:1024], excl=True)
        b3 = Tile("bank3", pairB[:, 0:512], excl=True)
        b4 = Tile("bank4", pairB[:, 512:1024], excl=True)
        SH["pairs"] = [(pairA, b1, b2), (pairB, b3, b4)]
        SH["banks"] = [SH["pTb_t"], b1, b2, b3, b4] + [S.ps("bank%d" % i, [128, 512], F32) for i in range(5, 8)]
        SH["G_reg"] = Tile("G_reg")
        emit_layer(nc, S, SH, 0, SEQ)
        S.sems["cc"] = st0.enter_context(nc.semaphore("s_cc"))
        RC = min(512, HALF)
        for i in range(HALF // RC):
            nc.gpsimd.collective_compute("AllGather", ALU.bypass, replica_groups=[[2 * b, 2 * b + 1] for b in range(B)],
                                         ins=[SH["x1_loc"][i * RC:(i + 1) * RC, :]],
                                         outs=[SH["GA"][i * 2 * RC:(i + 1) * 2 * RC, :]]).then_inc(S.sems["cc"], 1)
        SH["G_reg"].last_w = ("cc", HALF // RC)
        emit_layer(nc, S, SH, 1, SEQ)
    return nc


def rope_tables(pos):
    row = (pos // GRID_W).astype(np.float32)
    col = (pos % GRID_W).astype(np.float32)
    q = HD // 4
    inv = (10000.0 ** (-np.arange(q, dtype=np.float32) / q)).astype(np.float32)
    ar = row[None, :] * inv[:, None]
    ac = col[None, :] * inv[:, None]
    cr, sr, cc, sc = np.cos(ar), np.sin(ar), np.cos(ac), np.sin(ac)
    C = np.concatenate([cr, cr, cc, cc], axis=0)
    Sg = np.concatenate([-sr, sr, -sc, sc], axis=0)
    return (np.concatenate([C, C], 0).astype(np.float32), np.concatenate([Sg, Sg], 0).astype(np.float32))


def nbr_bias(rpb, g, gk, NT):
    rows = NT * 2
    out = np.full((8, 128, 128), NEG, np.float32)
    if gk < 0 or gk >= NT:
        return out
    ql = np.arange(128)
    r = 2 * g + ql // 64
    c = ql % 64
    kr = 2 * gk + ql // 64
    kc = ql % 64
    win_r = min(8, rows)
    rs = np.clip(r - win_r // 2, 0, rows - win_r)
    cs = np.clip(c - 8, 0, GRID_W - 16)
    valid = ((kr[:, None] >= rs[None, :]) & (kr[:, None] < rs[None, :] + win_r)
             & (kc[:, None] >= cs[None, :]) & (kc[:, None] < cs[None, :] + 16))
    di = kr[:, None] - r[None, :] + 7
    dj = kc[:, None] - c[None, :] + 15
    di = np.clip(di, 0, 14)
    dj = np.clip(dj, 0, 30)
    vals = rpb[:, di, dj]
    return np.where(valid[None], vals, np.float32(NEG)).astype(np.float32)


def chunk_rows(w):
    return np.ascontiguousarray(w.reshape(8, 128, w.shape[1]))


def prep_layer_inputs(layer, SEQ, xs, xcs, p):
    B = xs.shape[0]
    HALF = SEQ // 2
    EXT = HALF + 2 * HALO
    NT = SEQ // 128
    NTo = HALF // 128
    cfg = layer_cfg(layer)
    wi = p["w_in_even"][0] if layer == 0 else p["w_in_odd"][0]
    wo = p["w_out_even"][0] if layer == 0 else p["w_out_odd"][0]
    cols = []
    for f in cfg["fm"]:
        cols += f[1]
    for name, c0, n in cfg["tm"]:
        cols += list(range(c0, c0 + n))
    w_in_l = chunk_rows(np.ascontiguousarray(wi[:, cols]))
    w_out_l = chunk_rows(wo)
    w_mod_l = chunk_rows(p["w_mod"][layer])
    b_mod_l = np.ascontiguousarray(p["b_mod"][layer].reshape(24, 128).T)
    bgate = np.ascontiguousarray(np.broadcast_to(p["b_mod"][layer][2048:3072][None, :], (128, D_MODEL)))
    ident = np.eye(128, dtype=np.float32)
    blk = np.zeros((128, 128), np.float32)
    blk[:64, :64] = 1.0 / 64
    blk[64:, 64:] = 1.0 / 64
    perm = np.zeros((128, 128), np.float32)
    for m in range(128):
        k = m + 16 if (m % 32) < 16 else m - 16
        perm[k, m] = 1.0
    maps = []
    for b in range(B):
        for half in range(2):
            T0 = half * HALF
            pos_e = np.arange(T0 - HALO, T0 + HALF + HALO)
            valid_e = (pos_e >= 0) & (pos_e < SEQ)
            x_e = np.zeros((EXT, D_MODEL), np.float32)
            x_e[valid_e] = xs[b, pos_e[valid_e]]
            T1 = (1 - half) * HALF
            pos_o = np.arange(T1, T1 + HALF)
            x_u = np.concatenate([x_e, xs[b, pos_o]], axis=0)
            pos_u = np.concatenate([np.clip(pos_e, 0, SEQ - 1), pos_o])
            C, Sg = rope_tables(pos_u)
            cvec = np.stack([p["c"][b].reshape(8, 128).T, p["c_ctx"].reshape(8, 128).T], axis=-1)
            m = dict(x_u=x_u, xc=np.ascontiguousarray(xcs[b]), cvec=np.ascontiguousarray(cvec), w_mod=w_mod_l, b_mod=b_mod_l,
                     bgate=bgate, w_in=w_in_l, w_out=w_out_l, ident=ident, blk=blk, perm=perm, ropeC=C, ropeS=Sg)
            G0 = T0 // 128
            if layer == 0:
                m["gains"] = np.ascontiguousarray(np.stack([np.tile(p["a_q_norm"][0], 2), np.tile(p["a_k_norm"][0], 2)], axis=-1))
                rpb = p["b_rpb"][0]
                gi = min(max(G0 + 2, 2), NT - 3) if NT >= 6 else 0
                bi = np.stack([nbr_bias(rpb, gi, gi + j, NT) for j in range(-2, 3)], axis=0)
                m["bias_i"] = np.ascontiguousarray(bi.transpose(2, 1, 0, 3).reshape(128, 8, 640))
                ets = [0, 1, NTo - 2, NTo - 1]
                be = np.stack([np.stack([nbr_bias(rpb, G0 + t, G0 + t + j, NT) for j in range(-3, 4)], axis=0) for t in ets], axis=0)
                m["bias_e"] = np.ascontiguousarray(be.transpose(0, 2, 3, 1, 4).reshape(4, 8, 128, 896))
            else:
                a = np.arange(128)
                tri_prev = np.where(a[:, None] >= a[None, :], 0.0, NEG).astype(np.float32)
                tri_next = np.where(a[:, None] <= a[None, :], 0.0, NEG).astype(np.float32)
                full = np.full((128, 128), NEG, np.float32)
                first_prev = full if G0 == 0 else tri_prev
                last_next = full if G0 + NTo == NT else tri_next
                m["dmask"] = np.ascontiguousarray(np.stack([first_prev, last_next, tri_prev, tri_next], axis=1))
                m["lamv"] = np.ascontiguousarray(np.broadcast_to(p["c_lambda"][0].reshape(1, 256), (128, 256)))
                m["subln"] = np.ascontiguousarray(np.broadcast_to((p["c_subln"][0])[None, :], (128, 128)))
                m["sinks"] = np.ascontiguousarray(np.broadcast_to(p["d_sinks"][0][None, :], (128, 8)))
                m["fnorm"] = np.ascontiguousarray(np.broadcast_to(p["final_norm"][None, :], (128, D_MODEL)))
            maps.append(m)
    return maps


def prep_fused_inputs(SEQ, xs, xcs, p):
    m0 = prep_layer_inputs(0, SEQ, xs, xcs, p)
    m1 = prep_layer_inputs(1, SEQ, xs, xcs, p)
    shared = ("x_u", "xc", "cvec", "ident", "blk", "perm", "ropeC", "ropeS")
    maps = []
    for i, (a, b) in enumerate(zip(m0, m1)):
        half = i % 2
        m = {k: a[k] for k in shared}
        for k, v in a.items():
            if k not in shared:
                m["l0_" + k] = v
        for k, v in b.items():
            if k not in shared:
                m["l1_" + k] = v
        sel = np.zeros((128, 4), np.float32)
        sel[:, 0] = 1.0 if half == 1 else 0.0
        sel[:, 1] = 1.0 if half == 0 else 0.0
        sel[:, 2] = 1.0 if half == 1 else 0.0
        sel[:, 3] = 1.0 if half == 0 else 0.0
        m["sel"] = sel
        maps.append(m)
    return maps


def run_fused(SEQ, xs, xcs, p, runner=None):
    B = xs.shape[0]
    key = ("fused", SEQ, B)
    if key not in _NC_CACHE:
        _NC_CACHE[key] = build_fused(SEQ, B)
    nc = _NC_CACHE[key]
    maps = prep_fused_inputs(SEQ, xs, xcs, p)
    if runner is None:
        res = run_bass_kernel_spmd(nc, maps, core_ids=list(range(len(maps)))).results
    else:
        res = runner(nc, maps)
    HALF = SEQ // 2
    xo = np.zeros_like(xs)
    for b in range(B):
        for half in range(2):
            xo[b, half * HALF:(half + 1) * HALF] = res[2 * b + half]["out_x"]
    return xo


_NC_CACHE = {}


def kernel(x, c, ctx, c_ctx, w_mod, b_mod, w_in_even, w_out_even, a_q_norm, a_k_norm, b_rpb,
           w_in_odd, w_out_odd, c_lambda, c_subln, d_sinks, final_norm):
    p = dict(c=np.asarray(c, np.float32), c_ctx=np.asarray(c_ctx, np.float32), w_mod=np.asarray(w_mod, np.float32),
             b_mod=np.asarray(b_mod, np.float32), w_in_even=np.asarray(w_in_even, np.float32),
             w_out_even=np.asarray(w_out_even, np.float32), a_q_norm=np.asarray(a_q_norm, np.float32),
             a_k_norm=np.asarray(a_k_norm, np.float32), b_rpb=np.asarray(b_rpb, np.float32),
             w_in_odd=np.asarray(w_in_odd, np.float32), w_out_odd=np.asarray(w_out_odd, np.float32),
             c_lambda=np.asarray(c_lambda, np.float32), c_subln=np.asarray(c_subln, np.float32),
             d_sinks=np.asarray(d_sinks, np.float32), final_norm=np.asarray(final_norm, np.float32))
    xs = np.asarray(x, np.float32)
    xcs = np.asarray(ctx, np.float32)
    SEQ = xs.shape[1]
    return run_fused(SEQ, xs, xcs, p)
```

```python
import contextlib
import math
import numpy as np
import concourse.bass as bass
import concourse.mybir as mybir
from concourse.bass_utils import run_bass_kernel_spmd

F32 = mybir.dt.float32
BF16 = mybir.dt.bfloat16
AF = mybir.ActivationFunctionType
ALU = mybir.AluOpType

D_MODEL = 1024
CTX = 256
HD = 64
GRID_W = 64
SCALE = HD ** -0.5
EPS = 1e-6
NEG = -30000.0
HALO = 512
LQ = "sp"
SQ = "pool"


class Tile:
    __slots__ = ("name", "t", "last_w", "readers", "dsem", "dcount", "excl")

    def __init__(self, name, t=None, excl=False):
        self.name = name
        self.t = t
        self.last_w = None
        self.readers = {}
        self.dsem = None
        self.dcount = 0
        self.excl = excl

    def __getitem__(self, idx):
        return self.t[idx]


class Sched:
    def __init__(self, nc, stack):
        self.nc = nc
        self.stack = stack
        self.sem_stack = stack
        self.dtiles = []
        self.engs = {}
        self.sems = {}
        for en, e in (("pe", nc.tensor), ("act", nc.scalar), ("dve", nc.vector),
                      ("pool", nc.gpsimd), ("sp", nc.sync)):
            self.sems[en] = stack.enter_context(nc.semaphore("s_" + en))
            self.engs[en] = dict(eng=e, count=0, seen={}, key=en)
        self.epoch = 0
        self.nsem = 0
        self.prefix = ""
        self.free_dsems = []

    def sb(self, name, shape, dt):
        return Tile(name, self.stack.enter_context(self.nc.sbuf_tensor("sb_" + self.prefix + name, list(shape), dt)))

    def ps(self, name, shape, dt=F32):
        return Tile(name, self.stack.enter_context(self.nc.psum_tensor("ps_" + name, list(shape), dt)), excl=True)

    def _dsem(self, tile):
        if tile.dsem is None:
            if self.free_dsems:
                key, cnt = self.free_dsems.pop()
                tile.dsem = key
                tile.dcount = cnt
            else:
                key = "d%d" % self.nsem
                self.nsem += 1
                tile.dsem = key
                self.sems[key] = self.sem_stack.enter_context(self.nc.semaphore(key))
            self.dtiles.append(tile)
        return tile.dsem

    def new_epoch(self):
        self.epoch += 1
        for en, E in self.engs.items():
            key = "%s#%d" % (en, self.epoch)
            self.sems[key] = self.sem_stack.enter_context(self.nc.semaphore("s_%s_%d" % (en, self.epoch)))
            E["key"] = key
            E["count"] = 0

    def release_dsems(self):
        for t in self.dtiles:
            self.free_dsems.append((t.dsem, t.dcount))
            t.dsem = None
        self.dtiles = []

    def _wait_deps(self, en, reads, writes):
        E = self.engs[en]
        deps = {}

        def add(ev):
            if ev is None:
                return
            k, v = ev
            if deps.get(k, 0) < v:
                deps[k] = v

        me = E["key"]
        for t in reads:
            add(t.last_w)
            if t.excl:
                for k, v in t.readers.items():
                    if k != me:
                        add((k, v))
        for t in writes:
            add(t.last_w)
            for k, v in t.readers.items():
                if k != me:
                    add((k, v))
        for k, v in deps.items():
            if E["seen"].get(k, 0) < v:
                E["seen"][k] = v
                if k == me and en == "pe":
                    continue
                E["eng"].wait_ge(self.sems[k], v)

    def op(self, en, fn, reads=(), writes=()):
        E = self.engs[en]
        self._wait_deps(en, reads, writes)
        ins = fn(E["eng"])
        E["count"] += 1
        me = E["key"]
        ins.then_inc(self.sems[me], 1)
        for t in reads:
            t.readers[me] = E["count"]
        for t in writes:
            t.last_w = (me, E["count"])
            t.readers = {}
        return ins

    def dma(self, q, out, in_, reads=(), writes=(), sem_tile=None):
        E = self.engs[q]
        self._wait_deps(q, reads, writes)
        st = sem_tile if sem_tile is not None else (list(writes) + list(reads))[0]
        key = self._dsem(st)
        ins = E["eng"].dma_start(out=out, in_=in_)
        st.dcount += 16
        ins.then_inc(self.sems[key], 16)
        for t in reads:
            t.readers[key] = st.dcount
        for t in writes:
            t.last_w = (key, st.dcount)
            t.readers = {}
        return ins

    def barrier(self):
        for en, E in self.engs.items():
            for en2, E2 in self.engs.items():
                k2 = E2["key"]
                if en2 != en and E2["count"] and E["seen"].get(k2, 0) < E2["count"]:
                    E["seen"][k2] = E2["count"]
                    E["eng"].wait_ge(self.sems[k2], E2["count"])
            for t in self.dtiles:
                if t.dcount and E["seen"].get(t.dsem, 0) < t.dcount:
                    E["seen"][t.dsem] = t.dcount
                    E["eng"].wait_ge(self.sems[t.dsem], t.dcount)

    def finish(self, tiles, en="sp"):
        self._wait_deps(en, tiles, tiles)


class Rot:
    def __init__(self, tiles):
        self.tiles = tiles
        self.i = 0

    def next(self):
        t = self.tiles[self.i % len(self.tiles)]
        self.i += 1
        return t


def layer_cfg(layer):
    if layer == 0:
        qa, ka, va, qb, kb, vb, z = 0, 512, 640, 768, 1280, 1792, 2304
        fm = []
        for c in range(4):
            cols = list(range(qa + c * 64, qa + c * 64 + 64)) + list(range(qa + (4 + c) * 64, qa + (4 + c) * 64 + 64))
            fm.append(("qa%d" % c, cols, "q", True))
        fm.append(("ka", list(range(ka, ka + 128)), "k", True))
        for c in range(4):
            fm.append(("qb%d" % c, list(range(qb + c * 128, qb + c * 128 + 128)), None, False))
        for c in range(4):
            fm.append(("kb%d" % c, list(range(kb + c * 128, kb + c * 128 + 128)), None, False))
        tm = [("va", va, 128), ("vb", vb, 512), ("z", z, 1024)]
        dense_k, dense_v = ["ka"], ["va"]
        local_k, local_v = ["kb0", "kb1", "kb2", "kb3"], ["vb"]
    else:
        qc, kc, vc, qd, kd, vd, z = 0, 512, 1024, 1536, 2048, 2176, 2304
        fm = []
        for c in range(4):
            fm.append(("qc%d" % c, list(range(qc + c * 128, qc + c * 128 + 128)), None, True))
        for c in range(4):
            fm.append(("kc%d" % c, list(range(kc + c * 128, kc + c * 128 + 128)), None, True))
        for c in range(4):
            cols = list(range(qd + c * 64, qd + c * 64 + 64)) + list(range(qd + (4 + c) * 64, qd + (4 + c) * 64 + 64))
            fm.append(("qd%d" % c, cols, None, True))
        fm.append(("kd", list(range(kd, kd + 128)), None, True))
        tm = [("vc", vc, 512), ("vd", vd, 128), ("z", z, 1024)]
        dense_k, dense_v = ["kc0", "kc1", "kc2", "kc3"], ["vc"]
        local_k, local_v = ["kd"], ["vd"]
    return dict(fm=fm, tm=tm, dense_k=dense_k, dense_v=dense_v, local_k=local_k, local_v=local_v)


def lambda_init(layer):
    return 0.8 - 0.6 * math.exp(-0.3 * layer)


def emit_layer(nc, S, SH, layer, SEQ):
    HALF = SEQ // 2
    EXT = HALF + 2 * HALO
    NU = EXT + HALF + CTX
    NTo = HALF // 128
    NBo = HALF // 512
    U_OWN = HALO
    U_O = EXT
    U_C = EXT + HALF
    NKD = 2 * HALF + CTX
    NKC = NKD // 128
    last = layer == 1
    cfg = layer_cfg(layer)
    fm, tm = cfg["fm"], cfg["tm"]
    NFM = len(fm)
    fmi = {f[0]: i for i, f in enumerate(fm)}
    NCOL = NFM * 128 + sum(t[2] for t in tm)
    tmoff = {}
    o = NFM * 128
    for name, _, n in tm:
        tmoff[name] = o
        o += n
    lam0 = lambda_init(layer)

    LP = "l%d_" % layer
    S.prefix = LP

    def din(name, shape):
        return nc.dram_tensor(LP + name, list(shape), F32, kind="ExternalInput").ap()

    x_u, xc_in, cvec = SH["x_u"], SH["xc"], SH["cvec"]
    ident_d, blk_d, perm_d, ropeC, ropeS = SH["ident"], SH["blk"], SH["perm"], SH["ropeC"], SH["ropeS"]
    x1_loc, xc1_loc, GA = SH["x1_loc"], SH["xc1_loc"], SH["GA"]
    w_mod = din("w_mod", [8, 128, 3072])
    b_mod = din("b_mod", [128, 24])
    bgate = din("bgate", [128, D_MODEL])
    w_in = din("w_in", [8, 128, NCOL])
    w_out = din("w_out", [8, 128, D_MODEL])
    if layer == 0:
        gains = din("gains", [128, 2])
        bias_i = din("bias_i", [128, 8, 5 * 128])
        bias_e = din("bias_e", [4, 8, 128, 7 * 128])
    else:
        dmask = din("dmask", [128, 4, 128])
        lamv = din("lamv", [128, 256])
        subln = din("subln", [128, 128])
        sinks = din("sinks", [128, 8])
        fnorm = din("fnorm", [128, D_MODEL])
    if last:
        out_x = nc.dram_tensor("out_x", [HALF, D_MODEL], F32, kind="ExternalOutput").ap()
    else:
        out_x = x1_loc
        out_c = xc1_loc

    fmS = nc.dram_tensor(LP + "fmS", [NFM, 128, NU], BF16).ap()
    vdims = {"va": (2, 65), "vb": (8, 65), "vc": (4, 129), "vd": (2, 65)}
    vS = {}
    for name, _, n in tm:
        if name != "z":
            h, d = vdims[name]
            vS[name] = nc.dram_tensor(LP + "vS_" + name, [NU, h * d], BF16).ap()
    zS = nc.dram_tensor(LP + "zS", [NU, D_MODEL], BF16).ap()

    with contextlib.ExitStack() as st:
        S.stack = st
        st1 = contextlib.ExitStack()
        fm_reg = Tile("fm_reg")
        v_reg = Tile("v_reg")
        z_reg = Tile("z_reg")

        ident_f = S.sb("ident_f", [128, 128], F32)
        ident = S.sb("ident", [128, 128], BF16)
        blk = S.sb("blk", [128, 128], F32)
        perm = S.sb("perm", [128, 128], F32)
        epst = S.sb("epst", [128, 1], F32)
        zeros = S.sb("zeros", [128, 128], F32)
        junk = S.sb("junk", [128, D_MODEL], F32)
        ss_r = Rot([S.sb("ss%d" % i, [128, 2], F32) for i in range(2)])
        t1_r = Rot([S.sb("t1%d" % i, [128, 512], F32) for i in range(2)])
        gate_bc = S.sb("gate_bc", [128, 2, D_MODEL], F32)
        modt = S.sb("modt", [128, 24, 2], F32)
        bmodt = S.sb("bmodt", [128, 24], F32)
        cv = S.sb("cv", [128, 8, 2], F32)
        pTb_t = SH["pTb_t"]
        pTb = pTb_t.t
        banks = SH["banks"]
        sel_t = S.sb("sel_t", [128, 4], F32)
        S.dma(LQ, sel_t[:], SH["sel"], writes=[sel_t])
        xt2 = S.sb("xt2", [128, D_MODEL], F32)
        G_reg = SH["G_reg"]
        NTo_ = HALF // 128

        RCt = min(512, HALF) // 128

        def grow(r, t):
            return ((t // RCt) * 2 * RCt + r * RCt + (t % RCt)) * 128

        def load_x_tile(xt, utile):
            e = utile
            if layer == 0:
                if e >= (EXT + HALF) // 128:
                    c = e - (EXT + HALF) // 128
                    S.dma(LQ, xt[:], xc_in[c * 128:(c + 1) * 128, :], writes=[xt])
                else:
                    S.dma(LQ, xt[:], x_u[e * 128:(e + 1) * 128, :], writes=[xt])
                return
            if e >= (EXT + HALF) // 128:
                c = e - (EXT + HALF) // 128
                S.dma(LQ, xt[:], xc1_loc[c * 128:(c + 1) * 128, :], writes=[xt])
            elif e >= EXT // 128:
                o = e - EXT // 128
                S.dma(LQ, xt[:], GA[grow(0, o):grow(0, o) + 128, :], reads=[G_reg], writes=[xt])
                S.dma(LQ, xt2[:], GA[grow(1, o):grow(1, o) + 128, :], reads=[G_reg], writes=[xt2])
                S.op("act", lambda en: en.activation(out=xt[:], in_=xt[:], func=AF.Copy, scale=sel_t[:, 2:3]), reads=[xt, sel_t], writes=[xt])
                S.op("dve", lambda en: en.scalar_tensor_tensor(out=xt[:], in0=xt2[:], scalar=sel_t[:, 3:4], in1=xt[:], op0=ALU.mult, op1=ALU.add),
                     reads=[xt2, sel_t, xt], writes=[xt])
            elif 4 <= e < 4 + NTo_:
                t = e - 4
                S.dma(LQ, xt[:], x1_loc[t * 128:(t + 1) * 128, :], writes=[xt])
            elif e < 4:
                gt = grow(0, NTo_ - 4 + e)
                S.dma(LQ, xt[:], GA[gt:gt + 128, :], reads=[G_reg], writes=[xt])
                S.op("act", lambda en: en.activation(out=xt[:], in_=xt[:], func=AF.Copy, scale=sel_t[:, 0:1]), reads=[xt, sel_t], writes=[xt])
            else:
                gt = grow(1, e - 4 - NTo_)
                S.dma(LQ, xt[:], GA[gt:gt + 128, :], reads=[G_reg], writes=[xt])
                S.op("act", lambda en: en.activation(out=xt[:], in_=xt[:], func=AF.Copy, scale=sel_t[:, 1:2]), reads=[xt, sel_t], writes=[xt])

        S.dma(LQ, ident_f[:], ident_d, writes=[ident_f])
        S.dma(LQ, blk[:], blk_d, writes=[blk])
        S.dma(LQ, perm[:], perm_d, writes=[perm])
        S.dma(LQ, cv[:], cvec, writes=[cv])
        S.dma(LQ, bmodt[:], b_mod, writes=[bmodt])
        S.dma(LQ, gate_bc[:, 0, :], bgate, writes=[gate_bc])
        S.op("dve", lambda e: e.tensor_copy(out=ident[:], in_=ident_f[:]), reads=[ident_f], writes=[ident])
        S.op("pool", lambda e: e.memset(epst[:], EPS), writes=[epst])
        S.op("pool", lambda e: e.memset(zeros[:], 0.0), writes=[zeros])
        S.op("dve", lambda e: e.tensor_copy(out=gate_bc[:, 1, :], in_=gate_bc[:, 0, :]), reads=[gate_bc], writes=[gate_bc])
        S.stack = st
        if layer == 0:
            gn = S.sb("gn", [128, 2], F32)
            bi_t = S.sb("bi_t", [128, 8, 640], F32)
            S.dma(LQ, gn[:], gains, writes=[gn])
            S.dma(LQ, bi_t[:], bias_i, writes=[bi_t])
        else:
            dm_t = S.sb("dm_t", [128, 4, 128], F32)
            lam_t = S.sb("lam_t", [128, 256], F32)
            sub_t = S.sb("sub_t", [128, 128], F32)
            snk_t = S.sb("snk_t", [128, 8], F32)
            fn_t = S.sb("fn_t", [128, D_MODEL], F32)
            S.dma(LQ, dm_t[:], dmask, writes=[dm_t])
            S.dma(LQ, lam_t[:], lamv, writes=[lam_t])
            S.dma(LQ, sub_t[:], subln, writes=[sub_t])
            S.dma(LQ, snk_t[:], sinks, writes=[snk_t])
            S.dma(LQ, fn_t[:], fnorm, writes=[fn_t])
            lam_s = S.sb("lam_s", [128, 4], F32)
            lprod = S.sb("lprod", [128, 128], F32)
            esnk = S.sb("esnk", [128, 8], F32)
        S.stack = st1
        win = S.sb("win", [128, 8, NCOL], BF16)
        screp = S.sb("screp", [128, 2, 8, 128], F32)
        stage = Rot([S.sb("stage%d" % i, [128, 1024], F32) for i in range(2)])

        S.op("act", lambda e: e.activation(out=cv[:], in_=cv[:], func=AF.Silu), reads=[cv], writes=[cv])
        for w in range(2):
            for k in range(8):
                S.op("act", lambda e: e.activation(out=screp[:, w, k, :], in_=zeros[:], func=AF.Identity,
                                                   bias=cv[:, k, w:w + 1]), reads=[zeros, cv], writes=[screp])
        pmod = banks[5]
        pg = [banks[1], banks[2], banks[3], banks[4]]
        for k in range(8):
            for pi in range(3):
                stg = stage.next()
                S.dma(LQ, stg[:], w_mod[k, :, pi * 1024:(pi + 1) * 1024], writes=[stg])
                for jj in range(8):
                    j = pi * 8 + jj
                    S.op("pe", lambda e: e.matmul(pmod[:, j * 2:j * 2 + 2], lhsT=stg[:, jj * 128:(jj + 1) * 128], rhs=cv[:, k, :],
                                                  start=(k == 0 and j == 0), stop=(k == 7), skip_group_check=True),
                         reads=[stg, cv], writes=[pmod])
                if pi == 2:
                    for w in range(2):
                        for n in range(2):
                            S.op("pe", lambda e: e.matmul(pg[w * 2 + n][:], lhsT=screp[:, w, k, :],
                                                          rhs=stg[:, n * 512:(n + 1) * 512],
                                                          start=(k == 0), stop=(k == 7)),
                                 reads=[stg, screp], writes=[pg[w * 2 + n]])
        for w in range(2):
            S.op("dve", lambda e: e.tensor_tensor(out=modt[:, :, w], in0=pmod[:, 0:48].rearrange("p (j w) -> p j w", w=2)[:, :, w],
                                                  in1=bmodt[:], op=ALU.add), reads=[pmod, bmodt], writes=[modt])
            for n in range(2):
                S.op("dve", lambda e: e.tensor_tensor(out=gate_bc[:, w, n * 512:(n + 1) * 512], in0=pg[w * 2 + n][:],
                                                      in1=gate_bc[:, w, n * 512:(n + 1) * 512], op=ALU.add),
                     reads=[pg[w * 2 + n], gate_bc], writes=[gate_bc])
        S.op("dve", lambda e: e.tensor_scalar_add(out=modt[:, 8:16, :], in0=modt[:, 8:16, :], scalar1=1.0),
             reads=[modt], writes=[modt])

        cnt = 0
        for k in range(8):
            for c0 in range(0, NCOL, 1024):
                cn = min(1024, NCOL - c0)
                stg = stage.next()
                S.dma(LQ, stg[:, 0:cn], w_in[k, :, c0:c0 + cn], writes=[stg])
                S.op("dve", lambda e: e.tensor_copy(out=win[:, k, c0:c0 + cn], in_=stg[:, 0:cn]), reads=[stg], writes=[win])
                cnt += 1

        if layer == 1:
            S.op("dve", lambda e: e.tensor_tensor(out=lprod[:, 0:64], in0=lam_t[:, 0:64], in1=lam_t[:, 64:128], op=ALU.mult), reads=[lam_t], writes=[lprod])
            S.op("dve", lambda e: e.tensor_tensor(out=lprod[:, 64:128], in0=lam_t[:, 128:192], in1=lam_t[:, 192:256], op=ALU.mult), reads=[lam_t, lprod], writes=[lprod])
            S.op("dve", lambda e: e.reduce_sum(out=lam_s[:, 0:2], in_=lprod[:].rearrange("p (a d) -> p a d", a=2), axis=mybir.AxisListType.X), reads=[lprod], writes=[lam_s])
            S.op("act", lambda e: e.activation(out=lam_s[:, 0:2], in_=lam_s[:, 0:2], func=AF.Exp), reads=[lam_s], writes=[lam_s])
            S.op("dve", lambda e: e.tensor_tensor(out=lam_s[:, 2:3], in0=lam_s[:, 0:1], in1=lam_s[:, 1:2], op=ALU.subtract), reads=[lam_s], writes=[lam_s])
            S.op("dve", lambda e: e.tensor_scalar(out=lam_s[:, 3:4], in0=lam_s[:, 2:3], scalar1=lam0, scalar2=-1.0, op0=ALU.add, op1=ALU.mult), reads=[lam_s], writes=[lam_s])
            S.op("act", lambda e: e.activation(out=esnk[:], in_=snk_t[:], func=AF.Exp), reads=[snk_t], writes=[esnk])

        xt_r = Rot([S.sb("xt%d" % i, [128, D_MODEL], F32) for i in range(2)])
        xn_r = Rot([S.sb("xn%d" % i, [128, D_MODEL], BF16) for i in range(2)])
        hT_r = Rot([S.sb("hT%d" % i, [128, 8, 512], BF16) for i in range(2)])
        tabC_r = Rot([S.sb("tabC%d" % i, [128, 512], F32) for i in range(1)])
        tabS_r = Rot([S.sb("tabS%d" % i, [128, 512], F32) for i in range(1)])
        sq_r = Rot([S.sb("sq%d" % i, [128, 512], F32) for i in range(2)])
        rs_r = Rot([S.sb("rs%d" % i, [128, 512], F32) for i in range(2)])
        qn_r = Rot([S.sb("qn%d" % i, [128, 512], F32) for i in range(3)])
        t2_r = Rot([S.sb("t2%d" % i, [128, 512], F32) for i in range(2)])
        fo_r = Rot([S.sb("fo%d" % i, [128, 512], BF16) for i in range(4)])
        vst = {}
        for name in vS:
            h, d = vdims[name]
            vst[name] = Rot([S.sb("vst_%s%d" % (name, i), [128, h, d], BF16) for i in range(2)])
            for t in vst[name].tiles:
                S.op("pool", lambda e: e.memset(t[:], 1.0), writes=[t])
        zst_r = Rot([S.sb("zst%d" % i, [128, D_MODEL], BF16) for i in range(2)])
        bA = Rot([banks[1], banks[2], banks[3]])
        bB = Rot([banks[4], banks[5]])

        def p1_prep(bd):
            u0, ntiles, w = bd["u0"], bd["ntiles"], bd["w"]
            hT = hT_r.next()
            bd["hT"] = hT
            for ti in range(ntiles):
                xt = xt_r.next()
                ss = ss_r.next()
                xn = xn_r.next()
                load_x_tile(xt, u0 // 128 + ti)
                S.op("act", lambda e: e.activation(out=junk[:], in_=xt[:], func=AF.Square, accum_out=ss[:, 0:1]), reads=[xt], writes=[junk, ss])
                S.op("act", lambda e: e.activation(out=ss[:, 1:2], in_=ss[:, 0:1], func=AF.Ln, scale=1.0 / D_MODEL, bias=epst[:]), reads=[ss, epst], writes=[ss])
                S.op("act", lambda e: e.activation(out=ss[:, 1:2], in_=ss[:, 1:2], func=AF.Exp, scale=-0.5), reads=[ss], writes=[ss])
                S.op("act", lambda e: e.activation(out=xn[:], in_=xt[:], func=AF.Copy, scale=ss[:, 1:2]), reads=[xt, ss], writes=[xn])
                for k in range(8):
                    S.op("pe", lambda e: e.transpose(out=pTb[:, k, :], in_=xn[:, k * 128:(k + 1) * 128], identity=ident[:]),
                         reads=[xn, ident], writes=[pTb_t])
                for k in range(8):
                    S.op("dve", lambda e: e.tensor_scalar(out=hT[:, k, ti * 128:(ti + 1) * 128], in0=pTb[:, k, :],
                                                          scalar1=modt[:, 8 + k, w:w + 1], scalar2=modt[:, k, w:w + 1],
                                                          op0=ALU.mult, op1=ALU.add), reads=[pTb_t, modt], writes=[hT])
                yield

        def p1_mm(bd):
            u0, ntiles, fm_list, tm_list, rope_col0, hT = bd["u0"], bd["ntiles"], bd["fm_list"], bd["tm_list"], bd["rope_col0"], bd["hT"]
            ntok = ntiles * 128
            rope_needed = any(fm[i][3] for i in fm_list) and rope_col0 is not None
            if rope_needed:
                tC = tabC_r.next()
                tS = tabS_r.next()
                S.dma(LQ, tC[:, 0:ntok], ropeC[:, rope_col0:rope_col0 + ntok], writes=[tC])
                S.dma(LQ, tS[:, 0:ntok], ropeS[:, rope_col0:rope_col0 + ntok], writes=[tS])
            chs = [dict(i=i) for i in fm_list]

            def st_main(ch):
                i = ch["i"]
                pa = bA.next()
                for k in range(8):
                    S.op("pe", lambda e: e.matmul(pa[:, 0:ntok], lhsT=win[:, k, i * 128:(i + 1) * 128], rhs=hT[:, k, 0:ntok],
                                                  start=(k == 0), stop=(k == 7)), reads=[win, hT], writes=[pa])
                ch["pa"] = pa

            def st_norm(ch):
                i = ch["i"]
                name, _, nkind, roped = fm[i]
                pa = ch["pa"]
                fo = fo_r.next()
                ch["fo"] = fo
                do_rope = roped and rope_col0 is not None
                ch["do_rope"] = do_rope
                if nkind is None and not do_rope:
                    S.op("act", lambda e: e.activation(out=fo[:, 0:ntok], in_=pa[:, 0:ntok], func=AF.Copy), reads=[pa], writes=[fo])
                    return
                qn = qn_r.next()
                ch["qn"] = qn
                if nkind is not None:
                    sq = sq_r.next()
                    rs = rs_r.next()
                    pb = bB.next()
                    gcol = 0 if nkind == "q" else 1
                    S.op("act", lambda e: e.activation(out=sq[:, 0:ntok], in_=pa[:, 0:ntok], func=AF.Square), reads=[pa], writes=[sq])
                    S.op("pe", lambda e: e.matmul(pb[:, 0:ntok], lhsT=blk[:], rhs=sq[:, 0:ntok], start=True, stop=True), reads=[blk, sq], writes=[pb])
                    S.op("act", lambda e: e.activation(out=rs[:, 0:ntok], in_=pb[:, 0:ntok], func=AF.Ln, bias=epst[:]), reads=[pb, epst], writes=[rs])
                    S.op("act", lambda e: e.activation(out=rs[:, 0:ntok], in_=rs[:, 0:ntok], func=AF.Exp, scale=-0.5), reads=[rs], writes=[rs])
                    dst = qn if do_rope else fo
                    S.op("dve", lambda e: e.scalar_tensor_tensor(out=dst[:, 0:ntok], in0=pa[:, 0:ntok], scalar=gn[:, gcol:gcol + 1],
                                                                 in1=rs[:, 0:ntok], op0=ALU.mult, op1=ALU.mult),
                         reads=[pa, gn, rs], writes=[dst])
                else:
                    S.op("act", lambda e: e.activation(out=qn[:, 0:ntok], in_=pa[:, 0:ntok], func=AF.Copy), reads=[pa], writes=[qn])

            def st_rope(ch):
                i = ch["i"]
                fo = ch["fo"]
                if ch["do_rope"]:
                    qn = ch["qn"]
                    pb2 = bB.next()
                    t1 = t1_r.next()
                    t2 = t2_r.next()
                    S.op("pe", lambda e: e.matmul(pb2[:, 0:ntok], lhsT=perm[:], rhs=qn[:, 0:ntok], start=True, stop=True), reads=[perm, qn], writes=[pb2])
                    S.op("dve", lambda e: e.tensor_tensor(out=t1[:, 0:ntok], in0=qn[:, 0:ntok], in1=tC[:, 0:ntok], op=ALU.mult), reads=[qn, tC], writes=[t1])
                    S.op("dve", lambda e: e.tensor_tensor(out=t2[:, 0:ntok], in0=pb2[:, 0:ntok], in1=tS[:, 0:ntok], op=ALU.mult), reads=[pb2, tS], writes=[t2])
                    S.op("dve", lambda e: e.tensor_tensor(out=fo[:, 0:ntok], in0=t1[:, 0:ntok], in1=t2[:, 0:ntok], op=ALU.add), reads=[t1, t2], writes=[fo])
                S.dma(SQ, fmS[i, :, u0:u0 + ntok], fo[:, 0:ntok], reads=[fo], writes=[fm_reg], sem_tile=fo)

            nchs = len(chs)
            for step in range(nchs + 2):
                if step < nchs:
                    st_main(chs[step])
                if 0 <= step - 1 < nchs:
                    st_norm(chs[step - 1])
                if 0 <= step - 2 < nchs:
                    st_rope(chs[step - 2])
                yield
            for name in tm_list:
                col0 = tmoff[name]
                ncols = dict((t[0], t[2]) for t in tm)[name]
                for ti in range(ntiles):
                    for n0 in range(0, ncols, 512):
                        nn = min(512, ncols - n0)
                        pa = bA.next()
                        for k in range(8):
                            S.op("pe", lambda e: e.matmul(pa[:, 0:nn], lhsT=hT[:, k, ti * 128:(ti + 1) * 128],
                                                          rhs=win[:, k, col0 + n0:col0 + n0 + nn], start=(k == 0), stop=(k == 7)),
                                 reads=[win, hT], writes=[pa])
                        if name == "z":
                            if n0 == 0:
                                zst = zst_r.next()
                            S.op("act", lambda e: e.activation(out=zst[:, n0:n0 + nn], in_=pa[:, 0:nn], func=AF.Silu), reads=[pa], writes=[zst])
                            if n0 + nn == ncols:
                                S.dma(SQ, zS[u0 + ti * 128:u0 + (ti + 1) * 128, :], zst[:], reads=[zst], writes=[z_reg], sem_tile=zst)
                        else:
                            h, d = vdims[name]
                            dv = d - 1
                            vt = vst[name].next()
                            S.op("dve", lambda e: e.tensor_copy(out=vt[:, :, 0:dv], in_=pa[:, 0:nn].rearrange("p (h d) -> p h d", d=dv)),
                                 reads=[pa], writes=[vt])
                            S.dma(SQ, vS[name][u0 + ti * 128:u0 + (ti + 1) * 128, :], vt[:].rearrange("p h d -> p (h d)"),
                                  reads=[vt], writes=[v_reg], sem_tile=vt)
                        yield


        all_fm = list(range(NFM))
        all_tm = [t[0] for t in tm]
        lk = [fmi[n] for n in cfg["local_k"]]
        dk = [fmi[n] for n in cfg["dense_k"]]
        blocks = [dict(u0=U_C, ntiles=2, w=1, fm_list=all_fm, tm_list=all_tm, rope_col0=None)]
        eblocks = list(range(EXT // 512))
        eblocks = [b for b in eblocks if HALO <= b * 512 < HALO + HALF] + [b for b in eblocks if not (HALO <= b * 512 < HALO + HALF)]
        for b in eblocks:
            u0 = b * 512
            own = HALO <= u0 < HALO + HALF
            if own:
                blocks.append(dict(u0=u0, ntiles=4, w=0, fm_list=all_fm, tm_list=all_tm, rope_col0=u0))
            else:
                blocks.append(dict(u0=u0, ntiles=4, w=0, fm_list=lk, tm_list=cfg["local_v"], rope_col0=u0))
        for b in range(HALF // 512):
            u0 = U_O + b * 512
            blocks.append(dict(u0=u0, ntiles=4, w=0, fm_list=dk, tm_list=cfg["dense_v"], rope_col0=u0))
        for _ in p1_prep(blocks[0]):
            pass
        for bi, bd in enumerate(blocks):
            gm = p1_mm(bd)
            gp = p1_prep(blocks[bi + 1]) if bi + 1 < len(blocks) else None
            nsteps = len(bd["fm_list"]) + 2 + sum(((dict((t[0], t[2]) for t in tm)[nm] + 511) // 512) * bd["ntiles"] for nm in bd["tm_list"])
            ntl = blocks[bi + 1]["ntiles"] if gp is not None else 0
            every = max(1, nsteps // (ntl + 1)) if ntl else 0
            k = 0
            for _ in gm:
                k += 1
                if gp is not None and every and k % every == 0:
                    next(gp, None)
            if gp is not None:
                for _ in gp:
                    pass

        S.barrier()
        st1.close()
        st2 = contextlib.ExitStack()
        S.stack = st2
        wout = S.sb("wout", [128, 8, D_MODEL], BF16)
        stage2 = Rot([S.sb("stage2_%d" % i, [128, D_MODEL], F32) for i in range(2)])
        for k in range(8):
            stg = stage2.next()
            S.dma(LQ, stg[:], w_out[k], writes=[stg])
            S.op("dve", lambda e: e.tensor_copy(out=wout[:, k, :], in_=stg[:]), reads=[stg], writes=[wout])
        bS = Rot([banks[1], banks[2], banks[3]])
        bACC = Rot([banks[5], banks[6], banks[7]])
        nbuf_d = 1 if layer == 0 else 2
        KT_r = Rot([S.sb("KT%d" % i, [128, NKD], BF16) for i in range(nbuf_d)])
        VDW = 130 if layer == 0 else 129
        VD_r = Rot([S.sb("VD%d" % i, [128, NKC, VDW], BF16) for i in range(nbuf_d)])
        q_r = Rot([S.sb("qblk%d" % i, [128, 512], BF16) for i in range(6)])
        NLK = 9
        kl_r = Rot([S.sb("kl%d" % i, [128, NLK * 128], BF16) for i in range(5)])
        VLW = 130
        vl_r = Rot([S.sb("vl%d" % i, [128, NLK, VLW], BF16) for i in range(5)])
        y_t = [S.sb("y%d" % i, [128, D_MODEL], F32) for i in range(4)]
        rec_r = Rot([S.sb("rec%d" % i, [128, 8], F32) for i in range(4)])
        zl_r = Rot([S.sb("zl%d" % i, [128, D_MODEL], BF16) for i in range(1)])
        yb_r = Rot([S.sb("yb%d" % i, [128, D_MODEL], BF16) for i in range(1)])
        yT_r = Rot([S.sb("yT%d" % i, [128, 8, 128], BF16) for i in range(1)])
        xo_r = Rot([S.sb("xo%d" % i, [128, D_MODEL], F32) for i in range(1)])
        res_r = Rot([S.sb("res%d" % i, [128, D_MODEL], F32) for i in range(2)])
        be_r = Rot([S.sb("be%d" % i, [128, 7 * 128], F32) for i in range(3)]) if layer == 0 else None
        dcache = {}

        def load_dense(kname, vname, vc0, vw):
            key = (kname, vname, vc0)
            if dcache.get("key") == key:
                return dcache["KT"], dcache["VD"]
            KT = KT_r.next()
            VD = VD_r.next()
            i = fmi[kname]
            S.dma(LQ, KT[:, 0:HALF], fmS[i, :, U_OWN:U_OWN + HALF], reads=[fm_reg], writes=[KT])
            S.dma(LQ, KT[:, HALF:2 * HALF], fmS[i, :, U_O:U_O + HALF], reads=[fm_reg], writes=[KT])
            S.dma(LQ, KT[:, 2 * HALF:NKD], fmS[i, :, U_C:U_C + CTX], reads=[fm_reg], writes=[KT])
            for (c0, u0, n) in ((0, U_OWN, HALF), (HALF // 128, U_O, HALF), (2 * HALF // 128, U_C, CTX)):
                S.dma(LQ, VD[:, c0:c0 + n // 128, 0:vw], vS[vname][u0:u0 + n, vc0:vc0 + vw].rearrange("(k p) c -> p k c", p=128),
                      reads=[v_reg], writes=[VD])
            dcache.update(key=key, KT=KT, VD=VD)
            return KT, VD

        def attend(qt, pbase, NQ, kchunks, acc_list, vwidth, bias_fn=None):
            nq = NQ // 128
            n = len(kchunks)
            pend = []
            first = {}

            def issue_s(j):
                kt, kc0, vt, vap = kchunks[j]
                ps = bS.next()
                S.op("pe", lambda e: e.matmul(ps[:, 0:NQ], lhsT=kt[pbase:pbase + 64, kc0:kc0 + 128], rhs=qt[pbase:pbase + 64, 0:NQ],
                                              start=True, stop=True), reads=[kt, qt], writes=[ps])
                pT = pT_r.next()
                b = bias_fn(j) if bias_fn is not None else None
                if b is not None:
                    btile, bap = b
                    sb = sb_r.next()
                    S.op("dve", lambda e: e.scalar_tensor_tensor(out=sb[:, 0:NQ], in0=ps[:, 0:NQ], scalar=SCALE, in1=bap,
                                                                 op0=ALU.mult, op1=ALU.add), reads=[ps, btile], writes=[sb])
                    S.op("act", lambda e: e.activation(out=pT[:, 0:NQ], in_=sb[:, 0:NQ], func=AF.Exp), reads=[sb], writes=[pT])
                else:
                    S.op("act", lambda e: e.activation(out=pT[:, 0:NQ], in_=ps[:, 0:NQ], func=AF.Exp, scale=SCALE), reads=[ps], writes=[pT])
                return pT

            def issue_pv(j, pT):
                kt, kc0, vt, vap = kchunks[j]
                for s in range(nq):
                    acc, c0 = acc_list[s]
                    fst = first.get(id(acc), True)
                    first[id(acc)] = False
                    S.op("pe", lambda e: e.matmul(acc[:, c0:c0 + vwidth], lhsT=pT[:, s * 128:(s + 1) * 128], rhs=vap,
                                                  start=(j == 0 and fst), stop=(j == n - 1), skip_group_check=True),
                         reads=[pT, vt], writes=[acc])

            prev = None
            for j in range(n):
                pT = issue_s(j)
                if prev is not None:
                    issue_pv(prev[0], prev[1])
                prev = (j, pT)
            issue_pv(prev[0], prev[1])

        onesel_f = S.sb("onesel_f", [128, 2, 2], F32)
        S.op("pool", lambda e: e.memset(onesel_f[:], 0.0), writes=[onesel_f])
        S.op("pool", lambda e: e.memset(onesel_f[:, 0, 0:1], 1.0), writes=[onesel_f])
        S.op("pool", lambda e: e.memset(onesel_f[:, 1, 1:2], 1.0), writes=[onesel_f])
        den_acc = S.sb("den_acc", [128, 1024], F32) if layer == 1 else None
        onesel_b = S.sb("onesel_b", [128, 2, 2], BF16)
        S.op("dve", lambda e: e.tensor_copy(out=onesel_b[:], in_=onesel_f[:]), reads=[onesel_f], writes=[onesel_b])
        if layer == 1:
            dmb = S.sb("dmb", [128, 3, 384], F32)
            S.op("pool", lambda e: e.memset(dmb[:], 0.0), writes=[dmb])
            for var, (pi, ni) in enumerate(((2, 3), (0, 3), (2, 1))):
                S.op("dve", lambda e: e.tensor_copy(out=dmb[:, var, 0:128], in_=dm_t[:, pi, :]), reads=[dm_t], writes=[dmb])
                S.op("dve", lambda e: e.tensor_copy(out=dmb[:, var, 256:384], in_=dm_t[:, ni, :]), reads=[dm_t], writes=[dmb])
        rec16_r = Rot([S.sb("rec16_%d" % i, [128, 16], F32) for i in range(2)])
        Tq = S.sb("Tq", [128, 4, 256], F32) if layer == 1 else None
        pT2_r = Rot([S.sb("pT2_%d" % i, [128, 1024], BF16) for i in range(3)])
        oT_r = Rot([S.sb("oT%d" % i, [128, 512], F32) for i in range(2)])
        pairs = SH["pairs"]

        def dense_pair(qt, KT, VD, vap_fn, vw, accs, den=None):
            n = NKC

            def issue_s(j):
                pt, ta, tb = pairs[j % 2]
                for m, tt in ((0, ta), (1, tb)):
                    S.op("pe", lambda e: e.matmul(tt[:, 0:512], lhsT=KT[64 * m:64 * m + 64, j * 128:(j + 1) * 128],
                                                  rhs=qt[64 * m:64 * m + 64, 0:512], start=True, stop=True), reads=[KT, qt], writes=[tt])
                pT = pT2_r.next()
                S.op("act", lambda e: e.activation(out=pT[:], in_=pt[:, :], func=AF.Exp, scale=SCALE), reads=[ta, tb], writes=[pT])
                return pT

            def issue_pv(j, pT):
                for m in range(2):
                    S.op("pe", lambda e: e.matmul(accs[m][0:vw, 0:512], lhsT=vap_fn(j, m), rhs=pT[:, m * 512:(m + 1) * 512],
                                                  start=(j == 0), stop=(j == n - 1)), reads=[pT, VD], writes=[accs[m]])
                if den is not None:
                    if j == 0:
                        S.op("dve", lambda e: e.tensor_copy(out=den_acc[:, 0:512], in_=pT[:, 0:512]), reads=[pT], writes=[den_acc])
                    else:
                        S.op("dve", lambda e: e.tensor_tensor(out=den_acc[:, 0:512], in0=den_acc[:, 0:512], in1=pT[:, 0:512], op=ALU.add),
                             reads=[pT, den_acc], writes=[den_acc])
                    S.op("pe", lambda e: e.matmul(den[0:2, 0:512], lhsT=onesel_b[:, 1, :], rhs=pT[:, 512:1024],
                                                  start=(j == 0), stop=False, skip_group_check=True), reads=[pT, onesel_b], writes=[den])
                    if j == n - 1:
                        S.op("pe", lambda e: e.matmul(den[0:2, 0:512], lhsT=onesel_f[:, 0, :], rhs=den_acc[:, 0:512],
                                                      start=False, stop=True, skip_group_check=True), reads=[den_acc, onesel_f], writes=[den])

            pend = []
            for j in range(n):
                pend.append((j, issue_s(j)))
                if len(pend) > 2:
                    issue_pv(*pend.pop(0))
            while pend:
                issue_pv(*pend.pop(0))

        def untranspose(acc, rows, fin, width):
            oT = oT_r.next()
            S.op("act", lambda e: e.activation(out=oT[0:rows, :], in_=acc[0:rows, 0:512], func=AF.Copy), reads=[acc], writes=[oT])
            for sidx in range(4):
                S.op("pe", lambda e: e.transpose(out=fin[:, sidx * width:sidx * width + rows], in_=oT[0:rows, sidx * 128:(sidx + 1) * 128],
                                                 identity=ident_f[0:rows, 0:rows]), reads=[oT, ident_f], writes=[fin])

        wide_i = [0]

        def wide_a(job):
            qt, pbase, kl, nch, nb, bias = job["qt"], job["pbase"], job["kl"], job["nch"], job["nb"], job["bias"]
            pt, ta, tb = pairs[wide_i[0] % 2]
            wide_i[0] += 1
            for j in range(nch):
                tt = ta if j < 4 else tb
                S.op("pe", lambda e: e.matmul(pt[:, j * 128:(j + 1) * 128], lhsT=kl[pbase:pbase + 64, j * 128:(j + 1) * 128],
                                              rhs=qt[pbase:pbase + 64, 0:128], start=True, stop=True), reads=[kl, qt], writes=[tt])
            pT = pT2_r.next()
            used = [ta] + ([tb] if nch > 4 else [])
            if bias is not None:
                btile, bap = bias
                sbw = stage2.next()
                S.op("dve", lambda e: e.scalar_tensor_tensor(out=sbw[:, 0:nb * 128], in0=pt[:, 0:nb * 128], scalar=SCALE, in1=bap,
                                                             op0=ALU.mult, op1=ALU.add), reads=used + [btile], writes=[sbw])
                S.op("act", lambda e: e.activation(out=pT[:, 0:nb * 128], in_=sbw[:, 0:nb * 128], func=AF.Exp), reads=[sbw], writes=[pT])
                if nch > nb:
                    S.op("act", lambda e: e.activation(out=pT[:, nb * 128:nch * 128], in_=pt[:, nb * 128:nch * 128], func=AF.Exp, scale=SCALE),
                         reads=used, writes=[pT])
            else:
                S.op("act", lambda e: e.activation(out=pT[:, 0:nch * 128], in_=pt[:, 0:nch * 128], func=AF.Exp, scale=SCALE), reads=used, writes=[pT])
            job["pT"] = pT

        def wide_b(job):
            pT, vl, vc0, nch = job["pT"], job["vl"], job["vc0"], job["nch"]
            acc = bACC.next()
            for j in range(nch):
                S.op("pe", lambda e: e.matmul(acc[:, 0:65], lhsT=pT[:, j * 128:(j + 1) * 128], rhs=vl[:, j, vc0:vc0 + 65],
                                              start=(j == 0), stop=(j == nch - 1)), reads=[pT, vl], writes=[acc])
            finish_head([(acc, 0)], 65, [job["yt"]], job["ycol"], extra_den=job.get("extra_den"))

        def run_wide(jobs):
            pend = []
            for job in jobs:
                if "pre" in job:
                    job["pre"]()
                wide_a(job)
                pend.append(job)
                if len(pend) > 2:
                    wide_b(pend.pop(0))
            while pend:
                wide_b(pend.pop(0))

        def finish_head(acc_list, vwidth, y_tiles, ycol, extra_den=None, scale_ap=None):
            dv = vwidth - 1
            for s, (acc, c0) in enumerate(acc_list):
                rec = rec_r.next()
                if extra_den is not None:
                    S.op("dve", lambda e: e.tensor_tensor(out=rec[:, 0:1], in0=acc[:, c0 + dv:c0 + dv + 1], in1=extra_den, op=ALU.add),
                         reads=[acc, esnk], writes=[rec])
                    S.op("dve", lambda e: e.reciprocal(out=rec[:, 1:2], in_=rec[:, 0:1]), reads=[rec], writes=[rec])
                else:
                    S.op("dve", lambda e: e.reciprocal(out=rec[:, 1:2], in_=acc[:, c0 + dv:c0 + dv + 1]), reads=[acc], writes=[rec])
                yt = y_tiles[s]
                S.op("act", lambda e: e.activation(out=yt[:, ycol:ycol + dv], in_=acc[:, c0:c0 + dv], func=AF.Copy, scale=rec[:, 1:2]),
                     reads=[acc, rec], writes=[yt])

        def out_tile(yt, u_tok, src, src_row, w, dst, dst_row):
            zl = zl_r.next()
            yb = yb_r.next()
            yT = yT_r.next()
            xo = xo_r.next()
            res = res_r.next()
            S.dma(LQ, zl[:], zS[u_tok:u_tok + 128, :], reads=[z_reg], writes=[zl])
            load_x_tile(xo, u_tok // 128)
            S.op("dve", lambda e: e.tensor_tensor(out=yb[:], in0=yt[:], in1=zl[:], op=ALU.mult), reads=[yt, zl], writes=[yb])
            for k in range(8):
                S.op("pe", lambda e: e.transpose(out=pTb[:, k, :], in_=yb[:, k * 128:(k + 1) * 128], identity=ident[:]),
                     reads=[yb, ident], writes=[pTb_t])
            S.op("act", lambda e: e.activation(out=yT[:].rearrange("p k t -> p (k t)"), in_=pTb[:].rearrange("p k t -> p (k t)"), func=AF.Copy),
                 reads=[pTb_t], writes=[yT])
            for n in range(2):
                po = bACC.next()
                for k in range(8):
                    S.op("pe", lambda e: e.matmul(po[:], lhsT=yT[:, k, :], rhs=wout[:, k, n * 512:(n + 1) * 512], start=(k == 0), stop=(k == 7)),
                         reads=[yT, wout], writes=[po])
                S.op("dve", lambda e: e.tensor_tensor(out=res[:, n * 512:(n + 1) * 512], in0=po[:], in1=gate_bc[:, w, n * 512:(n + 1) * 512], op=ALU.mult),
                     reads=[po, gate_bc], writes=[res])
            S.op("dve", lambda e: e.tensor_tensor(out=res[:], in0=res[:], in1=xo[:], op=ALU.add), reads=[res, xo], writes=[res])
            if last:
                ss = ss_r.next()
                S.op("act", lambda e: e.activation(out=junk[:], in_=res[:], func=AF.Square, accum_out=ss[:, 0:1]), reads=[res], writes=[junk, ss])
                S.op("act", lambda e: e.activation(out=ss[:, 1:2], in_=ss[:, 0:1], func=AF.Ln, scale=1.0 / D_MODEL, bias=epst[:]), reads=[ss, epst], writes=[ss])
                S.op("act", lambda e: e.activation(out=ss[:, 1:2], in_=ss[:, 1:2], func=AF.Exp, scale=-0.5), reads=[ss], writes=[ss])
                S.op("act", lambda e: e.activation(out=xo[:], in_=res[:], func=AF.Copy, scale=ss[:, 1:2]), reads=[res, ss], writes=[xo])
                S.op("dve", lambda e: e.tensor_tensor(out=res[:], in0=xo[:], in1=fn_t[:], op=ALU.mult), reads=[xo, fn_t], writes=[res])
            S.dma(SQ, dst[dst_row:dst_row + 128, :], res[:], reads=[res], sem_tile=res)
            return res

        out_tiles = []

        def load_q(name, u0, n):
            qt = q_r.next()
            S.dma(LQ, qt[:, 0:n], fmS[fmi[name], :, u0:u0 + n], reads=[fm_reg], writes=[qt])
            return qt

        def load_local(knames_idx, vname, vc0, vw, utiles):
            kl = kl_r.next()
            vl = vl_r.next()
            pos = 0
            for (ut0, cnt) in utiles:
                S.dma(LQ, kl[:, pos * 128:(pos + cnt) * 128], fmS[knames_idx, :, ut0 * 128:(ut0 + cnt) * 128], reads=[fm_reg], writes=[kl])
                S.dma(LQ, vl[:, pos:pos + cnt, 0:vw], vS[vname][ut0 * 128:(ut0 + cnt) * 128, vc0:vc0 + vw].rearrange("(k p) c -> p k c", p=128),
                      reads=[v_reg], writes=[vl])
                pos += cnt
            return kl, vl

        UC_T = U_C // 128

        if layer == 0:
            for t in range(2):
                yt = y_t[t]
                u0 = U_C + t * 128
                jobs = []
                kl, vl = load_local(fmi["ka"], "va", 0, 130, [(UC_T, 2)])
                for c in range(4):
                    qt = load_q("qa%d" % c, u0, 128)
                    for s_ in range(2):
                        jobs.append(dict(qt=qt, pbase=64 * s_, kl=kl, vl=vl, vc0=s_ * 65, nch=2, nb=0, bias=None, yt=yt, ycol=(c + 4 * s_) * 64))
                run_wide(jobs)
                for c in range(4):
                    kl, vl = load_local(fmi["kb%d" % c], "vb", c * 130, 130, [(UC_T, 2)])
                    qt = load_q("qb%d" % c, u0, 128)
                    run_wide([dict(qt=qt, pbase=64 * s_, kl=kl, vl=vl, vc0=s_ * 65, nch=2, nb=0, bias=None, yt=yt, ycol=512 + (2 * c + s_) * 64)
                              for s_ in range(2)])
                out_tiles.append(out_tile(yt, u0, xc_in, t * 128, 1, out_c, t * 128))

        for qb in range(NBo):
            u0 = U_OWN + qb * 512
            if layer == 0:
                KT, VD = load_dense("ka", "va", 0, 130)
                for c in range(4):
                    qt = load_q("qa%d" % c, u0, 512)
                    accs = [banks[5], banks[6]]
                    dense_pair(qt, KT, VD, lambda j, m: VD[:, j, m * 65:(m + 1) * 65], 65, accs)
                    for s in range(2):
                        head = c + 4 * s
                        fin = banks[7]
                        untranspose(accs[s], 65, fin, 65)
                        finish_head([(fin, i * 65) for i in range(4)], 65, y_t, head * 64)
            else:
                for h in range(4):
                    KT, VD = load_dense("kc%d" % h, "vc", h * 129, 129)
                    qt = load_q("qc%d" % h, u0, 512)
                    accs = [banks[5], banks[6]]
                    den = banks[7]
                    dense_pair(qt, KT, VD, lambda j, m: VD[:, j, 0:128], 128, accs, den=den)
                    fins = [banks[1], banks[2]]
                    untranspose(accs[0], 128, fins[0], 128)
                    untranspose(accs[1], 128, fins[1], 128)
                    dfin = banks[3]
                    untranspose(den, 2, dfin, 2)
                    o_m = [[(fins[0], i * 128, i * 2 + 0) for i in range(4)], [(fins[1], i * 128, i * 2 + 1) for i in range(4)]]
                    R = rec16_r.next()
                    S.op("dve", lambda e: e.reciprocal(out=R[:, 0:8], in_=dfin[:, 0:8]), reads=[dfin], writes=[R])
                    S.op("dve", lambda e: e.tensor_scalar(out=R[:, 8:12], in0=R[:, 0:8].rearrange("p (s m) -> p s m", m=2)[:, :, 1],
                                                          scalar1=lam_s[:, 3:4], scalar2=None, op0=ALU.mult), reads=[R, lam_s], writes=[R])
                    for s in range(4):
                        a0, c0, d0 = o_m[0][s]
                        a1, c1, d1 = o_m[1][s]
                        S.op("act", lambda e: e.activation(out=Tq[:, s, 0:128], in_=a0[:, c0:c0 + 128], func=AF.Copy, scale=R[:, 2 * s:2 * s + 1]),
                             reads=[a0, R], writes=[Tq])
                        S.op("dve", lambda e: e.scalar_tensor_tensor(out=Tq[:, s, 128:256], in0=a1[:, c1:c1 + 128], scalar=R[:, 8 + s:9 + s], in1=Tq[:, s, 0:128],
                                                                     op0=ALU.mult, op1=ALU.add), reads=[a1, R, Tq], writes=[Tq])
                        S.op("act", lambda e: e.activation(out=Tq[:, s, 0:128], in_=Tq[:, s, 128:256], func=AF.Square, accum_out=R[:, 12 + s:13 + s]),
                             reads=[Tq], writes=[Tq, R])
                    S.op("act", lambda e: e.activation(out=R[:, 12:16], in_=R[:, 12:16], func=AF.Ln, scale=1.0 / 128, bias=epst[:]), reads=[R, epst], writes=[R])
                    S.op("act", lambda e: e.activation(out=R[:, 12:16], in_=R[:, 12:16], func=AF.Exp, scale=-0.5), reads=[R], writes=[R])
                    S.op("dve", lambda e: e.tensor_scalar_mul(out=R[:, 12:16], in0=R[:, 12:16], scalar1=1.0 - lam0), reads=[R], writes=[R])
                    for s in range(4):
                        yt = y_t[s]
                        S.op("dve", lambda e: e.scalar_tensor_tensor(out=yt[:, h * 128:(h + 1) * 128], in0=Tq[:, s, 128:256], scalar=R[:, 12 + s:13 + s], in1=sub_t[:],
                                                                     op0=ALU.mult, op1=ALU.mult), reads=[Tq, R, sub_t], writes=[yt])
            for tl in range(4):
                t = qb * 4 + tl
                ut = U_OWN // 128 + t
                yt = y_t[tl]
                if layer == 0:
                    edge = t < 2 or t >= NTo - 2
                    if edge:
                        et = t if t < 2 else 2 + (t - (NTo - 2))
                        lo, hi = ((-2, 3), (-2, 2), (-2, 2), (-3, 2))[et]
                    else:
                        lo, hi = -2, 2
                    nb = hi - lo + 1
                    runs = [(ut + lo, nb), (UC_T, 2)]
                    jobs = []
                    for c in range(4):
                        kl, vl = load_local(fmi["kb%d" % c], "vb", c * 130, 130, runs)
                        qt = load_q("qb%d" % c, ut * 128, 128)
                        for s_ in range(2):
                            head = 2 * c + s_
                            job = dict(qt=qt, pbase=64 * s_, kl=kl, vl=vl, vc0=s_ * 65, nch=nb + 2, nb=nb,
                                       yt=yt, ycol=512 + head * 64)
                            if edge:
                                def pre(job=job, et=et, head=head, lo=lo, hi=hi):
                                    be = be_r.next()
                                    S.dma(LQ, be[:], bias_e[et, head], writes=[be])
                                    job["bias"] = (be, be[:, (lo + 3) * 128:(hi + 4) * 128])
                                job["pre"] = pre
                            else:
                                job["bias"] = (bi_t, bi_t[:, head, 0:640])
                            jobs.append(job)
                    run_wide(jobs)
                else:
                    kl, vl = load_local(fmi["kd"], "vd", 0, 130, [(ut - 1, 3), (UC_T, 2)])
                    var = 1 if t == 0 else (2 if t == NTo - 1 else 0)
                    jobs = []
                    for c in range(4):
                        qt = load_q("qd%d" % c, ut * 128, 128)
                        for s_ in range(2):
                            head = c + 4 * s_
                            jobs.append(dict(qt=qt, pbase=64 * s_, kl=kl, vl=vl, vc0=s_ * 65, nch=5, nb=3, bias=(dmb, dmb[:, var, :]),
                                             yt=yt, ycol=512 + head * 64, extra_den=esnk[:, head:head + 1]))
                    run_wide(jobs)
                out_tiles.append(out_tile(yt, ut * 128, x_u, ut * 128, 0, out_x, t * 128))
        if last:
            S.finish(out_tiles)
        else:
            S.barrier()
        st2.close()
    if not last:
        S.release_dsems()
        S.new_epoch()


def build_fused(SEQ, B):
    HALF = SEQ // 2
    EXT = HALF + 2 * HALO
    nc = bass.Bass("TRN2", target_bir_lowering=False)

    def din(name, shape):
        return nc.dram_tensor(name, list(shape), F32, kind="ExternalInput").ap()

    SH = dict(x_u=din("x_u", [EXT + HALF, D_MODEL]), xc=din("xc", [CTX, D_MODEL]), cvec=din("cvec", [128, 8, 2]),
              ident=din("ident", [128, 128]), blk=din("blk", [128, 128]), perm=din("perm", [128, 128]),
              ropeC=din("ropeC", [128, EXT + HALF]), ropeS=din("ropeS", [128, EXT + HALF]), sel=din("sel", [128, 4]))
    SH["x1_loc"] = nc.dram_tensor("x1_loc", [HALF, D_MODEL], F32).ap()
    SH["xc1_loc"] = nc.dram_tensor("xc1_loc", [CTX, D_MODEL], F32).ap()
    SH["GA"] = nc.dram_tensor("x1_all", [2 * HALF, D_MODEL], F32).ap()
    with contextlib.ExitStack() as st0:
        S = Sched(nc, st0)
        SH["pTb_t"] = S.ps("bankT", [128, 8, 128], BF16)
        pairA = st0.enter_context(nc.psum_tensor("ps_pairA", [128, 1024], F32))
        pairB = st0.enter_context(nc.psum_tensor("ps_pairB", [128, 1024], F32))
        b1 = Tile("bank1", pairA[:, 0:512], excl=True)
        b2 = Tile("bank2", pairA[:, 512:1024], excl=True)
        b3 = Tile("bank3", pairB[:, 0:512], excl=True)
        b4 = Tile("bank4", pairB[:, 512:1024], excl=True)
        SH["pairs"] = [(pairA, b1, b2), (pairB, b3, b4)]
        SH["banks"] = [SH["pTb_t"], b1, b2, b3, b4] + [S.ps("bank%d" % i, [128, 512], F32) for i in range(5, 8)]
        SH["G_reg"] = Tile("G_reg")
        emit_layer(nc, S, SH, 0, SEQ)
        S.sems["cc"] = st0.enter_context(nc.semaphore("s_cc"))
        RC = min(512, HALF)
        for i in range(HALF // RC):
            nc.gpsimd.collective_compute("AllGather", ALU.bypass, replica_groups=[[2 * b, 2 * b + 1] for b in range(B)],
                                         ins=[SH["x1_loc"][i * RC:(i + 1) * RC, :]],
                                         outs=[SH["GA"][i * 2 * RC:(i + 1) * 2 * RC, :]]).then_inc(S.sems["cc"], 1)
        SH["G_reg"].last_w = ("cc", HALF // RC)
        emit_layer(nc, S, SH, 1, SEQ)
    return nc


def rope_tables(pos):
    row = (pos // GRID_W).astype(np.float32)
    col = (pos % GRID_W).astype(np.float32)
    q = HD // 4
    inv = (10000.0 ** (-np.arange(q, dtype=np.float32) / q)).astype(np.float32)
    ar = row[None, :] * inv[:, None]
    ac = col[None, :] * inv[:, None]
    cr, sr, cc, sc = np.cos(ar), np.sin(ar), np.cos(ac), np.sin(ac)
    C = np.concatenate([cr, cr, cc, cc], axis=0)
    Sg = np.concatenate([-sr, sr, -sc, sc], axis=0)
    return (np.concatenate([C, C], 0).astype(np.float32), np.concatenate([Sg, Sg], 0).astype(np.float32))


def nbr_bias(rpb, g, gk, NT):
    rows = NT * 2
    out = np.full((8, 128, 128), NEG, np.float32)
    if gk < 0 or gk >= NT:
        return out
    ql = np.arange(128)
    r = 2 * g + ql // 64
    c = ql % 64
    kr = 2 * gk + ql // 64
    kc = ql % 64
    win_r = min(8, rows)
    rs = np.clip(r - win_r // 2, 0, rows - win_r)
    cs = np.clip(c - 8, 0, GRID_W - 16)
    valid = ((kr[:, None] >= rs[None, :]) & (kr[:, None] < rs[None, :] + win_r)
             & (kc[:, None] >= cs[None, :]) & (kc[:, None] < cs[None, :] + 16))
    di = kr[:, None] - r[None, :] + 7
    dj = kc[:, None] - c[None, :] + 15
    di = np.clip(di, 0, 14)
    dj = np.clip(dj, 0, 30)
    vals = rpb[:, di, dj]
    return np.where(valid[None], vals, np.float32(NEG)).astype(np.float32)


def chunk_rows(w):
    return np.ascontiguousarray(w.reshape(8, 128, w.shape[1]))


def prep_layer_inputs(layer, SEQ, xs, xcs, p):
    B = xs.shape[0]
    HALF = SEQ // 2
    EXT = HALF + 2 * HALO
    NT = SEQ // 128
    NTo = HALF // 128
    cfg = layer_cfg(layer)
    wi = p["w_in_even"][0] if layer == 0 else p["w_in_odd"][0]
    wo = p["w_out_even"][0] if layer == 0 else p["w_out_odd"][0]
    cols = []
    for f in cfg["fm"]:
        cols += f[1]
    for name, c0, n in cfg["tm"]:
        cols += list(range(c0, c0 + n))
    w_in_l = chunk_rows(np.ascontiguousarray(wi[:, cols]))
    w_out_l = chunk_rows(wo)
    w_mod_l = chunk_rows(p["w_mod"][layer])
    b_mod_l = np.ascontiguousarray(p["b_mod"][layer].reshape(24, 128).T)
    bgate = np.ascontiguousarray(np.broadcast_to(p["b_mod"][layer][2048:3072][None, :], (128, D_MODEL)))
    ident = np.eye(128, dtype=np.float32)
    blk = np.zeros((128, 128), np.float32)
    blk[:64, :64] = 1.0 / 64
    blk[64:, 64:] = 1.0 / 64
    perm = np.zeros((128, 128), np.float32)
    for m in range(128):
        k = m + 16 if (m % 32) < 16 else m - 16
        perm[k, m] = 1.0
    maps = []
    for b in range(B):
        for half in range(2):
            T0 = half * HALF
            pos_e = np.arange(T0 - HALO, T0 + HALF + HALO)
            valid_e = (pos_e >= 0) & (pos_e < SEQ)
            x_e = np.zeros((EXT, D_MODEL), np.float32)
            x_e[valid_e] = xs[b, pos_e[valid_e]]
            T1 = (1 - half) * HALF
            pos_o = np.arange(T1, T1 + HALF)
            x_u = np.concatenate([x_e, xs[b, pos_o]], axis=0)
            pos_u = np.concatenate([np.clip(pos_e, 0, SEQ - 1), pos_o])
            C, Sg = rope_tables(pos_u)
            cvec = np.stack([p["c"][b].reshape(8, 128).T, p["c_ctx"].reshape(8, 128).T], axis=-1)
            m = dict(x_u=x_u, xc=np.ascontiguousarray(xcs[b]), cvec=np.ascontiguousarray(cvec), w_mod=w_mod_l, b_mod=b_mod_l,
                     bgate=bgate, w_in=w_in_l, w_out=w_out_l, ident=ident, blk=blk, perm=perm, ropeC=C, ropeS=Sg)
            G0 = T0 // 128
            if layer == 0:
                m["gains"] = np.ascontiguousarray(np.stack([np.tile(p["a_q_norm"][0], 2), np.tile(p["a_k_norm"][0], 2)], axis=-1))
                rpb = p["b_rpb"][0]
                gi = min(max(G0 + 2, 2), NT - 3) if NT >= 6 else 0
                bi = np.stack([nbr_bias(rpb, gi, gi + j, NT) for j in range(-2, 3)], axis=0)
                m["bias_i"] = np.ascontiguousarray(bi.transpose(2, 1, 0, 3).reshape(128, 8, 640))
                ets = [0, 1, NTo - 2, NTo - 1]
                be = np.stack([np.stack([nbr_bias(rpb, G0 + t, G0 + t + j, NT) for j in range(-3, 4)], axis=0) for t in ets], axis=0)
                m["bias_e"] = np.ascontiguousarray(be.transpose(0, 2, 3, 1, 4).reshape(4, 8, 128, 896))
            else:
                a = np.arange(128)
                tri_prev = np.where(a[:, None] >= a[None, :], 0.0, NEG).astype(np.float32)
                tri_next = np.where(a[:, None] <= a[None, :], 0.0, NEG).astype(np.float32)
                full = np.full((128, 128), NEG, np.float32)
                first_prev = full if G0 == 0 else tri_prev
                last_next = full if G0 + NTo == NT else tri_next
                m["dmask"] = np.ascontiguousarray(np.stack([first_prev, last_next, tri_prev, tri_next], axis=1))
                m["lamv"] = np.ascontiguousarray(np.broadcast_to(p["c_lambda"][0].reshape(1, 256), (128, 256)))
                m["subln"] = np.ascontiguousarray(np.broadcast_to((p["c_subln"][0])[None, :], (128, 128)))
                m["sinks"] = np.ascontiguousarray(np.broadcast_to(p["d_sinks"][0][None, :], (128, 8)))
                m["fnorm"] = np.ascontiguousarray(np.broadcast_to(p["final_norm"][None, :], (128, D_MODEL)))
            maps.append(m)
    return maps


def prep_fused_inputs(SEQ, xs, xcs, p):
    m0 = prep_layer_inputs(0, SEQ, xs, xcs, p)
    m1 = prep_layer_inputs(1, SEQ, xs, xcs, p)
    shared = ("x_u", "xc", "cvec", "ident", "blk", "perm", "ropeC", "ropeS")
    maps = []
    for i, (a, b) in enumerate(zip(m0, m1)):
        half = i % 2
        m = {k: a[k] for k in shared}
        for k, v in a.items():
            if k not in shared:
                m["l0_" + k] = v
        for k, v in b.items():
            if k not in shared:
                m["l1_" + k] = v
        sel = np.zeros((128, 4), np.float32)
        sel[:, 0] = 1.0 if half == 1 else 0.0
        sel[:, 1] = 1.0 if half == 0 else 0.0
        sel[:, 2] = 1.0 if half == 1 else 0.0
        sel[:, 3] = 1.0 if half == 0 else 0.0
        m["sel"] = sel
        maps.append(m)
    return maps


def run_fused(SEQ, xs, xcs, p, runner=None):
    B = xs.shape[0]
    key = ("fused", SEQ, B)
    if key not in _NC_CACHE:
        _NC_CACHE[key] = build_fused(SEQ, B)
    nc = _NC_CACHE[key]
    maps = prep_fused_inputs(SEQ, xs, xcs, p)
    if runner is None:
        res = run_bass_kernel_spmd(nc, maps, core_ids=list(range(len(maps)))).results
    else:
        res = runner(nc, maps)
    HALF = SEQ // 2
    xo = np.zeros_like(xs)
    for b in range(B):
        for half in range(2):
            xo[b, half * HALF:(half + 1) * HALF] = res[2 * b + half]["out_x"]
    return xo


_NC_CACHE = {}


def kernel(x, c, ctx, c_ctx, w_mod, b_mod, w_in_even, w_out_even, a_q_norm, a_k_norm, b_rpb,
           w_in_odd, w_out_odd, c_lambda, c_subln, d_sinks, final_norm):
    p = dict(c=np.asarray(c, np.float32), c_ctx=np.asarray(c_ctx, np.float32), w_mod=np.asarray(w_mod, np.float32),
             b_mod=np.asarray(b_mod, np.float32), w_in_even=np.asarray(w_in_even, np.float32),
             w_out_even=np.asarray(w_out_even, np.float32), a_q_norm=np.asarray(a_q_norm, np.float32),
             a_k_norm=np.asarray(a_k_norm, np.float32), b_rpb=np.asarray(b_rpb, np.float32),
             w_in_odd=np.asarray(w_in_odd, np.float32), w_out_odd=np.asarray(w_out_odd, np.float32),
             c_lambda=np.asarray(c_lambda, np.float32), c_subln=np.asarray(c_subln, np.float32),
             d_sinks=np.asarray(d_sinks, np.float32), final_norm=np.asarray(final_norm, np.float32))
    xs = np.asarray(x, np.float32)
    xcs = np.asarray(ctx, np.float32)
    SEQ = xs.shape[1]
    return run_fused(SEQ, xs, xcs, p)
```

```python
import contextlib
import math
import numpy as np
import concourse.bass as bass
import concourse.mybir as mybir
from concourse.bass_utils import run_bass_kernel_spmd

F32 = mybir.dt.float32
BF16 = mybir.dt.bfloat16
AF = mybir.ActivationFunctionType
ALU = mybir.AluOpType

D_MODEL = 1024
CTX = 256
HD = 64
GRID_W = 64
SCALE = HD ** -0.5
EPS = 1e-6
NEG = -30000.0
HALO = 512
LQ = "sp"
SQ = "pool"


class Tile:
    __slots__ = ("name", "t", "last_w", "readers", "dsem", "dcount", "excl")

    def __init__(self, name, t=None, excl=False):
        self.name = name
        self.t = t
        self.last_w = None
        self.readers = {}
        self.dsem = None
        self.dcount = 0
        self.excl = excl

    def __getitem__(self, idx):
        return self.t[idx]


class Sched:
    def __init__(self, nc, stack):
        self.nc = nc
        self.stack = stack
        self.sem_stack = stack
        self.dtiles = []
        self.engs = {}
        self.sems = {}
        for en, e in (("pe", nc.tensor), ("act", nc.scalar), ("dve", nc.vector),
                      ("pool", nc.gpsimd), ("sp", nc.sync)):
            self.sems[en] = stack.enter_context(nc.semaphore("s_" + en))
            self.engs[en] = dict(eng=e, count=0, seen={}, key=en)
        self.epoch = 0
        self.nsem = 0
        self.prefix = ""
        self.free_dsems = []

    def sb(self, name, shape, dt):
        return Tile(name, self.stack.enter_context(self.nc.sbuf_tensor("sb_" + self.prefix + name, list(shape), dt)))

    def ps(self, name, shape, dt=F32):
        return Tile(name, self.stack.enter_context(self.nc.psum_tensor("ps_" + name, list(shape), dt)), excl=True)

    def _dsem(self, tile):
        if tile.dsem is None:
            if self.free_dsems:
                key, cnt = self.free_dsems.pop()
                tile.dsem = key
                tile.dcount = cnt
            else:
                key = "d%d" % self.nsem
                self.nsem += 1
                tile.dsem = key
                self.sems[key] = self.sem_stack.enter_context(self.nc.semaphore(key))
            self.dtiles.append(tile)
        return tile.dsem

    def new_epoch(self):
        self.epoch += 1
        for en, E in self.engs.items():
            key = "%s#%d" % (en, self.epoch)
            self.sems[key] = self.sem_stack.enter_context(self.nc.semaphore("s_%s_%d" % (en, self.epoch)))
            E["key"] = key
            E["count"] = 0

    def release_dsems(self):
        for t in self.dtiles:
            self.free_dsems.append((t.dsem, t.dcount))
            t.dsem = None
        self.dtiles = []

    def _wait_deps(self, en, reads, writes):
        E = self.engs[en]
        deps = {}

        def add(ev):
            if ev is None:
                return
            k, v = ev
            if deps.get(k, 0) < v:
                deps[k] = v

        me = E["key"]
        for t in reads:
            add(t.last_w)
            if t.excl:
                for k, v in t.readers.items():
                    if k != me:
                        add((k, v))
        for t in writes:
            add(t.last_w)
            for k, v in t.readers.items():
                if k != me:
                    add((k, v))
        for k, v in deps.items():
            if E["seen"].get(k, 0) < v:
                E["seen"][k] = v
                if k == me and en == "pe":
                    continue
                E["eng"].wait_ge(self.sems[k], v)

    def op(self, en, fn, reads=(), writes=()):
        E = self.engs[en]
        self._wait_deps(en, reads, writes)
        ins = fn(E["eng"])
        E["count"] += 1
        me = E["key"]
        ins.then_inc(self.sems[me], 1)
        for t in reads:
            t.readers[me] = E["count"]
        for t in writes:
            t.last_w = (me, E["count"])
            t.readers = {}
        return ins

    def dma(self, q, out, in_, reads=(), writes=(), sem_tile=None):
        E = self.engs[q]
        self._wait_deps(q, reads, writes)
        st = sem_tile if sem_tile is not None else (list(writes) + list(reads))[0]
        key = self._dsem(st)
        ins = E["eng"].dma_start(out=out, in_=in_)
        st.dcount += 16
        ins.then_inc(self.sems[key], 16)
        for t in reads:
            t.readers[key] = st.dcount
        for t in writes:
            t.last_w = (key, st.dcount)
            t.readers = {}
        return ins

    def barrier(self):
        for en, E in self.engs.items():
            for en2, E2 in self.engs.items():
                k2 = E2["key"]
                if en2 != en and E2["count"] and E["seen"].get(k2, 0) < E2["count"]:
                    E["seen"][k2] = E2["count"]
                    E["eng"].wait_ge(self.sems[k2], E2["count"])
            for t in self.dtiles:
                if t.dcount and E["seen"].get(t.dsem, 0) < t.dcount:
                    E["seen"][t.dsem] = t.dcount
                    E["eng"].wait_ge(self.sems[t.dsem], t.dcount)

    def finish(self, tiles, en="sp"):
        self._wait_deps(en, tiles, tiles)


class Rot:
    def __init__(self, tiles):
        self.tiles = tiles
        self.i = 0

    def next(self):
        t = self.tiles[self.i % len(self.tiles)]
        self.i += 1
        return t


def layer_cfg(layer):
    if layer == 0:
        qa, ka, va, qb, kb, vb, z = 0, 512, 640, 768, 1280, 1792, 2304
        fm = []
        for c in range(4):
            cols = list(range(qa + c * 64, qa + c * 64 + 64)) + list(range(qa + (4 + c) * 64, qa + (4 + c) * 64 + 64))
            fm.append(("qa%d" % c, cols, "q", True))
        fm.append(("ka", list(range(ka, ka + 128)), "k", True))
        for c in range(4):
            fm.append(("qb%d" % c, list(range(qb + c * 128, qb + c * 128 + 128)), None, False))
        for c in range(4):
            fm.append(("kb%d" % c, list(range(kb + c * 128, kb + c * 128 + 128)), None, False))
        tm = [("va", va, 128), ("vb", vb, 512), ("z", z, 1024)]
        dense_k, dense_v = ["ka"], ["va"]
        local_k, local_v = ["kb0", "kb1", "kb2", "kb3"], ["vb"]
    else:
        qc, kc, vc, qd, kd, vd, z = 0, 512, 1024, 1536, 2048, 2176, 2304
        fm = []
        for c in range(4):
            fm.append(("qc%d" % c, list(range(qc + c * 128, qc + c * 128 + 128)), None, True))
        for c in range(4):
            fm.append(("kc%d" % c, list(range(kc + c * 128, kc + c * 128 + 128)), None, True))
        for c in range(4):
            cols = list(range(qd + c * 64, qd + c * 64 + 64)) + list(range(qd + (4 + c) * 64, qd + (4 + c) * 64 + 64))
            fm.append(("qd%d" % c, cols, None, True))
        fm.append(("kd", list(range(kd, kd + 128)), None, True))
        tm = [("vc", vc, 512), ("vd", vd, 128), ("z", z, 1024)]
        dense_k, dense_v = ["kc0", "kc1", "kc2", "kc3"], ["vc"]
        local_k, local_v = ["kd"], ["vd"]
    return dict(fm=fm, tm=tm, dense_k=dense_k, dense_v=dense_v, local_k=local_k, local_v=local_v)


def lambda_init(layer):
    return 0.8 - 0.6 * math.exp(-0.3 * layer)


def emit_layer(nc, S, SH, layer, SEQ):
    HALF = SEQ // 2
    EXT = HALF + 2 * HALO
    NU = EXT + HALF + CTX
    NTo = HALF // 128
    NBo = HALF // 512
    U_OWN = HALO
    U_O = EXT
    U_C = EXT + HALF
    NKD = 2 * HALF + CTX
    NKC = NKD // 128
    last = layer == 1
    cfg = layer_cfg(layer)
    fm, tm = cfg["fm"], cfg["tm"]
    NFM = len(fm)
    fmi = {f[0]: i for i, f in enumerate(fm)}
    NCOL = NFM * 128 + sum(t[2] for t in tm)
    tmoff = {}
    o = NFM * 128
    for name, _, n in tm:
        tmoff[name] = o
        o += n
    lam0 = lambda_init(layer)

    LP = "l%d_" % layer
    S.prefix = LP

    def din(name, shape):
        return nc.dram_tensor(LP + name, list(shape), F32, kind="ExternalInput").ap()

    x_u, xc_in, cvec = SH["x_u"], SH["xc"], SH["cvec"]
    ident_d, blk_d, perm_d, ropeC, ropeS = SH["ident"], SH["blk"], SH["perm"], SH["ropeC"], SH["ropeS"]
    x1_loc, xc1_loc, GA = SH["x1_loc"], SH["xc1_loc"], SH["GA"]
    w_mod = din("w_mod", [8, 128, 3072])
    b_mod = din("b_mod", [128, 24])
    bgate = din("bgate", [128, D_MODEL])
    w_in = din("w_in", [8, 128, NCOL])
    w_out = din("w_out", [8, 128, D_MODEL])
    if layer == 0:
        gains = din("gains", [128, 2])
        bias_i = din("bias_i", [128, 8, 5 * 128])
        bias_e = din("bias_e", [4, 8, 128, 7 * 128])
    else:
        dmask = din("dmask", [128, 4, 128])
        lamv = din("lamv", [128, 256])
        subln = din("subln", [128, 128])
        sinks = din("sinks", [128, 8])
        fnorm = din("fnorm", [128, D_MODEL])
    if last:
        out_x = nc.dram_tensor("out_x", [HALF, D_MODEL], F32, kind="ExternalOutput").ap()
    else:
        out_x = x1_loc
        out_c = xc1_loc

    fmS = nc.dram_tensor(LP + "fmS", [NFM, 128, NU], BF16).ap()
    vdims = {"va": (2, 65), "vb": (8, 65), "vc": (4, 129), "vd": (2, 65)}
    vS = {}
    for name, _, n in tm:
        if name != "z":
            h, d = vdims[name]
            vS[name] = nc.dram_tensor(LP + "vS_" + name, [NU, h * d], BF16).ap()
    zS = nc.dram_tensor(LP + "zS", [NU, D_MODEL], BF16).ap()

    with contextlib.ExitStack() as st:
        S.stack = st
        st1 = contextlib.ExitStack()
        fm_reg = Tile("fm_reg")
        v_reg = Tile("v_reg")
        z_reg = Tile("z_reg")

        ident_f = S.sb("ident_f", [128, 128], F32)
        ident = S.sb("ident", [128, 128], BF16)
        blk = S.sb("blk", [128, 128], F32)
        perm = S.sb("perm", [128, 128], F32)
        epst = S.sb("epst", [128, 1], F32)
        zeros = S.sb("zeros", [128, 128], F32)
        junk = S.sb("junk", [128, D_MODEL], F32)
        ss_r = Rot([S.sb("ss%d" % i, [128, 2], F32) for i in range(2)])
        t1_r = Rot([S.sb("t1%d" % i, [128, 512], F32) for i in range(2)])
        gate_bc = S.sb("gate_bc", [128, 2, D_MODEL], F32)
        modt = S.sb("modt", [128, 24, 2], F32)
        bmodt = S.sb("bmodt", [128, 24], F32)
        cv = S.sb("cv", [128, 8, 2], F32)
        pTb_t = SH["pTb_t"]
        pTb = pTb_t.t
        banks = SH["banks"]
        sel_t = S.sb("sel_t", [128, 4], F32)
        S.dma(LQ, sel_t[:], SH["sel"], writes=[sel_t])
        xt2 = S.sb("xt2", [128, D_MODEL], F32)
        G_reg = SH["G_reg"]
        NTo_ = HALF // 128

        RCt = min(512, HALF) // 128

        def grow(r, t):
            return ((t // RCt) * 2 * RCt + r * RCt + (t % RCt)) * 128

        def load_x_tile(xt, utile):
            e = utile
            if layer == 0:
                if e >= (EXT + HALF) // 128:
                    c = e - (EXT + HALF) // 128
                    S.dma(LQ, xt[:], xc_in[c * 128:(c + 1) * 128, :], writes=[xt])
                else:
                    S.dma(LQ, xt[:], x_u[e * 128:(e + 1) * 128, :], writes=[xt])
                return
            if e >= (EXT + HALF) // 128:
                c = e - (EXT + HALF) // 128
                S.dma(LQ, xt[:], xc1_loc[c * 128:(c + 1) * 128, :], writes=[xt])
            elif e >= EXT // 128:
                o = e - EXT // 128
                S.dma(LQ, xt[:], GA[grow(0, o):grow(0, o) + 128, :], reads=[G_reg], writes=[xt])
                S.dma(LQ, xt2[:], GA[grow(1, o):grow(1, o) + 128, :], reads=[G_reg], writes=[xt2])
                S.op("act", lambda en: en.activation(out=xt[:], in_=xt[:], func=AF.Copy, scale=sel_t[:, 2:3]), reads=[xt, sel_t], writes=[xt])
                S.op("dve", lambda en: en.scalar_tensor_tensor(out=xt[:], in0=xt2[:], scalar=sel_t[:, 3:4], in1=xt[:], op0=ALU.mult, op1=ALU.add),
                     reads=[xt2, sel_t, xt], writes=[xt])
            elif 4 <= e < 4 + NTo_:
                t = e - 4
                S.dma(LQ, xt[:], x1_loc[t * 128:(t + 1) * 128, :], writes=[xt])
            elif e < 4:
                gt = grow(0, NTo_ - 4 + e)
                S.dma(LQ, xt[:], GA[gt:gt + 128, :], reads=[G_reg], writes=[xt])
                S.op("act", lambda en: en.activation(out=xt[:], in_=xt[:], func=AF.Copy, scale=sel_t[:, 0:1]), reads=[xt, sel_t], writes=[xt])
            else:
                gt = grow(1, e - 4 - NTo_)
                S.dma(LQ, xt[:], GA[gt:gt + 128, :], reads=[G_reg], writes=[xt])
                S.op("act", lambda en: en.activation(out=xt[:], in_=xt[:], func=AF.Copy, scale=sel_t[:, 1:2]), reads=[xt, sel_t], writes=[xt])

        S.dma(LQ, ident_f[:], ident_d, writes=[ident_f])
        S.dma(LQ, blk[:], blk_d, writes=[blk])
        S.dma(LQ, perm[:], perm_d, writes=[perm])
        S.dma(LQ, cv[:], cvec, writes=[cv])
        S.dma(LQ, bmodt[:], b_mod, writes=[bmodt])
        S.dma(LQ, gate_bc[:, 0, :], bgate, writes=[gate_bc])
        S.op("dve", lambda e: e.tensor_copy(out=ident[:], in_=ident_f[:]), reads=[ident_f], writes=[ident])
        S.op("pool", lambda e: e.memset(epst[:], EPS), writes=[epst])
        S.op("pool", lambda e: e.memset(zeros[:], 0.0), writes=[zeros])
        S.op("dve", lambda e: e.tensor_copy(out=gate_bc[:, 1, :], in_=gate_bc[:, 0, :]), reads=[gate_bc], writes=[gate_bc])
        S.stack = st
        if layer == 0:
            gn = S.sb("gn", [128, 2], F32)
            bi_t = S.sb("bi_t", [128, 8, 640], F32)
            S.dma(LQ, gn[:], gains, writes=[gn])
            S.dma(LQ, bi_t[:], bias_i, writes=[bi_t])
        else:
            dm_t = S.sb("dm_t", [128, 4, 128], F32)
            lam_t = S.sb("lam_t", [128, 256], F32)
            sub_t = S.sb("sub_t", [128, 128], F32)
            snk_t = S.sb("snk_t", [128, 8], F32)
            fn_t = S.sb("fn_t", [128, D_MODEL], F32)
            S.dma(LQ, dm_t[:], dmask, writes=[dm_t])
            S.dma(LQ, lam_t[:], lamv, writes=[lam_t])
            S.dma(LQ, sub_t[:], subln, writes=[sub_t])
            S.dma(LQ, snk_t[:], sinks, writes=[snk_t])
            S.dma(LQ, fn_t[:], fnorm, writes=[fn_t])
            lam_s = S.sb("lam_s", [128, 4], F32)
            lprod = S.sb("lprod", [128, 128], F32)
            esnk = S.sb("esnk", [128, 8], F32)
        S.stack = st1
        win = S.sb("win", [128, 8, NCOL], BF16)
        sel2 = S.sb("sel2", [2, 2, 128], F32)
        S.dma(LQ, sel2[:], SH["sel2"], writes=[sel2])
        stage = Rot([S.sb("stage%d" % i, [128, 1024], F32) for i in range(2)])

        S.op("act", lambda e: e.activation(out=cv[:], in_=cv[:], func=AF.Silu), reads=[cv], writes=[cv])
        rowb = [banks[1], banks[2], banks[3], banks[4], banks[5], banks[6]]
        for k in range(8):
            for pi in range(3):
                stg = stage.next()
                S.dma(LQ, stg[:], w_mod[k, :, pi * 1024:(pi + 1) * 1024], writes=[stg])
                for n in range(2):
                    rb = rowb[pi * 2 + n]
                    S.op("pe", lambda e: e.matmul(rb[0:2, 0:512], lhsT=cv[:, k, :], rhs=stg[:, n * 512:(n + 1) * 512],
                                                  start=(k == 0), stop=(k == 7)), reads=[stg, cv], writes=[rb])
        modrow = S.sb("modrow", [2, 3072], F32)
        for i6 in range(6):
            S.op("act" if i6 % 2 else "dve",
                 (lambda e: e.activation(out=modrow[0:2, i6 * 512:(i6 + 1) * 512], in_=rowb[i6][0:2, 0:512], func=AF.Copy)) if i6 % 2 else
                 (lambda e: e.tensor_copy(out=modrow[0:2, i6 * 512:(i6 + 1) * 512], in_=rowb[i6][0:2, 0:512])),
                 reads=[rowb[i6]], writes=[modrow])
        pmod = banks[7]
        for j in range(16):
            S.op("pe", lambda e: e.transpose(out=pmod[:, j * 2:j * 2 + 2], in_=modrow[0:2, j * 128:(j + 1) * 128], identity=ident_f[0:2, 0:2]),
                 reads=[modrow, ident_f], writes=[pmod])
        pg = [banks[1], banks[2], banks[3], banks[4]]
        for w in range(2):
            for n in range(2):
                S.op("pe", lambda e: e.matmul(pg[w * 2 + n][:], lhsT=sel2[0:2, w, :], rhs=modrow[0:2, 2048 + n * 512:2048 + (n + 1) * 512],
                                              start=True, stop=True), reads=[sel2, modrow], writes=[pg[w * 2 + n]])
        for w in range(2):
            S.op("dve", lambda e: e.tensor_tensor(out=modt[:, 0:16, w], in0=pmod[:, 0:32].rearrange("p (j w) -> p j w", w=2)[:, :, w],
                                                  in1=bmodt[:, 0:16], op=ALU.add), reads=[pmod, bmodt], writes=[modt])
            for n in range(2):
                S.op("dve", lambda e: e.tensor_tensor(out=gate_bc[:, w, n * 512:(n + 1) * 512], in0=pg[w * 2 + n][:],
                                                      in1=gate_bc[:, w, n * 512:(n + 1) * 512], op=ALU.add),
                     reads=[pg[w * 2 + n], gate_bc], writes=[gate_bc])
        S.op("dve", lambda e: e.tensor_scalar_add(out=modt[:, 8:16, :], in0=modt[:, 8:16, :], scalar1=1.0),
             reads=[modt], writes=[modt])

        cnt = 0
        for k in range(8):
            for c0 in range(0, NCOL, 1024):
                cn = min(1024, NCOL - c0)
                stg = stage.next()
                S.dma(LQ, stg[:, 0:cn], w_in[k, :, c0:c0 + cn], writes=[stg])
                S.op("dve", lambda e: e.tensor_copy(out=win[:, k, c0:c0 + cn], in_=stg[:, 0:cn]), reads=[stg], writes=[win])
                cnt += 1

        if layer == 1:
            S.op("dve", lambda e: e.tensor_tensor(out=lprod[:, 0:64], in0=lam_t[:, 0:64], in1=lam_t[:, 64:128], op=ALU.mult), reads=[lam_t], writes=[lprod])
            S.op("dve", lambda e: e.tensor_tensor(out=lprod[:, 64:128], in0=lam_t[:, 128:192], in1=lam_t[:, 192:256], op=ALU.mult), reads=[lam_t, lprod], writes=[lprod])
            S.op("dve", lambda e: e.reduce_sum(out=lam_s[:, 0:2], in_=lprod[:].rearrange("p (a d) -> p a d", a=2), axis=mybir.AxisListType.X), reads=[lprod], writes=[lam_s])
            S.op("act", lambda e: e.activation(out=lam_s[:, 0:2], in_=lam_s[:, 0:2], func=AF.Exp), reads=[lam_s], writes=[lam_s])
            S.op("dve", lambda e: e.tensor_tensor(out=lam_s[:, 2:3], in0=lam_s[:, 0:1], in1=lam_s[:, 1:2], op=ALU.subtract), reads=[lam_s], writes=[lam_s])
            S.op("dve", lambda e: e.tensor_scalar(out=lam_s[:, 3:4], in0=lam_s[:, 2:3], scalar1=lam0, scalar2=-1.0, op0=ALU.add, op1=ALU.mult), reads=[lam_s], writes=[lam_s])
            S.op("act", lambda e: e.activation(out=esnk[:], in_=snk_t[:], func=AF.Exp), reads=[snk_t], writes=[esnk])

        xt_r = Rot([S.sb("xt%d" % i, [128, D_MODEL], F32) for i in range(2)])
        xn_r = Rot([S.sb("xn%d" % i, [128, D_MODEL], BF16) for i in range(2)])
        hT_r = Rot([S.sb("hT%d" % i, [128, 8, 512], BF16) for i in range(2)])
        tabC_r = Rot([S.sb("tabC%d" % i, [128, 512], F32) for i in range(1)])
        tabS_r = Rot([S.sb("tabS%d" % i, [128, 512], F32) for i in range(1)])
        sq_r = Rot([S.sb("sq%d" % i, [128, 512], F32) for i in range(2)])
        rs_r = Rot([S.sb("rs%d" % i, [128, 512], F32) for i in range(2)])
        qn_r = Rot([S.sb("qn%d" % i, [128, 512], F32) for i in range(3)])
        t2_r = Rot([S.sb("t2%d" % i, [128, 512], F32) for i in range(2)])
        fo_r = Rot([S.sb("fo%d" % i, [128, 512], BF16) for i in range(4)])
        vst = {}
        for name in vS:
            h, d = vdims[name]
            vst[name] = Rot([S.sb("vst_%s%d" % (name, i), [128, h, d], BF16) for i in range(2)])
            for t in vst[name].tiles:
                S.op("pool", lambda e: e.memset(t[:], 1.0), writes=[t])
        zst_r = Rot([S.sb("zst%d" % i, [128, D_MODEL], BF16) for i in range(2)])
        bA = Rot([banks[1], banks[2], banks[3]])
        bB = Rot([banks[4], banks[5]])

        def p1_prep(bd):
            u0, ntiles, w = bd["u0"], bd["ntiles"], bd["w"]
            hT = hT_r.next()
            bd["hT"] = hT
            for ti in range(ntiles):
                xt = xt_r.next()
                ss = ss_r.next()
                xn = xn_r.next()
                load_x_tile(xt, u0 // 128 + ti)
                S.op("act", lambda e: e.activation(out=junk[:], in_=xt[:], func=AF.Square, accum_out=ss[:, 0:1]), reads=[xt], writes=[junk, ss])
                S.op("act", lambda e: e.activation(out=ss[:, 1:2], in_=ss[:, 0:1], func=AF.Ln, scale=1.0 / D_MODEL, bias=epst[:]), reads=[ss, epst], writes=[ss])
                S.op("act", lambda e: e.activation(out=ss[:, 1:2], in_=ss[:, 1:2], func=AF.Exp, scale=-0.5), reads=[ss], writes=[ss])
                S.op("act", lambda e: e.activation(out=xn[:], in_=xt[:], func=AF.Copy, scale=ss[:, 1:2]), reads=[xt, ss], writes=[xn])
                for k in range(8):
                    S.op("pe", lambda e: e.transpose(out=pTb[:, k, :], in_=xn[:, k * 128:(k + 1) * 128], identity=ident[:]),
                         reads=[xn, ident], writes=[pTb_t])
                for k in range(8):
                    S.op("dve", lambda e: e.tensor_scalar(out=hT[:, k, ti * 128:(ti + 1) * 128], in0=pTb[:, k, :],
                                                          scalar1=modt[:, 8 + k, w:w + 1], scalar2=modt[:, k, w:w + 1],
                                                          op0=ALU.mult, op1=ALU.add), reads=[pTb_t, modt], writes=[hT])
                yield

        def p1_mm(bd):
            u0, ntiles, fm_list, tm_list, rope_col0, hT = bd["u0"], bd["ntiles"], bd["fm_list"], bd["tm_list"], bd["rope_col0"], bd["hT"]
            ntok = ntiles * 128
            rope_needed = any(fm[i][3] for i in fm_list) and rope_col0 is not None
            if rope_needed:
                tC = tabC_r.next()
                tS = tabS_r.next()
                S.dma(LQ, tC[:, 0:ntok], ropeC[:, rope_col0:rope_col0 + ntok], writes=[tC])
                S.dma(LQ, tS[:, 0:ntok], ropeS[:, rope_col0:rope_col0 + ntok], writes=[tS])
            chs = [dict(i=i) for i in fm_list]

            def st_main(ch):
                i = ch["i"]
                pa = bA.next()
                for k in range(8):
                    S.op("pe", lambda e: e.matmul(pa[:, 0:ntok], lhsT=win[:, k, i * 128:(i + 1) * 128], rhs=hT[:, k, 0:ntok],
                                                  start=(k == 0), stop=(k == 7)), reads=[win, hT], writes=[pa])
                ch["pa"] = pa

            def st_norm(ch):
                i = ch["i"]
                name, _, nkind, roped = fm[i]
                pa = ch["pa"]
                fo = fo_r.next()
                ch["fo"] = fo
                do_rope = roped and rope_col0 is not None
                ch["do_rope"] = do_rope
                if nkind is None and not do_rope:
                    S.op("act", lambda e: e.activation(out=fo[:, 0:ntok], in_=pa[:, 0:ntok], func=AF.Copy), reads=[pa], writes=[fo])
                    return
                qn = qn_r.next()
                ch["qn"] = qn
                if nkind is not None:
                    sq = sq_r.next()
                    rs = rs_r.next()
                    pb = bB.next()
                    gcol = 0 if nkind == "q" else 1
                    S.op("act", lambda e: e.activation(out=sq[:, 0:ntok], in_=pa[:, 0:ntok], func=AF.Square), reads=[pa], writes=[sq])
                    S.op("pe", lambda e: e.matmul(pb[:, 0:ntok], lhsT=blk[:], rhs=sq[:, 0:ntok], start=True, stop=True), reads=[blk, sq], writes=[pb])
                    S.op("act", lambda e: e.activation(out=rs[:, 0:ntok], in_=pb[:, 0:ntok], func=AF.Ln, bias=epst[:]), reads=[pb, epst], writes=[rs])
                    S.op("act", lambda e: e.activation(out=rs[:, 0:ntok], in_=rs[:, 0:ntok], func=AF.Exp, scale=-0.5), reads=[rs], writes=[rs])
                    dst = qn if do_rope else fo
                    S.op("dve", lambda e: e.scalar_tensor_tensor(out=dst[:, 0:ntok], in0=pa[:, 0:ntok], scalar=gn[:, gcol:gcol + 1],
                                                                 in1=rs[:, 0:ntok], op0=ALU.mult, op1=ALU.mult),
                         reads=[pa, gn, rs], writes=[dst])
                else:
                    S.op("act", lambda e: e.activation(out=qn[:, 0:ntok], in_=pa[:, 0:ntok], func=AF.Copy), reads=[pa], writes=[qn])

            def st_rope(ch):
                i = ch["i"]
                fo = ch["fo"]
                if ch["do_rope"]:
                    qn = ch["qn"]
                    pb2 = bB.next()
                    t1 = t1_r.next()
                    t2 = t2_r.next()
                    S.op("pe", lambda e: e.matmul(pb2[:, 0:ntok], lhsT=perm[:], rhs=qn[:, 0:ntok], start=True, stop=True), reads=[perm, qn], writes=[pb2])
                    S.op("dve", lambda e: e.tensor_tensor(out=t1[:, 0:ntok], in0=qn[:, 0:ntok], in1=tC[:, 0:ntok], op=ALU.mult), reads=[qn, tC], writes=[t1])
                    S.op("dve", lambda e: e.tensor_tensor(out=t2[:, 0:ntok], in0=pb2[:, 0:ntok], in1=tS[:, 0:ntok], op=ALU.mult), reads=[pb2, tS], writes=[t2])
                    S.op("dve", lambda e: e.tensor_tensor(out=fo[:, 0:ntok], in0=t1[:, 0:ntok], in1=t2[:, 0:ntok], op=ALU.add), reads=[t1, t2], writes=[fo])
                S.dma(SQ, fmS[i, :, u0:u0 + ntok], fo[:, 0:ntok], reads=[fo], writes=[fm_reg], sem_tile=fo)

            nchs = len(chs)
            for step in range(nchs + 2):
                if step < nchs:
                    st_main(chs[step])
                if 0 <= step - 1 < nchs:
                    st_norm(chs[step - 1])
                if 0 <= step - 2 < nchs:
                    st_rope(chs[step - 2])
                yield
            for name in tm_list:
                col0 = tmoff[name]
                ncols = dict((t[0], t[2]) for t in tm)[name]
                for ti in range(ntiles):
                    for n0 in range(0, ncols, 512):
                        nn = min(512, ncols - n0)
                        pa = bA.next()
                        for k in range(8):
                            S.op("pe", lambda e: e.matmul(pa[:, 0:nn], lhsT=hT[:, k, ti * 128:(ti + 1) * 128],
                                                          rhs=win[:, k, col0 + n0:col0 + n0 + nn], start=(k == 0), stop=(k == 7)),
                                 reads=[win, hT], writes=[pa])
                        if name == "z":
                            if n0 == 0:
                                zst = zst_r.next()
                            S.op("act", lambda e: e.activation(out=zst[:, n0:n0 + nn], in_=pa[:, 0:nn], func=AF.Silu), reads=[pa], writes=[zst])
                            if n0 + nn == ncols:
                                S.dma(SQ, zS[u0 + ti * 128:u0 + (ti + 1) * 128, :], zst[:], reads=[zst], writes=[z_reg], sem_tile=zst)
                        else:
                            h, d = vdims[name]
                            dv = d - 1
                            vt = vst[name].next()
                            S.op("dve", lambda e: e.tensor_copy(out=vt[:, :, 0:dv], in_=pa[:, 0:nn].rearrange("p (h d) -> p h d", d=dv)),
                                 reads=[pa], writes=[vt])
                            S.dma(SQ, vS[name][u0 + ti * 128:u0 + (ti + 1) * 128, :], vt[:].rearrange("p h d -> p (h d)"),
                                  reads=[vt], writes=[v_reg], sem_tile=vt)
                        yield


        all_fm = list(range(NFM))
        all_tm = [t[0] for t in tm]
        lk = [fmi[n] for n in cfg["local_k"]]
        dk = [fmi[n] for n in cfg["dense_k"]]
        blocks = [dict(u0=U_C, ntiles=2, w=1, fm_list=all_fm, tm_list=all_tm, rope_col0=None)]
        eblocks = list(range(EXT // 512))
        eblocks = [b for b in eblocks if HALO <= b * 512 < HALO + HALF] + [b for b in eblocks if not (HALO <= b * 512 < HALO + HALF)]
        for b in eblocks:
            u0 = b * 512
            own = HALO <= u0 < HALO + HALF
            if own:
                blocks.append(dict(u0=u0, ntiles=4, w=0, fm_list=all_fm, tm_list=all_tm, rope_col0=u0))
            else:
                blocks.append(dict(u0=u0, ntiles=4, w=0, fm_list=lk, tm_list=cfg["local_v"], rope_col0=u0))
        for b in range(HALF // 512):
            u0 = U_O + b * 512
            blocks.append(dict(u0=u0, ntiles=4, w=0, fm_list=dk, tm_list=cfg["dense_v"], rope_col0=u0))
        for _ in p1_prep(blocks[0]):
            pass
        for bi, bd in enumerate(blocks):
            gm = p1_mm(bd)
            gp = p1_prep(blocks[bi + 1]) if bi + 1 < len(blocks) else None
            nsteps = len(bd["fm_list"]) + 2 + sum(((dict((t[0], t[2]) for t in tm)[nm] + 511) // 512) * bd["ntiles"] for nm in bd["tm_list"])
            ntl = blocks[bi + 1]["ntiles"] if gp is not None else 0
            every = max(1, nsteps // (ntl + 1)) if ntl else 0
            k = 0
            for _ in gm:
                k += 1
                if gp is not None and every and k % every == 0:
                    next(gp, None)
            if gp is not None:
                for _ in gp:
                    pass

        S.barrier()
        st1.close()
        st2 = contextlib.ExitStack()
        S.stack = st2
        wout = S.sb("wout", [128, 8, D_MODEL], BF16)
        stage2 = Rot([S.sb("stage2_%d" % i, [128, D_MODEL], F32) for i in range(2)])
        for k in range(8):
            stg = stage2.next()
            S.dma(LQ, stg[:], w_out[k], writes=[stg])
            S.op("dve", lambda e: e.tensor_copy(out=wout[:, k, :], in_=stg[:]), reads=[stg], writes=[wout])
        bS = Rot([banks[1], banks[2], banks[3]])
        bACC = Rot([banks[5], banks[6], banks[7]])
        nbuf_d = 1 if layer == 0 else 2
        KT_r = Rot([S.sb("KT%d" % i, [128, NKD], BF16) for i in range(nbuf_d)])
        VDW = 130 if layer == 0 else 129
        VD_r = Rot([S.sb("VD%d" % i, [128, NKC, VDW], BF16) for i in range(nbuf_d)])
        q_r = Rot([S.sb("qblk%d" % i, [128, 512], BF16) for i in range(6)])
        NLK = 9
        kl_r = Rot([S.sb("kl%d" % i, [128, NLK * 128], BF16) for i in range(5)])
        VLW = 130
        vl_r = Rot([S.sb("vl%d" % i, [128, NLK, VLW], BF16) for i in range(5)])
        y_t = [S.sb("y%d" % i, [128, D_MODEL], F32) for i in range(4)]
        rec_r = Rot([S.sb("rec%d" % i, [128, 8], F32) for i in range(4)])
        zl_r = Rot([S.sb("zl%d" % i, [128, D_MODEL], BF16) for i in range(1)])
        yb_r = Rot([S.sb("yb%d" % i, [128, D_MODEL], BF16) for i in range(1)])
        yT_r = Rot([S.sb("yT%d" % i, [128, 8, 128], BF16) for i in range(1)])
        xo_r = Rot([S.sb("xo%d" % i, [128, D_MODEL], F32) for i in range(1)])
        res_r = Rot([S.sb("res%d" % i, [128, D_MODEL], F32) for i in range(2)])
        be_r = Rot([S.sb("be%d" % i, [128, 7 * 128], F32) for i in range(3)]) if layer == 0 else None
        dcache = {}

        def load_dense(kname, vname, vc0, vw):
            key = (kname, vname, vc0)
            if dcache.get("key") == key:
                return dcache["KT"], dcache["VD"]
            KT = KT_r.next()
            VD = VD_r.next()
            i = fmi[kname]
            S.dma(LQ, KT[:, 0:HALF], fmS[i, :, U_OWN:U_OWN + HALF], reads=[fm_reg], writes=[KT])
            S.dma(LQ, KT[:, HALF:2 * HALF], fmS[i, :, U_O:U_O + HALF], reads=[fm_reg], writes=[KT])
            S.dma(LQ, KT[:, 2 * HALF:NKD], fmS[i, :, U_C:U_C + CTX], reads=[fm_reg], writes=[KT])
            for (c0, u0, n) in ((0, U_OWN, HALF), (HALF // 128, U_O, HALF), (2 * HALF // 128, U_C, CTX)):
                S.dma(LQ, VD[:, c0:c0 + n // 128, 0:vw], vS[vname][u0:u0 + n, vc0:vc0 + vw].rearrange("(k p) c -> p k c", p=128),
                      reads=[v_reg], writes=[VD])
            dcache.update(key=key, KT=KT, VD=VD)
            return KT, VD

        def attend(qt, pbase, NQ, kchunks, acc_list, vwidth, bias_fn=None):
            nq = NQ // 128
            n = len(kchunks)
            pend = []
            first = {}

            def issue_s(j):
                kt, kc0, vt, vap = kchunks[j]
                ps = bS.next()
                S.op("pe", lambda e: e.matmul(ps[:, 0:NQ], lhsT=kt[pbase:pbase + 64, kc0:kc0 + 128], rhs=qt[pbase:pbase + 64, 0:NQ],
                                              start=True, stop=True), reads=[kt, qt], writes=[ps])
                pT = pT_r.next()
                b = bias_fn(j) if bias_fn is not None else None
                if b is not None:
                    btile, bap = b
                    sb = sb_r.next()
                    S.op("dve", lambda e: e.scalar_tensor_tensor(out=sb[:, 0:NQ], in0=ps[:, 0:NQ], scalar=SCALE, in1=bap,
                                                                 op0=ALU.mult, op1=ALU.add), reads=[ps, btile], writes=[sb])
                    S.op("act", lambda e: e.activation(out=pT[:, 0:NQ], in_=sb[:, 0:NQ], func=AF.Exp), reads=[sb], writes=[pT])
                else:
                    S.op("act", lambda e: e.activation(out=pT[:, 0:NQ], in_=ps[:, 0:NQ], func=AF.Exp, scale=SCALE), reads=[ps], writes=[pT])
                return pT

            def issue_pv(j, pT):
                kt, kc0, vt, vap = kchunks[j]
                for s in range(nq):
                    acc, c0 = acc_list[s]
                    fst = first.get(id(acc), True)
                    first[id(acc)] = False
                    S.op("pe", lambda e: e.matmul(acc[:, c0:c0 + vwidth], lhsT=pT[:, s * 128:(s + 1) * 128], rhs=vap,
                                                  start=(j == 0 and fst), stop=(j == n - 1), skip_group_check=True),
                         reads=[pT, vt], writes=[acc])

            prev = None
            for j in range(n):
                pT = issue_s(j)
                if prev is not None:
                    issue_pv(prev[0], prev[1])
                prev = (j, pT)
            issue_pv(prev[0], prev[1])

        onesel_f = S.sb("onesel_f", [128, 2, 2], F32)
        S.op("pool", lambda e: e.memset(onesel_f[:], 0.0), writes=[onesel_f])
        S.op("pool", lambda e: e.memset(onesel_f[:, 0, 0:1], 1.0), writes=[onesel_f])
        S.op("pool", lambda e: e.memset(onesel_f[:, 1, 1:2], 1.0), writes=[onesel_f])
        den_acc = S.sb("den_acc", [128, 1024], F32) if layer == 1 else None
        onesel_b = S.sb("onesel_b", [128, 2, 2], BF16)
        S.op("dve", lambda e: e.tensor_copy(out=onesel_b[:], in_=onesel_f[:]), reads=[onesel_f], writes=[onesel_b])
        if layer == 1:
            dmb = S.sb("dmb", [128, 3, 384], F32)
            S.op("pool", lambda e: e.memset(dmb[:], 0.0), writes=[dmb])
            for var, (pi, ni) in enumerate(((2, 3), (0, 3), (2, 1))):
                S.op("dve", lambda e: e.tensor_copy(out=dmb[:, var, 0:128], in_=dm_t[:, pi, :]), reads=[dm_t], writes=[dmb])
                S.op("dve", lambda e: e.tensor_copy(out=dmb[:, var, 256:384], in_=dm_t[:, ni, :]), reads=[dm_t], writes=[dmb])
        rec16_r = Rot([S.sb("rec16_%d" % i, [128, 16], F32) for i in range(2)])
        Tq = S.sb("Tq", [128, 4, 256], F32) if layer == 1 else None
        pT2_r = Rot([S.sb("pT2_%d" % i, [128, 1024], BF16) for i in range(3)])
        oT_r = Rot([S.sb("oT%d" % i, [128, 512], F32) for i in range(2)])
        pairs = SH["pairs"]

        def dense_pair(qt, KT, VD, vap_fn, vw, accs, den=None):
            n = NKC

            def issue_s(j):
                pt, ta, tb = pairs[j % 2]
                for m, tt in ((0, ta), (1, tb)):
                    S.op("pe", lambda e: e.matmul(tt[:, 0:512], lhsT=KT[64 * m:64 * m + 64, j * 128:(j + 1) * 128],
                                                  rhs=qt[64 * m:64 * m + 64, 0:512], start=True, stop=True), reads=[KT, qt], writes=[tt])
                pT = pT2_r.next()
                S.op("act", lambda e: e.activation(out=pT[:], in_=pt[:, :], func=AF.Exp, scale=SCALE), reads=[ta, tb], writes=[pT])
                return pT

            def issue_pv(j, pT):
                for m in range(2):
                    S.op("pe", lambda e: e.matmul(accs[m][0:vw, 0:512], lhsT=vap_fn(j, m), rhs=pT[:, m * 512:(m + 1) * 512],
                                                  start=(j == 0), stop=(j == n - 1)), reads=[pT, VD], writes=[accs[m]])
                if den is not None:
                    if j == 0:
                        S.op("dve", lambda e: e.tensor_copy(out=den_acc[:, 0:512], in_=pT[:, 0:512]), reads=[pT], writes=[den_acc])
                    else:
                        S.op("dve", lambda e: e.tensor_tensor(out=den_acc[:, 0:512], in0=den_acc[:, 0:512], in1=pT[:, 0:512], op=ALU.add),
                             reads=[pT, den_acc], writes=[den_acc])
                    S.op("pe", lambda e: e.matmul(den[0:2, 0:512], lhsT=onesel_b[:, 1, :], rhs=pT[:, 512:1024],
                                                  start=(j == 0), stop=False, skip_group_check=True), reads=[pT, onesel_b], writes=[den])
                    if j == n - 1:
                        S.op("pe", lambda e: e.matmul(den[0:2, 0:512], lhsT=onesel_f[:, 0, :], rhs=den_acc[:, 0:512],
                                                      start=False, stop=True, skip_group_check=True), reads=[den_acc, onesel_f], writes=[den])

            pend = []
            for j in range(n):
                pend.append((j, issue_s(j)))
                if len(pend) > 2:
                    issue_pv(*pend.pop(0))
            while pend:
                issue_pv(*pend.pop(0))

        def untranspose(acc, rows, fin, width):
            oT = oT_r.next()
            S.op("act", lambda e: e.activation(out=oT[0:rows, :], in_=acc[0:rows, 0:512], func=AF.Copy), reads=[acc], writes=[oT])
            for sidx in range(4):
                S.op("pe", lambda e: e.transpose(out=fin[:, sidx * width:sidx * width + rows], in_=oT[0:rows, sidx * 128:(sidx + 1) * 128],
                                                 identity=ident_f[0:rows, 0:rows]), reads=[oT, ident_f], writes=[fin])

        wide_i = [0]

        def wide_a(job):
            qt, pbase, kl, nch, nb, bias = job["qt"], job["pbase"], job["kl"], job["nch"], job["nb"], job["bias"]
            pt, ta, tb = pairs[wide_i[0] % 2]
            wide_i[0] += 1
            for j in range(nch):
                tt = ta if j < 4 else tb
                S.op("pe", lambda e: e.matmul(pt[:, j * 128:(j + 1) * 128], lhsT=kl[pbase:pbase + 64, j * 128:(j + 1) * 128],
                                              rhs=qt[pbase:pbase + 64, 0:128], start=True, stop=True), reads=[kl, qt], writes=[tt])
            pT = pT2_r.next()
            used = [ta] + ([tb] if nch > 4 else [])
            if bias is not None:
                btile, bap = bias
                sbw = stage2.next()
                S.op("dve", lambda e: e.scalar_tensor_tensor(out=sbw[:, 0:nb * 128], in0=pt[:, 0:nb * 128], scalar=SCALE, in1=bap,
                                                             op0=ALU.mult, op1=ALU.add), reads=used + [btile], writes=[sbw])
                S.op("act", lambda e: e.activation(out=pT[:, 0:nb * 128], in_=sbw[:, 0:nb * 128], func=AF.Exp), reads=[sbw], writes=[pT])
                if nch > nb:
                    S.op("act", lambda e: e.activation(out=pT[:, nb * 128:nch * 128], in_=pt[:, nb * 128:nch * 128], func=AF.Exp, scale=SCALE),
                         reads=used, writes=[pT])
            else:
                S.op("act", lambda e: e.activation(out=pT[:, 0:nch * 128], in_=pt[:, 0:nch * 128], func=AF.Exp, scale=SCALE), reads=used, writes=[pT])
            job["pT"] = pT

        def wide_b(job):
            pT, vl, vc0, nch = job["pT"], job["vl"], job["vc0"], job["nch"]
            acc = bACC.next()
            for j in range(nch):
                S.op("pe", lambda e: e.matmul(acc[:, 0:65], lhsT=pT[:, j * 128:(j + 1) * 128], rhs=vl[:, j, vc0:vc0 + 65],
                                              start=(j == 0), stop=(j == nch - 1)), reads=[pT, vl], writes=[acc])
            finish_head([(acc, 0)], 65, [job["yt"]], job["ycol"], extra_den=job.get("extra_den"))

        def run_wide(jobs):
            pend = []
            for job in jobs:
                if "pre" in job:
                    job["pre"]()
                wide_a(job)
                pend.append(job)
                if len(pend) > 2:
                    wide_b(pend.pop(0))
            while pend:
                wide_b(pend.pop(0))

        def finish_head(acc_list, vwidth, y_tiles, ycol, extra_den=None, scale_ap=None):
            dv = vwidth - 1
            for s, (acc, c0) in enumerate(acc_list):
                rec = rec_r.next()
                if extra_den is not None:
                    S.op("dve", lambda e: e.tensor_tensor(out=rec[:, 0:1], in0=acc[:, c0 + dv:c0 + dv + 1], in1=extra_den, op=ALU.add),
                         reads=[acc, esnk], writes=[rec])
                    S.op("dve", lambda e: e.reciprocal(out=rec[:, 1:2], in_=rec[:, 0:1]), reads=[rec], writes=[rec])
                else:
                    S.op("dve", lambda e: e.reciprocal(out=rec[:, 1:2], in_=acc[:, c0 + dv:c0 + dv + 1]), reads=[acc], writes=[rec])
                yt = y_tiles[s]
                S.op("act", lambda e: e.activation(out=yt[:, ycol:ycol + dv], in_=acc[:, c0:c0 + dv], func=AF.Copy, scale=rec[:, 1:2]),
                     reads=[acc, rec], writes=[yt])

        def out_tile(yt, u_tok, src, src_row, w, dst, dst_row):
            zl = zl_r.next()
            yb = yb_r.next()
            yT = yT_r.next()
            xo = xo_r.next()
            res = res_r.next()
            S.dma(LQ, zl[:], zS[u_tok:u_tok + 128, :], reads=[z_reg], writes=[zl])
            load_x_tile(xo, u_tok // 128)
            S.op("dve", lambda e: e.tensor_tensor(out=yb[:], in0=yt[:], in1=zl[:], op=ALU.mult), reads=[yt, zl], writes=[yb])
            for k in range(8):
                S.op("pe", lambda e: e.transpose(out=pTb[:, k, :], in_=yb[:, k * 128:(k + 1) * 128], identity=ident[:]),
                     reads=[yb, ident], writes=[pTb_t])
            S.op("act", lambda e: e.activation(out=yT[:].rearrange("p k t -> p (k t)"), in_=pTb[:].rearrange("p k t -> p (k t)"), func=AF.Copy),
                 reads=[pTb_t], writes=[yT])
            for n in range(2):
                po = bACC.next()
                for k in range(8):
                    S.op("pe", lambda e: e.matmul(po[:], lhsT=yT[:, k, :], rhs=wout[:, k, n * 512:(n + 1) * 512], start=(k == 0), stop=(k == 7)),
                         reads=[yT, wout], writes=[po])
                S.op("dve", lambda e: e.tensor_tensor(out=res[:, n * 512:(n + 1) * 512], in0=po[:], in1=gate_bc[:, w, n * 512:(n + 1) * 512], op=ALU.mult),
                     reads=[po, gate_bc], writes=[res])
            S.op("dve", lambda e: e.tensor_tensor(out=res[:], in0=res[:], in1=xo[:], op=ALU.add), reads=[res, xo], writes=[res])
            if last:
                ss = ss_r.next()
                S.op("act", lambda e: e.activation(out=junk[:], in_=res[:], func=AF.Square, accum_out=ss[:, 0:1]), reads=[res], writes=[junk, ss])
                S.op("act", lambda e: e.activation(out=ss[:, 1:2], in_=ss[:, 0:1], func=AF.Ln, scale=1.0 / D_MODEL, bias=epst[:]), reads=[ss, epst], writes=[ss])
                S.op("act", lambda e: e.activation(out=ss[:, 1:2], in_=ss[:, 1:2], func=AF.Exp, scale=-0.5), reads=[ss], writes=[ss])
                S.op("act", lambda e: e.activation(out=xo[:], in_=res[:], func=AF.Copy, scale=ss[:, 1:2]), reads=[res, ss], writes=[xo])
                S.op("dve", lambda e: e.tensor_tensor(out=res[:], in0=xo[:], in1=fn_t[:], op=ALU.mult), reads=[xo, fn_t], writes=[res])
            S.dma(SQ, dst[dst_row:dst_row + 128, :], res[:], reads=[res], sem_tile=res)
            return res

        out_tiles = []

        def load_q(name, u0, n):
            qt = q_r.next()
            S.dma(LQ, qt[:, 0:n], fmS[fmi[name], :, u0:u0 + n], reads=[fm_reg], writes=[qt])
            return qt

        def load_local(knames_idx, vname, vc0, vw, utiles):
            kl = kl_r.next()
            vl = vl_r.next()
            pos = 0
            for (ut0, cnt) in utiles:
                S.dma(LQ, kl[:, pos * 128:(pos + cnt) * 128], fmS[knames_idx, :, ut0 * 128:(ut0 + cnt) * 128], reads=[fm_reg], writes=[kl])
                S.dma(LQ, vl[:, pos:pos + cnt, 0:vw], vS[vname][ut0 * 128:(ut0 + cnt) * 128, vc0:vc0 + vw].rearrange("(k p) c -> p k c", p=128),
                      reads=[v_reg], writes=[vl])
                pos += cnt
            return kl, vl

        UC_T = U_C // 128

        if layer == 0:
            for t in range(2):
                yt = y_t[t]
                u0 = U_C + t * 128
                jobs = []
                kl, vl = load_local(fmi["ka"], "va", 0, 130, [(UC_T, 2)])
                for c in range(4):
                    qt = load_q("qa%d" % c, u0, 128)
                    for s_ in range(2):
                        jobs.append(dict(qt=qt, pbase=64 * s_, kl=kl, vl=vl, vc0=s_ * 65, nch=2, nb=0, bias=None, yt=yt, ycol=(c + 4 * s_) * 64))
                run_wide(jobs)
                for c in range(4):
                    kl, vl = load_local(fmi["kb%d" % c], "vb", c * 130, 130, [(UC_T, 2)])
                    qt = load_q("qb%d" % c, u0, 128)
                    run_wide([dict(qt=qt, pbase=64 * s_, kl=kl, vl=vl, vc0=s_ * 65, nch=2, nb=0, bias=None, yt=yt, ycol=512 + (2 * c + s_) * 64)
                              for s_ in range(2)])
                out_tiles.append(out_tile(yt, u0, xc_in, t * 128, 1, out_c, t * 128))

        for qb in range(NBo):
            u0 = U_OWN + qb * 512
            if layer == 0:
                KT, VD = load_dense("ka", "va", 0, 130)
                for c in range(4):
                    qt = load_q("qa%d" % c, u0, 512)
                    accs = [banks[5], banks[6]]
                    dense_pair(qt, KT, VD, lambda j, m: VD[:, j, m * 65:(m + 1) * 65], 65, accs)
                    for s in range(2):
                        head = c + 4 * s
                        fin = banks[7]
                        untranspose(accs[s], 65, fin, 65)
                        finish_head([(fin, i * 65) for i in range(4)], 65, y_t, head * 64)
            else:
                for h in range(4):
                    KT, VD = load_dense("kc%d" % h, "vc", h * 129, 129)
                    qt = load_q("qc%d" % h, u0, 512)
                    accs = [banks[5], banks[6]]
                    den = banks[7]
                    dense_pair(qt, KT, VD, lambda j, m: VD[:, j, 0:128], 128, accs, den=den)
                    fins = [banks[1], banks[2]]
                    untranspose(accs[0], 128, fins[0], 128)
                    untranspose(accs[1], 128, fins[1], 128)
                    dfin = banks[3]
                    untranspose(den, 2, dfin, 2)
                    o_m = [[(fins[0], i * 128, i * 2 + 0) for i in range(4)], [(fins[1], i * 128, i * 2 + 1) for i in range(4)]]
                    R = rec16_r.next()
                    S.op("dve", lambda e: e.reciprocal(out=R[:, 0:8], in_=dfin[:, 0:8]), reads=[dfin], writes=[R])
                    S.op("dve", lambda e: e.tensor_scalar(out=R[:, 8:12], in0=R[:, 0:8].rearrange("p (s m) -> p s m", m=2)[:, :, 1],
                                                          scalar1=lam_s[:, 3:4], scalar2=None, op0=ALU.mult), reads=[R, lam_s], writes=[R])
                    for s in range(4):
                        a0, c0, d0 = o_m[0][s]
                        a1, c1, d1 = o_m[1][s]
                        S.op("act", lambda e: e.activation(out=Tq[:, s, 0:128], in_=a0[:, c0:c0 + 128], func=AF.Copy, scale=R[:, 2 * s:2 * s + 1]),
                             reads=[a0, R], writes=[Tq])
                        S.op("dve", lambda e: e.scalar_tensor_tensor(out=Tq[:, s, 128:256], in0=a1[:, c1:c1 + 128], scalar=R[:, 8 + s:9 + s], in1=Tq[:, s, 0:128],
                                                                     op0=ALU.mult, op1=ALU.add), reads=[a1, R, Tq], writes=[Tq])
                        S.op("act", lambda e: e.activation(out=Tq[:, s, 0:128], in_=Tq[:, s, 128:256], func=AF.Square, accum_out=R[:, 12 + s:13 + s]),
                             reads=[Tq], writes=[Tq, R])
                    S.op("act", lambda e: e.activation(out=R[:, 12:16], in_=R[:, 12:16], func=AF.Ln, scale=1.0 / 128, bias=epst[:]), reads=[R, epst], writes=[R])
                    S.op("act", lambda e: e.activation(out=R[:, 12:16], in_=R[:, 12:16], func=AF.Exp, scale=-0.5), reads=[R], writes=[R])
                    S.op("dve", lambda e: e.tensor_scalar_mul(out=R[:, 12:16], in0=R[:, 12:16], scalar1=1.0 - lam0), reads=[R], writes=[R])
                    for s in range(4):
                        yt = y_t[s]
                        S.op("dve", lambda e: e.scalar_tensor_tensor(out=yt[:, h * 128:(h + 1) * 128], in0=Tq[:, s, 128:256], scalar=R[:, 12 + s:13 + s], in1=sub_t[:],
                                                                     op0=ALU.mult, op1=ALU.mult), reads=[Tq, R, sub_t], writes=[yt])
            for tl in range(4):
                t = qb * 4 + tl
                ut = U_OWN // 128 + t
                yt = y_t[tl]
                if layer == 0:
                    edge = t < 2 or t >= NTo - 2
                    if edge:
                        et = t if t < 2 else 2 + (t - (NTo - 2))
                        lo, hi = ((-2, 3), (-2, 2), (-2, 2), (-3, 2))[et]
                    else:
                        lo, hi = -2, 2
                    nb = hi - lo + 1
                    runs = [(ut + lo, nb), (UC_T, 2)]
                    jobs = []
                    for c in range(4):
                        kl, vl = load_local(fmi["kb%d" % c], "vb", c * 130, 130, runs)
                        qt = load_q("qb%d" % c, ut * 128, 128)
                        for s_ in range(2):
                            head = 2 * c + s_
                            job = dict(qt=qt, pbase=64 * s_, kl=kl, vl=vl, vc0=s_ * 65, nch=nb + 2, nb=nb,
                                       yt=yt, ycol=512 + head * 64)
                            if edge:
                                def pre(job=job, et=et, head=head, lo=lo, hi=hi):
                                    be = be_r.next()
                                    S.dma(LQ, be[:], bias_e[et, head], writes=[be])
                                    job["bias"] = (be, be[:, (lo + 3) * 128:(hi + 4) * 128])
                                job["pre"] = pre
                            else:
                                job["bias"] = (bi_t, bi_t[:, head, 0:640])
                            jobs.append(job)
                    run_wide(jobs)
                else:
                    kl, vl = load_local(fmi["kd"], "vd", 0, 130, [(ut - 1, 3), (UC_T, 2)])
                    var = 1 if t == 0 else (2 if t == NTo - 1 else 0)
                    jobs = []
                    for c in range(4):
                        qt = load_q("qd%d" % c, ut * 128, 128)
                        for s_ in range(2):
                            head = c + 4 * s_
                            jobs.append(dict(qt=qt, pbase=64 * s_, kl=kl, vl=vl, vc0=s_ * 65, nch=5, nb=3, bias=(dmb, dmb[:, var, :]),
                                             yt=yt, ycol=512 + head * 64, extra_den=esnk[:, head:head + 1]))
                    run_wide(jobs)
                out_tiles.append(out_tile(yt, ut * 128, x_u, ut * 128, 0, out_x, t * 128))
        if last:
            S.finish(out_tiles)
        else:
            S.barrier()
        st2.close()
    if not last:
        S.release_dsems()
        S.new_epoch()


def build_fused(SEQ, B):
    HALF = SEQ // 2
    EXT = HALF + 2 * HALO
    nc = bass.Bass("TRN2", target_bir_lowering=False)

    def din(name, shape):
        return nc.dram_tensor(name, list(shape), F32, kind="ExternalInput").ap()

    SH = dict(x_u=din("x_u", [EXT + HALF, D_MODEL]), xc=din("xc", [CTX, D_MODEL]), cvec=din("cvec", [128, 8, 2]),
              ident=din("ident", [128, 128]), blk=din("blk", [128, 128]), perm=din("perm", [128, 128]),
              ropeC=din("ropeC", [128, EXT + HALF]), ropeS=din("ropeS", [128, EXT + HALF]), sel=din("sel", [128, 4]), sel2=din("sel2", [2, 2, 128]))
    SH["x1_loc"] = nc.dram_tensor("x1_loc", [HALF, D_MODEL], F32).ap()
    SH["xc1_loc"] = nc.dram_tensor("xc1_loc", [CTX, D_MODEL], F32).ap()
    SH["GA"] = nc.dram_tensor("x1_all", [2 * HALF, D_MODEL], F32).ap()
    with contextlib.ExitStack() as st0:
        S = Sched(nc, st0)
        SH["pTb_t"] = S.ps("bankT", [128, 8, 128], BF16)
        pairA = st0.enter_context(nc.psum_tensor("ps_pairA", [128, 1024], F32))
        pairB = st0.enter_context(nc.psum_tensor("ps_pairB", [128, 1024], F32))
        b1 = Tile("bank1", pairA[:, 0:512], excl=True)
        b2 = Tile("bank2", pairA[:, 512:1024], excl=True)
        b3 = Tile("bank3", pairB[:, 0:512], excl=True)
        b4 = Tile("bank4", pairB[:, 512:1024], excl=True)
        SH["pairs"] = [(pairA, b1, b2), (pairB, b3, b4)]
        SH["banks"] = [SH["pTb_t"], b1, b2, b3, b4] + [S.ps("bank%d" % i, [128, 512], F32) for i in range(5, 8)]
        SH["G_reg"] = Tile("G_reg")
        emit_layer(nc, S, SH, 0, SEQ)
        S.sems["cc"] = st0.enter_context(nc.semaphore("s_cc"))
        RC = min(512, HALF)
        for i in range(HALF // RC):
            nc.gpsimd.collective_compute("AllGather", ALU.bypass, replica_groups=[[2 * b, 2 * b + 1] for b in range(B)],
                                         ins=[SH["x1_loc"][i * RC:(i + 1) * RC, :]],
                                         outs=[SH["GA"][i * 2 * RC:(i + 1) * 2 * RC, :]]).then_inc(S.sems["cc"], 1)
        SH["G_reg"].last_w = ("cc", HALF // RC)
        emit_layer(nc, S, SH, 1, SEQ)
    return nc


def rope_tables(pos):
    row = (pos // GRID_W).astype(np.float32)
    col = (pos % GRID_W).astype(np.float32)
    q = HD // 4
    inv = (10000.0 ** (-np.arange(q, dtype=np.float32) / q)).astype(np.float32)
    ar = row[None, :] * inv[:, None]
    ac = col[None, :] * inv[:, None]
    cr, sr, cc, sc = np.cos(ar), np.sin(ar), np.cos(ac), np.sin(ac)
    C = np.concatenate([cr, cr, cc, cc], axis=0)
    Sg = np.concatenate([-sr, sr, -sc, sc], axis=0)
    return (np.concatenate([C, C], 0).astype(np.float32), np.concatenate([Sg, Sg], 0).astype(np.float32))


def nbr_bias(rpb, g, gk, NT):
    rows = NT * 2
    out = np.full((8, 128, 128), NEG, np.float32)
    if gk < 0 or gk >= NT:
        return out
    ql = np.arange(128)
    r = 2 * g + ql // 64
    c = ql % 64
    kr = 2 * gk + ql // 64
    kc = ql % 64
    win_r = min(8, rows)
    rs = np.clip(r - win_r // 2, 0, rows - win_r)
    cs = np.clip(c - 8, 0, GRID_W - 16)
    valid = ((kr[:, None] >= rs[None, :]) & (kr[:, None] < rs[None, :] + win_r)
             & (kc[:, None] >= cs[None, :]) & (kc[:, None] < cs[None, :] + 16))
    di = kr[:, None] - r[None, :] + 7
    dj = kc[:, None] - c[None, :] + 15
    di = np.clip(di, 0, 14)
    dj = np.clip(dj, 0, 30)
    vals = rpb[:, di, dj]
    return np.where(valid[None], vals, np.float32(NEG)).astype(np.float32)


def chunk_rows(w):
    return np.ascontiguousarray(w.reshape(8, 128, w.shape[1]))


def prep_layer_inputs(layer, SEQ, xs, xcs, p):
    B = xs.shape[0]
    HALF = SEQ // 2
    EXT = HALF + 2 * HALO
    NT = SEQ // 128
    NTo = HALF // 128
    cfg = layer_cfg(layer)
    wi = p["w_in_even"][0] if layer == 0 else p["w_in_odd"][0]
    wo = p["w_out_even"][0] if layer == 0 else p["w_out_odd"][0]
    cols = []
    for f in cfg["fm"]:
        cols += f[1]
    for name, c0, n in cfg["tm"]:
        cols += list(range(c0, c0 + n))
    w_in_l = chunk_rows(np.ascontiguousarray(wi[:, cols]))
    w_out_l = chunk_rows(wo)
    w_mod_l = chunk_rows(p["w_mod"][layer])
    b_mod_l = np.ascontiguousarray(p["b_mod"][layer].reshape(24, 128).T)
    bgate = np.ascontiguousarray(np.broadcast_to(p["b_mod"][layer][2048:3072][None, :], (128, D_MODEL)))
    ident = np.eye(128, dtype=np.float32)
    blk = np.zeros((128, 128), np.float32)
    blk[:64, :64] = 1.0 / 64
    blk[64:, 64:] = 1.0 / 64
    perm = np.zeros((128, 128), np.float32)
    for m in range(128):
        k = m + 16 if (m % 32) < 16 else m - 16
        perm[k, m] = 1.0
    maps = []
    for b in range(B):
        for half in range(2):
            T0 = half * HALF
            pos_e = np.arange(T0 - HALO, T0 + HALF + HALO)
            valid_e = (pos_e >= 0) & (pos_e < SEQ)
            x_e = np.zeros((EXT, D_MODEL), np.float32)
            x_e[valid_e] = xs[b, pos_e[valid_e]]
            T1 = (1 - half) * HALF
            pos_o = np.arange(T1, T1 + HALF)
            x_u = np.concatenate([x_e, xs[b, pos_o]], axis=0)
            pos_u = np.concatenate([np.clip(pos_e, 0, SEQ - 1), pos_o])
            C, Sg = rope_tables(pos_u)
            cvec = np.stack([p["c"][b].reshape(8, 128).T, p["c_ctx"].reshape(8, 128).T], axis=-1)
            m = dict(x_u=x_u, xc=np.ascontiguousarray(xcs[b]), cvec=np.ascontiguousarray(cvec), w_mod=w_mod_l, b_mod=b_mod_l,
                     bgate=bgate, w_in=w_in_l, w_out=w_out_l, ident=ident, blk=blk, perm=perm, ropeC=C, ropeS=Sg)
            G0 = T0 // 128
            if layer == 0:
                m["gains"] = np.ascontiguousarray(np.stack([np.tile(p["a_q_norm"][0], 2), np.tile(p["a_k_norm"][0], 2)], axis=-1))
                rpb = p["b_rpb"][0]
                gi = min(max(G0 + 2, 2), NT - 3) if NT >= 6 else 0
                bi = np.stack([nbr_bias(rpb, gi, gi + j, NT) for j in range(-2, 3)], axis=0)
                m["bias_i"] = np.ascontiguousarray(bi.transpose(2, 1, 0, 3).reshape(128, 8, 640))
                ets = [0, 1, NTo - 2, NTo - 1]
                be = np.stack([np.stack([nbr_bias(rpb, G0 + t, G0 + t + j, NT) for j in range(-3, 4)], axis=0) for t in ets], axis=0)
                m["bias_e"] = np.ascontiguousarray(be.transpose(0, 2, 3, 1, 4).reshape(4, 8, 128, 896))
            else:
                a = np.arange(128)
                tri_prev = np.where(a[:, None] >= a[None, :], 0.0, NEG).astype(np.float32)
                tri_next = np.where(a[:, None] <= a[None, :], 0.0, NEG).astype(np.float32)
                full = np.full((128, 128), NEG, np.float32)
                first_prev = full if G0 == 0 else tri_prev
                last_next = full if G0 + NTo == NT else tri_next
                m["dmask"] = np.ascontiguousarray(np.stack([first_prev, last_next, tri_prev, tri_next], axis=1))
                m["lamv"] = np.ascontiguousarray(np.broadcast_to(p["c_lambda"][0].reshape(1, 256), (128, 256)))
                m["subln"] = np.ascontiguousarray(np.broadcast_to((p["c_subln"][0])[None, :], (128, 128)))
                m["sinks"] = np.ascontiguousarray(np.broadcast_to(p["d_sinks"][0][None, :], (128, 8)))
                m["fnorm"] = np.ascontiguousarray(np.broadcast_to(p["final_norm"][None, :], (128, D_MODEL)))
            maps.append(m)
    return maps


def prep_fused_inputs(SEQ, xs, xcs, p):
    m0 = prep_layer_inputs(0, SEQ, xs, xcs, p)
    m1 = prep_layer_inputs(1, SEQ, xs, xcs, p)
    shared = ("x_u", "xc", "cvec", "ident", "blk", "perm", "ropeC", "ropeS")
    maps = []
    for i, (a, b) in enumerate(zip(m0, m1)):
        half = i % 2
        m = {k: a[k] for k in shared}
        for k, v in a.items():
            if k not in shared:
                m["l0_" + k] = v
        for k, v in b.items():
            if k not in shared:
                m["l1_" + k] = v
        sel = np.zeros((128, 4), np.float32)
        sel[:, 0] = 1.0 if half == 1 else 0.0
        sel[:, 1] = 1.0 if half == 0 else 0.0
        sel[:, 2] = 1.0 if half == 1 else 0.0
        sel[:, 3] = 1.0 if half == 0 else 0.0
        m["sel"] = sel
        sel2 = np.zeros((2, 2, 128), np.float32)
        sel2[0, 0, :] = 1.0
        sel2[1, 1, :] = 1.0
        m["sel2"] = sel2
        maps.append(m)
    return maps


def run_fused(SEQ, xs, xcs, p, runner=None):
    B = xs.shape[0]
    key = ("fused", SEQ, B)
    if key not in _NC_CACHE:
        _NC_CACHE[key] = build_fused(SEQ, B)
    nc = _NC_CACHE[key]
    maps = prep_fused_inputs(SEQ, xs, xcs, p)
    if runner is None:
        res = run_bass_kernel_spmd(nc, maps, core_ids=list(range(len(maps)))).results
    else:
        res = runner(nc, maps)
    HALF = SEQ // 2
    xo = np.zeros_like(xs)
    for b in range(B):
        for half in range(2):
            xo[b, half * HALF:(half + 1) * HALF] = res[2 * b + half]["out_x"]
    return xo


_NC_CACHE = {}


def kernel(x, c, ctx, c_ctx, w_mod, b_mod, w_in_even, w_out_even, a_q_norm, a_k_norm, b_rpb,
           w_in_odd, w_out_odd, c_lambda, c_subln, d_sinks, final_norm):
    p = dict(c=np.asarray(c, np.float32), c_ctx=np.asarray(c_ctx, np.float32), w_mod=np.asarray(w_mod, np.float32),
             b_mod=np.asarray(b_mod, np.float32), w_in_even=np.asarray(w_in_even, np.float32),
             w_out_even=np.asarray(w_out_even, np.float32), a_q_norm=np.asarray(a_q_norm, np.float32),
             a_k_norm=np.asarray(a_k_norm, np.float32), b_rpb=np.asarray(b_rpb, np.float32),
             w_in_odd=np.asarray(w_in_odd, np.float32), w_out_odd=np.asarray(w_out_odd, np.float32),
             c_lambda=np.asarray(c_lambda, np.float32), c_subln=np.asarray(c_subln, np.float32),
             d_sinks=np.asarray(d_sinks, np.float32), final_norm=np.asarray(final_norm, np.float32))
    xs = np.asarray(x, np.float32)
    xcs = np.asarray(ctx, np.float32)
    SEQ = xs.shape[1]
    return run_fused(SEQ, xs, xcs, p)
```

```python
import contextlib
import math
import numpy as np
import concourse.bass as bass
import concourse.mybir as mybir
from concourse.bass_utils import run_bass_kernel_spmd

F32 = mybir.dt.float32
BF16 = mybir.dt.bfloat16
AF = mybir.ActivationFunctionType
ALU = mybir.AluOpType

D_MODEL = 1024
CTX = 256
HD = 64
GRID_W = 64
SCALE = HD ** -0.5
EPS = 1e-6
NEG = -30000.0
HALO = 512
LQ = "sp"
SQ = "pool"


class Tile:
    __slots__ = ("name", "t", "last_w", "readers", "dsem", "dcount", "excl")

    def __init__(self, name, t=None, excl=False):
        self.name = name
        self.t = t
        self.last_w = None
        self.readers = {}
        self.dsem = None
        self.dcount = 0
        self.excl = excl

    def __getitem__(self, idx):
        return self.t[idx]


class Sched:
    def __init__(self, nc, stack):
        self.nc = nc
        self.stack = stack
        self.sem_stack = stack
        self.dtiles = []
        self.engs = {}
        self.sems = {}
        for en, e in (("pe", nc.tensor), ("act", nc.scalar), ("dve", nc.vector),
                      ("pool", nc.gpsimd), ("sp", nc.sync)):
            self.sems[en] = stack.enter_context(nc.semaphore("s_" + en))
            self.engs[en] = dict(eng=e, count=0, seen={}, key=en)
        self.epoch = 0
        self.nsem = 0
        self.prefix = ""
        self.free_dsems = []

    def sb(self, name, shape, dt):
        return Tile(name, self.stack.enter_context(self.nc.sbuf_tensor("sb_" + self.prefix + name, list(shape), dt)))

    def ps(self, name, shape, dt=F32):
        return Tile(name, self.stack.enter_context(self.nc.psum_tensor("ps_" + name, list(shape), dt)), excl=True)

    def _dsem(self, tile):
        if tile.dsem is None:
            if self.free_dsems:
                key, cnt = self.free_dsems.pop()
                tile.dsem = key
                tile.dcount = cnt
            else:
                key = "d%d" % self.nsem
                self.nsem += 1
                tile.dsem = key
                self.sems[key] = self.sem_stack.enter_context(self.nc.semaphore(key))
            self.dtiles.append(tile)
        return tile.dsem

    def new_epoch(self):
        self.epoch += 1
        for en, E in self.engs.items():
            key = "%s#%d" % (en, self.epoch)
            self.sems[key] = self.sem_stack.enter_context(self.nc.semaphore("s_%s_%d" % (en, self.epoch)))
            E["key"] = key
            E["count"] = 0

    def release_dsems(self):
        for t in self.dtiles:
            self.free_dsems.append((t.dsem, t.dcount))
            t.dsem = None
        self.dtiles = []

    def _wait_deps(self, en, reads, writes):
        E = self.engs[en]
        deps = {}

        def add(ev):
            if ev is None:
                return
            k, v = ev
            if deps.get(k, 0) < v:
                deps[k] = v

        me = E["key"]
        for t in reads:
            add(t.last_w)
            if t.excl:
                for k, v in t.readers.items():
                    if k != me:
                        add((k, v))
        for t in writes:
            add(t.last_w)
            for k, v in t.readers.items():
                if k != me:
                    add((k, v))
        for k, v in deps.items():
            if E["seen"].get(k, 0) < v:
                E["seen"][k] = v
                if k == me and en == "pe":
                    continue
                E["eng"].wait_ge(self.sems[k], v)

    def op(self, en, fn, reads=(), writes=()):
        E = self.engs[en]
        self._wait_deps(en, reads, writes)
        ins = fn(E["eng"])
        E["count"] += 1
        me = E["key"]
        ins.then_inc(self.sems[me], 1)
        for t in reads:
            t.readers[me] = E["count"]
        for t in writes:
            t.last_w = (me, E["count"])
            t.readers = {}
        return ins

    def dma(self, q, out, in_, reads=(), writes=(), sem_tile=None):
        E = self.engs[q]
        self._wait_deps(q, reads, writes)
        st = sem_tile if sem_tile is not None else (list(writes) + list(reads))[0]
        key = self._dsem(st)
        ins = E["eng"].dma_start(out=out, in_=in_)
        st.dcount += 16
        ins.then_inc(self.sems[key], 16)
        for t in reads:
            t.readers[key] = st.dcount
        for t in writes:
            t.last_w = (key, st.dcount)
            t.readers = {}
        return ins

    def barrier(self):
        for en, E in self.engs.items():
            for en2, E2 in self.engs.items():
                k2 = E2["key"]
                if en2 != en and E2["count"] and E["seen"].get(k2, 0) < E2["count"]:
                    E["seen"][k2] = E2["count"]
                    E["eng"].wait_ge(self.sems[k2], E2["count"])
            for t in self.dtiles:
                if t.dcount and E["seen"].get(t.dsem, 0) < t.dcount:
                    E["seen"][t.dsem] = t.dcount
                    E["eng"].wait_ge(self.sems[t.dsem], t.dcount)

    def finish(self, tiles, en="sp"):
        self._wait_deps(en, tiles, tiles)


class Rot:
    def __init__(self, tiles):
        self.tiles = tiles
        self.i = 0

    def next(self):
        t = self.tiles[self.i % len(self.tiles)]
        self.i += 1
        return t


def layer_cfg(layer):
    if layer == 0:
        qa, ka, va, qb, kb, vb, z = 0, 512, 640, 768, 1280, 1792, 2304
        fm = []
        for c in range(4):
            cols = list(range(qa + c * 64, qa + c * 64 + 64)) + list(range(qa + (4 + c) * 64, qa + (4 + c) * 64 + 64))
            fm.append(("qa%d" % c, cols, "q", True))
        fm.append(("ka", list(range(ka, ka + 128)), "k", True))
        for c in range(4):
            fm.append(("qb%d" % c, list(range(qb + c * 128, qb + c * 128 + 128)), None, False))
        for c in range(4):
            fm.append(("kb%d" % c, list(range(kb + c * 128, kb + c * 128 + 128)), None, False))
        tm = [("va", va, 128), ("vb", vb, 512), ("z", z, 1024)]
        dense_k, dense_v = ["ka"], ["va"]
        local_k, local_v = ["kb0", "kb1", "kb2", "kb3"], ["vb"]
    else:
        qc, kc, vc, qd, kd, vd, z = 0, 512, 1024, 1536, 2048, 2176, 2304
        fm = []
        for c in range(4):
            fm.append(("qc%d" % c, list(range(qc + c * 128, qc + c * 128 + 128)), None, True))
        for c in range(4):
            fm.append(("kc%d" % c, list(range(kc + c * 128, kc + c * 128 + 128)), None, True))
        for c in range(4):
            cols = list(range(qd + c * 64, qd + c * 64 + 64)) + list(range(qd + (4 + c) * 64, qd + (4 + c) * 64 + 64))
            fm.append(("qd%d" % c, cols, None, True))
        fm.append(("kd", list(range(kd, kd + 128)), None, True))
        tm = [("vc", vc, 512), ("vd", vd, 128), ("z", z, 1024)]
        dense_k, dense_v = ["kc0", "kc1", "kc2", "kc3"], ["vc"]
        local_k, local_v = ["kd"], ["vd"]
    return dict(fm=fm, tm=tm, dense_k=dense_k, dense_v=dense_v, local_k=local_k, local_v=local_v)


def lambda_init(layer):
    return 0.8 - 0.6 * math.exp(-0.3 * layer)


def emit_layer(nc, S, SH, layer, SEQ):
    HALF = SEQ // 2
    EXT = HALF + 2 * HALO
    NU = EXT + HALF + CTX
    NTo = HALF // 128
    NBo = HALF // 512
    U_OWN = HALO
    U_O = EXT
    U_C = EXT + HALF
    NKD = 2 * HALF + CTX
    NKC = NKD // 128
    last = layer == 1
    cfg = layer_cfg(layer)
    fm, tm = cfg["fm"], cfg["tm"]
    NFM = len(fm)
    fmi = {f[0]: i for i, f in enumerate(fm)}
    NCOL = NFM * 128 + sum(t[2] for t in tm)
    tmoff = {}
    o = NFM * 128
    for name, _, n in tm:
        tmoff[name] = o
        o += n
    lam0 = lambda_init(layer)

    LP = "l%d_" % layer
    S.prefix = LP

    def din(name, shape):
        return nc.dram_tensor(LP + name, list(shape), F32, kind="ExternalInput").ap()

    x_u, xc_in, cvec = SH["x_u"], SH["xc"], SH["cvec"]
    ident_d, blk_d, perm_d, ropeC, ropeS = SH["ident"], SH["blk"], SH["perm"], SH["ropeC"], SH["ropeS"]
    x1_loc, xc1_loc, GA = SH["x1_loc"], SH["xc1_loc"], SH["GA"]
    w_mod = din("w_mod", [8, 128, 3072])
    b_mod = din("b_mod", [128, 24])
    bgate = din("bgate", [128, D_MODEL])
    w_in = din("w_in", [8, 128, NCOL])
    w_out = din("w_out", [8, 128, D_MODEL])
    if layer == 0:
        gains = din("gains", [128, 2])
        bias_i = din("bias_i", [128, 8, 5 * 128])
        bias_e = din("bias_e", [4, 8, 128, 7 * 128])
    else:
        dmask = din("dmask", [128, 4, 128])
        lamv = din("lamv", [128, 256])
        subln = din("subln", [128, 128])
        sinks = din("sinks", [128, 8])
        fnorm = din("fnorm", [128, D_MODEL])
    if last:
        out_x = nc.dram_tensor("out_x", [HALF, D_MODEL], F32, kind="ExternalOutput").ap()
    else:
        out_x = x1_loc
        out_c = xc1_loc

    fmS = nc.dram_tensor(LP + "fmS", [NFM, 128, NU], BF16).ap()
    vdims = {"va": (2, 65), "vb": (8, 65), "vc": (4, 129), "vd": (2, 65)}
    vS = {}
    for name, _, n in tm:
        if name != "z":
            h, d = vdims[name]
            vS[name] = nc.dram_tensor(LP + "vS_" + name, [NU, h * d], BF16).ap()
    zS = nc.dram_tensor(LP + "zS", [NU, D_MODEL], BF16).ap()

    with contextlib.ExitStack() as st:
        S.stack = st
        st1 = contextlib.ExitStack()
        fm_reg = Tile("fm_reg")
        v_reg = Tile("v_reg")
        z_reg = Tile("z_reg")

        ident_f = S.sb("ident_f", [128, 128], F32)
        ident = S.sb("ident", [128, 128], BF16)
        blk = S.sb("blk", [128, 128], F32)
        perm = S.sb("perm", [128, 128], F32)
        epst = S.sb("epst", [128, 1], F32)
        zeros = S.sb("zeros", [128, 128], F32)
        junk = S.sb("junk", [128, D_MODEL], F32)
        ss_r = Rot([S.sb("ss%d" % i, [128, 2], F32) for i in range(2)])
        t1_r = Rot([S.sb("t1%d" % i, [128, 512], F32) for i in range(2)])
        gate_bc = S.sb("gate_bc", [128, 2, D_MODEL], F32)
        modt = S.sb("modt", [128, 24, 2], F32)
        bmodt = S.sb("bmodt", [128, 24], F32)
        cv = S.sb("cv", [128, 8, 2], F32)
        pTb_t = SH["pTb_t"]
        pTb = pTb_t.t
        banks = SH["banks"]
        sel_t = S.sb("sel_t", [128, 4], F32)
        S.dma(LQ, sel_t[:], SH["sel"], writes=[sel_t])
        xt2 = S.sb("xt2", [128, D_MODEL], F32)
        G_reg = SH["G_reg"]
        NTo_ = HALF // 128

        RCt = min(512, HALF) // 128

        def grow(r, t):
            return ((t // RCt) * 2 * RCt + r * RCt + (t % RCt)) * 128

        def load_x_tile(xt, utile):
            e = utile
            if layer == 0:
                if e >= (EXT + HALF) // 128:
                    c = e - (EXT + HALF) // 128
                    S.dma(LQ, xt[:], xc_in[c * 128:(c + 1) * 128, :], writes=[xt])
                else:
                    S.dma(LQ, xt[:], x_u[e * 128:(e + 1) * 128, :], writes=[xt])
                return
            if e >= (EXT + HALF) // 128:
                c = e - (EXT + HALF) // 128
                S.dma(LQ, xt[:], xc1_loc[c * 128:(c + 1) * 128, :], writes=[xt])
            elif e >= EXT // 128:
                o = e - EXT // 128
                S.dma(LQ, xt[:], GA[grow(0, o):grow(0, o) + 128, :], reads=[G_reg], writes=[xt])
                S.dma(LQ, xt2[:], GA[grow(1, o):grow(1, o) + 128, :], reads=[G_reg], writes=[xt2])
                S.op("act", lambda en: en.activation(out=xt[:], in_=xt[:], func=AF.Copy, scale=sel_t[:, 2:3]), reads=[xt, sel_t], writes=[xt])
                S.op("dve", lambda en: en.scalar_tensor_tensor(out=xt[:], in0=xt2[:], scalar=sel_t[:, 3:4], in1=xt[:], op0=ALU.mult, op1=ALU.add),
                     reads=[xt2, sel_t, xt], writes=[xt])
            elif 4 <= e < 4 + NTo_:
                t = e - 4
                S.dma(LQ, xt[:], x1_loc[t * 128:(t + 1) * 128, :], writes=[xt])
            elif e < 4:
                gt = grow(0, NTo_ - 4 + e)
                S.dma(LQ, xt[:], GA[gt:gt + 128, :], reads=[G_reg], writes=[xt])
                S.op("act", lambda en: en.activation(out=xt[:], in_=xt[:], func=AF.Copy, scale=sel_t[:, 0:1]), reads=[xt, sel_t], writes=[xt])
            else:
                gt = grow(1, e - 4 - NTo_)
                S.dma(LQ, xt[:], GA[gt:gt + 128, :], reads=[G_reg], writes=[xt])
                S.op("act", lambda en: en.activation(out=xt[:], in_=xt[:], func=AF.Copy, scale=sel_t[:, 1:2]), reads=[xt, sel_t], writes=[xt])

        S.dma(LQ, ident_f[:], ident_d, writes=[ident_f])
        S.dma(LQ, blk[:], blk_d, writes=[blk])
        S.dma(LQ, perm[:], perm_d, writes=[perm])
        S.dma(LQ, cv[:], cvec, writes=[cv])
        S.dma(LQ, bmodt[:], b_mod, writes=[bmodt])
        S.dma(LQ, gate_bc[:, 0, :], bgate, writes=[gate_bc])
        S.op("dve", lambda e: e.tensor_copy(out=ident[:], in_=ident_f[:]), reads=[ident_f], writes=[ident])
        S.op("pool", lambda e: e.memset(epst[:], EPS), writes=[epst])
        S.op("pool", lambda e: e.memset(zeros[:], 0.0), writes=[zeros])
        S.op("dve", lambda e: e.tensor_copy(out=gate_bc[:, 1, :], in_=gate_bc[:, 0, :]), reads=[gate_bc], writes=[gate_bc])
        S.stack = st
        if layer == 0:
            gn = S.sb("gn", [128, 2], F32)
            bi_t = S.sb("bi_t", [128, 8, 640], F32)
            S.dma(LQ, gn[:], gains, writes=[gn])
            S.dma(LQ, bi_t[:], bias_i, writes=[bi_t])
        else:
            dm_t = S.sb("dm_t", [128, 4, 128], F32)
            lam_t = S.sb("lam_t", [128, 256], F32)
            sub_t = S.sb("sub_t", [128, 128], F32)
            snk_t = S.sb("snk_t", [128, 8], F32)
            fn_t = S.sb("fn_t", [128, D_MODEL], F32)
            S.dma(LQ, dm_t[:], dmask, writes=[dm_t])
            S.dma(LQ, lam_t[:], lamv, writes=[lam_t])
            S.dma(LQ, sub_t[:], subln, writes=[sub_t])
            S.dma(LQ, snk_t[:], sinks, writes=[snk_t])
            S.dma(LQ, fn_t[:], fnorm, writes=[fn_t])
            lam_s = S.sb("lam_s", [128, 4], F32)
            lprod = S.sb("lprod", [128, 128], F32)
            esnk = S.sb("esnk", [128, 8], F32)
        S.stack = st1
        win = S.sb("win", [128, 8, NCOL], BF16)
        sel2 = S.sb("sel2", [2, 2, 128], F32)
        S.dma(LQ, sel2[:], SH["sel2"], writes=[sel2])
        stage = Rot([S.sb("stage%d" % i, [128, 1024], F32) for i in range(4)])

        S.op("act", lambda e: e.activation(out=cv[:], in_=cv[:], func=AF.Silu), reads=[cv], writes=[cv])
        rowb = [banks[1], banks[2], banks[3], banks[4], banks[5], banks[6]]
        for k in range(8):
            for pi in range(3):
                stg = stage.next()
                S.dma("act" if (k * 3 + pi) % 2 else LQ, stg[:], w_mod[k, :, pi * 1024:(pi + 1) * 1024], writes=[stg])
                for n in range(2):
                    rb = rowb[pi * 2 + n]
                    S.op("pe", lambda e: e.matmul(rb[0:2, 0:512], lhsT=cv[:, k, :], rhs=stg[:, n * 512:(n + 1) * 512],
                                                  start=(k == 0), stop=(k == 7)), reads=[stg, cv], writes=[rb])
        modrow = S.sb("modrow", [2, 3072], F32)
        for i6 in range(6):
            S.op("act" if i6 % 2 else "dve",
                 (lambda e: e.activation(out=modrow[0:2, i6 * 512:(i6 + 1) * 512], in_=rowb[i6][0:2, 0:512], func=AF.Copy)) if i6 % 2 else
                 (lambda e: e.tensor_copy(out=modrow[0:2, i6 * 512:(i6 + 1) * 512], in_=rowb[i6][0:2, 0:512])),
                 reads=[rowb[i6]], writes=[modrow])
        pmod = banks[7]
        for j in range(16):
            S.op("pe", lambda e: e.transpose(out=pmod[:, j * 2:j * 2 + 2], in_=modrow[0:2, j * 128:(j + 1) * 128], identity=ident_f[0:2, 0:2]),
                 reads=[modrow, ident_f], writes=[pmod])
        pg = [banks[1], banks[2], banks[3], banks[4]]
        for w in range(2):
            for n in range(2):
                S.op("pe", lambda e: e.matmul(pg[w * 2 + n][:], lhsT=sel2[0:2, w, :], rhs=modrow[0:2, 2048 + n * 512:2048 + (n + 1) * 512],
                                              start=True, stop=True), reads=[sel2, modrow], writes=[pg[w * 2 + n]])
        for w in range(2):
            S.op("dve", lambda e: e.tensor_tensor(out=modt[:, 0:16, w], in0=pmod[:, 0:32].rearrange("p (j w) -> p j w", w=2)[:, :, w],
                                                  in1=bmodt[:, 0:16], op=ALU.add), reads=[pmod, bmodt], writes=[modt])
            for n in range(2):
                S.op("dve", lambda e: e.tensor_tensor(out=gate_bc[:, w, n * 512:(n + 1) * 512], in0=pg[w * 2 + n][:],
                                                      in1=gate_bc[:, w, n * 512:(n + 1) * 512], op=ALU.add),
                     reads=[pg[w * 2 + n], gate_bc], writes=[gate_bc])
        S.op("dve", lambda e: e.tensor_scalar_add(out=modt[:, 8:16, :], in0=modt[:, 8:16, :], scalar1=1.0),
             reads=[modt], writes=[modt])

        cnt = 0
        for k in range(8):
            for c0 in range(0, NCOL, 1024):
                cn = min(1024, NCOL - c0)
                stg = stage.next()
                S.dma("act" if cnt % 2 else LQ, stg[:, 0:cn], w_in[k, :, c0:c0 + cn], writes=[stg])
                S.op("dve", lambda e: e.tensor_copy(out=win[:, k, c0:c0 + cn], in_=stg[:, 0:cn]), reads=[stg], writes=[win])
                cnt += 1

        if layer == 1:
            S.op("dve", lambda e: e.tensor_tensor(out=lprod[:, 0:64], in0=lam_t[:, 0:64], in1=lam_t[:, 64:128], op=ALU.mult), reads=[lam_t], writes=[lprod])
            S.op("dve", lambda e: e.tensor_tensor(out=lprod[:, 64:128], in0=lam_t[:, 128:192], in1=lam_t[:, 192:256], op=ALU.mult), reads=[lam_t, lprod], writes=[lprod])
            S.op("dve", lambda e: e.reduce_sum(out=lam_s[:, 0:2], in_=lprod[:].rearrange("p (a d) -> p a d", a=2), axis=mybir.AxisListType.X), reads=[lprod], writes=[lam_s])
            S.op("act", lambda e: e.activation(out=lam_s[:, 0:2], in_=lam_s[:, 0:2], func=AF.Exp), reads=[lam_s], writes=[lam_s])
            S.op("dve", lambda e: e.tensor_tensor(out=lam_s[:, 2:3], in0=lam_s[:, 0:1], in1=lam_s[:, 1:2], op=ALU.subtract), reads=[lam_s], writes=[lam_s])
            S.op("dve", lambda e: e.tensor_scalar(out=lam_s[:, 3:4], in0=lam_s[:, 2:3], scalar1=lam0, scalar2=-1.0, op0=ALU.add, op1=ALU.mult), reads=[lam_s], writes=[lam_s])
            S.op("act", lambda e: e.activation(out=esnk[:], in_=snk_t[:], func=AF.Exp), reads=[snk_t], writes=[esnk])

        xt_r = Rot([S.sb("xt%d" % i, [128, D_MODEL], F32) for i in range(2)])
        xn_r = Rot([S.sb("xn%d" % i, [128, D_MODEL], BF16) for i in range(2)])
        hT_r = Rot([S.sb("hT%d" % i, [128, 8, 512], BF16) for i in range(2)])
        tabC_r = Rot([S.sb("tabC%d" % i, [128, 512], F32) for i in range(1)])
        tabS_r = Rot([S.sb("tabS%d" % i, [128, 512], F32) for i in range(1)])
        sq_r = Rot([S.sb("sq%d" % i, [128, 512], F32) for i in range(2)])
        rs_r = Rot([S.sb("rs%d" % i, [128, 512], F32) for i in range(2)])
        qn_r = Rot([S.sb("qn%d" % i, [128, 512], F32) for i in range(3)])
        t2_r = Rot([S.sb("t2%d" % i, [128, 512], F32) for i in range(2)])
        fo_r = Rot([S.sb("fo%d" % i, [128, 512], BF16) for i in range(4)])
        vst = {}
        for name in vS:
            h, d = vdims[name]
            vst[name] = Rot([S.sb("vst_%s%d" % (name, i), [128, h, d], BF16) for i in range(2)])
            for t in vst[name].tiles:
                S.op("pool", lambda e: e.memset(t[:], 1.0), writes=[t])
        zst_r = Rot([S.sb("zst%d" % i, [128, D_MODEL], BF16) for i in range(2)])
        bA = Rot([banks[1], banks[2], banks[3]])
        bB = Rot([banks[4], banks[5]])

        def p1_prep(bd):
            u0, ntiles, w = bd["u0"], bd["ntiles"], bd["w"]
            hT = hT_r.next()
            bd["hT"] = hT
            for ti in range(ntiles):
                xt = xt_r.next()
                ss = ss_r.next()
                xn = xn_r.next()
                load_x_tile(xt, u0 // 128 + ti)
                S.op("act", lambda e: e.activation(out=junk[:], in_=xt[:], func=AF.Square, accum_out=ss[:, 0:1]), reads=[xt], writes=[junk, ss])
                S.op("act", lambda e: e.activation(out=ss[:, 1:2], in_=ss[:, 0:1], func=AF.Ln, scale=1.0 / D_MODEL, bias=epst[:]), reads=[ss, epst], writes=[ss])
                S.op("act", lambda e: e.activation(out=ss[:, 1:2], in_=ss[:, 1:2], func=AF.Exp, scale=-0.5), reads=[ss], writes=[ss])
                S.op("act", lambda e: e.activation(out=xn[:], in_=xt[:], func=AF.Copy, scale=ss[:, 1:2]), reads=[xt, ss], writes=[xn])
                for k in range(8):
                    S.op("pe", lambda e: e.transpose(out=pTb[:, k, :], in_=xn[:, k * 128:(k + 1) * 128], identity=ident[:]),
                         reads=[xn, ident], writes=[pTb_t])
                for k in range(8):
                    S.op("dve", lambda e: e.tensor_scalar(out=hT[:, k, ti * 128:(ti + 1) * 128], in0=pTb[:, k, :],
                                                          scalar1=modt[:, 8 + k, w:w + 1], scalar2=modt[:, k, w:w + 1],
                                                          op0=ALU.mult, op1=ALU.add), reads=[pTb_t, modt], writes=[hT])
                yield

        def p1_mm(bd):
            u0, ntiles, fm_list, tm_list, rope_col0, hT = bd["u0"], bd["ntiles"], bd["fm_list"], bd["tm_list"], bd["rope_col0"], bd["hT"]
            ntok = ntiles * 128
            rope_needed = any(fm[i][3] for i in fm_list) and rope_col0 is not None
            if rope_needed:
                tC = tabC_r.next()
                tS = tabS_r.next()
                S.dma(LQ, tC[:, 0:ntok], ropeC[:, rope_col0:rope_col0 + ntok], writes=[tC])
                S.dma(LQ, tS[:, 0:ntok], ropeS[:, rope_col0:rope_col0 + ntok], writes=[tS])
            chs = [dict(i=i) for i in fm_list]

            def st_main(ch):
                i = ch["i"]
                pa = bA.next()
                for k in range(8):
                    S.op("pe", lambda e: e.matmul(pa[:, 0:ntok], lhsT=win[:, k, i * 128:(i + 1) * 128], rhs=hT[:, k, 0:ntok],
                                                  start=(k == 0), stop=(k == 7)), reads=[win, hT], writes=[pa])
                ch["pa"] = pa

            def st_norm(ch):
                i = ch["i"]
                name, _, nkind, roped = fm[i]
                pa = ch["pa"]
                fo = fo_r.next()
                ch["fo"] = fo
                do_rope = roped and rope_col0 is not None
                ch["do_rope"] = do_rope
                if nkind is None and not do_rope:
                    S.op("act", lambda e: e.activation(out=fo[:, 0:ntok], in_=pa[:, 0:ntok], func=AF.Copy), reads=[pa], writes=[fo])
                    return
                qn = qn_r.next()
                ch["qn"] = qn
                if nkind is not None:
                    sq = sq_r.next()
                    rs = rs_r.next()
                    pb = bB.next()
                    gcol = 0 if nkind == "q" else 1
                    S.op("act", lambda e: e.activation(out=sq[:, 0:ntok], in_=pa[:, 0:ntok], func=AF.Square), reads=[pa], writes=[sq])
                    S.op("pe", lambda e: e.matmul(pb[:, 0:ntok], lhsT=blk[:], rhs=sq[:, 0:ntok], start=True, stop=True), reads=[blk, sq], writes=[pb])
                    S.op("act", lambda e: e.activation(out=rs[:, 0:ntok], in_=pb[:, 0:ntok], func=AF.Ln, bias=epst[:]), reads=[pb, epst], writes=[rs])
                    S.op("act", lambda e: e.activation(out=rs[:, 0:ntok], in_=rs[:, 0:ntok], func=AF.Exp, scale=-0.5), reads=[rs], writes=[rs])
                    dst = qn if do_rope else fo
                    S.op("dve", lambda e: e.scalar_tensor_tensor(out=dst[:, 0:ntok], in0=pa[:, 0:ntok], scalar=gn[:, gcol:gcol + 1],
                                                                 in1=rs[:, 0:ntok], op0=ALU.mult, op1=ALU.mult),
                         reads=[pa, gn, rs], writes=[dst])
                else:
                    S.op("act", lambda e: e.activation(out=qn[:, 0:ntok], in_=pa[:, 0:ntok], func=AF.Copy), reads=[pa], writes=[qn])

            def st_rope(ch):
                i = ch["i"]
                fo = ch["fo"]
                if ch["do_rope"]:
                    qn = ch["qn"]
                    pb2 = bB.next()
                    t1 = t1_r.next()
                    t2 = t2_r.next()
                    S.op("pe", lambda e: e.matmul(pb2[:, 0:ntok], lhsT=perm[:], rhs=qn[:, 0:ntok], start=True, stop=True), reads=[perm, qn], writes=[pb2])
                    S.op("dve", lambda e: e.tensor_tensor(out=t1[:, 0:ntok], in0=qn[:, 0:ntok], in1=tC[:, 0:ntok], op=ALU.mult), reads=[qn, tC], writes=[t1])
                    S.op("dve", lambda e: e.tensor_tensor(out=t2[:, 0:ntok], in0=pb2[:, 0:ntok], in1=tS[:, 0:ntok], op=ALU.mult), reads=[pb2, tS], writes=[t2])
                    S.op("dve", lambda e: e.tensor_tensor(out=fo[:, 0:ntok], in0=t1[:, 0:ntok], in1=t2[:, 0:ntok], op=ALU.add), reads=[t1, t2], writes=[fo])
                S.dma(SQ, fmS[i, :, u0:u0 + ntok], fo[:, 0:ntok], reads=[fo], writes=[fm_reg], sem_tile=fo)

            nchs = len(chs)
            for step in range(nchs + 2):
                if step < nchs:
                    st_main(chs[step])
                if 0 <= step - 1 < nchs:
                    st_norm(chs[step - 1])
                if 0 <= step - 2 < nchs:
                    st_rope(chs[step - 2])
                yield
            for name in tm_list:
                col0 = tmoff[name]
                ncols = dict((t[0], t[2]) for t in tm)[name]
                for ti in range(ntiles):
                    for n0 in range(0, ncols, 512):
                        nn = min(512, ncols - n0)
                        pa = bA.next()
                        for k in range(8):
                            S.op("pe", lambda e: e.matmul(pa[:, 0:nn], lhsT=hT[:, k, ti * 128:(ti + 1) * 128],
                                                          rhs=win[:, k, col0 + n0:col0 + n0 + nn], start=(k == 0), stop=(k == 7)),
                                 reads=[win, hT], writes=[pa])
                        if name == "z":
                            if n0 == 0:
                                zst = zst_r.next()
                            S.op("act", lambda e: e.activation(out=zst[:, n0:n0 + nn], in_=pa[:, 0:nn], func=AF.Silu), reads=[pa], writes=[zst])
                            if n0 + nn == ncols:
                                S.dma(SQ, zS[u0 + ti * 128:u0 + (ti + 1) * 128, :], zst[:], reads=[zst], writes=[z_reg], sem_tile=zst)
                        else:
                            h, d = vdims[name]
                            dv = d - 1
                            vt = vst[name].next()
                            S.op("dve", lambda e: e.tensor_copy(out=vt[:, :, 0:dv], in_=pa[:, 0:nn].rearrange("p (h d) -> p h d", d=dv)),
                                 reads=[pa], writes=[vt])
                            S.dma(SQ, vS[name][u0 + ti * 128:u0 + (ti + 1) * 128, :], vt[:].rearrange("p h d -> p (h d)"),
                                  reads=[vt], writes=[v_reg], sem_tile=vt)
                        yield


        all_fm = list(range(NFM))
        all_tm = [t[0] for t in tm]
        lk = [fmi[n] for n in cfg["local_k"]]
        dk = [fmi[n] for n in cfg["dense_k"]]
        blocks = [dict(u0=U_C, ntiles=2, w=1, fm_list=all_fm, tm_list=all_tm, rope_col0=None)]
        eblocks = list(range(EXT // 512))
        eblocks = [b for b in eblocks if HALO <= b * 512 < HALO + HALF] + [b for b in eblocks if not (HALO <= b * 512 < HALO + HALF)]
        for b in eblocks:
            u0 = b * 512
            own = HALO <= u0 < HALO + HALF
            if own:
                blocks.append(dict(u0=u0, ntiles=4, w=0, fm_list=all_fm, tm_list=all_tm, rope_col0=u0))
            else:
                blocks.append(dict(u0=u0, ntiles=4, w=0, fm_list=lk, tm_list=cfg["local_v"], rope_col0=u0))
        for b in range(HALF // 512):
            u0 = U_O + b * 512
            blocks.append(dict(u0=u0, ntiles=4, w=0, fm_list=dk, tm_list=cfg["dense_v"], rope_col0=u0))
        for _ in p1_prep(blocks[0]):
            pass
        for bi, bd in enumerate(blocks):
            gm = p1_mm(bd)
            gp = p1_prep(blocks[bi + 1]) if bi + 1 < len(blocks) else None
            nsteps = len(bd["fm_list"]) + 2 + sum(((dict((t[0], t[2]) for t in tm)[nm] + 511) // 512) * bd["ntiles"] for nm in bd["tm_list"])
            ntl = blocks[bi + 1]["ntiles"] if gp is not None else 0
            every = max(1, nsteps // (ntl + 1)) if ntl else 0
            k = 0
            for _ in gm:
                k += 1
                if gp is not None and every and k % every == 0:
                    next(gp, None)
            if gp is not None:
                for _ in gp:
                    pass

        S.barrier()
        st1.close()
        st2 = contextlib.ExitStack()
        S.stack = st2
        wout = S.sb("wout", [128, 8, D_MODEL], BF16)
        stage2 = Rot([S.sb("stage2_%d" % i, [128, D_MODEL], F32) for i in range(2)])
        for k in range(8):
            stg = stage2.next()
            S.dma("act" if k % 2 else LQ, stg[:], w_out[k], writes=[stg])
            S.op("dve", lambda e: e.tensor_copy(out=wout[:, k, :], in_=stg[:]), reads=[stg], writes=[wout])
        bS = Rot([banks[1], banks[2], banks[3]])
        bACC = Rot([banks[5], banks[6], banks[7]])
        nbuf_d = 1 if layer == 0 else 2
        KT_r = Rot([S.sb("KT%d" % i, [128, NKD], BF16) for i in range(nbuf_d)])
        VDW = 130 if layer == 0 else 129
        VD_r = Rot([S.sb("VD%d" % i, [128, NKC, VDW], BF16) for i in range(nbuf_d)])
        q_r = Rot([S.sb("qblk%d" % i, [128, 512], BF16) for i in range(6)])
        NLK = 9
        kl_r = Rot([S.sb("kl%d" % i, [128, NLK * 128], BF16) for i in range(5)])
        VLW = 130
        vl_r = Rot([S.sb("vl%d" % i, [128, NLK, VLW], BF16) for i in range(5)])
        y_t = [S.sb("y%d" % i, [128, D_MODEL], F32) for i in range(4)]
        rec_r = Rot([S.sb("rec%d" % i, [128, 8], F32) for i in range(4)])
        zl_r = Rot([S.sb("zl%d" % i, [128, D_MODEL], BF16) for i in range(1)])
        yb_r = Rot([S.sb("yb%d" % i, [128, D_MODEL], BF16) for i in range(1)])
        yT_r = Rot([S.sb("yT%d" % i, [128, 8, 128], BF16) for i in range(1)])
        xo_r = Rot([S.sb("xo%d" % i, [128, D_MODEL], F32) for i in range(1)])
        res_r = Rot([S.sb("res%d" % i, [128, D_MODEL], F32) for i in range(2)])
        be_r = Rot([S.sb("be%d" % i, [128, 7 * 128], F32) for i in range(3)]) if layer == 0 else None
        dcache = {}

        def load_dense(kname, vname, vc0, vw):
            key = (kname, vname, vc0)
            if dcache.get("key") == key:
                return dcache["KT"], dcache["VD"]
            KT = KT_r.next()
            VD = VD_r.next()
            i = fmi[kname]
            S.dma(LQ, KT[:, 0:HALF], fmS[i, :, U_OWN:U_OWN + HALF], reads=[fm_reg], writes=[KT])
            S.dma(LQ, KT[:, HALF:2 * HALF], fmS[i, :, U_O:U_O + HALF], reads=[fm_reg], writes=[KT])
            S.dma(LQ, KT[:, 2 * HALF:NKD], fmS[i, :, U_C:U_C + CTX], reads=[fm_reg], writes=[KT])
            for (c0, u0, n) in ((0, U_OWN, HALF), (HALF // 128, U_O, HALF), (2 * HALF // 128, U_C, CTX)):
                S.dma(LQ, VD[:, c0:c0 + n // 128, 0:vw], vS[vname][u0:u0 + n, vc0:vc0 + vw].rearrange("(k p) c -> p k c", p=128),
                      reads=[v_reg], writes=[VD])
            dcache.update(key=key, KT=KT, VD=VD)
            return KT, VD

        def attend(qt, pbase, NQ, kchunks, acc_list, vwidth, bias_fn=None):
            nq = NQ // 128
            n = len(kchunks)
            pend = []
            first = {}

            def issue_s(j):
                kt, kc0, vt, vap = kchunks[j]
                ps = bS.next()
                S.op("pe", lambda e: e.matmul(ps[:, 0:NQ], lhsT=kt[pbase:pbase + 64, kc0:kc0 + 128], rhs=qt[pbase:pbase + 64, 0:NQ],
                                              start=True, stop=True), reads=[kt, qt], writes=[ps])
                pT = pT_r.next()
                b = bias_fn(j) if bias_fn is not None else None
                if b is not None:
                    btile, bap = b
                    sb = sb_r.next()
                    S.op("dve", lambda e: e.scalar_tensor_tensor(out=sb[:, 0:NQ], in0=ps[:, 0:NQ], scalar=SCALE, in1=bap,
                                                                 op0=ALU.mult, op1=ALU.add), reads=[ps, btile], writes=[sb])
                    S.op("act", lambda e: e.activation(out=pT[:, 0:NQ], in_=sb[:, 0:NQ], func=AF.Exp), reads=[sb], writes=[pT])
                else:
                    S.op("act", lambda e: e.activation(out=pT[:, 0:NQ], in_=ps[:, 0:NQ], func=AF.Exp, scale=SCALE), reads=[ps], writes=[pT])
                return pT

            def issue_pv(j, pT):
                kt, kc0, vt, vap = kchunks[j]
                for s in range(nq):
                    acc, c0 = acc_list[s]
                    fst = first.get(id(acc), True)
                    first[id(acc)] = False
                    S.op("pe", lambda e: e.matmul(acc[:, c0:c0 + vwidth], lhsT=pT[:, s * 128:(s + 1) * 128], rhs=vap,
                                                  start=(j == 0 and fst), stop=(j == n - 1), skip_group_check=True),
                         reads=[pT, vt], writes=[acc])

            prev = None
            for j in range(n):
                pT = issue_s(j)
                if prev is not None:
                    issue_pv(prev[0], prev[1])
                prev = (j, pT)
            issue_pv(prev[0], prev[1])

        onesel_f = S.sb("onesel_f", [128, 2, 2], F32)
        S.op("pool", lambda e: e.memset(onesel_f[:], 0.0), writes=[onesel_f])
        S.op("pool", lambda e: e.memset(onesel_f[:, 0, 0:1], 1.0), writes=[onesel_f])
        S.op("pool", lambda e: e.memset(onesel_f[:, 1, 1:2], 1.0), writes=[onesel_f])
        den_acc = S.sb("den_acc", [128, 1024], F32) if layer == 1 else None
        onesel_b = S.sb("onesel_b", [128, 2, 2], BF16)
        S.op("dve", lambda e: e.tensor_copy(out=onesel_b[:], in_=onesel_f[:]), reads=[onesel_f], writes=[onesel_b])
        if layer == 1:
            dmb = S.sb("dmb", [128, 3, 384], F32)
            S.op("pool", lambda e: e.memset(dmb[:], 0.0), writes=[dmb])
            for var, (pi, ni) in enumerate(((2, 3), (0, 3), (2, 1))):
                S.op("dve", lambda e: e.tensor_copy(out=dmb[:, var, 0:128], in_=dm_t[:, pi, :]), reads=[dm_t], writes=[dmb])
                S.op("dve", lambda e: e.tensor_copy(out=dmb[:, var, 256:384], in_=dm_t[:, ni, :]), reads=[dm_t], writes=[dmb])
        rec16_r = Rot([S.sb("rec16_%d" % i, [128, 16], F32) for i in range(2)])
        Tq = S.sb("Tq", [128, 4, 256], F32) if layer == 1 else None
        pT2_r = Rot([S.sb("pT2_%d" % i, [128, 1024], BF16) for i in range(3)])
        oT_r = Rot([S.sb("oT%d" % i, [128, 512], F32) for i in range(2)])
        pairs = SH["pairs"]

        def dense_pair(qt, KT, VD, vap_fn, vw, accs, den=None):
            n = NKC

            def issue_s(j):
                pt, ta, tb = pairs[j % 2]
                for m, tt in ((0, ta), (1, tb)):
                    S.op("pe", lambda e: e.matmul(tt[:, 0:512], lhsT=KT[64 * m:64 * m + 64, j * 128:(j + 1) * 128],
                                                  rhs=qt[64 * m:64 * m + 64, 0:512], start=True, stop=True), reads=[KT, qt], writes=[tt])
                pT = pT2_r.next()
                S.op("act", lambda e: e.activation(out=pT[:], in_=pt[:, :], func=AF.Exp, scale=SCALE), reads=[ta, tb], writes=[pT])
                return pT

            def issue_pv(j, pT):
                for m in range(2):
                    S.op("pe", lambda e: e.matmul(accs[m][0:vw, 0:512], lhsT=vap_fn(j, m), rhs=pT[:, m * 512:(m + 1) * 512],
                                                  start=(j == 0), stop=(j == n - 1)), reads=[pT, VD], writes=[accs[m]])
                if den is not None:
                    if j == 0:
                        S.op("dve", lambda e: e.tensor_copy(out=den_acc[:, 0:512], in_=pT[:, 0:512]), reads=[pT], writes=[den_acc])
                    else:
                        S.op("dve", lambda e: e.tensor_tensor(out=den_acc[:, 0:512], in0=den_acc[:, 0:512], in1=pT[:, 0:512], op=ALU.add),
                             reads=[pT, den_acc], writes=[den_acc])
                    S.op("pe", lambda e: e.matmul(den[0:2, 0:512], lhsT=onesel_b[:, 1, :], rhs=pT[:, 512:1024],
                                                  start=(j == 0), stop=False, skip_group_check=True), reads=[pT, onesel_b], writes=[den])
                    if j == n - 1:
                        S.op("pe", lambda e: e.matmul(den[0:2, 0:512], lhsT=onesel_f[:, 0, :], rhs=den_acc[:, 0:512],
                                                      start=False, stop=True, skip_group_check=True), reads=[den_acc, onesel_f], writes=[den])

            pend = []
            for j in range(n):
                pend.append((j, issue_s(j)))
                if len(pend) > 2:
                    issue_pv(*pend.pop(0))
            while pend:
                issue_pv(*pend.pop(0))

        def untranspose(acc, rows, fin, width):
            oT = oT_r.next()
            S.op("act", lambda e: e.activation(out=oT[0:rows, :], in_=acc[0:rows, 0:512], func=AF.Copy), reads=[acc], writes=[oT])
            for sidx in range(4):
                S.op("pe", lambda e: e.transpose(out=fin[:, sidx * width:sidx * width + rows], in_=oT[0:rows, sidx * 128:(sidx + 1) * 128],
                                                 identity=ident_f[0:rows, 0:rows]), reads=[oT, ident_f], writes=[fin])

        wide_i = [0]

        def wide_a(job):
            qt, pbase, kl, nch, nb, bias = job["qt"], job["pbase"], job["kl"], job["nch"], job["nb"], job["bias"]
            pt, ta, tb = pairs[wide_i[0] % 2]
            wide_i[0] += 1
            for j in range(nch):
                tt = ta if j < 4 else tb
                S.op("pe", lambda e: e.matmul(pt[:, j * 128:(j + 1) * 128], lhsT=kl[pbase:pbase + 64, j * 128:(j + 1) * 128],
                                              rhs=qt[pbase:pbase + 64, 0:128], start=True, stop=True), reads=[kl, qt], writes=[tt])
            pT = pT2_r.next()
            used = [ta] + ([tb] if nch > 4 else [])
            if bias is not None:
                btile, bap = bias
                sbw = stage2.next()
                S.op("dve", lambda e: e.scalar_tensor_tensor(out=sbw[:, 0:nb * 128], in0=pt[:, 0:nb * 128], scalar=SCALE, in1=bap,
                                                             op0=ALU.mult, op1=ALU.add), reads=used + [btile], writes=[sbw])
                S.op("act", lambda e: e.activation(out=pT[:, 0:nb * 128], in_=sbw[:, 0:nb * 128], func=AF.Exp), reads=[sbw], writes=[pT])
                if nch > nb:
                    S.op("act", lambda e: e.activation(out=pT[:, nb * 128:nch * 128], in_=pt[:, nb * 128:nch * 128], func=AF.Exp, scale=SCALE),
                         reads=used, writes=[pT])
            else:
                S.op("act", lambda e: e.activation(out=pT[:, 0:nch * 128], in_=pt[:, 0:nch * 128], func=AF.Exp, scale=SCALE), reads=used, writes=[pT])
            job["pT"] = pT

        def wide_b(job):
            pT, vl, vc0, nch = job["pT"], job["vl"], job["vc0"], job["nch"]
            acc = bACC.next()
            for j in range(nch):
                S.op("pe", lambda e: e.matmul(acc[:, 0:65], lhsT=pT[:, j * 128:(j + 1) * 128], rhs=vl[:, j, vc0:vc0 + 65],
                                              start=(j == 0), stop=(j == nch - 1)), reads=[pT, vl], writes=[acc])
            finish_head([(acc, 0)], 65, [job["yt"]], job["ycol"], extra_den=job.get("extra_den"))

        def run_wide(jobs):
            pend = []
            for job in jobs:
                if "pre" in job:
                    job["pre"]()
                wide_a(job)
                pend.append(job)
                if len(pend) > 2:
                    wide_b(pend.pop(0))
            while pend:
                wide_b(pend.pop(0))

        def finish_head(acc_list, vwidth, y_tiles, ycol, extra_den=None, scale_ap=None):
            dv = vwidth - 1
            for s, (acc, c0) in enumerate(acc_list):
                rec = rec_r.next()
                if extra_den is not None:
                    S.op("dve", lambda e: e.tensor_tensor(out=rec[:, 0:1], in0=acc[:, c0 + dv:c0 + dv + 1], in1=extra_den, op=ALU.add),
                         reads=[acc, esnk], writes=[rec])
                    S.op("dve", lambda e: e.reciprocal(out=rec[:, 1:2], in_=rec[:, 0:1]), reads=[rec], writes=[rec])
                else:
                    S.op("dve", lambda e: e.reciprocal(out=rec[:, 1:2], in_=acc[:, c0 + dv:c0 + dv + 1]), reads=[acc], writes=[rec])
                yt = y_tiles[s]
                S.op("act", lambda e: e.activation(out=yt[:, ycol:ycol + dv], in_=acc[:, c0:c0 + dv], func=AF.Copy, scale=rec[:, 1:2]),
                     reads=[acc, rec], writes=[yt])

        def out_tile(yt, u_tok, src, src_row, w, dst, dst_row):
            zl = zl_r.next()
            yb = yb_r.next()
            yT = yT_r.next()
            xo = xo_r.next()
            res = res_r.next()
            S.dma(LQ, zl[:], zS[u_tok:u_tok + 128, :], reads=[z_reg], writes=[zl])
            load_x_tile(xo, u_tok // 128)
            S.op("dve", lambda e: e.tensor_tensor(out=yb[:], in0=yt[:], in1=zl[:], op=ALU.mult), reads=[yt, zl], writes=[yb])
            for k in range(8):
                S.op("pe", lambda e: e.transpose(out=pTb[:, k, :], in_=yb[:, k * 128:(k + 1) * 128], identity=ident[:]),
                     reads=[yb, ident], writes=[pTb_t])
            S.op("act", lambda e: e.activation(out=yT[:].rearrange("p k t -> p (k t)"), in_=pTb[:].rearrange("p k t -> p (k t)"), func=AF.Copy),
                 reads=[pTb_t], writes=[yT])
            for n in range(2):
                po = bACC.next()
                for k in range(8):
                    S.op("pe", lambda e: e.matmul(po[:], lhsT=yT[:, k, :], rhs=wout[:, k, n * 512:(n + 1) * 512], start=(k == 0), stop=(k == 7)),
                         reads=[yT, wout], writes=[po])
                S.op("dve", lambda e: e.tensor_tensor(out=res[:, n * 512:(n + 1) * 512], in0=po[:], in1=gate_bc[:, w, n * 512:(n + 1) * 512], op=ALU.mult),
                     reads=[po, gate_bc], writes=[res])
            S.op("dve", lambda e: e.tensor_tensor(out=res[:], in0=res[:], in1=xo[:], op=ALU.add), reads=[res, xo], writes=[res])
            if last:
                ss = ss_r.next()
                S.op("act", lambda e: e.activation(out=junk[:], in_=res[:], func=AF.Square, accum_out=ss[:, 0:1]), reads=[res], writes=[junk, ss])
                S.op("act", lambda e: e.activation(out=ss[:, 1:2], in_=ss[:, 0:1], func=AF.Ln, scale=1.0 / D_MODEL, bias=epst[:]), reads=[ss, epst], writes=[ss])
                S.op("act", lambda e: e.activation(out=ss[:, 1:2], in_=ss[:, 1:2], func=AF.Exp, scale=-0.5), reads=[ss], writes=[ss])
                S.op("act", lambda e: e.activation(out=xo[:], in_=res[:], func=AF.Copy, scale=ss[:, 1:2]), reads=[res, ss], writes=[xo])
                S.op("dve", lambda e: e.tensor_tensor(out=res[:], in0=xo[:], in1=fn_t[:], op=ALU.mult), reads=[xo, fn_t], writes=[res])
            S.dma(SQ, dst[dst_row:dst_row + 128, :], res[:], reads=[res], sem_tile=res)
            return res

        out_tiles = []

        def load_q(name, u0, n):
            qt = q_r.next()
            S.dma(LQ, qt[:, 0:n], fmS[fmi[name], :, u0:u0 + n], reads=[fm_reg], writes=[qt])
            return qt

        def load_local(knames_idx, vname, vc0, vw, utiles):
            kl = kl_r.next()
            vl = vl_r.next()
            pos = 0
            for (ut0, cnt) in utiles:
                S.dma(LQ, kl[:, pos * 128:(pos + cnt) * 128], fmS[knames_idx, :, ut0 * 128:(ut0 + cnt) * 128], reads=[fm_reg], writes=[kl])
                S.dma(LQ, vl[:, pos:pos + cnt, 0:vw], vS[vname][ut0 * 128:(ut0 + cnt) * 128, vc0:vc0 + vw].rearrange("(k p) c -> p k c", p=128),
                      reads=[v_reg], writes=[vl])
                pos += cnt
            return kl, vl

        UC_T = U_C // 128

        if layer == 0:
            for t in range(2):
                yt = y_t[t]
                u0 = U_C + t * 128
                jobs = []
                kl, vl = load_local(fmi["ka"], "va", 0, 130, [(UC_T, 2)])
                for c in range(4):
                    qt = load_q("qa%d" % c, u0, 128)
                    for s_ in range(2):
                        jobs.append(dict(qt=qt, pbase=64 * s_, kl=kl, vl=vl, vc0=s_ * 65, nch=2, nb=0, bias=None, yt=yt, ycol=(c + 4 * s_) * 64))
                run_wide(jobs)
                for c in range(4):
                    kl, vl = load_local(fmi["kb%d" % c], "vb", c * 130, 130, [(UC_T, 2)])
                    qt = load_q("qb%d" % c, u0, 128)
                    run_wide([dict(qt=qt, pbase=64 * s_, kl=kl, vl=vl, vc0=s_ * 65, nch=2, nb=0, bias=None, yt=yt, ycol=512 + (2 * c + s_) * 64)
                              for s_ in range(2)])
                out_tiles.append(out_tile(yt, u0, xc_in, t * 128, 1, out_c, t * 128))

        for qb in range(NBo):
            u0 = U_OWN + qb * 512
            if layer == 0:
                KT, VD = load_dense("ka", "va", 0, 130)
                for c in range(4):
                    qt = load_q("qa%d" % c, u0, 512)
                    accs = [banks[5], banks[6]]
                    dense_pair(qt, KT, VD, lambda j, m: VD[:, j, m * 65:(m + 1) * 65], 65, accs)
                    for s in range(2):
                        head = c + 4 * s
                        fin = banks[7]
                        untranspose(accs[s], 65, fin, 65)
                        finish_head([(fin, i * 65) for i in range(4)], 65, y_t, head * 64)
            else:
                for h in range(4):
                    KT, VD = load_dense("kc%d" % h, "vc", h * 129, 129)
                    qt = load_q("qc%d" % h, u0, 512)
                    accs = [banks[5], banks[6]]
                    den = banks[7]
                    dense_pair(qt, KT, VD, lambda j, m: VD[:, j, 0:128], 128, accs, den=den)
                    fins = [banks[1], banks[2]]
                    untranspose(accs[0], 128, fins[0], 128)
                    untranspose(accs[1], 128, fins[1], 128)
                    dfin = banks[3]
                    untranspose(den, 2, dfin, 2)
                    o_m = [[(fins[0], i * 128, i * 2 + 0) for i in range(4)], [(fins[1], i * 128, i * 2 + 1) for i in range(4)]]
                    R = rec16_r.next()
                    S.op("dve", lambda e: e.reciprocal(out=R[:, 0:8], in_=dfin[:, 0:8]), reads=[dfin], writes=[R])
                    S.op("dve", lambda e: e.tensor_scalar(out=R[:, 8:12], in0=R[:, 0:8].rearrange("p (s m) -> p s m", m=2)[:, :, 1],
                                                          scalar1=lam_s[:, 3:4], scalar2=None, op0=ALU.mult), reads=[R, lam_s], writes=[R])
                    for s in range(4):
                        a0, c0, d0 = o_m[0][s]
                        a1, c1, d1 = o_m[1][s]
                        S.op("act", lambda e: e.activation(out=Tq[:, s, 0:128], in_=a0[:, c0:c0 + 128], func=AF.Copy, scale=R[:, 2 * s:2 * s + 1]),
                             reads=[a0, R], writes=[Tq])
                        S.op("dve", lambda e: e.scalar_tensor_tensor(out=Tq[:, s, 128:256], in0=a1[:, c1:c1 + 128], scalar=R[:, 8 + s:9 + s], in1=Tq[:, s, 0:128],
                                                                     op0=ALU.mult, op1=ALU.add), reads=[a1, R, Tq], writes=[Tq])
                        S.op("act", lambda e: e.activation(out=Tq[:, s, 0:128], in_=Tq[:, s, 128:256], func=AF.Square, accum_out=R[:, 12 + s:13 + s]),
                             reads=[Tq], writes=[Tq, R])
                    S.op("act", lambda e: e.activation(out=R[:, 12:16], in_=R[:, 12:16], func=AF.Ln, scale=1.0 / 128, bias=epst[:]), reads=[R, epst], writes=[R])
                    S.op("act", lambda e: e.activation(out=R[:, 12:16], in_=R[:, 12:16], func=AF.Exp, scale=-0.5), reads=[R], writes=[R])
                    S.op("dve", lambda e: e.tensor_scalar_mul(out=R[:, 12:16], in0=R[:, 12:16], scalar1=1.0 - lam0), reads=[R], writes=[R])
                    for s in range(4):
                        yt = y_t[s]
                        S.op("dve", lambda e: e.scalar_tensor_tensor(out=yt[:, h * 128:(h + 1) * 128], in0=Tq[:, s, 128:256], scalar=R[:, 12 + s:13 + s], in1=sub_t[:],
                                                                     op0=ALU.mult, op1=ALU.mult), reads=[Tq, R, sub_t], writes=[yt])
            for tl in range(4):
                t = qb * 4 + tl
                ut = U_OWN // 128 + t
                yt = y_t[tl]
                if layer == 0:
                    edge = t < 2 or t >= NTo - 2
                    if edge:
                        et = t if t < 2 else 2 + (t - (NTo - 2))
                        lo, hi = ((-2, 3), (-2, 2), (-2, 2), (-3, 2))[et]
                    else:
                        lo, hi = -2, 2
                    nb = hi - lo + 1
                    runs = [(ut + lo, nb), (UC_T, 2)]
                    jobs = []
                    for c in range(4):
                        kl, vl = load_local(fmi["kb%d" % c], "vb", c * 130, 130, runs)
                        qt = load_q("qb%d" % c, ut * 128, 128)
                        for s_ in range(2):
                            head = 2 * c + s_
                            job = dict(qt=qt, pbase=64 * s_, kl=kl, vl=vl, vc0=s_ * 65, nch=nb + 2, nb=nb,
                                       yt=yt, ycol=512 + head * 64)
                            if edge:
                                def pre(job=job, et=et, head=head, lo=lo, hi=hi):
                                    be = be_r.next()
                                    S.dma(LQ, be[:], bias_e[et, head], writes=[be])
                                    job["bias"] = (be, be[:, (lo + 3) * 128:(hi + 4) * 128])
                                job["pre"] = pre
                            else:
                                job["bias"] = (bi_t, bi_t[:, head, 0:640])
                            jobs.append(job)
                    run_wide(jobs)
                else:
                    kl, vl = load_local(fmi["kd"], "vd", 0, 130, [(ut - 1, 3), (UC_T, 2)])
                    var = 1 if t == 0 else (2 if t == NTo - 1 else 0)
                    jobs = []
                    for c in range(4):
                        qt = load_q("qd%d" % c, ut * 128, 128)
                        for s_ in range(2):
                            head = c + 4 * s_
                            jobs.append(dict(qt=qt, pbase=64 * s_, kl=kl, vl=vl, vc0=s_ * 65, nch=5, nb=3, bias=(dmb, dmb[:, var, :]),
                                             yt=yt, ycol=512 + head * 64, extra_den=esnk[:, head:head + 1]))
                    run_wide(jobs)
                out_tiles.append(out_tile(yt, ut * 128, x_u, ut * 128, 0, out_x, t * 128))
        if last:
            S.finish(out_tiles)
        else:
            S.barrier()
        st2.close()
    if not last:
        S.release_dsems()
        S.new_epoch()


def build_fused(SEQ, B):
    HALF = SEQ // 2
    EXT = HALF + 2 * HALO
    nc = bass.Bass("TRN2", target_bir_lowering=False)

    def din(name, shape):
        return nc.dram_tensor(name, list(shape), F32, kind="ExternalInput").ap()

    SH = dict(x_u=din("x_u", [EXT + HALF, D_MODEL]), xc=din("xc", [CTX, D_MODEL]), cvec=din("cvec", [128, 8, 2]),
              ident=din("ident", [128, 128]), blk=din("blk", [128, 128]), perm=din("perm", [128, 128]),
              ropeC=din("ropeC", [128, EXT + HALF]), ropeS=din("ropeS", [128, EXT + HALF]), sel=din("sel", [128, 4]), sel2=din("sel2", [2, 2, 128]))
    SH["x1_loc"] = nc.dram_tensor("x1_loc", [HALF, D_MODEL], F32).ap()
    SH["xc1_loc"] = nc.dram_tensor("xc1_loc", [CTX, D_MODEL], F32).ap()
    SH["GA"] = nc.dram_tensor("x1_all", [2 * HALF, D_MODEL], F32).ap()
    with contextlib.ExitStack() as st0:
        S = Sched(nc, st0)
        SH["pTb_t"] = S.ps("bankT", [128, 8, 128], BF16)
        pairA = st0.enter_context(nc.psum_tensor("ps_pairA", [128, 1024], F32))
        pairB = st0.enter_context(nc.psum_tensor("ps_pairB", [128, 1024], F32))
        b1 = Tile("bank1", pairA[:, 0:512], excl=True)
        b2 = Tile("bank2", pairA[:, 512:1024], excl=True)
        b3 = Tile("bank3", pairB[:, 0:512], excl=True)
        b4 = Tile("bank4", pairB[:, 512:1024], excl=True)
        SH["pairs"] = [(pairA, b1, b2), (pairB, b3, b4)]
        SH["banks"] = [SH["pTb_t"], b1, b2, b3, b4] + [S.ps("bank%d" % i, [128, 512], F32) for i in range(5, 8)]
        SH["G_reg"] = Tile("G_reg")
        emit_layer(nc, S, SH, 0, SEQ)
        S.sems["cc"] = st0.enter_context(nc.semaphore("s_cc"))
        RC = min(512, HALF)
        for i in range(HALF // RC):
            nc.gpsimd.collective_compute("AllGather", ALU.bypass, replica_groups=[[2 * b, 2 * b + 1] for b in range(B)],
                                         ins=[SH["x1_loc"][i * RC:(i + 1) * RC, :]],
                                         outs=[SH["GA"][i * 2 * RC:(i + 1) * 2 * RC, :]]).then_inc(S.sems["cc"], 1)
        SH["G_reg"].last_w = ("cc", HALF // RC)
        emit_layer(nc, S, SH, 1, SEQ)
    return nc


def rope_tables(pos):
    row = (pos // GRID_W).astype(np.float32)
    col = (pos % GRID_W).astype(np.float32)
    q = HD // 4
    inv = (10000.0 ** (-np.arange(q, dtype=np.float32) / q)).astype(np.float32)
    ar = row[None, :] * inv[:, None]
    ac = col[None, :] * inv[:, None]
    cr, sr, cc, sc = np.cos(ar), np.sin(ar), np.cos(ac), np.sin(ac)
    C = np.concatenate([cr, cr, cc, cc], axis=0)
    Sg = np.concatenate([-sr, sr, -sc, sc], axis=0)
    return (np.concatenate([C, C], 0).astype(np.float32), np.concatenate([Sg, Sg], 0).astype(np.float32))


def nbr_bias(rpb, g, gk, NT):
    rows = NT * 2
    out = np.full((8, 128, 128), NEG, np.float32)
    if gk < 0 or gk >= NT:
        return out
    ql = np.arange(128)
    r = 2 * g + ql // 64
    c = ql % 64
    kr = 2 * gk + ql // 64
    kc = ql % 64
    win_r = min(8, rows)
    rs = np.clip(r - win_r // 2, 0, rows - win_r)
    cs = np.clip(c - 8, 0, GRID_W - 16)
    valid = ((kr[:, None] >= rs[None, :]) & (kr[:, None] < rs[None, :] + win_r)
             & (kc[:, None] >= cs[None, :]) & (kc[:, None] < cs[None, :] + 16))
    di = kr[:, None] - r[None, :] + 7
    dj = kc[:, None] - c[None, :] + 15
    di = np.clip(di, 0, 14)
    dj = np.clip(dj, 0, 30)
    vals = rpb[:, di, dj]
    return np.where(valid[None], vals, np.float32(NEG)).astype(np.float32)


def chunk_rows(w):
    return np.ascontiguousarray(w.reshape(8, 128, w.shape[1]))


def prep_layer_inputs(layer, SEQ, xs, xcs, p):
    B = xs.shape[0]
    HALF = SEQ // 2
    EXT = HALF + 2 * HALO
    NT = SEQ // 128
    NTo = HALF // 128
    cfg = layer_cfg(layer)
    wi = p["w_in_even"][0] if layer == 0 else p["w_in_odd"][0]
    wo = p["w_out_even"][0] if layer == 0 else p["w_out_odd"][0]
    cols = []
    for f in cfg["fm"]:
        cols += f[1]
    for name, c0, n in cfg["tm"]:
        cols += list(range(c0, c0 + n))
    w_in_l = chunk_rows(np.ascontiguousarray(wi[:, cols]))
    w_out_l = chunk_rows(wo)
    w_mod_l = chunk_rows(p["w_mod"][layer])
    b_mod_l = np.ascontiguousarray(p["b_mod"][layer].reshape(24, 128).T)
    bgate = np.ascontiguousarray(np.broadcast_to(p["b_mod"][layer][2048:3072][None, :], (128, D_MODEL)))
    ident = np.eye(128, dtype=np.float32)
    blk = np.zeros((128, 128), np.float32)
    blk[:64, :64] = 1.0 / 64
    blk[64:, 64:] = 1.0 / 64
    perm = np.zeros((128, 128), np.float32)
    for m in range(128):
        k = m + 16 if (m % 32) < 16 else m - 16
        perm[k, m] = 1.0
    maps = []
    for b in range(B):
        for half in range(2):
            T0 = half * HALF
            pos_e = np.arange(T0 - HALO, T0 + HALF + HALO)
            valid_e = (pos_e >= 0) & (pos_e < SEQ)
            x_e = np.zeros((EXT, D_MODEL), np.float32)
            x_e[valid_e] = xs[b, pos_e[valid_e]]
            T1 = (1 - half) * HALF
            pos_o = np.arange(T1, T1 + HALF)
            x_u = np.concatenate([x_e, xs[b, pos_o]], axis=0)
            pos_u = np.concatenate([np.clip(pos_e, 0, SEQ - 1), pos_o])
            C, Sg = rope_tables(pos_u)
            cvec = np.stack([p["c"][b].reshape(8, 128).T, p["c_ctx"].reshape(8, 128).T], axis=-1)
            m = dict(x_u=x_u, xc=np.ascontiguousarray(xcs[b]), cvec=np.ascontiguousarray(cvec), w_mod=w_mod_l, b_mod=b_mod_l,
                     bgate=bgate, w_in=w_in_l, w_out=w_out_l, ident=ident, blk=blk, perm=perm, ropeC=C, ropeS=Sg)
            G0 = T0 // 128
            if layer == 0:
                m["gains"] = np.ascontiguousarray(np.stack([np.tile(p["a_q_norm"][0], 2), np.tile(p["a_k_norm"][0], 2)], axis=-1))
                rpb = p["b_rpb"][0]
                gi = min(max(G0 + 2, 2), NT - 3) if NT >= 6 else 0
                bi = np.stack([nbr_bias(rpb, gi, gi + j, NT) for j in range(-2, 3)], axis=0)
                m["bias_i"] = np.ascontiguousarray(bi.transpose(2, 1, 0, 3).reshape(128, 8, 640))
                ets = [0, 1, NTo - 2, NTo - 1]
                be = np.stack([np.stack([nbr_bias(rpb, G0 + t, G0 + t + j, NT) for j in range(-3, 4)], axis=0) for t in ets], axis=0)
                m["bias_e"] = np.ascontiguousarray(be.transpose(0, 2, 3, 1, 4).reshape(4, 8, 128, 896))
            else:
                a = np.arange(128)
                tri_prev = np.where(a[:, None] >= a[None, :], 0.0, NEG).astype(np.float32)
                tri_next = np.where(a[:, None] <= a[None, :], 0.0, NEG).astype(np.float32)
                full = np.full((128, 128), NEG, np.float32)
                first_prev = full if G0 == 0 else tri_prev
                last_next = full if G0 + NTo == NT else tri_next
                m["dmask"] = np.ascontiguousarray(np.stack([first_prev, last_next, tri_prev, tri_next], axis=1))
                m["lamv"] = np.ascontiguousarray(np.broadcast_to(p["c_lambda"][0].reshape(1, 256), (128, 256)))
                m["subln"] = np.ascontiguousarray(np.broadcast_to((p["c_subln"][0])[None, :], (128, 128)))
                m["sinks"] = np.ascontiguousarray(np.broadcast_to(p["d_sinks"][0][None, :], (128, 8)))
                m["fnorm"] = np.ascontiguousarray(np.broadcast_to(p["final_norm"][None, :], (128, D_MODEL)))
            maps.append(m)
    return maps


def prep_fused_inputs(SEQ, xs, xcs, p):
    m0 = prep_layer_inputs(0, SEQ, xs, xcs, p)
    m1 = prep_layer_inputs(1, SEQ, xs, xcs, p)
    shared = ("x_u", "xc", "cvec", "ident", "blk", "perm", "ropeC", "ropeS")
    maps = []
    for i, (a, b) in enumerate(zip(m0, m1)):
        half = i % 2
        m = {k: a[k] for k in shared}
        for k, v in a.items():
            if k not in shared:
                m["l0_" + k] = v
        for k, v in b.items():
            if k not in shared:
                m["l1_" + k] = v
        sel = np.zeros((128, 4), np.float32)
        sel[:, 0] = 1.0 if half == 1 else 0.0
        sel[:, 1] = 1.0 if half == 0 else 0.0
        sel[:, 2] = 1.0 if half == 1 else 0.0
        sel[:, 3] = 1.0 if half == 0 else 0.0
        m["sel"] = sel
        sel2 = np.zeros((2, 2, 128), np.float32)
        sel2[0, 0, :] = 1.0
        sel2[1, 1, :] = 1.0
        m["sel2"] = sel2
        maps.append(m)
    return maps


def run_fused(SEQ, xs, xcs, p, runner=None):
    B = xs.shape[0]
    key = ("fused", SEQ, B)
    if key not in _NC_CACHE:
        _NC_CACHE[key] = build_fused(SEQ, B)
    nc = _NC_CACHE[key]
    maps = prep_fused_inputs(SEQ, xs, xcs, p)
    if runner is None:
        res = run_bass_kernel_spmd(nc, maps, core_ids=list(range(len(maps)))).results
    else:
        res = runner(nc, maps)
    HALF = SEQ // 2
    xo = np.zeros_like(xs)
    for b in range(B):
        for half in range(2):
            xo[b, half * HALF:(half + 1) * HALF] = res[2 * b + half]["out_x"]
    return xo


_NC_CACHE = {}


def kernel(x, c, ctx, c_ctx, w_mod, b_mod, w_in_even, w_out_even, a_q_norm, a_k_norm, b_rpb,
           w_in_odd, w_out_odd, c_lambda, c_subln, d_sinks, final_norm):
    p = dict(c=np.asarray(c, np.float32), c_ctx=np.asarray(c_ctx, np.float32), w_mod=np.asarray(w_mod, np.float32),
             b_mod=np.asarray(b_mod, np.float32), w_in_even=np.asarray(w_in_even, np.float32),
             w_out_even=np.asarray(w_out_even, np.float32), a_q_norm=np.asarray(a_q_norm, np.float32),
             a_k_norm=np.asarray(a_k_norm, np.float32), b_rpb=np.asarray(b_rpb, np.float32),
             w_in_odd=np.asarray(w_in_odd, np.float32), w_out_odd=np.asarray(w_out_odd, np.float32),
             c_lambda=np.asarray(c_lambda, np.float32), c_subln=np.asarray(c_subln, np.float32),
             d_sinks=np.asarray(d_sinks, np.float32), final_norm=np.asarray(final_norm, np.float32))
    xs = np.asarray(x, np.float32)
    xcs = np.asarray(ctx, np.float32)
    SEQ = xs.shape[1]
    return run_fused(SEQ, xs, xcs, p)
```

```python
import contextlib
import math
import numpy as np
import concourse.bass as bass
import concourse.mybir as mybir
from concourse.bass_utils import run_bass_kernel_spmd

F32 = mybir.dt.float32
BF16 = mybir.dt.bfloat16
AF = mybir.ActivationFunctionType
ALU = mybir.AluOpType

D_MODEL = 1024
CTX = 256
HD = 64
GRID_W = 64
SCALE = HD ** -0.5
EPS = 1e-6
NEG = -30000.0
HALO = 512
LQ = "sp"
SQ = "pool"


class Tile:
    __slots__ = ("name", "t", "last_w", "readers", "dsem", "dcount", "excl")

    def __init__(self, name, t=None, excl=False):
        self.name = name
        self.t = t
        self.last_w = None
        self.readers = {}
        self.dsem = None
        self.dcount = 0
        self.excl = excl

    def __getitem__(self, idx):
        return self.t[idx]


class Sched:
    def __init__(self, nc, stack):
        self.nc = nc
        self.stack = stack
        self.sem_stack = stack
        self.dtiles = []
        self.engs = {}
        self.sems = {}
        for en, e in (("pe", nc.tensor), ("act", nc.scalar), ("dve", nc.vector),
                      ("pool", nc.gpsimd), ("sp", nc.sync)):
            self.sems[en] = stack.enter_context(nc.semaphore("s_" + en))
            self.engs[en] = dict(eng=e, count=0, seen={}, key=en)
        self.epoch = 0
        self.nsem = 0
        self.prefix = ""
        self.free_dsems = []

    def sb(self, name, shape, dt):
        return Tile(name, self.stack.enter_context(self.nc.sbuf_tensor("sb_" + self.prefix + name, list(shape), dt)))

    def ps(self, name, shape, dt=F32):
        return Tile(name, self.stack.enter_context(self.nc.psum_tensor("ps_" + name, list(shape), dt)), excl=True)

    def _dsem(self, tile):
        if tile.dsem is None:
            if self.free_dsems:
                key, cnt = self.free_dsems.pop()
                tile.dsem = key
                tile.dcount = cnt
            else:
                key = "d%d" % self.nsem
                self.nsem += 1
                tile.dsem = key
                self.sems[key] = self.sem_stack.enter_context(self.nc.semaphore(key))
            self.dtiles.append(tile)
        return tile.dsem

    def new_epoch(self):
        self.epoch += 1
        for en, E in self.engs.items():
            key = "%s#%d" % (en, self.epoch)
            self.sems[key] = self.sem_stack.enter_context(self.nc.semaphore("s_%s_%d" % (en, self.epoch)))
            E["key"] = key
            E["count"] = 0

    def release_dsems(self):
        for t in self.dtiles:
            self.free_dsems.append((t.dsem, t.dcount))
            t.dsem = None
        self.dtiles = []

    def _wait_deps(self, en, reads, writes):
        E = self.engs[en]
        deps = {}

        def add(ev):
            if ev is None:
                return
            k, v = ev
            if deps.get(k, 0) < v:
                deps[k] = v

        me = E["key"]
        for t in reads:
            add(t.last_w)
            if t.excl:
                for k, v in t.readers.items():
                    if k != me:
                        add((k, v))
        for t in writes:
            add(t.last_w)
            for k, v in t.readers.items():
                if k != me:
                    add((k, v))
        for k, v in deps.items():
            if E["seen"].get(k, 0) < v:
                E["seen"][k] = v
                if k == me and en == "pe":
                    continue
                E["eng"].wait_ge(self.sems[k], v)

    def op(self, en, fn, reads=(), writes=()):
        E = self.engs[en]
        self._wait_deps(en, reads, writes)
        ins = fn(E["eng"])
        E["count"] += 1
        me = E["key"]
        ins.then_inc(self.sems[me], 1)
        for t in reads:
            t.readers[me] = E["count"]
        for t in writes:
            t.last_w = (me, E["count"])
            t.readers = {}
        return ins

    def dma(self, q, out, in_, reads=(), writes=(), sem_tile=None):
        E = self.engs[q]
        self._wait_deps(q, reads, writes)
        st = sem_tile if sem_tile is not None else (list(writes) + list(reads))[0]
        key = self._dsem(st)
        ins = E["eng"].dma_start(out=out, in_=in_)
        st.dcount += 16
        ins.then_inc(self.sems[key], 16)
        for t in reads:
            t.readers[key] = st.dcount
        for t in writes:
            t.last_w = (key, st.dcount)
            t.readers = {}
        return ins

    def barrier(self):
        for en, E in self.engs.items():
            for en2, E2 in self.engs.items():
                k2 = E2["key"]
                if en2 != en and E2["count"] and E["seen"].get(k2, 0) < E2["count"]:
                    E["seen"][k2] = E2["count"]
                    E["eng"].wait_ge(self.sems[k2], E2["count"])
            for t in self.dtiles:
                if t.dcount and E["seen"].get(t.dsem, 0) < t.dcount:
                    E["seen"][t.dsem] = t.dcount
                    E["eng"].wait_ge(self.sems[t.dsem], t.dcount)

    def finish(self, tiles, en="sp"):
        self._wait_deps(en, tiles, tiles)


class Rot:
    def __init__(self, tiles):
        self.tiles = tiles
        self.i = 0

    def next(self):
        t = self.tiles[self.i % len(self.tiles)]
        self.i += 1
        return t


def layer_cfg(layer):
    if layer == 0:
        qa, ka, va, qb, kb, vb, z = 0, 512, 640, 768, 1280, 1792, 2304
        fm = []
        for c in range(4):
            cols = list(range(qa + c * 64, qa + c * 64 + 64)) + list(range(qa + (4 + c) * 64, qa + (4 + c) * 64 + 64))
            fm.append(("qa%d" % c, cols, "q", True))
        fm.append(("ka", list(range(ka, ka + 128)), "k", True))
        for c in range(4):
            fm.append(("qb%d" % c, list(range(qb + c * 128, qb + c * 128 + 128)), None, False))
        for c in range(4):
            fm.append(("kb%d" % c, list(range(kb + c * 128, kb + c * 128 + 128)), None, False))
        tm = [("va", va, 128), ("vb", vb, 512), ("z", z, 1024)]
        dense_k, dense_v = ["ka"], ["va"]
        local_k, local_v = ["kb0", "kb1", "kb2", "kb3"], ["vb"]
    else:
        qc, kc, vc, qd, kd, vd, z = 0, 512, 1024, 1536, 2048, 2176, 2304
        fm = []
        for c in range(4):
            fm.append(("qc%d" % c, list(range(qc + c * 128, qc + c * 128 + 128)), None, True))
        for c in range(4):
            fm.append(("kc%d" % c, list(range(kc + c * 128, kc + c * 128 + 128)), None, True))
        for c in range(4):
            cols = list(range(qd + c * 64, qd + c * 64 + 64)) + list(range(qd + (4 + c) * 64, qd + (4 + c) * 64 + 64))
            fm.append(("qd%d" % c, cols, None, True))
        fm.append(("kd", list(range(kd, kd + 128)), None, True))
        tm = [("vc", vc, 512), ("vd", vd, 128), ("z", z, 1024)]
        dense_k, dense_v = ["kc0", "kc1", "kc2", "kc3"], ["vc"]
        local_k, local_v = ["kd"], ["vd"]
    return dict(fm=fm, tm=tm, dense_k=dense_k, dense_v=dense_v, local_k=local_k, local_v=local_v)


def lambda_init(layer):
    return 0.8 - 0.6 * math.exp(-0.3 * layer)


def emit_layer(nc, S, SH, layer, SEQ):
    HALF = SEQ // 2
    EXT = HALF + 2 * HALO
    NU = EXT + HALF + CTX
    NTo = HALF // 128
    NBo = HALF // 512
    U_OWN = HALO
    U_O = EXT
    U_C = EXT + HALF
    NKD = 2 * HALF + CTX
    NKC = NKD // 128
    last = layer == 1
    cfg = layer_cfg(layer)
    fm, tm = cfg["fm"], cfg["tm"]
    NFM = len(fm)
    fmi = {f[0]: i for i, f in enumerate(fm)}
    NCOL = NFM * 128 + sum(t[2] for t in tm)
    tmoff = {}
    o = NFM * 128
    for name, _, n in tm:
        tmoff[name] = o
        o += n
    lam0 = lambda_init(layer)

    LP = "l%d_" % layer
    S.prefix = LP

    def din(name, shape):
        return nc.dram_tensor(LP + name, list(shape), F32, kind="ExternalInput").ap()

    x_u, xc_in, cvec = SH["x_u"], SH["xc"], SH["cvec"]
    ident_d, blk_d, perm_d, ropeC, ropeS = SH["ident"], SH["blk"], SH["perm"], SH["ropeC"], SH["ropeS"]
    x1_loc, xc1_loc, GA = SH["x1_loc"], SH["xc1_loc"], SH["GA"]
    w_mod = din("w_mod", [8, 128, 3072])
    b_mod = din("b_mod", [128, 24])
    bgate = din("bgate", [128, D_MODEL])
    w_in = din("w_in", [8, 128, NCOL])
    w_out = din("w_out", [8, 128, D_MODEL])
    if layer == 0:
        gains = din("gains", [128, 2])
        bias_i = din("bias_i", [128, 8, 5 * 128])
        bias_e = din("bias_e", [4, 8, 128, 7 * 128])
    else:
        dmask = din("dmask", [128, 4, 128])
        lamv = din("lamv", [128, 256])
        subln = din("subln", [128, 128])
        sinks = din("sinks", [128, 8])
        fnorm = din("fnorm", [128, D_MODEL])
    if last:
        out_x = nc.dram_tensor("out_x", [HALF, D_MODEL], F32, kind="ExternalOutput").ap()
    else:
        out_x = x1_loc
        out_c = xc1_loc

    fmS = nc.dram_tensor(LP + "fmS", [NFM, 128, NU], BF16).ap()
    vdims = {"va": (2, 65), "vb": (8, 65), "vc": (4, 129), "vd": (2, 65)}
    vS = {}
    for name, _, n in tm:
        if name != "z":
            h, d = vdims[name]
            vS[name] = nc.dram_tensor(LP + "vS_" + name, [NU, h * d], BF16).ap()
    zS = nc.dram_tensor(LP + "zS", [NU, D_MODEL], BF16).ap()

    with contextlib.ExitStack() as st:
        S.stack = st
        st1 = contextlib.ExitStack()
        fm_reg = Tile("fm_reg")
        v_reg = Tile("v_reg")
        z_reg = Tile("z_reg")

        ident_f = S.sb("ident_f", [128, 128], F32)
        ident = S.sb("ident", [128, 128], BF16)
        blk = S.sb("blk", [128, 128], F32)
        perm = S.sb("perm", [128, 128], F32)
        epst = S.sb("epst", [128, 1], F32)
        zeros = S.sb("zeros", [128, 128], F32)
        junk = S.sb("junk", [128, D_MODEL], F32)
        ss_r = Rot([S.sb("ss%d" % i, [128, 2], F32) for i in range(2)])
        t1_r = Rot([S.sb("t1%d" % i, [128, 512], F32) for i in range(2)])
        gate_bc = S.sb("gate_bc", [128, 2, D_MODEL], F32)
        modt = S.sb("modt", [128, 24, 2], F32)
        bmodt = S.sb("bmodt", [128, 24], F32)
        cv = S.sb("cv", [128, 8, 2], F32)
        pTb_t = SH["pTb_t"]
        pTb = pTb_t.t
        banks = SH["banks"]
        sel_t = S.sb("sel_t", [128, 4], F32)
        S.dma(LQ, sel_t[:], SH["sel"], writes=[sel_t])
        xt2 = S.sb("xt2", [128, D_MODEL], F32)
        G_reg = SH["G_reg"]
        NTo_ = HALF // 128

        RCt = min(512, HALF) // 128

        def grow(r, t):
            return ((t // RCt) * 2 * RCt + r * RCt + (t % RCt)) * 128

        def load_x_tile(xt, utile):
            e = utile
            if layer == 0:
                if e >= (EXT + HALF) // 128:
                    c = e - (EXT + HALF) // 128
                    S.dma(LQ, xt[:], xc_in[c * 128:(c + 1) * 128, :], writes=[xt])
                else:
                    S.dma(LQ, xt[:], x_u[e * 128:(e + 1) * 128, :], writes=[xt])
                return
            if e >= (EXT + HALF) // 128:
                c = e - (EXT + HALF) // 128
                S.dma(LQ, xt[:], xc1_loc[c * 128:(c + 1) * 128, :], writes=[xt])
            elif e >= EXT // 128:
                o = e - EXT // 128
                S.dma(LQ, xt[:], GA[grow(0, o):grow(0, o) + 128, :], reads=[G_reg], writes=[xt])
                S.dma(LQ, xt2[:], GA[grow(1, o):grow(1, o) + 128, :], reads=[G_reg], writes=[xt2])
                S.op("act", lambda en: en.activation(out=xt[:], in_=xt[:], func=AF.Copy, scale=sel_t[:, 2:3]), reads=[xt, sel_t], writes=[xt])
                S.op("dve", lambda en: en.scalar_tensor_tensor(out=xt[:], in0=xt2[:], scalar=sel_t[:, 3:4], in1=xt[:], op0=ALU.mult, op1=ALU.add),
                     reads=[xt2, sel_t, xt], writes=[xt])
            elif 4 <= e < 4 + NTo_:
                t = e - 4
                S.dma(LQ, xt[:], x1_loc[t * 128:(t + 1) * 128, :], writes=[xt])
            elif e < 4:
                gt = grow(0, NTo_ - 4 + e)
                S.dma(LQ, xt[:], GA[gt:gt + 128, :], reads=[G_reg], writes=[xt])
                S.op("act", lambda en: en.activation(out=xt[:], in_=xt[:], func=AF.Copy, scale=sel_t[:, 0:1]), reads=[xt, sel_t], writes=[xt])
            else:
                gt = grow(1, e - 4 - NTo_)
                S.dma(LQ, xt[:], GA[gt:gt + 128, :], reads=[G_reg], writes=[xt])
                S.op("act", lambda en: en.activation(out=xt[:], in_=xt[:], func=AF.Copy, scale=sel_t[:, 1:2]), reads=[xt, sel_t], writes=[xt])

        S.dma(LQ, ident_f[:], ident_d, writes=[ident_f])
        S.dma(LQ, blk[:], blk_d, writes=[blk])
        S.dma(LQ, perm[:], perm_d, writes=[perm])
        S.dma(LQ, cv[:], cvec, writes=[cv])
        S.dma(LQ, bmodt[:], b_mod, writes=[bmodt])
        S.dma(LQ, gate_bc[:, 0, :], bgate, writes=[gate_bc])
        S.op("dve", lambda e: e.tensor_copy(out=ident[:], in_=ident_f[:]), reads=[ident_f], writes=[ident])
        S.op("pool", lambda e: e.memset(epst[:], EPS), writes=[epst])
        S.op("pool", lambda e: e.memset(zeros[:], 0.0), writes=[zeros])
        S.op("dve", lambda e: e.tensor_copy(out=gate_bc[:, 1, :], in_=gate_bc[:, 0, :]), reads=[gate_bc], writes=[gate_bc])
        S.stack = st
        if layer == 0:
            gn = S.sb("gn", [128, 2], F32)
            bi_t = S.sb("bi_t", [128, 8, 640], F32)
            S.dma(LQ, gn[:], gains, writes=[gn])
            S.dma(LQ, bi_t[:], bias_i, writes=[bi_t])
        else:
            dm_t = S.sb("dm_t", [128, 4, 128], F32)
            lam_t = S.sb("lam_t", [128, 256], F32)
            sub_t = S.sb("sub_t", [128, 128], F32)
            snk_t = S.sb("snk_t", [128, 8], F32)
            fn_t = S.sb("fn_t", [128, D_MODEL], F32)
            S.dma(LQ, dm_t[:], dmask, writes=[dm_t])
            S.dma(LQ, lam_t[:], lamv, writes=[lam_t])
            S.dma(LQ, sub_t[:], subln, writes=[sub_t])
            S.dma(LQ, snk_t[:], sinks, writes=[snk_t])
            S.dma(LQ, fn_t[:], fnorm, writes=[fn_t])
            lam_s = S.sb("lam_s", [128, 4], F32)
            lprod = S.sb("lprod", [128, 128], F32)
            esnk = S.sb("esnk", [128, 8], F32)
        S.stack = st1
        win = S.sb("win", [128, 8, NCOL], BF16)
        sel2 = S.sb("sel2", [2, 2, 128], F32)
        S.dma(LQ, sel2[:], SH["sel2"], writes=[sel2])
        stage = Rot([S.sb("stage%d" % i, [128, 1024], F32) for i in range(4)])

        S.op("act", lambda e: e.activation(out=cv[:], in_=cv[:], func=AF.Silu), reads=[cv], writes=[cv])
        rowb = [banks[1], banks[2], banks[3], banks[4], banks[5], banks[6]]
        for k in range(8):
            for pi in range(3):
                stg = stage.next()
                S.dma("act" if (k * 3 + pi) % 2 else LQ, stg[:], w_mod[k, :, pi * 1024:(pi + 1) * 1024], writes=[stg])
                for n in range(2):
                    rb = rowb[pi * 2 + n]
                    S.op("pe", lambda e: e.matmul(rb[0:2, 0:512], lhsT=cv[:, k, :], rhs=stg[:, n * 512:(n + 1) * 512],
                                                  start=(k == 0), stop=(k == 7)), reads=[stg, cv], writes=[rb])
        modrow = S.sb("modrow", [2, 3072], F32)
        for i6 in range(6):
            S.op("act" if i6 % 2 else "dve",
                 (lambda e: e.activation(out=modrow[0:2, i6 * 512:(i6 + 1) * 512], in_=rowb[i6][0:2, 0:512], func=AF.Copy)) if i6 % 2 else
                 (lambda e: e.tensor_copy(out=modrow[0:2, i6 * 512:(i6 + 1) * 512], in_=rowb[i6][0:2, 0:512])),
                 reads=[rowb[i6]], writes=[modrow])
        pmod = banks[7]
        for j in range(16):
            S.op("pe", lambda e: e.transpose(out=pmod[:, j * 2:j * 2 + 2], in_=modrow[0:2, j * 128:(j + 1) * 128], identity=ident_f[0:2, 0:2]),
                 reads=[modrow, ident_f], writes=[pmod])
        pg = [banks[1], banks[2], banks[3], banks[4]]
        for w in range(2):
            for n in range(2):
                S.op("pe", lambda e: e.matmul(pg[w * 2 + n][:], lhsT=sel2[0:2, w, :], rhs=modrow[0:2, 2048 + n * 512:2048 + (n + 1) * 512],
                                              start=True, stop=True), reads=[sel2, modrow], writes=[pg[w * 2 + n]])
        for w in range(2):
            S.op("dve", lambda e: e.tensor_tensor(out=modt[:, 0:16, w], in0=pmod[:, 0:32].rearrange("p (j w) -> p j w", w=2)[:, :, w],
                                                  in1=bmodt[:, 0:16], op=ALU.add), reads=[pmod, bmodt], writes=[modt])
            for n in range(2):
                S.op("dve", lambda e: e.tensor_tensor(out=gate_bc[:, w, n * 512:(n + 1) * 512], in0=pg[w * 2 + n][:],
                                                      in1=gate_bc[:, w, n * 512:(n + 1) * 512], op=ALU.add),
                     reads=[pg[w * 2 + n], gate_bc], writes=[gate_bc])
        S.op("dve", lambda e: e.tensor_scalar_add(out=modt[:, 8:16, :], in0=modt[:, 8:16, :], scalar1=1.0),
             reads=[modt], writes=[modt])

        cnt = 0
        for k in range(8):
            for c0 in range(0, NCOL, 1024):
                cn = min(1024, NCOL - c0)
                stg = stage.next()
                S.dma("act" if cnt % 2 else LQ, stg[:, 0:cn], w_in[k, :, c0:c0 + cn], writes=[stg])
                S.op("dve", lambda e: e.tensor_copy(out=win[:, k, c0:c0 + cn], in_=stg[:, 0:cn]), reads=[stg], writes=[win])
                cnt += 1

        if layer == 1:
            S.op("dve", lambda e: e.tensor_tensor(out=lprod[:, 0:64], in0=lam_t[:, 0:64], in1=lam_t[:, 64:128], op=ALU.mult), reads=[lam_t], writes=[lprod])
            S.op("dve", lambda e: e.tensor_tensor(out=lprod[:, 64:128], in0=lam_t[:, 128:192], in1=lam_t[:, 192:256], op=ALU.mult), reads=[lam_t, lprod], writes=[lprod])
            S.op("dve", lambda e: e.reduce_sum(out=lam_s[:, 0:2], in_=lprod[:].rearrange("p (a d) -> p a d", a=2), axis=mybir.AxisListType.X), reads=[lprod], writes=[lam_s])
            S.op("act", lambda e: e.activation(out=lam_s[:, 0:2], in_=lam_s[:, 0:2], func=AF.Exp), reads=[lam_s], writes=[lam_s])
            S.op("dve", lambda e: e.tensor_tensor(out=lam_s[:, 2:3], in0=lam_s[:, 0:1], in1=lam_s[:, 1:2], op=ALU.subtract), reads=[lam_s], writes=[lam_s])
            S.op("dve", lambda e: e.tensor_scalar(out=lam_s[:, 3:4], in0=lam_s[:, 2:3], scalar1=lam0, scalar2=-1.0, op0=ALU.add, op1=ALU.mult), reads=[lam_s], writes=[lam_s])
            S.op("act", lambda e: e.activation(out=esnk[:], in_=snk_t[:], func=AF.Exp), reads=[snk_t], writes=[esnk])

        xt_r = Rot([S.sb("xt%d" % i, [128, D_MODEL], F32) for i in range(2)])
        xn_r = Rot([S.sb("xn%d" % i, [128, D_MODEL], BF16) for i in range(2)])
        hT_r = Rot([S.sb("hT%d" % i, [128, 8, 512], BF16) for i in range(2)])
        tabC_r = Rot([S.sb("tabC%d" % i, [128, 512], F32) for i in range(1)])
        tabS_r = Rot([S.sb("tabS%d" % i, [128, 512], F32) for i in range(1)])
        sq_r = Rot([S.sb("sq%d" % i, [128, 512], F32) for i in range(2)])
        rs_r = Rot([S.sb("rs%d" % i, [128, 512], F32) for i in range(2)])
        qn_r = Rot([S.sb("qn%d" % i, [128, 512], F32) for i in range(3)])
        t2_r = Rot([S.sb("t2%d" % i, [128, 512], F32) for i in range(2)])
        fo_r = Rot([S.sb("fo%d" % i, [128, 512], BF16) for i in range(4)])
        vst = {}
        for name in vS:
            h, d = vdims[name]
            vst[name] = Rot([S.sb("vst_%s%d" % (name, i), [128, h, d], BF16) for i in range(2)])
            for t in vst[name].tiles:
                S.op("pool", lambda e: e.memset(t[:], 1.0), writes=[t])
        zst_r = Rot([S.sb("zst%d" % i, [128, D_MODEL], BF16) for i in range(2)])
        bA = Rot([banks[1], banks[2], banks[3]])
        bB = Rot([banks[4], banks[5]])

        def p1_prep(bd):
            u0, ntiles, w = bd["u0"], bd["ntiles"], bd["w"]
            hT = hT_r.next()
            bd["hT"] = hT
            for ti in range(ntiles):
                xt = xt_r.next()
                ss = ss_r.next()
                xn = xn_r.next()
                load_x_tile(xt, u0 // 128 + ti)
                S.op("act", lambda e: e.activation(out=junk[:], in_=xt[:], func=AF.Square, accum_out=ss[:, 0:1]), reads=[xt], writes=[junk, ss])
                S.op("act", lambda e: e.activation(out=ss[:, 1:2], in_=ss[:, 0:1], func=AF.Ln, scale=1.0 / D_MODEL, bias=epst[:]), reads=[ss, epst], writes=[ss])
                S.op("act", lambda e: e.activation(out=ss[:, 1:2], in_=ss[:, 1:2], func=AF.Exp, scale=-0.5), reads=[ss], writes=[ss])
                S.op("act", lambda e: e.activation(out=xn[:], in_=xt[:], func=AF.Copy, scale=ss[:, 1:2]), reads=[xt, ss], writes=[xn])
                for k in range(8):
                    S.op("pe", lambda e: e.transpose(out=pTb[:, k, :], in_=xn[:, k * 128:(k + 1) * 128], identity=ident[:]),
                         reads=[xn, ident], writes=[pTb_t])
                for k in range(8):
                    S.op("dve", lambda e: e.tensor_scalar(out=hT[:, k, ti * 128:(ti + 1) * 128], in0=pTb[:, k, :],
                                                          scalar1=modt[:, 8 + k, w:w + 1], scalar2=modt[:, k, w:w + 1],
                                                          op0=ALU.mult, op1=ALU.add), reads=[pTb_t, modt], writes=[hT])
                yield

        def p1_mm(bd):
            u0, ntiles, fm_list, tm_list, rope_col0, hT = bd["u0"], bd["ntiles"], bd["fm_list"], bd["tm_list"], bd["rope_col0"], bd["hT"]
            ntok = ntiles * 128
            rope_needed = any(fm[i][3] for i in fm_list) and rope_col0 is not None
            if rope_needed:
                tC = tabC_r.next()
                tS = tabS_r.next()
                S.dma(LQ, tC[:, 0:ntok], ropeC[:, rope_col0:rope_col0 + ntok], writes=[tC])
                S.dma(LQ, tS[:, 0:ntok], ropeS[:, rope_col0:rope_col0 + ntok], writes=[tS])
            chs = [dict(i=i) for i in fm_list]

            def st_main(ch):
                i = ch["i"]
                pa = bA.next()
                for k in range(8):
                    S.op("pe", lambda e: e.matmul(pa[:, 0:ntok], lhsT=win[:, k, i * 128:(i + 1) * 128], rhs=hT[:, k, 0:ntok],
                                                  start=(k == 0), stop=(k == 7)), reads=[win, hT], writes=[pa])
                ch["pa"] = pa

            def st_norm(ch):
                i = ch["i"]
                name, _, nkind, roped = fm[i]
                pa = ch["pa"]
                fo = fo_r.next()
                ch["fo"] = fo
                do_rope = roped and rope_col0 is not None
                ch["do_rope"] = do_rope
                if nkind is None and not do_rope:
                    S.op("act", lambda e: e.activation(out=fo[:, 0:ntok], in_=pa[:, 0:ntok], func=AF.Copy), reads=[pa], writes=[fo])
                    return
                qn = qn_r.next()
                ch["qn"] = qn
                if nkind is not None:
                    sq = sq_r.next()
                    rs = rs_r.next()
                    pb = bB.next()
                    gcol = 0 if nkind == "q" else 1
                    S.op("act", lambda e: e.activation(out=sq[:, 0:ntok], in_=pa[:, 0:ntok], func=AF.Square), reads=[pa], writes=[sq])
                    S.op("pe", lambda e: e.matmul(pb[:, 0:ntok], lhsT=blk[:], rhs=sq[:, 0:ntok], start=True, stop=True), reads=[blk, sq], writes=[pb])
                    S.op("act", lambda e: e.activation(out=rs[:, 0:ntok], in_=pb[:, 0:ntok], func=AF.Ln, bias=epst[:]), reads=[pb, epst], writes=[rs])
                    S.op("act", lambda e: e.activation(out=rs[:, 0:ntok], in_=rs[:, 0:ntok], func=AF.Exp, scale=-0.5), reads=[rs], writes=[rs])
                    dst = qn if do_rope else fo
                    S.op("dve", lambda e: e.scalar_tensor_tensor(out=dst[:, 0:ntok], in0=pa[:, 0:ntok], scalar=gn[:, gcol:gcol + 1],
                                                                 in1=rs[:, 0:ntok], op0=ALU.mult, op1=ALU.mult),
                         reads=[pa, gn, rs], writes=[dst])
                else:
                    S.op("act", lambda e: e.activation(out=qn[:, 0:ntok], in_=pa[:, 0:ntok], func=AF.Copy), reads=[pa], writes=[qn])

            def st_rope(ch):
                i = ch["i"]
                fo = ch["fo"]
                if ch["do_rope"]:
                    qn = ch["qn"]
                    pb2 = bB.next()
                    t1 = t1_r.next()
                    t2 = t2_r.next()
                    S.op("pe", lambda e: e.matmul(pb2[:, 0:ntok], lhsT=perm[:], rhs=qn[:, 0:ntok], start=True, stop=True), reads=[perm, qn], writes=[pb2])
                    S.op("dve", lambda e: e.tensor_tensor(out=t1[:, 0:ntok], in0=qn[:, 0:ntok], in1=tC[:, 0:ntok], op=ALU.mult), reads=[qn, tC], writes=[t1])
                    S.op("dve", lambda e: e.tensor_tensor(out=t2[:, 0:ntok], in0=pb2[:, 0:ntok], in1=tS[:, 0:ntok], op=ALU.mult), reads=[pb2, tS], writes=[t2])
                    S.op("dve", lambda e: e.tensor_tensor(out=fo[:, 0:ntok], in0=t1[:, 0:ntok], in1=t2[:, 0:ntok], op=ALU.add), reads=[t1, t2], writes=[fo])
                S.dma(SQ, fmS[i, :, u0:u0 + ntok], fo[:, 0:ntok], reads=[fo], writes=[fm_reg], sem_tile=fo)

            nchs = len(chs)
            for step in range(nchs + 2):
                if step < nchs:
                    st_main(chs[step])
                if 0 <= step - 1 < nchs:
                    st_norm(chs[step - 1])
                if 0 <= step - 2 < nchs:
                    st_rope(chs[step - 2])
                yield
            for name in tm_list:
                col0 = tmoff[name]
                ncols = dict((t[0], t[2]) for t in tm)[name]
                for ti in range(ntiles):
                    for n0 in range(0, ncols, 512):
                        nn = min(512, ncols - n0)
                        pa = bA.next()
                        for k in range(8):
                            S.op("pe", lambda e: e.matmul(pa[:, 0:nn], lhsT=hT[:, k, ti * 128:(ti + 1) * 128],
                                                          rhs=win[:, k, col0 + n0:col0 + n0 + nn], start=(k == 0), stop=(k == 7)),
                                 reads=[win, hT], writes=[pa])
                        if name == "z":
                            if n0 == 0:
                                zst = zst_r.next()
                            S.op("act", lambda e: e.activation(out=zst[:, n0:n0 + nn], in_=pa[:, 0:nn], func=AF.Silu), reads=[pa], writes=[zst])
                            if n0 + nn == ncols:
                                S.dma(SQ, zS[u0 + ti * 128:u0 + (ti + 1) * 128, :], zst[:], reads=[zst], writes=[z_reg], sem_tile=zst)
                        else:
                            h, d = vdims[name]
                            dv = d - 1
                            vt = vst[name].next()
                            S.op("dve", lambda e: e.tensor_copy(out=vt[:, :, 0:dv], in_=pa[:, 0:nn].rearrange("p (h d) -> p h d", d=dv)),
                                 reads=[pa], writes=[vt])
                            S.dma(SQ, vS[name][u0 + ti * 128:u0 + (ti + 1) * 128, :], vt[:].rearrange("p h d -> p (h d)"),
                                  reads=[vt], writes=[v_reg], sem_tile=vt)
                        yield


        all_fm = list(range(NFM))
        all_tm = [t[0] for t in tm]
        lk = [fmi[n] for n in cfg["local_k"]]
        dk = [fmi[n] for n in cfg["dense_k"]]
        blocks = [dict(u0=U_C, ntiles=2, w=1, fm_list=all_fm, tm_list=all_tm, rope_col0=None)]
        eblocks = list(range(EXT // 512))
        eblocks = [b for b in eblocks if HALO <= b * 512 < HALO + HALF] + [b for b in eblocks if not (HALO <= b * 512 < HALO + HALF)]
        nh = 2 if layer == 0 else 1
        for b in eblocks:
            u0 = b * 512
            own = HALO <= u0 < HALO + HALF
            if own:
                blocks.append(dict(u0=u0, ntiles=4, w=0, fm_list=all_fm, tm_list=all_tm, rope_col0=u0))
            elif u0 < HALO:
                uh = HALO - nh * 128
                blocks.append(dict(u0=uh, ntiles=nh, w=0, fm_list=lk, tm_list=cfg["local_v"], rope_col0=uh))
            else:
                uh = HALO + HALF
                blocks.append(dict(u0=uh, ntiles=nh, w=0, fm_list=lk, tm_list=cfg["local_v"], rope_col0=uh))
        for b in range(HALF // 512):
            u0 = U_O + b * 512
            blocks.append(dict(u0=u0, ntiles=4, w=0, fm_list=dk, tm_list=cfg["dense_v"], rope_col0=u0))
        for _ in p1_prep(blocks[0]):
            pass
        for bi, bd in enumerate(blocks):
            gm = p1_mm(bd)
            gp = p1_prep(blocks[bi + 1]) if bi + 1 < len(blocks) else None
            nsteps = len(bd["fm_list"]) + 2 + sum(((dict((t[0], t[2]) for t in tm)[nm] + 511) // 512) * bd["ntiles"] for nm in bd["tm_list"])
            ntl = blocks[bi + 1]["ntiles"] if gp is not None else 0
            every = max(1, nsteps // (ntl + 1)) if ntl else 0
            k = 0
            for _ in gm:
                k += 1
                if gp is not None and every and k % every == 0:
                    next(gp, None)
            if gp is not None:
                for _ in gp:
                    pass

        S.barrier()
        st1.close()
        st2 = contextlib.ExitStack()
        S.stack = st2
        wout = S.sb("wout", [128, 8, D_MODEL], BF16)
        stage2 = Rot([S.sb("stage2_%d" % i, [128, D_MODEL], F32) for i in range(2)])
        for k in range(8):
            stg = stage2.next()
            S.dma("act" if k % 2 else LQ, stg[:], w_out[k], writes=[stg])
            S.op("dve", lambda e: e.tensor_copy(out=wout[:, k, :], in_=stg[:]), reads=[stg], writes=[wout])
        bS = Rot([banks[1], banks[2], banks[3]])
        bACC = Rot([banks[5], banks[6], banks[7]])
        nbuf_d = 1 if layer == 0 else 2
        KT_r = Rot([S.sb("KT%d" % i, [128, NKD], BF16) for i in range(nbuf_d)])
        VDW = 130 if layer == 0 else 129
        VD_r = Rot([S.sb("VD%d" % i, [128, NKC, VDW], BF16) for i in range(nbuf_d)])
        q_r = Rot([S.sb("qblk%d" % i, [128, 512], BF16) for i in range(6)])
        NLK = 9
        kl_r = Rot([S.sb("kl%d" % i, [128, NLK * 128], BF16) for i in range(5)])
        VLW = 130
        vl_r = Rot([S.sb("vl%d" % i, [128, NLK, VLW], BF16) for i in range(5)])
        y_t = [S.sb("y%d" % i, [128, D_MODEL], F32) for i in range(4)]
        rec_r = Rot([S.sb("rec%d" % i, [128, 8], F32) for i in range(4)])
        zl_r = Rot([S.sb("zl%d" % i, [128, D_MODEL], BF16) for i in range(1)])
        yb_r = Rot([S.sb("yb%d" % i, [128, D_MODEL], BF16) for i in range(1)])
        yT_r = Rot([S.sb("yT%d" % i, [128, 8, 128], BF16) for i in range(1)])
        xo_r = Rot([S.sb("xo%d" % i, [128, D_MODEL], F32) for i in range(1)])
        res_r = Rot([S.sb("res%d" % i, [128, D_MODEL], F32) for i in range(2)])
        be_r = Rot([S.sb("be%d" % i, [128, 7 * 128], F32) for i in range(3)]) if layer == 0 else None
        dcache = {}

        def load_dense(kname, vname, vc0, vw):
            key = (kname, vname, vc0)
            if dcache.get("key") == key:
                return dcache["KT"], dcache["VD"]
            KT = KT_r.next()
            VD = VD_r.next()
            i = fmi[kname]
            S.dma(LQ, KT[:, 0:HALF], fmS[i, :, U_OWN:U_OWN + HALF], reads=[fm_reg], writes=[KT])
            S.dma(LQ, KT[:, HALF:2 * HALF], fmS[i, :, U_O:U_O + HALF], reads=[fm_reg], writes=[KT])
            S.dma(LQ, KT[:, 2 * HALF:NKD], fmS[i, :, U_C:U_C + CTX], reads=[fm_reg], writes=[KT])
            for (c0, u0, n) in ((0, U_OWN, HALF), (HALF // 128, U_O, HALF), (2 * HALF // 128, U_C, CTX)):
                S.dma(LQ, VD[:, c0:c0 + n // 128, 0:vw], vS[vname][u0:u0 + n, vc0:vc0 + vw].rearrange("(k p) c -> p k c", p=128),
                      reads=[v_reg], writes=[VD])
            dcache.update(key=key, KT=KT, VD=VD)
            return KT, VD

        def attend(qt, pbase, NQ, kchunks, acc_list, vwidth, bias_fn=None):
            nq = NQ // 128
            n = len(kchunks)
            pend = []
            first = {}

            def issue_s(j):
                kt, kc0, vt, vap = kchunks[j]
                ps = bS.next()
                S.op("pe", lambda e: e.matmul(ps[:, 0:NQ], lhsT=kt[pbase:pbase + 64, kc0:kc0 + 128], rhs=qt[pbase:pbase + 64, 0:NQ],
                                              start=True, stop=True), reads=[kt, qt], writes=[ps])
                pT = pT_r.next()
                b = bias_fn(j) if bias_fn is not None else None
                if b is not None:
                    btile, bap = b
                    sb = sb_r.next()
                    S.op("dve", lambda e: e.scalar_tensor_tensor(out=sb[:, 0:NQ], in0=ps[:, 0:NQ], scalar=SCALE, in1=bap,
                                                                 op0=ALU.mult, op1=ALU.add), reads=[ps, btile], writes=[sb])
                    S.op("act", lambda e: e.activation(out=pT[:, 0:NQ], in_=sb[:, 0:NQ], func=AF.Exp), reads=[sb], writes=[pT])
                else:
                    S.op("act", lambda e: e.activation(out=pT[:, 0:NQ], in_=ps[:, 0:NQ], func=AF.Exp, scale=SCALE), reads=[ps], writes=[pT])
                return pT

            def issue_pv(j, pT):
                kt, kc0, vt, vap = kchunks[j]
                for s in range(nq):
                    acc, c0 = acc_list[s]
                    fst = first.get(id(acc), True)
                    first[id(acc)] = False
                    S.op("pe", lambda e: e.matmul(acc[:, c0:c0 + vwidth], lhsT=pT[:, s * 128:(s + 1) * 128], rhs=vap,
                                                  start=(j == 0 and fst), stop=(j == n - 1), skip_group_check=True),
                         reads=[pT, vt], writes=[acc])

            prev = None
            for j in range(n):
                pT = issue_s(j)
                if prev is not None:
                    issue_pv(prev[0], prev[1])
                prev = (j, pT)
            issue_pv(prev[0], prev[1])

        onesel_f = S.sb("onesel_f", [128, 2, 2], F32)
        S.op("pool", lambda e: e.memset(onesel_f[:], 0.0), writes=[onesel_f])
        S.op("pool", lambda e: e.memset(onesel_f[:, 0, 0:1], 1.0), writes=[onesel_f])
        S.op("pool", lambda e: e.memset(onesel_f[:, 1, 1:2], 1.0), writes=[onesel_f])
        den_acc = S.sb("den_acc", [128, 1024], F32) if layer == 1 else None
        onesel_b = S.sb("onesel_b", [128, 2, 2], BF16)
        S.op("dve", lambda e: e.tensor_copy(out=onesel_b[:], in_=onesel_f[:]), reads=[onesel_f], writes=[onesel_b])
        if layer == 1:
            dmb = S.sb("dmb", [128, 3, 384], F32)
            S.op("pool", lambda e: e.memset(dmb[:], 0.0), writes=[dmb])
            for var, (pi, ni) in enumerate(((2, 3), (0, 3), (2, 1))):
                S.op("dve", lambda e: e.tensor_copy(out=dmb[:, var, 0:128], in_=dm_t[:, pi, :]), reads=[dm_t], writes=[dmb])
                S.op("dve", lambda e: e.tensor_copy(out=dmb[:, var, 256:384], in_=dm_t[:, ni, :]), reads=[dm_t], writes=[dmb])
        rec16_r = Rot([S.sb("rec16_%d" % i, [128, 16], F32) for i in range(2)])
        Tq = S.sb("Tq", [128, 4, 256], F32) if layer == 1 else None
        pT2_r = Rot([S.sb("pT2_%d" % i, [128, 1024], BF16) for i in range(3)])
        oT_r = Rot([S.sb("oT%d" % i, [128, 512], F32) for i in range(2)])
        pairs = SH["pairs"]

        def dense_pair(qt, KT, VD, vap_fn, vw, accs, den=None):
            n = NKC

            def issue_s(j):
                pt, ta, tb = pairs[j % 2]
                for m, tt in ((0, ta), (1, tb)):
                    S.op("pe", lambda e: e.matmul(tt[:, 0:512], lhsT=KT[64 * m:64 * m + 64, j * 128:(j + 1) * 128],
                                                  rhs=qt[64 * m:64 * m + 64, 0:512], start=True, stop=True), reads=[KT, qt], writes=[tt])
                pT = pT2_r.next()
                S.op("act", lambda e: e.activation(out=pT[:], in_=pt[:, :], func=AF.Exp, scale=SCALE), reads=[ta, tb], writes=[pT])
                return pT

            def issue_pv(j, pT):
                for m in range(2):
                    S.op("pe", lambda e: e.matmul(accs[m][0:vw, 0:512], lhsT=vap_fn(j, m), rhs=pT[:, m * 512:(m + 1) * 512],
                                                  start=(j == 0), stop=(j == n - 1)), reads=[pT, VD], writes=[accs[m]])
                if den is not None:
                    if j == 0:
                        S.op("dve", lambda e: e.tensor_copy(out=den_acc[:, 0:512], in_=pT[:, 0:512]), reads=[pT], writes=[den_acc])
                    else:
                        S.op("dve", lambda e: e.tensor_tensor(out=den_acc[:, 0:512], in0=den_acc[:, 0:512], in1=pT[:, 0:512], op=ALU.add),
                             reads=[pT, den_acc], writes=[den_acc])
                    S.op("pe", lambda e: e.matmul(den[0:2, 0:512], lhsT=onesel_b[:, 1, :], rhs=pT[:, 512:1024],
                                                  start=(j == 0), stop=False, skip_group_check=True), reads=[pT, onesel_b], writes=[den])
                    if j == n - 1:
                        S.op("pe", lambda e: e.matmul(den[0:2, 0:512], lhsT=onesel_f[:, 0, :], rhs=den_acc[:, 0:512],
                                                      start=False, stop=True, skip_group_check=True), reads=[den_acc, onesel_f], writes=[den])

            pend = []
            for j in range(n):
                pend.append((j, issue_s(j)))
                if len(pend) > 2:
                    issue_pv(*pend.pop(0))
            while pend:
                issue_pv(*pend.pop(0))

        def untranspose(acc, rows, fin, width):
            oT = oT_r.next()
            S.op("act", lambda e: e.activation(out=oT[0:rows, :], in_=acc[0:rows, 0:512], func=AF.Copy), reads=[acc], writes=[oT])
            for sidx in range(4):
                S.op("pe", lambda e: e.transpose(out=fin[:, sidx * width:sidx * width + rows], in_=oT[0:rows, sidx * 128:(sidx + 1) * 128],
                                                 identity=ident_f[0:rows, 0:rows]), reads=[oT, ident_f], writes=[fin])

        wide_i = [0]

        def wide_a(job):
            qt, pbase, kl, nch, nb, bias = job["qt"], job["pbase"], job["kl"], job["nch"], job["nb"], job["bias"]
            pt, ta, tb = pairs[wide_i[0] % 2]
            wide_i[0] += 1
            for j in range(nch):
                tt = ta if j < 4 else tb
                S.op("pe", lambda e: e.matmul(pt[:, j * 128:(j + 1) * 128], lhsT=kl[pbase:pbase + 64, j * 128:(j + 1) * 128],
                                              rhs=qt[pbase:pbase + 64, 0:128], start=True, stop=True), reads=[kl, qt], writes=[tt])
            pT = pT2_r.next()
            used = [ta] + ([tb] if nch > 4 else [])
            if bias is not None:
                btile, bap = bias
                sbw = stage2.next()
                S.op("dve", lambda e: e.scalar_tensor_tensor(out=sbw[:, 0:nb * 128], in0=pt[:, 0:nb * 128], scalar=SCALE, in1=bap,
                                                             op0=ALU.mult, op1=ALU.add), reads=used + [btile], writes=[sbw])
                S.op("act", lambda e: e.activation(out=pT[:, 0:nb * 128], in_=sbw[:, 0:nb * 128], func=AF.Exp), reads=[sbw], writes=[pT])
                if nch > nb:
                    S.op("act", lambda e: e.activation(out=pT[:, nb * 128:nch * 128], in_=pt[:, nb * 128:nch * 128], func=AF.Exp, scale=SCALE),
                         reads=used, writes=[pT])
            else:
                S.op("act", lambda e: e.activation(out=pT[:, 0:nch * 128], in_=pt[:, 0:nch * 128], func=AF.Exp, scale=SCALE), reads=used, writes=[pT])
            job["pT"] = pT

        def wide_b(job):
            pT, vl, vc0, nch = job["pT"], job["vl"], job["vc0"], job["nch"]
            acc = bACC.next()
            for j in range(nch):
                S.op("pe", lambda e: e.matmul(acc[:, 0:65], lhsT=pT[:, j * 128:(j + 1) * 128], rhs=vl[:, j, vc0:vc0 + 65],
                                              start=(j == 0), stop=(j == nch - 1)), reads=[pT, vl], writes=[acc])
            finish_head([(acc, 0)], 65, [job["yt"]], job["ycol"], extra_den=job.get("extra_den"))

        def run_wide(jobs):
            pend = []
            for job in jobs:
                if "pre" in job:
                    job["pre"]()
                wide_a(job)
                pend.append(job)
                if len(pend) > 2:
                    wide_b(pend.pop(0))
            while pend:
                wide_b(pend.pop(0))

        def finish_head(acc_list, vwidth, y_tiles, ycol, extra_den=None, scale_ap=None):
            dv = vwidth - 1
            for s, (acc, c0) in enumerate(acc_list):
                rec = rec_r.next()
                if extra_den is not None:
                    S.op("dve", lambda e: e.tensor_tensor(out=rec[:, 0:1], in0=acc[:, c0 + dv:c0 + dv + 1], in1=extra_den, op=ALU.add),
                         reads=[acc, esnk], writes=[rec])
                    S.op("dve", lambda e: e.reciprocal(out=rec[:, 1:2], in_=rec[:, 0:1]), reads=[rec], writes=[rec])
                else:
                    S.op("dve", lambda e: e.reciprocal(out=rec[:, 1:2], in_=acc[:, c0 + dv:c0 + dv + 1]), reads=[acc], writes=[rec])
                yt = y_tiles[s]
                S.op("act", lambda e: e.activation(out=yt[:, ycol:ycol + dv], in_=acc[:, c0:c0 + dv], func=AF.Copy, scale=rec[:, 1:2]),
                     reads=[acc, rec], writes=[yt])

        def out_tile(yt, u_tok, src, src_row, w, dst, dst_row):
            zl = zl_r.next()
            yb = yb_r.next()
            yT = yT_r.next()
            xo = xo_r.next()
            res = res_r.next()
            S.dma(LQ, zl[:], zS[u_tok:u_tok + 128, :], reads=[z_reg], writes=[zl])
            load_x_tile(xo, u_tok // 128)
            S.op("dve", lambda e: e.tensor_tensor(out=yb[:], in0=yt[:], in1=zl[:], op=ALU.mult), reads=[yt, zl], writes=[yb])
            for k in range(8):
                S.op("pe", lambda e: e.transpose(out=pTb[:, k, :], in_=yb[:, k * 128:(k + 1) * 128], identity=ident[:]),
                     reads=[yb, ident], writes=[pTb_t])
            S.op("act", lambda e: e.activation(out=yT[:].rearrange("p k t -> p (k t)"), in_=pTb[:].rearrange("p k t -> p (k t)"), func=AF.Copy),
                 reads=[pTb_t], writes=[yT])
            for n in range(2):
                po = bACC.next()
                for k in range(8):
                    S.op("pe", lambda e: e.matmul(po[:], lhsT=yT[:, k, :], rhs=wout[:, k, n * 512:(n + 1) * 512], start=(k == 0), stop=(k == 7)),
                         reads=[yT, wout], writes=[po])
                S.op("dve", lambda e: e.tensor_tensor(out=res[:, n * 512:(n + 1) * 512], in0=po[:], in1=gate_bc[:, w, n * 512:(n + 1) * 512], op=ALU.mult),
                     reads=[po, gate_bc], writes=[res])
            S.op("dve", lambda e: e.tensor_tensor(out=res[:], in0=res[:], in1=xo[:], op=ALU.add), reads=[res, xo], writes=[res])
            if last:
                ss = ss_r.next()
                S.op("act", lambda e: e.activation(out=junk[:], in_=res[:], func=AF.Square, accum_out=ss[:, 0:1]), reads=[res], writes=[junk, ss])
                S.op("act", lambda e: e.activation(out=ss[:, 1:2], in_=ss[:, 0:1], func=AF.Ln, scale=1.0 / D_MODEL, bias=epst[:]), reads=[ss, epst], writes=[ss])
                S.op("act", lambda e: e.activation(out=ss[:, 1:2], in_=ss[:, 1:2], func=AF.Exp, scale=-0.5), reads=[ss], writes=[ss])
                S.op("act", lambda e: e.activation(out=xo[:], in_=res[:], func=AF.Copy, scale=ss[:, 1:2]), reads=[res, ss], writes=[xo])
                S.op("dve", lambda e: e.tensor_tensor(out=res[:], in0=xo[:], in1=fn_t[:], op=ALU.mult), reads=[xo, fn_t], writes=[res])
            S.dma(SQ, dst[dst_row:dst_row + 128, :], res[:], reads=[res], sem_tile=res)
            return res

        out_tiles = []

        def load_q(name, u0, n):
            qt = q_r.next()
            S.dma(LQ, qt[:, 0:n], fmS[fmi[name], :, u0:u0 + n], reads=[fm_reg], writes=[qt])
            return qt

        def load_local(knames_idx, vname, vc0, vw, utiles):
            kl = kl_r.next()
            vl = vl_r.next()
            pos = 0
            for (ut0, cnt) in utiles:
                S.dma(LQ, kl[:, pos * 128:(pos + cnt) * 128], fmS[knames_idx, :, ut0 * 128:(ut0 + cnt) * 128], reads=[fm_reg], writes=[kl])
                S.dma(LQ, vl[:, pos:pos + cnt, 0:vw], vS[vname][ut0 * 128:(ut0 + cnt) * 128, vc0:vc0 + vw].rearrange("(k p) c -> p k c", p=128),
                      reads=[v_reg], writes=[vl])
                pos += cnt
            return kl, vl

        UC_T = U_C // 128

        if layer == 0:
            for t in range(2):
                yt = y_t[t]
                u0 = U_C + t * 128
                jobs = []
                kl, vl = load_local(fmi["ka"], "va", 0, 130, [(UC_T, 2)])
                for c in range(4):
                    qt = load_q("qa%d" % c, u0, 128)
                    for s_ in range(2):
                        jobs.append(dict(qt=qt, pbase=64 * s_, kl=kl, vl=vl, vc0=s_ * 65, nch=2, nb=0, bias=None, yt=yt, ycol=(c + 4 * s_) * 64))
                run_wide(jobs)
                for c in range(4):
                    kl, vl = load_local(fmi["kb%d" % c], "vb", c * 130, 130, [(UC_T, 2)])
                    qt = load_q("qb%d" % c, u0, 128)
                    run_wide([dict(qt=qt, pbase=64 * s_, kl=kl, vl=vl, vc0=s_ * 65, nch=2, nb=0, bias=None, yt=yt, ycol=512 + (2 * c + s_) * 64)
                              for s_ in range(2)])
                out_tiles.append(out_tile(yt, u0, xc_in, t * 128, 1, out_c, t * 128))

        for qb in range(NBo):
            u0 = U_OWN + qb * 512
            if layer == 0:
                KT, VD = load_dense("ka", "va", 0, 130)
                for c in range(4):
                    qt = load_q("qa%d" % c, u0, 512)
                    accs = [banks[5], banks[6]]
                    dense_pair(qt, KT, VD, lambda j, m: VD[:, j, m * 65:(m + 1) * 65], 65, accs)
                    for s in range(2):
                        head = c + 4 * s
                        fin = banks[7]
                        untranspose(accs[s], 65, fin, 65)
                        finish_head([(fin, i * 65) for i in range(4)], 65, y_t, head * 64)
            else:
                for h in range(4):
                    KT, VD = load_dense("kc%d" % h, "vc", h * 129, 129)
                    qt = load_q("qc%d" % h, u0, 512)
                    accs = [banks[5], banks[6]]
                    den = banks[7]
                    dense_pair(qt, KT, VD, lambda j, m: VD[:, j, 0:128], 128, accs, den=den)
                    fins = [banks[1], banks[2]]
                    untranspose(accs[0], 128, fins[0], 128)
                    untranspose(accs[1], 128, fins[1], 128)
                    dfin = banks[3]
                    untranspose(den, 2, dfin, 2)
                    o_m = [[(fins[0], i * 128, i * 2 + 0) for i in range(4)], [(fins[1], i * 128, i * 2 + 1) for i in range(4)]]
                    R = rec16_r.next()
                    S.op("dve", lambda e: e.reciprocal(out=R[:, 0:8], in_=dfin[:, 0:8]), reads=[dfin], writes=[R])
                    S.op("dve", lambda e: e.tensor_scalar(out=R[:, 8:12], in0=R[:, 0:8].rearrange("p (s m) -> p s m", m=2)[:, :, 1],
                                                          scalar1=lam_s[:, 3:4], scalar2=None, op0=ALU.mult), reads=[R, lam_s], writes=[R])
                    for s in range(4):
                        a0, c0, d0 = o_m[0][s]
                        a1, c1, d1 = o_m[1][s]
                        S.op("act", lambda e: e.activation(out=Tq[:, s, 0:128], in_=a0[:, c0:c0 + 128], func=AF.Copy, scale=R[:, 2 * s:2 * s + 1]),
                             reads=[a0, R], writes=[Tq])
                        S.op("dve", lambda e: e.scalar_tensor_tensor(out=Tq[:, s, 128:256], in0=a1[:, c1:c1 + 128], scalar=R[:, 8 + s:9 + s], in1=Tq[:, s, 0:128],
                                                                     op0=ALU.mult, op1=ALU.add), reads=[a1, R, Tq], writes=[Tq])
                        S.op("act", lambda e: e.activation(out=Tq[:, s, 0:128], in_=Tq[:, s, 128:256], func=AF.Square, accum_out=R[:, 12 + s:13 + s]),
                             reads=[Tq], writes=[Tq, R])
                    S.op("act", lambda e: e.activation(out=R[:, 12:16], in_=R[:, 12:16], func=AF.Ln, scale=1.0 / 128, bias=epst[:]), reads=[R, epst], writes=[R])
                    S.op("act", lambda e: e.activation(out=R[:, 12:16], in_=R[:, 12:16], func=AF.Exp, scale=-0.5), reads=[R], writes=[R])
                    S.op("dve", lambda e: e.tensor_scalar_mul(out=R[:, 12:16], in0=R[:, 12:16], scalar1=1.0 - lam0), reads=[R], writes=[R])
                    for s in range(4):
                        yt = y_t[s]
                        S.op("dve", lambda e: e.scalar_tensor_tensor(out=yt[:, h * 128:(h + 1) * 128], in0=Tq[:, s, 128:256], scalar=R[:, 12 + s:13 + s], in1=sub_t[:],
                                                                     op0=ALU.mult, op1=ALU.mult), reads=[Tq, R, sub_t], writes=[yt])
            for tl in range(4):
                t = qb * 4 + tl
                ut = U_OWN // 128 + t
                yt = y_t[tl]
                if layer == 0:
                    edge = t < 2 or t >= NTo - 2
                    if edge:
                        et = t if t < 2 else 2 + (t - (NTo - 2))
                        lo, hi = ((-2, 3), (-2, 2), (-2, 2), (-3, 2))[et]
                    else:
                        lo, hi = -2, 2
                    nb = hi - lo + 1
                    runs = [(ut + lo, nb), (UC_T, 2)]
                    jobs = []
                    for c in range(4):
                        kl, vl = load_local(fmi["kb%d" % c], "vb", c * 130, 130, runs)
                        qt = load_q("qb%d" % c, ut * 128, 128)
                        for s_ in range(2):
                            head = 2 * c + s_
                            job = dict(qt=qt, pbase=64 * s_, kl=kl, vl=vl, vc0=s_ * 65, nch=nb + 2, nb=nb,
                                       yt=yt, ycol=512 + head * 64)
                            if edge:
                                def pre(job=job, et=et, head=head, lo=lo, hi=hi):
                                    be = be_r.next()
                                    S.dma(LQ, be[:], bias_e[et, head], writes=[be])
                                    job["bias"] = (be, be[:, (lo + 3) * 128:(hi + 4) * 128])
                                job["pre"] = pre
                            else:
                                job["bias"] = (bi_t, bi_t[:, head, 0:640])
                            jobs.append(job)
                    run_wide(jobs)
                else:
                    kl, vl = load_local(fmi["kd"], "vd", 0, 130, [(ut - 1, 3), (UC_T, 2)])
                    var = 1 if t == 0 else (2 if t == NTo - 1 else 0)
                    jobs = []
                    for c in range(4):
                        qt = load_q("qd%d" % c, ut * 128, 128)
                        for s_ in range(2):
                            head = c + 4 * s_
                            jobs.append(dict(qt=qt, pbase=64 * s_, kl=kl, vl=vl, vc0=s_ * 65, nch=5, nb=3, bias=(dmb, dmb[:, var, :]),
                                             yt=yt, ycol=512 + head * 64, extra_den=esnk[:, head:head + 1]))
                    run_wide(jobs)
                out_tiles.append(out_tile(yt, ut * 128, x_u, ut * 128, 0, out_x, t * 128))
        if last:
            S.finish(out_tiles)
        else:
            S.barrier()
        st2.close()
    if not last:
        S.release_dsems()
        S.new_epoch()


def build_fused(SEQ, B):
    HALF = SEQ // 2
    EXT = HALF + 2 * HALO
    nc = bass.Bass("TRN2", target_bir_lowering=False)

    def din(name, shape):
        return nc.dram_tensor(name, list(shape), F32, kind="ExternalInput").ap()

    SH = dict(x_u=din("x_u", [EXT + HALF, D_MODEL]), xc=din("xc", [CTX, D_MODEL]), cvec=din("cvec", [128, 8, 2]),
              ident=din("ident", [128, 128]), blk=din("blk", [128, 128]), perm=din("perm", [128, 128]),
              ropeC=din("ropeC", [128, EXT + HALF]), ropeS=din("ropeS", [128, EXT + HALF]), sel=din("sel", [128, 4]), sel2=din("sel2", [2, 2, 128]))
    SH["x1_loc"] = nc.dram_tensor("x1_loc", [HALF, D_MODEL], F32).ap()
    SH["xc1_loc"] = nc.dram_tensor("xc1_loc", [CTX, D_MODEL], F32).ap()
    SH["GA"] = nc.dram_tensor("x1_all", [2 * HALF, D_MODEL], F32).ap()
    with contextlib.ExitStack() as st0:
        S = Sched(nc, st0)
        SH["pTb_t"] = S.ps("bankT", [128, 8, 128], BF16)
        pairA = st0.enter_context(nc.psum_tensor("ps_pairA", [128, 1024], F32))
        pairB = st0.enter_context(nc.psum_tensor("ps_pairB", [128, 1024], F32))
        b1 = Tile("bank1", pairA[:, 0:512], excl=True)
        b2 = Tile("bank2", pairA[:, 512:1024], excl=True)
        b3 = Tile("bank3", pairB[:, 0:512], excl=True)
        b4 = Tile("bank4", pairB[:, 512:1024], excl=True)
        SH["pairs"] = [(pairA, b1, b2), (pairB, b3, b4)]
        SH["banks"] = [SH["pTb_t"], b1, b2, b3, b4] + [S.ps("bank%d" % i, [128, 512], F32) for i in range(5, 8)]
        SH["G_reg"] = Tile("G_reg")
        emit_layer(nc, S, SH, 0, SEQ)
        S.sems["cc"] = st0.enter_context(nc.semaphore("s_cc"))
        RC = min(512, HALF)
        for i in range(HALF // RC):
            nc.gpsimd.collective_compute("AllGather", ALU.bypass, replica_groups=[[2 * b, 2 * b + 1] for b in range(B)],
                                         ins=[SH["x1_loc"][i * RC:(i + 1) * RC, :]],
                                         outs=[SH["GA"][i * 2 * RC:(i + 1) * 2 * RC, :]]).then_inc(S.sems["cc"], 1)
        SH["G_reg"].last_w = ("cc", HALF // RC)
        emit_layer(nc, S, SH, 1, SEQ)
    return nc


def rope_tables(pos):
    row = (pos // GRID_W).astype(np.float32)
    col = (pos % GRID_W).astype(np.float32)
    q = HD // 4
    inv = (10000.0 ** (-np.arange(q, dtype=np.float32) / q)).astype(np.float32)
    ar = row[None, :] * inv[:, None]
    ac = col[None, :] * inv[:, None]
    cr, sr, cc, sc = np.cos(ar), np.sin(ar), np.cos(ac), np.sin(ac)
    C = np.concatenate([cr, cr, cc, cc], axis=0)
    Sg = np.concatenate([-sr, sr, -sc, sc], axis=0)
    return (np.concatenate([C, C], 0).astype(np.float32), np.concatenate([Sg, Sg], 0).astype(np.float32))


def nbr_bias(rpb, g, gk, NT):
    rows = NT * 2
    out = np.full((8, 128, 128), NEG, np.float32)
    if gk < 0 or gk >= NT:
        return out
    ql = np.arange(128)
    r = 2 * g + ql // 64
    c = ql % 64
    kr = 2 * gk + ql // 64
    kc = ql % 64
    win_r = min(8, rows)
    rs = np.clip(r - win_r // 2, 0, rows - win_r)
    cs = np.clip(c - 8, 0, GRID_W - 16)
    valid = ((kr[:, None] >= rs[None, :]) & (kr[:, None] < rs[None, :] + win_r)
             & (kc[:, None] >= cs[None, :]) & (kc[:, None] < cs[None, :] + 16))
    di = kr[:, None] - r[None, :] + 7
    dj = kc[:, None] - c[None, :] + 15
    di = np.clip(di, 0, 14)
    dj = np.clip(dj, 0, 30)
    vals = rpb[:, di, dj]
    return np.where(valid[None], vals, np.float32(NEG)).astype(np.float32)


def chunk_rows(w):
    return np.ascontiguousarray(w.reshape(8, 128, w.shape[1]))


def prep_layer_inputs(layer, SEQ, xs, xcs, p):
    B = xs.shape[0]
    HALF = SEQ // 2
    EXT = HALF + 2 * HALO
    NT = SEQ // 128
    NTo = HALF // 128
    cfg = layer_cfg(layer)
    wi = p["w_in_even"][0] if layer == 0 else p["w_in_odd"][0]
    wo = p["w_out_even"][0] if layer == 0 else p["w_out_odd"][0]
    cols = []
    for f in cfg["fm"]:
        cols += f[1]
    for name, c0, n in cfg["tm"]:
        cols += list(range(c0, c0 + n))
    w_in_l = chunk_rows(np.ascontiguousarray(wi[:, cols]))
    w_out_l = chunk_rows(wo)
    w_mod_l = chunk_rows(p["w_mod"][layer])
    b_mod_l = np.ascontiguousarray(p["b_mod"][layer].reshape(24, 128).T)
    bgate = np.ascontiguousarray(np.broadcast_to(p["b_mod"][layer][2048:3072][None, :], (128, D_MODEL)))
    ident = np.eye(128, dtype=np.float32)
    blk = np.zeros((128, 128), np.float32)
    blk[:64, :64] = 1.0 / 64
    blk[64:, 64:] = 1.0 / 64
    perm = np.zeros((128, 128), np.float32)
    for m in range(128):
        k = m + 16 if (m % 32) < 16 else m - 16
        perm[k, m] = 1.0
    maps = []
    for b in range(B):
        for half in range(2):
            T0 = half * HALF
            pos_e = np.arange(T0 - HALO, T0 + HALF + HALO)
            valid_e = (pos_e >= 0) & (pos_e < SEQ)
            x_e = np.zeros((EXT, D_MODEL), np.float32)
            x_e[valid_e] = xs[b, pos_e[valid_e]]
            T1 = (1 - half) * HALF
            pos_o = np.arange(T1, T1 + HALF)
            x_u = np.concatenate([x_e, xs[b, pos_o]], axis=0)
            pos_u = np.concatenate([np.clip(pos_e, 0, SEQ - 1), pos_o])
            C, Sg = rope_tables(pos_u)
            cvec = np.stack([p["c"][b].reshape(8, 128).T, p["c_ctx"].reshape(8, 128).T], axis=-1)
            m = dict(x_u=x_u, xc=np.ascontiguousarray(xcs[b]), cvec=np.ascontiguousarray(cvec), w_mod=w_mod_l, b_mod=b_mod_l,
                     bgate=bgate, w_in=w_in_l, w_out=w_out_l, ident=ident, blk=blk, perm=perm, ropeC=C, ropeS=Sg)
            G0 = T0 // 128
            if layer == 0:
                m["gains"] = np.ascontiguousarray(np.stack([np.tile(p["a_q_norm"][0], 2), np.tile(p["a_k_norm"][0], 2)], axis=-1))
                rpb = p["b_rpb"][0]
                gi = min(max(G0 + 2, 2), NT - 3) if NT >= 6 else 0
                bi = np.stack([nbr_bias(rpb, gi, gi + j, NT) for j in range(-2, 3)], axis=0)
                m["bias_i"] = np.ascontiguousarray(bi.transpose(2, 1, 0, 3).reshape(128, 8, 640))
                ets = [0, 1, NTo - 2, NTo - 1]
                be = np.stack([np.stack([nbr_bias(rpb, G0 + t, G0 + t + j, NT) for j in range(-3, 4)], axis=0) for t in ets], axis=0)
                m["bias_e"] = np.ascontiguousarray(be.transpose(0, 2, 3, 1, 4).reshape(4, 8, 128, 896))
            else:
                a = np.arange(128)
                tri_prev = np.where(a[:, None] >= a[None, :], 0.0, NEG).astype(np.float32)
                tri_next = np.where(a[:, None] <= a[None, :], 0.0, NEG).astype(np.float32)
                full = np.full((128, 128), NEG, np.float32)
                first_prev = full if G0 == 0 else tri_prev
                last_next = full if G0 + NTo == NT else tri_next
                m["dmask"] = np.ascontiguousarray(np.stack([first_prev, last_next, tri_prev, tri_next], axis=1))
                m["lamv"] = np.ascontiguousarray(np.broadcast_to(p["c_lambda"][0].reshape(1, 256), (128, 256)))
                m["subln"] = np.ascontiguousarray(np.broadcast_to((p["c_subln"][0])[None, :], (128, 128)))
                m["sinks"] = np.ascontiguousarray(np.broadcast_to(p["d_sinks"][0][None, :], (128, 8)))
                m["fnorm"] = np.ascontiguousarray(np.broadcast_to(p["final_norm"][None, :], (128, D_MODEL)))
            maps.append(m)
    return maps


def prep_fused_inputs(SEQ, xs, xcs, p):
    m0 = prep_layer_inputs(0, SEQ, xs, xcs, p)
    m1 = prep_layer_inputs(1, SEQ, xs, xcs, p)
    shared = ("x_u", "xc", "cvec", "ident", "blk", "perm", "ropeC", "ropeS")
    maps = []
    for i, (a, b) in enumerate(zip(m0, m1)):
        half = i % 2
        m = {k: a[k] for k in shared}
        for k, v in a.items():
            if k not in shared:
                m["l0_" + k] = v
        for k, v in b.items():
            if k not in shared:
                m["l1_" + k] = v
        sel = np.zeros((128, 4), np.float32)
        sel[:, 0] = 1.0 if half == 1 else 0.0
        sel[:, 1] = 1.0 if half == 0 else 0.0
        sel[:, 2] = 1.0 if half == 1 else 0.0
        sel[:, 3] = 1.0 if half == 0 else 0.0
        m["sel"] = sel
        sel2 = np.zeros((2, 2, 128), np.float32)
        sel2[0, 0, :] = 1.0
        sel2[1, 1, :] = 1.0
        m["sel2"] = sel2
        maps.append(m)
    return maps


def run_fused(SEQ, xs, xcs, p, runner=None):
    B = xs.shape[0]
    key = ("fused", SEQ, B)
    if key not in _NC_CACHE:
        _NC_CACHE[key] = build_fused(SEQ, B)
    nc = _NC_CACHE[key]
    maps = prep_fused_inputs(SEQ, xs, xcs, p)
    if runner is None:
        res = run_bass_kernel_spmd(nc, maps, core_ids=list(range(len(maps)))).results
    else:
        res = runner(nc, maps)
    HALF = SEQ // 2
    xo = np.zeros_like(xs)
    for b in range(B):
        for half in range(2):
            xo[b, half * HALF:(half + 1) * HALF] = res[2 * b + half]["out_x"]
    return xo


_NC_CACHE = {}


def kernel(x, c, ctx, c_ctx, w_mod, b_mod, w_in_even, w_out_even, a_q_norm, a_k_norm, b_rpb,
           w_in_odd, w_out_odd, c_lambda, c_subln, d_sinks, final_norm):
    p = dict(c=np.asarray(c, np.float32), c_ctx=np.asarray(c_ctx, np.float32), w_mod=np.asarray(w_mod, np.float32),
             b_mod=np.asarray(b_mod, np.float32), w_in_even=np.asarray(w_in_even, np.float32),
             w_out_even=np.asarray(w_out_even, np.float32), a_q_norm=np.asarray(a_q_norm, np.float32),
             a_k_norm=np.asarray(a_k_norm, np.float32), b_rpb=np.asarray(b_rpb, np.float32),
             w_in_odd=np.asarray(w_in_odd, np.float32), w_out_odd=np.asarray(w_out_odd, np.float32),
             c_lambda=np.asarray(c_lambda, np.float32), c_subln=np.asarray(c_subln, np.float32),
             d_sinks=np.asarray(d_sinks, np.float32), final_norm=np.asarray(final_norm, np.float32))
    xs = np.asarray(x, np.float32)
    xcs = np.asarray(ctx, np.float32)
    SEQ = xs.shape[1]
    return run_fused(SEQ, xs, xcs, p)
```
